# Optimizing a Trainium2 kernel written in Bass

```python
import math
import jax
import jax.numpy as jnp
from jax import lax
import numpy as np

D_MODEL = 2048
BATCH = 4
SEQ = 2048
DEPTH = 2
DEC_BATCH = 128
DEC_SEQ = 8
PAST_LEN = 16384
PAGE_SIZE = 128

N_META = 16
N_EVEN = (DEPTH + 1) // 2
N_ODD = DEPTH // 2
MIX_W = D_MODEL // 2
W_A = MIX_W
CONV_W = 31
S5_WIDTH = MIX_W
S5_GROUP = 16
S5_GROUPS = S5_WIDTH // S5_GROUP
S5_STATE = 64
W_C = MIX_W
M_HEADS = 4
M_DK = W_C // M_HEADS
M_DV = W_C // M_HEADS
M_CHUNK = 64
W_D = MIX_W
R_K = 64
R_HEADS = W_D // R_K
R_DECAY_LORA = 64
R_A_LORA = 64
R_SHIFT_SIZES = (W_D, W_D, W_D, R_DECAY_LORA, R_A_LORA)
R_SHIFT_COLS = sum(R_SHIFT_SIZES)
EVEN_SIZES = (W_A, W_A, W_A, S5_WIDTH, S5_WIDTH)
EVEN_IN = sum(EVEN_SIZES)
ODD_SIZES = (W_C, W_C, W_C, W_C, M_HEADS, M_HEADS, W_C, R_SHIFT_COLS, W_D)
ODD_IN = sum(ODD_SIZES)
LN_EPS = 1e-5
R_LN_EPS = 64e-5
ALPHA = (2 * DEPTH) ** 0.25
BETA = (8 * DEPTH) ** -0.25

kernel_name = 'hybrid_conv_s5_mlstm_rwkv7_step'


def _split(t, sizes):
    return jnp.split(t, np.cumsum(sizes)[:-1].tolist(), axis=-1)


def _ln(x, g, b, eps=LN_EPS):
    xf = x.astype(jnp.float32)
    mu = jnp.mean(xf, -1, keepdims=True)
    var = jnp.mean(jnp.square(xf - mu), -1, keepdims=True)
    return (xf - mu) * lax.rsqrt(var + eps) * g.astype(jnp.float32) + b.astype(jnp.float32)


def _head_norm(t, eps):
    mu = jnp.mean(t, -1, keepdims=True)
    var = jnp.mean(jnp.square(t - mu), -1, keepdims=True)
    return (t - mu) * lax.rsqrt(var + eps)


def _conformer_conv(u, g, buf, conv_w, conv_b, ln_g, ln_b, pw):
    h = u * jax.nn.sigmoid(g)
    hp = jnp.concatenate([buf.astype(h.dtype), h], axis=1)
    y = lax.conv_general_dilated(hp, conv_w.astype(h.dtype)[:, None, :], window_strides=(1,), padding='VALID',
                                 dimension_numbers=('NWC', 'WIO', 'NWC'), feature_group_count=W_A)
    y = y + conv_b.astype(h.dtype)
    y = jax.nn.silu(_ln(y, ln_g, ln_b)).astype(u.dtype)
    y = y @ pw.astype(u.dtype)
    return y, hp[:, -(CONV_W - 1):]


def _s5(u, x0_re, x0_im, lam_re, lam_im, log_dt, b_re, b_im, c_re, c_im, d, glu_w, glu_b):
    f32 = jnp.float32
    n, l, _ = u.shape
    uf = u.astype(f32).reshape(n, l, S5_GROUPS, S5_GROUP)
    dt = jnp.exp(log_dt.astype(f32))[:, None]
    lr = lam_re.astype(f32)
    li = lam_im.astype(f32)
    mag = jnp.exp(lr * dt)
    ar = mag * jnp.cos(li * dt)
    ai = mag * jnp.sin(li * dt)
    den = lr * lr + li * li
    qr = ((ar - 1.0) * lr + ai * li) / den
    qi = (ai * lr - (ar - 1.0) * li) / den
    br = b_re.astype(f32)
    bi = b_im.astype(f32)
    bbr = qr[..., None] * br - qi[..., None] * bi
    bbi = qr[..., None] * bi + qi[..., None] * br
    ur = jnp.einsum('nlgh,gph->nlgp', uf, bbr)
    ui = jnp.einsum('nlgh,gph->nlgp', uf, bbi)
    x0r = x0_re.astype(f32)
    x0i = x0_im.astype(f32)
    ur = ur.at[:, 0].add(ar * x0r - ai * x0i)
    ui = ui.at[:, 0].add(ar * x0i + ai * x0r)
    a_r = jnp.broadcast_to(ar, ur.shape)
    a_i = jnp.broadcast_to(ai, ui.shape)

    def combine(e1, e2):
        a1r, a1i, b1r, b1i = e1
        a2r, a2i, b2r, b2i = e2
        return (a1r * a2r - a1i * a2i, a1r * a2i + a1i * a2r,
                a2r * b1r - a2i * b1i + b2r, a2r * b1i + a2i * b1r + b2i)

    _, _, xr, xi = lax.associative_scan(combine, (a_r, a_i, ur, ui), axis=1)
    y = (jnp.einsum('ghp,nlgp->nlgh', c_re.astype(f32), xr)
         - jnp.einsum('ghp,nlgp->nlgh', c_im.astype(f32), xi))
    y = y.reshape(n, l, S5_WIDTH) + d.astype(f32) * u.astype(f32)
    y = jax.nn.gelu(y).astype(u.dtype)
    vg = y @ glu_w.astype(u.dtype) + glu_b.astype(u.dtype)
    v, gt = jnp.split(vg, 2, axis=-1)
    return v * jax.nn.sigmoid(gt), xr[:, -1], xi[:, -1]


def _mlstm_chunk(carry, inp):
    c, nv, m = carry
    q, k, v, ig, lf = inp
    t = q.shape[1]
    b = jnp.cumsum(lf, axis=1)
    m_t = b + jnp.maximum(m[:, None, :], lax.cummax(ig - b, axis=1))
    causal = jnp.tril(jnp.ones((t, t), dtype=bool))
    logd = b[:, :, None, :] - b[:, None, :, :] + ig[:, None, :, :] - m_t[:, :, None, :]
    dmat = jnp.exp(jnp.where(causal[None, :, :, None], logd, -jnp.inf))
    s = jnp.einsum('nthd,nshd->ntsh', q, k) * dmat
    h_intra = jnp.einsum('ntsh,nshd->nthd', s, v)
    n_intra = jnp.sum(s, axis=2)
    inter = jnp.exp(m[:, None, :] + b - m_t)
    h_inter = jnp.einsum('nthk,nhkv->nthv', q, c) * inter[..., None]
    n_inter = jnp.einsum('nthk,nhk->nth', q, nv) * inter
    denom = jnp.maximum(jnp.abs(n_intra + n_inter), jnp.exp(-m_t))
    h = (h_intra + h_inter) / denom[..., None]
    m_new = m_t[:, -1]
    b_end = b[:, -1]
    dec = jnp.exp(m + b_end - m_new)
    w_s = jnp.exp(b_end[:, None, :] - b + ig - m_new[:, None, :])
    c_new = c * dec[..., None, None] + jnp.einsum('nthk,nthv,nth->nhkv', k, v, w_s)
    n_new = nv * dec[..., None] + jnp.einsum('nthk,nth->nhk', k, w_s)
    return (c_new, n_new, m_new), h


def _mlstm(q, k, v, ig, lf, c0, n0, m0, lead):
    seqs = (q, k, v, ig, lf)
    state = (c0, n0, m0)
    hs = []
    if lead > 0:
        state, h_lead = _mlstm_chunk(state, tuple(a[:, :lead] for a in seqs))
        hs.append(h_lead)
    rest = q.shape[1] - lead
    t = M_CHUNK if rest % M_CHUNK == 0 else rest
    nc = rest // t

    def to_chunks(a):
        a = a[:, lead:]
        a = a.reshape((a.shape[0], nc, t) + a.shape[2:])
        return jnp.moveaxis(a, 1, 0)

    state, h_rest = lax.scan(_mlstm_chunk, state, tuple(to_chunks(a) for a in seqs))
    h_rest = jnp.moveaxis(h_rest, 0, 1)
    hs.append(h_rest.reshape((h_rest.shape[0], rest) + h_rest.shape[3:]))
    return jnp.concatenate(hs, axis=1), state


def _rwkv7(p, prev, s0, od, j):
    f32 = jnp.float32
    n, l, _ = p.shape
    pf = p.astype(f32)
    p_prev = jnp.concatenate([prev.astype(f32)[:, None, :], pf[:, :-1]], axis=1)
    pm = pf + (p_prev - pf) * od['mu'][j].astype(f32)
    r, k, v, w1, a1 = _split(pm, R_SHIFT_SIZES)
    w = -jax.nn.softplus(-(od['w0'][j].astype(f32) + jnp.tanh(w1) @ od['w2'][j].astype(f32))) - 0.5
    decay = jnp.exp(-jnp.exp(w))
    a = jax.nn.sigmoid(od['a0'][j].astype(f32) + a1 @ od['a2'][j].astype(f32))
    heads = lambda z: z.reshape(n, l, R_HEADS, R_K)
    kk = heads(k * od['kk'][j].astype(f32))
    kk = kk / jnp.maximum(jnp.sqrt(jnp.sum(kk * kk, -1, keepdims=True)), 1e-12)
    k = k * (1.0 + (a - 1.0) * od['ka'][j].astype(f32))
    r, k, v, decay, a = heads(r), heads(k), heads(v), heads(decay), heads(a)

    def step(s, inp):
        r_t, k_t, v_t, w_t, kk_t, a_t = inp
        sa = -jnp.einsum('nhvk,nhk->nhv', s, kk_t)
        s = (s * w_t[:, :, None, :] + sa[..., None] * (kk_t * a_t)[:, :, None, :]
             + v_t[..., None] * k_t[:, :, None, :])
        return s, jnp.einsum('nhvk,nhk->nhv', s, r_t)

    xs = tuple(jnp.moveaxis(z, 1, 0) for z in (r, k, v, decay, kk, a))
    s_fin, y = lax.scan(step, s0.astype(f32), xs)
    y = jnp.moveaxis(y, 0, 1)
    y = _head_norm(y, R_LN_EPS).reshape(n, l, W_D) * od['r_ln_g'][j].astype(f32) + od['r_ln_b'][j].astype(f32)
    bonus = jnp.sum(r * k * od['rk'][j].astype(f32), -1, keepdims=True) * v
    y = y + bonus.reshape(n, l, W_D)
    return y, s_fin, p[:, -1]


def _even_layer(x, conv_buf, s_re, s_im, ev, j):
    proj = x @ ev['w_in'][j].astype(x.dtype)
    u_a, g_a, z_a, u_b, z_b = _split(proj, EVEN_SIZES)
    y_a, new_buf = _conformer_conv(u_a, g_a, conv_buf, ev['conv_w'][j], ev['conv_b'][j],
                                   ev['a_ln_g'][j], ev['a_ln_b'][j], ev['pw'][j])
    y_b, new_re, new_im = _s5(u_b, s_re, s_im, ev['lam_re'][j], ev['lam_im'][j], ev['log_dt'][j],
                              ev['b_re'][j], ev['b_im'][j], ev['c_re'][j], ev['c_im'][j], ev['d'][j],
                              ev['glu_w'][j], ev['glu_b'][j])
    mix = jnp.concatenate([y_a * jax.nn.silu(z_a), y_b * jax.nn.silu(z_b)], axis=-1).astype(x.dtype)
    out = mix @ ev['w_out'][j].astype(x.dtype)
    x = _ln(ALPHA * x + out, ev['ln_g'][j], ev['ln_b'][j]).astype(x.dtype)
    return x, new_buf, new_re, new_im


def _odd_layer(x, mc, mn, mm, rs, rsh, od, j, lead):
    f32 = jnp.float32
    n, l, _ = x.shape
    proj = x @ od['w_in'][j].astype(x.dtype)
    q, k, v, o, ig, fg, z_c, p_d, z_d = _split(proj, ODD_SIZES)
    hd = lambda z, dd: z.astype(f32).reshape(n, l, M_HEADS, dd)
    qh = hd(q, M_DK) * (M_DK ** -0.5)
    kh = hd(k, M_DK)
    vh = hd(v, M_DV)
    igp = ig.astype(f32) + od['ig_b'][j].astype(f32)
    lf = jax.nn.log_sigmoid(fg.astype(f32) + od['fg_b'][j].astype(f32))
    h, (c_new, n_new, m_new) = _mlstm(qh, kh, vh, igp, lf, mc.astype(f32), mn.astype(f32), mm.astype(f32), lead)
    h = _head_norm(h, LN_EPS).reshape(n, l, W_C) * od['hn_g'][j].astype(f32)
    y_c = h * jax.nn.sigmoid(o.astype(f32)) * jax.nn.silu(z_c.astype(f32))
    y_d, s_new, shift_new = _rwkv7(p_d, rsh, rs, od, j)
    y_d = y_d * jax.nn.silu(z_d.astype(f32))
    mix = jnp.concatenate([y_c, y_d], axis=-1).astype(x.dtype)
    out = mix @ od['w_out'][j].astype(x.dtype)
    x = _ln(ALPHA * x + out, od['ln_g'][j], od['ln_b'][j]).astype(x.dtype)
    return x, c_new, n_new, m_new, s_new, shift_new


def _trunk(x, conv, ssm_re, ssm_im, mc, mn, mm, rs, rsh, lead, ev, od):
    ev_new = ([], [], [])
    od_new = ([], [], [], [], [])
    for layer in range(DEPTH):
        j = layer // 2
        if layer % 2 == 0:
            x, b_new, re_new, im_new = _even_layer(x, conv[j], ssm_re[j], ssm_im[j], ev, j)
            for lst, val in zip(ev_new, (b_new, re_new, im_new)):
                lst.append(val.astype(x.dtype))
        else:
            x, c_new, n_new, m_new, s_new, sh_new = _odd_layer(x, mc[j], mn[j], mm[j], rs[j], rsh[j], od, j, lead)
            for lst, val in zip(od_new, (c_new, n_new, m_new, s_new, sh_new)):
                lst.append(val.astype(x.dtype))
    st = [jnp.stack(lst, axis=0) for lst in ev_new + od_new]
    return x, st[0], st[1], st[2], st[3], st[4], st[5], st[6], st[7]


def setup_inputs(seed: int = 0) -> dict:
    key = jax.random.key(seed)
    it = iter(jax.random.split(key, 64))
    f32 = jnp.float32

    def nrm(shape, scale):
        return scale * jax.random.normal(next(it), shape, f32)

    def uni(shape, lo, hi):
        return jax.random.uniform(next(it), shape, f32, lo, hi)

    inp = {}
    inp['x_prompt'] = nrm((BATCH, SEQ, D_MODEL), 1.0)
    inp['x_sample'] = nrm((DEC_BATCH, DEC_SEQ, D_MODEL), 1.0)
    inp['state_conv'] = nrm((N_EVEN, DEC_BATCH, CONV_W - 1, W_A), 0.5)
    inp['state_ssm_re'] = nrm((N_EVEN, DEC_BATCH, S5_GROUPS, S5_STATE), 0.05)
    inp['state_ssm_im'] = nrm((N_EVEN, DEC_BATCH, S5_GROUPS, S5_STATE), 0.05)
    inp['state_mlstm_c'] = nrm((N_ODD, DEC_BATCH, M_HEADS, M_DK, M_DV), 0.05)
    inp['state_mlstm_n'] = nrm((N_ODD, DEC_BATCH, M_HEADS, M_DK), 0.05)
    inp['state_mlstm_m'] = nrm((N_ODD, DEC_BATCH, M_HEADS), 1.0)
    inp['state_rwkv_s'] = nrm((N_ODD, DEC_BATCH, R_HEADS, R_K, R_K), 0.1)
    inp['state_rwkv_shift'] = nrm((N_ODD, DEC_BATCH, R_SHIFT_COLS), 1.0)
    inp['meta_tokens'] = nrm((N_META, D_MODEL), 1.0)
    inp['ev_w_in'] = nrm((N_EVEN, D_MODEL, EVEN_IN), D_MODEL ** -0.5)
    inp['a_conv_w'] = nrm((N_EVEN, CONV_W, W_A), CONV_W ** -0.5)
    inp['a_conv_b'] = nrm((N_EVEN, W_A), 0.02)
    inp['a_ln_g'] = 1.0 + nrm((N_EVEN, W_A), 0.02)
    inp['a_ln_b'] = nrm((N_EVEN, W_A), 0.02)
    inp['a_pw'] = nrm((N_EVEN, W_A, W_A), W_A ** -0.5)
    inp['s5_lambda_re'] = -0.5 + nrm((N_EVEN, S5_GROUPS, S5_STATE), 0.01)
    inp['s5_lambda_im'] = (jnp.pi * jnp.arange(S5_STATE, dtype=f32))[None, None, :] + nrm((N_EVEN, S5_GROUPS, S5_STATE), 0.01)
    inp['s5_log_dt'] = uni((N_EVEN, S5_GROUPS), math.log(1e-3), math.log(1e-1))
    inp['s5_b_re'] = nrm((N_EVEN, S5_GROUPS, S5_STATE, S5_GROUP), (2 * S5_GROUP) ** -0.5)
    inp['s5_b_im'] = nrm((N_EVEN, S5_GROUPS, S5_STATE, S5_GROUP), (2 * S5_GROUP) ** -0.5)
    inp['s5_c_re'] = nrm((N_EVEN, S5_GROUPS, S5_GROUP, S5_STATE), S5_STATE ** -0.5)
    inp['s5_c_im'] = nrm((N_EVEN, S5_GROUPS, S5_GROUP, S5_STATE), S5_STATE ** -0.5)
    inp['s5_d'] = nrm((N_EVEN, S5_WIDTH), 1.0)
    inp['s5_glu_w'] = nrm((N_EVEN, S5_WIDTH, 2 * S5_WIDTH), S5_WIDTH ** -0.5)
    inp['s5_glu_b'] = nrm((N_EVEN, 2 * S5_WIDTH), 0.02)
    inp['ev_w_out'] = nrm((N_EVEN, W_A + S5_WIDTH, D_MODEL), BETA * (W_A + S5_WIDTH) ** -0.5)
    inp['ev_ln_g'] = 1.0 + nrm((N_EVEN, D_MODEL), 0.02)
    inp['ev_ln_b'] = nrm((N_EVEN, D_MODEL), 0.02)
    inp['od_w_in'] = nrm((N_ODD, D_MODEL, ODD_IN), D_MODEL ** -0.5)
    inp['m_ig_b'] = nrm((N_ODD, M_HEADS), 0.1)
    inp['m_fg_b'] = jnp.linspace(3.0, 6.0, M_HEADS, dtype=f32)[None, :] + nrm((N_ODD, M_HEADS), 0.01)
    inp['m_hn_g'] = 1.0 + nrm((N_ODD, W_C), 0.02)
    inp['r_mu'] = uni((N_ODD, R_SHIFT_COLS), 0.0, 1.0)
    w0_base = jnp.broadcast_to(jnp.linspace(-6.0, -1.0, R_K, dtype=f32)[None, :], (R_HEADS, R_K)).reshape(W_D)
    inp['r_w0'] = w0_base[None, :] + nrm((N_ODD, W_D), 0.1)
    inp['r_w2'] = nrm((N_ODD, R_DECAY_LORA, W_D), 0.5 * R_DECAY_LORA ** -0.5)
    inp['r_a0'] = nrm((N_ODD, W_D), 0.1)
    inp['r_a2'] = nrm((N_ODD, R_A_LORA, W_D), 0.5 * R_A_LORA ** -0.5)
    inp['r_kk'] = 0.85 + nrm((N_ODD, W_D), 0.02)
    inp['r_ka'] = 1.0 + nrm((N_ODD, W_D), 0.02)
    inp['r_rk'] = nrm((N_ODD, R_HEADS, R_K), 0.1)
    inp['r_ln_g'] = 1.0 + nrm((N_ODD, W_D), 0.02)
    inp['r_ln_b'] = nrm((N_ODD, W_D), 0.02)
    inp['od_w_out'] = nrm((N_ODD, W_C + W_D, D_MODEL), BETA * (W_C + W_D) ** -0.5)
    inp['od_ln_g'] = 1.0 + nrm((N_ODD, D_MODEL), 0.02)
    inp['od_ln_b'] = nrm((N_ODD, D_MODEL), 0.02)
    return inp


def reference(x_prompt, x_sample, state_conv, state_ssm_re, state_ssm_im, state_mlstm_c, state_mlstm_n,
              state_mlstm_m, state_rwkv_s, state_rwkv_shift, meta_tokens,
              ev_w_in, a_conv_w, a_conv_b, a_ln_g, a_ln_b, a_pw, s5_lambda_re, s5_lambda_im, s5_log_dt,
              s5_b_re, s5_b_im, s5_c_re, s5_c_im, s5_d, s5_glu_w, s5_glu_b, ev_w_out, ev_ln_g, ev_ln_b,
              od_w_in, m_ig_b, m_fg_b, m_hn_g, r_mu, r_w0, r_w2, r_a0, r_a2, r_kk, r_ka, r_rk,
              r_ln_g, r_ln_b, od_w_out, od_ln_g, od_ln_b):
    ev = {'w_in': ev_w_in, 'conv_w': a_conv_w, 'conv_b': a_conv_b, 'a_ln_g': a_ln_g, 'a_ln_b': a_ln_b,
          'pw': a_pw, 'lam_re': s5_lambda_re, 'lam_im': s5_lambda_im, 'log_dt': s5_log_dt,
          'b_re': s5_b_re, 'b_im': s5_b_im, 'c_re': s5_c_re, 'c_im': s5_c_im, 'd': s5_d,
          'glu_w': s5_glu_w, 'glu_b': s5_glu_b, 'w_out': ev_w_out, 'ln_g': ev_ln_g, 'ln_b': ev_ln_b}
    od = {'w_in': od_w_in, 'ig_b': m_ig_b, 'fg_b': m_fg_b, 'hn_g': m_hn_g, 'mu': r_mu, 'w0': r_w0,
          'w2': r_w2, 'a0': r_a0, 'a2': r_a2, 'kk': r_kk, 'ka': r_ka, 'rk': r_rk,
          'r_ln_g': r_ln_g, 'r_ln_b': r_ln_b, 'w_out': od_w_out, 'ln_g': od_ln_g, 'ln_b': od_ln_b}
    nb = x_prompt.shape[0]
    zf = lambda shape: jnp.zeros(shape, jnp.float32)
    meta = jnp.broadcast_to(meta_tokens.astype(x_prompt.dtype)[None], (nb, N_META, D_MODEL))
    xp = jnp.concatenate([meta, x_prompt], axis=1)
    (h_p, p_conv, p_sre, p_sim, p_mc, p_mn, p_mm, p_rs, p_rsh) = _trunk(
        xp, zf((N_EVEN, nb, CONV_W - 1, W_A)), zf((N_EVEN, nb, S5_GROUPS, S5_STATE)),
        zf((N_EVEN, nb, S5_GROUPS, S5_STATE)), zf((N_ODD, nb, M_HEADS, M_DK, M_DV)),
        zf((N_ODD, nb, M_HEADS, M_DK)), zf((N_ODD, nb, M_HEADS)), zf((N_ODD, nb, R_HEADS, R_K, R_K)),
        zf((N_ODD, nb, R_SHIFT_COLS)), N_META, ev, od)
    (y_sample, s_conv, s_sre, s_sim, s_mc, s_mn, s_mm, s_rs, s_rsh) = _trunk(
        x_sample, state_conv, state_ssm_re, state_ssm_im, state_mlstm_c, state_mlstm_n, state_mlstm_m,
        state_rwkv_s, state_rwkv_shift, 0, ev, od)
    y_prompt = h_p[:, N_META:]
    return (y_prompt, y_sample, p_conv, p_sre, p_sim, p_mc, p_mn, p_mm, p_rs, p_rsh,
            s_conv, s_sre, s_sim, s_mc, s_mn, s_mm, s_rs, s_rsh)
```

```python
import contextlib
import numpy as np
import concourse.bass as bass
import concourse.mybir as mybir
from concourse.bass_utils import run_bass_kernel_spmd

F32 = mybir.dt.float32
AF = mybir.ActivationFunctionType
ALU = mybir.AluOpType
AX = mybir.AxisListType

D = 2048
TT = 128
WG = 128
NCORES = 8
ALPHA = 4 ** 0.25
LN_EPS = 1e-5


class Reg:
    __slots__ = ("w", "r")

    def __init__(self):
        self.w = None
        self.r = []


class Buf:
    def __init__(self, ctx, name, t):
        self.ctx, self.name, self.t = ctx, name, t
        self.regs = {"_all": Reg()}
        self.dma_sem = None
        self.dma_cnt = 0

    def __call__(self, key, ap):
        return View(self, key, ap)

    def all(self):
        return View(self, None, self.t[:])

    def _sel(self, key):
        if key is None:
            return list(self.regs.values())
        if key not in self.regs:
            self.regs[key] = Reg()
        return [self.regs[key], self.regs["_all"]]

    def rdeps(self, key):
        return [r.w for r in self._sel(key) if r.w is not None]

    def wdeps(self, key):
        out = []
        for r in self._sel(key):
            if r.w is not None:
                out.append(r.w)
            out.extend(r.r)
        return out

    def note_read(self, key, tok):
        if key is None:
            for r in self.regs.values():
                r.r.append(tok)
        else:
            self._sel(key)[0].r.append(tok)

    def note_write(self, key, tok):
        if key is None:
            self.regs = {"_all": Reg()}
            self.regs["_all"].w = tok
        else:
            r = self._sel(key)[0]
            r.w = tok
            r.r = []


class View:
    __slots__ = ("buf", "key", "ap")

    def __init__(self, buf, key, ap):
        self.buf, self.key, self.ap = buf, key, ap


class Ctx:
    ENG = ("pe", "act", "dve", "pool", "sp")
    EPOCH = 30000

    def __init__(self, nc, es):
        self.nc, self.es = nc, es
        self.prog = {e: [] for e in self.ENG}
        self.cnt = {e: 0 for e in self.ENG}
        self.sem = {e: es.enter_context(nc.semaphore("sem_" + e)) for e in self.ENG}
        self.known = {e: {} for e in self.ENG}
        self.final = []
        self.total = {}
        self.nsem = 5
        self.nbytes = 0

    def sb(self, name, shape, dtype=F32):
        t = self.es.enter_context(self.nc.sbuf_tensor("sb_" + name, list(shape), dtype))
        n = 4
        for s in shape[1:]:
            n *= s
        self.nbytes += n
        return Buf(self, name, t)

    def ps(self, name, shape, dtype=F32):
        t = self.es.enter_context(self.nc.psum_tensor("ps_" + name, list(shape), dtype))
        return Buf(self, name, t)

    def need(self, e, tok):
        sem, val = tok
        k = id(sem)
        if self.known[e].get(k, 0) >= val:
            return
        self.known[e][k] = val
        self.prog[e].append(("wait", sem, val))

    def op(self, e, fn, reads=(), writes=()):
        for v in reads:
            if isinstance(v, View):
                for tok in v.buf.rdeps(v.key):
                    self.need(e, tok)
        for v in writes:
            if isinstance(v, View):
                for tok in v.buf.wdeps(v.key):
                    self.need(e, tok)
        if self.cnt[e] >= self.EPOCH:
            self.total[e] = self.total.get(e, 0) + self.cnt[e]
            self.sem[e] = self.es.enter_context(self.nc.semaphore("sem_%s_%d" % (e, self.total[e])))
            self.cnt[e] = 0
            self.nsem += 1
        self.cnt[e] += 1
        tok = (self.sem[e], self.cnt[e])
        self.prog[e].append(("op", fn, self.sem[e], 1))
        for v in reads:
            if isinstance(v, View):
                v.buf.note_read(v.key, tok)
        for v in writes:
            if isinstance(v, View):
                v.buf.note_write(v.key, tok)
        return tok

    def dma(self, out, in_, q="sp", slow=False):
        sbv = out if isinstance(out, View) else in_
        b = sbv.buf
        if b.dma_sem is None:
            b.dma_sem = self.es.enter_context(self.nc.semaphore("dq_" + b.name))
            self.nsem += 1
        if isinstance(in_, View):
            for tok in in_.buf.rdeps(in_.key):
                self.need(q, tok)
        if isinstance(out, View):
            for tok in out.buf.wdeps(out.key):
                self.need(q, tok)
        b.dma_cnt += 16
        tok = (b.dma_sem, b.dma_cnt)
        oap = out.ap if isinstance(out, View) else out
        iap = in_.ap if isinstance(in_, View) else in_
        if slow:
            fn = lambda eng, oap=oap, iap=iap: eng.dma_start(out=oap, in_=iap, allow_slow_non_contiguous=True)
        else:
            fn = lambda eng, oap=oap, iap=iap: eng.dma_start(out=oap, in_=iap)
        self.prog[q].append(("op", fn, b.dma_sem, 16))
        if isinstance(in_, View):
            in_.buf.note_read(in_.key, tok)
        if isinstance(out, View):
            out.buf.note_write(out.key, tok)
        else:
            self.final.append(tok)
        return tok

    def tt(self, e, out, in0, in1, op):
        return self.op(e, lambda g: g.tensor_tensor(out=out.ap, in0=in0.ap, in1=in1.ap, op=op),
                       reads=[in0, in1], writes=[out])

    def ts(self, e, out, in0, s1, s2, op0, op1=None):
        rd = [in0] + [s for s in (s1, s2) if isinstance(s, View)]
        a1 = s1.ap if isinstance(s1, View) else s1
        a2 = s2.ap if isinstance(s2, View) else s2
        if op1 is None:
            return self.op(e, lambda g: g.tensor_scalar(out=out.ap, in0=in0.ap, scalar1=a1, scalar2=None, op0=op0),
                           reads=rd, writes=[out])
        return self.op(e, lambda g: g.tensor_scalar(out=out.ap, in0=in0.ap, scalar1=a1, scalar2=a2, op0=op0, op1=op1),
                       reads=rd, writes=[out])

    def stt(self, e, out, in0, s, in1, op0, op1):
        rd = [in0, in1] + ([s] if isinstance(s, View) else [])
        a = s.ap if isinstance(s, View) else s
        return self.op(e, lambda g: g.scalar_tensor_tensor(out=out.ap, in0=in0.ap, scalar=a, in1=in1.ap, op0=op0, op1=op1),
                       reads=rd, writes=[out])

    def act(self, out, in_, func, bias=None, scale=None, e="act"):
        rd = [in_] + [s for s in (bias, scale) if isinstance(s, View)]
        kw = {}
        if bias is not None:
            kw["bias"] = bias.ap if isinstance(bias, View) else bias
        if scale is not None:
            kw["scale"] = scale.ap if isinstance(scale, View) else scale
        return self.op(e, lambda g: g.activation(out=out.ap, in_=in_.ap, func=func, **kw), reads=rd, writes=[out])

    def copy(self, e, out, in_):
        if e == "act":
            return self.act(out, in_, AF.Copy)
        return self.op(e, lambda g: g.tensor_copy(out=out.ap, in_=in_.ap), reads=[in_], writes=[out])

    def memset(self, e, out, val):
        return self.op(e, lambda g: g.memset(out.ap, val), writes=[out])

    def mm(self, out, lhsT, rhs, start=True, stop=True):
        return self.op("pe", lambda g: g.matmul(out.ap, lhsT=lhsT.ap, rhs=rhs.ap, start=start, stop=stop),
                       reads=[lhsT, rhs], writes=[out])

    def tr(self, out, in_, ident):
        return self.op("pe", lambda g: g.transpose(out.ap, in_.ap, ident.ap), reads=[in_, ident], writes=[out])

    def emit(self):
        nc = self.nc
        for tok in self.final:
            self.need("sp", tok)
        for e in self.ENG:
            if e != "sp" and self.cnt[e] > 0:
                self.need("sp", (self.sem[e], self.cnt[e]))
        engs = {"pe": "tensor", "act": "scalar", "dve": "vector", "pool": "gpsimd", "sp": "sync"}
        with nc.Block() as block:
            for e, attr in engs.items():
                items = self.prog[e]

                def body(eng, items=items):
                    for it in items:
                        if it[0] == "wait":
                            eng.wait_ge(it[1], it[2])
                        else:
                            it[1](eng).then_inc(it[2], it[3])

                getattr(block, attr)(body)


def make_consts():
    c = {}
    c["ident"] = np.eye(128, dtype=np.float32)
    c["ones"] = np.ones((128, 128), dtype=np.float32)
    r = np.arange(128)
    mrow = (r % 32) // 16
    mcol = np.arange(128) // 64
    bm = (mrow[:, None] == mcol[None, :]).astype(np.float32)
    ev_r = ((r // 32) % 2 == 0).astype(np.float32)
    c["bmask"] = bm * ev_r[:, None]
    c["bmask_o"] = bm * (1 - ev_r)[:, None]
    gl = np.arange(128) // 16
    cm = ((gl[None, :] % 2) == (r[:, None] // 64)).astype(np.float32)
    c["cmask"] = cm * ev_r[None, :]
    c["cmask_o"] = cm * (1 - ev_r)[None, :]
    BIG = 1e30
    i128 = np.arange(128)
    allow_p = (i128[:, None] <= i128[None, :])
    c["maskbig_p"] = np.where(allow_p, 0.0, BIG).astype(np.float32)
    c["segtri_p"] = allow_p.astype(np.float32)
    i64 = np.arange(64)
    allow_s = (i64[:, None] <= i64[None, :]) & ((i64[:, None] // 8) == (i64[None, :] // 8))
    c["maskbig_s"] = np.where(allow_s, 0.0, BIG).astype(np.float32)
    c["segtri_s"] = allow_s.astype(np.float32)
    r01p = np.ones((128, 128), np.float32); r01p[:, 0] = 0
    r01s = np.ones((128, 64), np.float32); r01s[:, ::8] = 0
    c["r01_p"], c["r01_s"] = r01p, r01s
    c["rneg_p"] = ((1 - r01p) * -BIG).astype(np.float32)
    c["rneg_s"] = ((1 - r01s) * -BIG).astype(np.float32)
    sel = np.zeros((8, 8, 128), np.float32)
    for k in range(8):
        sel[k, k, :] = 1
    c["sel"] = sel.reshape(8, 1024)
    segsel = ((i64[None, :] // 8) == np.arange(8)[:, None]).astype(np.float32)
    c["segsel"] = segsel
    c["segc"] = np.ascontiguousarray(segsel.T)
    c["segrow"] = np.ascontiguousarray(np.broadcast_to(segsel.reshape(1, 512), (128, 512))).astype(np.float32)
    c["blk"] = ((i128[:, None] // 64) == (i128[None, :] // 64)).astype(np.float32)
    st_p = (i128[:, None] < i128[None, :])
    st_s = (i64[:, None] < i64[None, :]) & ((i64[:, None] // 8) == (i64[None, :] // 8))
    c["sstri_p"] = st_p.astype(np.float32)
    c["sstriT_p"] = np.ascontiguousarray(st_p.T).astype(np.float32)
    c["sstri_s"] = st_s.astype(np.float32)
    c["sstriT_s"] = np.ascontiguousarray(st_s.T).astype(np.float32)
    return c


CONST_SHAPES = {"ident": [128, 128], "ones": [128, 128], "bmask": [128, 128], "cmask": [128, 128],
                "bmask_o": [128, 128], "cmask_o": [128, 128],
                "maskbig_p": [128, 128], "segtri_p": [128, 128], "maskbig_s": [64, 64], "segtri_s": [64, 64],
                "r01_p": [128, 128], "r01_s": [128, 64], "rneg_p": [128, 128], "rneg_s": [128, 64],
                "sel": [8, 1024], "segsel": [8, 64], "segc": [64, 8], "segrow": [128, 512], "blk": [128, 128],
                "sstri_p": [128, 128], "sstriT_p": [128, 128], "sstri_s": [64, 64], "sstriT_s": [64, 64]}


def tile_plan(cfg):
    tiles = []
    npt = cfg.get("n_prompt_tiles", 17)
    pos = 0
    for i in range(17):
        T = 16 if i == 16 else TT
        if i < npt:
            tiles.append(dict(kind="p", T=T, pos=pos, nseq=1, L=T, first=(i == 0), last=(i == npt - 1), s0=0))
        pos += T
    if cfg.get("sample", True):
        for h in range(2):
            tiles.append(dict(kind="s", T=64, pos=0, nseq=8, L=8, first=True, last=True, s0=8 * h))
    return tiles


def build(cfg):
    nc = bass.Bass("TRN2", target_bir_lowering=False)
    es = contextlib.ExitStack()
    C = Ctx(nc, es)
    dbg = cfg.get("debug", False)
    nlayers = cfg.get("layers", 2)

    def din(name, shape):
        return nc.dram_tensor(name, list(shape), F32, kind="ExternalInput").ap()

    def dout(name, shape):
        return nc.dram_tensor(name, list(shape), F32, kind="ExternalOutput").ap()

    I = {}
    I["xp"] = din("xp", [2048, D])
    I["meta"] = din("meta", [16, D])
    I["xs"] = din("xs", [128, D])
    I["s_conv"] = din("s_conv", [16 * 30, 1024])
    I["s_sre"] = din("s_sre", [16, 64, 64])
    I["s_sim"] = din("s_sim", [16, 64, 64])
    for nm, shp in CONST_SHAPES.items():
        I[nm] = din(nm, shp)
    I["s_mc"] = din("s_mc", [16, 4, 256, 256])
    I["s_mn"] = din("s_mn", [16, 4, 256])
    I["s_mm"] = din("s_mm", [16, 4])
    I["s_rs"] = din("s_rs", [16, 16, 64, 64])
    I["s_rsh"] = din("s_rsh", [16, 3200])
    I["od_w_in"] = din("od_w_in", [D, 9352])
    I["m_ig_b"] = din("m_ig_b", [1, 4])
    I["m_fg_b"] = din("m_fg_b", [1, 4])
    for nm in ("m_hn_g", "r_w0", "r_a0", "r_kk", "r_ka", "r_ln_g", "r_ln_b", "r_rk"):
        I[nm] = din(nm, [1, 1024])
    I["r_mu"] = din("r_mu", [1, 3200])
    I["r_w2"] = din("r_w2", [64, 1024])
    I["r_a2"] = din("r_a2", [64, 1024])
    I["od_w_out"] = din("od_w_out", [2048, D])
    I["od_ln_g"] = din("od_ln_g", [1, D])
    I["od_ln_b"] = din("od_ln_b", [1, D])
    I["ev_w_in"] = din("ev_w_in", [D, 5120])
    I["a_conv_w"] = din("a_conv_w", [31, 1024])
    for nm in ("a_conv_b", "a_ln_g", "a_ln_b", "s5_d"):
        I[nm] = din(nm, [1, 1024])
    I["a_pw"] = din("a_pw", [1024, 1024])
    I["s5_lambda_re"] = din("s5_lambda_re", [64, 64])
    I["s5_lambda_im"] = din("s5_lambda_im", [64, 64])
    I["s5_log_dt"] = din("s5_log_dt", [1, 64])
    I["s5_b_re"] = din("s5_b_re", [64, 64, 16])
    I["s5_b_im"] = din("s5_b_im", [64, 64, 16])
    I["s5_c_re"] = din("s5_c_re", [1024, 64])
    I["s5_c_im"] = din("s5_c_im", [1024, 64])
    I["s5_glu_w"] = din("s5_glu_w", [1024, 2048])
    I["s5_glu_b"] = din("s5_glu_b", [1, 2048])
    I["ev_w_out"] = din("ev_w_out", [2048, D])
    I["ev_ln_g"] = din("ev_ln_g", [1, D])
    I["ev_ln_b"] = din("ev_ln_b", [1, D])

    O = {}
    O["y_p"] = dout("y_p", [2048, D])
    O["y_s"] = dout("y_s", [128, D])
    O["p_conv"] = dout("p_conv", [30, 1024])
    O["p_sre"] = dout("p_sre", [64, 64])
    O["p_sim"] = dout("p_sim", [64, 64])
    O["s_conv_o"] = dout("s_conv_o", [16 * 30, 1024])
    O["s_sre_o"] = dout("s_sre_o", [16, 64, 64])
    O["s_sim_o"] = dout("s_sim_o", [16, 64, 64])
    O["p_mc"] = dout("p_mc", [4, 256, 256])
    O["p_mn"] = dout("p_mn", [4, 256])
    O["p_mm"] = dout("p_mm", [1, 4])
    O["p_rs"] = dout("p_rs", [16, 64, 64])
    O["p_rsh"] = dout("p_rsh", [1, 3200])
    O["s_mc_o"] = dout("s_mc_o", [16, 4, 256, 256])
    O["s_mn_o"] = dout("s_mn_o", [16, 4, 256])
    O["s_mm_o"] = dout("s_mm_o", [16, 4])
    O["s_rs_o"] = dout("s_rs_o", [16, 16, 64, 64])
    O["s_rsh_o"] = dout("s_rsh_o", [16, 3200])

    ident = C.sb("ident", [128, 128])
    ones = C.sb("ones", [128, 128])
    XT = C.sb("XT", [128, 16, TT])
    XS = C.sb("XS", [128, D])
    PJ = C.sb("PJ", [128, 33, TT])
    MIX = C.sb("MIX", [128, 16 * TT])
    WS = [C.sb("WS%d" % i, [128, 16, WG]) for i in range(2)]
    HB = C.sb("HB", [128, 8, 304])
    HC = C.sb("HC", [128, 8, 30])
    ACC = C.sb("ACC", [128, 8, TT])
    SQ2 = C.sb("SQ2", [128, 2, TT])
    ST = C.sb("ST", [128, 3, TT])
    VEC0 = C.sb("VEC0", [128, 8, 40])
    VECG = C.sb("VECG", [128, 16, 4])
    TMP = C.sb("TMP", [128, 128])
    XR = C.sb("XR", [128, 32, 129])
    XI = C.sb("XI", [128, 32, 129])
    S5C = C.sb("S5C", [128, 2, 32])
    S5A = C.sb("S5A", [128, 8, 32])
    S5T = C.sb("S5T", [128, 10, 32])
    SCS = C.sb("SCS", [128, 2, 32, 8])
    BRE = [C.sb("BRE%d" % i, [128, 8, 128]) for i in range(2)]
    BIM = [C.sb("BIM%d" % i, [128, 8, 128]) for i in range(2)]
    CRE = [C.sb("CRE%d" % i, [128, 8, 128]) for i in range(2)]
    CIM = [C.sb("CIM%d" % i, [128, 8, 128]) for i in range(2)]
    mask_bo = C.sb("mask_bo", [128, 128])
    mask_co = C.sb("mask_co", [128, 128])
    S5ST = C.sb("S5ST", [128, 128])
    mask_b = C.sb("mask_b", [128, 128])
    mask_c = C.sb("mask_c", [128, 128])

    PSA = [C.ps("PSA%d" % i, [128, 512]) for i in range(4)]
    PSB = [C.ps("PSB%d" % i, [128, 512]) for i in range(4)]

    def mixv(ct, T):
        return MIX(("t", ct), MIX.t[:, ct * TT:ct * TT + T])

    C.dma(ident.all(), I["ident"])
    C.dma(ones.all(), I["ones"])
    C.dma(mask_b.all(), I["bmask"])
    C.dma(mask_c.all(), I["cmask"])
    C.dma(mask_bo.all(), I["bmask_o"])
    C.dma(mask_co.all(), I["cmask_o"])

    rr = {"psa": 0, "psb": 0, "ws": 0, "ev": 0, "sq": 0}

    def next_psa():
        rr["psa"] = (rr["psa"] + 1) % 4
        return PSA[rr["psa"]]

    def next_psb():
        rr["psb"] = (rr["psb"] + 1) % 4
        return PSB[rr["psb"]]

    def ev_eng():
        rr["ev"] += 1
        return "dve" if rr["ev"] % 2 else "act"

    def load_rows_T(dst, rows, ncols, col0=0, ct0=0):
        r0 = 0
        for ap in rows:
            nr = ap.shape[0]
            C.dma(XS(None, XS.t[r0:r0 + nr, 0:ncols]), ap)
            r0 += nr
        nr = r0
        for ct in range(ncols // 128):
            ps = next_psb()
            C.tr(ps(None, ps.t[:, 0:nr]), XS(None, XS.t[0:nr, ct * 128:(ct + 1) * 128]), ident(None, ident.t[0:nr, 0:nr]))
            C.copy("dve", dst(None, dst.t[:, ct0 + ct, col0:col0 + nr]), ps(None, ps.t[:, 0:nr]))

    load_rows_T(VEC0, [I["a_conv_w"], I["a_conv_b"], I["a_ln_g"], I["a_ln_b"], I["s5_d"]], 1024)
    load_rows_T(VECG, [I["s5_glu_b"], I["ev_ln_g"], I["ev_ln_b"]], 2048)

    def gp_ap(ap2d):
        return ap2d.rearrange("(q m) p -> (m p) q", m=2)

    LR, LI, DT, AR, AI, NAI = range(6)
    A = lambda k: S5A(None, S5A.t[:, k, :])
    Tm = lambda k: S5T(None, S5T.t[:, k, :])
    for m in range(2):
        C.dma(S5A(None, S5A.t[64 * m:64 * m + 64, LR, :]), I["s5_lambda_re"].rearrange("(q m) p -> m p q", m=2)[m], slow=True)
        C.dma(S5A(None, S5A.t[64 * m:64 * m + 64, LI, :]), I["s5_lambda_im"].rearrange("(q m) p -> m p q", m=2)[m], slow=True)
    ldt = I["s5_log_dt"]
    for m in range(2):
        src = bass.AP(ldt.tensor, ldt.offset + m, [[0, 64], [2, 32]])
        C.dma(S5A(None, S5A.t[64 * m:64 * m + 64, DT, :]), src, slow=True)
    C.act(A(DT), A(DT), AF.Exp)
    C.tt("dve", Tm(0), A(LR), A(DT), ALU.mult)
    C.act(Tm(1), Tm(0), AF.Exp)
    C.tt("dve", Tm(2), A(LI), A(DT), ALU.mult)
    PI = float(np.pi)
    C.ts("dve", Tm(3), Tm(2), 1.0 / 32, None, ALU.mult)
    C.act(Tm(4), Tm(3), AF.Sin)
    C.ts("dve", Tm(3), Tm(3), PI / 2, None, ALU.add)
    C.act(Tm(5), Tm(3), AF.Sin)
    for _ in range(5):
        C.tt("dve", Tm(3), Tm(4), Tm(5), ALU.mult)
        C.tt("dve", Tm(8), Tm(5), Tm(5), ALU.mult)
        C.tt("dve", Tm(9), Tm(4), Tm(4), ALU.mult)
        C.ts("dve", Tm(4), Tm(3), 2.0, None, ALU.mult)
        C.tt("dve", Tm(5), Tm(8), Tm(9), ALU.subtract)
    C.tt("dve", A(AR), Tm(1), Tm(5), ALU.mult)
    C.tt("dve", A(AI), Tm(1), Tm(4), ALU.mult)
    C.ts("dve", A(NAI), A(AI), -1.0, None, ALU.mult)
    C.tt("dve", Tm(0), A(LR), A(LR), ALU.mult)
    C.tt("dve", Tm(1), A(LI), A(LI), ALU.mult)
    C.tt("dve", Tm(0), Tm(0), Tm(1), ALU.add)
    C.op("dve", lambda g: g.reciprocal(out=S5T.t[:, 0, :], in_=S5T.t[:, 0, :]), reads=[Tm(0)], writes=[Tm(0)])
    C.ts("dve", Tm(1), A(AR), -1.0, None, ALU.add)
    C.tt("dve", Tm(2), Tm(1), A(LR), ALU.mult)
    C.tt("dve", Tm(3), A(AI), A(LI), ALU.mult)
    C.tt("dve", Tm(2), Tm(2), Tm(3), ALU.add)
    C.tt("dve", Tm(6), Tm(2), Tm(0), ALU.mult)
    C.tt("dve", Tm(2), A(AI), A(LR), ALU.mult)
    C.tt("dve", Tm(3), Tm(1), A(LI), ALU.mult)
    C.tt("dve", Tm(2), Tm(2), Tm(3), ALU.subtract)
    C.tt("dve", Tm(7), Tm(2), Tm(0), ALU.mult)
    braw_t = PJ.t[:, 0:8, :].rearrange("p a b -> p (a b)").rearrange("p (k q h) -> p k q h", k=2, q=32)
    bb_t = PJ.t[:, 8:24, :].rearrange("p a b -> p (a b)").rearrange("p (k q h) -> p k q h", k=2, q=32)
    sc_t = PJ.t[:, 24:28, :].rearrange("p a b -> p (a b)").rearrange("p (q h) -> p q h", q=32)
    for k, nm in enumerate(("s5_b_re", "s5_b_im")):
        for m in range(2):
            C.dma(PJ(None, braw_t[64 * m:64 * m + 64, k, :, :]), I[nm].rearrange("(q m) p h -> m p q h", m=2)[m])
    qr_b = S5T(None, S5T.t[:, 6, :].unsqueeze(2).broadcast_to([128, 32, 16]))
    qi_b = S5T(None, S5T.t[:, 7, :].unsqueeze(2).broadcast_to([128, 32, 16]))
    br = PJ(None, braw_t[:, 0, :, :])
    bi = PJ(None, braw_t[:, 1, :, :])
    sc = PJ(None, sc_t)
    for dup in range(2):
        o_r = PJ(None, bb_t[:, 0, :, dup * 16:(dup + 1) * 16])
        o_i = PJ(None, bb_t[:, 1, :, dup * 16:(dup + 1) * 16])
        C.tt("dve", o_r, br, qr_b, ALU.mult)
        C.tt("dve", sc, bi, qi_b, ALU.mult)
        C.tt("dve", o_r, o_r, sc, ALU.subtract)
        C.tt("dve", o_i, bi, qr_b, ALU.mult)
        C.tt("dve", sc, br, qi_b, ALU.mult)
        C.tt("dve", o_i, o_i, sc, ALU.add)
    for k, dst in enumerate((BRE, BIM)):
        for gt in range(8):
            ps = next_psb()
            C.copy("dve", TMP.all(), PJ(None, bb_t[:, k, 4 * gt:4 * gt + 4, :]))
            C.tr(ps(None, ps.t[:, 0:128]), TMP.all(), ident.all())
            C.tt("dve", dst[0](None, dst[0].t[:, gt, :]), ps(None, ps.t[:, 0:128]), mask_b.all(), ALU.mult)
            C.tt("dve", dst[1](None, dst[1].t[:, gt, :]), ps(None, ps.t[:, 0:128]), mask_bo.all(), ALU.mult)
    for k, (nm, dst) in enumerate((("s5_c_re", CRE), ("s5_c_im", CIM))):
        for gt in range(8):
            for dup in range(2):
                C.dma(XS(None, XS.t[:, dup * 64:(dup + 1) * 64]), I[nm][gt * 128:(gt + 1) * 128, :])
            ps = next_psb()
            C.tr(ps(None, ps.t[:, 0:128]), XS(None, XS.t[:, 0:128]), ident.all())
            for par, mk in enumerate((mask_c, mask_co)):
                C.stt("dve", dst[par](None, dst[par].t[:, gt, :]), ps(None, ps.t[:, 0:128]), 1.0 if k == 0 else -1.0,
                      mk.all(), ALU.mult, ALU.mult)

    def stream_mm(W, nkt, coltiles, rhs_fn, T, evac):
        groups = []
        cur = []
        for ct in coltiles:
            if cur and (ct[0] + ct[1] - cur[0][0] > WG):
                groups.append(cur)
                cur = []
            cur.append(ct)
        if cur:
            groups.append(cur)
        Wv = W.rearrange("(kt p) c -> p kt c", p=128)
        j = 0
        for grp in groups:
            g0 = grp[0][0]
            gw = grp[-1][0] + grp[-1][1] - g0
            ws = WS[rr["ws"] % 2]
            rr["ws"] += 1
            C.dma(ws(None, ws.t[:, 0:nkt, 0:gw]), Wv[:, :, g0:g0 + gw])
            for (c0, w) in grp:
                ps = next_psa()
                pv = ps(None, ps.t[0:w, 0:T])
                for kt in range(nkt):
                    C.mm(pv, ws(None, ws.t[:, kt, c0 - g0:c0 - g0 + w]), rhs_fn(kt), start=(kt == 0), stop=(kt == nkt - 1))
                evac(j, pv)
                j += 1

    def layer_norm_cols(src, ntile, T, gcol, bcol, vec, func=AF.Identity, eps=LN_EPS):
        nch = ntile * 128
        ps = next_psb()
        ps2 = next_psb()
        for ct in range(ntile):
            sv = src(("t", ct), src.t[:, ct, 0:T])
            sq = SQ2(("s", rr["sq"] % 2), SQ2.t[:, rr["sq"] % 2, 0:T])
            rr["sq"] += 1
            C.act(sq, sv, AF.Square)
            C.mm(ps(None, ps.t[:, 0:T]), ones.all(), sv, start=(ct == 0), stop=(ct == ntile - 1))
            C.mm(ps2(None, ps2.t[:, 0:T]), ones.all(), sq, start=(ct == 0), stop=(ct == ntile - 1))
        st = lambda k: ST(None, ST.t[:, k, 0:T])
        C.ts("dve", st(0), ps(None, ps.t[:, 0:T]), 1.0 / nch, None, ALU.mult)
        C.ts("dve", st(1), ps2(None, ps2.t[:, 0:T]), 1.0 / nch, None, ALU.mult)
        C.tt("dve", st(2), st(0), st(0), ALU.mult)
        C.tt("dve", st(1), st(1), st(2), ALU.subtract)
        C.ts("dve", st(1), st(1), eps, None, ALU.add)
        C.act(st(1), st(1), AF.Sqrt)
        C.op("dve", lambda g, T=T: g.reciprocal(out=ST.t[:, 1, 0:T], in_=ST.t[:, 1, 0:T]), reads=[st(1)], writes=[st(1)])
        C.tt("dve", st(2), st(0), st(1), ALU.mult)
        C.ts("dve", st(2), st(2), -1.0, None, ALU.mult)
        for ct in range(ntile):
            e = "dve" if ct % 2 == 0 else "pool"
            sv = src(("t", ct), src.t[:, ct, 0:T])
            C.tt(e, sv, sv, st(1), ALU.mult)
            C.tt(e, sv, sv, st(2), ALU.add)
            C.act(sv, sv, func, bias=vec(None, vec.t[:, ct, bcol:bcol + 1]), scale=vec(None, vec.t[:, ct, gcol:gcol + 1]))

    def load_x(tile):
        n = tile["T"]
        if tile["kind"] == "s":
            C.dma(XS(None, XS.t[0:n, :]), I["xs"][tile["s0"] * 8:tile["s0"] * 8 + n, :])
        else:
            p0 = tile["pos"]
            r = 0
            if p0 < 16:
                C.dma(XS(None, XS.t[0:16, :]), I["meta"][:, :])
                r = 16
            x0 = p0 + r - 16
            C.dma(XS(None, XS.t[r:n, :]), I["xp"][x0:x0 + n - r, :])
        for dt_ in range(16):
            ps = next_psb()
            C.tr(ps(None, ps.t[:, 0:n]), XS(None, XS.t[0:n, dt_ * 128:(dt_ + 1) * 128]), ident(None, ident.t[0:n, 0:n]))
            C.copy("act" if dt_ % 2 else "dve", XT(("t", dt_), XT.t[:, dt_, 0:n]), ps(None, ps.t[:, 0:n]))

    def store_y(tile):
        n = tile["T"]
        for dt_ in range(16):
            ps = next_psb()
            C.tr(ps(None, ps.t[0:n, 0:128]), XT(("t", dt_), XT.t[:, dt_, 0:n]), ident.all())
            C.copy("act" if dt_ % 2 else "dve", XS(None, XS.t[0:n, dt_ * 128:(dt_ + 1) * 128]), ps(None, ps.t[0:n, 0:128]))
        if tile["kind"] == "s":
            C.dma(O["y_s"][tile["s0"] * 8:tile["s0"] * 8 + n, :], XS(None, XS.t[0:n, :]))
        else:
            p0 = tile["pos"]
            r = 16 if p0 < 16 else 0
            x0 = p0 + r - 16
            C.dma(O["y_p"][x0:x0 + n - r, :], XS(None, XS.t[r:n, :]))

    def out_proj_ln(W, tile, vec, gcol, bcol):
        T = tile["T"]

        def evac(j, pv):
            xv_ = XT(("t", j), XT.t[:, j, 0:T])
            C.stt("dve", xv_, xv_, ALPHA, pv, ALU.mult, ALU.add)

        stream_mm(W, 16, [(i * 128, 128) for i in range(16)], lambda kt: mixv(kt, T), T, evac)
        layer_norm_cols(XT, 16, T, gcol, bcol, vec)

    def layer0(tile):
        T, nseq, L = tile["T"], tile["nseq"], tile["L"]
        is_s = tile["kind"] == "s"
        s0 = tile["s0"]
        xrhs = lambda kt: XT(("t", kt), XT.t[:, kt, 0:T])
        pj = lambda j: PJ(("t", j), PJ.t[:, j, 0:T])

        fence(HB, HB.t[0:1, 0, 0:1])
        def evacA(j, pv):
            if j < 8:
                C.copy(ev_eng(), pj(j), pv)
            elif j < 16:
                C.act(pj(j), pv, AF.Sigmoid)
            else:
                C.act(pj(j), pv, AF.Silu)

        stream_mm(I["ev_w_in"], 16, [(i * 128, 128) for i in range(24)], xrhs, T, evacA)

        W_ = 30 + L

        def hb(ct, a, b):
            v = HB.t[:, ct, 0:nseq * W_].rearrange("p (n w) -> p n w", w=W_)[:, :, a:b]
            return HB(("t", ct), v)

        def tokv(buf, ct):
            return buf(("t", ct), buf.t[:, ct, 0:T].rearrange("p (n l) -> p n l", l=L))

        if is_s:
            for q in range(2):
                C.dma(XS(None, XS.t[0:120, 0:1024]), I["s_conv"][s0 * 30 + q * 120:s0 * 30 + (q + 1) * 120, :])
                for ct in range(8):
                    ps = next_psb()
                    C.tr(ps(None, ps.t[:, 0:120]), XS(None, XS.t[0:120, ct * 128:(ct + 1) * 128]), ident(None, ident.t[0:120, 0:120]))
                    dstv = HB.t[:, ct, 0:nseq * W_].rearrange("p (n w) -> p n w", w=W_)[:, 4 * q:4 * q + 4, 0:30]
                    C.copy("dve", HB(("t", ct), dstv), ps(None, ps.t[:, 0:120].rearrange("p (n r) -> p n r", r=30)))
        else:
            for ct in range(8):
                if tile["first"]:
                    C.memset("pool", hb(ct, 0, 30), 0.0)
                else:
                    C.copy("pool", hb(ct, 0, 30), HC(("t", ct), HC.t[:, ct, :].unsqueeze(1)))
        for ct in range(8):
            e = "dve" if ct % 2 == 0 else "pool"
            C.tt(e, hb(ct, 30, 30 + L), tokv(PJ, ct), tokv(PJ, 8 + ct), ALU.mult)
        for ct in range(8):
            e = "dve"
            acc = tokv(ACC, ct)
            wcol = lambda j, ct=ct: VEC0(None, VEC0.t[:, ct, j:j + 1])
            C.ts(e, acc, hb(ct, 0, L), wcol(0), wcol(31), ALU.mult, ALU.add)
            for j in range(1, 31):
                C.stt(e, acc, hb(ct, j, j + L), wcol(j), acc, ALU.mult, ALU.add)
        if is_s:
            for q in range(2):
                for ct in range(8):
                    ps = next_psb()
                    srcv = HB.t[:, ct, 0:nseq * W_].rearrange("p (n w) -> p n w", w=W_)[:, 4 * q:4 * q + 4, L:L + 30]
                    C.copy("pool", TMP(None, TMP.t[:, 0:120].rearrange("p (n r) -> p n r", r=30)), HB(("t", ct), srcv))
                    C.tr(ps(None, ps.t[0:120, 0:128]), TMP(None, TMP.t[:, 0:120]), ident.all())
                    C.copy("dve", XS(None, XS.t[0:120, ct * 128:(ct + 1) * 128]), ps(None, ps.t[0:120, 0:128]))
                C.dma(O["s_conv_o"][s0 * 30 + q * 120:s0 * 30 + (q + 1) * 120, :], XS(None, XS.t[0:120, 0:1024]))
        else:
            for ct in range(8):
                C.copy("pool", TMP(None, TMP.t[:, 0:30]), HB(("t", ct), HB.t[:, ct, L:L + 30]))
                C.copy("pool", HC(("t", ct), HC.t[:, ct, :]), TMP(None, TMP.t[:, 0:30]))
            if tile["last"]:
                for ct in range(8):
                    ps = next_psb()
                    C.tr(ps(None, ps.t[0:30, 0:128]), HC(("t", ct), HC.t[:, ct, :]), ident.all())
                    C.copy("dve", XS(None, XS.t[0:30, ct * 128:(ct + 1) * 128]), ps(None, ps.t[0:30, 0:128]))
                C.dma(O["p_conv"][:, :], XS(None, XS.t[0:30, 0:1024]))
        layer_norm_cols(ACC, 8, T, 32, 33, VEC0, func=AF.Silu)

        def evac_pw(j, pv):
            C.tt("dve", mixv(j, T), pv, pj(16 + j), ALU.mult)

        stream_mm(I["a_pw"], 8, [(i * 128, 128) for i in range(8)], lambda kt: ACC(("t", kt), ACC.t[:, kt, 0:T]), T, evac_pw)

        def evacB(j, pv):
            if j < 8:
                C.copy(ev_eng(), pj(j), pv)
            else:
                C.act(pj(j), pv, AF.Silu)

        stream_mm(I["ev_w_in"], 16, [(3072 + i * 128, 128) for i in range(16)], xrhs, T, evacB)

        Wx = 1 + L

        def xv(buf, a, b, p0=0, p1=32):
            v = buf.t[:, p0:p1, 0:nseq * Wx].rearrange("p q (n w) -> p q n w", w=Wx)[:, :, :, a:b]
            return buf(None, v)

        if is_s:
            for k, (nm, buf) in enumerate((("s_sre", XR), ("s_sim", XI))):
                for q in range(2):
                    for pr in range(16):
                        g0 = 2 * (16 * q + pr)
                        C.dma(S5ST(None, S5ST.t[pr * 8:(pr + 1) * 8, :]),
                              I[nm][s0:s0 + 8, g0:g0 + 2, :].rearrange("n m p -> n (m p)"))
                    ps = next_psb()
                    C.tr(ps(None, ps.t[:, 0:128]), S5ST.all(), ident.all())
                    C.copy("dve", xv(buf, 0, 1, 16 * q, 16 * q + 16),
                           ps(None, ps.t[:, 0:128].rearrange("p (q n o) -> p q n o", n=8, o=1)))
        else:
            for k, buf in enumerate((XR, XI)):
                if tile["first"]:
                    C.memset("pool", xv(buf, 0, 1), 0.0)
                else:
                    C.copy("pool", xv(buf, 0, 1), S5C(None, S5C.t[:, k, :].unsqueeze(2).unsqueeze(3)))
        for q4 in range(8):
            for k, (tab, buf) in enumerate(((BRE, XR), (BIM, XI))):
                ps = next_psa()
                for ip in range(4):
                    hf = ip // 2
                    tb = tab[ip % 2]
                    C.mm(ps(None, ps.t[:, ip * T:(ip + 1) * T]),
                         tb(None, tb.t[64 * hf:64 * hf + 64, q4, :]),
                         PJ(("t", q4), PJ.t[64 * hf:64 * hf + 64, q4, 0:T]))
                C.copy("act", xv(buf, 1, Wx, 4 * q4, 4 * q4 + 4),
                       ps(None, ps.t[:, 0:4 * T].rearrange("p (q n l) -> p q n l", q=4, l=L)))
        arb = S5A(None, S5A.t[:, AR, :].unsqueeze(2).broadcast_to([128, 32, nseq]))
        aib = S5A(None, S5A.t[:, AI, :].unsqueeze(2).broadcast_to([128, 32, nseq]))
        naib = S5A(None, S5A.t[:, NAI, :].unsqueeze(2).broadcast_to([128, 32, nseq]))
        if nseq == 1:
            t1v = S5T(None, S5T.t[:, 8, :].unsqueeze(2))
            t2v = S5T(None, S5T.t[:, 9, :].unsqueeze(2))
        else:
            t1v = SCS(None, SCS.t[:, 0, :, :])
            t2v = SCS(None, SCS.t[:, 1, :, :])

        def col(buf, t):
            v = buf.t[:, :, 0:nseq * Wx].rearrange("p q (n w) -> p q n w", w=Wx)[:, :, :, t]
            return buf(None, v)

        e = "pool"
        for t in range(L):
            C.tt(e, t1v, col(XR, t), arb, ALU.mult)
            C.tt(e, col(XR, t + 1), col(XR, t + 1), t1v, ALU.add)
            C.tt(e, t1v, col(XI, t), naib, ALU.mult)
            C.tt(e, col(XR, t + 1), col(XR, t + 1), t1v, ALU.add)
            C.tt(e, t2v, col(XI, t), arb, ALU.mult)
            C.tt(e, col(XI, t + 1), col(XI, t + 1), t2v, ALU.add)
            C.tt(e, t2v, col(XR, t), aib, ALU.mult)
            C.tt(e, col(XI, t + 1), col(XI, t + 1), t2v, ALU.add)
        if is_s:
            for k, (nm, buf) in enumerate((("s_sre_o", XR), ("s_sim_o", XI))):
                for q in range(2):
                    C.copy("pool", TMP(None, TMP.t[:, 0:128].rearrange("p (q n o) -> p q n o", n=8, o=1)),
                           xv(buf, L, L + 1, 16 * q, 16 * q + 16))
                    ps = next_psb()
                    C.tr(ps(None, ps.t[:, 0:128]), TMP(None, TMP.t[:, 0:128]), ident.all())
                    C.copy("dve", S5ST.all(), ps(None, ps.t[:, 0:128]))
                    for pr in range(16):
                        g0 = 2 * (16 * q + pr)
                        C.dma(O[nm][s0:s0 + 8, g0:g0 + 2, :].rearrange("n m p -> n (m p)"),
                              S5ST(None, S5ST.t[pr * 8:(pr + 1) * 8, :]))
        else:
            for k, buf in enumerate((XR, XI)):
                C.copy("pool", S5C(None, S5C.t[:, k, :].unsqueeze(2).unsqueeze(3)), xv(buf, L, L + 1))
            if tile["last"]:
                for k, nm in enumerate(("p_sre", "p_sim")):
                    ps = next_psb()
                    C.tr(ps(None, ps.t[0:32, 0:128]), S5C(None, S5C.t[:, k, :]), ident.all())
                    C.copy("dve", S5ST(None, S5ST.t[0:32, :]), ps(None, ps.t[0:32, 0:128]))
                    C.dma(O[nm].rearrange("(q m) p -> q (m p)", m=2), S5ST(None, S5ST.t[0:32, :]))
        for gt in range(8):
            ps = next_psa()
            for ip in range(4):
                pair = 4 * gt + ip
                hf = ip // 2
                ov = ps(None, ps.t[64 * hf:64 * hf + 64, 0:T])
                xr_ = XR(None, XR.t[:, pair, 0:nseq * Wx].rearrange("p (n w) -> p n w", w=Wx)[:, :, 1:Wx])
                xi_ = XI(None, XI.t[:, pair, 0:nseq * Wx].rearrange("p (n w) -> p n w", w=Wx)[:, :, 1:Wx])
                cr, ci = CRE[ip % 2], CIM[ip % 2]
                C.mm(ov, cr(None, cr.t[:, gt, 64 * hf:64 * hf + 64]), xr_, start=(ip % 2 == 0), stop=False)
                C.mm(ov, ci(None, ci.t[:, gt, 64 * hf:64 * hf + 64]), xi_, start=False, stop=(ip % 2 == 1))
            gv = pj(gt)
            C.stt("dve", gv, gv, VEC0(None, VEC0.t[:, gt, 34:35]), ps(None, ps.t[:, 0:T]), ALU.mult, ALU.add)
            C.act(gv, gv, AF.Gelu)

        def evac_glu(j, pv):
            if j < 8:
                C.act(ACC(("t", j), ACC.t[:, j, 0:T]), pv, AF.Identity, bias=VECG(None, VECG.t[:, j, 0:1]))
            else:
                jj = j - 8
                tv = TMP(None, TMP.t[:, 0:T])
                C.act(tv, pv, AF.Sigmoid, bias=VECG(None, VECG.t[:, j, 0:1]))
                C.tt("dve", tv, tv, ACC(("t", jj), ACC.t[:, jj, 0:T]), ALU.mult)
                C.tt("dve", mixv(8 + jj, T), tv, pj(8 + jj), ALU.mult)

        stream_mm(I["s5_glu_w"], 8, [(i * 128, 128) for i in range(16)], lambda kt: pj(kt), T, evac_glu)
        out_proj_ln(I["ev_w_out"], tile, VECG, 1, 2)

    do_rwkv = cfg.get("rwkv", True)
    MASKBIG = {"p": C.sb("mbig_p", [128, 128]), "s": C.sb("mbig_s", [64, 64])}
    SEGTRI = {"p": C.sb("stri_p", [128, 128]), "s": C.sb("stri_s", [64, 64])}
    R01 = {"p": C.sb("r01p", [128, 128]), "s": C.sb("r01s", [128, 64])}
    RNEG = {"p": C.sb("rnegp", [128, 128]), "s": C.sb("rnegs", [128, 64])}
    SEL = C.sb("SEL", [8, 8, 128])
    SEGSEL = C.sb("SEGSEL", [8, 64])
    SEGC = C.sb("SEGC", [64, 8])
    SEGROW = C.sb("SEGROW", [128, 8, 64])
    for k in ("p", "s"):
        C.dma(MASKBIG[k].all(), I["maskbig_" + k])
        C.dma(SEGTRI[k].all(), I["segtri_" + k])
        C.dma(R01[k].all(), I["r01_" + k])
        C.dma(RNEG[k].all(), I["rneg_" + k])
    C.dma(SEL.all(), I["sel"].rearrange("k (j m) -> k j m", j=8))
    C.dma(SEGSEL.all(), I["segsel"])
    C.dma(SEGC.all(), I["segc"])
    C.dma(SEGROW.all(), I["segrow"].rearrange("p (n t) -> p n t", n=8))

    VEC1 = C.sb("VEC1", [128, 8, 8])
    VMU = C.sb("VMU", [128, 25, 1])
    VECO = C.sb("VECO", [128, 16, 2])
    GB = C.sb("GB", [8, 1])
    load_rows_T(VEC1, [I[n_] for n_ in ("m_hn_g", "r_w0", "r_a0", "r_kk", "r_ka", "r_ln_g", "r_ln_b", "r_rk")], 1024)
    load_rows_T(VMU, [I["r_mu"][:, 0:2048]], 2048)
    load_rows_T(VMU, [I["r_mu"][:, 2048:3200]], 1152, ct0=16)
    load_rows_T(VECO, [I["od_ln_g"], I["od_ln_b"]], 2048)
    C.dma(GB(None, GB.t[0:4, :]), I["m_ig_b"].rearrange("o h -> h o"), slow=True)
    C.dma(GB(None, GB.t[4:8, :]), I["m_fg_b"].rearrange("o h -> h o"), slow=True)

    VTM = C.sb("VTM", [128, 4, 257])
    KW = C.sb("KW", [128, 1024])
    KWN = C.sb("KWN", [128, 256])
    GX = C.sb("GX", [8, 128])
    ROW = C.sb("ROW", [128, 8, 128])
    COL = C.sb("COL", [128, 64])
    DTB = C.sb("DTB", [128, 128])
    STB = C.sb("STB", [128, 128])
    P1S = C.sb("P1S", [128, 257])
    NUM = C.sb("NUM", [128, 257])
    HN = C.sb("HN", [128, 256])
    SM = C.sb("SM", [128, 16])
    CS = C.sb("CS", [128, 4, 2, 257])
    MCAR = C.sb("MCAR", [128, 4])
    CSS = [C.sb("CSS%d" % i, [128, 2, 257]) for i in range(2)]
    CSO = [C.sb("CSO%d" % i, [128, 2, 257]) for i in range(2)]
    MS = C.sb("MS", [8, 4])
    MSB = C.sb("MSB", [8, 128])
    MINIT = C.sb("MINIT", [128, 4, 8])
    DEC = C.sb("DEC", [128, 4, 8])
    MNEW = C.sb("MNEW", [128, 4, 8])
    QM = [C.sb("QM%d" % i, [128, 64]) for i in range(2)]
    C.memset("pool", VTM(None, VTM.t[:, :, 256:257]), 1.0)

    def stream_mm_tok(W, c0, ncols, T, evac):
        Wv = W.rearrange("(kt p) c -> p kt c", p=128)
        for g in range(ncols // WG):
            ws = WS[rr["ws"] % 2]
            rr["ws"] += 1
            C.dma(ws(None, ws.t[:, :, 0:WG]), Wv[:, :, c0 + g * WG:c0 + (g + 1) * WG])
            ps = next_psa()
            pv = ps(None, ps.t[0:T, 0:WG])
            for kt in range(16):
                C.mm(pv, XT(("t", kt), XT.t[:, kt, 0:T]), ws(None, ws.t[:, kt, 0:WG]), start=(kt == 0), stop=(kt == 15))
            evac(g, pv)

    def recip(e, out, in_):
        return C.op(e, lambda g: g.reciprocal(out=out.ap, in_=in_.ap), reads=[in_], writes=[out])

    def scan(out, d0, d1, init, op0, op1):
        rd = [d0, d1] + ([init] if isinstance(init, View) else [])
        ia = init.ap if isinstance(init, View) else init
        return C.op("dve", lambda g: g.tensor_tensor_scan(out=out.ap, data0=d0.ap, data1=d1.ap, initial=ia, op0=op0, op1=op1),
                    reads=rd, writes=[out])

    SR = C.sb("SR", [128, 8, 64])
    W2A2 = C.sb("W2A2", [128, 1024])
    BLK = C.sb("BLK", [128, 128])
    OMKA = C.sb("OMKA", [128, 8])
    SHC = C.sb("SHC", [128, 25])
    SUMB = C.sb("SUMB", [128, 8])
    C.dma(W2A2(None, W2A2.t[0:64, :]), I["r_w2"])
    C.dma(W2A2(None, W2A2.t[64:128, :]), I["r_a2"])
    C.dma(BLK.all(), I["blk"])
    C.ts("dve", OMKA.all(), VEC1(None, VEC1.t[:, :, 4]), -1.0, 1.0, ALU.mult, ALU.add)
    XIf = XI.t[:, :, :].rearrange("p a b -> p (a b)")
    T1 = XI("T1", XIf[:, 0:512].rearrange("p (j k) -> p j k", k=64))
    T2 = XI("T2", XIf[:, 512:1024].rearrange("p (j k) -> p j k", k=64))
    FSv = XIf[:, 1024:1536].rearrange("p (i t) -> p i t", t=128)
    SRS = [XI(("SRS", i), XIf[:, 1536 + 512 * i:2048 + 512 * i].rearrange("p (j k) -> p j k", k=64)) for i in range(2)]
    TWv = XIf[:, 2560:2688]
    ALLPS = PSA + PSB

    def next_ps8():
        rr["ps8"] = (rr.get("ps8", 0) + 1) % 8
        return ALLPS[rr["ps8"]]

    def rwkv(tile):
        T, nseq, L = tile["T"], tile["nseq"], tile["L"]
        is_s = tile["kind"] == "s"
        s0 = tile["s0"]
        Wx = 1 + L
        xrhs = lambda kt: XT(("t", kt), XT.t[:, kt, 0:T])
        pj = lambda j: PJ(("t", j), PJ.t[:, j, 0:T])
        pj3 = lambda j: PJ(("t", j), PJ.t[:, j, 0:T].rearrange("p (n l) -> p n l", l=L))

        def ppv(j0, j1, a, b):
            v = XR.t[:, j0:j1, 0:nseq * Wx].rearrange("p j (n w) -> p j n w", w=Wx)[:, :, :, a:b]
            return XR(("pp", j0) if j1 == j0 + 1 else None, v)

        if is_s:
            for (c0, ncol, ct0) in ((0, 2048, 0), (2048, 1152, 16)):
                C.dma(XS(None, XS.t[0:8, 0:ncol]), I["s_rsh"][s0:s0 + 8, c0:c0 + ncol])
                for ct in range(ncol // 128):
                    ps = next_psb()
                    C.tr(ps(None, ps.t[:, 0:8]), XS(None, XS.t[0:8, ct * 128:(ct + 1) * 128]), ident(None, ident.t[0:8, 0:8]))
                    C.copy("dve", ppv(ct0 + ct, ct0 + ct + 1, 0, 1), ps(None, ps.t[:, 0:8].rearrange("p (j n o) -> p j n o", j=1, o=1)))
        else:
            if tile["first"]:
                C.memset("pool", ppv(0, 25, 0, 1), 0.0)
            else:
                C.copy("pool", ppv(0, 25, 0, 1), SHC(None, SHC.t[:, :].unsqueeze(2).unsqueeze(3)))

        def evacR(j, pv):
            if j < 25:
                C.copy(ev_eng(), ppv(j, j + 1, 1, Wx), pv.buf(None, pv.ap.rearrange("p (j n l) -> p j n l", j=1, l=L)))
            else:
                C.act(pj(j), pv, AF.Silu)

        stream_mm(I["od_w_in"], 16, [(5128 + i * 128, 128) for i in range(33)], xrhs, T, evacR)

        if is_s:
            for (c0, ncol, ct0) in ((0, 2048, 0), (2048, 1152, 16)):
                for ct in range(ncol // 128):
                    ps = next_psb()
                    C.copy("pool", TMP(None, TMP.t[:, 0:8]), XR(("pp", ct0 + ct), XR.t[:, ct0 + ct, 0:nseq * Wx].rearrange("p (n w) -> p n w", w=Wx)[:, :, L]))
                    C.tr(ps(None, ps.t[0:8, 0:128]), TMP(None, TMP.t[:, 0:8]), ident.all())
                    C.copy("dve", XS(None, XS.t[0:8, ct * 128:(ct + 1) * 128]), ps(None, ps.t[0:8, 0:128]))
                C.dma(O["s_rsh_o"][s0:s0 + 8, c0:c0 + ncol], XS(None, XS.t[0:8, 0:ncol]))
        else:
            C.copy("pool", SHC(None, SHC.t[:, :].unsqueeze(2).unsqueeze(3)), ppv(0, 25, L, L + 1))
            if tile["last"]:
                for (c0, ncol, ct0) in ((0, 2048, 0), (2048, 1152, 16)):
                    for ct in range(ncol // 128):
                        ps = next_psb()
                        C.tr(ps(None, ps.t[0:1, 0:128]), SHC(None, SHC.t[:, ct0 + ct:ct0 + ct + 1]), ident.all())
                        C.copy("dve", XS(None, XS.t[0:1, ct * 128:(ct + 1) * 128]), ps(None, ps.t[0:1, 0:128]))
                    C.dma(O["p_rsh"][:, c0:c0 + ncol], XS(None, XS.t[0:1, 0:ncol]))
        for j in range(25):
            C.tt("pool", pj3(j), XR(("pp", j), XR.t[:, j, 0:nseq * Wx].rearrange("p (n w) -> p n w", w=Wx)[:, :, 0:L]),
                 XR(("pp", j), XR.t[:, j, 0:nseq * Wx].rearrange("p (n w) -> p n w", w=Wx)[:, :, 1:Wx]), ALU.subtract)
            C.stt("dve", pj3(j), pj3(j), VMU(None, VMU.t[:, j, 0:1]),
                  XR(("pp", j), XR.t[:, j, 0:nseq * Wx].rearrange("p (n w) -> p n w", w=Wx)[:, :, 1:Wx]), ALU.mult, ALU.add)

        VTMf = VTM.t[:, :, :].rearrange("p a b -> p (a b)")
        ROWf = ROW.t[:, :, :].rearrange("p a b -> p (a b)")
        KKt = lambda a, b: KW(None, KW.t[0:T, a:b])
        Wt = lambda a, b: VTM(None, VTMf[0:T, a:b])
        KKAt = lambda a, b: ROW(None, ROWf[0:T, a:b])
        KPt = lambda a, b: XS(None, XS.t[0:T, a:b])
        Rt = lambda a, b: XS(None, XS.t[0:T, 1024 + a:1024 + b])
        fs = lambda i: XI(("FS", i), FSv[:, i, 0:T])
        tw = XI("TW", TWv[0:64, 0:T])
        C.act(tw, PJ(("t", 24), PJ.t[0:64, 24, 0:T]), AF.Tanh)

        def to_tok(dst, src):
            ps = next_psb()
            C.tr(ps(None, ps.t[0:T, 0:128]), src, ident.all())
            C.copy(ev_eng(), dst, ps(None, ps.t[0:T, 0:128]))

        NE05 = -float(np.exp(-0.5))
        for ct in range(8):
            r_, k_, v_ = pj(ct), pj(8 + ct), pj(16 + ct)
            cs_ = slice(ct * 128, (ct + 1) * 128)
            ps = next_psa()
            C.mm(ps(None, ps.t[:, 0:T]), W2A2(None, W2A2.t[0:64, cs_]), tw)
            C.act(fs(0), ps(None, ps.t[:, 0:T]), AF.Sigmoid, bias=VEC1(None, VEC1.t[:, ct, 1:2]))
            C.act(fs(0), fs(0), AF.Exp, scale=NE05)
            to_tok(Wt(ct * 128, (ct + 1) * 128), fs(0))
            ps = next_psa()
            C.mm(ps(None, ps.t[:, 0:T]), W2A2(None, W2A2.t[64:128, cs_]), PJ(("t", 24), PJ.t[64:128, 24, 0:T]))
            C.act(fs(1), ps(None, ps.t[:, 0:T]), AF.Sigmoid, bias=VEC1(None, VEC1.t[:, ct, 2:3]))
            C.ts("dve", fs(2), k_, VEC1(None, VEC1.t[:, ct, 3:4]), None, ALU.mult)
            C.tt("pool", fs(3), fs(2), fs(2), ALU.mult)
            ps = next_psa()
            C.mm(ps(None, ps.t[:, 0:T]), BLK.all(), fs(3))
            C.act(fs(3), ps(None, ps.t[:, 0:T]), AF.Sqrt)
            C.ts("dve", fs(3), fs(3), 1e-12, None, ALU.max)
            recip("dve", fs(3), fs(3))
            C.tt("dve", fs(2), fs(2), fs(3), ALU.mult)
            to_tok(KKt(ct * 128, (ct + 1) * 128), fs(2))
            C.tt("dve", fs(3), fs(2), fs(1), ALU.mult)
            to_tok(KKAt(ct * 128, (ct + 1) * 128), fs(3))
            C.ts("dve", fs(1), fs(1), VEC1(None, VEC1.t[:, ct, 4:5]), OMKA(None, OMKA.t[:, ct:ct + 1]), ALU.mult, ALU.add)
            C.tt("dve", fs(1), fs(1), k_, ALU.mult)
            to_tok(KPt(ct * 128, (ct + 1) * 128), fs(1))
            to_tok(Rt(ct * 128, (ct + 1) * 128), r_)
            C.tt("dve", fs(3), r_, fs(1), ALU.mult)
            C.ts("dve", fs(3), fs(3), VEC1(None, VEC1.t[:, ct, 7:8]), None, ALU.mult)
            ps = next_psa()
            C.mm(ps(None, ps.t[:, 0:T]), BLK.all(), fs(3))
            C.tt("dve", ACC(("t", ct), ACC.t[:, ct, 0:T]), ps(None, ps.t[:, 0:T]), v_, ALU.mult)

        Yv = MIX.t[:, 8 * TT:16 * TT].rearrange("p (j t) -> p j t", j=8)
        srcs = (("kk", KW.t[0:T, :]), ("w", VTMf[0:T, 0:1024]), ("kka", ROWf[0:T, 0:1024]), ("kp", XS.t[0:T, 0:1024]), ("r", XS.t[0:T, 1024:2048]))
        bufs = {"kk": KW, "w": VTM, "kka": ROW, "kp": XS, "r": XS}
        for n in range(nseq):
            if is_s:
                sr = SRS[n % 2]
                C.dma(sr, I["s_rs"][s0 + n].rearrange("(j hp) v k -> (hp v) j k", hp=2))
            else:
                sr = SR.all()
                if tile["first"]:
                    C.memset("pool", sr, 0.0)
            for l in range(L):
                t = n * L + l
                oh = ident(None, ident.t[0:T, t:t + 1].broadcast_to([T, 64]))
                bc = {}
                for nm, ap in srcs:
                    ps = next_ps8()
                    xv = ap.rearrange("p (j hp k) -> p hp j k", hp=2, k=64)
                    C.mm(ps(None, ps.t[0:64, 0:512]), oh, bufs[nm](None, xv[:, 0]))
                    C.mm(ps(None, ps.t[64:128, 0:512]), oh, bufs[nm](None, xv[:, 1]))
                    bc[nm] = ps(None, ps.t[:, 0:512].rearrange("p (j k) -> p j k", k=64))
                C.tt("dve", T1, sr, bc["kk"], ALU.mult)
                C.op("dve", lambda g: g.reduce_sum(out=SUMB.t[:, :], in_=T1.ap, axis=AX.X), reads=[T1], writes=[SUMB.all()])
                C.tt("dve", sr, sr, bc["w"], ALU.mult)
                C.tt("dve", T2, bc["kka"], SUMB(None, SUMB.t[:, :].unsqueeze(2).broadcast_to([128, 8, 64])), ALU.mult)
                C.tt("dve", sr, sr, T2, ALU.subtract)
                C.tt("dve", T1, bc["kp"], PJ(None, PJ.t[:, 16:24, t].unsqueeze(2).broadcast_to([128, 8, 64])), ALU.mult)
                C.tt("dve", sr, sr, T1, ALU.add)
                C.tt("dve", T2, sr, bc["r"], ALU.mult)
                yv = MIX(None, Yv[:, :, t])
                C.op("dve", lambda g, yv=yv: g.reduce_sum(out=yv.ap, in_=T2.ap, axis=AX.X), reads=[T2], writes=[yv])
            if is_s:
                C.dma(O["s_rs_o"][s0 + n].rearrange("(j hp) v k -> (hp v) j k", hp=2), sr)
        if (not is_s) and tile["last"]:
            C.dma(O["p_rs"].rearrange("(j hp) v k -> (hp v) j k", hp=2), SR.all())

        for j in range(8):
            y = mixv(8 + j, T)
            ps = next_psa()
            C.mm(ps(None, ps.t[:, 0:T]), BLK.all(), y)
            C.tt("pool", fs(0), y, y, ALU.mult)
            ps2 = next_psa()
            C.mm(ps2(None, ps2.t[:, 0:T]), BLK.all(), fs(0))
            C.ts("dve", fs(1), ps(None, ps.t[:, 0:T]), 1.0 / 64, None, ALU.mult)
            C.ts("dve", fs(2), ps2(None, ps2.t[:, 0:T]), 1.0 / 64, None, ALU.mult)
            C.tt("dve", fs(3), fs(1), fs(1), ALU.mult)
            C.tt("dve", fs(2), fs(2), fs(3), ALU.subtract)
            C.ts("dve", fs(2), fs(2), 64e-5, None, ALU.add)
            C.act(fs(2), fs(2), AF.Sqrt)
            recip("dve", fs(2), fs(2))
            C.tt("dve", y, y, fs(1), ALU.subtract)
            C.tt("dve", y, y, fs(2), ALU.mult)
            C.act(y, y, AF.Identity, bias=VEC1(None, VEC1.t[:, j, 6:7]), scale=VEC1(None, VEC1.t[:, j, 5:6]))
            C.tt("dve", y, y, ACC(("t", j), ACC.t[:, j, 0:T]), ALU.add)
            C.tt("dve", y, y, pj(25 + j), ALU.mult)

    SSTRI = {"p": C.sb("sstri_p", [128, 128]), "s": C.sb("sstri_s", [64, 64])}
    SSTRIT = {"p": C.sb("sstriT_p", [128, 128]), "s": C.sb("sstriT_s", [64, 64])}
    for k_ in ("p", "s"):
        C.dma(SSTRI[k_].all(), I["sstri_" + k_])
        C.dma(SSTRIT[k_].all(), I["sstriT_" + k_])
    WLB = C.sb("WLB", [128, 8, 8])
    HBf = HB.t[:, :, :].rearrange("p a b -> p (a b)")
    KKv = KW.t[:, :].rearrange("p (j t) -> p j t", t=128)
    BTv = ROW.t
    XRf = XR.t[:, :, :].rearrange("p a b -> p (a b)")
    S0Tv = XRf[:, 0:4096].rearrange("p (n j v) -> p n j v", n=8, j=8)

    def fence(buf, ap):
        C.op("pool", lambda g: g.memset(ap, 0.0), writes=[buf.all()])

    def rwkv2(tile):
        T, nseq, L = tile["T"], tile["nseq"], tile["L"]
        is_s = tile["kind"] == "s"
        kd = tile["kind"]
        s0 = tile["s0"]
        Wx = 1 + L
        xrhs = lambda kt: XT(("t", kt), XT.t[:, kt, 0:T])
        pj = lambda j: PJ(("t", j), PJ.t[:, j, 0:T])
        pj3 = lambda j: PJ(("t", j), PJ.t[:, j, 0:T].rearrange("p (n l) -> p n l", l=L))

        def ppv(j0, j1, a, b):
            v = XR.t[:, j0:j1, 0:nseq * Wx].rearrange("p j (n w) -> p j n w", w=Wx)[:, :, :, a:b]
            return XR(("pp", j0) if j1 == j0 + 1 else None, v)

        if is_s:
            for (c0, ncol, ct0) in ((0, 2048, 0), (2048, 1152, 16)):
                C.dma(XS(None, XS.t[0:8, 0:ncol]), I["s_rsh"][s0:s0 + 8, c0:c0 + ncol])
                for ct in range(ncol // 128):
                    ps = next_psb()
                    C.tr(ps(None, ps.t[:, 0:8]), XS(None, XS.t[0:8, ct * 128:(ct + 1) * 128]), ident(None, ident.t[0:8, 0:8]))
                    C.copy("dve", ppv(ct0 + ct, ct0 + ct + 1, 0, 1), ps(None, ps.t[:, 0:8].rearrange("p (j n o) -> p j n o", j=1, o=1)))
        else:
            if tile["first"]:
                C.memset("pool", ppv(0, 25, 0, 1), 0.0)
            else:
                C.copy("pool", ppv(0, 25, 0, 1), SHC(None, SHC.t[:, :].unsqueeze(2).unsqueeze(3)))

        def evacR(j, pv):
            if j < 25:
                C.copy(ev_eng(), ppv(j, j + 1, 1, Wx), pv.buf(None, pv.ap.rearrange("p (j n l) -> p j n l", j=1, l=L)))
            else:
                C.act(pj(j), pv, AF.Silu)

        stream_mm(I["od_w_in"], 16, [(5128 + i * 128, 128) for i in range(33)], xrhs, T, evacR)

        if is_s:
            for (c0, ncol, ct0) in ((0, 2048, 0), (2048, 1152, 16)):
                for ct in range(ncol // 128):
                    ps = next_psb()
                    C.copy("pool", TMP(None, TMP.t[:, 0:8]), XR(("pp", ct0 + ct), XR.t[:, ct0 + ct, 0:nseq * Wx].rearrange("p (n w) -> p n w", w=Wx)[:, :, L]))
                    C.tr(ps(None, ps.t[0:8, 0:128]), TMP(None, TMP.t[:, 0:8]), ident.all())
                    C.copy("dve", XS(None, XS.t[0:8, ct * 128:(ct + 1) * 128]), ps(None, ps.t[0:8, 0:128]))
                C.dma(O["s_rsh_o"][s0:s0 + 8, c0:c0 + ncol], XS(None, XS.t[0:8, 0:ncol]))
        else:
            C.copy("pool", SHC(None, SHC.t[:, :].unsqueeze(2).unsqueeze(3)), ppv(0, 25, L, L + 1))
            if tile["last"]:
                for (c0, ncol, ct0) in ((0, 2048, 0), (2048, 1152, 16)):
                    for ct in range(ncol // 128):
                        ps = next_psb()
                        C.tr(ps(None, ps.t[0:1, 0:128]), SHC(None, SHC.t[:, ct0 + ct:ct0 + ct + 1]), ident.all())
                        C.copy("dve", XS(None, XS.t[0:1, ct * 128:(ct + 1) * 128]), ps(None, ps.t[0:1, 0:128]))
                    C.dma(O["p_rsh"][:, c0:c0 + ncol], XS(None, XS.t[0:1, 0:ncol]))
        for j in range(25):
            C.tt("pool", pj3(j), XR(("pp", j), XR.t[:, j, 0:nseq * Wx].rearrange("p (n w) -> p n w", w=Wx)[:, :, 0:L]),
                 XR(("pp", j), XR.t[:, j, 0:nseq * Wx].rearrange("p (n w) -> p n w", w=Wx)[:, :, 1:Wx]), ALU.subtract)
            C.stt("dve", pj3(j), pj3(j), VMU(None, VMU.t[:, j, 0:1]),
                  XR(("pp", j), XR.t[:, j, 0:nseq * Wx].rearrange("p (n w) -> p n w", w=Wx)[:, :, 1:Wx]), ALU.mult, ALU.add)

        s0t = lambda n, j, rs=slice(0, 128): XR(("st", n), S0Tv[rs, n, j, :])
        if is_s:
            fence(XR, XR.t[0:1, 0, 0:1])
            for n2 in range(0, 8, 2):
                stg = XS.t[:, :].rearrange("p (n j d k) -> p n j d k", n=2, j=8, d=2)
                for nn in range(2):
                    for d in range(2):
                        C.dma(XS(None, stg[:, nn, :, d, :]), I["s_rs"][s0 + n2 + nn].rearrange("(j hp) v k -> (hp v) j k", hp=2))
                for nn in range(2):
                    n = n2 + nn
                    for j in range(8):
                        ps = next_psb()
                        C.tr(ps(None, ps.t[:, 0:128]), XS(None, stg[:, nn, j, :, :]), ident.all())
                        C.copy("dve", s0t(n, j, slice(0, 64)), ps(None, ps.t[0:64, 0:64]))
                        C.copy("act", s0t(n, j, slice(64, 128)), ps(None, ps.t[64:128, 64:128]))
        else:
            if tile["first"]:
                C.memset("pool", SR.all(), 0.0)

        VTMf = VTM.t[:, :, :].rearrange("p a b -> p (a b)")
        Vtm = lambda hc: XS(None, XS.t[0:T, hc])
        Btm = lambda hc: XS(None, XS.t[0:T, 1024 + hc.start:1024 + hc.stop])
        Ktm = lambda hc: VTM(None, VTMf[0:T, hc])
        fs = lambda i: XI(("FS", i), FSv[:, i, 0:T])
        tw = XI("TW", TWv[0:64, 0:T])
        C.act(tw, PJ(("t", 24), PJ.t[0:64, 24, 0:T]), AF.Tanh)

        def to_tok(dst, src):
            ps = next_psb()
            C.tr(ps(None, ps.t[0:T, 0:128]), src, ident.all())
            C.copy(ev_eng(), dst, ps(None, ps.t[0:T, 0:128]))

        NE05 = -float(np.exp(-0.5))
        kkc = lambda ct, rs=slice(0, 128): KW(("c", ct), KKv[rs, ct, 0:T])
        btc = lambda ct, rs=slice(0, 128): ROW(("c", ct), BTv[rs, ct, 0:T])
        for ct in range(8):
            r_, k_, v_ = pj(ct), pj(8 + ct), pj(16 + ct)
            cs_ = slice(ct * 128, (ct + 1) * 128)
            ps = next_psa()
            C.mm(ps(None, ps.t[:, 0:T]), W2A2(None, W2A2.t[0:64, cs_]), tw)
            C.act(fs(0), ps(None, ps.t[:, 0:T]), AF.Sigmoid, bias=VEC1(None, VEC1.t[:, ct, 1:2]))
            C.ts("dve", fs(0), fs(0), NE05, None, ALU.mult)
            scan(fs(1), R01[kd](None, R01[kd].t[:, 0:T]), fs(0), 0.0, ALU.mult, ALU.add)
            ps = next_psa()
            C.mm(ps(None, ps.t[:, 0:T]), W2A2(None, W2A2.t[64:128, cs_]), PJ(("t", 24), PJ.t[64:128, 24, 0:T]))
            C.act(fs(2), ps(None, ps.t[:, 0:T]), AF.Sigmoid, bias=VEC1(None, VEC1.t[:, ct, 2:3]))
            C.ts("dve", kkc(ct), k_, VEC1(None, VEC1.t[:, ct, 3:4]), None, ALU.mult)
            C.tt("pool", fs(3), kkc(ct), kkc(ct), ALU.mult)
            ps = next_psa()
            C.mm(ps(None, ps.t[:, 0:T]), BLK.all(), fs(3))
            C.act(fs(3), ps(None, ps.t[:, 0:T]), AF.Sqrt)
            C.ts("dve", fs(3), fs(3), 1e-12, None, ALU.max)
            recip("dve", fs(3), fs(3))
            C.tt("dve", kkc(ct), kkc(ct), fs(3), ALU.mult)
            C.tt("dve", btc(ct), kkc(ct), fs(2), ALU.mult)
            C.ts("dve", fs(2), fs(2), VEC1(None, VEC1.t[:, ct, 4:5]), OMKA(None, OMKA.t[:, ct:ct + 1]), ALU.mult, ALU.add)
            C.tt("dve", fs(2), fs(2), k_, ALU.mult)
            C.tt("pool", fs(3), r_, fs(2), ALU.mult)
            C.ts("dve", fs(3), fs(3), VEC1(None, VEC1.t[:, ct, 7:8]), None, ALU.mult)
            ps = next_psa()
            C.mm(ps(None, ps.t[:, 0:T]), BLK.all(), fs(3))
            C.tt("dve", ACC(("t", ct), ACC.t[:, ct, 0:T]), ps(None, ps.t[:, 0:T]), v_, ALU.mult)
            C.act(fs(3), fs(1), AF.Exp)
            C.tt("dve", r_, r_, fs(3), ALU.mult)
            C.copy("pool", WLB(None, WLB.t[:, ct, 0:nseq]),
                   XI(("FS", 3), FSv[:, 3, 0:T].rearrange("p (n l) -> p n l", l=L)[:, :, L - 1]))
            C.tt("dve", fs(3), fs(1), fs(0), ALU.subtract)
            C.act(fs(3), fs(3), AF.Exp)
            C.tt("dve", kkc(ct), kkc(ct), fs(3), ALU.mult)
            C.act(fs(3), fs(1), AF.Exp, scale=-1.0)
            C.tt("dve", btc(ct), btc(ct), fs(3), ALU.mult)
            C.tt("dve", k_, fs(2), fs(3), ALU.mult)
            to_tok(Vtm(cs_), v_)
            to_tok(Btm(cs_), btc(ct))
            to_tok(Ktm(cs_), k_)

        fence(HB, HB.t[0:1, 0, 0:1])
        mat = lambda i: HB(("m", i), HBf[0:T, i * 128:i * 128 + T])
        half = lambda i, a: HB(("m", i), HBf[0:T, i * 128 + 64 * a:i * 128 + 64 * a + 64])
        nsq = max(0, int(np.ceil(np.log2(L))) - 1)
        idT = ident(None, ident.t[0:T, 0:T])
        mS = SSTRI[kd](None, SSTRI[kd].t[0:T, 0:T])
        mST = SSTRIT[kd](None, SSTRIT[kd].t[0:T, 0:T])
        mI = SEGTRI[kd](None, SEGTRI[kd].t[0:T, 0:T])
        if is_s:
            KKM, RM = T1, T2
        for j in range(8):
            Q = [[mat(8 * hp + 0), mat(8 * hp + 1)] for hp in range(2)]
            QT = [[mat(8 * hp + 2), mat(8 * hp + 3)] for hp in range(2)]
            Pm = [mat(8 * hp + 4) for hp in range(2)]
            BR = [mat(8 * hp + 5) for hp in range(2)]
            AK = [mat(8 * hp + 6) for hp in range(2)]
            KR = [mat(8 * hp + 7) for hp in range(2)]
            RHS = [half(16, hp) for hp in range(2)]
            SAT = [half(17, hp) for hp in range(2)]
            rsl = [slice(0, 64), slice(64, 128)]
            if is_s:
                C.tt("pool", KKM, KW(("c", j), KKv[:, j, 0:T].unsqueeze(1).broadcast_to([128, 8, T])), SEGROW(None, SEGROW.t[:, :, 0:T]), ALU.mult)
                C.tt("pool", RM, PJ(("t", j), PJ.t[:, j, 0:T].unsqueeze(1).broadcast_to([128, 8, T])), SEGROW(None, SEGROW.t[:, :, 0:T]), ALU.mult)
            for hp in range(2):
                rs = rsl[hp]
                rq = PJ(("t", j), PJ.t[rs, j, 0:T])
                kq = PJ(("t", 8 + j), PJ.t[rs, 8 + j, 0:T])
                ps = next_ps8()
                C.mm(ps(None, ps.t[0:T, 0:T]), btc(j, rs), kkc(j, rs))
                C.mm(ps(None, ps.t[0:T, T:2 * T]), btc(j, rs), rq)
                C.tt("dve", Q[hp][0], ps(None, ps.t[0:T, 0:T]), mS, ALU.mult)
                C.tt("dve", BR[hp], ps(None, ps.t[0:T, T:2 * T]), mI, ALU.mult)
                ps = next_ps8()
                C.mm(ps(None, ps.t[0:T, 0:T]), kq, kkc(j, rs))
                C.mm(ps(None, ps.t[0:T, T:2 * T]), kq, rq)
                C.tt("dve", AK[hp], ps(None, ps.t[0:T, 0:T]), mS, ALU.mult)
                C.tt("dve", KR[hp], ps(None, ps.t[0:T, T:2 * T]), mI, ALU.mult)
                ps = next_ps8()
                C.mm(ps(None, ps.t[0:T, 0:T]), kkc(j, rs), btc(j, rs))
                C.tt("dve", QT[hp][0], ps(None, ps.t[0:T, 0:T]), mST, ALU.mult)
                C.stt("dve", Pm[hp], Q[hp][0], -1.0, idT, ALU.mult, ALU.add)
            cur = 0
            for it in range(nsq):
                nxt = 1 - cur
                last_it = (it == nsq - 1)
                for hp in range(2):
                    if not last_it:
                        ps = next_ps8()
                        C.mm(ps(None, ps.t[0:T, 0:T]), QT[hp][cur], Q[hp][cur])
                        C.copy("act", Q[hp][nxt], ps(None, ps.t[0:T, 0:T]))
                    ps = next_ps8()
                    C.mm(ps(None, ps.t[0:T, 0:T]), Q[hp][cur], QT[hp][cur])
                    C.copy("dve", QT[hp][nxt], ps(None, ps.t[0:T, 0:T]))
                for hp in range(2):
                    ps = next_ps8()
                    C.mm(ps(None, ps.t[0:T, 0:T]), QT[hp][nxt], Pm[hp])
                    C.tt("dve", Pm[hp], Pm[hp], ps(None, ps.t[0:T, 0:T]), ALU.add)
                cur = nxt
            for hp in range(2):
                rs = rsl[hp]
                hc = slice((2 * j + hp) * 64, (2 * j + hp) * 64 + 64)
                ps = next_ps8()
                if is_s:
                    for n in range(nseq):
                        C.mm(ps(None, ps.t[0:T, 0:64]), XI("T1", KKM.ap[rs, n, :]), s0t(n, j, rs), start=(n == 0), stop=False)
                else:
                    C.mm(ps(None, ps.t[0:T, 0:64]), kkc(j, rs), SR(None, SR.t[rs, j, :]), start=True, stop=False)
                C.mm(ps(None, ps.t[0:T, 0:64]), AK[hp], Vtm(hc), start=False, stop=True)
                C.act(RHS[hp], ps(None, ps.t[0:T, 0:64]), AF.Identity, scale=-1.0)
                ps = next_ps8()
                C.mm(ps(None, ps.t[0:T, 0:64]), Pm[hp], RHS[hp])
                C.copy("dve", SAT[hp], ps(None, ps.t[0:T, 0:64]))
            psY = next_ps8()
            for hp in range(2):
                rs = rsl[hp]
                hc = slice((2 * j + hp) * 64, (2 * j + hp) * 64 + 64)
                ov = psY(None, psY.t[rs, 0:T])
                if is_s:
                    for n in range(nseq):
                        C.mm(ov, s0t(n, j, rs), XI("T2", RM.ap[rs, n, :]), start=(n == 0), stop=False)
                else:
                    C.mm(ov, SR(None, SR.t[rs, j, :]), PJ(("t", j), PJ.t[rs, j, 0:T]), start=True, stop=False)
                C.mm(ov, SAT[hp], BR[hp], start=False, stop=False)
                C.mm(ov, Vtm(hc), KR[hp], start=False, stop=True)
            C.copy("act", mixv(8 + j, T), psY(None, psY.t[:, 0:T]))
            tmpS = XI(("FS", 0), FSv[:, 0, 0:64])
            if is_s:
                SAM = [CSS[hp](None, CSS[hp].t[0:T, :, :].rearrange("p a b -> p (a b)")[:, 0:512].rearrange("p (n v) -> p n v", n=8)) for hp in range(2)]
                VM = [CSO[hp](None, CSO[hp].t[0:T, :, :].rearrange("p a b -> p (a b)")[:, 0:512].rearrange("p (n v) -> p n v", n=8)) for hp in range(2)]
                segc_b = SEGC(None, SEGC.t[0:T, :].unsqueeze(2).broadcast_to([T, 8, 64]))
                for hp in range(2):
                    hc = slice((2 * j + hp) * 64, (2 * j + hp) * 64 + 64)
                    C.tt("pool", SAM[hp], HB(("m", 17), HBf[0:T, 17 * 128 + 64 * hp:17 * 128 + 64 * hp + 64].unsqueeze(1).broadcast_to([T, 8, 64])), segc_b, ALU.mult)
                    C.tt("pool", VM[hp], XS(None, XS.t[0:T, hc].unsqueeze(1).broadcast_to([T, 8, 64])), segc_b, ALU.mult)
                for n in range(nseq):
                    psS = next_ps8()
                    for hp in range(2):
                        rs = rsl[hp]
                        hc = slice((2 * j + hp) * 64, (2 * j + hp) * 64 + 64)
                        C.mm(psS(None, psS.t[rs, 0:64]), Btm(hc), CSS[hp](None, SAM[hp].ap[:, n, :]), start=True, stop=False)
                        C.mm(psS(None, psS.t[rs, 0:64]), Ktm(hc), CSO[hp](None, VM[hp].ap[:, n, :]), start=False, stop=True)
                    wl = WLB(None, WLB.t[:, j, n:n + 1])
                    C.ts("dve", tmpS, s0t(n, j), wl, None, ALU.mult)
                    C.stt("dve", s0t(n, j), psS(None, psS.t[:, 0:64]), wl, tmpS, ALU.mult, ALU.add)
            else:
                psS = next_ps8()
                for hp in range(2):
                    rs = rsl[hp]
                    hc = slice((2 * j + hp) * 64, (2 * j + hp) * 64 + 64)
                    C.mm(psS(None, psS.t[rs, 0:64]), Btm(hc), SAT[hp], start=True, stop=False)
                    C.mm(psS(None, psS.t[rs, 0:64]), Ktm(hc), Vtm(hc), start=False, stop=True)
                wl = WLB(None, WLB.t[:, j, 0:1])
                srj = SR(None, SR.t[:, j, :])
                C.ts("dve", tmpS, srj, wl, None, ALU.mult)
                C.stt("dve", srj, psS(None, psS.t[:, 0:64]), wl, tmpS, ALU.mult, ALU.add)

        def state_out(src_fn, dst):
            for j in range(8):
                ps = next_psb()
                C.tr(ps(None, ps.t[0:64, 0:128]), src_fn(j), ident.all())
                C.copy(ev_eng(), XS(None, XS.t[0:64, j * 128:(j + 1) * 128]), ps(None, ps.t[0:64, 0:128]))
            C.dma(dst.rearrange("(j hp) v k -> v j hp k", hp=2), XS(None, XS.t[0:64, 0:1024].rearrange("p (j hp k) -> p j hp k", j=8, hp=2)))

        if is_s:
            for n in range(nseq):
                state_out(lambda j, n=n: s0t(n, j), O["s_rs_o"][s0 + n])
        elif tile["last"]:
            state_out(lambda j: SR(None, SR.t[:, j, :]), O["p_rs"])

        for j in range(8):
            y = mixv(8 + j, T)
            ps = next_psa()
            C.mm(ps(None, ps.t[:, 0:T]), BLK.all(), y)
            C.tt("pool", fs(0), y, y, ALU.mult)
            ps2 = next_psa()
            C.mm(ps2(None, ps2.t[:, 0:T]), BLK.all(), fs(0))
            C.ts("dve", fs(1), ps(None, ps.t[:, 0:T]), 1.0 / 64, None, ALU.mult)
            C.ts("dve", fs(2), ps2(None, ps2.t[:, 0:T]), 1.0 / 64, None, ALU.mult)
            C.tt("dve", fs(3), fs(1), fs(1), ALU.mult)
            C.tt("dve", fs(2), fs(2), fs(3), ALU.subtract)
            C.ts("dve", fs(2), fs(2), 64e-5, None, ALU.add)
            C.act(fs(2), fs(2), AF.Sqrt)
            recip("dve", fs(2), fs(2))
            C.tt("dve", y, y, fs(1), ALU.subtract)
            C.tt("dve", y, y, fs(2), ALU.mult)
            C.act(y, y, AF.Identity, bias=VEC1(None, VEC1.t[:, j, 6:7]), scale=VEC1(None, VEC1.t[:, j, 5:6]))
            C.tt("dve", y, y, ACC(("t", j), ACC.t[:, j, 0:T]), ALU.add)
            C.tt("dve", y, y, pj(25 + j), ALU.mult)

    def layer1(tile):
        T, nseq, L = tile["T"], tile["nseq"], tile["L"]
        is_s = tile["kind"] == "s"
        kd = tile["kind"]
        s0 = tile["s0"]
        xrhs = lambda kt: XT(("t", kt), XT.t[:, kt, 0:T])
        pj = lambda j: PJ(("t", j), PJ.t[:, j, 0:T])
        W = I["od_w_in"]

        def evacM(j, pv):
            if j < 8:
                C.act(pj(j), pv, AF.Identity, scale=1.0 / 16.0)
            elif j < 16:
                C.copy(ev_eng(), pj(j), pv)
            elif j < 24:
                C.act(pj(j), pv, AF.Sigmoid)
            elif j == 24:
                C.copy("dve", PJ(("t", 32), PJ.t[0:8, 32, 0:T]), pv)
            else:
                C.act(pj(j - 1), pv, AF.Silu)

        cols = [(i * 128, 128) for i in range(16)] + [(3072 + i * 128, 128) for i in range(8)] + [(4096, 8)] + \
               [(4104 + i * 128, 128) for i in range(8)]
        stream_mm(W, 16, cols, xrhs, T, evacM)

        def evacV(g, pv):
            hh, half = divmod(g, 256 // WG)
            C.copy(ev_eng(), VTM(None, VTM.t[0:T, hh, half * WG:(half + 1) * WG]), pv)

        C.memset("pool", VTM(None, VTM.t[:, :, 256:257]), 1.0)
        stream_mm_tok(W, 2048, 1024, T, evacV)

        gx = GX(None, GX.t[0:8, 0:T])
        C.ts("dve", gx, PJ(("t", 32), PJ.t[0:8, 32, 0:T]), GB.all(), None, ALU.add)
        col = lambda a, b: COL(None, COL.t[0:T, a:b])
        ps = next_psb()
        C.tr(ps(None, ps.t[0:T, 0:8]), gx, ident(None, ident.t[0:8, 0:8]))
        C.copy("dve", col(0, 8), ps(None, ps.t[0:T, 0:8]))
        C.act(col(8, 12), col(4, 8), AF.Exp, scale=-1.0)
        C.act(col(8, 12), col(8, 12), AF.Ln, bias=1.0)
        C.ts("dve", col(8, 12), col(8, 12), -1.0, None, ALU.mult)
        ps = next_psb()
        C.mm(ps(None, ps.t[0:T, 0:4]), SEGTRI[kd](None, SEGTRI[kd].t[0:T, 0:T]), col(8, 12))
        C.copy("dve", col(12, 16), ps(None, ps.t[0:T, 0:4]))
        C.tt("dve", col(16, 20), col(0, 4), col(12, 16), ALU.subtract)
        if is_s:
            C.dma(MS.all(), I["s_mm"][s0:s0 + 8, :])
            ps = next_psb()
            C.mm(ps(None, ps.t[0:T, 0:4]), SEGSEL(None, SEGSEL.t[0:8, 0:T]), MS.all())
            C.copy("dve", col(20, 24), ps(None, ps.t[0:T, 0:4]))
            for h in range(4):
                C.copy("dve", MSB.all(), MS(None, MS.t[:, h:h + 1].broadcast_to([8, 128])))
                ps = next_psb()
                C.mm(ps(None, ps.t[:, 0:8]), MSB.all(), ident(None, ident.t[0:8, 0:8]))
                C.copy("dve", MINIT(None, MINIT.t[:, h, :]), ps(None, ps.t[:, 0:8]))
        else:
            if tile["first"]:
                C.memset("dve", MCAR.all(), 0.0)
                C.memset("pool", CS.all(), 0.0)
            C.copy("dve", col(20, 24), MCAR(None, MCAR.t[0:T, :]))
            C.copy("dve", MINIT(None, MINIT.t[:, :, 0:1]), MCAR(None, MCAR.t[:, :].unsqueeze(2)))

        row = lambda k: ROW(None, ROW.t[:, k, 0:T])
        ends = lambda k: ROW(None, ROW.t[:, k, 0:T].rearrange("p (n l) -> p n l", l=L)[:, :, L - 1])
        starts = lambda k: ROW(None, ROW.t[:, k, 0:T].rearrange("p (n l) -> p n l", l=L)[:, :, 0])
        for h in range(4):
            minit_r = MINIT(None, MINIT.t[:, h, 0:nseq])
            ps = next_psb()
            C.mm(ps(None, ps.t[:, 0:T]), SEL(None, SEL.t[0:8, h, :]), gx)
            C.copy("dve", row(0), ps(None, ps.t[:, 0:T]))
            ps = next_psb()
            C.mm(ps(None, ps.t[:, 0:T]), SEL(None, SEL.t[0:8, 4 + h, :]), gx)
            C.act(row(1), ps(None, ps.t[:, 0:T]), AF.Exp, scale=-1.0)
            C.act(row(1), row(1), AF.Ln, bias=1.0)
            C.ts("dve", row(1), row(1), -1.0, None, ALU.mult)
            scan(row(2), R01[kd](None, R01[kd].t[:, 0:T]), row(1), 0.0, ALU.mult, ALU.add)
            C.tt("dve", row(3), row(0), row(2), ALU.subtract)
            C.tt("dve", starts(3), starts(3), minit_r, ALU.max)
            scan(row(4), RNEG[kd](None, RNEG[kd].t[:, 0:T]), row(3), -1e30, ALU.add, ALU.max)
            C.copy("dve", ROW(None, ROW.t[:, 5, 0:T].rearrange("p (n l) -> p n l", l=L)),
                   ROW(None, ROW.t[:, 4, 0:T].rearrange("p (n l) -> p n l", l=L)[:, :, L - 1:L].broadcast_to([128, nseq, L])))
            C.tt("dve", MNEW(None, MNEW.t[:, h, 0:nseq]), ends(2), ends(4), ALU.add)
            C.tt("dve", DEC(None, DEC.t[:, h, 0:nseq]), minit_r, ends(4), ALU.subtract)
            C.act(DEC(None, DEC.t[:, h, 0:nseq]), DEC(None, DEC.t[:, h, 0:nseq]), AF.Exp)
            ps = next_psb()
            C.mm(ps(None, ps.t[0:T, 0:1]), row(4), ident(None, ident.t[:, 0:1]))
            C.mm(ps(None, ps.t[0:T, 1:2]), row(5), ident(None, ident.t[:, 0:1]))
            C.copy("dve", col(24, 26), ps(None, ps.t[0:T, 0:2]))
            C.tt("dve", DTB(None, DTB.t[0:T, 0:T]), ROW(None, ROW.t[0:T, 4, 0:T]), MASKBIG[kd](None, MASKBIG[kd].t[0:T, 0:T]), ALU.add)
            C.act(DTB(None, DTB.t[0:T, 0:T]), DTB(None, DTB.t[0:T, 0:T]), AF.Exp, scale=-1.0, bias=col(16 + h, 17 + h))
            ps = next_psa()
            for kt in range(2):
                C.mm(ps(None, ps.t[0:T, 0:T]), pj(8 + 2 * h + kt), pj(2 * h + kt), start=(kt == 0), stop=(kt == 1))
            C.tt("dve", STB(None, STB.t[0:T, 0:T]), ps(None, ps.t[0:T, 0:T]), DTB(None, DTB.t[0:T, 0:T]), ALU.mult)
            ps1 = next_psa()
            C.mm(ps1(None, ps1.t[0:T, 0:257]), STB(None, STB.t[0:T, 0:T]), VTM(None, VTM.t[0:T, h, :]))
            C.copy("act", P1S(None, P1S.t[0:T, :]), ps1(None, ps1.t[0:T, 0:257]))
            ps2 = next_psa()
            if is_s:
                i_ = 0
                for n in range(nseq):
                    cs = CSS[n % 2]
                    C.dma(cs(None, cs.t[:, :, 0:256]), I["s_mc"][s0 + n, h].rearrange("(kt p) v -> p kt v", p=128))
                    C.dma(cs(None, cs.t[:, :, 256:257]), I["s_mn"][s0 + n, h].rearrange("(kt p o) -> p kt o", p=128, o=1), slow=True)
                    for kt in range(2):
                        qm = QM[i_ % 2]
                        i_ += 1
                        C.tt("pool", qm(None, qm.t[:, 0:T]), pj(2 * h + kt), SEGROW(None, SEGROW.t[:, n, 0:T]), ALU.mult)
                        C.mm(ps2(None, ps2.t[0:T, 0:257]), qm(None, qm.t[:, 0:T]), cs(None, cs.t[:, kt, :]),
                             start=(n == 0 and kt == 0), stop=(n == nseq - 1 and kt == 1))
            else:
                for kt in range(2):
                    C.mm(ps2(None, ps2.t[0:T, 0:257]), pj(2 * h + kt), CS(("h", h), CS.t[:, h, kt, :]), start=(kt == 0), stop=(kt == 1))
            sm = lambda a: SM(None, SM.t[0:T, a:a + 1])
            C.tt("dve", sm(0), col(20 + h, 21 + h), col(24, 25), ALU.subtract)
            C.act(sm(0), sm(0), AF.Exp)
            C.stt("dve", NUM(None, NUM.t[0:T, :]), ps2(None, ps2.t[0:T, 0:257]), sm(0), P1S(None, P1S.t[0:T, :]), ALU.mult, ALU.add)
            C.tt("dve", sm(1), col(12 + h, 13 + h), col(24, 25), ALU.add)
            C.act(sm(1), sm(1), AF.Exp, scale=-1.0)
            C.ts("dve", sm(2), NUM(None, NUM.t[0:T, 256:257]), -1.0, None, ALU.mult)
            C.tt("dve", sm(2), sm(2), NUM(None, NUM.t[0:T, 256:257]), ALU.max)
            C.tt("dve", sm(2), sm(2), sm(1), ALU.max)
            recip("dve", sm(2), sm(2))
            C.op("dve", lambda g, T=T: g.reduce_sum(out=SM.t[0:T, 3:4], in_=NUM.t[0:T, 0:256], axis=AX.X),
                 reads=[NUM(None, NUM.t[0:T, 0:256])], writes=[sm(3)])
            C.tt("dve", sm(3), sm(3), sm(2), ALU.mult)
            C.ts("dve", sm(3), sm(3), 1.0 / 256, None, ALU.mult)
            hn = HN(None, HN.t[0:T, :])
            C.ts("dve", hn, NUM(None, NUM.t[0:T, 0:256]), sm(2), sm(3), ALU.mult, ALU.subtract)
            C.tt("dve", P1S(None, P1S.t[0:T, 0:256]), hn, hn, ALU.mult)
            C.op("dve", lambda g, T=T: g.reduce_sum(out=SM.t[0:T, 4:5], in_=P1S.t[0:T, 0:256], axis=AX.X),
                 reads=[P1S(None, P1S.t[0:T, 0:256])], writes=[sm(4)])
            C.ts("dve", sm(4), sm(4), 1.0 / 256, LN_EPS, ALU.mult, ALU.add)
            C.act(sm(4), sm(4), AF.Sqrt)
            recip("dve", sm(4), sm(4))
            C.ts("dve", hn, hn, sm(4), None, ALU.mult)
            for kt in range(2):
                ct = 2 * h + kt
                ps = next_psb()
                C.tr(ps(None, ps.t[:, 0:T]), HN(None, HN.t[0:T, kt * 128:(kt + 1) * 128]), ident(None, ident.t[0:T, 0:T]))
                C.stt("dve", mixv(ct, T), ps(None, ps.t[:, 0:T]), VEC1(None, VEC1.t[:, ct, 0:1]), pj(16 + ct), ALU.mult, ALU.mult)
                C.tt("dve", mixv(ct, T), mixv(ct, T), pj(24 + ct), ALU.mult)
            C.tt("dve", sm(5), col(16 + h, 17 + h), col(25, 26), ALU.subtract)
            C.act(sm(5), sm(5), AF.Exp)
            for kt in range(2):
                ps = next_psb()
                C.tr(ps(None, ps.t[0:T, 0:128]), pj(8 + 2 * h + kt), ident.all())
                C.ts("dve", KW(None, KW.t[0:T, h * 256 + kt * 128:h * 256 + (kt + 1) * 128]), ps(None, ps.t[0:T, 0:128]), sm(5), None, ALU.mult)
            if is_s:
                for n in range(nseq):
                    cs = CSS[n % 2]
                    co = CSO[n % 2]
                    C.dma(cs(None, cs.t[:, :, 0:256]), I["s_mc"][s0 + n, h].rearrange("(kt p) v -> p kt v", p=128))
                    C.dma(cs(None, cs.t[:, :, 256:257]), I["s_mn"][s0 + n, h].rearrange("(kt p o) -> p kt o", p=128, o=1), slow=True)
                    C.ts("pool", KWN(None, KWN.t[0:T, :]), KW(None, KW.t[0:T, h * 256:(h + 1) * 256]), SEGC(None, SEGC.t[0:T, n:n + 1]), None, ALU.mult)
                    for kt in range(2):
                        ps = next_psa()
                        C.mm(ps(None, ps.t[:, 0:257]), KWN(None, KWN.t[0:T, kt * 128:(kt + 1) * 128]), VTM(None, VTM.t[0:T, h, :]))
                        C.stt("dve", co(None, co.t[:, kt, :]), cs(None, cs.t[:, kt, :]), DEC(None, DEC.t[:, h, n:n + 1]), ps(None, ps.t[:, 0:257]), ALU.mult, ALU.add)
                    C.dma(O["s_mc_o"][s0 + n, h].rearrange("(kt p) v -> p kt v", p=128), co(None, co.t[:, :, 0:256]))
                    C.dma(O["s_mn_o"][s0 + n, h].rearrange("(kt p o) -> p kt o", p=128, o=1), co(None, co.t[:, :, 256:257]), slow=True)
                C.dma(O["s_mm_o"][s0:s0 + 8, h:h + 1].rearrange("n o -> o n"), MNEW(None, MNEW.t[0:1, h, 0:8]), slow=True)
            else:
                for kt in range(2):
                    ps = next_psa()
                    C.mm(ps(None, ps.t[:, 0:257]), KW(None, KW.t[0:T, h * 256 + kt * 128:h * 256 + (kt + 1) * 128]), VTM(None, VTM.t[0:T, h, :]))
                    csv = CS(("h", h), CS.t[:, h, kt, :])
                    C.stt("dve", csv, csv, DEC(None, DEC.t[:, h, 0:1]), ps(None, ps.t[:, 0:257]), ALU.mult, ALU.add)
                C.copy("dve", MCAR(None, MCAR.t[:, h:h + 1]), MNEW(None, MNEW.t[:, h, 0:1]))
        if (not is_s) and tile["last"]:
            for h in range(4):
                C.dma(O["p_mc"][h].rearrange("(kt p) v -> p kt v", p=128), CS(("h", h), CS.t[:, h, :, 0:256]))
                C.dma(O["p_mn"][h].rearrange("(kt p o) -> p kt o", p=128, o=1), CS(("h", h), CS.t[:, h, :, 256:257]), slow=True)
            C.dma(O["p_mm"][:, :], MCAR(None, MCAR.t[0:1, :]))

        if do_rwkv:
            (rwkv2 if cfg.get("rwkv2", True) else rwkv)(tile)
        else:
            for ct in range(8, 16):
                C.memset("pool", mixv(ct, T), 0.0)
        out_proj_ln(I["od_w_out"], tile, VECO, 0, 1)

    for tile in tile_plan(cfg):
        load_x(tile)
        if nlayers >= 1:
            layer0(tile)
        if nlayers >= 2:
            layer1(tile)
        store_y(tile)

    C.emit()
    es.close()
    return nc, C


def make_in_maps(inp, cores, consts):
    maps = []
    f = lambda a: np.ascontiguousarray(a, dtype=np.float32)
    for c in cores:
        s = c % 4
        m = {}
        m["xp"] = f(inp["x_prompt"][s])
        m["meta"] = f(inp["meta_tokens"])
        m["xs"] = f(inp["x_sample"][16 * c:16 * c + 16].reshape(128, D))
        m["s_conv"] = f(inp["state_conv"][0, 16 * c:16 * c + 16].reshape(480, 1024))
        m["s_sre"] = f(inp["state_ssm_re"][0, 16 * c:16 * c + 16])
        m["s_sim"] = f(inp["state_ssm_im"][0, 16 * c:16 * c + 16])
        m.update(consts)
        sl = slice(16 * c, 16 * c + 16)
        m["s_mc"] = f(inp["state_mlstm_c"][0, sl])
        m["s_mn"] = f(inp["state_mlstm_n"][0, sl])
        m["s_mm"] = f(inp["state_mlstm_m"][0, sl])
        m["s_rs"] = f(inp["state_rwkv_s"][0, sl])
        m["s_rsh"] = f(inp["state_rwkv_shift"][0, sl])
        m["od_w_in"] = f(inp["od_w_in"][0])
        for nm in ("m_ig_b", "m_fg_b", "m_hn_g", "r_w0", "r_a0", "r_kk", "r_ka", "r_ln_g", "r_ln_b", "r_rk", "r_mu", "od_ln_g", "od_ln_b"):
            m[nm] = f(inp[nm][0].reshape(1, -1))
        for nm in ("r_w2", "r_a2", "od_w_out"):
            m[nm] = f(inp[nm][0])
        m["ev_w_in"] = f(inp["ev_w_in"][0])
        m["a_conv_w"] = f(inp["a_conv_w"][0])
        for nm in ("a_conv_b", "a_ln_g", "a_ln_b", "s5_d", "s5_log_dt", "s5_glu_b", "ev_ln_g", "ev_ln_b"):
            m[nm] = f(inp[nm][0].reshape(1, -1))
        m["a_pw"] = f(inp["a_pw"][0])
        for nm in ("s5_lambda_re", "s5_lambda_im", "s5_b_re", "s5_b_im", "s5_glu_w", "ev_w_out"):
            m[nm] = f(inp[nm][0])
        m["s5_c_re"] = f(inp["s5_c_re"][0].reshape(1024, 64))
        m["s5_c_im"] = f(inp["s5_c_im"][0].reshape(1024, 64))
        maps.append(m)
    return maps


def kernel(**inp):
    cfg = {}
    nc, C = build(cfg)
    consts = make_consts()
    cores = list(range(NCORES))
    maps = make_in_maps(inp, cores, consts)
    res = run_bass_kernel_spmd(nc, maps, core_ids=cores)
    R = res.results
    B = 4
    cat = lambda k, shp: np.concatenate([R[c][k].reshape((16,) + shp) for c in range(NCORES)], 0)[None]
    stk = lambda k, shp: np.stack([R[c][k].reshape(shp) for c in range(B)], 0)[None]
    y_p = np.stack([R[c]["y_p"] for c in range(B)], 0)
    y_s = np.concatenate([R[c]["y_s"].reshape(16, 8, D) for c in range(NCORES)], 0)
    return (y_p, y_s,
            stk("p_conv", (30, 1024)), stk("p_sre", (64, 64)), stk("p_sim", (64, 64)), stk("p_mc", (4, 256, 256)),
            stk("p_mn", (4, 256)), stk("p_mm", (4,)), stk("p_rs", (16, 64, 64)), stk("p_rsh", (3200,)),
            cat("s_conv_o", (30, 1024)), cat("s_sre_o", (64, 64)), cat("s_sim_o", (64, 64)), cat("s_mc_o", (4, 256, 256)),
            cat("s_mn_o", (4, 256)), cat("s_mm_o", (4,)), cat("s_rs_o", (16, 64, 64)), cat("s_rsh_o", (3200,)))
```

```python
import contextlib
import numpy as np
import concourse.bass as bass
import concourse.mybir as mybir
from concourse.bass_utils import run_bass_kernel_spmd

F32 = mybir.dt.float32
BF16 = mybir.dt.bfloat16
AF = mybir.ActivationFunctionType
ALU = mybir.AluOpType
AX = mybir.AxisListType

D = 2048
TT = 128
WG = 256
NCORES = 8
ALPHA = 4 ** 0.25
LN_EPS = 1e-5


class Reg:
    __slots__ = ("w", "r")

    def __init__(self):
        self.w = None
        self.r = []


class Buf:
    def __init__(self, ctx, name, t):
        self.ctx, self.name, self.t = ctx, name, t
        self.regs = {"_all": Reg()}
        self.dma_sem = None
        self.dma_cnt = 0

    def __call__(self, key, ap):
        return View(self, key, ap)

    def all(self):
        return View(self, None, self.t[:])

    def _sel(self, key):
        if key is None:
            return list(self.regs.values())
        if key not in self.regs:
            self.regs[key] = Reg()
        return [self.regs[key], self.regs["_all"]]

    def rdeps(self, key):
        return [r.w for r in self._sel(key) if r.w is not None]

    def wdeps(self, key):
        out = []
        for r in self._sel(key):
            if r.w is not None:
                out.append(r.w)
            out.extend(r.r)
        return out

    def note_read(self, key, tok):
        if key is None:
            for r in self.regs.values():
                r.r.append(tok)
        else:
            self._sel(key)[0].r.append(tok)

    def note_write(self, key, tok):
        if key is None:
            self.regs = {"_all": Reg()}
            self.regs["_all"].w = tok
        else:
            r = self._sel(key)[0]
            r.w = tok
            r.r = []


class View:
    __slots__ = ("buf", "key", "ap")

    def __init__(self, buf, key, ap):
        self.buf, self.key, self.ap = buf, key, ap


class Ctx:
    ENG = ("pe", "act", "dve", "pool", "sp")
    EPOCH = 30000

    def __init__(self, nc, es):
        self.nc, self.es = nc, es
        self.prog = {e: [] for e in self.ENG}
        self.cnt = {e: 0 for e in self.ENG}
        self.sem = {e: es.enter_context(nc.semaphore("sem_" + e)) for e in self.ENG}
        self.known = {e: {} for e in self.ENG}
        self.final = []
        self.total = {}
        self.nsem = 5
        self.nbytes = 0

    def sb(self, name, shape, dtype=F32):
        t = self.es.enter_context(self.nc.sbuf_tensor("sb_" + name, list(shape), dtype))
        n = 4
        for s in shape[1:]:
            n *= s
        self.nbytes += n
        return Buf(self, name, t)

    def ps(self, name, shape, dtype=F32):
        t = self.es.enter_context(self.nc.psum_tensor("ps_" + name, list(shape), dtype))
        return Buf(self, name, t)

    def need(self, e, tok):
        sem, val = tok
        k = id(sem)
        if self.known[e].get(k, 0) >= val:
            return
        self.known[e][k] = val
        self.prog[e].append(("wait", sem, val))

    def op(self, e, fn, reads=(), writes=()):
        for v in reads:
            if isinstance(v, View):
                for tok in v.buf.rdeps(v.key):
                    self.need(e, tok)
        for v in writes:
            if isinstance(v, View):
                for tok in v.buf.wdeps(v.key):
                    self.need(e, tok)
        if self.cnt[e] >= self.EPOCH:
            self.total[e] = self.total.get(e, 0) + self.cnt[e]
            self.sem[e] = self.es.enter_context(self.nc.semaphore("sem_%s_%d" % (e, self.total[e])))
            self.cnt[e] = 0
            self.nsem += 1
        self.cnt[e] += 1
        tok = (self.sem[e], self.cnt[e])
        self.prog[e].append(("op", fn, self.sem[e], 1))
        for v in reads:
            if isinstance(v, View):
                v.buf.note_read(v.key, tok)
        for v in writes:
            if isinstance(v, View):
                v.buf.note_write(v.key, tok)
        return tok

    def dma(self, out, in_, q="sp", slow=False):
        sbv = out if isinstance(out, View) else in_
        b = sbv.buf
        if b.dma_sem is None:
            b.dma_sem = self.es.enter_context(self.nc.semaphore("dq_" + b.name))
            self.nsem += 1
        if isinstance(in_, View):
            for tok in in_.buf.rdeps(in_.key):
                self.need(q, tok)
        if isinstance(out, View):
            for tok in out.buf.wdeps(out.key):
                self.need(q, tok)
        b.dma_cnt += 16
        tok = (b.dma_sem, b.dma_cnt)
        oap = out.ap if isinstance(out, View) else out
        iap = in_.ap if isinstance(in_, View) else in_
        if slow:
            fn = lambda eng, oap=oap, iap=iap: eng.dma_start(out=oap, in_=iap, allow_slow_non_contiguous=True)
        else:
            fn = lambda eng, oap=oap, iap=iap: eng.dma_start(out=oap, in_=iap)
        self.prog[q].append(("op", fn, b.dma_sem, 16))
        if isinstance(in_, View):
            in_.buf.note_read(in_.key, tok)
        if isinstance(out, View):
            out.buf.note_write(out.key, tok)
        else:
            self.final.append(tok)
        return tok

    def tt(self, e, out, in0, in1, op):
        return self.op(e, lambda g: g.tensor_tensor(out=out.ap, in0=in0.ap, in1=in1.ap, op=op),
                       reads=[in0, in1], writes=[out])

    def ts(self, e, out, in0, s1, s2, op0, op1=None):
        rd = [in0] + [s for s in (s1, s2) if isinstance(s, View)]
        a1 = s1.ap if isinstance(s1, View) else s1
        a2 = s2.ap if isinstance(s2, View) else s2
        if op1 is None:
            return self.op(e, lambda g: g.tensor_scalar(out=out.ap, in0=in0.ap, scalar1=a1, scalar2=None, op0=op0),
                           reads=rd, writes=[out])
        return self.op(e, lambda g: g.tensor_scalar(out=out.ap, in0=in0.ap, scalar1=a1, scalar2=a2, op0=op0, op1=op1),
                       reads=rd, writes=[out])

    def stt(self, e, out, in0, s, in1, op0, op1):
        rd = [in0, in1] + ([s] if isinstance(s, View) else [])
        a = s.ap if isinstance(s, View) else s
        return self.op(e, lambda g: g.scalar_tensor_tensor(out=out.ap, in0=in0.ap, scalar=a, in1=in1.ap, op0=op0, op1=op1),
                       reads=rd, writes=[out])

    def act(self, out, in_, func, bias=None, scale=None, e="act"):
        rd = [in_] + [s for s in (bias, scale) if isinstance(s, View)]
        kw = {}
        if bias is not None:
            kw["bias"] = bias.ap if isinstance(bias, View) else bias
        if scale is not None:
            kw["scale"] = scale.ap if isinstance(scale, View) else scale
        return self.op(e, lambda g: g.activation(out=out.ap, in_=in_.ap, func=func, **kw), reads=rd, writes=[out])

    def copy(self, e, out, in_):
        if e == "act":
            return self.act(out, in_, AF.Copy)
        return self.op(e, lambda g: g.tensor_copy(out=out.ap, in_=in_.ap), reads=[in_], writes=[out])

    def memset(self, e, out, val):
        return self.op(e, lambda g: g.memset(out.ap, val), writes=[out])

    def mm(self, out, lhsT, rhs, start=True, stop=True):
        return self.op("pe", lambda g: g.matmul(out.ap, lhsT=lhsT.ap, rhs=rhs.ap, start=start, stop=stop),
                       reads=[lhsT, rhs], writes=[out])

    def tr(self, out, in_, ident):
        return self.op("pe", lambda g: g.transpose(out.ap, in_.ap, ident.ap), reads=[in_, ident], writes=[out])

    def emit(self):
        nc = self.nc
        for tok in self.final:
            self.need("sp", tok)
        for e in self.ENG:
            if e != "sp" and self.cnt[e] > 0:
                self.need("sp", (self.sem[e], self.cnt[e]))
        engs = {"pe": "tensor", "act": "scalar", "dve": "vector", "pool": "gpsimd", "sp": "sync"}
        with nc.Block() as block:
            for e, attr in engs.items():
                items = self.prog[e]

                def body(eng, items=items):
                    for it in items:
                        if it[0] == "wait":
                            eng.wait_ge(it[1], it[2])
                        else:
                            it[1](eng).then_inc(it[2], it[3])

                getattr(block, attr)(body)


def make_consts():
    c = {}
    c["ident"] = np.eye(128, dtype=np.float32)
    c["ones"] = np.ones((128, 128), dtype=np.float32)
    r = np.arange(128)
    mrow = (r % 32) // 16
    mcol = np.arange(128) // 64
    bm = (mrow[:, None] == mcol[None, :]).astype(np.float32)
    ev_r = ((r // 32) % 2 == 0).astype(np.float32)
    c["bmask"] = bm * ev_r[:, None]
    c["bmask_o"] = bm * (1 - ev_r)[:, None]
    gl = np.arange(128) // 16
    cm = ((gl[None, :] % 2) == (r[:, None] // 64)).astype(np.float32)
    c["cmask"] = cm * ev_r[None, :]
    c["cmask_o"] = cm * (1 - ev_r)[None, :]
    BIG = 1e30
    i128 = np.arange(128)
    allow_p = (i128[:, None] <= i128[None, :])
    c["maskbig_p"] = np.where(allow_p, 0.0, BIG).astype(np.float32)
    c["segtri_p"] = allow_p.astype(np.float32)
    i64 = np.arange(64)
    allow_s = (i64[:, None] <= i64[None, :]) & ((i64[:, None] // 8) == (i64[None, :] // 8))
    c["maskbig_s"] = np.where(allow_s, 0.0, BIG).astype(np.float32)
    c["segtri_s"] = allow_s.astype(np.float32)
    r01p = np.ones((128, 128), np.float32); r01p[:, 0] = 0
    r01s = np.ones((128, 64), np.float32); r01s[:, ::8] = 0
    c["r01_p"], c["r01_s"] = r01p, r01s
    c["rneg_p"] = ((1 - r01p) * -BIG).astype(np.float32)
    c["rneg_s"] = ((1 - r01s) * -BIG).astype(np.float32)
    segsel = ((i64[None, :] // 8) == np.arange(8)[:, None]).astype(np.float32)
    c["segsel"] = segsel
    c["segc"] = np.ascontiguousarray(segsel.T)
    c["segrow"] = np.ascontiguousarray(np.broadcast_to(segsel.reshape(1, 512), (128, 512))).astype(np.float32)
    c["blk"] = ((i128[:, None] // 64) == (i128[None, :] // 64)).astype(np.float32)
    st_p = (i128[:, None] < i128[None, :])
    st_s = (i64[:, None] < i64[None, :]) & ((i64[:, None] // 8) == (i64[None, :] // 8))
    c["sstri_p"] = st_p.astype(np.float32)
    c["sstriT_p"] = np.ascontiguousarray(st_p.T).astype(np.float32)
    c["sstri_s"] = st_s.astype(np.float32)
    c["sstriT_s"] = np.ascontiguousarray(st_s.T).astype(np.float32)
    return c


CONST_SHAPES = {"ident": [128, 128], "ones": [128, 128], "bmask": [128, 128], "cmask": [128, 128],
                "bmask_o": [128, 128], "cmask_o": [128, 128],
                "maskbig_p": [128, 128], "segtri_p": [128, 128], "maskbig_s": [64, 64], "segtri_s": [64, 64],
                "r01_p": [128, 128], "r01_s": [128, 64], "rneg_p": [128, 128], "rneg_s": [128, 64],
                "segsel": [8, 64], "segc": [64, 8], "segrow": [128, 512], "blk": [128, 128],
                "sstri_p": [128, 128], "sstriT_p": [128, 128], "sstri_s": [64, 64], "sstriT_s": [64, 64]}


def tile_plan(cfg):
    tiles = []
    npt = cfg.get("n_prompt_tiles", 17)
    pos = 0
    for i in range(17):
        T = 16 if i == 16 else TT
        if i < npt:
            tiles.append(dict(kind="p", T=T, pos=pos, nseq=1, L=T, first=(i == 0), last=(i == npt - 1), s0=0))
        pos += T
    if cfg.get("sample", True):
        for h in range(2):
            tiles.append(dict(kind="s", T=64, pos=0, nseq=8, L=8, first=True, last=True, s0=8 * h))
    return tiles


def build(cfg):
    nc = bass.Bass("TRN2", target_bir_lowering=False)
    es = contextlib.ExitStack()
    C = Ctx(nc, es)
    dbg = cfg.get("debug", False)
    nlayers = cfg.get("layers", 2)

    def din(name, shape):
        return nc.dram_tensor(name, list(shape), F32, kind="ExternalInput").ap()

    def dout(name, shape):
        return nc.dram_tensor(name, list(shape), F32, kind="ExternalOutput").ap()

    I = {}
    I["xp"] = din("xp", [2048, D])
    I["meta"] = din("meta", [16, D])
    I["xs"] = din("xs", [128, D])
    I["s_conv"] = din("s_conv", [16 * 30, 1024])
    I["s_sre"] = din("s_sre", [16, 64, 64])
    I["s_sim"] = din("s_sim", [16, 64, 64])
    for nm, shp in CONST_SHAPES.items():
        I[nm] = din(nm, shp)
    I["s_mc"] = din("s_mc", [16, 4, 256, 256])
    I["s_mn"] = din("s_mn", [16, 4, 256])
    I["s_mm"] = din("s_mm", [16, 4])
    I["s_rs"] = din("s_rs", [16, 16, 64, 64])
    I["s_rsh"] = din("s_rsh", [16, 3200])
    I["od_w_in"] = din("od_w_in", [D, 9352])
    I["m_ig_b"] = din("m_ig_b", [1, 4])
    I["m_fg_b"] = din("m_fg_b", [1, 4])
    for nm in ("m_hn_g", "r_w0", "r_a0", "r_kk", "r_ka", "r_ln_g", "r_ln_b", "r_rk"):
        I[nm] = din(nm, [1, 1024])
    I["r_mu"] = din("r_mu", [1, 3200])
    I["r_w2"] = din("r_w2", [64, 1024])
    I["r_a2"] = din("r_a2", [64, 1024])
    I["od_w_out"] = din("od_w_out", [2048, D])
    I["od_ln_g"] = din("od_ln_g", [1, D])
    I["od_ln_b"] = din("od_ln_b", [1, D])
    I["ev_w_in"] = din("ev_w_in", [D, 5120])
    I["a_conv_w"] = din("a_conv_w", [31, 1024])
    for nm in ("a_conv_b", "a_ln_g", "a_ln_b", "s5_d"):
        I[nm] = din(nm, [1, 1024])
    I["a_pw"] = din("a_pw", [1024, 1024])
    I["s5_lambda_re"] = din("s5_lambda_re", [64, 64])
    I["s5_lambda_im"] = din("s5_lambda_im", [64, 64])
    I["s5_log_dt"] = din("s5_log_dt", [1, 64])
    I["s5_b_re"] = din("s5_b_re", [64, 64, 16])
    I["s5_b_im"] = din("s5_b_im", [64, 64, 16])
    I["s5_c_re"] = din("s5_c_re", [1024, 64])
    I["s5_c_im"] = din("s5_c_im", [1024, 64])
    I["s5_glu_w"] = din("s5_glu_w", [1024, 2048])
    I["s5_glu_b"] = din("s5_glu_b", [1, 2048])
    I["ev_w_out"] = din("ev_w_out", [2048, D])
    I["ev_ln_g"] = din("ev_ln_g", [1, D])
    I["ev_ln_b"] = din("ev_ln_b", [1, D])

    O = {}
    O["y_p"] = dout("y_p", [2048, D])
    O["y_s"] = dout("y_s", [128, D])
    O["p_conv"] = dout("p_conv", [30, 1024])
    O["p_sre"] = dout("p_sre", [64, 64])
    O["p_sim"] = dout("p_sim", [64, 64])
    O["s_conv_o"] = dout("s_conv_o", [16 * 30, 1024])
    O["s_sre_o"] = dout("s_sre_o", [16, 64, 64])
    O["s_sim_o"] = dout("s_sim_o", [16, 64, 64])
    O["p_mc"] = dout("p_mc", [4, 256, 256])
    O["p_mn"] = dout("p_mn", [4, 256])
    O["p_mm"] = dout("p_mm", [1, 4])
    O["p_rs"] = dout("p_rs", [16, 64, 64])
    O["p_rsh"] = dout("p_rsh", [1, 3200])
    O["s_mc_o"] = dout("s_mc_o", [16, 4, 256, 256])
    O["s_mn_o"] = dout("s_mn_o", [16, 4, 256])
    O["s_mm_o"] = dout("s_mm_o", [16, 4])
    O["s_rs_o"] = dout("s_rs_o", [16, 16, 64, 64])
    O["s_rsh_o"] = dout("s_rsh_o", [16, 3200])

    ident = C.sb("ident", [128, 128])
    ones = C.sb("ones", [128, 128])
    XT = C.sb("XT", [128, 16, TT])
    XS = C.sb("XS", [128, D])
    PJ = C.sb("PJ", [128, 33, TT])
    MIX = C.sb("MIX", [128, 16 * TT], BF16)
    XTB = C.sb("XTB", [128, 16, TT], BF16)
    GELB = C.sb("GELB", [128, 8, TT], BF16)
    WS = [C.sb("WS%d" % i, [128, 16, WG], BF16) for i in range(2)]
    HB = C.sb("HB", [128, 8, 304])
    HC = C.sb("HC", [128, 8, 30])
    ACC = C.sb("ACC", [128, 8, TT])
    SQ2 = C.sb("SQ2", [128, 2, TT])
    ST = C.sb("ST", [128, 3, TT])
    VEC0 = C.sb("VEC0", [128, 8, 40])
    VECG = C.sb("VECG", [128, 16, 4])
    TMP = C.sb("TMP", [128, 128])
    XR = C.sb("XR", [128, 32, 129])
    XI = C.sb("XI", [128, 32, 129])
    S5C = C.sb("S5C", [128, 2, 32])
    S5A = C.sb("S5A", [128, 8, 32])
    S5T = C.sb("S5T", [128, 10, 32])
    SCS = C.sb("SCS", [128, 2, 32, 8])
    BRE = [C.sb("BRE%d" % i, [128, 8, 128]) for i in range(2)]
    BIM = [C.sb("BIM%d" % i, [128, 8, 128]) for i in range(2)]
    CRE = [C.sb("CRE%d" % i, [128, 8, 128]) for i in range(2)]
    CIM = [C.sb("CIM%d" % i, [128, 8, 128]) for i in range(2)]
    mask_bo = C.sb("mask_bo", [128, 128])
    mask_co = C.sb("mask_co", [128, 128])
    S5ST = C.sb("S5ST", [128, 128])
    mask_b = C.sb("mask_b", [128, 128])
    mask_c = C.sb("mask_c", [128, 128])

    PSA = [C.ps("PSA%d" % i, [128, 512]) for i in range(4)]
    PSB = [C.ps("PSB%d" % i, [128, 512]) for i in range(4)]

    def mixv(ct, T):
        return MIX(("t", ct), MIX.t[:, ct * TT:ct * TT + T])

    C.dma(ident.all(), I["ident"])
    C.dma(ones.all(), I["ones"])
    C.dma(mask_b.all(), I["bmask"])
    C.dma(mask_c.all(), I["cmask"])
    C.dma(mask_bo.all(), I["bmask_o"])
    C.dma(mask_co.all(), I["cmask_o"])

    rr = {"psa": 0, "psb": 0, "ws": 0, "ev": 0, "sq": 0}

    def next_psa():
        rr["psa"] = (rr["psa"] + 1) % 4
        return PSA[rr["psa"]]

    def next_psb():
        rr["psb"] = (rr["psb"] + 1) % 4
        return PSB[rr["psb"]]

    def ev_eng():
        rr["ev"] += 1
        return "dve" if rr["ev"] % 2 else "act"

    def load_rows_T(dst, rows, ncols, col0=0, ct0=0):
        r0 = 0
        for ap in rows:
            nr = ap.shape[0]
            C.dma(XS(None, XS.t[r0:r0 + nr, 0:ncols]), ap)
            r0 += nr
        nr = r0
        for ct in range(ncols // 128):
            ps = next_psb()
            C.tr(ps(None, ps.t[:, 0:nr]), XS(None, XS.t[0:nr, ct * 128:(ct + 1) * 128]), ident(None, ident.t[0:nr, 0:nr]))
            C.copy("dve", dst(None, dst.t[:, ct0 + ct, col0:col0 + nr]), ps(None, ps.t[:, 0:nr]))

    load_rows_T(VEC0, [I["a_conv_w"], I["a_conv_b"], I["a_ln_g"], I["a_ln_b"], I["s5_d"]], 1024)
    load_rows_T(VECG, [I["s5_glu_b"], I["ev_ln_g"], I["ev_ln_b"]], 2048)

    def gp_ap(ap2d):
        return ap2d.rearrange("(q m) p -> (m p) q", m=2)

    LR, LI, DT, AR, AI, NAI = range(6)
    A = lambda k: S5A(None, S5A.t[:, k, :])
    Tm = lambda k: S5T(None, S5T.t[:, k, :])
    for m in range(2):
        C.dma(S5A(None, S5A.t[64 * m:64 * m + 64, LR, :]), I["s5_lambda_re"].rearrange("(q m) p -> m p q", m=2)[m], slow=True)
        C.dma(S5A(None, S5A.t[64 * m:64 * m + 64, LI, :]), I["s5_lambda_im"].rearrange("(q m) p -> m p q", m=2)[m], slow=True)
    ldt = I["s5_log_dt"]
    for m in range(2):
        src = bass.AP(ldt.tensor, ldt.offset + m, [[0, 64], [2, 32]])
        C.dma(S5A(None, S5A.t[64 * m:64 * m + 64, DT, :]), src, slow=True)
    C.act(A(DT), A(DT), AF.Exp)
    C.tt("dve", Tm(0), A(LR), A(DT), ALU.mult)
    C.act(Tm(1), Tm(0), AF.Exp)
    C.tt("dve", Tm(2), A(LI), A(DT), ALU.mult)
    PI = float(np.pi)
    C.ts("dve", Tm(3), Tm(2), 1.0 / 32, None, ALU.mult)
    C.act(Tm(4), Tm(3), AF.Sin)
    C.ts("dve", Tm(3), Tm(3), PI / 2, None, ALU.add)
    C.act(Tm(5), Tm(3), AF.Sin)
    for _ in range(5):
        C.tt("dve", Tm(3), Tm(4), Tm(5), ALU.mult)
        C.tt("dve", Tm(8), Tm(5), Tm(5), ALU.mult)
        C.tt("dve", Tm(9), Tm(4), Tm(4), ALU.mult)
        C.ts("dve", Tm(4), Tm(3), 2.0, None, ALU.mult)
        C.tt("dve", Tm(5), Tm(8), Tm(9), ALU.subtract)
    C.tt("dve", A(AR), Tm(1), Tm(5), ALU.mult)
    C.tt("dve", A(AI), Tm(1), Tm(4), ALU.mult)
    C.ts("dve", A(NAI), A(AI), -1.0, None, ALU.mult)
    C.tt("dve", Tm(0), A(LR), A(LR), ALU.mult)
    C.tt("dve", Tm(1), A(LI), A(LI), ALU.mult)
    C.tt("dve", Tm(0), Tm(0), Tm(1), ALU.add)
    C.op("dve", lambda g: g.reciprocal(out=S5T.t[:, 0, :], in_=S5T.t[:, 0, :]), reads=[Tm(0)], writes=[Tm(0)])
    C.ts("dve", Tm(1), A(AR), -1.0, None, ALU.add)
    C.tt("dve", Tm(2), Tm(1), A(LR), ALU.mult)
    C.tt("dve", Tm(3), A(AI), A(LI), ALU.mult)
    C.tt("dve", Tm(2), Tm(2), Tm(3), ALU.add)
    C.tt("dve", Tm(6), Tm(2), Tm(0), ALU.mult)
    C.tt("dve", Tm(2), A(AI), A(LR), ALU.mult)
    C.tt("dve", Tm(3), Tm(1), A(LI), ALU.mult)
    C.tt("dve", Tm(2), Tm(2), Tm(3), ALU.subtract)
    C.tt("dve", Tm(7), Tm(2), Tm(0), ALU.mult)
    braw_t = PJ.t[:, 0:8, :].rearrange("p a b -> p (a b)").rearrange("p (k q h) -> p k q h", k=2, q=32)
    bb_t = PJ.t[:, 8:24, :].rearrange("p a b -> p (a b)").rearrange("p (k q h) -> p k q h", k=2, q=32)
    sc_t = PJ.t[:, 24:28, :].rearrange("p a b -> p (a b)").rearrange("p (q h) -> p q h", q=32)
    for k, nm in enumerate(("s5_b_re", "s5_b_im")):
        for m in range(2):
            C.dma(PJ(None, braw_t[64 * m:64 * m + 64, k, :, :]), I[nm].rearrange("(q m) p h -> m p q h", m=2)[m])
    qr_b = S5T(None, S5T.t[:, 6, :].unsqueeze(2).broadcast_to([128, 32, 16]))
    qi_b = S5T(None, S5T.t[:, 7, :].unsqueeze(2).broadcast_to([128, 32, 16]))
    br = PJ(None, braw_t[:, 0, :, :])
    bi = PJ(None, braw_t[:, 1, :, :])
    sc = PJ(None, sc_t)
    for dup in range(2):
        o_r = PJ(None, bb_t[:, 0, :, dup * 16:(dup + 1) * 16])
        o_i = PJ(None, bb_t[:, 1, :, dup * 16:(dup + 1) * 16])
        C.tt("dve", o_r, br, qr_b, ALU.mult)
        C.tt("dve", sc, bi, qi_b, ALU.mult)
        C.tt("dve", o_r, o_r, sc, ALU.subtract)
        C.tt("dve", o_i, bi, qr_b, ALU.mult)
        C.tt("dve", sc, br, qi_b, ALU.mult)
        C.tt("dve", o_i, o_i, sc, ALU.add)
    for k, dst in enumerate((BRE, BIM)):
        for gt in range(8):
            ps = next_psb()
            C.copy("dve", TMP.all(), PJ(None, bb_t[:, k, 4 * gt:4 * gt + 4, :]))
            C.tr(ps(None, ps.t[:, 0:128]), TMP.all(), ident.all())
            C.tt("dve", dst[0](None, dst[0].t[:, gt, :]), ps(None, ps.t[:, 0:128]), mask_b.all(), ALU.mult)
            C.tt("dve", dst[1](None, dst[1].t[:, gt, :]), ps(None, ps.t[:, 0:128]), mask_bo.all(), ALU.mult)
    for k, (nm, dst) in enumerate((("s5_c_re", CRE), ("s5_c_im", CIM))):
        for gt in range(8):
            for dup in range(2):
                C.dma(XS(None, XS.t[:, dup * 64:(dup + 1) * 64]), I[nm][gt * 128:(gt + 1) * 128, :])
            ps = next_psb()
            C.tr(ps(None, ps.t[:, 0:128]), XS(None, XS.t[:, 0:128]), ident.all())
            for par, mk in enumerate((mask_c, mask_co)):
                C.stt("dve", dst[par](None, dst[par].t[:, gt, :]), ps(None, ps.t[:, 0:128]), 1.0 if k == 0 else -1.0,
                      mk.all(), ALU.mult, ALU.mult)

    def stream_mm(W, nkt, coltiles, rhs_fn, T, evac):
        groups = []
        cur = []
        for ct in coltiles:
            if cur and (ct[0] + ct[1] - cur[0][0] > WG):
                groups.append(cur)
                cur = []
            cur.append(ct)
        if cur:
            groups.append(cur)
        Wv = W.rearrange("(kt p) c -> p kt c", p=128)
        j = 0
        for grp in groups:
            g0 = grp[0][0]
            gw = grp[-1][0] + grp[-1][1] - g0
            ws = WS[rr["ws"] % 2]
            rr["ws"] += 1
            C.dma(ws(None, ws.t[:, 0:nkt, 0:gw]), Wv[:, :, g0:g0 + gw], q="pool")
            for (c0, w) in grp:
                ps = next_psa()
                pv = ps(None, ps.t[0:w, 0:T])
                for kt in range(nkt):
                    C.mm(pv, ws(None, ws.t[:, kt, c0 - g0:c0 - g0 + w]), rhs_fn(kt), start=(kt == 0), stop=(kt == nkt - 1))
                evac(j, pv)
                j += 1

    def layer_norm_cols(src, ntile, T, gcol, bcol, vec, func=AF.Identity, eps=LN_EPS, dst_fn=None, also_fn=None):
        nch = ntile * 128
        ps = next_psb()
        ps2 = next_psb()
        for ct in range(ntile):
            sv = src(("t", ct), src.t[:, ct, 0:T])
            sq = SQ2(("s", rr["sq"] % 2), SQ2.t[:, rr["sq"] % 2, 0:T])
            rr["sq"] += 1
            C.act(sq, sv, AF.Square)
            C.mm(ps(None, ps.t[:, 0:T]), ones.all(), sv, start=(ct == 0), stop=(ct == ntile - 1))
            C.mm(ps2(None, ps2.t[:, 0:T]), ones.all(), sq, start=(ct == 0), stop=(ct == ntile - 1))
        st = lambda k: ST(None, ST.t[:, k, 0:T])
        C.ts("dve", st(0), ps(None, ps.t[:, 0:T]), 1.0 / nch, None, ALU.mult)
        C.ts("dve", st(1), ps2(None, ps2.t[:, 0:T]), 1.0 / nch, None, ALU.mult)
        C.tt("dve", st(2), st(0), st(0), ALU.mult)
        C.tt("dve", st(1), st(1), st(2), ALU.subtract)
        C.ts("dve", st(1), st(1), eps, None, ALU.add)
        C.act(st(1), st(1), AF.Sqrt)
        C.op("dve", lambda g, T=T: g.reciprocal(out=ST.t[:, 1, 0:T], in_=ST.t[:, 1, 0:T]), reads=[st(1)], writes=[st(1)])
        C.tt("dve", st(2), st(0), st(1), ALU.mult)
        C.ts("dve", st(2), st(2), -1.0, None, ALU.mult)
        for ct in range(ntile):
            e = "dve" if ct % 2 == 0 else "pool"
            sv = src(("t", ct), src.t[:, ct, 0:T])
            C.tt(e, sv, sv, st(1), ALU.mult)
            C.tt(e, sv, sv, st(2), ALU.add)
            dv = dst_fn(ct) if dst_fn is not None else sv
            C.act(dv, sv, func, bias=vec(None, vec.t[:, ct, bcol:bcol + 1]), scale=vec(None, vec.t[:, ct, gcol:gcol + 1]))
            if also_fn is not None:
                C.copy("pool" if ct % 2 else "act", also_fn(ct), dv)

    def load_x(tile):
        n = tile["T"]
        if tile["kind"] == "s":
            C.dma(XS(None, XS.t[0:n, :]), I["xs"][tile["s0"] * 8:tile["s0"] * 8 + n, :])
        else:
            p0 = tile["pos"]
            r = 0
            if p0 < 16:
                C.dma(XS(None, XS.t[0:16, :]), I["meta"][:, :])
                r = 16
            x0 = p0 + r - 16
            C.dma(XS(None, XS.t[r:n, :]), I["xp"][x0:x0 + n - r, :])
        for dt_ in range(16):
            ps = next_psb()
            C.tr(ps(None, ps.t[:, 0:n]), XS(None, XS.t[0:n, dt_ * 128:(dt_ + 1) * 128]), ident(None, ident.t[0:n, 0:n]))
            C.copy("act" if dt_ % 2 else "dve", XT(("t", dt_), XT.t[:, dt_, 0:n]), ps(None, ps.t[:, 0:n]))
            C.copy("pool", XTB(("t", dt_), XTB.t[:, dt_, 0:n]), XT(("t", dt_), XT.t[:, dt_, 0:n]))

    def store_y(tile):
        n = tile["T"]
        for dt_ in range(16):
            ps = next_psb()
            C.tr(ps(None, ps.t[0:n, 0:128]), XT(("t", dt_), XT.t[:, dt_, 0:n]), ident.all())
            C.copy("act" if dt_ % 2 else "dve", XS(None, XS.t[0:n, dt_ * 128:(dt_ + 1) * 128]), ps(None, ps.t[0:n, 0:128]))
        if tile["kind"] == "s":
            C.dma(O["y_s"][tile["s0"] * 8:tile["s0"] * 8 + n, :], XS(None, XS.t[0:n, :]))
        else:
            p0 = tile["pos"]
            r = 16 if p0 < 16 else 0
            x0 = p0 + r - 16
            C.dma(O["y_p"][x0:x0 + n - r, :], XS(None, XS.t[r:n, :]))

    def out_proj_ln(W, tile, vec, gcol, bcol):
        T = tile["T"]

        def evac(j, pv):
            xv_ = XT(("t", j), XT.t[:, j, 0:T])
            C.stt("dve", xv_, xv_, ALPHA, pv, ALU.mult, ALU.add)

        stream_mm(W, 16, [(i * 128, 128) for i in range(16)], lambda kt: mixv(kt, T), T, evac)
        layer_norm_cols(XT, 16, T, gcol, bcol, vec, also_fn=lambda ct: XTB(("t", ct), XTB.t[:, ct, 0:T]))

    def layer0(tile):
        T, nseq, L = tile["T"], tile["nseq"], tile["L"]
        is_s = tile["kind"] == "s"
        s0 = tile["s0"]
        xrhs = lambda kt: XTB(("t", kt), XTB.t[:, kt, 0:T])
        pj = lambda j: PJ(("t", j), PJ.t[:, j, 0:T])

        fence(HB, HB.t[0:1, 0, 0:1])
        def evacA(j, pv):
            if j < 8:
                C.copy(ev_eng(), pj(j), pv)
            elif j < 16:
                C.act(pj(j), pv, AF.Sigmoid)
            else:
                C.act(pj(j), pv, AF.Silu)

        stream_mm(I["ev_w_in"], 16, [(i * 128, 128) for i in range(24)], xrhs, T, evacA)

        W_ = 30 + L

        def hb(ct, a, b):
            v = HB.t[:, ct, 0:nseq * W_].rearrange("p (n w) -> p n w", w=W_)[:, :, a:b]
            return HB(("t", ct), v)

        def tokv(buf, ct):
            return buf(("t", ct), buf.t[:, ct, 0:T].rearrange("p (n l) -> p n l", l=L))

        if is_s:
            for q in range(2):
                C.dma(XS(None, XS.t[0:120, 0:1024]), I["s_conv"][s0 * 30 + q * 120:s0 * 30 + (q + 1) * 120, :])
                for ct in range(8):
                    ps = next_psb()
                    C.tr(ps(None, ps.t[:, 0:120]), XS(None, XS.t[0:120, ct * 128:(ct + 1) * 128]), ident(None, ident.t[0:120, 0:120]))
                    dstv = HB.t[:, ct, 0:nseq * W_].rearrange("p (n w) -> p n w", w=W_)[:, 4 * q:4 * q + 4, 0:30]
                    C.copy("dve", HB(("t", ct), dstv), ps(None, ps.t[:, 0:120].rearrange("p (n r) -> p n r", r=30)))
        else:
            for ct in range(8):
                if tile["first"]:
                    C.memset("pool", hb(ct, 0, 30), 0.0)
                else:
                    C.copy("pool", hb(ct, 0, 30), HC(("t", ct), HC.t[:, ct, :].unsqueeze(1)))
        for ct in range(8):
            e = "dve" if ct % 2 == 0 else "pool"
            C.tt(e, hb(ct, 30, 30 + L), tokv(PJ, ct), tokv(PJ, 8 + ct), ALU.mult)
        for ct in range(8):
            e = "dve"
            acc = tokv(ACC, ct)
            wcol = lambda j, ct=ct: VEC0(None, VEC0.t[:, ct, j:j + 1])
            C.ts(e, acc, hb(ct, 0, L), wcol(0), wcol(31), ALU.mult, ALU.add)
            for j in range(1, 31):
                C.stt(e, acc, hb(ct, j, j + L), wcol(j), acc, ALU.mult, ALU.add)
        if is_s:
            for q in range(2):
                for ct in range(8):
                    ps = next_psb()
                    srcv = HB.t[:, ct, 0:nseq * W_].rearrange("p (n w) -> p n w", w=W_)[:, 4 * q:4 * q + 4, L:L + 30]
                    C.copy("pool", TMP(None, TMP.t[:, 0:120].rearrange("p (n r) -> p n r", r=30)), HB(("t", ct), srcv))
                    C.tr(ps(None, ps.t[0:120, 0:128]), TMP(None, TMP.t[:, 0:120]), ident.all())
                    C.copy("dve", XS(None, XS.t[0:120, ct * 128:(ct + 1) * 128]), ps(None, ps.t[0:120, 0:128]))
                C.dma(O["s_conv_o"][s0 * 30 + q * 120:s0 * 30 + (q + 1) * 120, :], XS(None, XS.t[0:120, 0:1024]))
        else:
            for ct in range(8):
                C.copy("pool", TMP(None, TMP.t[:, 0:30]), HB(("t", ct), HB.t[:, ct, L:L + 30]))
                C.copy("pool", HC(("t", ct), HC.t[:, ct, :]), TMP(None, TMP.t[:, 0:30]))
            if tile["last"]:
                for ct in range(8):
                    ps = next_psb()
                    C.tr(ps(None, ps.t[0:30, 0:128]), HC(("t", ct), HC.t[:, ct, :]), ident.all())
                    C.copy("dve", XS(None, XS.t[0:30, ct * 128:(ct + 1) * 128]), ps(None, ps.t[0:30, 0:128]))
                C.dma(O["p_conv"][:, :], XS(None, XS.t[0:30, 0:1024]))
        layer_norm_cols(ACC, 8, T, 32, 33, VEC0, func=AF.Silu, dst_fn=lambda ct: mixv(8 + ct, T))

        def evac_pw(j, pv):
            C.tt("dve", mixv(j, T), pv, pj(16 + j), ALU.mult)

        stream_mm(I["a_pw"], 8, [(i * 128, 128) for i in range(8)], lambda kt: mixv(8 + kt, T), T, evac_pw)

        def evacB(j, pv):
            if j < 8:
                C.copy(ev_eng(), pj(j), pv)
            else:
                C.act(pj(j), pv, AF.Silu)

        stream_mm(I["ev_w_in"], 16, [(3072 + i * 128, 128) for i in range(16)], xrhs, T, evacB)

        Wx = 1 + L

        def xv(buf, a, b, p0=0, p1=32):
            v = buf.t[:, p0:p1, 0:nseq * Wx].rearrange("p q (n w) -> p q n w", w=Wx)[:, :, :, a:b]
            return buf(None, v)

        if is_s:
            for k, (nm, buf) in enumerate((("s_sre", XR), ("s_sim", XI))):
                for q in range(2):
                    for pr in range(16):
                        g0 = 2 * (16 * q + pr)
                        C.dma(S5ST(None, S5ST.t[pr * 8:(pr + 1) * 8, :]),
                              I[nm][s0:s0 + 8, g0:g0 + 2, :].rearrange("n m p -> n (m p)"))
                    ps = next_psb()
                    C.tr(ps(None, ps.t[:, 0:128]), S5ST.all(), ident.all())
                    C.copy("dve", xv(buf, 0, 1, 16 * q, 16 * q + 16),
                           ps(None, ps.t[:, 0:128].rearrange("p (q n o) -> p q n o", n=8, o=1)))
        else:
            for k, buf in enumerate((XR, XI)):
                if tile["first"]:
                    C.memset("pool", xv(buf, 0, 1), 0.0)
                else:
                    C.copy("pool", xv(buf, 0, 1), S5C(None, S5C.t[:, k, :].unsqueeze(2).unsqueeze(3)))
        for q4 in range(8):
            for k, (tab, buf) in enumerate(((BRE, XR), (BIM, XI))):
                ps = next_psa()
                for ip in range(4):
                    hf = ip // 2
                    tb = tab[ip % 2]
                    C.mm(ps(None, ps.t[:, ip * T:(ip + 1) * T]),
                         tb(None, tb.t[64 * hf:64 * hf + 64, q4, :]),
                         PJ(("t", q4), PJ.t[64 * hf:64 * hf + 64, q4, 0:T]))
                C.copy("act", xv(buf, 1, Wx, 4 * q4, 4 * q4 + 4),
                       ps(None, ps.t[:, 0:4 * T].rearrange("p (q n l) -> p q n l", q=4, l=L)))
        arb = S5A(None, S5A.t[:, AR, :].unsqueeze(2).broadcast_to([128, 32, nseq]))
        aib = S5A(None, S5A.t[:, AI, :].unsqueeze(2).broadcast_to([128, 32, nseq]))
        naib = S5A(None, S5A.t[:, NAI, :].unsqueeze(2).broadcast_to([128, 32, nseq]))
        if nseq == 1:
            t1v = S5T(None, S5T.t[:, 8, :].unsqueeze(2))
            t2v = S5T(None, S5T.t[:, 9, :].unsqueeze(2))
        else:
            t1v = SCS(None, SCS.t[:, 0, :, :])
            t2v = SCS(None, SCS.t[:, 1, :, :])

        def col(buf, t):
            v = buf.t[:, :, 0:nseq * Wx].rearrange("p q (n w) -> p q n w", w=Wx)[:, :, :, t]
            return buf(None, v)

        e = "pool"
        for t in range(L):
            C.tt(e, t1v, col(XR, t), arb, ALU.mult)
            C.tt(e, col(XR, t + 1), col(XR, t + 1), t1v, ALU.add)
            C.tt(e, t1v, col(XI, t), naib, ALU.mult)
            C.tt(e, col(XR, t + 1), col(XR, t + 1), t1v, ALU.add)
            C.tt(e, t2v, col(XI, t), arb, ALU.mult)
            C.tt(e, col(XI, t + 1), col(XI, t + 1), t2v, ALU.add)
            C.tt(e, t2v, col(XR, t), aib, ALU.mult)
            C.tt(e, col(XI, t + 1), col(XI, t + 1), t2v, ALU.add)
        if is_s:
            for k, (nm, buf) in enumerate((("s_sre_o", XR), ("s_sim_o", XI))):
                for q in range(2):
                    C.copy("pool", TMP(None, TMP.t[:, 0:128].rearrange("p (q n o) -> p q n o", n=8, o=1)),
                           xv(buf, L, L + 1, 16 * q, 16 * q + 16))
                    ps = next_psb()
                    C.tr(ps(None, ps.t[:, 0:128]), TMP(None, TMP.t[:, 0:128]), ident.all())
                    C.copy("dve", S5ST.all(), ps(None, ps.t[:, 0:128]))
                    for pr in range(16):
                        g0 = 2 * (16 * q + pr)
                        C.dma(O[nm][s0:s0 + 8, g0:g0 + 2, :].rearrange("n m p -> n (m p)"),
                              S5ST(None, S5ST.t[pr * 8:(pr + 1) * 8, :]))
        else:
            for k, buf in enumerate((XR, XI)):
                C.copy("pool", S5C(None, S5C.t[:, k, :].unsqueeze(2).unsqueeze(3)), xv(buf, L, L + 1))
            if tile["last"]:
                for k, nm in enumerate(("p_sre", "p_sim")):
                    ps = next_psb()
                    C.tr(ps(None, ps.t[0:32, 0:128]), S5C(None, S5C.t[:, k, :]), ident.all())
                    C.copy("dve", S5ST(None, S5ST.t[0:32, :]), ps(None, ps.t[0:32, 0:128]))
                    C.dma(O[nm].rearrange("(q m) p -> q (m p)", m=2), S5ST(None, S5ST.t[0:32, :]))
        for gt in range(8):
            ps = next_psa()
            for ip in range(4):
                pair = 4 * gt + ip
                hf = ip // 2
                ov = ps(None, ps.t[64 * hf:64 * hf + 64, 0:T])
                xr_ = XR(None, XR.t[:, pair, 0:nseq * Wx].rearrange("p (n w) -> p n w", w=Wx)[:, :, 1:Wx])
                xi_ = XI(None, XI.t[:, pair, 0:nseq * Wx].rearrange("p (n w) -> p n w", w=Wx)[:, :, 1:Wx])
                cr, ci = CRE[ip % 2], CIM[ip % 2]
                C.mm(ov, cr(None, cr.t[:, gt, 64 * hf:64 * hf + 64]), xr_, start=(ip % 2 == 0), stop=False)
                C.mm(ov, ci(None, ci.t[:, gt, 64 * hf:64 * hf + 64]), xi_, start=False, stop=(ip % 2 == 1))
            gv = pj(gt)
            C.stt("dve", gv, gv, VEC0(None, VEC0.t[:, gt, 34:35]), ps(None, ps.t[:, 0:T]), ALU.mult, ALU.add)
            C.act(GELB(("t", gt), GELB.t[:, gt, 0:T]), gv, AF.Gelu)

        def evac_glu(j, pv):
            if j < 8:
                C.act(ACC(("t", j), ACC.t[:, j, 0:T]), pv, AF.Identity, bias=VECG(None, VECG.t[:, j, 0:1]))
            else:
                jj = j - 8
                tv = TMP(None, TMP.t[:, 0:T])
                C.act(tv, pv, AF.Sigmoid, bias=VECG(None, VECG.t[:, j, 0:1]))
                C.tt("dve", tv, tv, ACC(("t", jj), ACC.t[:, jj, 0:T]), ALU.mult)
                C.tt("dve", mixv(8 + jj, T), tv, pj(8 + jj), ALU.mult)

        stream_mm(I["s5_glu_w"], 8, [(i * 128, 128) for i in range(16)], lambda kt: GELB(("t", kt), GELB.t[:, kt, 0:T]), T, evac_glu)
        out_proj_ln(I["ev_w_out"], tile, VECG, 1, 2)

    do_rwkv = cfg.get("rwkv", True)
    MASKBIG = {"p": C.sb("mbig_p", [128, 128]), "s": C.sb("mbig_s", [64, 64])}
    SEGTRI = {"p": C.sb("stri_p", [128, 128]), "s": C.sb("stri_s", [64, 64])}
    R01 = {"p": C.sb("r01p", [128, 128]), "s": C.sb("r01s", [128, 64])}
    RNEG = {"p": C.sb("rnegp", [128, 128]), "s": C.sb("rnegs", [128, 64])}
    SEGSEL = C.sb("SEGSEL", [8, 64])
    SEGC = C.sb("SEGC", [64, 8])
    SEGROW = C.sb("SEGROW", [128, 8, 64])
    for k in ("p", "s"):
        C.dma(MASKBIG[k].all(), I["maskbig_" + k])
        C.dma(SEGTRI[k].all(), I["segtri_" + k])
        C.dma(R01[k].all(), I["r01_" + k])
        C.dma(RNEG[k].all(), I["rneg_" + k])
    C.dma(SEGSEL.all(), I["segsel"])
    C.dma(SEGC.all(), I["segc"])
    C.dma(SEGROW.all(), I["segrow"].rearrange("p (n t) -> p n t", n=8))

    VEC1 = C.sb("VEC1", [128, 8, 8])
    VMU = C.sb("VMU", [128, 25, 1])
    VECO = C.sb("VECO", [128, 16, 2])
    GB = C.sb("GB", [8, 1])
    load_rows_T(VEC1, [I[n_] for n_ in ("m_hn_g", "r_w0", "r_a0", "r_kk", "r_ka", "r_ln_g", "r_ln_b", "r_rk")], 1024)
    load_rows_T(VMU, [I["r_mu"][:, 0:2048]], 2048)
    load_rows_T(VMU, [I["r_mu"][:, 2048:3200]], 1152, ct0=16)
    load_rows_T(VECO, [I["od_ln_g"], I["od_ln_b"]], 2048)
    C.dma(GB(None, GB.t[0:4, :]), I["m_ig_b"].rearrange("o h -> h o"), slow=True)
    C.dma(GB(None, GB.t[4:8, :]), I["m_fg_b"].rearrange("o h -> h o"), slow=True)

    VTM = C.sb("VTM", [128, 4, 257])
    KW = C.sb("KW", [128, 1024])
    KWN = C.sb("KWN", [128, 256])
    GX = C.sb("GX", [8, 128])
    ROW = C.sb("ROW", [128, 8, 128])
    COL = C.sb("COL", [128, 64])
    DTB = C.sb("DTB", [128, 128])
    STB = C.sb("STB", [128, 128])
    P1S = C.sb("P1S", [128, 257])
    NUM = C.sb("NUM", [128, 257])
    HN = C.sb("HN", [128, 256])
    SM = C.sb("SM", [128, 16])
    CS = C.sb("CS", [128, 4, 2, 257])
    MCAR = C.sb("MCAR", [128, 4])
    CSS = [C.sb("CSS%d" % i, [128, 2, 257]) for i in range(2)]
    CSO = [C.sb("CSO%d" % i, [128, 2, 257]) for i in range(2)]
    MS = C.sb("MS", [8, 4])
    MSB = C.sb("MSB", [8, 128])
    MINIT = C.sb("MINIT", [128, 4, 8])
    DEC = C.sb("DEC", [128, 4, 8])
    MNEW = C.sb("MNEW", [128, 4, 8])
    QM = [C.sb("QM%d" % i, [128, 64]) for i in range(2)]
    C.memset("pool", VTM(None, VTM.t[:, :, 256:257]), 1.0)

    def stream_mm_tok(W, c0, ncols, T, evac):
        Wv = W.rearrange("(kt p) c -> p kt c", p=128)
        for g in range(ncols // WG):
            ws = WS[rr["ws"] % 2]
            rr["ws"] += 1
            C.dma(ws(None, ws.t[:, :, 0:WG]), Wv[:, :, c0 + g * WG:c0 + (g + 1) * WG], q="pool")
            ps = next_psa()
            pv = ps(None, ps.t[0:T, 0:WG])
            for kt in range(16):
                C.mm(pv, XTB(("t", kt), XTB.t[:, kt, 0:T]), ws(None, ws.t[:, kt, 0:WG]), start=(kt == 0), stop=(kt == 15))
            evac(g, pv)

    def recip(e, out, in_):
        return C.op(e, lambda g: g.reciprocal(out=out.ap, in_=in_.ap), reads=[in_], writes=[out])

    def scan(out, d0, d1, init, op0, op1):
        rd = [d0, d1] + ([init] if isinstance(init, View) else [])
        ia = init.ap if isinstance(init, View) else init
        return C.op("dve", lambda g: g.tensor_tensor_scan(out=out.ap, data0=d0.ap, data1=d1.ap, initial=ia, op0=op0, op1=op1),
                    reads=rd, writes=[out])

    SR = C.sb("SR", [128, 8, 64])
    W2A2 = C.sb("W2A2", [128, 1024])
    BLK = C.sb("BLK", [128, 128])
    OMKA = C.sb("OMKA", [128, 8])
    SHC = C.sb("SHC", [128, 25])
    SUMB = C.sb("SUMB", [128, 8])
    C.dma(W2A2(None, W2A2.t[0:64, :]), I["r_w2"])
    C.dma(W2A2(None, W2A2.t[64:128, :]), I["r_a2"])
    C.dma(BLK.all(), I["blk"])
    C.ts("dve", OMKA.all(), VEC1(None, VEC1.t[:, :, 4]), -1.0, 1.0, ALU.mult, ALU.add)
    XIf = XI.t[:, :, :].rearrange("p a b -> p (a b)")
    T1 = XI("T1", XIf[:, 0:512].rearrange("p (j k) -> p j k", k=64))
    T2 = XI("T2", XIf[:, 512:1024].rearrange("p (j k) -> p j k", k=64))
    FSv = XIf[:, 1024:1536].rearrange("p (i t) -> p i t", t=128)
    SRS = [XI(("SRS", i), XIf[:, 1536 + 512 * i:2048 + 512 * i].rearrange("p (j k) -> p j k", k=64)) for i in range(2)]
    TWv = XIf[:, 2560:2688]
    ALLPS = PSA + PSB

    def next_ps8():
        rr["ps8"] = (rr.get("ps8", 0) + 1) % 8
        return ALLPS[rr["ps8"]]

    def rwkv(tile):
        T, nseq, L = tile["T"], tile["nseq"], tile["L"]
        is_s = tile["kind"] == "s"
        s0 = tile["s0"]
        Wx = 1 + L
        xrhs = lambda kt: XTB(("t", kt), XTB.t[:, kt, 0:T])
        pj = lambda j: PJ(("t", j), PJ.t[:, j, 0:T])
        pj3 = lambda j: PJ(("t", j), PJ.t[:, j, 0:T].rearrange("p (n l) -> p n l", l=L))

        def ppv(j0, j1, a, b):
            v = XR.t[:, j0:j1, 0:nseq * Wx].rearrange("p j (n w) -> p j n w", w=Wx)[:, :, :, a:b]
            return XR(("pp", j0) if j1 == j0 + 1 else None, v)

        if is_s:
            for (c0, ncol, ct0) in ((0, 2048, 0), (2048, 1152, 16)):
                C.dma(XS(None, XS.t[0:8, 0:ncol]), I["s_rsh"][s0:s0 + 8, c0:c0 + ncol])
                for ct in range(ncol // 128):
                    ps = next_psb()
                    C.tr(ps(None, ps.t[:, 0:8]), XS(None, XS.t[0:8, ct * 128:(ct + 1) * 128]), ident(None, ident.t[0:8, 0:8]))
                    C.copy("dve", ppv(ct0 + ct, ct0 + ct + 1, 0, 1), ps(None, ps.t[:, 0:8].rearrange("p (j n o) -> p j n o", j=1, o=1)))
        else:
            if tile["first"]:
                C.memset("pool", ppv(0, 25, 0, 1), 0.0)
            else:
                C.copy("pool", ppv(0, 25, 0, 1), SHC(None, SHC.t[:, :].unsqueeze(2).unsqueeze(3)))

        def evacR(j, pv):
            if j < 25:
                C.copy(ev_eng(), ppv(j, j + 1, 1, Wx), pv.buf(None, pv.ap.rearrange("p (j n l) -> p j n l", j=1, l=L)))
            else:
                C.act(pj(j), pv, AF.Silu)

        stream_mm(I["od_w_in"], 16, [(5128 + i * 128, 128) for i in range(33)], xrhs, T, evacR)

        if is_s:
            for (c0, ncol, ct0) in ((0, 2048, 0), (2048, 1152, 16)):
                for ct in range(ncol // 128):
                    ps = next_psb()
                    C.copy("pool", TMP(None, TMP.t[:, 0:8]), XR(("pp", ct0 + ct), XR.t[:, ct0 + ct, 0:nseq * Wx].rearrange("p (n w) -> p n w", w=Wx)[:, :, L]))
                    C.tr(ps(None, ps.t[0:8, 0:128]), TMP(None, TMP.t[:, 0:8]), ident.all())
                    C.copy("dve", XS(None, XS.t[0:8, ct * 128:(ct + 1) * 128]), ps(None, ps.t[0:8, 0:128]))
                C.dma(O["s_rsh_o"][s0:s0 + 8, c0:c0 + ncol], XS(None, XS.t[0:8, 0:ncol]))
        else:
            C.copy("pool", SHC(None, SHC.t[:, :].unsqueeze(2).unsqueeze(3)), ppv(0, 25, L, L + 1))
            if tile["last"]:
                for (c0, ncol, ct0) in ((0, 2048, 0), (2048, 1152, 16)):
                    for ct in range(ncol // 128):
                        ps = next_psb()
                        C.tr(ps(None, ps.t[0:1, 0:128]), SHC(None, SHC.t[:, ct0 + ct:ct0 + ct + 1]), ident.all())
                        C.copy("dve", XS(None, XS.t[0:1, ct * 128:(ct + 1) * 128]), ps(None, ps.t[0:1, 0:128]))
                    C.dma(O["p_rsh"][:, c0:c0 + ncol], XS(None, XS.t[0:1, 0:ncol]))
        for j in range(25):
            C.tt("pool", pj3(j), XR(("pp", j), XR.t[:, j, 0:nseq * Wx].rearrange("p (n w) -> p n w", w=Wx)[:, :, 0:L]),
                 XR(("pp", j), XR.t[:, j, 0:nseq * Wx].rearrange("p (n w) -> p n w", w=Wx)[:, :, 1:Wx]), ALU.subtract)
            C.stt("dve", pj3(j), pj3(j), VMU(None, VMU.t[:, j, 0:1]),
                  XR(("pp", j), XR.t[:, j, 0:nseq * Wx].rearrange("p (n w) -> p n w", w=Wx)[:, :, 1:Wx]), ALU.mult, ALU.add)

        VTMf = VTM.t[:, :, :].rearrange("p a b -> p (a b)")
        ROWf = ROW.t[:, :, :].rearrange("p a b -> p (a b)")
        KKt = lambda a, b: KW(None, KW.t[0:T, a:b])
        Wt = lambda a, b: VTM(None, VTMf[0:T, a:b])
        KKAt = lambda a, b: ROW(None, ROWf[0:T, a:b])
        KPt = lambda a, b: XS(None, XS.t[0:T, a:b])
        Rt = lambda a, b: XS(None, XS.t[0:T, 1024 + a:1024 + b])
        fs = lambda i: XI(("FS", i), FSv[:, i, 0:T])
        tw = XI("TW", TWv[0:64, 0:T])
        C.act(tw, PJ(("t", 24), PJ.t[0:64, 24, 0:T]), AF.Tanh)

        def to_tok(dst, src):
            ps = next_psb()
            C.tr(ps(None, ps.t[0:T, 0:128]), src, ident.all())
            C.copy(ev_eng(), dst, ps(None, ps.t[0:T, 0:128]))

        NE05 = -float(np.exp(-0.5))
        for ct in range(8):
            r_, k_, v_ = pj(ct), pj(8 + ct), pj(16 + ct)
            cs_ = slice(ct * 128, (ct + 1) * 128)
            ps = next_psa()
            C.mm(ps(None, ps.t[:, 0:T]), W2A2(None, W2A2.t[0:64, cs_]), tw)
            C.act(fs(0), ps(None, ps.t[:, 0:T]), AF.Sigmoid, bias=VEC1(None, VEC1.t[:, ct, 1:2]))
            C.act(fs(0), fs(0), AF.Exp, scale=NE05)
            to_tok(Wt(ct * 128, (ct + 1) * 128), fs(0))
            ps = next_psa()
            C.mm(ps(None, ps.t[:, 0:T]), W2A2(None, W2A2.t[64:128, cs_]), PJ(("t", 24), PJ.t[64:128, 24, 0:T]))
            C.act(fs(1), ps(None, ps.t[:, 0:T]), AF.Sigmoid, bias=VEC1(None, VEC1.t[:, ct, 2:3]))
            C.ts("dve", fs(2), k_, VEC1(None, VEC1.t[:, ct, 3:4]), None, ALU.mult)
            C.tt("pool", fs(3), fs(2), fs(2), ALU.mult)
            ps = next_psa()
            C.mm(ps(None, ps.t[:, 0:T]), BLK.all(), fs(3))
            C.act(fs(3), ps(None, ps.t[:, 0:T]), AF.Sqrt)
            C.ts("dve", fs(3), fs(3), 1e-12, None, ALU.max)
            recip("dve", fs(3), fs(3))
            C.tt("dve", fs(2), fs(2), fs(3), ALU.mult)
            to_tok(KKt(ct * 128, (ct + 1) * 128), fs(2))
            C.tt("dve", fs(3), fs(2), fs(1), ALU.mult)
            to_tok(KKAt(ct * 128, (ct + 1) * 128), fs(3))
            C.ts("dve", fs(1), fs(1), VEC1(None, VEC1.t[:, ct, 4:5]), OMKA(None, OMKA.t[:, ct:ct + 1]), ALU.mult, ALU.add)
            C.tt("dve", fs(1), fs(1), k_, ALU.mult)
            to_tok(KPt(ct * 128, (ct + 1) * 128), fs(1))
            to_tok(Rt(ct * 128, (ct + 1) * 128), r_)
            C.tt("dve", fs(3), r_, fs(1), ALU.mult)
            C.ts("dve", fs(3), fs(3), VEC1(None, VEC1.t[:, ct, 7:8]), None, ALU.mult)
            ps = next_psa()
            C.mm(ps(None, ps.t[:, 0:T]), BLK.all(), fs(3))
            C.tt("dve", ACC(("t", ct), ACC.t[:, ct, 0:T]), ps(None, ps.t[:, 0:T]), v_, ALU.mult)

        Yv = MIX.t[:, 8 * TT:16 * TT].rearrange("p (j t) -> p j t", j=8)
        srcs = (("kk", KW.t[0:T, :]), ("w", VTMf[0:T, 0:1024]), ("kka", ROWf[0:T, 0:1024]), ("kp", XS.t[0:T, 0:1024]), ("r", XS.t[0:T, 1024:2048]))
        bufs = {"kk": KW, "w": VTM, "kka": ROW, "kp": XS, "r": XS}
        for n in range(nseq):
            if is_s:
                sr = SRS[n % 2]
                C.dma(sr, I["s_rs"][s0 + n].rearrange("(j hp) v k -> (hp v) j k", hp=2))
            else:
                sr = SR.all()
                if tile["first"]:
                    C.memset("pool", sr, 0.0)
            for l in range(L):
                t = n * L + l
                oh = ident(None, ident.t[0:T, t:t + 1].broadcast_to([T, 64]))
                bc = {}
                for nm, ap in srcs:
                    ps = next_ps8()
                    xv = ap.rearrange("p (j hp k) -> p hp j k", hp=2, k=64)
                    C.mm(ps(None, ps.t[0:64, 0:512]), oh, bufs[nm](None, xv[:, 0]))
                    C.mm(ps(None, ps.t[64:128, 0:512]), oh, bufs[nm](None, xv[:, 1]))
                    bc[nm] = ps(None, ps.t[:, 0:512].rearrange("p (j k) -> p j k", k=64))
                C.tt("dve", T1, sr, bc["kk"], ALU.mult)
                C.op("dve", lambda g: g.reduce_sum(out=SUMB.t[:, :], in_=T1.ap, axis=AX.X), reads=[T1], writes=[SUMB.all()])
                C.tt("dve", sr, sr, bc["w"], ALU.mult)
                C.tt("dve", T2, bc["kka"], SUMB(None, SUMB.t[:, :].unsqueeze(2).broadcast_to([128, 8, 64])), ALU.mult)
                C.tt("dve", sr, sr, T2, ALU.subtract)
                C.tt("dve", T1, bc["kp"], PJ(None, PJ.t[:, 16:24, t].unsqueeze(2).broadcast_to([128, 8, 64])), ALU.mult)
                C.tt("dve", sr, sr, T1, ALU.add)
                C.tt("dve", T2, sr, bc["r"], ALU.mult)
                yv = MIX(None, Yv[:, :, t])
                C.op("dve", lambda g, yv=yv: g.reduce_sum(out=yv.ap, in_=T2.ap, axis=AX.X), reads=[T2], writes=[yv])
            if is_s:
                C.dma(O["s_rs_o"][s0 + n].rearrange("(j hp) v k -> (hp v) j k", hp=2), sr)
        if (not is_s) and tile["last"]:
            C.dma(O["p_rs"].rearrange("(j hp) v k -> (hp v) j k", hp=2), SR.all())

        for j in range(8):
            y = mixv(8 + j, T)
            ps = next_psa()
            C.mm(ps(None, ps.t[:, 0:T]), BLK.all(), y)
            C.tt("pool", fs(0), y, y, ALU.mult)
            ps2 = next_psa()
            C.mm(ps2(None, ps2.t[:, 0:T]), BLK.all(), fs(0))
            C.ts("dve", fs(1), ps(None, ps.t[:, 0:T]), 1.0 / 64, None, ALU.mult)
            C.ts("dve", fs(2), ps2(None, ps2.t[:, 0:T]), 1.0 / 64, None, ALU.mult)
            C.tt("dve", fs(3), fs(1), fs(1), ALU.mult)
            C.tt("dve", fs(2), fs(2), fs(3), ALU.subtract)
            C.ts("dve", fs(2), fs(2), 64e-5, None, ALU.add)
            C.act(fs(2), fs(2), AF.Sqrt)
            recip("dve", fs(2), fs(2))
            C.tt("dve", y, y, fs(1), ALU.subtract)
            C.tt("dve", y, y, fs(2), ALU.mult)
            C.act(y, y, AF.Identity, bias=VEC1(None, VEC1.t[:, j, 6:7]), scale=VEC1(None, VEC1.t[:, j, 5:6]))
            C.tt("dve", y, y, ACC(("t", j), ACC.t[:, j, 0:T]), ALU.add)
            C.tt("dve", y, y, pj(25 + j), ALU.mult)

    SSTRI = {"p": C.sb("sstri_p", [128, 128]), "s": C.sb("sstri_s", [64, 64])}
    SSTRIT = {"p": C.sb("sstriT_p", [128, 128]), "s": C.sb("sstriT_s", [64, 64])}
    for k_ in ("p", "s"):
        C.dma(SSTRI[k_].all(), I["sstri_" + k_])
        C.dma(SSTRIT[k_].all(), I["sstriT_" + k_])
    WLB = C.sb("WLB", [128, 8, 8])
    HBf = HB.t[:, :, :].rearrange("p a b -> p (a b)")
    KKv = KW.t[:, :].rearrange("p (j t) -> p j t", t=128)
    BTv = ROW.t
    XRf = XR.t[:, :, :].rearrange("p a b -> p (a b)")
    S0Tv = XRf[:, 0:4096].rearrange("p (n j v) -> p n j v", n=8, j=8)

    def fence(buf, ap):
        C.op("pool", lambda g: g.memset(ap, 0.0), writes=[buf.all()])

    def rwkv2(tile):
        T, nseq, L = tile["T"], tile["nseq"], tile["L"]
        is_s = tile["kind"] == "s"
        kd = tile["kind"]
        s0 = tile["s0"]
        Wx = 1 + L
        xrhs = lambda kt: XTB(("t", kt), XTB.t[:, kt, 0:T])
        pj = lambda j: PJ(("t", j), PJ.t[:, j, 0:T])
        pj3 = lambda j: PJ(("t", j), PJ.t[:, j, 0:T].rearrange("p (n l) -> p n l", l=L))

        def ppv(j0, j1, a, b):
            v = XR.t[:, j0:j1, 0:nseq * Wx].rearrange("p j (n w) -> p j n w", w=Wx)[:, :, :, a:b]
            return XR(("pp", j0) if j1 == j0 + 1 else None, v)

        if is_s:
            for (c0, ncol, ct0) in ((0, 2048, 0), (2048, 1152, 16)):
                C.dma(XS(None, XS.t[0:8, 0:ncol]), I["s_rsh"][s0:s0 + 8, c0:c0 + ncol])
                for ct in range(ncol // 128):
                    ps = next_psb()
                    C.tr(ps(None, ps.t[:, 0:8]), XS(None, XS.t[0:8, ct * 128:(ct + 1) * 128]), ident(None, ident.t[0:8, 0:8]))
                    C.copy("dve", ppv(ct0 + ct, ct0 + ct + 1, 0, 1), ps(None, ps.t[:, 0:8].rearrange("p (j n o) -> p j n o", j=1, o=1)))
        else:
            if tile["first"]:
                C.memset("pool", ppv(0, 25, 0, 1), 0.0)
            else:
                C.copy("pool", ppv(0, 25, 0, 1), SHC(None, SHC.t[:, :].unsqueeze(2).unsqueeze(3)))

        def evacR(j, pv):
            if j < 25:
                C.copy(ev_eng(), ppv(j, j + 1, 1, Wx), pv.buf(None, pv.ap.rearrange("p (j n l) -> p j n l", j=1, l=L)))
            else:
                C.act(pj(j), pv, AF.Silu)

        stream_mm(I["od_w_in"], 16, [(5128 + i * 128, 128) for i in range(33)], xrhs, T, evacR)

        if is_s:
            for (c0, ncol, ct0) in ((0, 2048, 0), (2048, 1152, 16)):
                for ct in range(ncol // 128):
                    ps = next_psb()
                    C.copy("pool", TMP(None, TMP.t[:, 0:8]), XR(("pp", ct0 + ct), XR.t[:, ct0 + ct, 0:nseq * Wx].rearrange("p (n w) -> p n w", w=Wx)[:, :, L]))
                    C.tr(ps(None, ps.t[0:8, 0:128]), TMP(None, TMP.t[:, 0:8]), ident.all())
                    C.copy("dve", XS(None, XS.t[0:8, ct * 128:(ct + 1) * 128]), ps(None, ps.t[0:8, 0:128]))
                C.dma(O["s_rsh_o"][s0:s0 + 8, c0:c0 + ncol], XS(None, XS.t[0:8, 0:ncol]))
        else:
            C.copy("pool", SHC(None, SHC.t[:, :].unsqueeze(2).unsqueeze(3)), ppv(0, 25, L, L + 1))
            if tile["last"]:
                for (c0, ncol, ct0) in ((0, 2048, 0), (2048, 1152, 16)):
                    for ct in range(ncol // 128):
                        ps = next_psb()
                        C.tr(ps(None, ps.t[0:1, 0:128]), SHC(None, SHC.t[:, ct0 + ct:ct0 + ct + 1]), ident.all())
                        C.copy("dve", XS(None, XS.t[0:1, ct * 128:(ct + 1) * 128]), ps(None, ps.t[0:1, 0:128]))
                    C.dma(O["p_rsh"][:, c0:c0 + ncol], XS(None, XS.t[0:1, 0:ncol]))
        for j in range(25):
            C.tt("pool", pj3(j), XR(("pp", j), XR.t[:, j, 0:nseq * Wx].rearrange("p (n w) -> p n w", w=Wx)[:, :, 0:L]),
                 XR(("pp", j), XR.t[:, j, 0:nseq * Wx].rearrange("p (n w) -> p n w", w=Wx)[:, :, 1:Wx]), ALU.subtract)
            C.stt("dve", pj3(j), pj3(j), VMU(None, VMU.t[:, j, 0:1]),
                  XR(("pp", j), XR.t[:, j, 0:nseq * Wx].rearrange("p (n w) -> p n w", w=Wx)[:, :, 1:Wx]), ALU.mult, ALU.add)

        s0t = lambda n, j, rs=slice(0, 128): XR(("st", n), S0Tv[rs, n, j, :])
        if is_s:
            fence(XR, XR.t[0:1, 0, 0:1])
            for n2 in range(0, 8, 2):
                stg = XS.t[:, :].rearrange("p (n j d k) -> p n j d k", n=2, j=8, d=2)
                for nn in range(2):
                    for d in range(2):
                        C.dma(XS(None, stg[:, nn, :, d, :]), I["s_rs"][s0 + n2 + nn].rearrange("(j hp) v k -> (hp v) j k", hp=2))
                for nn in range(2):
                    n = n2 + nn
                    for j in range(8):
                        ps = next_psb()
                        C.tr(ps(None, ps.t[:, 0:128]), XS(None, stg[:, nn, j, :, :]), ident.all())
                        C.copy("dve", s0t(n, j, slice(0, 64)), ps(None, ps.t[0:64, 0:64]))
                        C.copy("act", s0t(n, j, slice(64, 128)), ps(None, ps.t[64:128, 64:128]))
        else:
            if tile["first"]:
                C.memset("pool", SR.all(), 0.0)

        VTMf = VTM.t[:, :, :].rearrange("p a b -> p (a b)")
        Vtm = lambda hc: XS(None, XS.t[0:T, hc])
        Btm = lambda hc: XS(None, XS.t[0:T, 1024 + hc.start:1024 + hc.stop])
        Ktm = lambda hc: VTM(None, VTMf[0:T, hc])
        fs = lambda i: XI(("FS", i), FSv[:, i, 0:T])
        tw = XI("TW", TWv[0:64, 0:T])
        C.act(tw, PJ(("t", 24), PJ.t[0:64, 24, 0:T]), AF.Tanh)

        def to_tok(dst, src):
            ps = next_psb()
            C.tr(ps(None, ps.t[0:T, 0:128]), src, ident.all())
            C.copy(ev_eng(), dst, ps(None, ps.t[0:T, 0:128]))

        NE05 = -float(np.exp(-0.5))
        kkc = lambda ct, rs=slice(0, 128): KW(("c", ct), KKv[rs, ct, 0:T])
        btc = lambda ct, rs=slice(0, 128): ROW(("c", ct), BTv[rs, ct, 0:T])
        for ct in range(8):
            r_, k_, v_ = pj(ct), pj(8 + ct), pj(16 + ct)
            cs_ = slice(ct * 128, (ct + 1) * 128)
            ps = next_psa()
            C.mm(ps(None, ps.t[:, 0:T]), W2A2(None, W2A2.t[0:64, cs_]), tw)
            C.act(fs(0), ps(None, ps.t[:, 0:T]), AF.Sigmoid, bias=VEC1(None, VEC1.t[:, ct, 1:2]))
            C.ts("dve", fs(0), fs(0), NE05, None, ALU.mult)
            scan(fs(1), R01[kd](None, R01[kd].t[:, 0:T]), fs(0), 0.0, ALU.mult, ALU.add)
            ps = next_psa()
            C.mm(ps(None, ps.t[:, 0:T]), W2A2(None, W2A2.t[64:128, cs_]), PJ(("t", 24), PJ.t[64:128, 24, 0:T]))
            C.act(fs(2), ps(None, ps.t[:, 0:T]), AF.Sigmoid, bias=VEC1(None, VEC1.t[:, ct, 2:3]))
            C.ts("dve", kkc(ct), k_, VEC1(None, VEC1.t[:, ct, 3:4]), None, ALU.mult)
            C.tt("pool", fs(3), kkc(ct), kkc(ct), ALU.mult)
            ps = next_psa()
            C.mm(ps(None, ps.t[:, 0:T]), BLK.all(), fs(3))
            C.act(fs(3), ps(None, ps.t[:, 0:T]), AF.Sqrt)
            C.ts("dve", fs(3), fs(3), 1e-12, None, ALU.max)
            recip("dve", fs(3), fs(3))
            C.tt("dve", kkc(ct), kkc(ct), fs(3), ALU.mult)
            C.tt("dve", btc(ct), kkc(ct), fs(2), ALU.mult)
            C.ts("dve", fs(2), fs(2), VEC1(None, VEC1.t[:, ct, 4:5]), OMKA(None, OMKA.t[:, ct:ct + 1]), ALU.mult, ALU.add)
            C.tt("dve", fs(2), fs(2), k_, ALU.mult)
            C.tt("pool", fs(3), r_, fs(2), ALU.mult)
            C.ts("dve", fs(3), fs(3), VEC1(None, VEC1.t[:, ct, 7:8]), None, ALU.mult)
            ps = next_psa()
            C.mm(ps(None, ps.t[:, 0:T]), BLK.all(), fs(3))
            C.tt("dve", ACC(("t", ct), ACC.t[:, ct, 0:T]), ps(None, ps.t[:, 0:T]), v_, ALU.mult)
            C.act(fs(3), fs(1), AF.Exp)
            C.tt("dve", r_, r_, fs(3), ALU.mult)
            C.copy("pool", WLB(None, WLB.t[:, ct, 0:nseq]),
                   XI(("FS", 3), FSv[:, 3, 0:T].rearrange("p (n l) -> p n l", l=L)[:, :, L - 1]))
            C.tt("dve", fs(3), fs(1), fs(0), ALU.subtract)
            C.act(fs(3), fs(3), AF.Exp)
            C.tt("dve", kkc(ct), kkc(ct), fs(3), ALU.mult)
            C.act(fs(3), fs(1), AF.Exp, scale=-1.0)
            C.tt("dve", btc(ct), btc(ct), fs(3), ALU.mult)
            C.tt("dve", k_, fs(2), fs(3), ALU.mult)
            to_tok(Vtm(cs_), v_)
            to_tok(Btm(cs_), btc(ct))
            to_tok(Ktm(cs_), k_)

        fence(HB, HB.t[0:1, 0, 0:1])
        mat = lambda i: HB(("m", i), HBf[0:T, i * 128:i * 128 + T])
        half = lambda i, a: HB(("m", i), HBf[0:T, i * 128 + 64 * a:i * 128 + 64 * a + 64])
        nsq = max(0, int(np.ceil(np.log2(L))) - 1)
        idT = ident(None, ident.t[0:T, 0:T])
        mS = SSTRI[kd](None, SSTRI[kd].t[0:T, 0:T])
        mST = SSTRIT[kd](None, SSTRIT[kd].t[0:T, 0:T])
        mI = SEGTRI[kd](None, SEGTRI[kd].t[0:T, 0:T])
        if is_s:
            KKM, RM = T1, T2
        for j in range(8):
            Q = [[mat(8 * hp + 0), mat(8 * hp + 1)] for hp in range(2)]
            QT = [[mat(8 * hp + 2), mat(8 * hp + 3)] for hp in range(2)]
            Pm = [mat(8 * hp + 4) for hp in range(2)]
            BR = [mat(8 * hp + 5) for hp in range(2)]
            AK = [mat(8 * hp + 6) for hp in range(2)]
            KR = [mat(8 * hp + 7) for hp in range(2)]
            RHS = [half(16, hp) for hp in range(2)]
            SAT = [half(17, hp) for hp in range(2)]
            rsl = [slice(0, 64), slice(64, 128)]
            if is_s:
                C.tt("pool", KKM, KW(("c", j), KKv[:, j, 0:T].unsqueeze(1).broadcast_to([128, 8, T])), SEGROW(None, SEGROW.t[:, :, 0:T]), ALU.mult)
                C.tt("pool", RM, PJ(("t", j), PJ.t[:, j, 0:T].unsqueeze(1).broadcast_to([128, 8, T])), SEGROW(None, SEGROW.t[:, :, 0:T]), ALU.mult)
            for hp in range(2):
                rs = rsl[hp]
                rq = PJ(("t", j), PJ.t[rs, j, 0:T])
                kq = PJ(("t", 8 + j), PJ.t[rs, 8 + j, 0:T])
                ps = next_ps8()
                C.mm(ps(None, ps.t[0:T, 0:T]), btc(j, rs), kkc(j, rs))
                C.mm(ps(None, ps.t[0:T, T:2 * T]), btc(j, rs), rq)
                C.tt("dve", Q[hp][0], ps(None, ps.t[0:T, 0:T]), mS, ALU.mult)
                C.tt("dve", BR[hp], ps(None, ps.t[0:T, T:2 * T]), mI, ALU.mult)
                ps = next_ps8()
                C.mm(ps(None, ps.t[0:T, 0:T]), kq, kkc(j, rs))
                C.mm(ps(None, ps.t[0:T, T:2 * T]), kq, rq)
                C.tt("dve", AK[hp], ps(None, ps.t[0:T, 0:T]), mS, ALU.mult)
                C.tt("dve", KR[hp], ps(None, ps.t[0:T, T:2 * T]), mI, ALU.mult)
                ps = next_ps8()
                C.mm(ps(None, ps.t[0:T, 0:T]), kkc(j, rs), btc(j, rs))
                C.tt("dve", QT[hp][0], ps(None, ps.t[0:T, 0:T]), mST, ALU.mult)
                C.stt("dve", Pm[hp], Q[hp][0], -1.0, idT, ALU.mult, ALU.add)
            cur = 0
            for it in range(nsq):
                nxt = 1 - cur
                last_it = (it == nsq - 1)
                for hp in range(2):
                    if not last_it:
                        ps = next_ps8()
                        C.mm(ps(None, ps.t[0:T, 0:T]), QT[hp][cur], Q[hp][cur])
                        C.copy("act", Q[hp][nxt], ps(None, ps.t[0:T, 0:T]))
                    ps = next_ps8()
                    C.mm(ps(None, ps.t[0:T, 0:T]), Q[hp][cur], QT[hp][cur])
                    C.copy("dve", QT[hp][nxt], ps(None, ps.t[0:T, 0:T]))
                for hp in range(2):
                    ps = next_ps8()
                    C.mm(ps(None, ps.t[0:T, 0:T]), QT[hp][nxt], Pm[hp])
                    C.tt("dve", Pm[hp], Pm[hp], ps(None, ps.t[0:T, 0:T]), ALU.add)
                cur = nxt
            for hp in range(2):
                rs = rsl[hp]
                hc = slice((2 * j + hp) * 64, (2 * j + hp) * 64 + 64)
                ps = next_ps8()
                if is_s:
                    for n in range(nseq):
                        C.mm(ps(None, ps.t[0:T, 0:64]), XI("T1", KKM.ap[rs, n, :]), s0t(n, j, rs), start=(n == 0), stop=False)
                else:
                    C.mm(ps(None, ps.t[0:T, 0:64]), kkc(j, rs), SR(None, SR.t[rs, j, :]), start=True, stop=False)
                C.mm(ps(None, ps.t[0:T, 0:64]), AK[hp], Vtm(hc), start=False, stop=True)
                C.act(RHS[hp], ps(None, ps.t[0:T, 0:64]), AF.Identity, scale=-1.0)
                ps = next_ps8()
                C.mm(ps(None, ps.t[0:T, 0:64]), Pm[hp], RHS[hp])
                C.copy("dve", SAT[hp], ps(None, ps.t[0:T, 0:64]))
            psY = next_ps8()
            for hp in range(2):
                rs = rsl[hp]
                hc = slice((2 * j + hp) * 64, (2 * j + hp) * 64 + 64)
                ov = psY(None, psY.t[rs, 0:T])
                if is_s:
                    for n in range(nseq):
                        C.mm(ov, s0t(n, j, rs), XI("T2", RM.ap[rs, n, :]), start=(n == 0), stop=False)
                else:
                    C.mm(ov, SR(None, SR.t[rs, j, :]), PJ(("t", j), PJ.t[rs, j, 0:T]), start=True, stop=False)
                C.mm(ov, SAT[hp], BR[hp], start=False, stop=False)
                C.mm(ov, Vtm(hc), KR[hp], start=False, stop=True)
            y = TMP(None, TMP.t[:, 0:T])
            C.copy("act", y, psY(None, psY.t[:, 0:T]))
            ps = next_psa()
            C.mm(ps(None, ps.t[:, 0:T]), BLK.all(), y)
            C.tt("pool", fs(0), y, y, ALU.mult)
            ps2 = next_psa()
            C.mm(ps2(None, ps2.t[:, 0:T]), BLK.all(), fs(0))
            C.ts("dve", fs(1), ps(None, ps.t[:, 0:T]), 1.0 / 64, None, ALU.mult)
            C.ts("dve", fs(2), ps2(None, ps2.t[:, 0:T]), 1.0 / 64, None, ALU.mult)
            C.tt("dve", fs(3), fs(1), fs(1), ALU.mult)
            C.tt("dve", fs(2), fs(2), fs(3), ALU.subtract)
            C.ts("dve", fs(2), fs(2), 64e-5, None, ALU.add)
            C.act(fs(2), fs(2), AF.Sqrt)
            recip("dve", fs(2), fs(2))
            C.tt("dve", y, y, fs(1), ALU.subtract)
            C.tt("dve", y, y, fs(2), ALU.mult)
            C.act(y, y, AF.Identity, bias=VEC1(None, VEC1.t[:, j, 6:7]), scale=VEC1(None, VEC1.t[:, j, 5:6]))
            C.tt("dve", y, y, ACC(("t", j), ACC.t[:, j, 0:T]), ALU.add)
            C.tt("dve", mixv(8 + j, T), y, pj(25 + j), ALU.mult)
            tmpS = XI(("FS", 0), FSv[:, 0, 0:64])
            if is_s:
                SAM = [CSS[hp](None, CSS[hp].t[0:T, :, :].rearrange("p a b -> p (a b)")[:, 0:512].rearrange("p (n v) -> p n v", n=8)) for hp in range(2)]
                VM = [CSO[hp](None, CSO[hp].t[0:T, :, :].rearrange("p a b -> p (a b)")[:, 0:512].rearrange("p (n v) -> p n v", n=8)) for hp in range(2)]
                segc_b = SEGC(None, SEGC.t[0:T, :].unsqueeze(2).broadcast_to([T, 8, 64]))
                for hp in range(2):
                    hc = slice((2 * j + hp) * 64, (2 * j + hp) * 64 + 64)
                    C.tt("pool", SAM[hp], HB(("m", 17), HBf[0:T, 17 * 128 + 64 * hp:17 * 128 + 64 * hp + 64].unsqueeze(1).broadcast_to([T, 8, 64])), segc_b, ALU.mult)
                    C.tt("pool", VM[hp], XS(None, XS.t[0:T, hc].unsqueeze(1).broadcast_to([T, 8, 64])), segc_b, ALU.mult)
                for n in range(nseq):
                    psS = next_ps8()
                    for hp in range(2):
                        rs = rsl[hp]
                        hc = slice((2 * j + hp) * 64, (2 * j + hp) * 64 + 64)
                        C.mm(psS(None, psS.t[rs, 0:64]), Btm(hc), CSS[hp](None, SAM[hp].ap[:, n, :]), start=True, stop=False)
                        C.mm(psS(None, psS.t[rs, 0:64]), Ktm(hc), CSO[hp](None, VM[hp].ap[:, n, :]), start=False, stop=True)
                    wl = WLB(None, WLB.t[:, j, n:n + 1])
                    C.ts("dve", tmpS, s0t(n, j), wl, None, ALU.mult)
                    C.stt("dve", s0t(n, j), psS(None, psS.t[:, 0:64]), wl, tmpS, ALU.mult, ALU.add)
            else:
                psS = next_ps8()
                for hp in range(2):
                    rs = rsl[hp]
                    hc = slice((2 * j + hp) * 64, (2 * j + hp) * 64 + 64)
                    C.mm(psS(None, psS.t[rs, 0:64]), Btm(hc), SAT[hp], start=True, stop=False)
                    C.mm(psS(None, psS.t[rs, 0:64]), Ktm(hc), Vtm(hc), start=False, stop=True)
                wl = WLB(None, WLB.t[:, j, 0:1])
                srj = SR(None, SR.t[:, j, :])
                C.ts("dve", tmpS, srj, wl, None, ALU.mult)
                C.stt("dve", srj, psS(None, psS.t[:, 0:64]), wl, tmpS, ALU.mult, ALU.add)

        def state_out(src_fn, dst):
            for j in range(8):
                ps = next_psb()
                C.tr(ps(None, ps.t[0:64, 0:128]), src_fn(j), ident.all())
                C.copy(ev_eng(), XS(None, XS.t[0:64, j * 128:(j + 1) * 128]), ps(None, ps.t[0:64, 0:128]))
            C.dma(dst.rearrange("(j hp) v k -> v j hp k", hp=2), XS(None, XS.t[0:64, 0:1024].rearrange("p (j hp k) -> p j hp k", j=8, hp=2)))

        if is_s:
            for n in range(nseq):
                state_out(lambda j, n=n: s0t(n, j), O["s_rs_o"][s0 + n])
        elif tile["last"]:
            state_out(lambda j: SR(None, SR.t[:, j, :]), O["p_rs"])


    def layer1(tile):
        T, nseq, L = tile["T"], tile["nseq"], tile["L"]
        is_s = tile["kind"] == "s"
        kd = tile["kind"]
        s0 = tile["s0"]
        xrhs = lambda kt: XTB(("t", kt), XTB.t[:, kt, 0:T])
        pj = lambda j: PJ(("t", j), PJ.t[:, j, 0:T])
        W = I["od_w_in"]

        def evacM(j, pv):
            if j < 8:
                C.act(pj(j), pv, AF.Identity, scale=1.0 / 16.0)
            elif j < 16:
                C.copy(ev_eng(), pj(j), pv)
            elif j < 24:
                C.act(pj(j), pv, AF.Sigmoid)
            elif j == 24:
                C.copy("dve", PJ(("t", 32), PJ.t[0:8, 32, 0:T]), pv)
            else:
                C.act(pj(j - 1), pv, AF.Silu)

        cols = [(i * 128, 128) for i in range(16)] + [(3072 + i * 128, 128) for i in range(8)] + [(4096, 8)] + \
               [(4104 + i * 128, 128) for i in range(8)]
        stream_mm(W, 16, cols, xrhs, T, evacM)

        def evacV(g, pv):
            hh, half = divmod(g, 256 // WG)
            C.copy(ev_eng(), VTM(None, VTM.t[0:T, hh, half * WG:(half + 1) * WG]), pv)

        C.memset("pool", VTM(None, VTM.t[:, :, 256:257]), 1.0)
        stream_mm_tok(W, 2048, 1024, T, evacV)

        gx = GX(None, GX.t[0:8, 0:T])
        C.ts("dve", gx, PJ(("t", 32), PJ.t[0:8, 32, 0:T]), GB.all(), None, ALU.add)
        col = lambda a, b: COL(None, COL.t[0:T, a:b])
        ps = next_psb()
        C.tr(ps(None, ps.t[0:T, 0:8]), gx, ident(None, ident.t[0:8, 0:8]))
        C.copy("dve", col(0, 8), ps(None, ps.t[0:T, 0:8]))
        C.act(col(8, 12), col(4, 8), AF.Exp, scale=-1.0)
        C.act(col(8, 12), col(8, 12), AF.Ln, bias=1.0)
        C.ts("dve", col(8, 12), col(8, 12), -1.0, None, ALU.mult)
        ps = next_psb()
        C.mm(ps(None, ps.t[0:T, 0:4]), SEGTRI[kd](None, SEGTRI[kd].t[0:T, 0:T]), col(8, 12))
        C.copy("dve", col(12, 16), ps(None, ps.t[0:T, 0:4]))
        C.tt("dve", col(16, 20), col(0, 4), col(12, 16), ALU.subtract)
        if is_s:
            C.dma(MS.all(), I["s_mm"][s0:s0 + 8, :])
            ps = next_psb()
            C.mm(ps(None, ps.t[0:T, 0:4]), SEGSEL(None, SEGSEL.t[0:8, 0:T]), MS.all())
            C.copy("dve", col(20, 24), ps(None, ps.t[0:T, 0:4]))
            for h in range(4):
                C.copy("dve", MSB.all(), MS(None, MS.t[:, h:h + 1].broadcast_to([8, 128])))
                ps = next_psb()
                C.mm(ps(None, ps.t[:, 0:8]), MSB.all(), ident(None, ident.t[0:8, 0:8]))
                C.copy("dve", MINIT(None, MINIT.t[:, h, :]), ps(None, ps.t[:, 0:8]))
        else:
            if tile["first"]:
                C.memset("dve", MCAR.all(), 0.0)
                C.memset("pool", CS.all(), 0.0)
            C.copy("dve", col(20, 24), MCAR(None, MCAR.t[0:T, :]))
            C.copy("dve", MINIT(None, MINIT.t[:, :, 0:1]), MCAR(None, MCAR.t[:, :].unsqueeze(2)))

        row = lambda k: ROW(None, ROW.t[:, k, 0:T])
        ends = lambda k: ROW(None, ROW.t[:, k, 0:T].rearrange("p (n l) -> p n l", l=L)[:, :, L - 1])
        starts = lambda k: ROW(None, ROW.t[:, k, 0:T].rearrange("p (n l) -> p n l", l=L)[:, :, 0])
        for h in range(4):
            minit_r = MINIT(None, MINIT.t[:, h, 0:nseq])
            ps = next_psb()
            C.mm(ps(None, ps.t[:, 0:T]), ident(None, ident.t[0:8, h:h + 1].broadcast_to([8, 128])), gx)
            C.copy("dve", row(0), ps(None, ps.t[:, 0:T]))
            ps = next_psb()
            C.mm(ps(None, ps.t[:, 0:T]), ident(None, ident.t[0:8, 4 + h:5 + h].broadcast_to([8, 128])), gx)
            C.act(row(1), ps(None, ps.t[:, 0:T]), AF.Exp, scale=-1.0)
            C.act(row(1), row(1), AF.Ln, bias=1.0)
            C.ts("dve", row(1), row(1), -1.0, None, ALU.mult)
            scan(row(2), R01[kd](None, R01[kd].t[:, 0:T]), row(1), 0.0, ALU.mult, ALU.add)
            C.tt("dve", row(3), row(0), row(2), ALU.subtract)
            C.tt("dve", starts(3), starts(3), minit_r, ALU.max)
            scan(row(4), RNEG[kd](None, RNEG[kd].t[:, 0:T]), row(3), -1e30, ALU.add, ALU.max)
            C.copy("dve", ROW(None, ROW.t[:, 5, 0:T].rearrange("p (n l) -> p n l", l=L)),
                   ROW(None, ROW.t[:, 4, 0:T].rearrange("p (n l) -> p n l", l=L)[:, :, L - 1:L].broadcast_to([128, nseq, L])))
            C.tt("dve", MNEW(None, MNEW.t[:, h, 0:nseq]), ends(2), ends(4), ALU.add)
            C.tt("dve", DEC(None, DEC.t[:, h, 0:nseq]), minit_r, ends(4), ALU.subtract)
            C.act(DEC(None, DEC.t[:, h, 0:nseq]), DEC(None, DEC.t[:, h, 0:nseq]), AF.Exp)
            ps = next_psb()
            C.mm(ps(None, ps.t[0:T, 0:1]), row(4), ident(None, ident.t[:, 0:1]))
            C.mm(ps(None, ps.t[0:T, 1:2]), row(5), ident(None, ident.t[:, 0:1]))
            C.copy("dve", col(24, 26), ps(None, ps.t[0:T, 0:2]))
            C.tt("dve", DTB(None, DTB.t[0:T, 0:T]), ROW(None, ROW.t[0:T, 4, 0:T]), MASKBIG[kd](None, MASKBIG[kd].t[0:T, 0:T]), ALU.add)
            C.act(DTB(None, DTB.t[0:T, 0:T]), DTB(None, DTB.t[0:T, 0:T]), AF.Exp, scale=-1.0, bias=col(16 + h, 17 + h))
            ps = next_psa()
            for kt in range(2):
                C.mm(ps(None, ps.t[0:T, 0:T]), pj(8 + 2 * h + kt), pj(2 * h + kt), start=(kt == 0), stop=(kt == 1))
            C.tt("dve", STB(None, STB.t[0:T, 0:T]), ps(None, ps.t[0:T, 0:T]), DTB(None, DTB.t[0:T, 0:T]), ALU.mult)
            ps1 = next_psa()
            C.mm(ps1(None, ps1.t[0:T, 0:257]), STB(None, STB.t[0:T, 0:T]), VTM(None, VTM.t[0:T, h, :]))
            C.copy("act", P1S(None, P1S.t[0:T, :]), ps1(None, ps1.t[0:T, 0:257]))
            ps2 = next_psa()
            if is_s:
                i_ = 0
                for n in range(nseq):
                    cs = CSS[n % 2]
                    C.dma(cs(None, cs.t[:, :, 0:256]), I["s_mc"][s0 + n, h].rearrange("(kt p) v -> p kt v", p=128))
                    C.dma(cs(None, cs.t[:, :, 256:257]), I["s_mn"][s0 + n, h].rearrange("(kt p o) -> p kt o", p=128, o=1), slow=True)
                    for kt in range(2):
                        qm = QM[i_ % 2]
                        i_ += 1
                        C.tt("pool", qm(None, qm.t[:, 0:T]), pj(2 * h + kt), SEGROW(None, SEGROW.t[:, n, 0:T]), ALU.mult)
                        C.mm(ps2(None, ps2.t[0:T, 0:257]), qm(None, qm.t[:, 0:T]), cs(None, cs.t[:, kt, :]),
                             start=(n == 0 and kt == 0), stop=(n == nseq - 1 and kt == 1))
            else:
                for kt in range(2):
                    C.mm(ps2(None, ps2.t[0:T, 0:257]), pj(2 * h + kt), CS(("h", h), CS.t[:, h, kt, :]), start=(kt == 0), stop=(kt == 1))
            sm = lambda a: SM(None, SM.t[0:T, a:a + 1])
            C.tt("dve", sm(0), col(20 + h, 21 + h), col(24, 25), ALU.subtract)
            C.act(sm(0), sm(0), AF.Exp)
            C.stt("dve", NUM(None, NUM.t[0:T, :]), ps2(None, ps2.t[0:T, 0:257]), sm(0), P1S(None, P1S.t[0:T, :]), ALU.mult, ALU.add)
            C.tt("dve", sm(1), col(12 + h, 13 + h), col(24, 25), ALU.add)
            C.act(sm(1), sm(1), AF.Exp, scale=-1.0)
            C.ts("dve", sm(2), NUM(None, NUM.t[0:T, 256:257]), -1.0, None, ALU.mult)
            C.tt("dve", sm(2), sm(2), NUM(None, NUM.t[0:T, 256:257]), ALU.max)
            C.tt("dve", sm(2), sm(2), sm(1), ALU.max)
            recip("dve", sm(2), sm(2))
            C.op("dve", lambda g, T=T: g.reduce_sum(out=SM.t[0:T, 3:4], in_=NUM.t[0:T, 0:256], axis=AX.X),
                 reads=[NUM(None, NUM.t[0:T, 0:256])], writes=[sm(3)])
            C.tt("dve", sm(3), sm(3), sm(2), ALU.mult)
            C.ts("dve", sm(3), sm(3), 1.0 / 256, None, ALU.mult)
            hn = HN(None, HN.t[0:T, :])
            C.ts("dve", hn, NUM(None, NUM.t[0:T, 0:256]), sm(2), sm(3), ALU.mult, ALU.subtract)
            C.tt("dve", P1S(None, P1S.t[0:T, 0:256]), hn, hn, ALU.mult)
            C.op("dve", lambda g, T=T: g.reduce_sum(out=SM.t[0:T, 4:5], in_=P1S.t[0:T, 0:256], axis=AX.X),
                 reads=[P1S(None, P1S.t[0:T, 0:256])], writes=[sm(4)])
            C.ts("dve", sm(4), sm(4), 1.0 / 256, LN_EPS, ALU.mult, ALU.add)
            C.act(sm(4), sm(4), AF.Sqrt)
            recip("dve", sm(4), sm(4))
            C.ts("dve", hn, hn, sm(4), None, ALU.mult)
            for kt in range(2):
                ct = 2 * h + kt
                ps = next_psb()
                C.tr(ps(None, ps.t[:, 0:T]), HN(None, HN.t[0:T, kt * 128:(kt + 1) * 128]), ident(None, ident.t[0:T, 0:T]))
                scr = P1S(None, P1S.t[:, 0:T])
                C.stt("dve", scr, ps(None, ps.t[:, 0:T]), VEC1(None, VEC1.t[:, ct, 0:1]), pj(16 + ct), ALU.mult, ALU.mult)
                C.tt("dve", mixv(ct, T), scr, pj(24 + ct), ALU.mult)
            C.tt("dve", sm(5), col(16 + h, 17 + h), col(25, 26), ALU.subtract)
            C.act(sm(5), sm(5), AF.Exp)
            for kt in range(2):
                ps = next_psb()
                C.tr(ps(None, ps.t[0:T, 0:128]), pj(8 + 2 * h + kt), ident.all())
                C.ts("dve", KW(None, KW.t[0:T, h * 256 + kt * 128:h * 256 + (kt + 1) * 128]), ps(None, ps.t[0:T, 0:128]), sm(5), None, ALU.mult)
            if is_s:
                for n in range(nseq):
                    cs = CSS[n % 2]
                    co = CSO[n % 2]
                    C.dma(cs(None, cs.t[:, :, 0:256]), I["s_mc"][s0 + n, h].rearrange("(kt p) v -> p kt v", p=128))
                    C.dma(cs(None, cs.t[:, :, 256:257]), I["s_mn"][s0 + n, h].rearrange("(kt p o) -> p kt o", p=128, o=1), slow=True)
                    C.ts("pool", KWN(None, KWN.t[0:T, :]), KW(None, KW.t[0:T, h * 256:(h + 1) * 256]), SEGC(None, SEGC.t[0:T, n:n + 1]), None, ALU.mult)
                    for kt in range(2):
                        ps = next_psa()
                        C.mm(ps(None, ps.t[:, 0:257]), KWN(None, KWN.t[0:T, kt * 128:(kt + 1) * 128]), VTM(None, VTM.t[0:T, h, :]))
                        C.stt("dve", co(None, co.t[:, kt, :]), cs(None, cs.t[:, kt, :]), DEC(None, DEC.t[:, h, n:n + 1]), ps(None, ps.t[:, 0:257]), ALU.mult, ALU.add)
                    C.dma(O["s_mc_o"][s0 + n, h].rearrange("(kt p) v -> p kt v", p=128), co(None, co.t[:, :, 0:256]))
                    C.dma(O["s_mn_o"][s0 + n, h].rearrange("(kt p o) -> p kt o", p=128, o=1), co(None, co.t[:, :, 256:257]), slow=True)
                C.dma(O["s_mm_o"][s0:s0 + 8, h:h + 1].rearrange("n o -> o n"), MNEW(None, MNEW.t[0:1, h, 0:8]), slow=True)
            else:
                for kt in range(2):
                    ps = next_psa()
                    C.mm(ps(None, ps.t[:, 0:257]), KW(None, KW.t[0:T, h * 256 + kt * 128:h * 256 + (kt + 1) * 128]), VTM(None, VTM.t[0:T, h, :]))
                    csv = CS(("h", h), CS.t[:, h, kt, :])
                    C.stt("dve", csv, csv, DEC(None, DEC.t[:, h, 0:1]), ps(None, ps.t[:, 0:257]), ALU.mult, ALU.add)
                C.copy("dve", MCAR(None, MCAR.t[:, h:h + 1]), MNEW(None, MNEW.t[:, h, 0:1]))
        if (not is_s) and tile["last"]:
            for h in range(4):
                C.dma(O["p_mc"][h].rearrange("(kt p) v -> p kt v", p=128), CS(("h", h), CS.t[:, h, :, 0:256]))
                C.dma(O["p_mn"][h].rearrange("(kt p o) -> p kt o", p=128, o=1), CS(("h", h), CS.t[:, h, :, 256:257]), slow=True)
            C.dma(O["p_mm"][:, :], MCAR(None, MCAR.t[0:1, :]))

        if do_rwkv:
            (rwkv2 if cfg.get("rwkv2", True) else rwkv)(tile)
        else:
            for ct in range(8, 16):
                C.memset("pool", mixv(ct, T), 0.0)
        out_proj_ln(I["od_w_out"], tile, VECO, 0, 1)

    for tile in tile_plan(cfg):
        load_x(tile)
        if nlayers >= 1:
            layer0(tile)
        if nlayers >= 2:
            layer1(tile)
        store_y(tile)

    C.emit()
    es.close()
    return nc, C


def make_in_maps(inp, cores, consts):
    maps = []
    f = lambda a: np.ascontiguousarray(a, dtype=np.float32)
    for c in cores:
        s = c % 4
        m = {}
        m["xp"] = f(inp["x_prompt"][s])
        m["meta"] = f(inp["meta_tokens"])
        m["xs"] = f(inp["x_sample"][16 * c:16 * c + 16].reshape(128, D))
        m["s_conv"] = f(inp["state_conv"][0, 16 * c:16 * c + 16].reshape(480, 1024))
        m["s_sre"] = f(inp["state_ssm_re"][0, 16 * c:16 * c + 16])
        m["s_sim"] = f(inp["state_ssm_im"][0, 16 * c:16 * c + 16])
        m.update(consts)
        sl = slice(16 * c, 16 * c + 16)
        m["s_mc"] = f(inp["state_mlstm_c"][0, sl])
        m["s_mn"] = f(inp["state_mlstm_n"][0, sl])
        m["s_mm"] = f(inp["state_mlstm_m"][0, sl])
        m["s_rs"] = f(inp["state_rwkv_s"][0, sl])
        m["s_rsh"] = f(inp["state_rwkv_shift"][0, sl])
        m["od_w_in"] = f(inp["od_w_in"][0])
        for nm in ("m_ig_b", "m_fg_b", "m_hn_g", "r_w0", "r_a0", "r_kk", "r_ka", "r_ln_g", "r_ln_b", "r_rk", "r_mu", "od_ln_g", "od_ln_b"):
            m[nm] = f(inp[nm][0].reshape(1, -1))
        for nm in ("r_w2", "r_a2", "od_w_out"):
            m[nm] = f(inp[nm][0])
        m["ev_w_in"] = f(inp["ev_w_in"][0])
        m["a_conv_w"] = f(inp["a_conv_w"][0])
        for nm in ("a_conv_b", "a_ln_g", "a_ln_b", "s5_d", "s5_log_dt", "s5_glu_b", "ev_ln_g", "ev_ln_b"):
            m[nm] = f(inp[nm][0].reshape(1, -1))
        m["a_pw"] = f(inp["a_pw"][0])
        for nm in ("s5_lambda_re", "s5_lambda_im", "s5_b_re", "s5_b_im", "s5_glu_w", "ev_w_out"):
            m[nm] = f(inp[nm][0])
        m["s5_c_re"] = f(inp["s5_c_re"][0].reshape(1024, 64))
        m["s5_c_im"] = f(inp["s5_c_im"][0].reshape(1024, 64))
        maps.append(m)
    return maps


def kernel(**inp):
    cfg = {}
    nc, C = build(cfg)
    consts = make_consts()
    cores = list(range(NCORES))
    maps = make_in_maps(inp, cores, consts)
    res = run_bass_kernel_spmd(nc, maps, core_ids=cores)
    R = res.results
    B = 4
    cat = lambda k, shp: np.concatenate([R[c][k].reshape((16,) + shp) for c in range(NCORES)], 0)[None]
    stk = lambda k, shp: np.stack([R[c][k].reshape(shp) for c in range(B)], 0)[None]
    y_p = np.stack([R[c]["y_p"] for c in range(B)], 0)
    y_s = np.concatenate([R[c]["y_s"].reshape(16, 8, D) for c in range(NCORES)], 0)
    return (y_p, y_s,
            stk("p_conv", (30, 1024)), stk("p_sre", (64, 64)), stk("p_sim", (64, 64)), stk("p_mc", (4, 256, 256)),
            stk("p_mn", (4, 256)), stk("p_mm", (4,)), stk("p_rs", (16, 64, 64)), stk("p_rsh", (3200,)),
            cat("s_conv_o", (30, 1024)), cat("s_sre_o", (64, 64)), cat("s_sim_o", (64, 64)), cat("s_mc_o", (4, 256, 256)),
            cat("s_mn_o", (4, 256)), cat("s_mm_o", (4,)), cat("s_rs_o", (16, 64, 64)), cat("s_rsh_o", (3200,)))
```

```python
import contextlib
import numpy as np
import concourse.bass as bass
import concourse.mybir as mybir
from concourse.bass_utils import run_bass_kernel_spmd

F32 = mybir.dt.float32
BF16 = mybir.dt.bfloat16
AF = mybir.ActivationFunctionType
ALU = mybir.AluOpType
AX = mybir.AxisListType

D = 2048
TT = 128
WG = 128
NWS = 4
NCORES = 8
ALPHA = 4 ** 0.25
LN_EPS = 1e-5


class Reg:
    __slots__ = ("w", "r")

    def __init__(self):
        self.w = None
        self.r = []


class Buf:
    def __init__(self, ctx, name, t):
        self.ctx, self.name, self.t = ctx, name, t
        self.regs = {"_all": Reg()}
        self.dma_sem = None
        self.dma_cnt = 0

    def __call__(self, key, ap):
        return View(self, key, ap)

    def all(self):
        return View(self, None, self.t[:])

    def _sel(self, key):
        if key is None:
            return list(self.regs.values())
        if key not in self.regs:
            self.regs[key] = Reg()
        return [self.regs[key], self.regs["_all"]]

    def rdeps(self, key):
        return [r.w for r in self._sel(key) if r.w is not None]

    def wdeps(self, key):
        out = []
        for r in self._sel(key):
            if r.w is not None:
                out.append(r.w)
            out.extend(r.r)
        return out

    def note_read(self, key, tok):
        if key is None:
            for r in self.regs.values():
                r.r.append(tok)
        else:
            self._sel(key)[0].r.append(tok)

    def note_write(self, key, tok):
        if key is None:
            self.regs = {"_all": Reg()}
            self.regs["_all"].w = tok
        else:
            r = self._sel(key)[0]
            r.w = tok
            r.r = []


class View:
    __slots__ = ("buf", "key", "ap")

    def __init__(self, buf, key, ap):
        self.buf, self.key, self.ap = buf, key, ap


class Ctx:
    ENG = ("pe", "act", "dve", "pool", "sp")
    EPOCH = 30000

    def __init__(self, nc, es):
        self.nc, self.es = nc, es
        self.prog = {e: [] for e in self.ENG}
        self.cnt = {e: 0 for e in self.ENG}
        self.sem = {e: es.enter_context(nc.semaphore("sem_" + e)) for e in self.ENG}
        self.known = {e: {} for e in self.ENG}
        self.final = []
        self.total = {}
        self.nsem = 5
        self.nbytes = 0

    def sb(self, name, shape, dtype=F32):
        t = self.es.enter_context(self.nc.sbuf_tensor("sb_" + name, list(shape), dtype))
        n = 4
        for s in shape[1:]:
            n *= s
        self.nbytes += n
        return Buf(self, name, t)

    def ps(self, name, shape, dtype=F32):
        t = self.es.enter_context(self.nc.psum_tensor("ps_" + name, list(shape), dtype))
        return Buf(self, name, t)

    def need(self, e, tok):
        sem, val = tok
        k = id(sem)
        if self.known[e].get(k, 0) >= val:
            return
        self.known[e][k] = val
        self.prog[e].append(("wait", sem, val))

    def op(self, e, fn, reads=(), writes=()):
        for v in reads:
            if isinstance(v, View):
                for tok in v.buf.rdeps(v.key):
                    self.need(e, tok)
        for v in writes:
            if isinstance(v, View):
                for tok in v.buf.wdeps(v.key):
                    self.need(e, tok)
        if self.cnt[e] >= self.EPOCH:
            self.total[e] = self.total.get(e, 0) + self.cnt[e]
            self.sem[e] = self.es.enter_context(self.nc.semaphore("sem_%s_%d" % (e, self.total[e])))
            self.cnt[e] = 0
            self.nsem += 1
        self.cnt[e] += 1
        tok = (self.sem[e], self.cnt[e])
        self.prog[e].append(("op", fn, self.sem[e], 1))
        for v in reads:
            if isinstance(v, View):
                v.buf.note_read(v.key, tok)
        for v in writes:
            if isinstance(v, View):
                v.buf.note_write(v.key, tok)
        return tok

    def dma(self, out, in_, q="sp", slow=False):
        sbv = out if isinstance(out, View) else in_
        b = sbv.buf
        if b.dma_sem is None:
            b.dma_sem = self.es.enter_context(self.nc.semaphore("dq_" + b.name))
            self.nsem += 1
        if isinstance(in_, View):
            for tok in in_.buf.rdeps(in_.key):
                self.need(q, tok)
        if isinstance(out, View):
            for tok in out.buf.wdeps(out.key):
                self.need(q, tok)
        b.dma_cnt += 16
        tok = (b.dma_sem, b.dma_cnt)
        oap = out.ap if isinstance(out, View) else out
        iap = in_.ap if isinstance(in_, View) else in_
        if slow:
            fn = lambda eng, oap=oap, iap=iap: eng.dma_start(out=oap, in_=iap, allow_slow_non_contiguous=True)
        else:
            fn = lambda eng, oap=oap, iap=iap: eng.dma_start(out=oap, in_=iap)
        self.prog[q].append(("op", fn, b.dma_sem, 16))
        if isinstance(in_, View):
            in_.buf.note_read(in_.key, tok)
        if isinstance(out, View):
            out.buf.note_write(out.key, tok)
        else:
            self.final.append(tok)
        return tok

    def tt(self, e, out, in0, in1, op):
        return self.op(e, lambda g: g.tensor_tensor(out=out.ap, in0=in0.ap, in1=in1.ap, op=op),
                       reads=[in0, in1], writes=[out])

    def ts(self, e, out, in0, s1, s2, op0, op1=None):
        rd = [in0] + [s for s in (s1, s2) if isinstance(s, View)]
        a1 = s1.ap if isinstance(s1, View) else s1
        a2 = s2.ap if isinstance(s2, View) else s2
        if op1 is None:
            return self.op(e, lambda g: g.tensor_scalar(out=out.ap, in0=in0.ap, scalar1=a1, scalar2=None, op0=op0),
                           reads=rd, writes=[out])
        return self.op(e, lambda g: g.tensor_scalar(out=out.ap, in0=in0.ap, scalar1=a1, scalar2=a2, op0=op0, op1=op1),
                       reads=rd, writes=[out])

    def stt(self, e, out, in0, s, in1, op0, op1):
        rd = [in0, in1] + ([s] if isinstance(s, View) else [])
        a = s.ap if isinstance(s, View) else s
        return self.op(e, lambda g: g.scalar_tensor_tensor(out=out.ap, in0=in0.ap, scalar=a, in1=in1.ap, op0=op0, op1=op1),
                       reads=rd, writes=[out])

    def act(self, out, in_, func, bias=None, scale=None, e="act"):
        rd = [in_] + [s for s in (bias, scale) if isinstance(s, View)]
        kw = {}
        if bias is not None:
            kw["bias"] = bias.ap if isinstance(bias, View) else bias
        if scale is not None:
            kw["scale"] = scale.ap if isinstance(scale, View) else scale
        return self.op(e, lambda g: g.activation(out=out.ap, in_=in_.ap, func=func, **kw), reads=rd, writes=[out])

    def copy(self, e, out, in_):
        if e == "act":
            return self.act(out, in_, AF.Copy)
        return self.op(e, lambda g: g.tensor_copy(out=out.ap, in_=in_.ap), reads=[in_], writes=[out])

    def memset(self, e, out, val):
        return self.op(e, lambda g: g.memset(out.ap, val), writes=[out])

    def mm(self, out, lhsT, rhs, start=True, stop=True):
        return self.op("pe", lambda g: g.matmul(out.ap, lhsT=lhsT.ap, rhs=rhs.ap, start=start, stop=stop),
                       reads=[lhsT, rhs], writes=[out])

    def tr(self, out, in_, ident):
        return self.op("pe", lambda g: g.transpose(out.ap, in_.ap, ident.ap), reads=[in_, ident], writes=[out])

    def emit(self):
        nc = self.nc
        for tok in self.final:
            self.need("sp", tok)
        for e in self.ENG:
            if e != "sp" and self.cnt[e] > 0:
                self.need("sp", (self.sem[e], self.cnt[e]))
        engs = {"pe": "tensor", "act": "scalar", "dve": "vector", "pool": "gpsimd", "sp": "sync"}
        with nc.Block() as block:
            for e, attr in engs.items():
                items = self.prog[e]

                def body(eng, items=items):
                    for it in items:
                        if it[0] == "wait":
                            eng.wait_ge(it[1], it[2])
                        else:
                            it[1](eng).then_inc(it[2], it[3])

                getattr(block, attr)(body)


def make_consts():
    c = {}
    c["ident"] = np.eye(128, dtype=np.float32)
    c["ones"] = np.ones((128, 128), dtype=np.float32)
    r = np.arange(128)
    mrow = (r % 32) // 16
    mcol = np.arange(128) // 64
    bm = (mrow[:, None] == mcol[None, :]).astype(np.float32)
    ev_r = ((r // 32) % 2 == 0).astype(np.float32)
    c["bmask"] = bm * ev_r[:, None]
    c["bmask_o"] = bm * (1 - ev_r)[:, None]
    gl = np.arange(128) // 16
    cm = ((gl[None, :] % 2) == (r[:, None] // 64)).astype(np.float32)
    c["cmask"] = cm * ev_r[None, :]
    c["cmask_o"] = cm * (1 - ev_r)[None, :]
    BIG = 1e30
    i128 = np.arange(128)
    allow_p = (i128[:, None] <= i128[None, :])
    c["maskbig_p"] = np.where(allow_p, 0.0, BIG).astype(np.float32)
    c["segtri_p"] = allow_p.astype(np.float32)
    i64 = np.arange(64)
    allow_s = (i64[:, None] <= i64[None, :]) & ((i64[:, None] // 8) == (i64[None, :] // 8))
    c["maskbig_s"] = np.where(allow_s, 0.0, BIG).astype(np.float32)
    c["segtri_s"] = allow_s.astype(np.float32)
    r01p = np.ones((128, 128), np.float32); r01p[:, 0] = 0
    r01s = np.ones((128, 64), np.float32); r01s[:, ::8] = 0
    c["r01_p"], c["r01_s"] = r01p, r01s
    c["rneg_p"] = ((1 - r01p) * -BIG).astype(np.float32)
    c["rneg_s"] = ((1 - r01s) * -BIG).astype(np.float32)
    segsel = ((i64[None, :] // 8) == np.arange(8)[:, None]).astype(np.float32)
    c["segsel"] = segsel
    c["segc"] = np.ascontiguousarray(segsel.T)
    c["segrow"] = np.ascontiguousarray(np.broadcast_to(segsel.reshape(1, 512), (128, 512))).astype(np.float32)
    c["blk"] = ((i128[:, None] // 64) == (i128[None, :] // 64)).astype(np.float32)
    st_p = (i128[:, None] < i128[None, :])
    st_s = (i64[:, None] < i64[None, :]) & ((i64[:, None] // 8) == (i64[None, :] // 8))
    c["sstri_p"] = st_p.astype(np.float32)
    c["sstriT_p"] = np.ascontiguousarray(st_p.T).astype(np.float32)
    c["sstri_s"] = st_s.astype(np.float32)
    c["sstriT_s"] = np.ascontiguousarray(st_s.T).astype(np.float32)
    return c


CONST_SHAPES = {"ident": [128, 128], "ones": [128, 128], "bmask": [128, 128], "cmask": [128, 128],
                "bmask_o": [128, 128], "cmask_o": [128, 128],
                "maskbig_p": [128, 128], "segtri_p": [128, 128], "maskbig_s": [64, 64], "segtri_s": [64, 64],
                "r01_p": [128, 128], "r01_s": [128, 64], "rneg_p": [128, 128], "rneg_s": [128, 64],
                "segsel": [8, 64], "segc": [64, 8], "segrow": [128, 512], "blk": [128, 128],
                "sstri_p": [128, 128], "sstriT_p": [128, 128], "sstri_s": [64, 64], "sstriT_s": [64, 64]}


def tile_plan(cfg):
    tiles = []
    npt = cfg.get("n_prompt_tiles", 17)
    pos = 0
    for i in range(17):
        T = 16 if i == 16 else TT
        if i < npt:
            tiles.append(dict(kind="p", T=T, pos=pos, nseq=1, L=T, first=(i == 0), last=(i == npt - 1), s0=0))
        pos += T
    if cfg.get("sample", True):
        for h in range(2):
            tiles.append(dict(kind="s", T=64, pos=0, nseq=8, L=8, first=True, last=True, s0=8 * h))
    return tiles


def build(cfg):
    nc = bass.Bass("TRN2", target_bir_lowering=False)
    es = contextlib.ExitStack()
    C = Ctx(nc, es)
    dbg = cfg.get("debug", False)
    nlayers = cfg.get("layers", 2)

    def din(name, shape):
        return nc.dram_tensor(name, list(shape), F32, kind="ExternalInput").ap()

    def dout(name, shape):
        return nc.dram_tensor(name, list(shape), F32, kind="ExternalOutput").ap()

    I = {}
    I["xp"] = din("xp", [2048, D])
    I["meta"] = din("meta", [16, D])
    I["xs"] = din("xs", [128, D])
    I["s_conv"] = din("s_conv", [16 * 30, 1024])
    I["s_sre"] = din("s_sre", [16, 64, 64])
    I["s_sim"] = din("s_sim", [16, 64, 64])
    for nm, shp in CONST_SHAPES.items():
        I[nm] = din(nm, shp)
    I["s_mc"] = din("s_mc", [16, 4, 256, 256])
    I["s_mn"] = din("s_mn", [16, 4, 256])
    I["s_mm"] = din("s_mm", [16, 4])
    I["s_rs"] = din("s_rs", [16, 16, 64, 64])
    I["s_rsh"] = din("s_rsh", [16, 3200])
    I["od_w_in"] = din("od_w_in", [D, 9352])
    I["m_ig_b"] = din("m_ig_b", [1, 4])
    I["m_fg_b"] = din("m_fg_b", [1, 4])
    for nm in ("m_hn_g", "r_w0", "r_a0", "r_kk", "r_ka", "r_ln_g", "r_ln_b", "r_rk"):
        I[nm] = din(nm, [1, 1024])
    I["r_mu"] = din("r_mu", [1, 3200])
    I["r_w2"] = din("r_w2", [64, 1024])
    I["r_a2"] = din("r_a2", [64, 1024])
    I["od_w_out"] = din("od_w_out", [2048, D])
    I["od_ln_g"] = din("od_ln_g", [1, D])
    I["od_ln_b"] = din("od_ln_b", [1, D])
    I["ev_w_in"] = din("ev_w_in", [D, 5120])
    I["a_conv_w"] = din("a_conv_w", [31, 1024])
    for nm in ("a_conv_b", "a_ln_g", "a_ln_b", "s5_d"):
        I[nm] = din(nm, [1, 1024])
    I["a_pw"] = din("a_pw", [1024, 1024])
    I["s5_lambda_re"] = din("s5_lambda_re", [64, 64])
    I["s5_lambda_im"] = din("s5_lambda_im", [64, 64])
    I["s5_log_dt"] = din("s5_log_dt", [1, 64])
    I["s5_b_re"] = din("s5_b_re", [64, 64, 16])
    I["s5_b_im"] = din("s5_b_im", [64, 64, 16])
    I["s5_c_re"] = din("s5_c_re", [1024, 64])
    I["s5_c_im"] = din("s5_c_im", [1024, 64])
    I["s5_glu_w"] = din("s5_glu_w", [1024, 2048])
    I["s5_glu_b"] = din("s5_glu_b", [1, 2048])
    I["ev_w_out"] = din("ev_w_out", [2048, D])
    I["ev_ln_g"] = din("ev_ln_g", [1, D])
    I["ev_ln_b"] = din("ev_ln_b", [1, D])

    O = {}
    O["y_p"] = dout("y_p", [2048, D])
    O["y_s"] = dout("y_s", [128, D])
    O["p_conv"] = dout("p_conv", [30, 1024])
    O["p_sre"] = dout("p_sre", [64, 64])
    O["p_sim"] = dout("p_sim", [64, 64])
    O["s_conv_o"] = dout("s_conv_o", [16 * 30, 1024])
    O["s_sre_o"] = dout("s_sre_o", [16, 64, 64])
    O["s_sim_o"] = dout("s_sim_o", [16, 64, 64])
    O["p_mc"] = dout("p_mc", [4, 256, 256])
    O["p_mn"] = dout("p_mn", [4, 256])
    O["p_mm"] = dout("p_mm", [1, 4])
    O["p_rs"] = dout("p_rs", [16, 64, 64])
    O["p_rsh"] = dout("p_rsh", [1, 3200])
    O["s_mc_o"] = dout("s_mc_o", [16, 4, 256, 256])
    O["s_mn_o"] = dout("s_mn_o", [16, 4, 256])
    O["s_mm_o"] = dout("s_mm_o", [16, 4])
    O["s_rs_o"] = dout("s_rs_o", [16, 16, 64, 64])
    O["s_rsh_o"] = dout("s_rsh_o", [16, 3200])

    ident = C.sb("ident", [128, 128])
    ones = C.sb("ones", [128, 128])
    XT = C.sb("XT", [128, 16, TT])
    XS = C.sb("XS", [128, D])
    PJ = C.sb("PJ", [128, 33, TT])
    MIX = C.sb("MIX", [128, 16 * TT], BF16)
    XTB = C.sb("XTB", [128, 16, TT], BF16)
    GELB = C.sb("GELB", [128, 8, TT], BF16)
    WS = [C.sb("WS%d" % i, [128, 16, WG], BF16) for i in range(NWS)]
    HB = C.sb("HB", [128, 8, 304])
    HC = C.sb("HC", [128, 8, 30])
    ACC = C.sb("ACC", [128, 8, TT])
    SQ2 = C.sb("SQ2", [128, 2, TT])
    ST = C.sb("ST", [128, 3, TT])
    VEC0 = C.sb("VEC0", [128, 8, 40])
    VECG = C.sb("VECG", [128, 16, 4])
    TMP = C.sb("TMP", [128, 128])
    XR = C.sb("XR", [128, 32, 129])
    XI = C.sb("XI", [128, 32, 129])
    S5C = C.sb("S5C", [128, 2, 32])
    S5A = C.sb("S5A", [128, 8, 32])
    S5T = C.sb("S5T", [128, 10, 32])
    SCS = C.sb("SCS", [128, 2, 32, 8])
    BRE = [C.sb("BRE%d" % i, [128, 8, 128]) for i in range(2)]
    BIM = [C.sb("BIM%d" % i, [128, 8, 128]) for i in range(2)]
    CRE = [C.sb("CRE%d" % i, [128, 8, 128]) for i in range(2)]
    CIM = [C.sb("CIM%d" % i, [128, 8, 128]) for i in range(2)]
    mask_bo = C.sb("mask_bo", [128, 128])
    mask_co = C.sb("mask_co", [128, 128])
    S5ST = C.sb("S5ST", [128, 128])
    mask_b = C.sb("mask_b", [128, 128])
    mask_c = C.sb("mask_c", [128, 128])

    PSA = [C.ps("PSA%d" % i, [128, 512]) for i in range(4)]
    PSB = [C.ps("PSB%d" % i, [128, 512]) for i in range(4)]

    def mixv(ct, T):
        return MIX(("t", ct), MIX.t[:, ct * TT:ct * TT + T])

    C.dma(ident.all(), I["ident"])
    C.dma(ones.all(), I["ones"])
    C.dma(mask_b.all(), I["bmask"])
    C.dma(mask_c.all(), I["cmask"])
    C.dma(mask_bo.all(), I["bmask_o"])
    C.dma(mask_co.all(), I["cmask_o"])

    rr = {"psa": 0, "psb": 0, "ws": 0, "ev": 0, "sq": 0}

    def next_psa():
        rr["psa"] = (rr["psa"] + 1) % 4
        return PSA[rr["psa"]]

    def next_psb():
        rr["psb"] = (rr["psb"] + 1) % 4
        return PSB[rr["psb"]]

    def ev_eng():
        rr["ev"] += 1
        return "dve" if rr["ev"] % 2 else "act"

    def load_rows_T(dst, rows, ncols, col0=0, ct0=0):
        r0 = 0
        for ap in rows:
            nr = ap.shape[0]
            C.dma(XS(None, XS.t[r0:r0 + nr, 0:ncols]), ap)
            r0 += nr
        nr = r0
        for ct in range(ncols // 128):
            ps = next_psb()
            C.tr(ps(None, ps.t[:, 0:nr]), XS(None, XS.t[0:nr, ct * 128:(ct + 1) * 128]), ident(None, ident.t[0:nr, 0:nr]))
            C.copy("dve", dst(None, dst.t[:, ct0 + ct, col0:col0 + nr]), ps(None, ps.t[:, 0:nr]))

    load_rows_T(VEC0, [I["a_conv_w"], I["a_conv_b"], I["a_ln_g"], I["a_ln_b"], I["s5_d"]], 1024)
    load_rows_T(VECG, [I["s5_glu_b"], I["ev_ln_g"], I["ev_ln_b"]], 2048)

    def gp_ap(ap2d):
        return ap2d.rearrange("(q m) p -> (m p) q", m=2)

    LR, LI, DT, AR, AI, NAI = range(6)
    A = lambda k: S5A(None, S5A.t[:, k, :])
    Tm = lambda k: S5T(None, S5T.t[:, k, :])
    for m in range(2):
        C.dma(S5A(None, S5A.t[64 * m:64 * m + 64, LR, :]), I["s5_lambda_re"].rearrange("(q m) p -> m p q", m=2)[m], slow=True)
        C.dma(S5A(None, S5A.t[64 * m:64 * m + 64, LI, :]), I["s5_lambda_im"].rearrange("(q m) p -> m p q", m=2)[m], slow=True)
    ldt = I["s5_log_dt"]
    for m in range(2):
        src = bass.AP(ldt.tensor, ldt.offset + m, [[0, 64], [2, 32]])
        C.dma(S5A(None, S5A.t[64 * m:64 * m + 64, DT, :]), src, slow=True)
    C.act(A(DT), A(DT), AF.Exp)
    C.tt("dve", Tm(0), A(LR), A(DT), ALU.mult)
    C.act(Tm(1), Tm(0), AF.Exp)
    C.tt("dve", Tm(2), A(LI), A(DT), ALU.mult)
    PI = float(np.pi)
    C.ts("dve", Tm(3), Tm(2), 1.0 / 32, None, ALU.mult)
    C.act(Tm(4), Tm(3), AF.Sin)
    C.ts("dve", Tm(3), Tm(3), PI / 2, None, ALU.add)
    C.act(Tm(5), Tm(3), AF.Sin)
    for _ in range(5):
        C.tt("dve", Tm(3), Tm(4), Tm(5), ALU.mult)
        C.tt("dve", Tm(8), Tm(5), Tm(5), ALU.mult)
        C.tt("dve", Tm(9), Tm(4), Tm(4), ALU.mult)
        C.ts("dve", Tm(4), Tm(3), 2.0, None, ALU.mult)
        C.tt("dve", Tm(5), Tm(8), Tm(9), ALU.subtract)
    C.tt("dve", A(AR), Tm(1), Tm(5), ALU.mult)
    C.tt("dve", A(AI), Tm(1), Tm(4), ALU.mult)
    C.ts("dve", A(NAI), A(AI), -1.0, None, ALU.mult)
    MAG = 6
    C.copy("dve", A(MAG), Tm(1))
    s5tab = nc.dram_tensor("s5tab", [2, 128, 1024], F32).ap()
    tC = lambda a, b: XR(None, XR.t[:, :, a:b])
    tS = lambda a, b: XI(None, XI.t[:, :, a:b])
    C.copy("dve", tC(0, 1), S5T(None, S5T.t[:, 5, :].unsqueeze(2)))
    C.copy("dve", tS(0, 1), S5T(None, S5T.t[:, 4, :].unsqueeze(2)))
    n_ = 1
    while n_ < 32:
        cn = XR(None, XR.t[:, :, n_ - 1:n_].broadcast_to([128, 32, n_]))
        sn = XI(None, XI.t[:, :, n_ - 1:n_].broadcast_to([128, 32, n_]))
        u1 = XR(None, XR.t[:, :, 64:64 + n_])
        u2 = XI(None, XI.t[:, :, 64:64 + n_])
        C.tt("dve", u1, tS(0, n_), sn, ALU.mult)
        C.tt("dve", tC(n_, 2 * n_), tC(0, n_), cn, ALU.mult)
        C.tt("dve", tC(n_, 2 * n_), tC(n_, 2 * n_), u1, ALU.subtract)
        C.tt("dve", u2, tC(0, n_), sn, ALU.mult)
        C.tt("dve", tS(n_, 2 * n_), tS(0, n_), cn, ALU.mult)
        C.tt("dve", tS(n_, 2 * n_), tS(n_, 2 * n_), u2, ALU.add)
        n_ *= 2
    tab_tok = [C.dma(s5tab[0].rearrange("p (q t) -> p q t", t=32), tC(0, 32)),
               C.dma(s5tab[1].rearrange("p (q t) -> p q t", t=32), tS(0, 32))]
    for tk in tab_tok:
        C.need("sp", tk)
    C.tt("dve", Tm(0), A(LR), A(LR), ALU.mult)
    C.tt("dve", Tm(1), A(LI), A(LI), ALU.mult)
    C.tt("dve", Tm(0), Tm(0), Tm(1), ALU.add)
    C.op("dve", lambda g: g.reciprocal(out=S5T.t[:, 0, :], in_=S5T.t[:, 0, :]), reads=[Tm(0)], writes=[Tm(0)])
    C.ts("dve", Tm(1), A(AR), -1.0, None, ALU.add)
    C.tt("dve", Tm(2), Tm(1), A(LR), ALU.mult)
    C.tt("dve", Tm(3), A(AI), A(LI), ALU.mult)
    C.tt("dve", Tm(2), Tm(2), Tm(3), ALU.add)
    C.tt("dve", Tm(6), Tm(2), Tm(0), ALU.mult)
    C.tt("dve", Tm(2), A(AI), A(LR), ALU.mult)
    C.tt("dve", Tm(3), Tm(1), A(LI), ALU.mult)
    C.tt("dve", Tm(2), Tm(2), Tm(3), ALU.subtract)
    C.tt("dve", Tm(7), Tm(2), Tm(0), ALU.mult)
    braw_t = PJ.t[:, 0:8, :].rearrange("p a b -> p (a b)").rearrange("p (k q h) -> p k q h", k=2, q=32)
    bb_t = PJ.t[:, 8:24, :].rearrange("p a b -> p (a b)").rearrange("p (k q h) -> p k q h", k=2, q=32)
    sc_t = PJ.t[:, 24:28, :].rearrange("p a b -> p (a b)").rearrange("p (q h) -> p q h", q=32)
    for k, nm in enumerate(("s5_b_re", "s5_b_im")):
        for m in range(2):
            C.dma(PJ(None, braw_t[64 * m:64 * m + 64, k, :, :]), I[nm].rearrange("(q m) p h -> m p q h", m=2)[m])
    qr_b = S5T(None, S5T.t[:, 6, :].unsqueeze(2).broadcast_to([128, 32, 16]))
    qi_b = S5T(None, S5T.t[:, 7, :].unsqueeze(2).broadcast_to([128, 32, 16]))
    br = PJ(None, braw_t[:, 0, :, :])
    bi = PJ(None, braw_t[:, 1, :, :])
    sc = PJ(None, sc_t)
    for dup in range(2):
        o_r = PJ(None, bb_t[:, 0, :, dup * 16:(dup + 1) * 16])
        o_i = PJ(None, bb_t[:, 1, :, dup * 16:(dup + 1) * 16])
        C.tt("dve", o_r, br, qr_b, ALU.mult)
        C.tt("dve", sc, bi, qi_b, ALU.mult)
        C.tt("dve", o_r, o_r, sc, ALU.subtract)
        C.tt("dve", o_i, bi, qr_b, ALU.mult)
        C.tt("dve", sc, br, qi_b, ALU.mult)
        C.tt("dve", o_i, o_i, sc, ALU.add)
    for k, dst in enumerate((BRE, BIM)):
        for gt in range(8):
            ps = next_psb()
            C.copy("dve", TMP.all(), PJ(None, bb_t[:, k, 4 * gt:4 * gt + 4, :]))
            C.tr(ps(None, ps.t[:, 0:128]), TMP.all(), ident.all())
            C.tt("dve", dst[0](None, dst[0].t[:, gt, :]), ps(None, ps.t[:, 0:128]), mask_b.all(), ALU.mult)
            C.tt("dve", dst[1](None, dst[1].t[:, gt, :]), ps(None, ps.t[:, 0:128]), mask_bo.all(), ALU.mult)
    for k, (nm, dst) in enumerate((("s5_c_re", CRE), ("s5_c_im", CIM))):
        for gt in range(8):
            for dup in range(2):
                C.dma(XS(None, XS.t[:, dup * 64:(dup + 1) * 64]), I[nm][gt * 128:(gt + 1) * 128, :])
            ps = next_psb()
            C.tr(ps(None, ps.t[:, 0:128]), XS(None, XS.t[:, 0:128]), ident.all())
            for par, mk in enumerate((mask_c, mask_co)):
                C.stt("dve", dst[par](None, dst[par].t[:, gt, :]), ps(None, ps.t[:, 0:128]), 1.0 if k == 0 else -1.0,
                      mk.all(), ALU.mult, ALU.mult)

    def stream_mm(W, nkt, coltiles, rhs_fn, T, evac):
        groups = []
        cur = []
        for ct in coltiles:
            if cur and (ct[0] + ct[1] - cur[0][0] > WG):
                groups.append(cur)
                cur = []
            cur.append(ct)
        if cur:
            groups.append(cur)
        Wv = W.rearrange("(kt p) c -> p kt c", p=128)
        j = 0
        for grp in groups:
            g0 = grp[0][0]
            gw = grp[-1][0] + grp[-1][1] - g0
            ws = WS[rr["ws"] % NWS]
            rr["ws"] += 1
            C.dma(ws(None, ws.t[:, 0:nkt, 0:gw]), Wv[:, :, g0:g0 + gw], q="pool")
            for (c0, w) in grp:
                ps = next_psa()
                pv = ps(None, ps.t[0:w, 0:T])
                for kt in range(nkt):
                    C.mm(pv, ws(None, ws.t[:, kt, c0 - g0:c0 - g0 + w]), rhs_fn(kt), start=(kt == 0), stop=(kt == nkt - 1))
                evac(j, pv)
                j += 1

    def layer_norm_cols(src, ntile, T, gcol, bcol, vec, func=AF.Identity, eps=LN_EPS, dst_fn=None, also_fn=None):
        nch = ntile * 128
        ps = next_psb()
        ps2 = next_psb()
        for ct in range(ntile):
            sv = src(("t", ct), src.t[:, ct, 0:T])
            sq = SQ2(("s", rr["sq"] % 2), SQ2.t[:, rr["sq"] % 2, 0:T])
            rr["sq"] += 1
            C.act(sq, sv, AF.Square)
            C.mm(ps(None, ps.t[:, 0:T]), ones.all(), sv, start=(ct == 0), stop=(ct == ntile - 1))
            C.mm(ps2(None, ps2.t[:, 0:T]), ones.all(), sq, start=(ct == 0), stop=(ct == ntile - 1))
        st = lambda k: ST(None, ST.t[:, k, 0:T])
        C.ts("dve", st(0), ps(None, ps.t[:, 0:T]), 1.0 / nch, None, ALU.mult)
        C.ts("dve", st(1), ps2(None, ps2.t[:, 0:T]), 1.0 / nch, None, ALU.mult)
        C.tt("dve", st(2), st(0), st(0), ALU.mult)
        C.tt("dve", st(1), st(1), st(2), ALU.subtract)
        C.ts("dve", st(1), st(1), eps, None, ALU.add)
        C.act(st(1), st(1), AF.Sqrt)
        C.op("dve", lambda g, T=T: g.reciprocal(out=ST.t[:, 1, 0:T], in_=ST.t[:, 1, 0:T]), reads=[st(1)], writes=[st(1)])
        C.tt("dve", st(2), st(0), st(1), ALU.mult)
        C.ts("dve", st(2), st(2), -1.0, None, ALU.mult)
        for ct in range(ntile):
            e = "dve" if ct % 2 == 0 else "pool"
            sv = src(("t", ct), src.t[:, ct, 0:T])
            C.tt(e, sv, sv, st(1), ALU.mult)
            C.tt(e, sv, sv, st(2), ALU.add)
            dv = dst_fn(ct) if dst_fn is not None else sv
            C.act(dv, sv, func, bias=vec(None, vec.t[:, ct, bcol:bcol + 1]), scale=vec(None, vec.t[:, ct, gcol:gcol + 1]))
            if also_fn is not None:
                C.copy("pool" if ct % 2 else "act", also_fn(ct), dv)

    def load_x(tile):
        n = tile["T"]
        if tile["kind"] == "s":
            C.dma(XS(None, XS.t[0:n, :]), I["xs"][tile["s0"] * 8:tile["s0"] * 8 + n, :])
        else:
            p0 = tile["pos"]
            r = 0
            if p0 < 16:
                C.dma(XS(None, XS.t[0:16, :]), I["meta"][:, :])
                r = 16
            x0 = p0 + r - 16
            C.dma(XS(None, XS.t[r:n, :]), I["xp"][x0:x0 + n - r, :])
        for dt_ in range(16):
            ps = next_psb()
            C.tr(ps(None, ps.t[:, 0:n]), XS(None, XS.t[0:n, dt_ * 128:(dt_ + 1) * 128]), ident(None, ident.t[0:n, 0:n]))
            C.copy("act" if dt_ % 2 else "dve", XT(("t", dt_), XT.t[:, dt_, 0:n]), ps(None, ps.t[:, 0:n]))
            C.copy("pool", XTB(("t", dt_), XTB.t[:, dt_, 0:n]), XT(("t", dt_), XT.t[:, dt_, 0:n]))

    def store_y(tile):
        n = tile["T"]
        for dt_ in range(16):
            ps = next_psb()
            C.tr(ps(None, ps.t[0:n, 0:128]), XT(("t", dt_), XT.t[:, dt_, 0:n]), ident.all())
            C.copy("act" if dt_ % 2 else "dve", XS(None, XS.t[0:n, dt_ * 128:(dt_ + 1) * 128]), ps(None, ps.t[0:n, 0:128]))
        if tile["kind"] == "s":
            C.dma(O["y_s"][tile["s0"] * 8:tile["s0"] * 8 + n, :], XS(None, XS.t[0:n, :]))
        else:
            p0 = tile["pos"]
            r = 16 if p0 < 16 else 0
            x0 = p0 + r - 16
            C.dma(O["y_p"][x0:x0 + n - r, :], XS(None, XS.t[r:n, :]))

    def out_proj_ln(W, tile, vec, gcol, bcol):
        T = tile["T"]

        def evac(j, pv):
            xv_ = XT(("t", j), XT.t[:, j, 0:T])
            C.stt("dve", xv_, xv_, ALPHA, pv, ALU.mult, ALU.add)

        stream_mm(W, 16, [(i * 128, 128) for i in range(16)], lambda kt: mixv(kt, T), T, evac)
        layer_norm_cols(XT, 16, T, gcol, bcol, vec, also_fn=lambda ct: XTB(("t", ct), XTB.t[:, ct, 0:T]))

    def layer0(tile):
        T, nseq, L = tile["T"], tile["nseq"], tile["L"]
        is_s = tile["kind"] == "s"
        s0 = tile["s0"]
        xrhs = lambda kt: XTB(("t", kt), XTB.t[:, kt, 0:T])
        pj = lambda j: PJ(("t", j), PJ.t[:, j, 0:T])

        fence(HB, HB.t[0:1, 0, 0:1])
        def evacA(j, pv):
            if j < 8:
                C.copy(ev_eng(), pj(j), pv)
            elif j < 16:
                C.act(pj(j), pv, AF.Sigmoid)
            else:
                C.act(pj(j), pv, AF.Silu)

        stream_mm(I["ev_w_in"], 16, [(i * 128, 128) for i in range(24)], xrhs, T, evacA)

        W_ = 30 + L

        def hb(ct, a, b):
            v = HB.t[:, ct, 0:nseq * W_].rearrange("p (n w) -> p n w", w=W_)[:, :, a:b]
            return HB(("t", ct), v)

        def tokv(buf, ct):
            return buf(("t", ct), buf.t[:, ct, 0:T].rearrange("p (n l) -> p n l", l=L))

        if is_s:
            for q in range(2):
                C.dma(XS(None, XS.t[0:120, 0:1024]), I["s_conv"][s0 * 30 + q * 120:s0 * 30 + (q + 1) * 120, :])
                for ct in range(8):
                    ps = next_psb()
                    C.tr(ps(None, ps.t[:, 0:120]), XS(None, XS.t[0:120, ct * 128:(ct + 1) * 128]), ident(None, ident.t[0:120, 0:120]))
                    dstv = HB.t[:, ct, 0:nseq * W_].rearrange("p (n w) -> p n w", w=W_)[:, 4 * q:4 * q + 4, 0:30]
                    C.copy("dve", HB(("t", ct), dstv), ps(None, ps.t[:, 0:120].rearrange("p (n r) -> p n r", r=30)))
        else:
            for ct in range(8):
                if tile["first"]:
                    C.memset("pool", hb(ct, 0, 30), 0.0)
                else:
                    C.copy("pool", hb(ct, 0, 30), HC(("t", ct), HC.t[:, ct, :].unsqueeze(1)))
        for ct in range(8):
            e = "dve" if ct % 2 == 0 else "pool"
            C.tt(e, hb(ct, 30, 30 + L), tokv(PJ, ct), tokv(PJ, 8 + ct), ALU.mult)
        for ct in range(8):
            e = "dve"
            acc = tokv(ACC, ct)
            wcol = lambda j, ct=ct: VEC0(None, VEC0.t[:, ct, j:j + 1])
            C.ts(e, acc, hb(ct, 0, L), wcol(0), wcol(31), ALU.mult, ALU.add)
            for j in range(1, 31):
                C.stt(e, acc, hb(ct, j, j + L), wcol(j), acc, ALU.mult, ALU.add)
        if is_s:
            for q in range(2):
                for ct in range(8):
                    ps = next_psb()
                    srcv = HB.t[:, ct, 0:nseq * W_].rearrange("p (n w) -> p n w", w=W_)[:, 4 * q:4 * q + 4, L:L + 30]
                    C.copy("pool", TMP(None, TMP.t[:, 0:120].rearrange("p (n r) -> p n r", r=30)), HB(("t", ct), srcv))
                    C.tr(ps(None, ps.t[0:120, 0:128]), TMP(None, TMP.t[:, 0:120]), ident.all())
                    C.copy("dve", XS(None, XS.t[0:120, ct * 128:(ct + 1) * 128]), ps(None, ps.t[0:120, 0:128]))
                C.dma(O["s_conv_o"][s0 * 30 + q * 120:s0 * 30 + (q + 1) * 120, :], XS(None, XS.t[0:120, 0:1024]))
        else:
            for ct in range(8):
                C.copy("pool", TMP(None, TMP.t[:, 0:30]), HB(("t", ct), HB.t[:, ct, L:L + 30]))
                C.copy("pool", HC(("t", ct), HC.t[:, ct, :]), TMP(None, TMP.t[:, 0:30]))
            if tile["last"]:
                for ct in range(8):
                    ps = next_psb()
                    C.tr(ps(None, ps.t[0:30, 0:128]), HC(("t", ct), HC.t[:, ct, :]), ident.all())
                    C.copy("dve", XS(None, XS.t[0:30, ct * 128:(ct + 1) * 128]), ps(None, ps.t[0:30, 0:128]))
                C.dma(O["p_conv"][:, :], XS(None, XS.t[0:30, 0:1024]))
        layer_norm_cols(ACC, 8, T, 32, 33, VEC0, func=AF.Silu, dst_fn=lambda ct: mixv(8 + ct, T))

        def evac_pw(j, pv):
            C.tt("dve", mixv(j, T), pv, pj(16 + j), ALU.mult)

        stream_mm(I["a_pw"], 8, [(i * 128, 128) for i in range(8)], lambda kt: mixv(8 + kt, T), T, evac_pw)

        def evacB(j, pv):
            if j < 8:
                C.copy(ev_eng(), pj(j), pv)
            else:
                C.act(pj(j), pv, AF.Silu)

        stream_mm(I["ev_w_in"], 16, [(3072 + i * 128, 128) for i in range(16)], xrhs, T, evacB)

        Wx = 1 + L

        def xv(buf, a, b, p0=0, p1=32):
            v = buf.t[:, p0:p1, 0:nseq * Wx].rearrange("p q (n w) -> p q n w", w=Wx)[:, :, :, a:b]
            return buf(None, v)

        if is_s:
            for k, (nm, buf) in enumerate((("s_sre", XR), ("s_sim", XI))):
                for q in range(2):
                    for pr in range(16):
                        g0 = 2 * (16 * q + pr)
                        C.dma(S5ST(None, S5ST.t[pr * 8:(pr + 1) * 8, :]),
                              I[nm][s0:s0 + 8, g0:g0 + 2, :].rearrange("n m p -> n (m p)"))
                    ps = next_psb()
                    C.tr(ps(None, ps.t[:, 0:128]), S5ST.all(), ident.all())
                    C.copy("dve", xv(buf, 0, 1, 16 * q, 16 * q + 16),
                           ps(None, ps.t[:, 0:128].rearrange("p (q n o) -> p q n o", n=8, o=1)))
        else:
            for k, buf in enumerate((XR, XI)):
                if tile["first"]:
                    C.memset("pool", xv(buf, 0, 1), 0.0)
                else:
                    C.copy("pool", xv(buf, 0, 1), S5C(None, S5C.t[:, k, :].unsqueeze(2).unsqueeze(3)))
        for q4 in range(8):
            for k, (tab, buf) in enumerate(((BRE, XR), (BIM, XI))):
                ps = next_psa()
                for ip in range(4):
                    hf = ip // 2
                    tb = tab[ip % 2]
                    C.mm(ps(None, ps.t[:, ip * T:(ip + 1) * T]),
                         tb(None, tb.t[64 * hf:64 * hf + 64, q4, :]),
                         PJ(("t", q4), PJ.t[64 * hf:64 * hf + 64, q4, 0:T]))
                C.copy("act", xv(buf, 1, Wx, 4 * q4, 4 * q4 + 4),
                       ps(None, ps.t[:, 0:4 * T].rearrange("p (q n l) -> p q n l", q=4, l=L)))
        arb = S5A(None, S5A.t[:, AR, :].unsqueeze(2).broadcast_to([128, 32, nseq]))
        aib = S5A(None, S5A.t[:, AI, :].unsqueeze(2).broadcast_to([128, 32, nseq]))
        naib = S5A(None, S5A.t[:, NAI, :].unsqueeze(2).broadcast_to([128, 32, nseq]))
        if nseq == 1:
            t1v = S5T(None, S5T.t[:, 8, :].unsqueeze(2))
            t2v = S5T(None, S5T.t[:, 9, :].unsqueeze(2))
        else:
            t1v = SCS(None, SCS.t[:, 0, :, :])
            t2v = SCS(None, SCS.t[:, 1, :, :])

        def col(buf, t):
            v = buf.t[:, :, 0:nseq * Wx].rearrange("p q (n w) -> p q n w", w=Wx)[:, :, :, t]
            return buf(None, v)

        if is_s:
            e = "dve"
            for t in range(L):
                C.tt(e, t1v, col(XR, t), arb, ALU.mult)
                C.tt(e, col(XR, t + 1), col(XR, t + 1), t1v, ALU.add)
                C.tt(e, t1v, col(XI, t), naib, ALU.mult)
                C.tt(e, col(XR, t + 1), col(XR, t + 1), t1v, ALU.add)
                C.tt(e, t2v, col(XI, t), arb, ALU.mult)
                C.tt(e, col(XI, t + 1), col(XI, t + 1), t2v, ALU.add)
                C.tt(e, t2v, col(XR, t), aib, ALU.mult)
                C.tt(e, col(XI, t + 1), col(XI, t + 1), t2v, ALU.add)
        else:
            VTMf_ = VTM.t[:, :, :].rearrange("p a b -> p (a b)")
            PJf_ = PJ.t[:, 24:32, :].rearrange("p a b -> p (a b)")
            ROWf_ = ROW.t[:, :, :].rearrange("p a b -> p (a b)")
            C.dma(VTM(None, VTMf_[:, 0:1024]), s5tab[0])
            C.dma(PJ(None, PJf_), s5tab[1])
            for t0 in range(0, T, 32):
                Tc = min(32, T - t0)
                ec = VTM(None, VTMf_[:, 0:1024].rearrange("p (q t) -> p q t", t=32)[:, :, 0:Tc])
                es = PJ(None, PJf_.rearrange("p (q t) -> p q t", t=32)[:, :, 0:Tc])
                w1 = KW(None, KW.t[:, :].rearrange("p (q t) -> p q t", t=32)[:, :, 0:Tc])
                w2 = ROW(None, ROWf_.rearrange("p (q t) -> p q t", t=32)[:, :, 0:Tc])
                ur = XR(None, XR.t[:, :, 1 + t0:1 + t0 + Tc])
                ui = XI(None, XI.t[:, :, 1 + t0:1 + t0 + Tc])
                C.tt("pool", w1, ur, es, ALU.mult)
                C.tt("dve", w2, ui, es, ALU.mult)
                C.tt("dve", ur, ur, ec, ALU.mult)
                C.tt("pool", ui, ui, ec, ALU.mult)
                C.tt("dve", ur, ur, w2, ALU.add)
                C.tt("pool", ui, ui, w1, ALU.subtract)
                for pr in range(32):
                    rho = S5A(None, S5A.t[:, MAG, pr:pr + 1].broadcast_to([128, Tc]))
                    for buf in (XR, XI):
                        seg = buf(None, buf.t[:, pr, 1 + t0:1 + t0 + Tc])
                        scan(seg, rho, seg, buf(None, buf.t[:, pr, t0:t0 + 1]), ALU.mult, ALU.add)
                C.tt("pool", w1, ur, es, ALU.mult)
                C.tt("dve", w2, ui, es, ALU.mult)
                C.tt("dve", ur, ur, ec, ALU.mult)
                C.tt("pool", ui, ui, ec, ALU.mult)
                C.tt("dve", ur, ur, w2, ALU.subtract)
                C.tt("pool", ui, ui, w1, ALU.add)
        if is_s:
            for k, (nm, buf) in enumerate((("s_sre_o", XR), ("s_sim_o", XI))):
                for q in range(2):
                    C.copy("pool", TMP(None, TMP.t[:, 0:128].rearrange("p (q n o) -> p q n o", n=8, o=1)),
                           xv(buf, L, L + 1, 16 * q, 16 * q + 16))
                    ps = next_psb()
                    C.tr(ps(None, ps.t[:, 0:128]), TMP(None, TMP.t[:, 0:128]), ident.all())
                    C.copy("dve", S5ST.all(), ps(None, ps.t[:, 0:128]))
                    for pr in range(16):
                        g0 = 2 * (16 * q + pr)
                        C.dma(O[nm][s0:s0 + 8, g0:g0 + 2, :].rearrange("n m p -> n (m p)"),
                              S5ST(None, S5ST.t[pr * 8:(pr + 1) * 8, :]))
        else:
            for k, buf in enumerate((XR, XI)):
                C.copy("pool", S5C(None, S5C.t[:, k, :].unsqueeze(2).unsqueeze(3)), xv(buf, L, L + 1))
            if tile["last"]:
                for k, nm in enumerate(("p_sre", "p_sim")):
                    ps = next_psb()
                    C.tr(ps(None, ps.t[0:32, 0:128]), S5C(None, S5C.t[:, k, :]), ident.all())
                    C.copy("dve", S5ST(None, S5ST.t[0:32, :]), ps(None, ps.t[0:32, 0:128]))
                    C.dma(O[nm].rearrange("(q m) p -> q (m p)", m=2), S5ST(None, S5ST.t[0:32, :]))
        for gt in range(8):
            ps = next_psa()
            for ip in range(4):
                pair = 4 * gt + ip
                hf = ip // 2
                ov = ps(None, ps.t[64 * hf:64 * hf + 64, 0:T])
                xr_ = XR(None, XR.t[:, pair, 0:nseq * Wx].rearrange("p (n w) -> p n w", w=Wx)[:, :, 1:Wx])
                xi_ = XI(None, XI.t[:, pair, 0:nseq * Wx].rearrange("p (n w) -> p n w", w=Wx)[:, :, 1:Wx])
                cr, ci = CRE[ip % 2], CIM[ip % 2]
                C.mm(ov, cr(None, cr.t[:, gt, 64 * hf:64 * hf + 64]), xr_, start=(ip % 2 == 0), stop=False)
                C.mm(ov, ci(None, ci.t[:, gt, 64 * hf:64 * hf + 64]), xi_, start=False, stop=(ip % 2 == 1))
            gv = pj(gt)
            C.stt("dve", gv, gv, VEC0(None, VEC0.t[:, gt, 34:35]), ps(None, ps.t[:, 0:T]), ALU.mult, ALU.add)
            C.act(GELB(("t", gt), GELB.t[:, gt, 0:T]), gv, AF.Gelu)

        def evac_glu(j, pv):
            if j < 8:
                C.act(ACC(("t", j), ACC.t[:, j, 0:T]), pv, AF.Identity, bias=VECG(None, VECG.t[:, j, 0:1]))
            else:
                jj = j - 8
                tv = TMP(None, TMP.t[:, 0:T])
                C.act(tv, pv, AF.Sigmoid, bias=VECG(None, VECG.t[:, j, 0:1]))
                C.tt("dve", tv, tv, ACC(("t", jj), ACC.t[:, jj, 0:T]), ALU.mult)
                C.tt("dve", mixv(8 + jj, T), tv, pj(8 + jj), ALU.mult)

        stream_mm(I["s5_glu_w"], 8, [(i * 128, 128) for i in range(16)], lambda kt: GELB(("t", kt), GELB.t[:, kt, 0:T]), T, evac_glu)
        out_proj_ln(I["ev_w_out"], tile, VECG, 1, 2)

    do_rwkv = cfg.get("rwkv", True)
    MASKBIG = {"p": C.sb("mbig_p", [128, 128]), "s": C.sb("mbig_s", [64, 64])}
    SEGTRI = {"p": C.sb("stri_p", [128, 128]), "s": C.sb("stri_s", [64, 64])}
    R01 = {"p": C.sb("r01p", [128, 128]), "s": C.sb("r01s", [128, 64])}
    RNEG = {"p": C.sb("rnegp", [128, 128]), "s": C.sb("rnegs", [128, 64])}
    SEGSEL = C.sb("SEGSEL", [8, 64])
    SEGC = C.sb("SEGC", [64, 8])
    SEGROW = C.sb("SEGROW", [128, 8, 64])
    for k in ("p", "s"):
        C.dma(MASKBIG[k].all(), I["maskbig_" + k])
        C.dma(SEGTRI[k].all(), I["segtri_" + k])
        C.dma(R01[k].all(), I["r01_" + k])
        C.dma(RNEG[k].all(), I["rneg_" + k])
    C.dma(SEGSEL.all(), I["segsel"])
    C.dma(SEGC.all(), I["segc"])
    C.dma(SEGROW.all(), I["segrow"].rearrange("p (n t) -> p n t", n=8))

    VEC1 = C.sb("VEC1", [128, 8, 8])
    VMU = C.sb("VMU", [128, 25, 1])
    VECO = C.sb("VECO", [128, 16, 2])
    GB = C.sb("GB", [8, 1])
    load_rows_T(VEC1, [I[n_] for n_ in ("m_hn_g", "r_w0", "r_a0", "r_kk", "r_ka", "r_ln_g", "r_ln_b", "r_rk")], 1024)
    load_rows_T(VMU, [I["r_mu"][:, 0:2048]], 2048)
    load_rows_T(VMU, [I["r_mu"][:, 2048:3200]], 1152, ct0=16)
    load_rows_T(VECO, [I["od_ln_g"], I["od_ln_b"]], 2048)
    C.dma(GB(None, GB.t[0:4, :]), I["m_ig_b"].rearrange("o h -> h o"), slow=True)
    C.dma(GB(None, GB.t[4:8, :]), I["m_fg_b"].rearrange("o h -> h o"), slow=True)

    VTM = C.sb("VTM", [128, 4, 257])
    KW = C.sb("KW", [128, 1024])
    KWN = C.sb("KWN", [128, 256])
    GX = C.sb("GX", [8, 128])
    ROW = C.sb("ROW", [128, 8, 128])
    COL = C.sb("COL", [128, 64])
    DTB = C.sb("DTB", [128, 128])
    STB = C.sb("STB", [128, 128])
    P1S = C.sb("P1S", [128, 257])
    NUM = C.sb("NUM", [128, 257])
    HN = C.sb("HN", [128, 256])
    SM = C.sb("SM", [128, 16])
    CS = C.sb("CS", [128, 4, 2, 257])
    MCAR = C.sb("MCAR", [128, 4])
    CSS = [C.sb("CSS%d" % i, [128, 2, 257]) for i in range(2)]
    CSO = [C.sb("CSO%d" % i, [128, 2, 257]) for i in range(2)]
    MS = C.sb("MS", [8, 4])
    MSB = C.sb("MSB", [8, 128])
    MINIT = C.sb("MINIT", [128, 4, 8])
    DEC = C.sb("DEC", [128, 4, 8])
    MNEW = C.sb("MNEW", [128, 4, 8])
    QM = [C.sb("QM%d" % i, [128, 64]) for i in range(2)]
    C.memset("pool", VTM(None, VTM.t[:, :, 256:257]), 1.0)

    def stream_mm_tok(W, c0, ncols, T, evac):
        Wv = W.rearrange("(kt p) c -> p kt c", p=128)
        for g in range(ncols // WG):
            ws = WS[rr["ws"] % NWS]
            rr["ws"] += 1
            C.dma(ws(None, ws.t[:, :, 0:WG]), Wv[:, :, c0 + g * WG:c0 + (g + 1) * WG], q="pool")
            ps = next_psa()
            pv = ps(None, ps.t[0:T, 0:WG])
            for kt in range(16):
                C.mm(pv, XTB(("t", kt), XTB.t[:, kt, 0:T]), ws(None, ws.t[:, kt, 0:WG]), start=(kt == 0), stop=(kt == 15))
            evac(g, pv)

    def recip(e, out, in_):
        return C.op(e, lambda g: g.reciprocal(out=out.ap, in_=in_.ap), reads=[in_], writes=[out])

    def scan(out, d0, d1, init, op0, op1):
        rd = [d0, d1] + ([init] if isinstance(init, View) else [])
        ia = init.ap if isinstance(init, View) else init
        return C.op("dve", lambda g: g.tensor_tensor_scan(out=out.ap, data0=d0.ap, data1=d1.ap, initial=ia, op0=op0, op1=op1),
                    reads=rd, writes=[out])

    SR = C.sb("SR", [128, 8, 64])
    W2A2 = C.sb("W2A2", [128, 1024])
    BLK = C.sb("BLK", [128, 128])
    OMKA = C.sb("OMKA", [128, 8])
    SHC = C.sb("SHC", [128, 25])
    SUMB = C.sb("SUMB", [128, 8])
    C.dma(W2A2(None, W2A2.t[0:64, :]), I["r_w2"])
    C.dma(W2A2(None, W2A2.t[64:128, :]), I["r_a2"])
    C.dma(BLK.all(), I["blk"])
    C.ts("dve", OMKA.all(), VEC1(None, VEC1.t[:, :, 4]), -1.0, 1.0, ALU.mult, ALU.add)
    XIf = XI.t[:, :, :].rearrange("p a b -> p (a b)")
    T1 = XI("T1", XIf[:, 0:512].rearrange("p (j k) -> p j k", k=64))
    T2 = XI("T2", XIf[:, 512:1024].rearrange("p (j k) -> p j k", k=64))
    FSv = XIf[:, 1024:1536].rearrange("p (i t) -> p i t", t=128)
    SRS = [XI(("SRS", i), XIf[:, 1536 + 512 * i:2048 + 512 * i].rearrange("p (j k) -> p j k", k=64)) for i in range(2)]
    TWv = XIf[:, 2560:2688]
    ALLPS = PSA + PSB

    def next_ps8():
        rr["ps8"] = (rr.get("ps8", 0) + 1) % 8
        return ALLPS[rr["ps8"]]

    def rwkv(tile):
        T, nseq, L = tile["T"], tile["nseq"], tile["L"]
        is_s = tile["kind"] == "s"
        s0 = tile["s0"]
        Wx = 1 + L
        xrhs = lambda kt: XTB(("t", kt), XTB.t[:, kt, 0:T])
        pj = lambda j: PJ(("t", j), PJ.t[:, j, 0:T])
        pj3 = lambda j: PJ(("t", j), PJ.t[:, j, 0:T].rearrange("p (n l) -> p n l", l=L))

        def ppv(j0, j1, a, b):
            v = XR.t[:, j0:j1, 0:nseq * Wx].rearrange("p j (n w) -> p j n w", w=Wx)[:, :, :, a:b]
            return XR(("pp", j0) if j1 == j0 + 1 else None, v)

        if is_s:
            for (c0, ncol, ct0) in ((0, 2048, 0), (2048, 1152, 16)):
                C.dma(XS(None, XS.t[0:8, 0:ncol]), I["s_rsh"][s0:s0 + 8, c0:c0 + ncol])
                for ct in range(ncol // 128):
                    ps = next_psb()
                    C.tr(ps(None, ps.t[:, 0:8]), XS(None, XS.t[0:8, ct * 128:(ct + 1) * 128]), ident(None, ident.t[0:8, 0:8]))
                    C.copy("dve", ppv(ct0 + ct, ct0 + ct + 1, 0, 1), ps(None, ps.t[:, 0:8].rearrange("p (j n o) -> p j n o", j=1, o=1)))
        else:
            if tile["first"]:
                C.memset("pool", ppv(0, 25, 0, 1), 0.0)
            else:
                C.copy("pool", ppv(0, 25, 0, 1), SHC(None, SHC.t[:, :].unsqueeze(2).unsqueeze(3)))

        def evacR(j, pv):
            if j < 25:
                C.copy(ev_eng(), ppv(j, j + 1, 1, Wx), pv.buf(None, pv.ap.rearrange("p (j n l) -> p j n l", j=1, l=L)))
            else:
                C.act(pj(j), pv, AF.Silu)

        stream_mm(I["od_w_in"], 16, [(5128 + i * 128, 128) for i in range(33)], xrhs, T, evacR)

        if is_s:
            for (c0, ncol, ct0) in ((0, 2048, 0), (2048, 1152, 16)):
                for ct in range(ncol // 128):
                    ps = next_psb()
                    C.copy("pool", TMP(None, TMP.t[:, 0:8]), XR(("pp", ct0 + ct), XR.t[:, ct0 + ct, 0:nseq * Wx].rearrange("p (n w) -> p n w", w=Wx)[:, :, L]))
                    C.tr(ps(None, ps.t[0:8, 0:128]), TMP(None, TMP.t[:, 0:8]), ident.all())
                    C.copy("dve", XS(None, XS.t[0:8, ct * 128:(ct + 1) * 128]), ps(None, ps.t[0:8, 0:128]))
                C.dma(O["s_rsh_o"][s0:s0 + 8, c0:c0 + ncol], XS(None, XS.t[0:8, 0:ncol]))
        else:
            C.copy("pool", SHC(None, SHC.t[:, :].unsqueeze(2).unsqueeze(3)), ppv(0, 25, L, L + 1))
            if tile["last"]:
                for (c0, ncol, ct0) in ((0, 2048, 0), (2048, 1152, 16)):
                    for ct in range(ncol // 128):
                        ps = next_psb()
                        C.tr(ps(None, ps.t[0:1, 0:128]), SHC(None, SHC.t[:, ct0 + ct:ct0 + ct + 1]), ident.all())
                        C.copy("dve", XS(None, XS.t[0:1, ct * 128:(ct + 1) * 128]), ps(None, ps.t[0:1, 0:128]))
                    C.dma(O["p_rsh"][:, c0:c0 + ncol], XS(None, XS.t[0:1, 0:ncol]))
        for j in range(25):
            C.tt("pool", pj3(j), XR(("pp", j), XR.t[:, j, 0:nseq * Wx].rearrange("p (n w) -> p n w", w=Wx)[:, :, 0:L]),
                 XR(("pp", j), XR.t[:, j, 0:nseq * Wx].rearrange("p (n w) -> p n w", w=Wx)[:, :, 1:Wx]), ALU.subtract)
            C.stt("dve", pj3(j), pj3(j), VMU(None, VMU.t[:, j, 0:1]),
                  XR(("pp", j), XR.t[:, j, 0:nseq * Wx].rearrange("p (n w) -> p n w", w=Wx)[:, :, 1:Wx]), ALU.mult, ALU.add)

        VTMf = VTM.t[:, :, :].rearrange("p a b -> p (a b)")
        ROWf = ROW.t[:, :, :].rearrange("p a b -> p (a b)")
        KKt = lambda a, b: KW(None, KW.t[0:T, a:b])
        Wt = lambda a, b: VTM(None, VTMf[0:T, a:b])
        KKAt = lambda a, b: ROW(None, ROWf[0:T, a:b])
        KPt = lambda a, b: XS(None, XS.t[0:T, a:b])
        Rt = lambda a, b: XS(None, XS.t[0:T, 1024 + a:1024 + b])
        fs = lambda i: XI(("FS", i), FSv[:, i, 0:T])
        tw = XI("TW", TWv[0:64, 0:T])
        C.act(tw, PJ(("t", 24), PJ.t[0:64, 24, 0:T]), AF.Tanh)

        def to_tok(dst, src):
            ps = next_psb()
            C.tr(ps(None, ps.t[0:T, 0:128]), src, ident.all())
            C.copy(ev_eng(), dst, ps(None, ps.t[0:T, 0:128]))

        NE05 = -float(np.exp(-0.5))
        for ct in range(8):
            r_, k_, v_ = pj(ct), pj(8 + ct), pj(16 + ct)
            cs_ = slice(ct * 128, (ct + 1) * 128)
            ps = next_psa()
            C.mm(ps(None, ps.t[:, 0:T]), W2A2(None, W2A2.t[0:64, cs_]), tw)
            C.act(fs(0), ps(None, ps.t[:, 0:T]), AF.Sigmoid, bias=VEC1(None, VEC1.t[:, ct, 1:2]))
            C.act(fs(0), fs(0), AF.Exp, scale=NE05)
            to_tok(Wt(ct * 128, (ct + 1) * 128), fs(0))
            ps = next_psa()
            C.mm(ps(None, ps.t[:, 0:T]), W2A2(None, W2A2.t[64:128, cs_]), PJ(("t", 24), PJ.t[64:128, 24, 0:T]))
            C.act(fs(1), ps(None, ps.t[:, 0:T]), AF.Sigmoid, bias=VEC1(None, VEC1.t[:, ct, 2:3]))
            C.ts("dve", fs(2), k_, VEC1(None, VEC1.t[:, ct, 3:4]), None, ALU.mult)
            C.tt("pool", fs(3), fs(2), fs(2), ALU.mult)
            ps = next_psa()
            C.mm(ps(None, ps.t[:, 0:T]), BLK.all(), fs(3))
            C.act(fs(3), ps(None, ps.t[:, 0:T]), AF.Sqrt)
            C.ts("dve", fs(3), fs(3), 1e-12, None, ALU.max)
            recip("dve", fs(3), fs(3))
            C.tt("dve", fs(2), fs(2), fs(3), ALU.mult)
            to_tok(KKt(ct * 128, (ct + 1) * 128), fs(2))
            C.tt("dve", fs(3), fs(2), fs(1), ALU.mult)
            to_tok(KKAt(ct * 128, (ct + 1) * 128), fs(3))
            C.ts("dve", fs(1), fs(1), VEC1(None, VEC1.t[:, ct, 4:5]), OMKA(None, OMKA.t[:, ct:ct + 1]), ALU.mult, ALU.add)
            C.tt("dve", fs(1), fs(1), k_, ALU.mult)
            to_tok(KPt(ct * 128, (ct + 1) * 128), fs(1))
            to_tok(Rt(ct * 128, (ct + 1) * 128), r_)
            C.tt("dve", fs(3), r_, fs(1), ALU.mult)
            C.ts("dve", fs(3), fs(3), VEC1(None, VEC1.t[:, ct, 7:8]), None, ALU.mult)
            ps = next_psa()
            C.mm(ps(None, ps.t[:, 0:T]), BLK.all(), fs(3))
            C.tt("dve", ACC(("t", ct), ACC.t[:, ct, 0:T]), ps(None, ps.t[:, 0:T]), v_, ALU.mult)

        Yv = MIX.t[:, 8 * TT:16 * TT].rearrange("p (j t) -> p j t", j=8)
        srcs = (("kk", KW.t[0:T, :]), ("w", VTMf[0:T, 0:1024]), ("kka", ROWf[0:T, 0:1024]), ("kp", XS.t[0:T, 0:1024]), ("r", XS.t[0:T, 1024:2048]))
        bufs = {"kk": KW, "w": VTM, "kka": ROW, "kp": XS, "r": XS}
        for n in range(nseq):
            if is_s:
                sr = SRS[n % 2]
                C.dma(sr, I["s_rs"][s0 + n].rearrange("(j hp) v k -> (hp v) j k", hp=2))
            else:
                sr = SR.all()
                if tile["first"]:
                    C.memset("pool", sr, 0.0)
            for l in range(L):
                t = n * L + l
                oh = ident(None, ident.t[0:T, t:t + 1].broadcast_to([T, 64]))
                bc = {}
                for nm, ap in srcs:
                    ps = next_ps8()
                    xv = ap.rearrange("p (j hp k) -> p hp j k", hp=2, k=64)
                    C.mm(ps(None, ps.t[0:64, 0:512]), oh, bufs[nm](None, xv[:, 0]))
                    C.mm(ps(None, ps.t[64:128, 0:512]), oh, bufs[nm](None, xv[:, 1]))
                    bc[nm] = ps(None, ps.t[:, 0:512].rearrange("p (j k) -> p j k", k=64))
                C.tt("dve", T1, sr, bc["kk"], ALU.mult)
                C.op("dve", lambda g: g.reduce_sum(out=SUMB.t[:, :], in_=T1.ap, axis=AX.X), reads=[T1], writes=[SUMB.all()])
                C.tt("dve", sr, sr, bc["w"], ALU.mult)
                C.tt("dve", T2, bc["kka"], SUMB(None, SUMB.t[:, :].unsqueeze(2).broadcast_to([128, 8, 64])), ALU.mult)
                C.tt("dve", sr, sr, T2, ALU.subtract)
                C.tt("dve", T1, bc["kp"], PJ(None, PJ.t[:, 16:24, t].unsqueeze(2).broadcast_to([128, 8, 64])), ALU.mult)
                C.tt("dve", sr, sr, T1, ALU.add)
                C.tt("dve", T2, sr, bc["r"], ALU.mult)
                yv = MIX(None, Yv[:, :, t])
                C.op("dve", lambda g, yv=yv: g.reduce_sum(out=yv.ap, in_=T2.ap, axis=AX.X), reads=[T2], writes=[yv])
            if is_s:
                C.dma(O["s_rs_o"][s0 + n].rearrange("(j hp) v k -> (hp v) j k", hp=2), sr)
        if (not is_s) and tile["last"]:
            C.dma(O["p_rs"].rearrange("(j hp) v k -> (hp v) j k", hp=2), SR.all())

        for j in range(8):
            y = mixv(8 + j, T)
            ps = next_psa()
            C.mm(ps(None, ps.t[:, 0:T]), BLK.all(), y)
            C.tt("pool", fs(0), y, y, ALU.mult)
            ps2 = next_psa()
            C.mm(ps2(None, ps2.t[:, 0:T]), BLK.all(), fs(0))
            C.ts("dve", fs(1), ps(None, ps.t[:, 0:T]), 1.0 / 64, None, ALU.mult)
            C.ts("dve", fs(2), ps2(None, ps2.t[:, 0:T]), 1.0 / 64, None, ALU.mult)
            C.tt("dve", fs(3), fs(1), fs(1), ALU.mult)
            C.tt("dve", fs(2), fs(2), fs(3), ALU.subtract)
            C.ts("dve", fs(2), fs(2), 64e-5, None, ALU.add)
            C.act(fs(2), fs(2), AF.Sqrt)
            recip("dve", fs(2), fs(2))
            C.tt("dve", y, y, fs(1), ALU.subtract)
            C.tt("dve", y, y, fs(2), ALU.mult)
            C.act(y, y, AF.Identity, bias=VEC1(None, VEC1.t[:, j, 6:7]), scale=VEC1(None, VEC1.t[:, j, 5:6]))
            C.tt("dve", y, y, ACC(("t", j), ACC.t[:, j, 0:T]), ALU.add)
            C.tt("dve", y, y, pj(25 + j), ALU.mult)

    SSTRI = {"p": C.sb("sstri_p", [128, 128]), "s": C.sb("sstri_s", [64, 64])}
    SSTRIT = {"p": C.sb("sstriT_p", [128, 128]), "s": C.sb("sstriT_s", [64, 64])}
    for k_ in ("p", "s"):
        C.dma(SSTRI[k_].all(), I["sstri_" + k_])
        C.dma(SSTRIT[k_].all(), I["sstriT_" + k_])
    WLB = C.sb("WLB", [128, 8, 8])
    HBf = HB.t[:, :, :].rearrange("p a b -> p (a b)")
    KKv = KW.t[:, :].rearrange("p (j t) -> p j t", t=128)
    BTv = ROW.t
    XRf = XR.t[:, :, :].rearrange("p a b -> p (a b)")
    S0Tv = XRf[:, 0:4096].rearrange("p (n j v) -> p n j v", n=8, j=8)

    def fence(buf, ap):
        C.op("pool", lambda g: g.memset(ap, 0.0), writes=[buf.all()])

    def rwkv2(tile):
        T, nseq, L = tile["T"], tile["nseq"], tile["L"]
        is_s = tile["kind"] == "s"
        kd = tile["kind"]
        s0 = tile["s0"]
        Wx = 1 + L
        xrhs = lambda kt: XTB(("t", kt), XTB.t[:, kt, 0:T])
        pj = lambda j: PJ(("t", j), PJ.t[:, j, 0:T])
        pj3 = lambda j: PJ(("t", j), PJ.t[:, j, 0:T].rearrange("p (n l) -> p n l", l=L))

        def ppv(j0, j1, a, b):
            v = XR.t[:, j0:j1, 0:nseq * Wx].rearrange("p j (n w) -> p j n w", w=Wx)[:, :, :, a:b]
            return XR(("pp", j0) if j1 == j0 + 1 else None, v)

        if is_s:
            for (c0, ncol, ct0) in ((0, 2048, 0), (2048, 1152, 16)):
                C.dma(XS(None, XS.t[0:8, 0:ncol]), I["s_rsh"][s0:s0 + 8, c0:c0 + ncol])
                for ct in range(ncol // 128):
                    ps = next_psb()
                    C.tr(ps(None, ps.t[:, 0:8]), XS(None, XS.t[0:8, ct * 128:(ct + 1) * 128]), ident(None, ident.t[0:8, 0:8]))
                    C.copy("dve", ppv(ct0 + ct, ct0 + ct + 1, 0, 1), ps(None, ps.t[:, 0:8].rearrange("p (j n o) -> p j n o", j=1, o=1)))
        else:
            if tile["first"]:
                C.memset("pool", ppv(0, 25, 0, 1), 0.0)
            else:
                C.copy("pool", ppv(0, 25, 0, 1), SHC(None, SHC.t[:, :].unsqueeze(2).unsqueeze(3)))

        def evacR(j, pv):
            if j < 25:
                C.copy(ev_eng(), ppv(j, j + 1, 1, Wx), pv.buf(None, pv.ap.rearrange("p (j n l) -> p j n l", j=1, l=L)))
            else:
                C.act(pj(j), pv, AF.Silu)

        stream_mm(I["od_w_in"], 16, [(5128 + i * 128, 128) for i in range(33)], xrhs, T, evacR)

        if is_s:
            for (c0, ncol, ct0) in ((0, 2048, 0), (2048, 1152, 16)):
                for ct in range(ncol // 128):
                    ps = next_psb()
                    C.copy("pool", TMP(None, TMP.t[:, 0:8]), XR(("pp", ct0 + ct), XR.t[:, ct0 + ct, 0:nseq * Wx].rearrange("p (n w) -> p n w", w=Wx)[:, :, L]))
                    C.tr(ps(None, ps.t[0:8, 0:128]), TMP(None, TMP.t[:, 0:8]), ident.all())
                    C.copy("dve", XS(None, XS.t[0:8, ct * 128:(ct + 1) * 128]), ps(None, ps.t[0:8, 0:128]))
                C.dma(O["s_rsh_o"][s0:s0 + 8, c0:c0 + ncol], XS(None, XS.t[0:8, 0:ncol]))
        else:
            C.copy("pool", SHC(None, SHC.t[:, :].unsqueeze(2).unsqueeze(3)), ppv(0, 25, L, L + 1))
            if tile["last"]:
                for (c0, ncol, ct0) in ((0, 2048, 0), (2048, 1152, 16)):
                    for ct in range(ncol // 128):
                        ps = next_psb()
                        C.tr(ps(None, ps.t[0:1, 0:128]), SHC(None, SHC.t[:, ct0 + ct:ct0 + ct + 1]), ident.all())
                        C.copy("dve", XS(None, XS.t[0:1, ct * 128:(ct + 1) * 128]), ps(None, ps.t[0:1, 0:128]))
                    C.dma(O["p_rsh"][:, c0:c0 + ncol], XS(None, XS.t[0:1, 0:ncol]))
        for j in range(25):
            C.tt("pool", pj3(j), XR(("pp", j), XR.t[:, j, 0:nseq * Wx].rearrange("p (n w) -> p n w", w=Wx)[:, :, 0:L]),
                 XR(("pp", j), XR.t[:, j, 0:nseq * Wx].rearrange("p (n w) -> p n w", w=Wx)[:, :, 1:Wx]), ALU.subtract)
            C.stt("dve", pj3(j), pj3(j), VMU(None, VMU.t[:, j, 0:1]),
                  XR(("pp", j), XR.t[:, j, 0:nseq * Wx].rearrange("p (n w) -> p n w", w=Wx)[:, :, 1:Wx]), ALU.mult, ALU.add)

        s0t = lambda n, j, rs=slice(0, 128): XR(("st", n), S0Tv[rs, n, j, :])
        if is_s:
            fence(XR, XR.t[0:1, 0, 0:1])
            for n2 in range(0, 8, 2):
                stg = XS.t[:, :].rearrange("p (n j d k) -> p n j d k", n=2, j=8, d=2)
                for nn in range(2):
                    for d in range(2):
                        C.dma(XS(None, stg[:, nn, :, d, :]), I["s_rs"][s0 + n2 + nn].rearrange("(j hp) v k -> (hp v) j k", hp=2))
                for nn in range(2):
                    n = n2 + nn
                    for j in range(8):
                        ps = next_psb()
                        C.tr(ps(None, ps.t[:, 0:128]), XS(None, stg[:, nn, j, :, :]), ident.all())
                        C.copy("dve", s0t(n, j, slice(0, 64)), ps(None, ps.t[0:64, 0:64]))
                        C.copy("act", s0t(n, j, slice(64, 128)), ps(None, ps.t[64:128, 64:128]))
        else:
            if tile["first"]:
                C.memset("pool", SR.all(), 0.0)

        VTMf = VTM.t[:, :, :].rearrange("p a b -> p (a b)")
        Vtm = lambda hc: XS(None, XS.t[0:T, hc])
        Btm = lambda hc: XS(None, XS.t[0:T, 1024 + hc.start:1024 + hc.stop])
        Ktm = lambda hc: VTM(None, VTMf[0:T, hc])
        fs = lambda i: XI(("FS", i), FSv[:, i, 0:T])
        tw = XI("TW", TWv[0:64, 0:T])
        C.act(tw, PJ(("t", 24), PJ.t[0:64, 24, 0:T]), AF.Tanh)

        def to_tok(dst, src):
            ps = next_psb()
            C.tr(ps(None, ps.t[0:T, 0:128]), src, ident.all())
            C.copy(ev_eng(), dst, ps(None, ps.t[0:T, 0:128]))

        NE05 = -float(np.exp(-0.5))
        kkc = lambda ct, rs=slice(0, 128): KW(("c", ct), KKv[rs, ct, 0:T])
        btc = lambda ct, rs=slice(0, 128): ROW(("c", ct), BTv[rs, ct, 0:T])
        for ct in range(8):
            r_, k_, v_ = pj(ct), pj(8 + ct), pj(16 + ct)
            cs_ = slice(ct * 128, (ct + 1) * 128)
            ps = next_psa()
            C.mm(ps(None, ps.t[:, 0:T]), W2A2(None, W2A2.t[0:64, cs_]), tw)
            C.act(fs(0), ps(None, ps.t[:, 0:T]), AF.Sigmoid, bias=VEC1(None, VEC1.t[:, ct, 1:2]))
            C.ts("dve", fs(0), fs(0), NE05, None, ALU.mult)
            scan(fs(1), R01[kd](None, R01[kd].t[:, 0:T]), fs(0), 0.0, ALU.mult, ALU.add)
            ps = next_psa()
            C.mm(ps(None, ps.t[:, 0:T]), W2A2(None, W2A2.t[64:128, cs_]), PJ(("t", 24), PJ.t[64:128, 24, 0:T]))
            C.act(fs(2), ps(None, ps.t[:, 0:T]), AF.Sigmoid, bias=VEC1(None, VEC1.t[:, ct, 2:3]))
            C.ts("dve", kkc(ct), k_, VEC1(None, VEC1.t[:, ct, 3:4]), None, ALU.mult)
            C.tt("pool", fs(3), kkc(ct), kkc(ct), ALU.mult)
            ps = next_psa()
            C.mm(ps(None, ps.t[:, 0:T]), BLK.all(), fs(3))
            C.act(fs(3), ps(None, ps.t[:, 0:T]), AF.Sqrt)
            C.ts("dve", fs(3), fs(3), 1e-12, None, ALU.max)
            recip("dve", fs(3), fs(3))
            C.tt("dve", kkc(ct), kkc(ct), fs(3), ALU.mult)
            C.tt("dve", btc(ct), kkc(ct), fs(2), ALU.mult)
            C.ts("dve", fs(2), fs(2), VEC1(None, VEC1.t[:, ct, 4:5]), OMKA(None, OMKA.t[:, ct:ct + 1]), ALU.mult, ALU.add)
            C.tt("dve", fs(2), fs(2), k_, ALU.mult)
            C.tt("pool", fs(3), r_, fs(2), ALU.mult)
            C.ts("dve", fs(3), fs(3), VEC1(None, VEC1.t[:, ct, 7:8]), None, ALU.mult)
            ps = next_psa()
            C.mm(ps(None, ps.t[:, 0:T]), BLK.all(), fs(3))
            C.tt("dve", ACC(("t", ct), ACC.t[:, ct, 0:T]), ps(None, ps.t[:, 0:T]), v_, ALU.mult)
            C.act(fs(3), fs(1), AF.Exp)
            C.tt("dve", r_, r_, fs(3), ALU.mult)
            C.copy("pool", WLB(None, WLB.t[:, ct, 0:nseq]),
                   XI(("FS", 3), FSv[:, 3, 0:T].rearrange("p (n l) -> p n l", l=L)[:, :, L - 1]))
            C.tt("dve", fs(3), fs(1), fs(0), ALU.subtract)
            C.act(fs(3), fs(3), AF.Exp)
            C.tt("dve", kkc(ct), kkc(ct), fs(3), ALU.mult)
            C.act(fs(3), fs(1), AF.Exp, scale=-1.0)
            C.tt("dve", btc(ct), btc(ct), fs(3), ALU.mult)
            C.tt("dve", k_, fs(2), fs(3), ALU.mult)
            to_tok(Vtm(cs_), v_)
            to_tok(Btm(cs_), btc(ct))
            to_tok(Ktm(cs_), k_)

        fence(HB, HB.t[0:1, 0, 0:1])
        mat = lambda i: HB(("m", i), HBf[0:T, i * 128:i * 128 + T])
        half = lambda i, a: HB(("m", i), HBf[0:T, i * 128 + 64 * a:i * 128 + 64 * a + 64])
        nsq = max(0, int(np.ceil(np.log2(L))) - 1)
        idT = ident(None, ident.t[0:T, 0:T])
        mS = SSTRI[kd](None, SSTRI[kd].t[0:T, 0:T])
        mST = SSTRIT[kd](None, SSTRIT[kd].t[0:T, 0:T])
        mI = SEGTRI[kd](None, SEGTRI[kd].t[0:T, 0:T])
        if is_s:
            KKM, RM = T1, T2
        for j in range(8):
            Q = [[mat(8 * hp + 0), mat(8 * hp + 1)] for hp in range(2)]
            QT = [[mat(8 * hp + 2), mat(8 * hp + 3)] for hp in range(2)]
            Pm = [mat(8 * hp + 4) for hp in range(2)]
            BR = [mat(8 * hp + 5) for hp in range(2)]
            AK = [mat(8 * hp + 6) for hp in range(2)]
            KR = [mat(8 * hp + 7) for hp in range(2)]
            RHS = [half(16, hp) for hp in range(2)]
            SAT = [half(17, hp) for hp in range(2)]
            rsl = [slice(0, 64), slice(64, 128)]
            if is_s:
                C.tt("pool", KKM, KW(("c", j), KKv[:, j, 0:T].unsqueeze(1).broadcast_to([128, 8, T])), SEGROW(None, SEGROW.t[:, :, 0:T]), ALU.mult)
                C.tt("pool", RM, PJ(("t", j), PJ.t[:, j, 0:T].unsqueeze(1).broadcast_to([128, 8, T])), SEGROW(None, SEGROW.t[:, :, 0:T]), ALU.mult)
            for hp in range(2):
                rs = rsl[hp]
                rq = PJ(("t", j), PJ.t[rs, j, 0:T])
                kq = PJ(("t", 8 + j), PJ.t[rs, 8 + j, 0:T])
                ps = next_ps8()
                C.mm(ps(None, ps.t[0:T, 0:T]), btc(j, rs), kkc(j, rs))
                C.mm(ps(None, ps.t[0:T, T:2 * T]), btc(j, rs), rq)
                C.tt("dve", Q[hp][0], ps(None, ps.t[0:T, 0:T]), mS, ALU.mult)
                C.tt("dve", BR[hp], ps(None, ps.t[0:T, T:2 * T]), mI, ALU.mult)
                ps = next_ps8()
                C.mm(ps(None, ps.t[0:T, 0:T]), kq, kkc(j, rs))
                C.mm(ps(None, ps.t[0:T, T:2 * T]), kq, rq)
                C.tt("dve", AK[hp], ps(None, ps.t[0:T, 0:T]), mS, ALU.mult)
                C.tt("dve", KR[hp], ps(None, ps.t[0:T, T:2 * T]), mI, ALU.mult)
                ps = next_ps8()
                C.mm(ps(None, ps.t[0:T, 0:T]), kkc(j, rs), btc(j, rs))
                C.tt("dve", QT[hp][0], ps(None, ps.t[0:T, 0:T]), mST, ALU.mult)
                C.stt("dve", Pm[hp], Q[hp][0], -1.0, idT, ALU.mult, ALU.add)
            cur = 0
            for it in range(nsq):
                nxt = 1 - cur
                last_it = (it == nsq - 1)
                for hp in range(2):
                    if not last_it:
                        ps = next_ps8()
                        C.mm(ps(None, ps.t[0:T, 0:T]), QT[hp][cur], Q[hp][cur])
                        C.copy("act", Q[hp][nxt], ps(None, ps.t[0:T, 0:T]))
                    ps = next_ps8()
                    C.mm(ps(None, ps.t[0:T, 0:T]), Q[hp][cur], QT[hp][cur])
                    C.copy("dve", QT[hp][nxt], ps(None, ps.t[0:T, 0:T]))
                for hp in range(2):
                    ps = next_ps8()
                    C.mm(ps(None, ps.t[0:T, 0:T]), QT[hp][nxt], Pm[hp])
                    C.tt("dve", Pm[hp], Pm[hp], ps(None, ps.t[0:T, 0:T]), ALU.add)
                cur = nxt
            for hp in range(2):
                rs = rsl[hp]
                hc = slice((2 * j + hp) * 64, (2 * j + hp) * 64 + 64)
                ps = next_ps8()
                if is_s:
                    for n in range(nseq):
                        C.mm(ps(None, ps.t[0:T, 0:64]), XI("T1", KKM.ap[rs, n, :]), s0t(n, j, rs), start=(n == 0), stop=False)
                else:
                    C.mm(ps(None, ps.t[0:T, 0:64]), kkc(j, rs), SR(None, SR.t[rs, j, :]), start=True, stop=False)
                C.mm(ps(None, ps.t[0:T, 0:64]), AK[hp], Vtm(hc), start=False, stop=True)
                C.act(RHS[hp], ps(None, ps.t[0:T, 0:64]), AF.Identity, scale=-1.0)
                ps = next_ps8()
                C.mm(ps(None, ps.t[0:T, 0:64]), Pm[hp], RHS[hp])
                C.copy("dve", SAT[hp], ps(None, ps.t[0:T, 0:64]))
            psY = next_ps8()
            for hp in range(2):
                rs = rsl[hp]
                hc = slice((2 * j + hp) * 64, (2 * j + hp) * 64 + 64)
                ov = psY(None, psY.t[rs, 0:T])
                if is_s:
                    for n in range(nseq):
                        C.mm(ov, s0t(n, j, rs), XI("T2", RM.ap[rs, n, :]), start=(n == 0), stop=False)
                else:
                    C.mm(ov, SR(None, SR.t[rs, j, :]), PJ(("t", j), PJ.t[rs, j, 0:T]), start=True, stop=False)
                C.mm(ov, SAT[hp], BR[hp], start=False, stop=False)
                C.mm(ov, Vtm(hc), KR[hp], start=False, stop=True)
            y = TMP(None, TMP.t[:, 0:T])
            C.copy("act", y, psY(None, psY.t[:, 0:T]))
            ps = next_psa()
            C.mm(ps(None, ps.t[:, 0:T]), BLK.all(), y)
            C.tt("pool", fs(0), y, y, ALU.mult)
            ps2 = next_psa()
            C.mm(ps2(None, ps2.t[:, 0:T]), BLK.all(), fs(0))
            C.ts("dve", fs(1), ps(None, ps.t[:, 0:T]), 1.0 / 64, None, ALU.mult)
            C.ts("dve", fs(2), ps2(None, ps2.t[:, 0:T]), 1.0 / 64, None, ALU.mult)
            C.tt("dve", fs(3), fs(1), fs(1), ALU.mult)
            C.tt("dve", fs(2), fs(2), fs(3), ALU.subtract)
            C.ts("dve", fs(2), fs(2), 64e-5, None, ALU.add)
            C.act(fs(2), fs(2), AF.Sqrt)
            recip("dve", fs(2), fs(2))
            C.tt("dve", y, y, fs(1), ALU.subtract)
            C.tt("dve", y, y, fs(2), ALU.mult)
            C.act(y, y, AF.Identity, bias=VEC1(None, VEC1.t[:, j, 6:7]), scale=VEC1(None, VEC1.t[:, j, 5:6]))
            C.tt("dve", y, y, ACC(("t", j), ACC.t[:, j, 0:T]), ALU.add)
            C.tt("dve", mixv(8 + j, T), y, pj(25 + j), ALU.mult)
            tmpS = XI(("FS", 0), FSv[:, 0, 0:64])
            if is_s:
                SAM = [CSS[hp](None, CSS[hp].t[0:T, :, :].rearrange("p a b -> p (a b)")[:, 0:512].rearrange("p (n v) -> p n v", n=8)) for hp in range(2)]
                VM = [CSO[hp](None, CSO[hp].t[0:T, :, :].rearrange("p a b -> p (a b)")[:, 0:512].rearrange("p (n v) -> p n v", n=8)) for hp in range(2)]
                segc_b = SEGC(None, SEGC.t[0:T, :].unsqueeze(2).broadcast_to([T, 8, 64]))
                for hp in range(2):
                    hc = slice((2 * j + hp) * 64, (2 * j + hp) * 64 + 64)
                    C.tt("pool", SAM[hp], HB(("m", 17), HBf[0:T, 17 * 128 + 64 * hp:17 * 128 + 64 * hp + 64].unsqueeze(1).broadcast_to([T, 8, 64])), segc_b, ALU.mult)
                    C.tt("pool", VM[hp], XS(None, XS.t[0:T, hc].unsqueeze(1).broadcast_to([T, 8, 64])), segc_b, ALU.mult)
                for n in range(nseq):
                    psS = next_ps8()
                    for hp in range(2):
                        rs = rsl[hp]
                        hc = slice((2 * j + hp) * 64, (2 * j + hp) * 64 + 64)
                        C.mm(psS(None, psS.t[rs, 0:64]), Btm(hc), CSS[hp](None, SAM[hp].ap[:, n, :]), start=True, stop=False)
                        C.mm(psS(None, psS.t[rs, 0:64]), Ktm(hc), CSO[hp](None, VM[hp].ap[:, n, :]), start=False, stop=True)
                    wl = WLB(None, WLB.t[:, j, n:n + 1])
                    C.ts("dve", tmpS, s0t(n, j), wl, None, ALU.mult)
                    C.stt("dve", s0t(n, j), psS(None, psS.t[:, 0:64]), wl, tmpS, ALU.mult, ALU.add)
            else:
                psS = next_ps8()
                for hp in range(2):
                    rs = rsl[hp]
                    hc = slice((2 * j + hp) * 64, (2 * j + hp) * 64 + 64)
                    C.mm(psS(None, psS.t[rs, 0:64]), Btm(hc), SAT[hp], start=True, stop=False)
                    C.mm(psS(None, psS.t[rs, 0:64]), Ktm(hc), Vtm(hc), start=False, stop=True)
                wl = WLB(None, WLB.t[:, j, 0:1])
                srj = SR(None, SR.t[:, j, :])
                C.ts("dve", tmpS, srj, wl, None, ALU.mult)
                C.stt("dve", srj, psS(None, psS.t[:, 0:64]), wl, tmpS, ALU.mult, ALU.add)

        def state_out(src_fn, dst):
            for j in range(8):
                ps = next_psb()
                C.tr(ps(None, ps.t[0:64, 0:128]), src_fn(j), ident.all())
                C.copy(ev_eng(), XS(None, XS.t[0:64, j * 128:(j + 1) * 128]), ps(None, ps.t[0:64, 0:128]))
            C.dma(dst.rearrange("(j hp) v k -> v j hp k", hp=2), XS(None, XS.t[0:64, 0:1024].rearrange("p (j hp k) -> p j hp k", j=8, hp=2)))

        if is_s:
            for n in range(nseq):
                state_out(lambda j, n=n: s0t(n, j), O["s_rs_o"][s0 + n])
        elif tile["last"]:
            state_out(lambda j: SR(None, SR.t[:, j, :]), O["p_rs"])


    def layer1(tile):
        T, nseq, L = tile["T"], tile["nseq"], tile["L"]
        is_s = tile["kind"] == "s"
        kd = tile["kind"]
        s0 = tile["s0"]
        xrhs = lambda kt: XTB(("t", kt), XTB.t[:, kt, 0:T])
        pj = lambda j: PJ(("t", j), PJ.t[:, j, 0:T])
        W = I["od_w_in"]

        def evacM(j, pv):
            if j < 8:
                C.act(pj(j), pv, AF.Identity, scale=1.0 / 16.0)
            elif j < 16:
                C.copy(ev_eng(), pj(j), pv)
            elif j < 24:
                C.act(pj(j), pv, AF.Sigmoid)
            elif j == 24:
                C.copy("dve", PJ(("t", 32), PJ.t[0:8, 32, 0:T]), pv)
            else:
                C.act(pj(j - 1), pv, AF.Silu)

        cols = [(i * 128, 128) for i in range(16)] + [(3072 + i * 128, 128) for i in range(8)] + [(4096, 8)] + \
               [(4104 + i * 128, 128) for i in range(8)]
        stream_mm(W, 16, cols, xrhs, T, evacM)

        def evacV(g, pv):
            hh, half = divmod(g, 256 // WG)
            C.copy(ev_eng(), VTM(None, VTM.t[0:T, hh, half * WG:(half + 1) * WG]), pv)

        C.memset("pool", VTM(None, VTM.t[:, :, 256:257]), 1.0)
        stream_mm_tok(W, 2048, 1024, T, evacV)

        gx = GX(None, GX.t[0:8, 0:T])
        C.ts("dve", gx, PJ(("t", 32), PJ.t[0:8, 32, 0:T]), GB.all(), None, ALU.add)
        col = lambda a, b: COL(None, COL.t[0:T, a:b])
        ps = next_psb()
        C.tr(ps(None, ps.t[0:T, 0:8]), gx, ident(None, ident.t[0:8, 0:8]))
        C.copy("dve", col(0, 8), ps(None, ps.t[0:T, 0:8]))
        C.act(col(8, 12), col(4, 8), AF.Exp, scale=-1.0)
        C.act(col(8, 12), col(8, 12), AF.Ln, bias=1.0)
        C.ts("dve", col(8, 12), col(8, 12), -1.0, None, ALU.mult)
        ps = next_psb()
        C.mm(ps(None, ps.t[0:T, 0:4]), SEGTRI[kd](None, SEGTRI[kd].t[0:T, 0:T]), col(8, 12))
        C.copy("dve", col(12, 16), ps(None, ps.t[0:T, 0:4]))
        C.tt("dve", col(16, 20), col(0, 4), col(12, 16), ALU.subtract)
        if is_s:
            C.dma(MS.all(), I["s_mm"][s0:s0 + 8, :])
            ps = next_psb()
            C.mm(ps(None, ps.t[0:T, 0:4]), SEGSEL(None, SEGSEL.t[0:8, 0:T]), MS.all())
            C.copy("dve", col(20, 24), ps(None, ps.t[0:T, 0:4]))
            for h in range(4):
                C.copy("dve", MSB.all(), MS(None, MS.t[:, h:h + 1].broadcast_to([8, 128])))
                ps = next_psb()
                C.mm(ps(None, ps.t[:, 0:8]), MSB.all(), ident(None, ident.t[0:8, 0:8]))
                C.copy("dve", MINIT(None, MINIT.t[:, h, :]), ps(None, ps.t[:, 0:8]))
        else:
            if tile["first"]:
                C.memset("dve", MCAR.all(), 0.0)
                C.memset("pool", CS.all(), 0.0)
            C.copy("dve", col(20, 24), MCAR(None, MCAR.t[0:T, :]))
            C.copy("dve", MINIT(None, MINIT.t[:, :, 0:1]), MCAR(None, MCAR.t[:, :].unsqueeze(2)))

        row = lambda k: ROW(None, ROW.t[:, k, 0:T])
        ends = lambda k: ROW(None, ROW.t[:, k, 0:T].rearrange("p (n l) -> p n l", l=L)[:, :, L - 1])
        starts = lambda k: ROW(None, ROW.t[:, k, 0:T].rearrange("p (n l) -> p n l", l=L)[:, :, 0])
        for h in range(4):
            minit_r = MINIT(None, MINIT.t[:, h, 0:nseq])
            ps = next_psb()
            C.mm(ps(None, ps.t[:, 0:T]), ident(None, ident.t[0:8, h:h + 1].broadcast_to([8, 128])), gx)
            C.copy("dve", row(0), ps(None, ps.t[:, 0:T]))
            ps = next_psb()
            C.mm(ps(None, ps.t[:, 0:T]), ident(None, ident.t[0:8, 4 + h:5 + h].broadcast_to([8, 128])), gx)
            C.act(row(1), ps(None, ps.t[:, 0:T]), AF.Exp, scale=-1.0)
            C.act(row(1), row(1), AF.Ln, bias=1.0)
            C.ts("dve", row(1), row(1), -1.0, None, ALU.mult)
            scan(row(2), R01[kd](None, R01[kd].t[:, 0:T]), row(1), 0.0, ALU.mult, ALU.add)
            C.tt("dve", row(3), row(0), row(2), ALU.subtract)
            C.tt("dve", starts(3), starts(3), minit_r, ALU.max)
            scan(row(4), RNEG[kd](None, RNEG[kd].t[:, 0:T]), row(3), -1e30, ALU.add, ALU.max)
            C.copy("dve", ROW(None, ROW.t[:, 5, 0:T].rearrange("p (n l) -> p n l", l=L)),
                   ROW(None, ROW.t[:, 4, 0:T].rearrange("p (n l) -> p n l", l=L)[:, :, L - 1:L].broadcast_to([128, nseq, L])))
            C.tt("dve", MNEW(None, MNEW.t[:, h, 0:nseq]), ends(2), ends(4), ALU.add)
            C.tt("dve", DEC(None, DEC.t[:, h, 0:nseq]), minit_r, ends(4), ALU.subtract)
            C.act(DEC(None, DEC.t[:, h, 0:nseq]), DEC(None, DEC.t[:, h, 0:nseq]), AF.Exp)
            ps = next_psb()
            C.mm(ps(None, ps.t[0:T, 0:1]), row(4), ident(None, ident.t[:, 0:1]))
            C.mm(ps(None, ps.t[0:T, 1:2]), row(5), ident(None, ident.t[:, 0:1]))
            C.copy("dve", col(24, 26), ps(None, ps.t[0:T, 0:2]))
            C.tt("dve", DTB(None, DTB.t[0:T, 0:T]), ROW(None, ROW.t[0:T, 4, 0:T]), MASKBIG[kd](None, MASKBIG[kd].t[0:T, 0:T]), ALU.add)
            C.act(DTB(None, DTB.t[0:T, 0:T]), DTB(None, DTB.t[0:T, 0:T]), AF.Exp, scale=-1.0, bias=col(16 + h, 17 + h))
            ps = next_psa()
            for kt in range(2):
                C.mm(ps(None, ps.t[0:T, 0:T]), pj(8 + 2 * h + kt), pj(2 * h + kt), start=(kt == 0), stop=(kt == 1))
            C.tt("dve", STB(None, STB.t[0:T, 0:T]), ps(None, ps.t[0:T, 0:T]), DTB(None, DTB.t[0:T, 0:T]), ALU.mult)
            ps1 = next_psa()
            C.mm(ps1(None, ps1.t[0:T, 0:257]), STB(None, STB.t[0:T, 0:T]), VTM(None, VTM.t[0:T, h, :]))
            C.copy("act", P1S(None, P1S.t[0:T, :]), ps1(None, ps1.t[0:T, 0:257]))
            ps2 = next_psa()
            if is_s:
                i_ = 0
                for n in range(nseq):
                    cs = CSS[n % 2]
                    C.dma(cs(None, cs.t[:, :, 0:256]), I["s_mc"][s0 + n, h].rearrange("(kt p) v -> p kt v", p=128))
                    C.dma(cs(None, cs.t[:, :, 256:257]), I["s_mn"][s0 + n, h].rearrange("(kt p o) -> p kt o", p=128, o=1), slow=True)
                    for kt in range(2):
                        qm = QM[i_ % 2]
                        i_ += 1
                        C.tt("pool", qm(None, qm.t[:, 0:T]), pj(2 * h + kt), SEGROW(None, SEGROW.t[:, n, 0:T]), ALU.mult)
                        C.mm(ps2(None, ps2.t[0:T, 0:257]), qm(None, qm.t[:, 0:T]), cs(None, cs.t[:, kt, :]),
                             start=(n == 0 and kt == 0), stop=(n == nseq - 1 and kt == 1))
            else:
                for kt in range(2):
                    C.mm(ps2(None, ps2.t[0:T, 0:257]), pj(2 * h + kt), CS(("h", h), CS.t[:, h, kt, :]), start=(kt == 0), stop=(kt == 1))
            sm = lambda a: SM(None, SM.t[0:T, a:a + 1])
            C.tt("dve", sm(0), col(20 + h, 21 + h), col(24, 25), ALU.subtract)
            C.act(sm(0), sm(0), AF.Exp)
            C.stt("dve", NUM(None, NUM.t[0:T, :]), ps2(None, ps2.t[0:T, 0:257]), sm(0), P1S(None, P1S.t[0:T, :]), ALU.mult, ALU.add)
            C.tt("dve", sm(1), col(12 + h, 13 + h), col(24, 25), ALU.add)
            C.act(sm(1), sm(1), AF.Exp, scale=-1.0)
            C.ts("dve", sm(2), NUM(None, NUM.t[0:T, 256:257]), -1.0, None, ALU.mult)
            C.tt("dve", sm(2), sm(2), NUM(None, NUM.t[0:T, 256:257]), ALU.max)
            C.tt("dve", sm(2), sm(2), sm(1), ALU.max)
            recip("dve", sm(2), sm(2))
            C.op("dve", lambda g, T=T: g.reduce_sum(out=SM.t[0:T, 3:4], in_=NUM.t[0:T, 0:256], axis=AX.X),
                 reads=[NUM(None, NUM.t[0:T, 0:256])], writes=[sm(3)])
            C.tt("dve", sm(3), sm(3), sm(2), ALU.mult)
            C.ts("dve", sm(3), sm(3), 1.0 / 256, None, ALU.mult)
            hn = HN(None, HN.t[0:T, :])
            C.ts("dve", hn, NUM(None, NUM.t[0:T, 0:256]), sm(2), sm(3), ALU.mult, ALU.subtract)
            C.tt("dve", P1S(None, P1S.t[0:T, 0:256]), hn, hn, ALU.mult)
            C.op("dve", lambda g, T=T: g.reduce_sum(out=SM.t[0:T, 4:5], in_=P1S.t[0:T, 0:256], axis=AX.X),
                 reads=[P1S(None, P1S.t[0:T, 0:256])], writes=[sm(4)])
            C.ts("dve", sm(4), sm(4), 1.0 / 256, LN_EPS, ALU.mult, ALU.add)
            C.act(sm(4), sm(4), AF.Sqrt)
            recip("dve", sm(4), sm(4))
            C.ts("dve", hn, hn, sm(4), None, ALU.mult)
            for kt in range(2):
                ct = 2 * h + kt
                ps = next_psb()
                C.tr(ps(None, ps.t[:, 0:T]), HN(None, HN.t[0:T, kt * 128:(kt + 1) * 128]), ident(None, ident.t[0:T, 0:T]))
                scr = P1S(None, P1S.t[:, 0:T])
                C.stt("dve", scr, ps(None, ps.t[:, 0:T]), VEC1(None, VEC1.t[:, ct, 0:1]), pj(16 + ct), ALU.mult, ALU.mult)
                C.tt("dve", mixv(ct, T), scr, pj(24 + ct), ALU.mult)
            C.tt("dve", sm(5), col(16 + h, 17 + h), col(25, 26), ALU.subtract)
            C.act(sm(5), sm(5), AF.Exp)
            for kt in range(2):
                ps = next_psb()
                C.tr(ps(None, ps.t[0:T, 0:128]), pj(8 + 2 * h + kt), ident.all())
                C.ts("dve", KW(None, KW.t[0:T, h * 256 + kt * 128:h * 256 + (kt + 1) * 128]), ps(None, ps.t[0:T, 0:128]), sm(5), None, ALU.mult)
            if is_s:
                for n in range(nseq):
                    cs = CSS[n % 2]
                    co = CSO[n % 2]
                    C.dma(cs(None, cs.t[:, :, 0:256]), I["s_mc"][s0 + n, h].rearrange("(kt p) v -> p kt v", p=128))
                    C.dma(cs(None, cs.t[:, :, 256:257]), I["s_mn"][s0 + n, h].rearrange("(kt p o) -> p kt o", p=128, o=1), slow=True)
                    C.ts("pool", KWN(None, KWN.t[0:T, :]), KW(None, KW.t[0:T, h * 256:(h + 1) * 256]), SEGC(None, SEGC.t[0:T, n:n + 1]), None, ALU.mult)
                    for kt in range(2):
                        ps = next_psa()
                        C.mm(ps(None, ps.t[:, 0:257]), KWN(None, KWN.t[0:T, kt * 128:(kt + 1) * 128]), VTM(None, VTM.t[0:T, h, :]))
                        C.stt("dve", co(None, co.t[:, kt, :]), cs(None, cs.t[:, kt, :]), DEC(None, DEC.t[:, h, n:n + 1]), ps(None, ps.t[:, 0:257]), ALU.mult, ALU.add)
                    C.dma(O["s_mc_o"][s0 + n, h].rearrange("(kt p) v -> p kt v", p=128), co(None, co.t[:, :, 0:256]))
                    C.dma(O["s_mn_o"][s0 + n, h].rearrange("(kt p o) -> p kt o", p=128, o=1), co(None, co.t[:, :, 256:257]), slow=True)
                C.dma(O["s_mm_o"][s0:s0 + 8, h:h + 1].rearrange("n o -> o n"), MNEW(None, MNEW.t[0:1, h, 0:8]), slow=True)
            else:
                for kt in range(2):
                    ps = next_psa()
                    C.mm(ps(None, ps.t[:, 0:257]), KW(None, KW.t[0:T, h * 256 + kt * 128:h * 256 + (kt + 1) * 128]), VTM(None, VTM.t[0:T, h, :]))
                    csv = CS(("h", h), CS.t[:, h, kt, :])
                    C.stt("dve", csv, csv, DEC(None, DEC.t[:, h, 0:1]), ps(None, ps.t[:, 0:257]), ALU.mult, ALU.add)
                C.copy("dve", MCAR(None, MCAR.t[:, h:h + 1]), MNEW(None, MNEW.t[:, h, 0:1]))
        if (not is_s) and tile["last"]:
            for h in range(4):
                C.dma(O["p_mc"][h].rearrange("(kt p) v -> p kt v", p=128), CS(("h", h), CS.t[:, h, :, 0:256]))
                C.dma(O["p_mn"][h].rearrange("(kt p o) -> p kt o", p=128, o=1), CS(("h", h), CS.t[:, h, :, 256:257]), slow=True)
            C.dma(O["p_mm"][:, :], MCAR(None, MCAR.t[0:1, :]))

        if do_rwkv:
            (rwkv2 if cfg.get("rwkv2", True) else rwkv)(tile)
        else:
            for ct in range(8, 16):
                C.memset("pool", mixv(ct, T), 0.0)
        out_proj_ln(I["od_w_out"], tile, VECO, 0, 1)

    for tile in tile_plan(cfg):
        load_x(tile)
        if nlayers >= 1:
            layer0(tile)
        if nlayers >= 2:
            layer1(tile)
        store_y(tile)

    C.emit()
    es.close()
    return nc, C


def make_in_maps(inp, cores, consts):
    maps = []
    f = lambda a: np.ascontiguousarray(a, dtype=np.float32)
    for c in cores:
        s = c % 4
        m = {}
        m["xp"] = f(inp["x_prompt"][s])
        m["meta"] = f(inp["meta_tokens"])
        m["xs"] = f(inp["x_sample"][16 * c:16 * c + 16].reshape(128, D))
        m["s_conv"] = f(inp["state_conv"][0, 16 * c:16 * c + 16].reshape(480, 1024))
        m["s_sre"] = f(inp["state_ssm_re"][0, 16 * c:16 * c + 16])
        m["s_sim"] = f(inp["state_ssm_im"][0, 16 * c:16 * c + 16])
        m.update(consts)
        sl = slice(16 * c, 16 * c + 16)
        m["s_mc"] = f(inp["state_mlstm_c"][0, sl])
        m["s_mn"] = f(inp["state_mlstm_n"][0, sl])
        m["s_mm"] = f(inp["state_mlstm_m"][0, sl])
        m["s_rs"] = f(inp["state_rwkv_s"][0, sl])
        m["s_rsh"] = f(inp["state_rwkv_shift"][0, sl])
        m["od_w_in"] = f(inp["od_w_in"][0])
        for nm in ("m_ig_b", "m_fg_b", "m_hn_g", "r_w0", "r_a0", "r_kk", "r_ka", "r_ln_g", "r_ln_b", "r_rk", "r_mu", "od_ln_g", "od_ln_b"):
            m[nm] = f(inp[nm][0].reshape(1, -1))
        for nm in ("r_w2", "r_a2", "od_w_out"):
            m[nm] = f(inp[nm][0])
        m["ev_w_in"] = f(inp["ev_w_in"][0])
        m["a_conv_w"] = f(inp["a_conv_w"][0])
        for nm in ("a_conv_b", "a_ln_g", "a_ln_b", "s5_d", "s5_log_dt", "s5_glu_b", "ev_ln_g", "ev_ln_b"):
            m[nm] = f(inp[nm][0].reshape(1, -1))
        m["a_pw"] = f(inp["a_pw"][0])
        for nm in ("s5_lambda_re", "s5_lambda_im", "s5_b_re", "s5_b_im", "s5_glu_w", "ev_w_out"):
            m[nm] = f(inp[nm][0])
        m["s5_c_re"] = f(inp["s5_c_re"][0].reshape(1024, 64))
        m["s5_c_im"] = f(inp["s5_c_im"][0].reshape(1024, 64))
        maps.append(m)
    return maps


def kernel(**inp):
    cfg = {}
    nc, C = build(cfg)
    consts = make_consts()
    cores = list(range(NCORES))
    maps = make_in_maps(inp, cores, consts)
    res = run_bass_kernel_spmd(nc, maps, core_ids=cores)
    R = res.results
    B = 4
    cat = lambda k, shp: np.concatenate([R[c][k].reshape((16,) + shp) for c in range(NCORES)], 0)[None]
    stk = lambda k, shp: np.stack([R[c][k].reshape(shp) for c in range(B)], 0)[None]
    y_p = np.stack([R[c]["y_p"] for c in range(B)], 0)
    y_s = np.concatenate([R[c]["y_s"].reshape(16, 8, D) for c in range(NCORES)], 0)
    return (y_p, y_s,
            stk("p_conv", (30, 1024)), stk("p_sre", (64, 64)), stk("p_sim", (64, 64)), stk("p_mc", (4, 256, 256)),
            stk("p_mn", (4, 256)), stk("p_mm", (4,)), stk("p_rs", (16, 64, 64)), stk("p_rsh", (3200,)),
            cat("s_conv_o", (30, 1024)), cat("s_sre_o", (64, 64)), cat("s_sim_o", (64, 64)), cat("s_mc_o", (4, 256, 256)),
            cat("s_mn_o", (4, 256)), cat("s_mm_o", (4,)), cat("s_rs_o", (16, 64, 64)), cat("s_rsh_o", (3200,)))
```

```python
import contextlib
import numpy as np
import concourse.bass as bass
import concourse.mybir as mybir
from concourse.bass_utils import run_bass_kernel_spmd

F32 = mybir.dt.float32
BF16 = mybir.dt.bfloat16
AF = mybir.ActivationFunctionType
ALU = mybir.AluOpType
AX = mybir.AxisListType

D = 2048
TT = 128
WG = 512
KQ = 4
NWS = 4
NCORES = 8
ALPHA = 4 ** 0.25
LN_EPS = 1e-5


class Reg:
    __slots__ = ("w", "r")

    def __init__(self):
        self.w = None
        self.r = []


class Buf:
    def __init__(self, ctx, name, t):
        self.ctx, self.name, self.t = ctx, name, t
        self.regs = {"_all": Reg()}
        self.dma_sem = None
        self.dma_cnt = 0

    def __call__(self, key, ap):
        return View(self, key, ap)

    def all(self):
        return View(self, None, self.t[:])

    def _sel(self, key):
        if key is None:
            return list(self.regs.values())
        if key not in self.regs:
            self.regs[key] = Reg()
        return [self.regs[key], self.regs["_all"]]

    def rdeps(self, key):
        return [r.w for r in self._sel(key) if r.w is not None]

    def wdeps(self, key):
        out = []
        for r in self._sel(key):
            if r.w is not None:
                out.append(r.w)
            out.extend(r.r)
        return out

    def note_read(self, key, tok):
        if key is None:
            for r in self.regs.values():
                r.r.append(tok)
        else:
            self._sel(key)[0].r.append(tok)

    def note_write(self, key, tok):
        if key is None:
            self.regs = {"_all": Reg()}
            self.regs["_all"].w = tok
        else:
            r = self._sel(key)[0]
            r.w = tok
            r.r = []


class View:
    __slots__ = ("buf", "key", "ap")

    def __init__(self, buf, key, ap):
        self.buf, self.key, self.ap = buf, key, ap


class Ctx:
    ENG = ("pe", "act", "dve", "pool", "sp")
    EPOCH = 30000

    def __init__(self, nc, es):
        self.nc, self.es = nc, es
        self.prog = {e: [] for e in self.ENG}
        self.cnt = {e: 0 for e in self.ENG}
        self.sem = {e: es.enter_context(nc.semaphore("sem_" + e)) for e in self.ENG}
        self.known = {e: {} for e in self.ENG}
        self.final = []
        self.total = {}
        self.nsem = 5
        self.nbytes = 0

    def sb(self, name, shape, dtype=F32):
        t = self.es.enter_context(self.nc.sbuf_tensor("sb_" + name, list(shape), dtype))
        n = 4
        for s in shape[1:]:
            n *= s
        self.nbytes += n
        return Buf(self, name, t)

    def ps(self, name, shape, dtype=F32):
        t = self.es.enter_context(self.nc.psum_tensor("ps_" + name, list(shape), dtype))
        return Buf(self, name, t)

    def need(self, e, tok):
        sem, val = tok
        k = id(sem)
        if self.known[e].get(k, 0) >= val:
            return
        self.known[e][k] = val
        self.prog[e].append(("wait", sem, val))

    def op(self, e, fn, reads=(), writes=()):
        for v in reads:
            if isinstance(v, View):
                for tok in v.buf.rdeps(v.key):
                    self.need(e, tok)
        for v in writes:
            if isinstance(v, View):
                for tok in v.buf.wdeps(v.key):
                    self.need(e, tok)
        if self.cnt[e] >= self.EPOCH:
            self.total[e] = self.total.get(e, 0) + self.cnt[e]
            self.sem[e] = self.es.enter_context(self.nc.semaphore("sem_%s_%d" % (e, self.total[e])))
            self.cnt[e] = 0
            self.nsem += 1
        self.cnt[e] += 1
        tok = (self.sem[e], self.cnt[e])
        self.prog[e].append(("op", fn, self.sem[e], 1))
        for v in reads:
            if isinstance(v, View):
                v.buf.note_read(v.key, tok)
        for v in writes:
            if isinstance(v, View):
                v.buf.note_write(v.key, tok)
        return tok

    def dma(self, out, in_, q="sp", slow=False):
        sbv = out if isinstance(out, View) else in_
        b = sbv.buf
        if b.dma_sem is None:
            b.dma_sem = self.es.enter_context(self.nc.semaphore("dq_" + b.name))
            self.nsem += 1
        if isinstance(in_, View):
            for tok in in_.buf.rdeps(in_.key):
                self.need(q, tok)
        if isinstance(out, View):
            for tok in out.buf.wdeps(out.key):
                self.need(q, tok)
        b.dma_cnt += 16
        tok = (b.dma_sem, b.dma_cnt)
        oap = out.ap if isinstance(out, View) else out
        iap = in_.ap if isinstance(in_, View) else in_
        if slow:
            fn = lambda eng, oap=oap, iap=iap: eng.dma_start(out=oap, in_=iap, allow_slow_non_contiguous=True)
        else:
            fn = lambda eng, oap=oap, iap=iap: eng.dma_start(out=oap, in_=iap)
        self.prog[q].append(("op", fn, b.dma_sem, 16))
        if isinstance(in_, View):
            in_.buf.note_read(in_.key, tok)
        if isinstance(out, View):
            out.buf.note_write(out.key, tok)
        else:
            self.final.append(tok)
        return tok

    def tt(self, e, out, in0, in1, op):
        return self.op(e, lambda g: g.tensor_tensor(out=out.ap, in0=in0.ap, in1=in1.ap, op=op),
                       reads=[in0, in1], writes=[out])

    def ts(self, e, out, in0, s1, s2, op0, op1=None):
        rd = [in0] + [s for s in (s1, s2) if isinstance(s, View)]
        a1 = s1.ap if isinstance(s1, View) else s1
        a2 = s2.ap if isinstance(s2, View) else s2
        if op1 is None:
            return self.op(e, lambda g: g.tensor_scalar(out=out.ap, in0=in0.ap, scalar1=a1, scalar2=None, op0=op0),
                           reads=rd, writes=[out])
        return self.op(e, lambda g: g.tensor_scalar(out=out.ap, in0=in0.ap, scalar1=a1, scalar2=a2, op0=op0, op1=op1),
                       reads=rd, writes=[out])

    def stt(self, e, out, in0, s, in1, op0, op1):
        rd = [in0, in1] + ([s] if isinstance(s, View) else [])
        a = s.ap if isinstance(s, View) else s
        return self.op(e, lambda g: g.scalar_tensor_tensor(out=out.ap, in0=in0.ap, scalar=a, in1=in1.ap, op0=op0, op1=op1),
                       reads=rd, writes=[out])

    def act(self, out, in_, func, bias=None, scale=None, e="act"):
        rd = [in_] + [s for s in (bias, scale) if isinstance(s, View)]
        kw = {}
        if bias is not None:
            kw["bias"] = bias.ap if isinstance(bias, View) else bias
        if scale is not None:
            kw["scale"] = scale.ap if isinstance(scale, View) else scale
        return self.op(e, lambda g: g.activation(out=out.ap, in_=in_.ap, func=func, **kw), reads=rd, writes=[out])

    def copy(self, e, out, in_):
        if e == "act":
            return self.act(out, in_, AF.Copy)
        return self.op(e, lambda g: g.tensor_copy(out=out.ap, in_=in_.ap), reads=[in_], writes=[out])

    def memset(self, e, out, val):
        return self.op(e, lambda g: g.memset(out.ap, val), writes=[out])

    def mm(self, out, lhsT, rhs, start=True, stop=True):
        return self.op("pe", lambda g: g.matmul(out.ap, lhsT=lhsT.ap, rhs=rhs.ap, start=start, stop=stop),
                       reads=[lhsT, rhs], writes=[out])

    def tr(self, out, in_, ident):
        return self.op("pe", lambda g: g.transpose(out.ap, in_.ap, ident.ap), reads=[in_, ident], writes=[out])

    def emit(self):
        nc = self.nc
        for tok in self.final:
            self.need("sp", tok)
        for e in self.ENG:
            if e != "sp" and self.cnt[e] > 0:
                self.need("sp", (self.sem[e], self.cnt[e]))
        engs = {"pe": "tensor", "act": "scalar", "dve": "vector", "pool": "gpsimd", "sp": "sync"}
        with nc.Block() as block:
            for e, attr in engs.items():
                items = self.prog[e]

                def body(eng, items=items):
                    for it in items:
                        if it[0] == "wait":
                            eng.wait_ge(it[1], it[2])
                        else:
                            it[1](eng).then_inc(it[2], it[3])

                getattr(block, attr)(body)


def make_consts():
    c = {}
    c["ident"] = np.eye(128, dtype=np.float32)
    c["ones"] = np.ones((128, 128), dtype=np.float32)
    r = np.arange(128)
    mrow = (r % 32) // 16
    mcol = np.arange(128) // 64
    bm = (mrow[:, None] == mcol[None, :]).astype(np.float32)
    ev_r = ((r // 32) % 2 == 0).astype(np.float32)
    c["bmask"] = bm * ev_r[:, None]
    c["bmask_o"] = bm * (1 - ev_r)[:, None]
    gl = np.arange(128) // 16
    cm = ((gl[None, :] % 2) == (r[:, None] // 64)).astype(np.float32)
    c["cmask"] = cm * ev_r[None, :]
    c["cmask_o"] = cm * (1 - ev_r)[None, :]
    BIG = 1e30
    i128 = np.arange(128)
    allow_p = (i128[:, None] <= i128[None, :])
    c["maskbig_p"] = np.where(allow_p, 0.0, BIG).astype(np.float32)
    c["segtri_p"] = allow_p.astype(np.float32)
    i64 = np.arange(64)
    allow_s = (i64[:, None] <= i64[None, :]) & ((i64[:, None] // 8) == (i64[None, :] // 8))
    c["maskbig_s"] = np.where(allow_s, 0.0, BIG).astype(np.float32)
    c["segtri_s"] = allow_s.astype(np.float32)
    r01p = np.ones((128, 128), np.float32); r01p[:, 0] = 0
    r01s = np.ones((128, 64), np.float32); r01s[:, ::8] = 0
    c["r01_p"], c["r01_s"] = r01p, r01s
    c["rneg_p"] = ((1 - r01p) * -BIG).astype(np.float32)
    c["rneg_s"] = ((1 - r01s) * -BIG).astype(np.float32)
    segsel = ((i64[None, :] // 8) == np.arange(8)[:, None]).astype(np.float32)
    c["segsel"] = segsel
    c["segc"] = np.ascontiguousarray(segsel.T)
    c["segrow"] = np.ascontiguousarray(np.broadcast_to(segsel.reshape(1, 512), (128, 512))).astype(np.float32)
    c["blk"] = ((i128[:, None] // 64) == (i128[None, :] // 64)).astype(np.float32)
    st_p = (i128[:, None] < i128[None, :])
    st_s = (i64[:, None] < i64[None, :]) & ((i64[:, None] // 8) == (i64[None, :] // 8))
    c["sstri_p"] = st_p.astype(np.float32)
    c["sstriT_p"] = np.ascontiguousarray(st_p.T).astype(np.float32)
    c["sstri_s"] = st_s.astype(np.float32)
    c["sstriT_s"] = np.ascontiguousarray(st_s.T).astype(np.float32)
    return c


CONST_SHAPES = {"ident": [128, 128], "ones": [128, 128], "bmask": [128, 128], "cmask": [128, 128],
                "bmask_o": [128, 128], "cmask_o": [128, 128],
                "maskbig_p": [128, 128], "segtri_p": [128, 128], "maskbig_s": [64, 64], "segtri_s": [64, 64],
                "r01_p": [128, 128], "r01_s": [128, 64], "rneg_p": [128, 128], "rneg_s": [128, 64],
                "segsel": [8, 64], "segc": [64, 8], "segrow": [128, 512], "blk": [128, 128],
                "sstri_p": [128, 128], "sstriT_p": [128, 128], "sstri_s": [64, 64], "sstriT_s": [64, 64]}


def tile_plan(cfg):
    tiles = []
    npt = cfg.get("n_prompt_tiles", 17)
    pos = 0
    for i in range(17):
        T = 16 if i == 16 else TT
        if i < npt:
            tiles.append(dict(kind="p", T=T, pos=pos, nseq=1, L=T, first=(i == 0), last=(i == npt - 1), s0=0))
        pos += T
    if cfg.get("sample", True):
        for h in range(2):
            tiles.append(dict(kind="s", T=64, pos=0, nseq=8, L=8, first=True, last=True, s0=8 * h))
    return tiles


def build(cfg):
    nc = bass.Bass("TRN2", target_bir_lowering=False)
    es = contextlib.ExitStack()
    C = Ctx(nc, es)
    dbg = cfg.get("debug", False)
    nlayers = cfg.get("layers", 2)

    def din(name, shape):
        return nc.dram_tensor(name, list(shape), F32, kind="ExternalInput").ap()

    def dout(name, shape):
        return nc.dram_tensor(name, list(shape), F32, kind="ExternalOutput").ap()

    I = {}
    I["xp"] = din("xp", [2048, D])
    I["meta"] = din("meta", [16, D])
    I["xs"] = din("xs", [128, D])
    I["s_conv"] = din("s_conv", [16 * 30, 1024])
    I["s_sre"] = din("s_sre", [16, 64, 64])
    I["s_sim"] = din("s_sim", [16, 64, 64])
    for nm, shp in CONST_SHAPES.items():
        I[nm] = din(nm, shp)
    I["s_mc"] = din("s_mc", [16, 4, 256, 256])
    I["s_mn"] = din("s_mn", [16, 4, 256])
    I["s_mm"] = din("s_mm", [16, 4])
    I["s_rs"] = din("s_rs", [16, 16, 64, 64])
    I["s_rsh"] = din("s_rsh", [16, 3200])
    I["od_w_in"] = din("od_w_in", [D, 9352])
    I["m_ig_b"] = din("m_ig_b", [1, 4])
    I["m_fg_b"] = din("m_fg_b", [1, 4])
    for nm in ("m_hn_g", "r_w0", "r_a0", "r_kk", "r_ka", "r_ln_g", "r_ln_b", "r_rk"):
        I[nm] = din(nm, [1, 1024])
    I["r_mu"] = din("r_mu", [1, 3200])
    I["r_w2"] = din("r_w2", [64, 1024])
    I["r_a2"] = din("r_a2", [64, 1024])
    I["od_w_out"] = din("od_w_out", [2048, D])
    I["od_ln_g"] = din("od_ln_g", [1, D])
    I["od_ln_b"] = din("od_ln_b", [1, D])
    I["ev_w_in"] = din("ev_w_in", [D, 5120])
    I["a_conv_w"] = din("a_conv_w", [31, 1024])
    for nm in ("a_conv_b", "a_ln_g", "a_ln_b", "s5_d"):
        I[nm] = din(nm, [1, 1024])
    I["a_pw"] = din("a_pw", [1024, 1024])
    I["s5_lambda_re"] = din("s5_lambda_re", [64, 64])
    I["s5_lambda_im"] = din("s5_lambda_im", [64, 64])
    I["s5_log_dt"] = din("s5_log_dt", [1, 64])
    I["s5_b_re"] = din("s5_b_re", [64, 64, 16])
    I["s5_b_im"] = din("s5_b_im", [64, 64, 16])
    I["s5_c_re"] = din("s5_c_re", [1024, 64])
    I["s5_c_im"] = din("s5_c_im", [1024, 64])
    I["s5_glu_w"] = din("s5_glu_w", [1024, 2048])
    I["s5_glu_b"] = din("s5_glu_b", [1, 2048])
    I["ev_w_out"] = din("ev_w_out", [2048, D])
    I["ev_ln_g"] = din("ev_ln_g", [1, D])
    I["ev_ln_b"] = din("ev_ln_b", [1, D])

    O = {}
    O["y_p"] = dout("y_p", [2048, D])
    O["y_s"] = dout("y_s", [128, D])
    O["p_conv"] = dout("p_conv", [30, 1024])
    O["p_sre"] = dout("p_sre", [64, 64])
    O["p_sim"] = dout("p_sim", [64, 64])
    O["s_conv_o"] = dout("s_conv_o", [16 * 30, 1024])
    O["s_sre_o"] = dout("s_sre_o", [16, 64, 64])
    O["s_sim_o"] = dout("s_sim_o", [16, 64, 64])
    O["p_mc"] = dout("p_mc", [4, 256, 256])
    O["p_mn"] = dout("p_mn", [4, 256])
    O["p_mm"] = dout("p_mm", [1, 4])
    O["p_rs"] = dout("p_rs", [16, 64, 64])
    O["p_rsh"] = dout("p_rsh", [1, 3200])
    O["s_mc_o"] = dout("s_mc_o", [16, 4, 256, 256])
    O["s_mn_o"] = dout("s_mn_o", [16, 4, 256])
    O["s_mm_o"] = dout("s_mm_o", [16, 4])
    O["s_rs_o"] = dout("s_rs_o", [16, 16, 64, 64])
    O["s_rsh_o"] = dout("s_rsh_o", [16, 3200])

    ident = C.sb("ident", [128, 128])
    ones = C.sb("ones", [128, 128])
    XT = C.sb("XT", [128, 16, TT])
    XS = C.sb("XS", [128, D])
    PJ = C.sb("PJ", [128, 33, TT])
    MIX = C.sb("MIX", [128, 16 * TT], BF16)
    XTB = C.sb("XTB", [128, 16, TT], BF16)
    GELB = C.sb("GELB", [128, 8, TT], BF16)
    WS = [C.sb("WS%d" % i, [128, KQ, WG], BF16) for i in range(NWS)]
    HB = C.sb("HB", [128, 8, 304])
    HC = C.sb("HC", [128, 8, 30])
    ACC = C.sb("ACC", [128, 8, TT])
    SQ2 = C.sb("SQ2", [128, 2, TT])
    ST = C.sb("ST", [128, 3, TT])
    VEC0 = C.sb("VEC0", [128, 8, 40])
    VECG = C.sb("VECG", [128, 16, 4])
    TMP = C.sb("TMP", [128, 128])
    XR = C.sb("XR", [128, 32, 129])
    XI = C.sb("XI", [128, 32, 129])
    S5C = C.sb("S5C", [128, 2, 32])
    S5A = C.sb("S5A", [128, 8, 32])
    S5T = C.sb("S5T", [128, 10, 32])
    SCS = C.sb("SCS", [128, 2, 32, 8])
    BRE = [C.sb("BRE%d" % i, [128, 8, 128]) for i in range(2)]
    BIM = [C.sb("BIM%d" % i, [128, 8, 128]) for i in range(2)]
    CRE = [C.sb("CRE%d" % i, [128, 8, 128]) for i in range(2)]
    CIM = [C.sb("CIM%d" % i, [128, 8, 128]) for i in range(2)]
    mask_bo = C.sb("mask_bo", [128, 128])
    mask_co = C.sb("mask_co", [128, 128])
    S5ST = C.sb("S5ST", [128, 128])
    mask_b = C.sb("mask_b", [128, 128])
    mask_c = C.sb("mask_c", [128, 128])

    PSA = [C.ps("PSA%d" % i, [128, 512]) for i in range(4)]
    PSB = [C.ps("PSB%d" % i, [128, 512]) for i in range(4)]

    def mixv(ct, T):
        return MIX(("t", ct), MIX.t[:, ct * TT:ct * TT + T])

    C.dma(ident.all(), I["ident"])
    C.dma(ones.all(), I["ones"])
    C.dma(mask_b.all(), I["bmask"])
    C.dma(mask_c.all(), I["cmask"])
    C.dma(mask_bo.all(), I["bmask_o"])
    C.dma(mask_co.all(), I["cmask_o"])

    rr = {"psa": 0, "psb": 0, "ws": 0, "ev": 0, "sq": 0, "tm": 0}

    def next_psa():
        rr["psa"] = (rr["psa"] + 1) % 4
        return PSA[rr["psa"]]

    def next_psb():
        rr["psb"] = (rr["psb"] + 1) % 4
        return PSB[rr["psb"]]

    def ev_eng():
        rr["ev"] += 1
        return "dve" if rr["ev"] % 2 else "act"

    def load_rows_T(dst, rows, ncols, col0=0, ct0=0):
        r0 = 0
        for ap in rows:
            nr = ap.shape[0]
            C.dma(XS(None, XS.t[r0:r0 + nr, 0:ncols]), ap)
            r0 += nr
        nr = r0
        for ct in range(ncols // 128):
            ps = next_psb()
            C.tr(ps(None, ps.t[:, 0:nr]), XS(None, XS.t[0:nr, ct * 128:(ct + 1) * 128]), ident(None, ident.t[0:nr, 0:nr]))
            C.copy("dve", dst(None, dst.t[:, ct0 + ct, col0:col0 + nr]), ps(None, ps.t[:, 0:nr]))

    load_rows_T(VEC0, [I["a_conv_w"], I["a_conv_b"], I["a_ln_g"], I["a_ln_b"], I["s5_d"]], 1024)
    load_rows_T(VECG, [I["s5_glu_b"], I["ev_ln_g"], I["ev_ln_b"]], 2048)

    def gp_ap(ap2d):
        return ap2d.rearrange("(q m) p -> (m p) q", m=2)

    LR, LI, DT, AR, AI, NAI = range(6)
    A = lambda k: S5A(None, S5A.t[:, k, :])
    Tm = lambda k: S5T(None, S5T.t[:, k, :])
    for m in range(2):
        C.dma(S5A(None, S5A.t[64 * m:64 * m + 64, LR, :]), I["s5_lambda_re"].rearrange("(q m) p -> m p q", m=2)[m], slow=True)
        C.dma(S5A(None, S5A.t[64 * m:64 * m + 64, LI, :]), I["s5_lambda_im"].rearrange("(q m) p -> m p q", m=2)[m], slow=True)
    ldt = I["s5_log_dt"]
    for m in range(2):
        src = bass.AP(ldt.tensor, ldt.offset + m, [[0, 64], [2, 32]])
        C.dma(S5A(None, S5A.t[64 * m:64 * m + 64, DT, :]), src, slow=True)
    C.act(A(DT), A(DT), AF.Exp)
    C.tt("dve", Tm(0), A(LR), A(DT), ALU.mult)
    C.act(Tm(1), Tm(0), AF.Exp)
    C.tt("dve", Tm(2), A(LI), A(DT), ALU.mult)
    PI = float(np.pi)
    C.ts("dve", Tm(3), Tm(2), 1.0 / 32, None, ALU.mult)
    C.act(Tm(4), Tm(3), AF.Sin)
    C.ts("dve", Tm(3), Tm(3), PI / 2, None, ALU.add)
    C.act(Tm(5), Tm(3), AF.Sin)
    for _ in range(5):
        C.tt("dve", Tm(3), Tm(4), Tm(5), ALU.mult)
        C.tt("dve", Tm(8), Tm(5), Tm(5), ALU.mult)
        C.tt("dve", Tm(9), Tm(4), Tm(4), ALU.mult)
        C.ts("dve", Tm(4), Tm(3), 2.0, None, ALU.mult)
        C.tt("dve", Tm(5), Tm(8), Tm(9), ALU.subtract)
    C.tt("dve", A(AR), Tm(1), Tm(5), ALU.mult)
    C.tt("dve", A(AI), Tm(1), Tm(4), ALU.mult)
    C.ts("dve", A(NAI), A(AI), -1.0, None, ALU.mult)
    MAG = 6
    C.copy("dve", A(MAG), Tm(1))
    s5tab = nc.dram_tensor("s5tab", [2, 128, 1024], F32).ap()
    tC = lambda a, b: XR(None, XR.t[:, :, a:b])
    tS = lambda a, b: XI(None, XI.t[:, :, a:b])
    C.copy("dve", tC(0, 1), S5T(None, S5T.t[:, 5, :].unsqueeze(2)))
    C.copy("dve", tS(0, 1), S5T(None, S5T.t[:, 4, :].unsqueeze(2)))
    n_ = 1
    while n_ < 32:
        cn = XR(None, XR.t[:, :, n_ - 1:n_].broadcast_to([128, 32, n_]))
        sn = XI(None, XI.t[:, :, n_ - 1:n_].broadcast_to([128, 32, n_]))
        u1 = XR(None, XR.t[:, :, 64:64 + n_])
        u2 = XI(None, XI.t[:, :, 64:64 + n_])
        C.tt("dve", u1, tS(0, n_), sn, ALU.mult)
        C.tt("dve", tC(n_, 2 * n_), tC(0, n_), cn, ALU.mult)
        C.tt("dve", tC(n_, 2 * n_), tC(n_, 2 * n_), u1, ALU.subtract)
        C.tt("dve", u2, tC(0, n_), sn, ALU.mult)
        C.tt("dve", tS(n_, 2 * n_), tS(0, n_), cn, ALU.mult)
        C.tt("dve", tS(n_, 2 * n_), tS(n_, 2 * n_), u2, ALU.add)
        n_ *= 2
    tab_tok = [C.dma(s5tab[0].rearrange("p (q t) -> p q t", t=32), tC(0, 32)),
               C.dma(s5tab[1].rearrange("p (q t) -> p q t", t=32), tS(0, 32))]
    for tk in tab_tok:
        C.need("sp", tk)
    C.tt("dve", Tm(0), A(LR), A(LR), ALU.mult)
    C.tt("dve", Tm(1), A(LI), A(LI), ALU.mult)
    C.tt("dve", Tm(0), Tm(0), Tm(1), ALU.add)
    C.op("dve", lambda g: g.reciprocal(out=S5T.t[:, 0, :], in_=S5T.t[:, 0, :]), reads=[Tm(0)], writes=[Tm(0)])
    C.ts("dve", Tm(1), A(AR), -1.0, None, ALU.add)
    C.tt("dve", Tm(2), Tm(1), A(LR), ALU.mult)
    C.tt("dve", Tm(3), A(AI), A(LI), ALU.mult)
    C.tt("dve", Tm(2), Tm(2), Tm(3), ALU.add)
    C.tt("dve", Tm(6), Tm(2), Tm(0), ALU.mult)
    C.tt("dve", Tm(2), A(AI), A(LR), ALU.mult)
    C.tt("dve", Tm(3), Tm(1), A(LI), ALU.mult)
    C.tt("dve", Tm(2), Tm(2), Tm(3), ALU.subtract)
    C.tt("dve", Tm(7), Tm(2), Tm(0), ALU.mult)
    braw_t = PJ.t[:, 0:8, :].rearrange("p a b -> p (a b)").rearrange("p (k q h) -> p k q h", k=2, q=32)
    bb_t = PJ.t[:, 8:24, :].rearrange("p a b -> p (a b)").rearrange("p (k q h) -> p k q h", k=2, q=32)
    sc_t = PJ.t[:, 24:28, :].rearrange("p a b -> p (a b)").rearrange("p (q h) -> p q h", q=32)
    for k, nm in enumerate(("s5_b_re", "s5_b_im")):
        for m in range(2):
            C.dma(PJ(None, braw_t[64 * m:64 * m + 64, k, :, :]), I[nm].rearrange("(q m) p h -> m p q h", m=2)[m])
    qr_b = S5T(None, S5T.t[:, 6, :].unsqueeze(2).broadcast_to([128, 32, 16]))
    qi_b = S5T(None, S5T.t[:, 7, :].unsqueeze(2).broadcast_to([128, 32, 16]))
    br = PJ(None, braw_t[:, 0, :, :])
    bi = PJ(None, braw_t[:, 1, :, :])
    sc = PJ(None, sc_t)
    for dup in range(2):
        o_r = PJ(None, bb_t[:, 0, :, dup * 16:(dup + 1) * 16])
        o_i = PJ(None, bb_t[:, 1, :, dup * 16:(dup + 1) * 16])
        C.tt("dve", o_r, br, qr_b, ALU.mult)
        C.tt("dve", sc, bi, qi_b, ALU.mult)
        C.tt("dve", o_r, o_r, sc, ALU.subtract)
        C.tt("dve", o_i, bi, qr_b, ALU.mult)
        C.tt("dve", sc, br, qi_b, ALU.mult)
        C.tt("dve", o_i, o_i, sc, ALU.add)
    for k, dst in enumerate((BRE, BIM)):
        for gt in range(8):
            ps = next_psb()
            C.copy("dve", TMP.all(), PJ(None, bb_t[:, k, 4 * gt:4 * gt + 4, :]))
            C.tr(ps(None, ps.t[:, 0:128]), TMP.all(), ident.all())
            C.tt("dve", dst[0](None, dst[0].t[:, gt, :]), ps(None, ps.t[:, 0:128]), mask_b.all(), ALU.mult)
            C.tt("dve", dst[1](None, dst[1].t[:, gt, :]), ps(None, ps.t[:, 0:128]), mask_bo.all(), ALU.mult)
    for k, (nm, dst) in enumerate((("s5_c_re", CRE), ("s5_c_im", CIM))):
        for gt in range(8):
            for dup in range(2):
                C.dma(XS(None, XS.t[:, dup * 64:(dup + 1) * 64]), I[nm][gt * 128:(gt + 1) * 128, :])
            ps = next_psb()
            C.tr(ps(None, ps.t[:, 0:128]), XS(None, XS.t[:, 0:128]), ident.all())
            for par, mk in enumerate((mask_c, mask_co)):
                C.stt("dve", dst[par](None, dst[par].t[:, gt, :]), ps(None, ps.t[:, 0:128]), 1.0 if k == 0 else -1.0,
                      mk.all(), ALU.mult, ALU.mult)

    def stream_mm(W, nkt, coltiles, rhs_fn, T, evac):
        groups = []
        cur = []
        for ct in coltiles:
            if cur and (ct[0] + ct[1] - cur[0][0] > WG):
                groups.append(cur)
                cur = []
            cur.append(ct)
        if cur:
            groups.append(cur)
        Wv = W.rearrange("(kt p) c -> p kt c", p=128)
        j = 0
        for grp in groups:
            g0 = grp[0][0]
            gw = grp[-1][0] + grp[-1][1] - g0
            ps = next_psa()
            pacc = ps(None, ps.t[0:T, 0:gw])
            for kq in range(0, nkt, KQ):
                ws = WS[rr["ws"] % NWS]
                rr["ws"] += 1
                nk = min(KQ, nkt - kq)
                C.dma(ws(None, ws.t[:, 0:nk, 0:gw]), Wv[:, kq:kq + nk, g0:g0 + gw], q="pool")
                for k in range(nk):
                    kt = kq + k
                    C.mm(pacc, rhs_fn(kt), ws(None, ws.t[:, k, 0:gw]), start=(kt == 0), stop=(kt == nkt - 1))
            i_ = rr["tm"] % 2
            rr["tm"] += 1
            C.copy(ev_eng(), XS(("tm", i_), XS.t[0:T, i_ * 512:i_ * 512 + gw]), pacc)
            for (c0, w) in grp:
                pt = next_psb()
                C.tr(pt(None, pt.t[0:w, 0:T]), XS(("tm", i_), XS.t[0:T, i_ * 512 + c0 - g0:i_ * 512 + c0 - g0 + w]),
                     ident(None, ident.t[0:T, 0:T]))
                evac(j, pt(None, pt.t[0:w, 0:T]))
                j += 1

    def layer_norm_cols(src, ntile, T, gcol, bcol, vec, func=AF.Identity, eps=LN_EPS, dst_fn=None, also_fn=None):
        nch = ntile * 128
        ps = next_psb()
        ps2 = next_psb()
        for ct in range(ntile):
            sv = src(("t", ct), src.t[:, ct, 0:T])
            sq = SQ2(("s", rr["sq"] % 2), SQ2.t[:, rr["sq"] % 2, 0:T])
            rr["sq"] += 1
            C.act(sq, sv, AF.Square)
            C.mm(ps(None, ps.t[:, 0:T]), ones.all(), sv, start=(ct == 0), stop=(ct == ntile - 1))
            C.mm(ps2(None, ps2.t[:, 0:T]), ones.all(), sq, start=(ct == 0), stop=(ct == ntile - 1))
        st = lambda k: ST(None, ST.t[:, k, 0:T])
        C.ts("dve", st(0), ps(None, ps.t[:, 0:T]), 1.0 / nch, None, ALU.mult)
        C.ts("dve", st(1), ps2(None, ps2.t[:, 0:T]), 1.0 / nch, None, ALU.mult)
        C.tt("dve", st(2), st(0), st(0), ALU.mult)
        C.tt("dve", st(1), st(1), st(2), ALU.subtract)
        C.ts("dve", st(1), st(1), eps, None, ALU.add)
        C.act(st(1), st(1), AF.Sqrt)
        C.op("dve", lambda g, T=T: g.reciprocal(out=ST.t[:, 1, 0:T], in_=ST.t[:, 1, 0:T]), reads=[st(1)], writes=[st(1)])
        C.tt("dve", st(2), st(0), st(1), ALU.mult)
        C.ts("dve", st(2), st(2), -1.0, None, ALU.mult)
        for ct in range(ntile):
            e = "dve" if ct % 2 == 0 else "pool"
            sv = src(("t", ct), src.t[:, ct, 0:T])
            C.tt(e, sv, sv, st(1), ALU.mult)
            C.tt(e, sv, sv, st(2), ALU.add)
            dv = dst_fn(ct) if dst_fn is not None else sv
            C.act(dv, sv, func, bias=vec(None, vec.t[:, ct, bcol:bcol + 1]), scale=vec(None, vec.t[:, ct, gcol:gcol + 1]))
            if also_fn is not None:
                C.copy("pool" if ct % 2 else "act", also_fn(ct), dv)

    def load_x(tile):
        n = tile["T"]
        if tile["kind"] == "s":
            C.dma(XS(None, XS.t[0:n, :]), I["xs"][tile["s0"] * 8:tile["s0"] * 8 + n, :])
        else:
            p0 = tile["pos"]
            r = 0
            if p0 < 16:
                C.dma(XS(None, XS.t[0:16, :]), I["meta"][:, :])
                r = 16
            x0 = p0 + r - 16
            C.dma(XS(None, XS.t[r:n, :]), I["xp"][x0:x0 + n - r, :])
        for dt_ in range(16):
            ps = next_psb()
            C.tr(ps(None, ps.t[:, 0:n]), XS(None, XS.t[0:n, dt_ * 128:(dt_ + 1) * 128]), ident(None, ident.t[0:n, 0:n]))
            C.copy("act" if dt_ % 2 else "dve", XT(("t", dt_), XT.t[:, dt_, 0:n]), ps(None, ps.t[:, 0:n]))
            C.copy("pool", XTB(("t", dt_), XTB.t[:, dt_, 0:n]), XT(("t", dt_), XT.t[:, dt_, 0:n]))

    def store_y(tile):
        n = tile["T"]
        for dt_ in range(16):
            ps = next_psb()
            C.tr(ps(None, ps.t[0:n, 0:128]), XT(("t", dt_), XT.t[:, dt_, 0:n]), ident.all())
            C.copy("act" if dt_ % 2 else "dve", XS(None, XS.t[0:n, dt_ * 128:(dt_ + 1) * 128]), ps(None, ps.t[0:n, 0:128]))
        if tile["kind"] == "s":
            C.dma(O["y_s"][tile["s0"] * 8:tile["s0"] * 8 + n, :], XS(None, XS.t[0:n, :]))
        else:
            p0 = tile["pos"]
            r = 16 if p0 < 16 else 0
            x0 = p0 + r - 16
            C.dma(O["y_p"][x0:x0 + n - r, :], XS(None, XS.t[r:n, :]))

    def out_proj_ln(W, tile, vec, gcol, bcol):
        T = tile["T"]

        def evac(j, pv):
            xv_ = XT(("t", j), XT.t[:, j, 0:T])
            C.stt("dve", xv_, xv_, ALPHA, pv, ALU.mult, ALU.add)

        stream_mm(W, 16, [(i * 128, 128) for i in range(16)], lambda kt: mixv(kt, T), T, evac)
        layer_norm_cols(XT, 16, T, gcol, bcol, vec, also_fn=lambda ct: XTB(("t", ct), XTB.t[:, ct, 0:T]))

    def layer0(tile):
        T, nseq, L = tile["T"], tile["nseq"], tile["L"]
        is_s = tile["kind"] == "s"
        s0 = tile["s0"]
        xrhs = lambda kt: XTB(("t", kt), XTB.t[:, kt, 0:T])
        pj = lambda j: PJ(("t", j), PJ.t[:, j, 0:T])

        fence(HB, HB.t[0:1, 0, 0:1])
        def evacA(j, pv):
            if j < 8:
                C.copy(ev_eng(), pj(j), pv)
            elif j < 16:
                C.act(pj(j), pv, AF.Sigmoid)
            else:
                C.act(pj(j), pv, AF.Silu)

        stream_mm(I["ev_w_in"], 16, [(i * 128, 128) for i in range(24)], xrhs, T, evacA)

        W_ = 30 + L

        def hb(ct, a, b):
            v = HB.t[:, ct, 0:nseq * W_].rearrange("p (n w) -> p n w", w=W_)[:, :, a:b]
            return HB(("t", ct), v)

        def tokv(buf, ct):
            return buf(("t", ct), buf.t[:, ct, 0:T].rearrange("p (n l) -> p n l", l=L))

        if is_s:
            for q in range(2):
                C.dma(XS(None, XS.t[0:120, 0:1024]), I["s_conv"][s0 * 30 + q * 120:s0 * 30 + (q + 1) * 120, :])
                for ct in range(8):
                    ps = next_psb()
                    C.tr(ps(None, ps.t[:, 0:120]), XS(None, XS.t[0:120, ct * 128:(ct + 1) * 128]), ident(None, ident.t[0:120, 0:120]))
                    dstv = HB.t[:, ct, 0:nseq * W_].rearrange("p (n w) -> p n w", w=W_)[:, 4 * q:4 * q + 4, 0:30]
                    C.copy("dve", HB(("t", ct), dstv), ps(None, ps.t[:, 0:120].rearrange("p (n r) -> p n r", r=30)))
        else:
            for ct in range(8):
                if tile["first"]:
                    C.memset("pool", hb(ct, 0, 30), 0.0)
                else:
                    C.copy("pool", hb(ct, 0, 30), HC(("t", ct), HC.t[:, ct, :].unsqueeze(1)))
        for ct in range(8):
            e = "dve" if ct % 2 == 0 else "pool"
            C.tt(e, hb(ct, 30, 30 + L), tokv(PJ, ct), tokv(PJ, 8 + ct), ALU.mult)
        for ct in range(8):
            e = "dve"
            acc = tokv(ACC, ct)
            wcol = lambda j, ct=ct: VEC0(None, VEC0.t[:, ct, j:j + 1])
            C.ts(e, acc, hb(ct, 0, L), wcol(0), wcol(31), ALU.mult, ALU.add)
            for j in range(1, 31):
                C.stt(e, acc, hb(ct, j, j + L), wcol(j), acc, ALU.mult, ALU.add)
        if is_s:
            for q in range(2):
                for ct in range(8):
                    ps = next_psb()
                    srcv = HB.t[:, ct, 0:nseq * W_].rearrange("p (n w) -> p n w", w=W_)[:, 4 * q:4 * q + 4, L:L + 30]
                    C.copy("pool", TMP(None, TMP.t[:, 0:120].rearrange("p (n r) -> p n r", r=30)), HB(("t", ct), srcv))
                    C.tr(ps(None, ps.t[0:120, 0:128]), TMP(None, TMP.t[:, 0:120]), ident.all())
                    C.copy("dve", XS(None, XS.t[0:120, ct * 128:(ct + 1) * 128]), ps(None, ps.t[0:120, 0:128]))
                C.dma(O["s_conv_o"][s0 * 30 + q * 120:s0 * 30 + (q + 1) * 120, :], XS(None, XS.t[0:120, 0:1024]))
        else:
            for ct in range(8):
                C.copy("pool", TMP(None, TMP.t[:, 0:30]), HB(("t", ct), HB.t[:, ct, L:L + 30]))
                C.copy("pool", HC(("t", ct), HC.t[:, ct, :]), TMP(None, TMP.t[:, 0:30]))
            if tile["last"]:
                for ct in range(8):
                    ps = next_psb()
                    C.tr(ps(None, ps.t[0:30, 0:128]), HC(("t", ct), HC.t[:, ct, :]), ident.all())
                    C.copy("dve", XS(None, XS.t[0:30, ct * 128:(ct + 1) * 128]), ps(None, ps.t[0:30, 0:128]))
                C.dma(O["p_conv"][:, :], XS(None, XS.t[0:30, 0:1024]))
        layer_norm_cols(ACC, 8, T, 32, 33, VEC0, func=AF.Silu, dst_fn=lambda ct: mixv(8 + ct, T))

        def evac_pw(j, pv):
            C.tt("dve", mixv(j, T), pv, pj(16 + j), ALU.mult)

        stream_mm(I["a_pw"], 8, [(i * 128, 128) for i in range(8)], lambda kt: mixv(8 + kt, T), T, evac_pw)

        def evacB(j, pv):
            if j < 8:
                C.copy(ev_eng(), pj(j), pv)
            else:
                C.act(pj(j), pv, AF.Silu)

        stream_mm(I["ev_w_in"], 16, [(3072 + i * 128, 128) for i in range(16)], xrhs, T, evacB)

        Wx = 1 + L

        def xv(buf, a, b, p0=0, p1=32):
            v = buf.t[:, p0:p1, 0:nseq * Wx].rearrange("p q (n w) -> p q n w", w=Wx)[:, :, :, a:b]
            return buf(None, v)

        if is_s:
            for k, (nm, buf) in enumerate((("s_sre", XR), ("s_sim", XI))):
                for q in range(2):
                    for pr in range(16):
                        g0 = 2 * (16 * q + pr)
                        C.dma(S5ST(None, S5ST.t[pr * 8:(pr + 1) * 8, :]),
                              I[nm][s0:s0 + 8, g0:g0 + 2, :].rearrange("n m p -> n (m p)"))
                    ps = next_psb()
                    C.tr(ps(None, ps.t[:, 0:128]), S5ST.all(), ident.all())
                    C.copy("dve", xv(buf, 0, 1, 16 * q, 16 * q + 16),
                           ps(None, ps.t[:, 0:128].rearrange("p (q n o) -> p q n o", n=8, o=1)))
        else:
            for k, buf in enumerate((XR, XI)):
                if tile["first"]:
                    C.memset("pool", xv(buf, 0, 1), 0.0)
                else:
                    C.copy("pool", xv(buf, 0, 1), S5C(None, S5C.t[:, k, :].unsqueeze(2).unsqueeze(3)))
        for q4 in range(8):
            for k, (tab, buf) in enumerate(((BRE, XR), (BIM, XI))):
                ps = next_psa()
                for ip in range(4):
                    hf = ip // 2
                    tb = tab[ip % 2]
                    C.mm(ps(None, ps.t[:, ip * T:(ip + 1) * T]),
                         tb(None, tb.t[64 * hf:64 * hf + 64, q4, :]),
                         PJ(("t", q4), PJ.t[64 * hf:64 * hf + 64, q4, 0:T]))
                C.copy("act", xv(buf, 1, Wx, 4 * q4, 4 * q4 + 4),
                       ps(None, ps.t[:, 0:4 * T].rearrange("p (q n l) -> p q n l", q=4, l=L)))
        arb = S5A(None, S5A.t[:, AR, :].unsqueeze(2).broadcast_to([128, 32, nseq]))
        aib = S5A(None, S5A.t[:, AI, :].unsqueeze(2).broadcast_to([128, 32, nseq]))
        naib = S5A(None, S5A.t[:, NAI, :].unsqueeze(2).broadcast_to([128, 32, nseq]))
        if nseq == 1:
            t1v = S5T(None, S5T.t[:, 8, :].unsqueeze(2))
            t2v = S5T(None, S5T.t[:, 9, :].unsqueeze(2))
        else:
            t1v = SCS(None, SCS.t[:, 0, :, :])
            t2v = SCS(None, SCS.t[:, 1, :, :])

        def col(buf, t):
            v = buf.t[:, :, 0:nseq * Wx].rearrange("p q (n w) -> p q n w", w=Wx)[:, :, :, t]
            return buf(None, v)

        if is_s:
            e = "dve"
            for t in range(L):
                C.tt(e, t1v, col(XR, t), arb, ALU.mult)
                C.tt(e, col(XR, t + 1), col(XR, t + 1), t1v, ALU.add)
                C.tt(e, t1v, col(XI, t), naib, ALU.mult)
                C.tt(e, col(XR, t + 1), col(XR, t + 1), t1v, ALU.add)
                C.tt(e, t2v, col(XI, t), arb, ALU.mult)
                C.tt(e, col(XI, t + 1), col(XI, t + 1), t2v, ALU.add)
                C.tt(e, t2v, col(XR, t), aib, ALU.mult)
                C.tt(e, col(XI, t + 1), col(XI, t + 1), t2v, ALU.add)
        else:
            VTMf_ = VTM.t[:, :, :].rearrange("p a b -> p (a b)")
            PJf_ = PJ.t[:, 24:32, :].rearrange("p a b -> p (a b)")
            ROWf_ = ROW.t[:, :, :].rearrange("p a b -> p (a b)")
            C.dma(VTM(None, VTMf_[:, 0:1024]), s5tab[0])
            C.dma(PJ(None, PJf_), s5tab[1])
            for t0 in range(0, T, 32):
                Tc = min(32, T - t0)
                ec = VTM(None, VTMf_[:, 0:1024].rearrange("p (q t) -> p q t", t=32)[:, :, 0:Tc])
                es = PJ(None, PJf_.rearrange("p (q t) -> p q t", t=32)[:, :, 0:Tc])
                w1 = KW(None, KW.t[:, :].rearrange("p (q t) -> p q t", t=32)[:, :, 0:Tc])
                w2 = ROW(None, ROWf_.rearrange("p (q t) -> p q t", t=32)[:, :, 0:Tc])
                ur = XR(None, XR.t[:, :, 1 + t0:1 + t0 + Tc])
                ui = XI(None, XI.t[:, :, 1 + t0:1 + t0 + Tc])
                C.tt("pool", w1, ur, es, ALU.mult)
                C.tt("dve", w2, ui, es, ALU.mult)
                C.tt("dve", ur, ur, ec, ALU.mult)
                C.tt("pool", ui, ui, ec, ALU.mult)
                C.tt("dve", ur, ur, w2, ALU.add)
                C.tt("pool", ui, ui, w1, ALU.subtract)
                for pr in range(32):
                    rho = S5A(None, S5A.t[:, MAG, pr:pr + 1].broadcast_to([128, Tc]))
                    for buf in (XR, XI):
                        seg = buf(None, buf.t[:, pr, 1 + t0:1 + t0 + Tc])
                        scan(seg, rho, seg, buf(None, buf.t[:, pr, t0:t0 + 1]), ALU.mult, ALU.add)
                C.tt("pool", w1, ur, es, ALU.mult)
                C.tt("dve", w2, ui, es, ALU.mult)
                C.tt("dve", ur, ur, ec, ALU.mult)
                C.tt("pool", ui, ui, ec, ALU.mult)
                C.tt("dve", ur, ur, w2, ALU.subtract)
                C.tt("pool", ui, ui, w1, ALU.add)
        if is_s:
            for k, (nm, buf) in enumerate((("s_sre_o", XR), ("s_sim_o", XI))):
                for q in range(2):
                    C.copy("pool", TMP(None, TMP.t[:, 0:128].rearrange("p (q n o) -> p q n o", n=8, o=1)),
                           xv(buf, L, L + 1, 16 * q, 16 * q + 16))
                    ps = next_psb()
                    C.tr(ps(None, ps.t[:, 0:128]), TMP(None, TMP.t[:, 0:128]), ident.all())
                    C.copy("dve", S5ST.all(), ps(None, ps.t[:, 0:128]))
                    for pr in range(16):
                        g0 = 2 * (16 * q + pr)
                        C.dma(O[nm][s0:s0 + 8, g0:g0 + 2, :].rearrange("n m p -> n (m p)"),
                              S5ST(None, S5ST.t[pr * 8:(pr + 1) * 8, :]))
        else:
            for k, buf in enumerate((XR, XI)):
                C.copy("pool", S5C(None, S5C.t[:, k, :].unsqueeze(2).unsqueeze(3)), xv(buf, L, L + 1))
            if tile["last"]:
                for k, nm in enumerate(("p_sre", "p_sim")):
                    ps = next_psb()
                    C.tr(ps(None, ps.t[0:32, 0:128]), S5C(None, S5C.t[:, k, :]), ident.all())
                    C.copy("dve", S5ST(None, S5ST.t[0:32, :]), ps(None, ps.t[0:32, 0:128]))
                    C.dma(O[nm].rearrange("(q m) p -> q (m p)", m=2), S5ST(None, S5ST.t[0:32, :]))
        for gt in range(8):
            ps = next_psa()
            for ip in range(4):
                pair = 4 * gt + ip
                hf = ip // 2
                ov = ps(None, ps.t[64 * hf:64 * hf + 64, 0:T])
                xr_ = XR(None, XR.t[:, pair, 0:nseq * Wx].rearrange("p (n w) -> p n w", w=Wx)[:, :, 1:Wx])
                xi_ = XI(None, XI.t[:, pair, 0:nseq * Wx].rearrange("p (n w) -> p n w", w=Wx)[:, :, 1:Wx])
                cr, ci = CRE[ip % 2], CIM[ip % 2]
                C.mm(ov, cr(None, cr.t[:, gt, 64 * hf:64 * hf + 64]), xr_, start=(ip % 2 == 0), stop=False)
                C.mm(ov, ci(None, ci.t[:, gt, 64 * hf:64 * hf + 64]), xi_, start=False, stop=(ip % 2 == 1))
            gv = pj(gt)
            C.stt("dve", gv, gv, VEC0(None, VEC0.t[:, gt, 34:35]), ps(None, ps.t[:, 0:T]), ALU.mult, ALU.add)
            C.act(GELB(("t", gt), GELB.t[:, gt, 0:T]), gv, AF.Gelu)

        def evac_glu(j, pv):
            if j < 8:
                C.act(ACC(("t", j), ACC.t[:, j, 0:T]), pv, AF.Identity, bias=VECG(None, VECG.t[:, j, 0:1]))
            else:
                jj = j - 8
                tv = TMP(None, TMP.t[:, 0:T])
                C.act(tv, pv, AF.Sigmoid, bias=VECG(None, VECG.t[:, j, 0:1]))
                C.tt("dve", tv, tv, ACC(("t", jj), ACC.t[:, jj, 0:T]), ALU.mult)
                C.tt("dve", mixv(8 + jj, T), tv, pj(8 + jj), ALU.mult)

        stream_mm(I["s5_glu_w"], 8, [(i * 128, 128) for i in range(16)], lambda kt: GELB(("t", kt), GELB.t[:, kt, 0:T]), T, evac_glu)
        out_proj_ln(I["ev_w_out"], tile, VECG, 1, 2)

    do_rwkv = cfg.get("rwkv", True)
    MASKBIG = {"p": C.sb("mbig_p", [128, 128]), "s": C.sb("mbig_s", [64, 64])}
    SEGTRI = {"p": C.sb("stri_p", [128, 128]), "s": C.sb("stri_s", [64, 64])}
    R01 = {"p": C.sb("r01p", [128, 128]), "s": C.sb("r01s", [128, 64])}
    RNEG = {"p": C.sb("rnegp", [128, 128]), "s": C.sb("rnegs", [128, 64])}
    SEGSEL = C.sb("SEGSEL", [8, 64])
    SEGC = C.sb("SEGC", [64, 8])
    SEGROW = C.sb("SEGROW", [128, 8, 64])
    for k in ("p", "s"):
        C.dma(MASKBIG[k].all(), I["maskbig_" + k])
        C.dma(SEGTRI[k].all(), I["segtri_" + k])
        C.dma(R01[k].all(), I["r01_" + k])
        C.dma(RNEG[k].all(), I["rneg_" + k])
    C.dma(SEGSEL.all(), I["segsel"])
    C.dma(SEGC.all(), I["segc"])
    C.dma(SEGROW.all(), I["segrow"].rearrange("p (n t) -> p n t", n=8))

    VEC1 = C.sb("VEC1", [128, 8, 8])
    VMU = C.sb("VMU", [128, 25, 1])
    VECO = C.sb("VECO", [128, 16, 2])
    GB = C.sb("GB", [8, 1])
    load_rows_T(VEC1, [I[n_] for n_ in ("m_hn_g", "r_w0", "r_a0", "r_kk", "r_ka", "r_ln_g", "r_ln_b", "r_rk")], 1024)
    load_rows_T(VMU, [I["r_mu"][:, 0:2048]], 2048)
    load_rows_T(VMU, [I["r_mu"][:, 2048:3200]], 1152, ct0=16)
    load_rows_T(VECO, [I["od_ln_g"], I["od_ln_b"]], 2048)
    C.dma(GB(None, GB.t[0:4, :]), I["m_ig_b"].rearrange("o h -> h o"), slow=True)
    C.dma(GB(None, GB.t[4:8, :]), I["m_fg_b"].rearrange("o h -> h o"), slow=True)

    VTM = C.sb("VTM", [128, 4, 257])
    KW = C.sb("KW", [128, 1024])
    KWN = C.sb("KWN", [128, 256])
    GX = C.sb("GX", [8, 128])
    ROW = C.sb("ROW", [128, 8, 128])
    COL = C.sb("COL", [128, 64])
    DTB = C.sb("DTB", [128, 128])
    STB = C.sb("STB", [128, 128])
    P1S = C.sb("P1S", [128, 257])
    NUM = C.sb("NUM", [128, 257])
    HN = C.sb("HN", [128, 256])
    SM = C.sb("SM", [128, 16])
    CS = C.sb("CS", [128, 4, 2, 257])
    MCAR = C.sb("MCAR", [128, 4])
    CSS = [C.sb("CSS%d" % i, [128, 2, 257]) for i in range(2)]
    CSO = [C.sb("CSO%d" % i, [128, 2, 257]) for i in range(2)]
    MS = C.sb("MS", [8, 4])
    MSB = C.sb("MSB", [8, 128])
    MINIT = C.sb("MINIT", [128, 4, 8])
    DEC = C.sb("DEC", [128, 4, 8])
    MNEW = C.sb("MNEW", [128, 4, 8])
    QM = [C.sb("QM%d" % i, [128, 64]) for i in range(2)]
    C.memset("pool", VTM(None, VTM.t[:, :, 256:257]), 1.0)

    def stream_mm_tok(W, c0, ncols, T, evac):
        Wv = W.rearrange("(kt p) c -> p kt c", p=128)
        for g in range(ncols // WG):
            ps = next_psa()
            pv = ps(None, ps.t[0:T, 0:WG])
            for kq in range(0, 16, KQ):
                ws = WS[rr["ws"] % NWS]
                rr["ws"] += 1
                C.dma(ws(None, ws.t[:, :, 0:WG]), Wv[:, kq:kq + KQ, c0 + g * WG:c0 + (g + 1) * WG], q="pool")
                for k in range(KQ):
                    kt = kq + k
                    C.mm(pv, XTB(("t", kt), XTB.t[:, kt, 0:T]), ws(None, ws.t[:, k, 0:WG]), start=(kt == 0), stop=(kt == 15))
            evac(g, pv)

    def recip(e, out, in_):
        return C.op(e, lambda g: g.reciprocal(out=out.ap, in_=in_.ap), reads=[in_], writes=[out])

    def scan(out, d0, d1, init, op0, op1):
        rd = [d0, d1] + ([init] if isinstance(init, View) else [])
        ia = init.ap if isinstance(init, View) else init
        return C.op("dve", lambda g: g.tensor_tensor_scan(out=out.ap, data0=d0.ap, data1=d1.ap, initial=ia, op0=op0, op1=op1),
                    reads=rd, writes=[out])

    SR = C.sb("SR", [128, 8, 64])
    W2A2 = C.sb("W2A2", [128, 1024])
    BLK = C.sb("BLK", [128, 128])
    OMKA = C.sb("OMKA", [128, 8])
    SHC = C.sb("SHC", [128, 25])
    SUMB = C.sb("SUMB", [128, 8])
    C.dma(W2A2(None, W2A2.t[0:64, :]), I["r_w2"])
    C.dma(W2A2(None, W2A2.t[64:128, :]), I["r_a2"])
    C.dma(BLK.all(), I["blk"])
    C.ts("dve", OMKA.all(), VEC1(None, VEC1.t[:, :, 4]), -1.0, 1.0, ALU.mult, ALU.add)
    XIf = XI.t[:, :, :].rearrange("p a b -> p (a b)")
    T1 = XI("T1", XIf[:, 0:512].rearrange("p (j k) -> p j k", k=64))
    T2 = XI("T2", XIf[:, 512:1024].rearrange("p (j k) -> p j k", k=64))
    FSv = XIf[:, 1024:1536].rearrange("p (i t) -> p i t", t=128)
    SRS = [XI(("SRS", i), XIf[:, 1536 + 512 * i:2048 + 512 * i].rearrange("p (j k) -> p j k", k=64)) for i in range(2)]
    TWv = XIf[:, 2560:2688]
    ALLPS = PSA + PSB

    def next_ps8():
        rr["ps8"] = (rr.get("ps8", 0) + 1) % 8
        return ALLPS[rr["ps8"]]

    def rwkv(tile):
        T, nseq, L = tile["T"], tile["nseq"], tile["L"]
        is_s = tile["kind"] == "s"
        s0 = tile["s0"]
        Wx = 1 + L
        xrhs = lambda kt: XTB(("t", kt), XTB.t[:, kt, 0:T])
        pj = lambda j: PJ(("t", j), PJ.t[:, j, 0:T])
        pj3 = lambda j: PJ(("t", j), PJ.t[:, j, 0:T].rearrange("p (n l) -> p n l", l=L))

        def ppv(j0, j1, a, b):
            v = XR.t[:, j0:j1, 0:nseq * Wx].rearrange("p j (n w) -> p j n w", w=Wx)[:, :, :, a:b]
            return XR(("pp", j0) if j1 == j0 + 1 else None, v)

        if is_s:
            for (c0, ncol, ct0) in ((0, 2048, 0), (2048, 1152, 16)):
                C.dma(XS(None, XS.t[0:8, 0:ncol]), I["s_rsh"][s0:s0 + 8, c0:c0 + ncol])
                for ct in range(ncol // 128):
                    ps = next_psb()
                    C.tr(ps(None, ps.t[:, 0:8]), XS(None, XS.t[0:8, ct * 128:(ct + 1) * 128]), ident(None, ident.t[0:8, 0:8]))
                    C.copy("dve", ppv(ct0 + ct, ct0 + ct + 1, 0, 1), ps(None, ps.t[:, 0:8].rearrange("p (j n o) -> p j n o", j=1, o=1)))
        else:
            if tile["first"]:
                C.memset("pool", ppv(0, 25, 0, 1), 0.0)
            else:
                C.copy("pool", ppv(0, 25, 0, 1), SHC(None, SHC.t[:, :].unsqueeze(2).unsqueeze(3)))

        def evacR(j, pv):
            if j < 25:
                C.copy(ev_eng(), ppv(j, j + 1, 1, Wx), pv.buf(None, pv.ap.rearrange("p (j n l) -> p j n l", j=1, l=L)))
            else:
                C.act(pj(j), pv, AF.Silu)

        stream_mm(I["od_w_in"], 16, [(5128 + i * 128, 128) for i in range(33)], xrhs, T, evacR)

        if is_s:
            for (c0, ncol, ct0) in ((0, 2048, 0), (2048, 1152, 16)):
                for ct in range(ncol // 128):
                    ps = next_psb()
                    C.copy("pool", TMP(None, TMP.t[:, 0:8]), XR(("pp", ct0 + ct), XR.t[:, ct0 + ct, 0:nseq * Wx].rearrange("p (n w) -> p n w", w=Wx)[:, :, L]))
                    C.tr(ps(None, ps.t[0:8, 0:128]), TMP(None, TMP.t[:, 0:8]), ident.all())
                    C.copy("dve", XS(None, XS.t[0:8, ct * 128:(ct + 1) * 128]), ps(None, ps.t[0:8, 0:128]))
                C.dma(O["s_rsh_o"][s0:s0 + 8, c0:c0 + ncol], XS(None, XS.t[0:8, 0:ncol]))
        else:
            C.copy("pool", SHC(None, SHC.t[:, :].unsqueeze(2).unsqueeze(3)), ppv(0, 25, L, L + 1))
            if tile["last"]:
                for (c0, ncol, ct0) in ((0, 2048, 0), (2048, 1152, 16)):
                    for ct in range(ncol // 128):
                        ps = next_psb()
                        C.tr(ps(None, ps.t[0:1, 0:128]), SHC(None, SHC.t[:, ct0 + ct:ct0 + ct + 1]), ident.all())
                        C.copy("dve", XS(None, XS.t[0:1, ct * 128:(ct + 1) * 128]), ps(None, ps.t[0:1, 0:128]))
                    C.dma(O["p_rsh"][:, c0:c0 + ncol], XS(None, XS.t[0:1, 0:ncol]))
        for j in range(25):
            C.tt("pool", pj3(j), XR(("pp", j), XR.t[:, j, 0:nseq * Wx].rearrange("p (n w) -> p n w", w=Wx)[:, :, 0:L]),
                 XR(("pp", j), XR.t[:, j, 0:nseq * Wx].rearrange("p (n w) -> p n w", w=Wx)[:, :, 1:Wx]), ALU.subtract)
            C.stt("dve", pj3(j), pj3(j), VMU(None, VMU.t[:, j, 0:1]),
                  XR(("pp", j), XR.t[:, j, 0:nseq * Wx].rearrange("p (n w) -> p n w", w=Wx)[:, :, 1:Wx]), ALU.mult, ALU.add)

        VTMf = VTM.t[:, :, :].rearrange("p a b -> p (a b)")
        ROWf = ROW.t[:, :, :].rearrange("p a b -> p (a b)")
        KKt = lambda a, b: KW(None, KW.t[0:T, a:b])
        Wt = lambda a, b: VTM(None, VTMf[0:T, a:b])
        KKAt = lambda a, b: ROW(None, ROWf[0:T, a:b])
        KPt = lambda a, b: XS(None, XS.t[0:T, a:b])
        Rt = lambda a, b: XS(None, XS.t[0:T, 1024 + a:1024 + b])
        fs = lambda i: XI(("FS", i), FSv[:, i, 0:T])
        tw = XI("TW", TWv[0:64, 0:T])
        C.act(tw, PJ(("t", 24), PJ.t[0:64, 24, 0:T]), AF.Tanh)

        def to_tok(dst, src):
            ps = next_psb()
            C.tr(ps(None, ps.t[0:T, 0:128]), src, ident.all())
            C.copy(ev_eng(), dst, ps(None, ps.t[0:T, 0:128]))

        NE05 = -float(np.exp(-0.5))
        for ct in range(8):
            r_, k_, v_ = pj(ct), pj(8 + ct), pj(16 + ct)
            cs_ = slice(ct * 128, (ct + 1) * 128)
            ps = next_psa()
            C.mm(ps(None, ps.t[:, 0:T]), W2A2(None, W2A2.t[0:64, cs_]), tw)
            C.act(fs(0), ps(None, ps.t[:, 0:T]), AF.Sigmoid, bias=VEC1(None, VEC1.t[:, ct, 1:2]))
            C.act(fs(0), fs(0), AF.Exp, scale=NE05)
            to_tok(Wt(ct * 128, (ct + 1) * 128), fs(0))
            ps = next_psa()
            C.mm(ps(None, ps.t[:, 0:T]), W2A2(None, W2A2.t[64:128, cs_]), PJ(("t", 24), PJ.t[64:128, 24, 0:T]))
            C.act(fs(1), ps(None, ps.t[:, 0:T]), AF.Sigmoid, bias=VEC1(None, VEC1.t[:, ct, 2:3]))
            C.ts("dve", fs(2), k_, VEC1(None, VEC1.t[:, ct, 3:4]), None, ALU.mult)
            C.tt("pool", fs(3), fs(2), fs(2), ALU.mult)
            ps = next_psa()
            C.mm(ps(None, ps.t[:, 0:T]), BLK.all(), fs(3))
            C.act(fs(3), ps(None, ps.t[:, 0:T]), AF.Sqrt)
            C.ts("dve", fs(3), fs(3), 1e-12, None, ALU.max)
            recip("dve", fs(3), fs(3))
            C.tt("dve", fs(2), fs(2), fs(3), ALU.mult)
            to_tok(KKt(ct * 128, (ct + 1) * 128), fs(2))
            C.tt("dve", fs(3), fs(2), fs(1), ALU.mult)
            to_tok(KKAt(ct * 128, (ct + 1) * 128), fs(3))
            C.ts("dve", fs(1), fs(1), VEC1(None, VEC1.t[:, ct, 4:5]), OMKA(None, OMKA.t[:, ct:ct + 1]), ALU.mult, ALU.add)
            C.tt("dve", fs(1), fs(1), k_, ALU.mult)
            to_tok(KPt(ct * 128, (ct + 1) * 128), fs(1))
            to_tok(Rt(ct * 128, (ct + 1) * 128), r_)
            C.tt("dve", fs(3), r_, fs(1), ALU.mult)
            C.ts("dve", fs(3), fs(3), VEC1(None, VEC1.t[:, ct, 7:8]), None, ALU.mult)
            ps = next_psa()
            C.mm(ps(None, ps.t[:, 0:T]), BLK.all(), fs(3))
            C.tt("dve", ACC(("t", ct), ACC.t[:, ct, 0:T]), ps(None, ps.t[:, 0:T]), v_, ALU.mult)

        Yv = MIX.t[:, 8 * TT:16 * TT].rearrange("p (j t) -> p j t", j=8)
        srcs = (("kk", KW.t[0:T, :]), ("w", VTMf[0:T, 0:1024]), ("kka", ROWf[0:T, 0:1024]), ("kp", XS.t[0:T, 0:1024]), ("r", XS.t[0:T, 1024:2048]))
        bufs = {"kk": KW, "w": VTM, "kka": ROW, "kp": XS, "r": XS}
        for n in range(nseq):
            if is_s:
                sr = SRS[n % 2]
                C.dma(sr, I["s_rs"][s0 + n].rearrange("(j hp) v k -> (hp v) j k", hp=2))
            else:
                sr = SR.all()
                if tile["first"]:
                    C.memset("pool", sr, 0.0)
            for l in range(L):
                t = n * L + l
                oh = ident(None, ident.t[0:T, t:t + 1].broadcast_to([T, 64]))
                bc = {}
                for nm, ap in srcs:
                    ps = next_ps8()
                    xv = ap.rearrange("p (j hp k) -> p hp j k", hp=2, k=64)
                    C.mm(ps(None, ps.t[0:64, 0:512]), oh, bufs[nm](None, xv[:, 0]))
                    C.mm(ps(None, ps.t[64:128, 0:512]), oh, bufs[nm](None, xv[:, 1]))
                    bc[nm] = ps(None, ps.t[:, 0:512].rearrange("p (j k) -> p j k", k=64))
                C.tt("dve", T1, sr, bc["kk"], ALU.mult)
                C.op("dve", lambda g: g.reduce_sum(out=SUMB.t[:, :], in_=T1.ap, axis=AX.X), reads=[T1], writes=[SUMB.all()])
                C.tt("dve", sr, sr, bc["w"], ALU.mult)
                C.tt("dve", T2, bc["kka"], SUMB(None, SUMB.t[:, :].unsqueeze(2).broadcast_to([128, 8, 64])), ALU.mult)
                C.tt("dve", sr, sr, T2, ALU.subtract)
                C.tt("dve", T1, bc["kp"], PJ(None, PJ.t[:, 16:24, t].unsqueeze(2).broadcast_to([128, 8, 64])), ALU.mult)
                C.tt("dve", sr, sr, T1, ALU.add)
                C.tt("dve", T2, sr, bc["r"], ALU.mult)
                yv = MIX(None, Yv[:, :, t])
                C.op("dve", lambda g, yv=yv: g.reduce_sum(out=yv.ap, in_=T2.ap, axis=AX.X), reads=[T2], writes=[yv])
            if is_s:
                C.dma(O["s_rs_o"][s0 + n].rearrange("(j hp) v k -> (hp v) j k", hp=2), sr)
        if (not is_s) and tile["last"]:
            C.dma(O["p_rs"].rearrange("(j hp) v k -> (hp v) j k", hp=2), SR.all())

        for j in range(8):
            y = mixv(8 + j, T)
            ps = next_psa()
            C.mm(ps(None, ps.t[:, 0:T]), BLK.all(), y)
            C.tt("pool", fs(0), y, y, ALU.mult)
            ps2 = next_psa()
            C.mm(ps2(None, ps2.t[:, 0:T]), BLK.all(), fs(0))
            C.ts("dve", fs(1), ps(None, ps.t[:, 0:T]), 1.0 / 64, None, ALU.mult)
            C.ts("dve", fs(2), ps2(None, ps2.t[:, 0:T]), 1.0 / 64, None, ALU.mult)
            C.tt("dve", fs(3), fs(1), fs(1), ALU.mult)
            C.tt("dve", fs(2), fs(2), fs(3), ALU.subtract)
            C.ts("dve", fs(2), fs(2), 64e-5, None, ALU.add)
            C.act(fs(2), fs(2), AF.Sqrt)
            recip("dve", fs(2), fs(2))
            C.tt("dve", y, y, fs(1), ALU.subtract)
            C.tt("dve", y, y, fs(2), ALU.mult)
            C.act(y, y, AF.Identity, bias=VEC1(None, VEC1.t[:, j, 6:7]), scale=VEC1(None, VEC1.t[:, j, 5:6]))
            C.tt("dve", y, y, ACC(("t", j), ACC.t[:, j, 0:T]), ALU.add)
            C.tt("dve", y, y, pj(25 + j), ALU.mult)

    SSTRI = {"p": C.sb("sstri_p", [128, 128]), "s": C.sb("sstri_s", [64, 64])}
    SSTRIT = {"p": C.sb("sstriT_p", [128, 128]), "s": C.sb("sstriT_s", [64, 64])}
    for k_ in ("p", "s"):
        C.dma(SSTRI[k_].all(), I["sstri_" + k_])
        C.dma(SSTRIT[k_].all(), I["sstriT_" + k_])
    WLB = C.sb("WLB", [128, 8, 8])
    HBf = HB.t[:, :, :].rearrange("p a b -> p (a b)")
    KKv = KW.t[:, :].rearrange("p (j t) -> p j t", t=128)
    BTv = ROW.t
    XRf = XR.t[:, :, :].rearrange("p a b -> p (a b)")
    S0Tv = XRf[:, 0:4096].rearrange("p (n j v) -> p n j v", n=8, j=8)

    def fence(buf, ap):
        C.op("pool", lambda g: g.memset(ap, 0.0), writes=[buf.all()])

    def rwkv2(tile):
        T, nseq, L = tile["T"], tile["nseq"], tile["L"]
        is_s = tile["kind"] == "s"
        kd = tile["kind"]
        s0 = tile["s0"]
        Wx = 1 + L
        xrhs = lambda kt: XTB(("t", kt), XTB.t[:, kt, 0:T])
        pj = lambda j: PJ(("t", j), PJ.t[:, j, 0:T])
        pj3 = lambda j: PJ(("t", j), PJ.t[:, j, 0:T].rearrange("p (n l) -> p n l", l=L))

        def ppv(j0, j1, a, b):
            v = XR.t[:, j0:j1, 0:nseq * Wx].rearrange("p j (n w) -> p j n w", w=Wx)[:, :, :, a:b]
            return XR(("pp", j0) if j1 == j0 + 1 else None, v)

        if is_s:
            for (c0, ncol, ct0) in ((0, 2048, 0), (2048, 1152, 16)):
                C.dma(XS(None, XS.t[0:8, 0:ncol]), I["s_rsh"][s0:s0 + 8, c0:c0 + ncol])
                for ct in range(ncol // 128):
                    ps = next_psb()
                    C.tr(ps(None, ps.t[:, 0:8]), XS(None, XS.t[0:8, ct * 128:(ct + 1) * 128]), ident(None, ident.t[0:8, 0:8]))
                    C.copy("dve", ppv(ct0 + ct, ct0 + ct + 1, 0, 1), ps(None, ps.t[:, 0:8].rearrange("p (j n o) -> p j n o", j=1, o=1)))
        else:
            if tile["first"]:
                C.memset("pool", ppv(0, 25, 0, 1), 0.0)
            else:
                C.copy("pool", ppv(0, 25, 0, 1), SHC(None, SHC.t[:, :].unsqueeze(2).unsqueeze(3)))

        def evacR(j, pv):
            if j < 25:
                C.copy(ev_eng(), ppv(j, j + 1, 1, Wx), pv.buf(None, pv.ap.rearrange("p (j n l) -> p j n l", j=1, l=L)))
            else:
                C.act(pj(j), pv, AF.Silu)

        stream_mm(I["od_w_in"], 16, [(5128 + i * 128, 128) for i in range(33)], xrhs, T, evacR)

        if is_s:
            for (c0, ncol, ct0) in ((0, 2048, 0), (2048, 1152, 16)):
                for ct in range(ncol // 128):
                    ps = next_psb()
                    C.copy("pool", TMP(None, TMP.t[:, 0:8]), XR(("pp", ct0 + ct), XR.t[:, ct0 + ct, 0:nseq * Wx].rearrange("p (n w) -> p n w", w=Wx)[:, :, L]))
                    C.tr(ps(None, ps.t[0:8, 0:128]), TMP(None, TMP.t[:, 0:8]), ident.all())
                    C.copy("dve", XS(None, XS.t[0:8, ct * 128:(ct + 1) * 128]), ps(None, ps.t[0:8, 0:128]))
                C.dma(O["s_rsh_o"][s0:s0 + 8, c0:c0 + ncol], XS(None, XS.t[0:8, 0:ncol]))
        else:
            C.copy("pool", SHC(None, SHC.t[:, :].unsqueeze(2).unsqueeze(3)), ppv(0, 25, L, L + 1))
            if tile["last"]:
                for (c0, ncol, ct0) in ((0, 2048, 0), (2048, 1152, 16)):
                    for ct in range(ncol // 128):
                        ps = next_psb()
                        C.tr(ps(None, ps.t[0:1, 0:128]), SHC(None, SHC.t[:, ct0 + ct:ct0 + ct + 1]), ident.all())
                        C.copy("dve", XS(None, XS.t[0:1, ct * 128:(ct + 1) * 128]), ps(None, ps.t[0:1, 0:128]))
                    C.dma(O["p_rsh"][:, c0:c0 + ncol], XS(None, XS.t[0:1, 0:ncol]))
        for j in range(25):
            C.tt("pool", pj3(j), XR(("pp", j), XR.t[:, j, 0:nseq * Wx].rearrange("p (n w) -> p n w", w=Wx)[:, :, 0:L]),
                 XR(("pp", j), XR.t[:, j, 0:nseq * Wx].rearrange("p (n w) -> p n w", w=Wx)[:, :, 1:Wx]), ALU.subtract)
            C.stt("dve", pj3(j), pj3(j), VMU(None, VMU.t[:, j, 0:1]),
                  XR(("pp", j), XR.t[:, j, 0:nseq * Wx].rearrange("p (n w) -> p n w", w=Wx)[:, :, 1:Wx]), ALU.mult, ALU.add)

        s0t = lambda n, j, rs=slice(0, 128): XR(("st", n), S0Tv[rs, n, j, :])
        if is_s:
            fence(XR, XR.t[0:1, 0, 0:1])
            for n2 in range(0, 8, 2):
                stg = XS.t[:, :].rearrange("p (n j d k) -> p n j d k", n=2, j=8, d=2)
                for nn in range(2):
                    for d in range(2):
                        C.dma(XS(None, stg[:, nn, :, d, :]), I["s_rs"][s0 + n2 + nn].rearrange("(j hp) v k -> (hp v) j k", hp=2))
                for nn in range(2):
                    n = n2 + nn
                    for j in range(8):
                        ps = next_psb()
                        C.tr(ps(None, ps.t[:, 0:128]), XS(None, stg[:, nn, j, :, :]), ident.all())
                        C.copy("dve", s0t(n, j, slice(0, 64)), ps(None, ps.t[0:64, 0:64]))
                        C.copy("act", s0t(n, j, slice(64, 128)), ps(None, ps.t[64:128, 64:128]))
        else:
            if tile["first"]:
                C.memset("pool", SR.all(), 0.0)

        VTMf = VTM.t[:, :, :].rearrange("p a b -> p (a b)")
        Vtm = lambda hc: XS(None, XS.t[0:T, hc])
        Btm = lambda hc: XS(None, XS.t[0:T, 1024 + hc.start:1024 + hc.stop])
        Ktm = lambda hc: VTM(None, VTMf[0:T, hc])
        fs = lambda i: XI(("FS", i), FSv[:, i, 0:T])
        tw = XI("TW", TWv[0:64, 0:T])
        C.act(tw, PJ(("t", 24), PJ.t[0:64, 24, 0:T]), AF.Tanh)

        def to_tok(dst, src):
            ps = next_psb()
            C.tr(ps(None, ps.t[0:T, 0:128]), src, ident.all())
            C.copy(ev_eng(), dst, ps(None, ps.t[0:T, 0:128]))

        NE05 = -float(np.exp(-0.5))
        kkc = lambda ct, rs=slice(0, 128): KW(("c", ct), KKv[rs, ct, 0:T])
        btc = lambda ct, rs=slice(0, 128): ROW(("c", ct), BTv[rs, ct, 0:T])
        for ct in range(8):
            r_, k_, v_ = pj(ct), pj(8 + ct), pj(16 + ct)
            cs_ = slice(ct * 128, (ct + 1) * 128)
            ps = next_psa()
            C.mm(ps(None, ps.t[:, 0:T]), W2A2(None, W2A2.t[0:64, cs_]), tw)
            C.act(fs(0), ps(None, ps.t[:, 0:T]), AF.Sigmoid, bias=VEC1(None, VEC1.t[:, ct, 1:2]))
            C.ts("dve", fs(0), fs(0), NE05, None, ALU.mult)
            scan(fs(1), R01[kd](None, R01[kd].t[:, 0:T]), fs(0), 0.0, ALU.mult, ALU.add)
            ps = next_psa()
            C.mm(ps(None, ps.t[:, 0:T]), W2A2(None, W2A2.t[64:128, cs_]), PJ(("t", 24), PJ.t[64:128, 24, 0:T]))
            C.act(fs(2), ps(None, ps.t[:, 0:T]), AF.Sigmoid, bias=VEC1(None, VEC1.t[:, ct, 2:3]))
            C.ts("dve", kkc(ct), k_, VEC1(None, VEC1.t[:, ct, 3:4]), None, ALU.mult)
            C.tt("pool", fs(3), kkc(ct), kkc(ct), ALU.mult)
            ps = next_psa()
            C.mm(ps(None, ps.t[:, 0:T]), BLK.all(), fs(3))
            C.act(fs(3), ps(None, ps.t[:, 0:T]), AF.Sqrt)
            C.ts("dve", fs(3), fs(3), 1e-12, None, ALU.max)
            recip("dve", fs(3), fs(3))
            C.tt("dve", kkc(ct), kkc(ct), fs(3), ALU.mult)
            C.tt("dve", btc(ct), kkc(ct), fs(2), ALU.mult)
            C.ts("dve", fs(2), fs(2), VEC1(None, VEC1.t[:, ct, 4:5]), OMKA(None, OMKA.t[:, ct:ct + 1]), ALU.mult, ALU.add)
            C.tt("dve", fs(2), fs(2), k_, ALU.mult)
            C.tt("pool", fs(3), r_, fs(2), ALU.mult)
            C.ts("dve", fs(3), fs(3), VEC1(None, VEC1.t[:, ct, 7:8]), None, ALU.mult)
            ps = next_psa()
            C.mm(ps(None, ps.t[:, 0:T]), BLK.all(), fs(3))
            C.tt("dve", ACC(("t", ct), ACC.t[:, ct, 0:T]), ps(None, ps.t[:, 0:T]), v_, ALU.mult)
            C.act(fs(3), fs(1), AF.Exp)
            C.tt("dve", r_, r_, fs(3), ALU.mult)
            C.copy("pool", WLB(None, WLB.t[:, ct, 0:nseq]),
                   XI(("FS", 3), FSv[:, 3, 0:T].rearrange("p (n l) -> p n l", l=L)[:, :, L - 1]))
            C.tt("dve", fs(3), fs(1), fs(0), ALU.subtract)
            C.act(fs(3), fs(3), AF.Exp)
            C.tt("dve", kkc(ct), kkc(ct), fs(3), ALU.mult)
            C.act(fs(3), fs(1), AF.Exp, scale=-1.0)
            C.tt("dve", btc(ct), btc(ct), fs(3), ALU.mult)
            C.tt("dve", k_, fs(2), fs(3), ALU.mult)
            to_tok(Vtm(cs_), v_)
            to_tok(Btm(cs_), btc(ct))
            to_tok(Ktm(cs_), k_)

        fence(HB, HB.t[0:1, 0, 0:1])
        mat = lambda i: HB(("m", i), HBf[0:T, i * 128:i * 128 + T])
        half = lambda i, a: HB(("m", i), HBf[0:T, i * 128 + 64 * a:i * 128 + 64 * a + 64])
        nsq = max(0, int(np.ceil(np.log2(L))) - 1)
        idT = ident(None, ident.t[0:T, 0:T])
        mS = SSTRI[kd](None, SSTRI[kd].t[0:T, 0:T])
        mST = SSTRIT[kd](None, SSTRIT[kd].t[0:T, 0:T])
        mI = SEGTRI[kd](None, SEGTRI[kd].t[0:T, 0:T])
        if is_s:
            KKM, RM = T1, T2
        for j in range(8):
            Q = [[mat(8 * hp + 0), mat(8 * hp + 1)] for hp in range(2)]
            QT = [[mat(8 * hp + 2), mat(8 * hp + 3)] for hp in range(2)]
            Pm = [mat(8 * hp + 4) for hp in range(2)]
            BR = [mat(8 * hp + 5) for hp in range(2)]
            AK = [mat(8 * hp + 6) for hp in range(2)]
            KR = [mat(8 * hp + 7) for hp in range(2)]
            RHS = [half(16, hp) for hp in range(2)]
            SAT = [half(17, hp) for hp in range(2)]
            rsl = [slice(0, 64), slice(64, 128)]
            if is_s:
                C.tt("pool", KKM, KW(("c", j), KKv[:, j, 0:T].unsqueeze(1).broadcast_to([128, 8, T])), SEGROW(None, SEGROW.t[:, :, 0:T]), ALU.mult)
                C.tt("pool", RM, PJ(("t", j), PJ.t[:, j, 0:T].unsqueeze(1).broadcast_to([128, 8, T])), SEGROW(None, SEGROW.t[:, :, 0:T]), ALU.mult)
            for hp in range(2):
                rs = rsl[hp]
                rq = PJ(("t", j), PJ.t[rs, j, 0:T])
                kq = PJ(("t", 8 + j), PJ.t[rs, 8 + j, 0:T])
                ps = next_ps8()
                C.mm(ps(None, ps.t[0:T, 0:T]), btc(j, rs), kkc(j, rs))
                C.mm(ps(None, ps.t[0:T, T:2 * T]), btc(j, rs), rq)
                C.tt("dve", Q[hp][0], ps(None, ps.t[0:T, 0:T]), mS, ALU.mult)
                C.tt("dve", BR[hp], ps(None, ps.t[0:T, T:2 * T]), mI, ALU.mult)
                ps = next_ps8()
                C.mm(ps(None, ps.t[0:T, 0:T]), kq, kkc(j, rs))
                C.mm(ps(None, ps.t[0:T, T:2 * T]), kq, rq)
                C.tt("dve", AK[hp], ps(None, ps.t[0:T, 0:T]), mS, ALU.mult)
                C.tt("dve", KR[hp], ps(None, ps.t[0:T, T:2 * T]), mI, ALU.mult)
                ps = next_ps8()
                C.mm(ps(None, ps.t[0:T, 0:T]), kkc(j, rs), btc(j, rs))
                C.tt("dve", QT[hp][0], ps(None, ps.t[0:T, 0:T]), mST, ALU.mult)
                C.stt("dve", Pm[hp], Q[hp][0], -1.0, idT, ALU.mult, ALU.add)
            cur = 0
            for it in range(nsq):
                nxt = 1 - cur
                last_it = (it == nsq - 1)
                for hp in range(2):
                    if not last_it:
                        ps = next_ps8()
                        C.mm(ps(None, ps.t[0:T, 0:T]), QT[hp][cur], Q[hp][cur])
                        C.copy("act", Q[hp][nxt], ps(None, ps.t[0:T, 0:T]))
                    ps = next_ps8()
                    C.mm(ps(None, ps.t[0:T, 0:T]), Q[hp][cur], QT[hp][cur])
                    C.copy("dve", QT[hp][nxt], ps(None, ps.t[0:T, 0:T]))
                for hp in range(2):
                    ps = next_ps8()
                    C.mm(ps(None, ps.t[0:T, 0:T]), QT[hp][nxt], Pm[hp])
                    C.tt("dve", Pm[hp], Pm[hp], ps(None, ps.t[0:T, 0:T]), ALU.add)
                cur = nxt
            for hp in range(2):
                rs = rsl[hp]
                hc = slice((2 * j + hp) * 64, (2 * j + hp) * 64 + 64)
                ps = next_ps8()
                if is_s:
                    for n in range(nseq):
                        C.mm(ps(None, ps.t[0:T, 0:64]), XI("T1", KKM.ap[rs, n, :]), s0t(n, j, rs), start=(n == 0), stop=False)
                else:
                    C.mm(ps(None, ps.t[0:T, 0:64]), kkc(j, rs), SR(None, SR.t[rs, j, :]), start=True, stop=False)
                C.mm(ps(None, ps.t[0:T, 0:64]), AK[hp], Vtm(hc), start=False, stop=True)
                C.act(RHS[hp], ps(None, ps.t[0:T, 0:64]), AF.Identity, scale=-1.0)
                ps = next_ps8()
                C.mm(ps(None, ps.t[0:T, 0:64]), Pm[hp], RHS[hp])
                C.copy("dve", SAT[hp], ps(None, ps.t[0:T, 0:64]))
            psY = next_ps8()
            for hp in range(2):
                rs = rsl[hp]
                hc = slice((2 * j + hp) * 64, (2 * j + hp) * 64 + 64)
                ov = psY(None, psY.t[rs, 0:T])
                if is_s:
                    for n in range(nseq):
                        C.mm(ov, s0t(n, j, rs), XI("T2", RM.ap[rs, n, :]), start=(n == 0), stop=False)
                else:
                    C.mm(ov, SR(None, SR.t[rs, j, :]), PJ(("t", j), PJ.t[rs, j, 0:T]), start=True, stop=False)
                C.mm(ov, SAT[hp], BR[hp], start=False, stop=False)
                C.mm(ov, Vtm(hc), KR[hp], start=False, stop=True)
            y = TMP(None, TMP.t[:, 0:T])
            C.copy("act", y, psY(None, psY.t[:, 0:T]))
            ps = next_psa()
            C.mm(ps(None, ps.t[:, 0:T]), BLK.all(), y)
            C.tt("pool", fs(0), y, y, ALU.mult)
            ps2 = next_psa()
            C.mm(ps2(None, ps2.t[:, 0:T]), BLK.all(), fs(0))
            C.ts("dve", fs(1), ps(None, ps.t[:, 0:T]), 1.0 / 64, None, ALU.mult)
            C.ts("dve", fs(2), ps2(None, ps2.t[:, 0:T]), 1.0 / 64, None, ALU.mult)
            C.tt("dve", fs(3), fs(1), fs(1), ALU.mult)
            C.tt("dve", fs(2), fs(2), fs(3), ALU.subtract)
            C.ts("dve", fs(2), fs(2), 64e-5, None, ALU.add)
            C.act(fs(2), fs(2), AF.Sqrt)
            recip("dve", fs(2), fs(2))
            C.tt("dve", y, y, fs(1), ALU.subtract)
            C.tt("dve", y, y, fs(2), ALU.mult)
            C.act(y, y, AF.Identity, bias=VEC1(None, VEC1.t[:, j, 6:7]), scale=VEC1(None, VEC1.t[:, j, 5:6]))
            C.tt("dve", y, y, ACC(("t", j), ACC.t[:, j, 0:T]), ALU.add)
            C.tt("dve", mixv(8 + j, T), y, pj(25 + j), ALU.mult)
            tmpS = XI(("FS", 0), FSv[:, 0, 0:64])
            if is_s:
                SAM = [CSS[hp](None, CSS[hp].t[0:T, :, :].rearrange("p a b -> p (a b)")[:, 0:512].rearrange("p (n v) -> p n v", n=8)) for hp in range(2)]
                VM = [CSO[hp](None, CSO[hp].t[0:T, :, :].rearrange("p a b -> p (a b)")[:, 0:512].rearrange("p (n v) -> p n v", n=8)) for hp in range(2)]
                segc_b = SEGC(None, SEGC.t[0:T, :].unsqueeze(2).broadcast_to([T, 8, 64]))
                for hp in range(2):
                    hc = slice((2 * j + hp) * 64, (2 * j + hp) * 64 + 64)
                    C.tt("pool", SAM[hp], HB(("m", 17), HBf[0:T, 17 * 128 + 64 * hp:17 * 128 + 64 * hp + 64].unsqueeze(1).broadcast_to([T, 8, 64])), segc_b, ALU.mult)
                    C.tt("pool", VM[hp], XS(None, XS.t[0:T, hc].unsqueeze(1).broadcast_to([T, 8, 64])), segc_b, ALU.mult)
                for n in range(nseq):
                    psS = next_ps8()
                    for hp in range(2):
                        rs = rsl[hp]
                        hc = slice((2 * j + hp) * 64, (2 * j + hp) * 64 + 64)
                        C.mm(psS(None, psS.t[rs, 0:64]), Btm(hc), CSS[hp](None, SAM[hp].ap[:, n, :]), start=True, stop=False)
                        C.mm(psS(None, psS.t[rs, 0:64]), Ktm(hc), CSO[hp](None, VM[hp].ap[:, n, :]), start=False, stop=True)
                    wl = WLB(None, WLB.t[:, j, n:n + 1])
                    C.ts("dve", tmpS, s0t(n, j), wl, None, ALU.mult)
                    C.stt("dve", s0t(n, j), psS(None, psS.t[:, 0:64]), wl, tmpS, ALU.mult, ALU.add)
            else:
                psS = next_ps8()
                for hp in range(2):
                    rs = rsl[hp]
                    hc = slice((2 * j + hp) * 64, (2 * j + hp) * 64 + 64)
                    C.mm(psS(None, psS.t[rs, 0:64]), Btm(hc), SAT[hp], start=True, stop=False)
                    C.mm(psS(None, psS.t[rs, 0:64]), Ktm(hc), Vtm(hc), start=False, stop=True)
                wl = WLB(None, WLB.t[:, j, 0:1])
                srj = SR(None, SR.t[:, j, :])
                C.ts("dve", tmpS, srj, wl, None, ALU.mult)
                C.stt("dve", srj, psS(None, psS.t[:, 0:64]), wl, tmpS, ALU.mult, ALU.add)

        def state_out(src_fn, dst):
            for j in range(8):
                ps = next_psb()
                C.tr(ps(None, ps.t[0:64, 0:128]), src_fn(j), ident.all())
                C.copy(ev_eng(), XS(None, XS.t[0:64, j * 128:(j + 1) * 128]), ps(None, ps.t[0:64, 0:128]))
            C.dma(dst.rearrange("(j hp) v k -> v j hp k", hp=2), XS(None, XS.t[0:64, 0:1024].rearrange("p (j hp k) -> p j hp k", j=8, hp=2)))

        if is_s:
            for n in range(nseq):
                state_out(lambda j, n=n: s0t(n, j), O["s_rs_o"][s0 + n])
        elif tile["last"]:
            state_out(lambda j: SR(None, SR.t[:, j, :]), O["p_rs"])


    def layer1(tile):
        T, nseq, L = tile["T"], tile["nseq"], tile["L"]
        is_s = tile["kind"] == "s"
        kd = tile["kind"]
        s0 = tile["s0"]
        xrhs = lambda kt: XTB(("t", kt), XTB.t[:, kt, 0:T])
        pj = lambda j: PJ(("t", j), PJ.t[:, j, 0:T])
        W = I["od_w_in"]

        def evacM(j, pv):
            if j < 8:
                C.act(pj(j), pv, AF.Identity, scale=1.0 / 16.0)
            elif j < 16:
                C.copy(ev_eng(), pj(j), pv)
            elif j < 24:
                C.act(pj(j), pv, AF.Sigmoid)
            elif j == 24:
                C.copy("dve", PJ(("t", 32), PJ.t[0:8, 32, 0:T]), pv)
            else:
                C.act(pj(j - 1), pv, AF.Silu)

        cols = [(i * 128, 128) for i in range(16)] + [(3072 + i * 128, 128) for i in range(8)] + [(4096, 8)] + \
               [(4104 + i * 128, 128) for i in range(8)]
        stream_mm(W, 16, cols, xrhs, T, evacM)

        def evacV(g, pv):
            C.copy(ev_eng(), VTM(None, VTM.t[0:T, 2 * g:2 * g + 2, 0:256]), pv.buf(None, pv.ap.rearrange("p (h v) -> p h v", h=2)))

        C.memset("pool", VTM(None, VTM.t[:, :, 256:257]), 1.0)
        stream_mm_tok(W, 2048, 1024, T, evacV)

        gx = GX(None, GX.t[0:8, 0:T])
        C.ts("dve", gx, PJ(("t", 32), PJ.t[0:8, 32, 0:T]), GB.all(), None, ALU.add)
        col = lambda a, b: COL(None, COL.t[0:T, a:b])
        ps = next_psb()
        C.tr(ps(None, ps.t[0:T, 0:8]), gx, ident(None, ident.t[0:8, 0:8]))
        C.copy("dve", col(0, 8), ps(None, ps.t[0:T, 0:8]))
        C.act(col(8, 12), col(4, 8), AF.Exp, scale=-1.0)
        C.act(col(8, 12), col(8, 12), AF.Ln, bias=1.0)
        C.ts("dve", col(8, 12), col(8, 12), -1.0, None, ALU.mult)
        ps = next_psb()
        C.mm(ps(None, ps.t[0:T, 0:4]), SEGTRI[kd](None, SEGTRI[kd].t[0:T, 0:T]), col(8, 12))
        C.copy("dve", col(12, 16), ps(None, ps.t[0:T, 0:4]))
        C.tt("dve", col(16, 20), col(0, 4), col(12, 16), ALU.subtract)
        if is_s:
            C.dma(MS.all(), I["s_mm"][s0:s0 + 8, :])
            ps = next_psb()
            C.mm(ps(None, ps.t[0:T, 0:4]), SEGSEL(None, SEGSEL.t[0:8, 0:T]), MS.all())
            C.copy("dve", col(20, 24), ps(None, ps.t[0:T, 0:4]))
            for h in range(4):
                C.copy("dve", MSB.all(), MS(None, MS.t[:, h:h + 1].broadcast_to([8, 128])))
                ps = next_psb()
                C.mm(ps(None, ps.t[:, 0:8]), MSB.all(), ident(None, ident.t[0:8, 0:8]))
                C.copy("dve", MINIT(None, MINIT.t[:, h, :]), ps(None, ps.t[:, 0:8]))
        else:
            if tile["first"]:
                C.memset("dve", MCAR.all(), 0.0)
                C.memset("pool", CS.all(), 0.0)
            C.copy("dve", col(20, 24), MCAR(None, MCAR.t[0:T, :]))
            C.copy("dve", MINIT(None, MINIT.t[:, :, 0:1]), MCAR(None, MCAR.t[:, :].unsqueeze(2)))

        row = lambda k: ROW(None, ROW.t[:, k, 0:T])
        ends = lambda k: ROW(None, ROW.t[:, k, 0:T].rearrange("p (n l) -> p n l", l=L)[:, :, L - 1])
        starts = lambda k: ROW(None, ROW.t[:, k, 0:T].rearrange("p (n l) -> p n l", l=L)[:, :, 0])
        for h in range(4):
            minit_r = MINIT(None, MINIT.t[:, h, 0:nseq])
            ps = next_psb()
            C.mm(ps(None, ps.t[:, 0:T]), ident(None, ident.t[0:8, h:h + 1].broadcast_to([8, 128])), gx)
            C.copy("dve", row(0), ps(None, ps.t[:, 0:T]))
            ps = next_psb()
            C.mm(ps(None, ps.t[:, 0:T]), ident(None, ident.t[0:8, 4 + h:5 + h].broadcast_to([8, 128])), gx)
            C.act(row(1), ps(None, ps.t[:, 0:T]), AF.Exp, scale=-1.0)
            C.act(row(1), row(1), AF.Ln, bias=1.0)
            C.ts("dve", row(1), row(1), -1.0, None, ALU.mult)
            scan(row(2), R01[kd](None, R01[kd].t[:, 0:T]), row(1), 0.0, ALU.mult, ALU.add)
            C.tt("dve", row(3), row(0), row(2), ALU.subtract)
            C.tt("dve", starts(3), starts(3), minit_r, ALU.max)
            scan(row(4), RNEG[kd](None, RNEG[kd].t[:, 0:T]), row(3), -1e30, ALU.add, ALU.max)
            C.copy("dve", ROW(None, ROW.t[:, 5, 0:T].rearrange("p (n l) -> p n l", l=L)),
                   ROW(None, ROW.t[:, 4, 0:T].rearrange("p (n l) -> p n l", l=L)[:, :, L - 1:L].broadcast_to([128, nseq, L])))
            C.tt("dve", MNEW(None, MNEW.t[:, h, 0:nseq]), ends(2), ends(4), ALU.add)
            C.tt("dve", DEC(None, DEC.t[:, h, 0:nseq]), minit_r, ends(4), ALU.subtract)
            C.act(DEC(None, DEC.t[:, h, 0:nseq]), DEC(None, DEC.t[:, h, 0:nseq]), AF.Exp)
            ps = next_psb()
            C.mm(ps(None, ps.t[0:T, 0:1]), row(4), ident(None, ident.t[:, 0:1]))
            C.mm(ps(None, ps.t[0:T, 1:2]), row(5), ident(None, ident.t[:, 0:1]))
            C.copy("dve", col(24, 26), ps(None, ps.t[0:T, 0:2]))
            C.tt("dve", DTB(None, DTB.t[0:T, 0:T]), ROW(None, ROW.t[0:T, 4, 0:T]), MASKBIG[kd](None, MASKBIG[kd].t[0:T, 0:T]), ALU.add)
            C.act(DTB(None, DTB.t[0:T, 0:T]), DTB(None, DTB.t[0:T, 0:T]), AF.Exp, scale=-1.0, bias=col(16 + h, 17 + h))
            ps = next_psa()
            for kt in range(2):
                C.mm(ps(None, ps.t[0:T, 0:T]), pj(8 + 2 * h + kt), pj(2 * h + kt), start=(kt == 0), stop=(kt == 1))
            C.tt("dve", STB(None, STB.t[0:T, 0:T]), ps(None, ps.t[0:T, 0:T]), DTB(None, DTB.t[0:T, 0:T]), ALU.mult)
            ps1 = next_psa()
            C.mm(ps1(None, ps1.t[0:T, 0:257]), STB(None, STB.t[0:T, 0:T]), VTM(None, VTM.t[0:T, h, :]))
            C.copy("act", P1S(None, P1S.t[0:T, :]), ps1(None, ps1.t[0:T, 0:257]))
            ps2 = next_psa()
            if is_s:
                i_ = 0
                for n in range(nseq):
                    cs = CSS[n % 2]
                    C.dma(cs(None, cs.t[:, :, 0:256]), I["s_mc"][s0 + n, h].rearrange("(kt p) v -> p kt v", p=128))
                    C.dma(cs(None, cs.t[:, :, 256:257]), I["s_mn"][s0 + n, h].rearrange("(kt p o) -> p kt o", p=128, o=1), slow=True)
                    for kt in range(2):
                        qm = QM[i_ % 2]
                        i_ += 1
                        C.tt("pool", qm(None, qm.t[:, 0:T]), pj(2 * h + kt), SEGROW(None, SEGROW.t[:, n, 0:T]), ALU.mult)
                        C.mm(ps2(None, ps2.t[0:T, 0:257]), qm(None, qm.t[:, 0:T]), cs(None, cs.t[:, kt, :]),
                             start=(n == 0 and kt == 0), stop=(n == nseq - 1 and kt == 1))
            else:
                for kt in range(2):
                    C.mm(ps2(None, ps2.t[0:T, 0:257]), pj(2 * h + kt), CS(("h", h), CS.t[:, h, kt, :]), start=(kt == 0), stop=(kt == 1))
            sm = lambda a: SM(None, SM.t[0:T, a:a + 1])
            C.tt("dve", sm(0), col(20 + h, 21 + h), col(24, 25), ALU.subtract)
            C.act(sm(0), sm(0), AF.Exp)
            C.stt("dve", NUM(None, NUM.t[0:T, :]), ps2(None, ps2.t[0:T, 0:257]), sm(0), P1S(None, P1S.t[0:T, :]), ALU.mult, ALU.add)
            C.tt("dve", sm(1), col(12 + h, 13 + h), col(24, 25), ALU.add)
            C.act(sm(1), sm(1), AF.Exp, scale=-1.0)
            C.ts("dve", sm(2), NUM(None, NUM.t[0:T, 256:257]), -1.0, None, ALU.mult)
            C.tt("dve", sm(2), sm(2), NUM(None, NUM.t[0:T, 256:257]), ALU.max)
            C.tt("dve", sm(2), sm(2), sm(1), ALU.max)
            recip("dve", sm(2), sm(2))
            C.op("dve", lambda g, T=T: g.reduce_sum(out=SM.t[0:T, 3:4], in_=NUM.t[0:T, 0:256], axis=AX.X),
                 reads=[NUM(None, NUM.t[0:T, 0:256])], writes=[sm(3)])
            C.tt("dve", sm(3), sm(3), sm(2), ALU.mult)
            C.ts("dve", sm(3), sm(3), 1.0 / 256, None, ALU.mult)
            hn = HN(None, HN.t[0:T, :])
            C.ts("dve", hn, NUM(None, NUM.t[0:T, 0:256]), sm(2), sm(3), ALU.mult, ALU.subtract)
            C.tt("dve", P1S(None, P1S.t[0:T, 0:256]), hn, hn, ALU.mult)
            C.op("dve", lambda g, T=T: g.reduce_sum(out=SM.t[0:T, 4:5], in_=P1S.t[0:T, 0:256], axis=AX.X),
                 reads=[P1S(None, P1S.t[0:T, 0:256])], writes=[sm(4)])
            C.ts("dve", sm(4), sm(4), 1.0 / 256, LN_EPS, ALU.mult, ALU.add)
            C.act(sm(4), sm(4), AF.Sqrt)
            recip("dve", sm(4), sm(4))
            C.ts("dve", hn, hn, sm(4), None, ALU.mult)
            for kt in range(2):
                ct = 2 * h + kt
                ps = next_psb()
                C.tr(ps(None, ps.t[:, 0:T]), HN(None, HN.t[0:T, kt * 128:(kt + 1) * 128]), ident(None, ident.t[0:T, 0:T]))
                scr = P1S(None, P1S.t[:, 0:T])
                C.stt("dve", scr, ps(None, ps.t[:, 0:T]), VEC1(None, VEC1.t[:, ct, 0:1]), pj(16 + ct), ALU.mult, ALU.mult)
                C.tt("dve", mixv(ct, T), scr, pj(24 + ct), ALU.mult)
            C.tt("dve", sm(5), col(16 + h, 17 + h), col(25, 26), ALU.subtract)
            C.act(sm(5), sm(5), AF.Exp)
            for kt in range(2):
                ps = next_psb()
                C.tr(ps(None, ps.t[0:T, 0:128]), pj(8 + 2 * h + kt), ident.all())
                C.ts("dve", KW(None, KW.t[0:T, h * 256 + kt * 128:h * 256 + (kt + 1) * 128]), ps(None, ps.t[0:T, 0:128]), sm(5), None, ALU.mult)
            if is_s:
                for n in range(nseq):
                    cs = CSS[n % 2]
                    co = CSO[n % 2]
                    C.dma(cs(None, cs.t[:, :, 0:256]), I["s_mc"][s0 + n, h].rearrange("(kt p) v -> p kt v", p=128))
                    C.dma(cs(None, cs.t[:, :, 256:257]), I["s_mn"][s0 + n, h].rearrange("(kt p o) -> p kt o", p=128, o=1), slow=True)
                    C.ts("pool", KWN(None, KWN.t[0:T, :]), KW(None, KW.t[0:T, h * 256:(h + 1) * 256]), SEGC(None, SEGC.t[0:T, n:n + 1]), None, ALU.mult)
                    for kt in range(2):
                        ps = next_psa()
                        C.mm(ps(None, ps.t[:, 0:257]), KWN(None, KWN.t[0:T, kt * 128:(kt + 1) * 128]), VTM(None, VTM.t[0:T, h, :]))
                        C.stt("dve", co(None, co.t[:, kt, :]), cs(None, cs.t[:, kt, :]), DEC(None, DEC.t[:, h, n:n + 1]), ps(None, ps.t[:, 0:257]), ALU.mult, ALU.add)
                    C.dma(O["s_mc_o"][s0 + n, h].rearrange("(kt p) v -> p kt v", p=128), co(None, co.t[:, :, 0:256]))
                    C.dma(O["s_mn_o"][s0 + n, h].rearrange("(kt p o) -> p kt o", p=128, o=1), co(None, co.t[:, :, 256:257]), slow=True)
                C.dma(O["s_mm_o"][s0:s0 + 8, h:h + 1].rearrange("n o -> o n"), MNEW(None, MNEW.t[0:1, h, 0:8]), slow=True)
            else:
                for kt in range(2):
                    ps = next_psa()
                    C.mm(ps(None, ps.t[:, 0:257]), KW(None, KW.t[0:T, h * 256 + kt * 128:h * 256 + (kt + 1) * 128]), VTM(None, VTM.t[0:T, h, :]))
                    csv = CS(("h", h), CS.t[:, h, kt, :])
                    C.stt("dve", csv, csv, DEC(None, DEC.t[:, h, 0:1]), ps(None, ps.t[:, 0:257]), ALU.mult, ALU.add)
                C.copy("dve", MCAR(None, MCAR.t[:, h:h + 1]), MNEW(None, MNEW.t[:, h, 0:1]))
        if (not is_s) and tile["last"]:
            for h in range(4):
                C.dma(O["p_mc"][h].rearrange("(kt p) v -> p kt v", p=128), CS(("h", h), CS.t[:, h, :, 0:256]))
                C.dma(O["p_mn"][h].rearrange("(kt p o) -> p kt o", p=128, o=1), CS(("h", h), CS.t[:, h, :, 256:257]), slow=True)
            C.dma(O["p_mm"][:, :], MCAR(None, MCAR.t[0:1, :]))

        if do_rwkv:
            (rwkv2 if cfg.get("rwkv2", True) else rwkv)(tile)
        else:
            for ct in range(8, 16):
                C.memset("pool", mixv(ct, T), 0.0)
        out_proj_ln(I["od_w_out"], tile, VECO, 0, 1)

    for tile in tile_plan(cfg):
        load_x(tile)
        if nlayers >= 1:
            layer0(tile)
        if nlayers >= 2:
            layer1(tile)
        store_y(tile)

    C.emit()
    es.close()
    return nc, C


def make_in_maps(inp, cores, consts):
    maps = []
    f = lambda a: np.ascontiguousarray(a, dtype=np.float32)
    for c in cores:
        s = c % 4
        m = {}
        m["xp"] = f(inp["x_prompt"][s])
        m["meta"] = f(inp["meta_tokens"])
        m["xs"] = f(inp["x_sample"][16 * c:16 * c + 16].reshape(128, D))
        m["s_conv"] = f(inp["state_conv"][0, 16 * c:16 * c + 16].reshape(480, 1024))
        m["s_sre"] = f(inp["state_ssm_re"][0, 16 * c:16 * c + 16])
        m["s_sim"] = f(inp["state_ssm_im"][0, 16 * c:16 * c + 16])
        m.update(consts)
        sl = slice(16 * c, 16 * c + 16)
        m["s_mc"] = f(inp["state_mlstm_c"][0, sl])
        m["s_mn"] = f(inp["state_mlstm_n"][0, sl])
        m["s_mm"] = f(inp["state_mlstm_m"][0, sl])
        m["s_rs"] = f(inp["state_rwkv_s"][0, sl])
        m["s_rsh"] = f(inp["state_rwkv_shift"][0, sl])
        m["od_w_in"] = f(inp["od_w_in"][0])
        for nm in ("m_ig_b", "m_fg_b", "m_hn_g", "r_w0", "r_a0", "r_kk", "r_ka", "r_ln_g", "r_ln_b", "r_rk", "r_mu", "od_ln_g", "od_ln_b"):
            m[nm] = f(inp[nm][0].reshape(1, -1))
        for nm in ("r_w2", "r_a2", "od_w_out"):
            m[nm] = f(inp[nm][0])
        m["ev_w_in"] = f(inp["ev_w_in"][0])
        m["a_conv_w"] = f(inp["a_conv_w"][0])
        for nm in ("a_conv_b", "a_ln_g", "a_ln_b", "s5_d", "s5_log_dt", "s5_glu_b", "ev_ln_g", "ev_ln_b"):
            m[nm] = f(inp[nm][0].reshape(1, -1))
        m["a_pw"] = f(inp["a_pw"][0])
        for nm in ("s5_lambda_re", "s5_lambda_im", "s5_b_re", "s5_b_im", "s5_glu_w", "ev_w_out"):
            m[nm] = f(inp[nm][0])
        m["s5_c_re"] = f(inp["s5_c_re"][0].reshape(1024, 64))
        m["s5_c_im"] = f(inp["s5_c_im"][0].reshape(1024, 64))
        maps.append(m)
    return maps


def kernel(**inp):
    cfg = {}
    nc, C = build(cfg)
    consts = make_consts()
    cores = list(range(NCORES))
    maps = make_in_maps(inp, cores, consts)
    res = run_bass_kernel_spmd(nc, maps, core_ids=cores)
    R = res.results
    B = 4
    cat = lambda k, shp: np.concatenate([R[c][k].reshape((16,) + shp) for c in range(NCORES)], 0)[None]
    stk = lambda k, shp: np.stack([R[c][k].reshape(shp) for c in range(B)], 0)[None]
    y_p = np.stack([R[c]["y_p"] for c in range(B)], 0)
    y_s = np.concatenate([R[c]["y_s"].reshape(16, 8, D) for c in range(NCORES)], 0)
    return (y_p, y_s,
            stk("p_conv", (30, 1024)), stk("p_sre", (64, 64)), stk("p_sim", (64, 64)), stk("p_mc", (4, 256, 256)),
            stk("p_mn", (4, 256)), stk("p_mm", (4,)), stk("p_rs", (16, 64, 64)), stk("p_rsh", (3200,)),
            cat("s_conv_o", (30, 1024)), cat("s_sre_o", (64, 64)), cat("s_sim_o", (64, 64)), cat("s_mc_o", (4, 256, 256)),
            cat("s_mn_o", (4, 256)), cat("s_mm_o", (4,)), cat("s_rs_o", (16, 64, 64)), cat("s_rsh_o", (3200,)))
```

```python
import contextlib
import numpy as np
import concourse.bass as bass
import concourse.mybir as mybir
from concourse.bass_utils import run_bass_kernel_spmd

F32 = mybir.dt.float32
BF16 = mybir.dt.bfloat16
AF = mybir.ActivationFunctionType
ALU = mybir.AluOpType
AX = mybir.AxisListType

D = 2048
TT = 128
WG = 512
KQ = 4
NWS = 4
NSLAB = 200
NCORES = 8
ALPHA = 4 ** 0.25
LN_EPS = 1e-5


class Reg:
    __slots__ = ("w", "r")

    def __init__(self):
        self.w = None
        self.r = []


class Buf:
    def __init__(self, ctx, name, t):
        self.ctx, self.name, self.t = ctx, name, t
        self.regs = {"_all": Reg()}
        self.dma_sem = None
        self.dma_cnt = 0

    def __call__(self, key, ap):
        return View(self, key, ap)

    def all(self):
        return View(self, None, self.t[:])

    def _sel(self, key):
        if key is None:
            return list(self.regs.values())
        if key not in self.regs:
            self.regs[key] = Reg()
        return [self.regs[key], self.regs["_all"]]

    def rdeps(self, key):
        return [r.w for r in self._sel(key) if r.w is not None]

    def wdeps(self, key):
        out = []
        for r in self._sel(key):
            if r.w is not None:
                out.append(r.w)
            out.extend(r.r)
        return out

    def note_read(self, key, tok):
        if key is None:
            for r in self.regs.values():
                r.r.append(tok)
        else:
            self._sel(key)[0].r.append(tok)

    def note_write(self, key, tok):
        if key is None:
            self.regs = {"_all": Reg()}
            self.regs["_all"].w = tok
        else:
            r = self._sel(key)[0]
            r.w = tok
            r.r = []


class View:
    __slots__ = ("buf", "key", "ap")

    def __init__(self, buf, key, ap):
        self.buf, self.key, self.ap = buf, key, ap


class Ctx:
    ENG = ("pe", "act", "dve", "pool", "sp")
    EPOCH = 30000

    def __init__(self, nc, es):
        self.nc, self.es = nc, es
        self.prog = {e: [] for e in self.ENG}
        self.cnt = {e: 0 for e in self.ENG}
        self.sem = {e: es.enter_context(nc.semaphore("sem_" + e)) for e in self.ENG}
        self.known = {e: {} for e in self.ENG}
        self.final = []
        self.total = {}
        self.nsem = 5
        self.nbytes = 0

    def sb(self, name, shape, dtype=F32):
        t = self.es.enter_context(self.nc.sbuf_tensor("sb_" + name, list(shape), dtype))
        n = 4
        for s in shape[1:]:
            n *= s
        self.nbytes += n
        return Buf(self, name, t)

    def ps(self, name, shape, dtype=F32):
        t = self.es.enter_context(self.nc.psum_tensor("ps_" + name, list(shape), dtype))
        return Buf(self, name, t)

    def need(self, e, tok):
        sem, val = tok
        k = id(sem)
        if self.known[e].get(k, 0) >= val:
            return
        self.known[e][k] = val
        self.prog[e].append(("wait", sem, val))

    def op(self, e, fn, reads=(), writes=()):
        for v in reads:
            if isinstance(v, View):
                for tok in v.buf.rdeps(v.key):
                    self.need(e, tok)
        for v in writes:
            if isinstance(v, View):
                for tok in v.buf.wdeps(v.key):
                    self.need(e, tok)
        if self.cnt[e] >= self.EPOCH:
            self.total[e] = self.total.get(e, 0) + self.cnt[e]
            self.sem[e] = self.es.enter_context(self.nc.semaphore("sem_%s_%d" % (e, self.total[e])))
            self.cnt[e] = 0
            self.nsem += 1
        self.cnt[e] += 1
        tok = (self.sem[e], self.cnt[e])
        self.prog[e].append(("op", fn, self.sem[e], 1))
        for v in reads:
            if isinstance(v, View):
                v.buf.note_read(v.key, tok)
        for v in writes:
            if isinstance(v, View):
                v.buf.note_write(v.key, tok)
        return tok

    def dma(self, out, in_, q="sp", slow=False):
        sbv = out if isinstance(out, View) else in_
        b = sbv.buf
        if b.dma_sem is None:
            b.dma_sem = self.es.enter_context(self.nc.semaphore("dq_" + b.name))
            self.nsem += 1
        if isinstance(in_, View):
            for tok in in_.buf.rdeps(in_.key):
                self.need(q, tok)
        if isinstance(out, View):
            for tok in out.buf.wdeps(out.key):
                self.need(q, tok)
        b.dma_cnt += 16
        tok = (b.dma_sem, b.dma_cnt)
        oap = out.ap if isinstance(out, View) else out
        iap = in_.ap if isinstance(in_, View) else in_
        if slow:
            fn = lambda eng, oap=oap, iap=iap: eng.dma_start(out=oap, in_=iap, allow_slow_non_contiguous=True)
        else:
            fn = lambda eng, oap=oap, iap=iap: eng.dma_start(out=oap, in_=iap)
        self.prog[q].append(("op", fn, b.dma_sem, 16))
        if isinstance(in_, View):
            in_.buf.note_read(in_.key, tok)
        if isinstance(out, View):
            out.buf.note_write(out.key, tok)
        else:
            self.final.append(tok)
        return tok

    def tt(self, e, out, in0, in1, op):
        return self.op(e, lambda g: g.tensor_tensor(out=out.ap, in0=in0.ap, in1=in1.ap, op=op),
                       reads=[in0, in1], writes=[out])

    def ts(self, e, out, in0, s1, s2, op0, op1=None):
        rd = [in0] + [s for s in (s1, s2) if isinstance(s, View)]
        a1 = s1.ap if isinstance(s1, View) else s1
        a2 = s2.ap if isinstance(s2, View) else s2
        if op1 is None:
            return self.op(e, lambda g: g.tensor_scalar(out=out.ap, in0=in0.ap, scalar1=a1, scalar2=None, op0=op0),
                           reads=rd, writes=[out])
        return self.op(e, lambda g: g.tensor_scalar(out=out.ap, in0=in0.ap, scalar1=a1, scalar2=a2, op0=op0, op1=op1),
                       reads=rd, writes=[out])

    def stt(self, e, out, in0, s, in1, op0, op1):
        rd = [in0, in1] + ([s] if isinstance(s, View) else [])
        a = s.ap if isinstance(s, View) else s
        return self.op(e, lambda g: g.scalar_tensor_tensor(out=out.ap, in0=in0.ap, scalar=a, in1=in1.ap, op0=op0, op1=op1),
                       reads=rd, writes=[out])

    def act(self, out, in_, func, bias=None, scale=None, e="act"):
        rd = [in_] + [s for s in (bias, scale) if isinstance(s, View)]
        kw = {}
        if bias is not None:
            kw["bias"] = bias.ap if isinstance(bias, View) else bias
        if scale is not None:
            kw["scale"] = scale.ap if isinstance(scale, View) else scale
        return self.op(e, lambda g: g.activation(out=out.ap, in_=in_.ap, func=func, **kw), reads=rd, writes=[out])

    def copy(self, e, out, in_):
        if e == "act":
            return self.act(out, in_, AF.Copy)
        return self.op(e, lambda g: g.tensor_copy(out=out.ap, in_=in_.ap), reads=[in_], writes=[out])

    def memset(self, e, out, val):
        return self.op(e, lambda g: g.memset(out.ap, val), writes=[out])

    def mm(self, out, lhsT, rhs, start=True, stop=True):
        return self.op("pe", lambda g: g.matmul(out.ap, lhsT=lhsT.ap, rhs=rhs.ap, start=start, stop=stop),
                       reads=[lhsT, rhs], writes=[out])

    def tr(self, out, in_, ident):
        return self.op("pe", lambda g: g.transpose(out.ap, in_.ap, ident.ap), reads=[in_, ident], writes=[out])

    def emit(self):
        nc = self.nc
        for tok in self.final:
            self.need("sp", tok)
        for e in self.ENG:
            if e != "sp" and self.cnt[e] > 0:
                self.need("sp", (self.sem[e], self.cnt[e]))
        engs = {"pe": "tensor", "act": "scalar", "dve": "vector", "pool": "gpsimd", "sp": "sync"}
        with nc.Block() as block:
            for e, attr in engs.items():
                items = self.prog[e]

                def body(eng, items=items):
                    for it in items:
                        if it[0] == "wait":
                            eng.wait_ge(it[1], it[2])
                        else:
                            it[1](eng).then_inc(it[2], it[3])

                getattr(block, attr)(body)


def make_consts():
    c = {}
    c["ident"] = np.eye(128, dtype=np.float32)
    c["ones"] = np.ones((128, 128), dtype=np.float32)
    r = np.arange(128)
    mrow = (r % 32) // 16
    mcol = np.arange(128) // 64
    bm = (mrow[:, None] == mcol[None, :]).astype(np.float32)
    ev_r = ((r // 32) % 2 == 0).astype(np.float32)
    c["bmask"] = bm * ev_r[:, None]
    c["bmask_o"] = bm * (1 - ev_r)[:, None]
    gl = np.arange(128) // 16
    cm = ((gl[None, :] % 2) == (r[:, None] // 64)).astype(np.float32)
    c["cmask"] = cm * ev_r[None, :]
    c["cmask_o"] = cm * (1 - ev_r)[None, :]
    BIG = 1e30
    i128 = np.arange(128)
    allow_p = (i128[:, None] <= i128[None, :])
    c["maskbig_p"] = np.where(allow_p, 0.0, BIG).astype(np.float32)
    c["segtri_p"] = allow_p.astype(np.float32)
    i64 = np.arange(64)
    allow_s = (i64[:, None] <= i64[None, :]) & ((i64[:, None] // 8) == (i64[None, :] // 8))
    c["maskbig_s"] = np.where(allow_s, 0.0, BIG).astype(np.float32)
    c["segtri_s"] = allow_s.astype(np.float32)
    r01p = np.ones((128, 128), np.float32); r01p[:, 0] = 0
    r01s = np.ones((128, 64), np.float32); r01s[:, ::8] = 0
    c["r01_p"], c["r01_s"] = r01p, r01s
    c["rneg_p"] = ((1 - r01p) * -BIG).astype(np.float32)
    c["rneg_s"] = ((1 - r01s) * -BIG).astype(np.float32)
    segsel = ((i64[None, :] // 8) == np.arange(8)[:, None]).astype(np.float32)
    c["segsel"] = segsel
    c["segc"] = np.ascontiguousarray(segsel.T)
    c["segrow"] = np.ascontiguousarray(np.broadcast_to(segsel.reshape(1, 512), (128, 512))).astype(np.float32)
    c["blk"] = ((i128[:, None] // 64) == (i128[None, :] // 64)).astype(np.float32)
    st_p = (i128[:, None] < i128[None, :])
    st_s = (i64[:, None] < i64[None, :]) & ((i64[:, None] // 8) == (i64[None, :] // 8))
    c["sstri_p"] = st_p.astype(np.float32)
    c["sstriT_p"] = np.ascontiguousarray(st_p.T).astype(np.float32)
    c["sstri_s"] = st_s.astype(np.float32)
    c["sstriT_s"] = np.ascontiguousarray(st_s.T).astype(np.float32)
    return c


CONST_SHAPES = {"ident": [128, 128], "ones": [128, 128], "bmask": [128, 128], "cmask": [128, 128],
                "bmask_o": [128, 128], "cmask_o": [128, 128],
                "maskbig_p": [128, 128], "segtri_p": [128, 128], "maskbig_s": [64, 64], "segtri_s": [64, 64],
                "r01_p": [128, 128], "r01_s": [128, 64], "rneg_p": [128, 128], "rneg_s": [128, 64],
                "segsel": [8, 64], "segc": [64, 8], "segrow": [128, 512], "blk": [128, 128],
                "sstri_p": [128, 128], "sstriT_p": [128, 128], "sstri_s": [64, 64], "sstriT_s": [64, 64]}


def tile_plan(cfg):
    tiles = []
    npt = cfg.get("n_prompt_tiles", 17)
    pos = 0
    for i in range(17):
        T = 16 if i == 16 else TT
        if i < npt:
            tiles.append(dict(kind="p", T=T, pos=pos, nseq=1, L=T, first=(i == 0), last=(i == npt - 1), s0=0))
        pos += T
    if cfg.get("sample", True):
        for h in range(2):
            tiles.append(dict(kind="s", T=64, pos=0, nseq=8, L=8, first=True, last=True, s0=8 * h))
    return tiles


def build(cfg):
    nc = bass.Bass("TRN2", target_bir_lowering=False)
    es = contextlib.ExitStack()
    C = Ctx(nc, es)
    dbg = cfg.get("debug", False)
    nlayers = cfg.get("layers", 2)

    def din(name, shape):
        return nc.dram_tensor(name, list(shape), F32, kind="ExternalInput").ap()

    def dout(name, shape):
        return nc.dram_tensor(name, list(shape), F32, kind="ExternalOutput").ap()

    I = {}
    I["xp"] = din("xp", [2048, D])
    I["meta"] = din("meta", [16, D])
    I["xs"] = din("xs", [128, D])
    I["s_conv"] = din("s_conv", [16 * 30, 1024])
    I["s_sre"] = din("s_sre", [16, 64, 64])
    I["s_sim"] = din("s_sim", [16, 64, 64])
    for nm, shp in CONST_SHAPES.items():
        I[nm] = din(nm, shp)
    I["s_mc"] = din("s_mc", [16, 4, 256, 256])
    I["s_mn"] = din("s_mn", [16, 4, 256])
    I["s_mm"] = din("s_mm", [16, 4])
    I["s_rs"] = din("s_rs", [16, 16, 64, 64])
    I["s_rsh"] = din("s_rsh", [16, 3200])
    I["od_w_in"] = din("od_w_in", [D, 9352])
    I["m_ig_b"] = din("m_ig_b", [1, 4])
    I["m_fg_b"] = din("m_fg_b", [1, 4])
    for nm in ("m_hn_g", "r_w0", "r_a0", "r_kk", "r_ka", "r_ln_g", "r_ln_b", "r_rk"):
        I[nm] = din(nm, [1, 1024])
    I["r_mu"] = din("r_mu", [1, 3200])
    I["r_w2"] = din("r_w2", [64, 1024])
    I["r_a2"] = din("r_a2", [64, 1024])
    I["od_w_out"] = din("od_w_out", [2048, D])
    I["od_ln_g"] = din("od_ln_g", [1, D])
    I["od_ln_b"] = din("od_ln_b", [1, D])
    I["ev_w_in"] = din("ev_w_in", [D, 5120])
    I["a_conv_w"] = din("a_conv_w", [31, 1024])
    for nm in ("a_conv_b", "a_ln_g", "a_ln_b", "s5_d"):
        I[nm] = din(nm, [1, 1024])
    I["a_pw"] = din("a_pw", [1024, 1024])
    I["s5_lambda_re"] = din("s5_lambda_re", [64, 64])
    I["s5_lambda_im"] = din("s5_lambda_im", [64, 64])
    I["s5_log_dt"] = din("s5_log_dt", [1, 64])
    I["s5_b_re"] = din("s5_b_re", [64, 64, 16])
    I["s5_b_im"] = din("s5_b_im", [64, 64, 16])
    I["s5_c_re"] = din("s5_c_re", [1024, 64])
    I["s5_c_im"] = din("s5_c_im", [1024, 64])
    I["s5_glu_w"] = din("s5_glu_w", [1024, 2048])
    I["s5_glu_b"] = din("s5_glu_b", [1, 2048])
    I["ev_w_out"] = din("ev_w_out", [2048, D])
    I["ev_ln_g"] = din("ev_ln_g", [1, D])
    I["ev_ln_b"] = din("ev_ln_b", [1, D])

    O = {}
    O["y_p"] = dout("y_p", [2048, D])
    O["y_s"] = dout("y_s", [128, D])
    O["p_conv"] = dout("p_conv", [30, 1024])
    O["p_sre"] = dout("p_sre", [64, 64])
    O["p_sim"] = dout("p_sim", [64, 64])
    O["s_conv_o"] = dout("s_conv_o", [16 * 30, 1024])
    O["s_sre_o"] = dout("s_sre_o", [16, 64, 64])
    O["s_sim_o"] = dout("s_sim_o", [16, 64, 64])
    O["p_mc"] = dout("p_mc", [4, 256, 256])
    O["p_mn"] = dout("p_mn", [4, 256])
    O["p_mm"] = dout("p_mm", [1, 4])
    O["p_rs"] = dout("p_rs", [16, 64, 64])
    O["p_rsh"] = dout("p_rsh", [1, 3200])
    O["s_mc_o"] = dout("s_mc_o", [16, 4, 256, 256])
    O["s_mn_o"] = dout("s_mn_o", [16, 4, 256])
    O["s_mm_o"] = dout("s_mm_o", [16, 4])
    O["s_rs_o"] = dout("s_rs_o", [16, 16, 64, 64])
    O["s_rsh_o"] = dout("s_rsh_o", [16, 3200])

    ident = C.sb("ident", [128, 128])
    ones = C.sb("ones", [128, 128])
    XT = C.sb("XT", [128, 16, TT])
    XS = C.sb("XS", [128, D])
    PJ = C.sb("PJ", [128, 33, TT])
    MIX = C.sb("MIX", [128, 16 * TT], BF16)
    XTB = C.sb("XTB", [128, 16, TT], BF16)
    GELB = C.sb("GELB", [128, 8, TT], BF16)
    WS = [C.sb("WS%d" % i, [128, KQ, WG], BF16) for i in range(NWS)]
    HB = C.sb("HB", [128, 8, 304])
    HC = C.sb("HC", [128, 8, 30])
    ACC = C.sb("ACC", [128, 8, TT])
    SQ2 = C.sb("SQ2", [128, 2, TT])
    ST = C.sb("ST", [128, 3, TT])
    VEC0 = C.sb("VEC0", [128, 8, 40])
    VECG = C.sb("VECG", [128, 16, 4])
    TMP = C.sb("TMP", [128, 128])
    XR = C.sb("XR", [128, 32, 129])
    XI = C.sb("XI", [128, 32, 129])
    S5C = C.sb("S5C", [128, 2, 32])
    S5A = C.sb("S5A", [128, 8, 32])
    S5T = C.sb("S5T", [128, 10, 32])
    SCS = C.sb("SCS", [128, 2, 32, 8])
    BRE = [C.sb("BRE%d" % i, [128, 8, 128]) for i in range(2)]
    BIM = [C.sb("BIM%d" % i, [128, 8, 128]) for i in range(2)]
    CRE = [C.sb("CRE%d" % i, [128, 8, 128]) for i in range(2)]
    CIM = [C.sb("CIM%d" % i, [128, 8, 128]) for i in range(2)]
    mask_bo = C.sb("mask_bo", [128, 128])
    mask_co = C.sb("mask_co", [128, 128])
    S5ST = C.sb("S5ST", [128, 128])
    mask_b = C.sb("mask_b", [128, 128])
    mask_c = C.sb("mask_c", [128, 128])

    PSA = [C.ps("PSA%d" % i, [128, 512]) for i in range(4)]
    PSB = [C.ps("PSB%d" % i, [128, 512]) for i in range(4)]

    def mixv(ct, T):
        return MIX(("t", ct), MIX.t[:, ct * TT:ct * TT + T])

    C.dma(ident.all(), I["ident"])
    C.dma(ones.all(), I["ones"])
    C.dma(mask_b.all(), I["bmask"])
    C.dma(mask_c.all(), I["cmask"])
    C.dma(mask_bo.all(), I["bmask_o"])
    C.dma(mask_co.all(), I["cmask_o"])

    rr = {"psa": 0, "psb": 0, "ws": 0, "ev": 0, "sq": 0, "tm": 0}

    def next_psa():
        rr["psa"] = (rr["psa"] + 1) % 4
        return PSA[rr["psa"]]

    def next_psb():
        rr["psb"] = (rr["psb"] + 1) % 4
        return PSB[rr["psb"]]

    def ev_eng():
        rr["ev"] += 1
        return "dve" if rr["ev"] % 2 else "act"

    def load_rows_T(dst, rows, ncols, col0=0, ct0=0):
        r0 = 0
        for ap in rows:
            nr = ap.shape[0]
            C.dma(XS(None, XS.t[r0:r0 + nr, 0:ncols]), ap)
            r0 += nr
        nr = r0
        for ct in range(ncols // 128):
            ps = next_psb()
            C.tr(ps(None, ps.t[:, 0:nr]), XS(None, XS.t[0:nr, ct * 128:(ct + 1) * 128]), ident(None, ident.t[0:nr, 0:nr]))
            C.copy("dve", dst(None, dst.t[:, ct0 + ct, col0:col0 + nr]), ps(None, ps.t[:, 0:nr]))

    load_rows_T(VEC0, [I["a_conv_w"], I["a_conv_b"], I["a_ln_g"], I["a_ln_b"], I["s5_d"]], 1024)
    load_rows_T(VECG, [I["s5_glu_b"], I["ev_ln_g"], I["ev_ln_b"]], 2048)

    def gp_ap(ap2d):
        return ap2d.rearrange("(q m) p -> (m p) q", m=2)

    LR, LI, DT, AR, AI, NAI = range(6)
    A = lambda k: S5A(None, S5A.t[:, k, :])
    Tm = lambda k: S5T(None, S5T.t[:, k, :])
    for m in range(2):
        C.dma(S5A(None, S5A.t[64 * m:64 * m + 64, LR, :]), I["s5_lambda_re"].rearrange("(q m) p -> m p q", m=2)[m], slow=True)
        C.dma(S5A(None, S5A.t[64 * m:64 * m + 64, LI, :]), I["s5_lambda_im"].rearrange("(q m) p -> m p q", m=2)[m], slow=True)
    ldt = I["s5_log_dt"]
    for m in range(2):
        src = bass.AP(ldt.tensor, ldt.offset + m, [[0, 64], [2, 32]])
        C.dma(S5A(None, S5A.t[64 * m:64 * m + 64, DT, :]), src, slow=True)
    C.act(A(DT), A(DT), AF.Exp)
    C.tt("dve", Tm(0), A(LR), A(DT), ALU.mult)
    C.act(Tm(1), Tm(0), AF.Exp)
    C.tt("dve", Tm(2), A(LI), A(DT), ALU.mult)
    PI = float(np.pi)
    C.ts("dve", Tm(3), Tm(2), 1.0 / 32, None, ALU.mult)
    C.act(Tm(4), Tm(3), AF.Sin)
    C.ts("dve", Tm(3), Tm(3), PI / 2, None, ALU.add)
    C.act(Tm(5), Tm(3), AF.Sin)
    for _ in range(5):
        C.tt("dve", Tm(3), Tm(4), Tm(5), ALU.mult)
        C.tt("dve", Tm(8), Tm(5), Tm(5), ALU.mult)
        C.tt("dve", Tm(9), Tm(4), Tm(4), ALU.mult)
        C.ts("dve", Tm(4), Tm(3), 2.0, None, ALU.mult)
        C.tt("dve", Tm(5), Tm(8), Tm(9), ALU.subtract)
    C.tt("dve", A(AR), Tm(1), Tm(5), ALU.mult)
    C.tt("dve", A(AI), Tm(1), Tm(4), ALU.mult)
    C.ts("dve", A(NAI), A(AI), -1.0, None, ALU.mult)
    MAG = 6
    C.copy("dve", A(MAG), Tm(1))
    s5tab = nc.dram_tensor("s5tab", [2, 128, 1024], F32).ap()
    tC = lambda a, b: XR(None, XR.t[:, :, a:b])
    tS = lambda a, b: XI(None, XI.t[:, :, a:b])
    C.copy("dve", tC(0, 1), S5T(None, S5T.t[:, 5, :].unsqueeze(2)))
    C.copy("dve", tS(0, 1), S5T(None, S5T.t[:, 4, :].unsqueeze(2)))
    n_ = 1
    while n_ < 32:
        cn = XR(None, XR.t[:, :, n_ - 1:n_].broadcast_to([128, 32, n_]))
        sn = XI(None, XI.t[:, :, n_ - 1:n_].broadcast_to([128, 32, n_]))
        u1 = XR(None, XR.t[:, :, 64:64 + n_])
        u2 = XI(None, XI.t[:, :, 64:64 + n_])
        C.tt("dve", u1, tS(0, n_), sn, ALU.mult)
        C.tt("dve", tC(n_, 2 * n_), tC(0, n_), cn, ALU.mult)
        C.tt("dve", tC(n_, 2 * n_), tC(n_, 2 * n_), u1, ALU.subtract)
        C.tt("dve", u2, tC(0, n_), sn, ALU.mult)
        C.tt("dve", tS(n_, 2 * n_), tS(0, n_), cn, ALU.mult)
        C.tt("dve", tS(n_, 2 * n_), tS(n_, 2 * n_), u2, ALU.add)
        n_ *= 2
    tab_tok = [C.dma(s5tab[0].rearrange("p (q t) -> p q t", t=32), tC(0, 32)),
               C.dma(s5tab[1].rearrange("p (q t) -> p q t", t=32), tS(0, 32))]
    for tk in tab_tok:
        C.need("sp", tk)
    C.tt("dve", Tm(0), A(LR), A(LR), ALU.mult)
    C.tt("dve", Tm(1), A(LI), A(LI), ALU.mult)
    C.tt("dve", Tm(0), Tm(0), Tm(1), ALU.add)
    C.op("dve", lambda g: g.reciprocal(out=S5T.t[:, 0, :], in_=S5T.t[:, 0, :]), reads=[Tm(0)], writes=[Tm(0)])
    C.ts("dve", Tm(1), A(AR), -1.0, None, ALU.add)
    C.tt("dve", Tm(2), Tm(1), A(LR), ALU.mult)
    C.tt("dve", Tm(3), A(AI), A(LI), ALU.mult)
    C.tt("dve", Tm(2), Tm(2), Tm(3), ALU.add)
    C.tt("dve", Tm(6), Tm(2), Tm(0), ALU.mult)
    C.tt("dve", Tm(2), A(AI), A(LR), ALU.mult)
    C.tt("dve", Tm(3), Tm(1), A(LI), ALU.mult)
    C.tt("dve", Tm(2), Tm(2), Tm(3), ALU.subtract)
    C.tt("dve", Tm(7), Tm(2), Tm(0), ALU.mult)
    braw_t = PJ.t[:, 0:8, :].rearrange("p a b -> p (a b)").rearrange("p (k q h) -> p k q h", k=2, q=32)
    bb_t = PJ.t[:, 8:24, :].rearrange("p a b -> p (a b)").rearrange("p (k q h) -> p k q h", k=2, q=32)
    sc_t = PJ.t[:, 24:28, :].rearrange("p a b -> p (a b)").rearrange("p (q h) -> p q h", q=32)
    for k, nm in enumerate(("s5_b_re", "s5_b_im")):
        for m in range(2):
            C.dma(PJ(None, braw_t[64 * m:64 * m + 64, k, :, :]), I[nm].rearrange("(q m) p h -> m p q h", m=2)[m])
    qr_b = S5T(None, S5T.t[:, 6, :].unsqueeze(2).broadcast_to([128, 32, 16]))
    qi_b = S5T(None, S5T.t[:, 7, :].unsqueeze(2).broadcast_to([128, 32, 16]))
    br = PJ(None, braw_t[:, 0, :, :])
    bi = PJ(None, braw_t[:, 1, :, :])
    sc = PJ(None, sc_t)
    for dup in range(2):
        o_r = PJ(None, bb_t[:, 0, :, dup * 16:(dup + 1) * 16])
        o_i = PJ(None, bb_t[:, 1, :, dup * 16:(dup + 1) * 16])
        C.tt("dve", o_r, br, qr_b, ALU.mult)
        C.tt("dve", sc, bi, qi_b, ALU.mult)
        C.tt("dve", o_r, o_r, sc, ALU.subtract)
        C.tt("dve", o_i, bi, qr_b, ALU.mult)
        C.tt("dve", sc, br, qi_b, ALU.mult)
        C.tt("dve", o_i, o_i, sc, ALU.add)
    for k, dst in enumerate((BRE, BIM)):
        for gt in range(8):
            ps = next_psb()
            C.copy("dve", TMP.all(), PJ(None, bb_t[:, k, 4 * gt:4 * gt + 4, :]))
            C.tr(ps(None, ps.t[:, 0:128]), TMP.all(), ident.all())
            C.tt("dve", dst[0](None, dst[0].t[:, gt, :]), ps(None, ps.t[:, 0:128]), mask_b.all(), ALU.mult)
            C.tt("dve", dst[1](None, dst[1].t[:, gt, :]), ps(None, ps.t[:, 0:128]), mask_bo.all(), ALU.mult)
    for k, (nm, dst) in enumerate((("s5_c_re", CRE), ("s5_c_im", CIM))):
        for gt in range(8):
            for dup in range(2):
                C.dma(XS(None, XS.t[:, dup * 64:(dup + 1) * 64]), I[nm][gt * 128:(gt + 1) * 128, :])
            ps = next_psb()
            C.tr(ps(None, ps.t[:, 0:128]), XS(None, XS.t[:, 0:128]), ident.all())
            for par, mk in enumerate((mask_c, mask_co)):
                C.stt("dve", dst[par](None, dst[par].t[:, gt, :]), ps(None, ps.t[:, 0:128]), 1.0 if k == 0 else -1.0,
                      mk.all(), ALU.mult, ALU.mult)

    WSCR = nc.dram_tensor("wscr", [NSLAB, 128, KQ * WG], BF16).ap()
    wst = {"id": 0, "first": True, "tok": {}}

    def load_slab(ws, src, nk, gw):
        sid = wst["id"]
        wst["id"] += 1
        assert sid < NSLAB
        dstv = ws(None, ws.t[:, 0:nk, 0:gw])
        scr = WSCR[sid][:, 0:nk * gw].rearrange("p (k c) -> p k c", k=nk)
        if wst["first"]:
            C.dma(dstv, src, q="pool")
            wst["tok"][sid] = C.dma(scr, dstv, q="sp")
        else:
            C.need("sp", wst["tok"][sid])
            C.dma(dstv, scr, q="sp")

    def stream_mm(W, nkt, coltiles, rhs_fn, T, evac):
        groups = []
        cur = []
        for ct in coltiles:
            if cur and (ct[0] + ct[1] - cur[0][0] > WG):
                groups.append(cur)
                cur = []
            cur.append(ct)
        if cur:
            groups.append(cur)
        Wv = W.rearrange("(kt p) c -> p kt c", p=128)
        j = 0
        for grp in groups:
            g0 = grp[0][0]
            gw = grp[-1][0] + grp[-1][1] - g0
            ps = next_psa()
            pacc = ps(None, ps.t[0:T, 0:gw])
            for kq in range(0, nkt, KQ):
                ws = WS[rr["ws"] % NWS]
                rr["ws"] += 1
                nk = min(KQ, nkt - kq)
                load_slab(ws, Wv[:, kq:kq + nk, g0:g0 + gw], nk, gw)
                for k in range(nk):
                    kt = kq + k
                    C.mm(pacc, rhs_fn(kt), ws(None, ws.t[:, k, 0:gw]), start=(kt == 0), stop=(kt == nkt - 1))
            i_ = rr["tm"] % 2
            rr["tm"] += 1
            C.copy(ev_eng(), XS(("tm", i_), XS.t[0:T, i_ * 512:i_ * 512 + gw]), pacc)
            for (c0, w) in grp:
                pt = next_psb()
                C.tr(pt(None, pt.t[0:w, 0:T]), XS(("tm", i_), XS.t[0:T, i_ * 512 + c0 - g0:i_ * 512 + c0 - g0 + w]),
                     ident(None, ident.t[0:T, 0:T]))
                evac(j, pt(None, pt.t[0:w, 0:T]))
                j += 1

    def layer_norm_cols(src, ntile, T, gcol, bcol, vec, func=AF.Identity, eps=LN_EPS, dst_fn=None, also_fn=None):
        nch = ntile * 128
        ps = next_psb()
        ps2 = next_psb()
        for ct in range(ntile):
            sv = src(("t", ct), src.t[:, ct, 0:T])
            sq = SQ2(("s", rr["sq"] % 2), SQ2.t[:, rr["sq"] % 2, 0:T])
            rr["sq"] += 1
            C.act(sq, sv, AF.Square)
            C.mm(ps(None, ps.t[:, 0:T]), ones.all(), sv, start=(ct == 0), stop=(ct == ntile - 1))
            C.mm(ps2(None, ps2.t[:, 0:T]), ones.all(), sq, start=(ct == 0), stop=(ct == ntile - 1))
        st = lambda k: ST(None, ST.t[:, k, 0:T])
        C.ts("dve", st(0), ps(None, ps.t[:, 0:T]), 1.0 / nch, None, ALU.mult)
        C.ts("dve", st(1), ps2(None, ps2.t[:, 0:T]), 1.0 / nch, None, ALU.mult)
        C.tt("dve", st(2), st(0), st(0), ALU.mult)
        C.tt("dve", st(1), st(1), st(2), ALU.subtract)
        C.ts("dve", st(1), st(1), eps, None, ALU.add)
        C.act(st(1), st(1), AF.Sqrt)
        C.op("dve", lambda g, T=T: g.reciprocal(out=ST.t[:, 1, 0:T], in_=ST.t[:, 1, 0:T]), reads=[st(1)], writes=[st(1)])
        C.tt("dve", st(2), st(0), st(1), ALU.mult)
        C.ts("dve", st(2), st(2), -1.0, None, ALU.mult)
        for ct in range(ntile):
            e = "dve" if ct % 2 == 0 else "pool"
            sv = src(("t", ct), src.t[:, ct, 0:T])
            C.tt(e, sv, sv, st(1), ALU.mult)
            C.tt(e, sv, sv, st(2), ALU.add)
            dv = dst_fn(ct) if dst_fn is not None else sv
            C.act(dv, sv, func, bias=vec(None, vec.t[:, ct, bcol:bcol + 1]), scale=vec(None, vec.t[:, ct, gcol:gcol + 1]))
            if also_fn is not None:
                C.copy("pool" if ct % 2 else "act", also_fn(ct), dv)

    def load_x(tile):
        n = tile["T"]
        if tile["kind"] == "s":
            C.dma(XS(None, XS.t[0:n, :]), I["xs"][tile["s0"] * 8:tile["s0"] * 8 + n, :])
        else:
            p0 = tile["pos"]
            r = 0
            if p0 < 16:
                C.dma(XS(None, XS.t[0:16, :]), I["meta"][:, :])
                r = 16
            x0 = p0 + r - 16
            C.dma(XS(None, XS.t[r:n, :]), I["xp"][x0:x0 + n - r, :])
        for dt_ in range(16):
            ps = next_psb()
            C.tr(ps(None, ps.t[:, 0:n]), XS(None, XS.t[0:n, dt_ * 128:(dt_ + 1) * 128]), ident(None, ident.t[0:n, 0:n]))
            C.copy("act" if dt_ % 2 else "dve", XT(("t", dt_), XT.t[:, dt_, 0:n]), ps(None, ps.t[:, 0:n]))
            C.copy("pool", XTB(("t", dt_), XTB.t[:, dt_, 0:n]), XT(("t", dt_), XT.t[:, dt_, 0:n]))

    def store_y(tile):
        n = tile["T"]
        for dt_ in range(16):
            ps = next_psb()
            C.tr(ps(None, ps.t[0:n, 0:128]), XT(("t", dt_), XT.t[:, dt_, 0:n]), ident.all())
            C.copy("act" if dt_ % 2 else "dve", XS(None, XS.t[0:n, dt_ * 128:(dt_ + 1) * 128]), ps(None, ps.t[0:n, 0:128]))
        if tile["kind"] == "s":
            C.dma(O["y_s"][tile["s0"] * 8:tile["s0"] * 8 + n, :], XS(None, XS.t[0:n, :]))
        else:
            p0 = tile["pos"]
            r = 16 if p0 < 16 else 0
            x0 = p0 + r - 16
            C.dma(O["y_p"][x0:x0 + n - r, :], XS(None, XS.t[r:n, :]))

    def out_proj_ln(W, tile, vec, gcol, bcol):
        T = tile["T"]

        def evac(j, pv):
            xv_ = XT(("t", j), XT.t[:, j, 0:T])
            C.stt("dve", xv_, xv_, ALPHA, pv, ALU.mult, ALU.add)

        stream_mm(W, 16, [(i * 128, 128) for i in range(16)], lambda kt: mixv(kt, T), T, evac)
        layer_norm_cols(XT, 16, T, gcol, bcol, vec, also_fn=lambda ct: XTB(("t", ct), XTB.t[:, ct, 0:T]))

    def layer0(tile):
        T, nseq, L = tile["T"], tile["nseq"], tile["L"]
        is_s = tile["kind"] == "s"
        s0 = tile["s0"]
        xrhs = lambda kt: XTB(("t", kt), XTB.t[:, kt, 0:T])
        pj = lambda j: PJ(("t", j), PJ.t[:, j, 0:T])

        fence(HB, HB.t[0:1, 0, 0:1])
        def evacA(j, pv):
            if j < 8:
                C.copy(ev_eng(), pj(j), pv)
            elif j < 16:
                C.act(pj(j), pv, AF.Sigmoid)
            else:
                C.act(pj(j), pv, AF.Silu)

        stream_mm(I["ev_w_in"], 16, [(i * 128, 128) for i in range(24)], xrhs, T, evacA)

        W_ = 30 + L

        def hb(ct, a, b):
            v = HB.t[:, ct, 0:nseq * W_].rearrange("p (n w) -> p n w", w=W_)[:, :, a:b]
            return HB(("t", ct), v)

        def tokv(buf, ct):
            return buf(("t", ct), buf.t[:, ct, 0:T].rearrange("p (n l) -> p n l", l=L))

        if is_s:
            for q in range(2):
                C.dma(XS(None, XS.t[0:120, 0:1024]), I["s_conv"][s0 * 30 + q * 120:s0 * 30 + (q + 1) * 120, :])
                for ct in range(8):
                    ps = next_psb()
                    C.tr(ps(None, ps.t[:, 0:120]), XS(None, XS.t[0:120, ct * 128:(ct + 1) * 128]), ident(None, ident.t[0:120, 0:120]))
                    dstv = HB.t[:, ct, 0:nseq * W_].rearrange("p (n w) -> p n w", w=W_)[:, 4 * q:4 * q + 4, 0:30]
                    C.copy("dve", HB(("t", ct), dstv), ps(None, ps.t[:, 0:120].rearrange("p (n r) -> p n r", r=30)))
        else:
            for ct in range(8):
                if tile["first"]:
                    C.memset("pool", hb(ct, 0, 30), 0.0)
                else:
                    C.copy("pool", hb(ct, 0, 30), HC(("t", ct), HC.t[:, ct, :].unsqueeze(1)))
        for ct in range(8):
            e = "dve" if ct % 2 == 0 else "pool"
            C.tt(e, hb(ct, 30, 30 + L), tokv(PJ, ct), tokv(PJ, 8 + ct), ALU.mult)
        for ct in range(8):
            e = "dve"
            acc = tokv(ACC, ct)
            wcol = lambda j, ct=ct: VEC0(None, VEC0.t[:, ct, j:j + 1])
            C.ts(e, acc, hb(ct, 0, L), wcol(0), wcol(31), ALU.mult, ALU.add)
            for j in range(1, 31):
                C.stt(e, acc, hb(ct, j, j + L), wcol(j), acc, ALU.mult, ALU.add)
        if is_s:
            for q in range(2):
                for ct in range(8):
                    ps = next_psb()
                    srcv = HB.t[:, ct, 0:nseq * W_].rearrange("p (n w) -> p n w", w=W_)[:, 4 * q:4 * q + 4, L:L + 30]
                    C.copy("pool", TMP(None, TMP.t[:, 0:120].rearrange("p (n r) -> p n r", r=30)), HB(("t", ct), srcv))
                    C.tr(ps(None, ps.t[0:120, 0:128]), TMP(None, TMP.t[:, 0:120]), ident.all())
                    C.copy("dve", XS(None, XS.t[0:120, ct * 128:(ct + 1) * 128]), ps(None, ps.t[0:120, 0:128]))
                C.dma(O["s_conv_o"][s0 * 30 + q * 120:s0 * 30 + (q + 1) * 120, :], XS(None, XS.t[0:120, 0:1024]))
        else:
            for ct in range(8):
                C.copy("pool", TMP(None, TMP.t[:, 0:30]), HB(("t", ct), HB.t[:, ct, L:L + 30]))
                C.copy("pool", HC(("t", ct), HC.t[:, ct, :]), TMP(None, TMP.t[:, 0:30]))
            if tile["last"]:
                for ct in range(8):
                    ps = next_psb()
                    C.tr(ps(None, ps.t[0:30, 0:128]), HC(("t", ct), HC.t[:, ct, :]), ident.all())
                    C.copy("dve", XS(None, XS.t[0:30, ct * 128:(ct + 1) * 128]), ps(None, ps.t[0:30, 0:128]))
                C.dma(O["p_conv"][:, :], XS(None, XS.t[0:30, 0:1024]))
        layer_norm_cols(ACC, 8, T, 32, 33, VEC0, func=AF.Silu, dst_fn=lambda ct: mixv(8 + ct, T))

        def evac_pw(j, pv):
            C.tt("dve", mixv(j, T), pv, pj(16 + j), ALU.mult)

        stream_mm(I["a_pw"], 8, [(i * 128, 128) for i in range(8)], lambda kt: mixv(8 + kt, T), T, evac_pw)

        def evacB(j, pv):
            if j < 8:
                C.copy(ev_eng(), pj(j), pv)
            else:
                C.act(pj(j), pv, AF.Silu)

        stream_mm(I["ev_w_in"], 16, [(3072 + i * 128, 128) for i in range(16)], xrhs, T, evacB)

        Wx = 1 + L

        def xv(buf, a, b, p0=0, p1=32):
            v = buf.t[:, p0:p1, 0:nseq * Wx].rearrange("p q (n w) -> p q n w", w=Wx)[:, :, :, a:b]
            return buf(None, v)

        if is_s:
            for k, (nm, buf) in enumerate((("s_sre", XR), ("s_sim", XI))):
                for q in range(2):
                    for pr in range(16):
                        g0 = 2 * (16 * q + pr)
                        C.dma(S5ST(None, S5ST.t[pr * 8:(pr + 1) * 8, :]),
                              I[nm][s0:s0 + 8, g0:g0 + 2, :].rearrange("n m p -> n (m p)"))
                    ps = next_psb()
                    C.tr(ps(None, ps.t[:, 0:128]), S5ST.all(), ident.all())
                    C.copy("dve", xv(buf, 0, 1, 16 * q, 16 * q + 16),
                           ps(None, ps.t[:, 0:128].rearrange("p (q n o) -> p q n o", n=8, o=1)))
        else:
            for k, buf in enumerate((XR, XI)):
                if tile["first"]:
                    C.memset("pool", xv(buf, 0, 1), 0.0)
                else:
                    C.copy("pool", xv(buf, 0, 1), S5C(None, S5C.t[:, k, :].unsqueeze(2).unsqueeze(3)))
        for q4 in range(8):
            for k, (tab, buf) in enumerate(((BRE, XR), (BIM, XI))):
                ps = next_psa()
                for ip in range(4):
                    hf = ip // 2
                    tb = tab[ip % 2]
                    C.mm(ps(None, ps.t[:, ip * T:(ip + 1) * T]),
                         tb(None, tb.t[64 * hf:64 * hf + 64, q4, :]),
                         PJ(("t", q4), PJ.t[64 * hf:64 * hf + 64, q4, 0:T]))
                C.copy("act", xv(buf, 1, Wx, 4 * q4, 4 * q4 + 4),
                       ps(None, ps.t[:, 0:4 * T].rearrange("p (q n l) -> p q n l", q=4, l=L)))
        arb = S5A(None, S5A.t[:, AR, :].unsqueeze(2).broadcast_to([128, 32, nseq]))
        aib = S5A(None, S5A.t[:, AI, :].unsqueeze(2).broadcast_to([128, 32, nseq]))
        naib = S5A(None, S5A.t[:, NAI, :].unsqueeze(2).broadcast_to([128, 32, nseq]))
        if nseq == 1:
            t1v = S5T(None, S5T.t[:, 8, :].unsqueeze(2))
            t2v = S5T(None, S5T.t[:, 9, :].unsqueeze(2))
        else:
            t1v = SCS(None, SCS.t[:, 0, :, :])
            t2v = SCS(None, SCS.t[:, 1, :, :])

        def col(buf, t):
            v = buf.t[:, :, 0:nseq * Wx].rearrange("p q (n w) -> p q n w", w=Wx)[:, :, :, t]
            return buf(None, v)

        if is_s:
            e = "dve"
            for t in range(L):
                C.tt(e, t1v, col(XR, t), arb, ALU.mult)
                C.tt(e, col(XR, t + 1), col(XR, t + 1), t1v, ALU.add)
                C.tt(e, t1v, col(XI, t), naib, ALU.mult)
                C.tt(e, col(XR, t + 1), col(XR, t + 1), t1v, ALU.add)
                C.tt(e, t2v, col(XI, t), arb, ALU.mult)
                C.tt(e, col(XI, t + 1), col(XI, t + 1), t2v, ALU.add)
                C.tt(e, t2v, col(XR, t), aib, ALU.mult)
                C.tt(e, col(XI, t + 1), col(XI, t + 1), t2v, ALU.add)
        else:
            VTMf_ = VTM.t[:, :, :].rearrange("p a b -> p (a b)")
            PJf_ = PJ.t[:, 24:32, :].rearrange("p a b -> p (a b)")
            ROWf_ = ROW.t[:, :, :].rearrange("p a b -> p (a b)")
            C.dma(VTM(None, VTMf_[:, 0:1024]), s5tab[0])
            C.dma(PJ(None, PJf_), s5tab[1])
            for t0 in range(0, T, 32):
                Tc = min(32, T - t0)
                ec = VTM(None, VTMf_[:, 0:1024].rearrange("p (q t) -> p q t", t=32)[:, :, 0:Tc])
                es = PJ(None, PJf_.rearrange("p (q t) -> p q t", t=32)[:, :, 0:Tc])
                w1 = KW(None, KW.t[:, :].rearrange("p (q t) -> p q t", t=32)[:, :, 0:Tc])
                w2 = ROW(None, ROWf_.rearrange("p (q t) -> p q t", t=32)[:, :, 0:Tc])
                ur = XR(None, XR.t[:, :, 1 + t0:1 + t0 + Tc])
                ui = XI(None, XI.t[:, :, 1 + t0:1 + t0 + Tc])
                C.tt("pool", w1, ur, es, ALU.mult)
                C.tt("dve", w2, ui, es, ALU.mult)
                C.tt("dve", ur, ur, ec, ALU.mult)
                C.tt("pool", ui, ui, ec, ALU.mult)
                C.tt("dve", ur, ur, w2, ALU.add)
                C.tt("pool", ui, ui, w1, ALU.subtract)
                for pr in range(32):
                    rho = S5A(None, S5A.t[:, MAG, pr:pr + 1].broadcast_to([128, Tc]))
                    for buf in (XR, XI):
                        seg = buf(None, buf.t[:, pr, 1 + t0:1 + t0 + Tc])
                        scan(seg, rho, seg, buf(None, buf.t[:, pr, t0:t0 + 1]), ALU.mult, ALU.add)
                C.tt("pool", w1, ur, es, ALU.mult)
                C.tt("dve", w2, ui, es, ALU.mult)
                C.tt("dve", ur, ur, ec, ALU.mult)
                C.tt("pool", ui, ui, ec, ALU.mult)
                C.tt("dve", ur, ur, w2, ALU.subtract)
                C.tt("pool", ui, ui, w1, ALU.add)
        if is_s:
            for k, (nm, buf) in enumerate((("s_sre_o", XR), ("s_sim_o", XI))):
                for q in range(2):
                    C.copy("pool", TMP(None, TMP.t[:, 0:128].rearrange("p (q n o) -> p q n o", n=8, o=1)),
                           xv(buf, L, L + 1, 16 * q, 16 * q + 16))
                    ps = next_psb()
                    C.tr(ps(None, ps.t[:, 0:128]), TMP(None, TMP.t[:, 0:128]), ident.all())
                    C.copy("dve", S5ST.all(), ps(None, ps.t[:, 0:128]))
                    for pr in range(16):
                        g0 = 2 * (16 * q + pr)
                        C.dma(O[nm][s0:s0 + 8, g0:g0 + 2, :].rearrange("n m p -> n (m p)"),
                              S5ST(None, S5ST.t[pr * 8:(pr + 1) * 8, :]))
        else:
            for k, buf in enumerate((XR, XI)):
                C.copy("pool", S5C(None, S5C.t[:, k, :].unsqueeze(2).unsqueeze(3)), xv(buf, L, L + 1))
            if tile["last"]:
                for k, nm in enumerate(("p_sre", "p_sim")):
                    ps = next_psb()
                    C.tr(ps(None, ps.t[0:32, 0:128]), S5C(None, S5C.t[:, k, :]), ident.all())
                    C.copy("dve", S5ST(None, S5ST.t[0:32, :]), ps(None, ps.t[0:32, 0:128]))
                    C.dma(O[nm].rearrange("(q m) p -> q (m p)", m=2), S5ST(None, S5ST.t[0:32, :]))
        for gt in range(8):
            ps = next_psa()
            for ip in range(4):
                pair = 4 * gt + ip
                hf = ip // 2
                ov = ps(None, ps.t[64 * hf:64 * hf + 64, 0:T])
                xr_ = XR(None, XR.t[:, pair, 0:nseq * Wx].rearrange("p (n w) -> p n w", w=Wx)[:, :, 1:Wx])
                xi_ = XI(None, XI.t[:, pair, 0:nseq * Wx].rearrange("p (n w) -> p n w", w=Wx)[:, :, 1:Wx])
                cr, ci = CRE[ip % 2], CIM[ip % 2]
                C.mm(ov, cr(None, cr.t[:, gt, 64 * hf:64 * hf + 64]), xr_, start=(ip % 2 == 0), stop=False)
                C.mm(ov, ci(None, ci.t[:, gt, 64 * hf:64 * hf + 64]), xi_, start=False, stop=(ip % 2 == 1))
            gv = pj(gt)
            C.stt("dve", gv, gv, VEC0(None, VEC0.t[:, gt, 34:35]), ps(None, ps.t[:, 0:T]), ALU.mult, ALU.add)
            C.act(GELB(("t", gt), GELB.t[:, gt, 0:T]), gv, AF.Gelu)

        def evac_glu(j, pv):
            if j < 8:
                C.act(ACC(("t", j), ACC.t[:, j, 0:T]), pv, AF.Identity, bias=VECG(None, VECG.t[:, j, 0:1]))
            else:
                jj = j - 8
                tv = TMP(None, TMP.t[:, 0:T])
                C.act(tv, pv, AF.Sigmoid, bias=VECG(None, VECG.t[:, j, 0:1]))
                C.tt("dve", tv, tv, ACC(("t", jj), ACC.t[:, jj, 0:T]), ALU.mult)
                C.tt("dve", mixv(8 + jj, T), tv, pj(8 + jj), ALU.mult)

        stream_mm(I["s5_glu_w"], 8, [(i * 128, 128) for i in range(16)], lambda kt: GELB(("t", kt), GELB.t[:, kt, 0:T]), T, evac_glu)
        out_proj_ln(I["ev_w_out"], tile, VECG, 1, 2)

    do_rwkv = cfg.get("rwkv", True)
    MASKBIG = {"p": C.sb("mbig_p", [128, 128]), "s": C.sb("mbig_s", [64, 64])}
    SEGTRI = {"p": C.sb("stri_p", [128, 128]), "s": C.sb("stri_s", [64, 64])}
    R01 = {"p": C.sb("r01p", [128, 128]), "s": C.sb("r01s", [128, 64])}
    RNEG = {"p": C.sb("rnegp", [128, 128]), "s": C.sb("rnegs", [128, 64])}
    SEGSEL = C.sb("SEGSEL", [8, 64])
    SEGC = C.sb("SEGC", [64, 8])
    SEGROW = C.sb("SEGROW", [128, 8, 64])
    for k in ("p", "s"):
        C.dma(MASKBIG[k].all(), I["maskbig_" + k])
        C.dma(SEGTRI[k].all(), I["segtri_" + k])
        C.dma(R01[k].all(), I["r01_" + k])
        C.dma(RNEG[k].all(), I["rneg_" + k])
    C.dma(SEGSEL.all(), I["segsel"])
    C.dma(SEGC.all(), I["segc"])
    C.dma(SEGROW.all(), I["segrow"].rearrange("p (n t) -> p n t", n=8))

    VEC1 = C.sb("VEC1", [128, 8, 8])
    VMU = C.sb("VMU", [128, 25, 1])
    VECO = C.sb("VECO", [128, 16, 2])
    GB = C.sb("GB", [8, 1])
    load_rows_T(VEC1, [I[n_] for n_ in ("m_hn_g", "r_w0", "r_a0", "r_kk", "r_ka", "r_ln_g", "r_ln_b", "r_rk")], 1024)
    load_rows_T(VMU, [I["r_mu"][:, 0:2048]], 2048)
    load_rows_T(VMU, [I["r_mu"][:, 2048:3200]], 1152, ct0=16)
    load_rows_T(VECO, [I["od_ln_g"], I["od_ln_b"]], 2048)
    C.dma(GB(None, GB.t[0:4, :]), I["m_ig_b"].rearrange("o h -> h o"), slow=True)
    C.dma(GB(None, GB.t[4:8, :]), I["m_fg_b"].rearrange("o h -> h o"), slow=True)

    VTM = C.sb("VTM", [128, 4, 257])
    KW = C.sb("KW", [128, 1024])
    KWN = C.sb("KWN", [128, 256])
    GX = C.sb("GX", [8, 128])
    ROW = C.sb("ROW", [128, 8, 128])
    COL = C.sb("COL", [128, 64])
    DTB = C.sb("DTB", [128, 128])
    STB = C.sb("STB", [128, 128])
    P1S = C.sb("P1S", [128, 257])
    NUM = C.sb("NUM", [128, 257])
    HN = C.sb("HN", [128, 256])
    SM = C.sb("SM", [128, 16])
    CS = C.sb("CS", [128, 4, 2, 257])
    MCAR = C.sb("MCAR", [128, 4])
    CSS = [C.sb("CSS%d" % i, [128, 2, 257]) for i in range(2)]
    CSO = [C.sb("CSO%d" % i, [128, 2, 257]) for i in range(2)]
    MS = C.sb("MS", [8, 4])
    MSB = C.sb("MSB", [8, 128])
    MINIT = C.sb("MINIT", [128, 4, 8])
    DEC = C.sb("DEC", [128, 4, 8])
    MNEW = C.sb("MNEW", [128, 4, 8])
    QM = [C.sb("QM%d" % i, [128, 64]) for i in range(2)]
    C.memset("pool", VTM(None, VTM.t[:, :, 256:257]), 1.0)

    def stream_mm_tok(W, c0, ncols, T, evac):
        Wv = W.rearrange("(kt p) c -> p kt c", p=128)
        for g in range(ncols // WG):
            ps = next_psa()
            pv = ps(None, ps.t[0:T, 0:WG])
            for kq in range(0, 16, KQ):
                ws = WS[rr["ws"] % NWS]
                rr["ws"] += 1
                load_slab(ws, Wv[:, kq:kq + KQ, c0 + g * WG:c0 + (g + 1) * WG], KQ, WG)
                for k in range(KQ):
                    kt = kq + k
                    C.mm(pv, XTB(("t", kt), XTB.t[:, kt, 0:T]), ws(None, ws.t[:, k, 0:WG]), start=(kt == 0), stop=(kt == 15))
            evac(g, pv)

    def recip(e, out, in_):
        return C.op(e, lambda g: g.reciprocal(out=out.ap, in_=in_.ap), reads=[in_], writes=[out])

    def scan(out, d0, d1, init, op0, op1):
        rd = [d0, d1] + ([init] if isinstance(init, View) else [])
        ia = init.ap if isinstance(init, View) else init
        return C.op("dve", lambda g: g.tensor_tensor_scan(out=out.ap, data0=d0.ap, data1=d1.ap, initial=ia, op0=op0, op1=op1),
                    reads=rd, writes=[out])

    SR = C.sb("SR", [128, 8, 64])
    W2A2 = C.sb("W2A2", [128, 1024])
    BLK = C.sb("BLK", [128, 128])
    OMKA = C.sb("OMKA", [128, 8])
    SHC = C.sb("SHC", [128, 25])
    SUMB = C.sb("SUMB", [128, 8])
    C.dma(W2A2(None, W2A2.t[0:64, :]), I["r_w2"])
    C.dma(W2A2(None, W2A2.t[64:128, :]), I["r_a2"])
    C.dma(BLK.all(), I["blk"])
    C.ts("dve", OMKA.all(), VEC1(None, VEC1.t[:, :, 4]), -1.0, 1.0, ALU.mult, ALU.add)
    XIf = XI.t[:, :, :].rearrange("p a b -> p (a b)")
    T1 = XI("T1", XIf[:, 0:512].rearrange("p (j k) -> p j k", k=64))
    T2 = XI("T2", XIf[:, 512:1024].rearrange("p (j k) -> p j k", k=64))
    FSv = XIf[:, 1024:1536].rearrange("p (i t) -> p i t", t=128)
    SRS = [XI(("SRS", i), XIf[:, 1536 + 512 * i:2048 + 512 * i].rearrange("p (j k) -> p j k", k=64)) for i in range(2)]
    TWv = XIf[:, 2560:2688]
    ALLPS = PSA + PSB

    def next_ps8():
        rr["ps8"] = (rr.get("ps8", 0) + 1) % 8
        return ALLPS[rr["ps8"]]

    def rwkv(tile):
        T, nseq, L = tile["T"], tile["nseq"], tile["L"]
        is_s = tile["kind"] == "s"
        s0 = tile["s0"]
        Wx = 1 + L
        xrhs = lambda kt: XTB(("t", kt), XTB.t[:, kt, 0:T])
        pj = lambda j: PJ(("t", j), PJ.t[:, j, 0:T])
        pj3 = lambda j: PJ(("t", j), PJ.t[:, j, 0:T].rearrange("p (n l) -> p n l", l=L))

        def ppv(j0, j1, a, b):
            v = XR.t[:, j0:j1, 0:nseq * Wx].rearrange("p j (n w) -> p j n w", w=Wx)[:, :, :, a:b]
            return XR(("pp", j0) if j1 == j0 + 1 else None, v)

        if is_s:
            for (c0, ncol, ct0) in ((0, 2048, 0), (2048, 1152, 16)):
                C.dma(XS(None, XS.t[0:8, 0:ncol]), I["s_rsh"][s0:s0 + 8, c0:c0 + ncol])
                for ct in range(ncol // 128):
                    ps = next_psb()
                    C.tr(ps(None, ps.t[:, 0:8]), XS(None, XS.t[0:8, ct * 128:(ct + 1) * 128]), ident(None, ident.t[0:8, 0:8]))
                    C.copy("dve", ppv(ct0 + ct, ct0 + ct + 1, 0, 1), ps(None, ps.t[:, 0:8].rearrange("p (j n o) -> p j n o", j=1, o=1)))
        else:
            if tile["first"]:
                C.memset("pool", ppv(0, 25, 0, 1), 0.0)
            else:
                C.copy("pool", ppv(0, 25, 0, 1), SHC(None, SHC.t[:, :].unsqueeze(2).unsqueeze(3)))

        def evacR(j, pv):
            if j < 25:
                C.copy(ev_eng(), ppv(j, j + 1, 1, Wx), pv.buf(None, pv.ap.rearrange("p (j n l) -> p j n l", j=1, l=L)))
            else:
                C.act(pj(j), pv, AF.Silu)

        stream_mm(I["od_w_in"], 16, [(5128 + i * 128, 128) for i in range(33)], xrhs, T, evacR)

        if is_s:
            for (c0, ncol, ct0) in ((0, 2048, 0), (2048, 1152, 16)):
                for ct in range(ncol // 128):
                    ps = next_psb()
                    C.copy("pool", TMP(None, TMP.t[:, 0:8]), XR(("pp", ct0 + ct), XR.t[:, ct0 + ct, 0:nseq * Wx].rearrange("p (n w) -> p n w", w=Wx)[:, :, L]))
                    C.tr(ps(None, ps.t[0:8, 0:128]), TMP(None, TMP.t[:, 0:8]), ident.all())
                    C.copy("dve", XS(None, XS.t[0:8, ct * 128:(ct + 1) * 128]), ps(None, ps.t[0:8, 0:128]))
                C.dma(O["s_rsh_o"][s0:s0 + 8, c0:c0 + ncol], XS(None, XS.t[0:8, 0:ncol]))
        else:
            C.copy("pool", SHC(None, SHC.t[:, :].unsqueeze(2).unsqueeze(3)), ppv(0, 25, L, L + 1))
            if tile["last"]:
                for (c0, ncol, ct0) in ((0, 2048, 0), (2048, 1152, 16)):
                    for ct in range(ncol // 128):
                        ps = next_psb()
                        C.tr(ps(None, ps.t[0:1, 0:128]), SHC(None, SHC.t[:, ct0 + ct:ct0 + ct + 1]), ident.all())
                        C.copy("dve", XS(None, XS.t[0:1, ct * 128:(ct + 1) * 128]), ps(None, ps.t[0:1, 0:128]))
                    C.dma(O["p_rsh"][:, c0:c0 + ncol], XS(None, XS.t[0:1, 0:ncol]))
        for j in range(25):
            C.tt("pool", pj3(j), XR(("pp", j), XR.t[:, j, 0:nseq * Wx].rearrange("p (n w) -> p n w", w=Wx)[:, :, 0:L]),
                 XR(("pp", j), XR.t[:, j, 0:nseq * Wx].rearrange("p (n w) -> p n w", w=Wx)[:, :, 1:Wx]), ALU.subtract)
            C.stt("dve", pj3(j), pj3(j), VMU(None, VMU.t[:, j, 0:1]),
                  XR(("pp", j), XR.t[:, j, 0:nseq * Wx].rearrange("p (n w) -> p n w", w=Wx)[:, :, 1:Wx]), ALU.mult, ALU.add)

        VTMf = VTM.t[:, :, :].rearrange("p a b -> p (a b)")
        ROWf = ROW.t[:, :, :].rearrange("p a b -> p (a b)")
        KKt = lambda a, b: KW(None, KW.t[0:T, a:b])
        Wt = lambda a, b: VTM(None, VTMf[0:T, a:b])
        KKAt = lambda a, b: ROW(None, ROWf[0:T, a:b])
        KPt = lambda a, b: XS(None, XS.t[0:T, a:b])
        Rt = lambda a, b: XS(None, XS.t[0:T, 1024 + a:1024 + b])
        fs = lambda i: XI(("FS", i), FSv[:, i, 0:T])
        tw = XI("TW", TWv[0:64, 0:T])
        C.act(tw, PJ(("t", 24), PJ.t[0:64, 24, 0:T]), AF.Tanh)

        def to_tok(dst, src):
            ps = next_psb()
            C.tr(ps(None, ps.t[0:T, 0:128]), src, ident.all())
            C.copy(ev_eng(), dst, ps(None, ps.t[0:T, 0:128]))

        NE05 = -float(np.exp(-0.5))
        for ct in range(8):
            r_, k_, v_ = pj(ct), pj(8 + ct), pj(16 + ct)
            cs_ = slice(ct * 128, (ct + 1) * 128)
            ps = next_psa()
            C.mm(ps(None, ps.t[:, 0:T]), W2A2(None, W2A2.t[0:64, cs_]), tw)
            C.act(fs(0), ps(None, ps.t[:, 0:T]), AF.Sigmoid, bias=VEC1(None, VEC1.t[:, ct, 1:2]))
            C.act(fs(0), fs(0), AF.Exp, scale=NE05)
            to_tok(Wt(ct * 128, (ct + 1) * 128), fs(0))
            ps = next_psa()
            C.mm(ps(None, ps.t[:, 0:T]), W2A2(None, W2A2.t[64:128, cs_]), PJ(("t", 24), PJ.t[64:128, 24, 0:T]))
            C.act(fs(1), ps(None, ps.t[:, 0:T]), AF.Sigmoid, bias=VEC1(None, VEC1.t[:, ct, 2:3]))
            C.ts("dve", fs(2), k_, VEC1(None, VEC1.t[:, ct, 3:4]), None, ALU.mult)
            C.tt("pool", fs(3), fs(2), fs(2), ALU.mult)
            ps = next_psa()
            C.mm(ps(None, ps.t[:, 0:T]), BLK.all(), fs(3))
            C.act(fs(3), ps(None, ps.t[:, 0:T]), AF.Sqrt)
            C.ts("dve", fs(3), fs(3), 1e-12, None, ALU.max)
            recip("dve", fs(3), fs(3))
            C.tt("dve", fs(2), fs(2), fs(3), ALU.mult)
            to_tok(KKt(ct * 128, (ct + 1) * 128), fs(2))
            C.tt("dve", fs(3), fs(2), fs(1), ALU.mult)
            to_tok(KKAt(ct * 128, (ct + 1) * 128), fs(3))
            C.ts("dve", fs(1), fs(1), VEC1(None, VEC1.t[:, ct, 4:5]), OMKA(None, OMKA.t[:, ct:ct + 1]), ALU.mult, ALU.add)
            C.tt("dve", fs(1), fs(1), k_, ALU.mult)
            to_tok(KPt(ct * 128, (ct + 1) * 128), fs(1))
            to_tok(Rt(ct * 128, (ct + 1) * 128), r_)
            C.tt("dve", fs(3), r_, fs(1), ALU.mult)
            C.ts("dve", fs(3), fs(3), VEC1(None, VEC1.t[:, ct, 7:8]), None, ALU.mult)
            ps = next_psa()
            C.mm(ps(None, ps.t[:, 0:T]), BLK.all(), fs(3))
            C.tt("dve", ACC(("t", ct), ACC.t[:, ct, 0:T]), ps(None, ps.t[:, 0:T]), v_, ALU.mult)

        Yv = MIX.t[:, 8 * TT:16 * TT].rearrange("p (j t) -> p j t", j=8)
        srcs = (("kk", KW.t[0:T, :]), ("w", VTMf[0:T, 0:1024]), ("kka", ROWf[0:T, 0:1024]), ("kp", XS.t[0:T, 0:1024]), ("r", XS.t[0:T, 1024:2048]))
        bufs = {"kk": KW, "w": VTM, "kka": ROW, "kp": XS, "r": XS}
        for n in range(nseq):
            if is_s:
                sr = SRS[n % 2]
                C.dma(sr, I["s_rs"][s0 + n].rearrange("(j hp) v k -> (hp v) j k", hp=2))
            else:
                sr = SR.all()
                if tile["first"]:
                    C.memset("pool", sr, 0.0)
            for l in range(L):
                t = n * L + l
                oh = ident(None, ident.t[0:T, t:t + 1].broadcast_to([T, 64]))
                bc = {}
                for nm, ap in srcs:
                    ps = next_ps8()
                    xv = ap.rearrange("p (j hp k) -> p hp j k", hp=2, k=64)
                    C.mm(ps(None, ps.t[0:64, 0:512]), oh, bufs[nm](None, xv[:, 0]))
                    C.mm(ps(None, ps.t[64:128, 0:512]), oh, bufs[nm](None, xv[:, 1]))
                    bc[nm] = ps(None, ps.t[:, 0:512].rearrange("p (j k) -> p j k", k=64))
                C.tt("dve", T1, sr, bc["kk"], ALU.mult)
                C.op("dve", lambda g: g.reduce_sum(out=SUMB.t[:, :], in_=T1.ap, axis=AX.X), reads=[T1], writes=[SUMB.all()])
                C.tt("dve", sr, sr, bc["w"], ALU.mult)
                C.tt("dve", T2, bc["kka"], SUMB(None, SUMB.t[:, :].unsqueeze(2).broadcast_to([128, 8, 64])), ALU.mult)
                C.tt("dve", sr, sr, T2, ALU.subtract)
                C.tt("dve", T1, bc["kp"], PJ(None, PJ.t[:, 16:24, t].unsqueeze(2).broadcast_to([128, 8, 64])), ALU.mult)
                C.tt("dve", sr, sr, T1, ALU.add)
                C.tt("dve", T2, sr, bc["r"], ALU.mult)
                yv = MIX(None, Yv[:, :, t])
                C.op("dve", lambda g, yv=yv: g.reduce_sum(out=yv.ap, in_=T2.ap, axis=AX.X), reads=[T2], writes=[yv])
            if is_s:
                C.dma(O["s_rs_o"][s0 + n].rearrange("(j hp) v k -> (hp v) j k", hp=2), sr)
        if (not is_s) and tile["last"]:
            C.dma(O["p_rs"].rearrange("(j hp) v k -> (hp v) j k", hp=2), SR.all())

        for j in range(8):
            y = mixv(8 + j, T)
            ps = next_psa()
            C.mm(ps(None, ps.t[:, 0:T]), BLK.all(), y)
            C.tt("pool", fs(0), y, y, ALU.mult)
            ps2 = next_psa()
            C.mm(ps2(None, ps2.t[:, 0:T]), BLK.all(), fs(0))
            C.ts("dve", fs(1), ps(None, ps.t[:, 0:T]), 1.0 / 64, None, ALU.mult)
            C.ts("dve", fs(2), ps2(None, ps2.t[:, 0:T]), 1.0 / 64, None, ALU.mult)
            C.tt("dve", fs(3), fs(1), fs(1), ALU.mult)
            C.tt("dve", fs(2), fs(2), fs(3), ALU.subtract)
            C.ts("dve", fs(2), fs(2), 64e-5, None, ALU.add)
            C.act(fs(2), fs(2), AF.Sqrt)
            recip("dve", fs(2), fs(2))
            C.tt("dve", y, y, fs(1), ALU.subtract)
            C.tt("dve", y, y, fs(2), ALU.mult)
            C.act(y, y, AF.Identity, bias=VEC1(None, VEC1.t[:, j, 6:7]), scale=VEC1(None, VEC1.t[:, j, 5:6]))
            C.tt("dve", y, y, ACC(("t", j), ACC.t[:, j, 0:T]), ALU.add)
            C.tt("dve", y, y, pj(25 + j), ALU.mult)

    SSTRI = {"p": C.sb("sstri_p", [128, 128]), "s": C.sb("sstri_s", [64, 64])}
    SSTRIT = {"p": C.sb("sstriT_p", [128, 128]), "s": C.sb("sstriT_s", [64, 64])}
    for k_ in ("p", "s"):
        C.dma(SSTRI[k_].all(), I["sstri_" + k_])
        C.dma(SSTRIT[k_].all(), I["sstriT_" + k_])
    WLB = C.sb("WLB", [128, 8, 8])
    HBf = HB.t[:, :, :].rearrange("p a b -> p (a b)")
    KKv = KW.t[:, :].rearrange("p (j t) -> p j t", t=128)
    BTv = ROW.t
    XRf = XR.t[:, :, :].rearrange("p a b -> p (a b)")
    S0Tv = XRf[:, 0:4096].rearrange("p (n j v) -> p n j v", n=8, j=8)

    def fence(buf, ap):
        C.op("pool", lambda g: g.memset(ap, 0.0), writes=[buf.all()])

    def rwkv2(tile):
        T, nseq, L = tile["T"], tile["nseq"], tile["L"]
        is_s = tile["kind"] == "s"
        kd = tile["kind"]
        s0 = tile["s0"]
        Wx = 1 + L
        xrhs = lambda kt: XTB(("t", kt), XTB.t[:, kt, 0:T])
        pj = lambda j: PJ(("t", j), PJ.t[:, j, 0:T])
        pj3 = lambda j: PJ(("t", j), PJ.t[:, j, 0:T].rearrange("p (n l) -> p n l", l=L))

        def ppv(j0, j1, a, b):
            v = XR.t[:, j0:j1, 0:nseq * Wx].rearrange("p j (n w) -> p j n w", w=Wx)[:, :, :, a:b]
            return XR(("pp", j0) if j1 == j0 + 1 else None, v)

        if is_s:
            for (c0, ncol, ct0) in ((0, 2048, 0), (2048, 1152, 16)):
                C.dma(XS(None, XS.t[0:8, 0:ncol]), I["s_rsh"][s0:s0 + 8, c0:c0 + ncol])
                for ct in range(ncol // 128):
                    ps = next_psb()
                    C.tr(ps(None, ps.t[:, 0:8]), XS(None, XS.t[0:8, ct * 128:(ct + 1) * 128]), ident(None, ident.t[0:8, 0:8]))
                    C.copy("dve", ppv(ct0 + ct, ct0 + ct + 1, 0, 1), ps(None, ps.t[:, 0:8].rearrange("p (j n o) -> p j n o", j=1, o=1)))
        else:
            if tile["first"]:
                C.memset("pool", ppv(0, 25, 0, 1), 0.0)
            else:
                C.copy("pool", ppv(0, 25, 0, 1), SHC(None, SHC.t[:, :].unsqueeze(2).unsqueeze(3)))

        def evacR(j, pv):
            if j < 25:
                C.copy(ev_eng(), ppv(j, j + 1, 1, Wx), pv.buf(None, pv.ap.rearrange("p (j n l) -> p j n l", j=1, l=L)))
            else:
                C.act(pj(j), pv, AF.Silu)

        stream_mm(I["od_w_in"], 16, [(5128 + i * 128, 128) for i in range(33)], xrhs, T, evacR)

        if is_s:
            for (c0, ncol, ct0) in ((0, 2048, 0), (2048, 1152, 16)):
                for ct in range(ncol // 128):
                    ps = next_psb()
                    C.copy("pool", TMP(None, TMP.t[:, 0:8]), XR(("pp", ct0 + ct), XR.t[:, ct0 + ct, 0:nseq * Wx].rearrange("p (n w) -> p n w", w=Wx)[:, :, L]))
                    C.tr(ps(None, ps.t[0:8, 0:128]), TMP(None, TMP.t[:, 0:8]), ident.all())
                    C.copy("dve", XS(None, XS.t[0:8, ct * 128:(ct + 1) * 128]), ps(None, ps.t[0:8, 0:128]))
                C.dma(O["s_rsh_o"][s0:s0 + 8, c0:c0 + ncol], XS(None, XS.t[0:8, 0:ncol]))
        else:
            C.copy("pool", SHC(None, SHC.t[:, :].unsqueeze(2).unsqueeze(3)), ppv(0, 25, L, L + 1))
            if tile["last"]:
                for (c0, ncol, ct0) in ((0, 2048, 0), (2048, 1152, 16)):
                    for ct in range(ncol // 128):
                        ps = next_psb()
                        C.tr(ps(None, ps.t[0:1, 0:128]), SHC(None, SHC.t[:, ct0 + ct:ct0 + ct + 1]), ident.all())
                        C.copy("dve", XS(None, XS.t[0:1, ct * 128:(ct + 1) * 128]), ps(None, ps.t[0:1, 0:128]))
                    C.dma(O["p_rsh"][:, c0:c0 + ncol], XS(None, XS.t[0:1, 0:ncol]))
        for j in range(25):
            C.tt("pool", pj3(j), XR(("pp", j), XR.t[:, j, 0:nseq * Wx].rearrange("p (n w) -> p n w", w=Wx)[:, :, 0:L]),
                 XR(("pp", j), XR.t[:, j, 0:nseq * Wx].rearrange("p (n w) -> p n w", w=Wx)[:, :, 1:Wx]), ALU.subtract)
            C.stt("dve", pj3(j), pj3(j), VMU(None, VMU.t[:, j, 0:1]),
                  XR(("pp", j), XR.t[:, j, 0:nseq * Wx].rearrange("p (n w) -> p n w", w=Wx)[:, :, 1:Wx]), ALU.mult, ALU.add)

        s0t = lambda n, j, rs=slice(0, 128): XR(("st", n), S0Tv[rs, n, j, :])
        if is_s:
            fence(XR, XR.t[0:1, 0, 0:1])
            for n2 in range(0, 8, 2):
                stg = XS.t[:, :].rearrange("p (n j d k) -> p n j d k", n=2, j=8, d=2)
                for nn in range(2):
                    for d in range(2):
                        C.dma(XS(None, stg[:, nn, :, d, :]), I["s_rs"][s0 + n2 + nn].rearrange("(j hp) v k -> (hp v) j k", hp=2))
                for nn in range(2):
                    n = n2 + nn
                    for j in range(8):
                        ps = next_psb()
                        C.tr(ps(None, ps.t[:, 0:128]), XS(None, stg[:, nn, j, :, :]), ident.all())
                        C.copy("dve", s0t(n, j, slice(0, 64)), ps(None, ps.t[0:64, 0:64]))
                        C.copy("act", s0t(n, j, slice(64, 128)), ps(None, ps.t[64:128, 64:128]))
        else:
            if tile["first"]:
                C.memset("pool", SR.all(), 0.0)

        VTMf = VTM.t[:, :, :].rearrange("p a b -> p (a b)")
        Vtm = lambda hc: XS(None, XS.t[0:T, hc])
        Btm = lambda hc: XS(None, XS.t[0:T, 1024 + hc.start:1024 + hc.stop])
        Ktm = lambda hc: VTM(None, VTMf[0:T, hc])
        fs = lambda i: XI(("FS", i), FSv[:, i, 0:T])
        tw = XI("TW", TWv[0:64, 0:T])
        C.act(tw, PJ(("t", 24), PJ.t[0:64, 24, 0:T]), AF.Tanh)

        def to_tok(dst, src):
            ps = next_psb()
            C.tr(ps(None, ps.t[0:T, 0:128]), src, ident.all())
            C.copy(ev_eng(), dst, ps(None, ps.t[0:T, 0:128]))

        NE05 = -float(np.exp(-0.5))
        kkc = lambda ct, rs=slice(0, 128): KW(("c", ct), KKv[rs, ct, 0:T])
        btc = lambda ct, rs=slice(0, 128): ROW(("c", ct), BTv[rs, ct, 0:T])
        for ct in range(8):
            r_, k_, v_ = pj(ct), pj(8 + ct), pj(16 + ct)
            cs_ = slice(ct * 128, (ct + 1) * 128)
            ps = next_psa()
            C.mm(ps(None, ps.t[:, 0:T]), W2A2(None, W2A2.t[0:64, cs_]), tw)
            C.act(fs(0), ps(None, ps.t[:, 0:T]), AF.Sigmoid, bias=VEC1(None, VEC1.t[:, ct, 1:2]))
            C.ts("dve", fs(0), fs(0), NE05, None, ALU.mult)
            scan(fs(1), R01[kd](None, R01[kd].t[:, 0:T]), fs(0), 0.0, ALU.mult, ALU.add)
            ps = next_psa()
            C.mm(ps(None, ps.t[:, 0:T]), W2A2(None, W2A2.t[64:128, cs_]), PJ(("t", 24), PJ.t[64:128, 24, 0:T]))
            C.act(fs(2), ps(None, ps.t[:, 0:T]), AF.Sigmoid, bias=VEC1(None, VEC1.t[:, ct, 2:3]))
            C.ts("dve", kkc(ct), k_, VEC1(None, VEC1.t[:, ct, 3:4]), None, ALU.mult)
            C.tt("pool", fs(3), kkc(ct), kkc(ct), ALU.mult)
            ps = next_psa()
            C.mm(ps(None, ps.t[:, 0:T]), BLK.all(), fs(3))
            C.act(fs(3), ps(None, ps.t[:, 0:T]), AF.Sqrt)
            C.ts("dve", fs(3), fs(3), 1e-12, None, ALU.max)
            recip("dve", fs(3), fs(3))
            C.tt("dve", kkc(ct), kkc(ct), fs(3), ALU.mult)
            C.tt("dve", btc(ct), kkc(ct), fs(2), ALU.mult)
            C.ts("dve", fs(2), fs(2), VEC1(None, VEC1.t[:, ct, 4:5]), OMKA(None, OMKA.t[:, ct:ct + 1]), ALU.mult, ALU.add)
            C.tt("dve", fs(2), fs(2), k_, ALU.mult)
            C.tt("pool", fs(3), r_, fs(2), ALU.mult)
            C.ts("dve", fs(3), fs(3), VEC1(None, VEC1.t[:, ct, 7:8]), None, ALU.mult)
            ps = next_psa()
            C.mm(ps(None, ps.t[:, 0:T]), BLK.all(), fs(3))
            C.tt("dve", ACC(("t", ct), ACC.t[:, ct, 0:T]), ps(None, ps.t[:, 0:T]), v_, ALU.mult)
            C.act(fs(3), fs(1), AF.Exp)
            C.tt("dve", r_, r_, fs(3), ALU.mult)
            C.copy("pool", WLB(None, WLB.t[:, ct, 0:nseq]),
                   XI(("FS", 3), FSv[:, 3, 0:T].rearrange("p (n l) -> p n l", l=L)[:, :, L - 1]))
            C.tt("dve", fs(3), fs(1), fs(0), ALU.subtract)
            C.act(fs(3), fs(3), AF.Exp)
            C.tt("dve", kkc(ct), kkc(ct), fs(3), ALU.mult)
            C.act(fs(3), fs(1), AF.Exp, scale=-1.0)
            C.tt("dve", btc(ct), btc(ct), fs(3), ALU.mult)
            C.tt("dve", k_, fs(2), fs(3), ALU.mult)
            to_tok(Vtm(cs_), v_)
            to_tok(Btm(cs_), btc(ct))
            to_tok(Ktm(cs_), k_)

        fence(HB, HB.t[0:1, 0, 0:1])
        mat = lambda i: HB(("m", i), HBf[0:T, i * 128:i * 128 + T])
        half = lambda i, a: HB(("m", i), HBf[0:T, i * 128 + 64 * a:i * 128 + 64 * a + 64])
        nsq = max(0, int(np.ceil(np.log2(L))) - 1)
        idT = ident(None, ident.t[0:T, 0:T])
        mS = SSTRI[kd](None, SSTRI[kd].t[0:T, 0:T])
        mST = SSTRIT[kd](None, SSTRIT[kd].t[0:T, 0:T])
        mI = SEGTRI[kd](None, SEGTRI[kd].t[0:T, 0:T])
        if is_s:
            KKM, RM = T1, T2
        for j in range(8):
            Q = [[mat(8 * hp + 0), mat(8 * hp + 1)] for hp in range(2)]
            QT = [[mat(8 * hp + 2), mat(8 * hp + 3)] for hp in range(2)]
            Pm = [mat(8 * hp + 4) for hp in range(2)]
            BR = [mat(8 * hp + 5) for hp in range(2)]
            AK = [mat(8 * hp + 6) for hp in range(2)]
            KR = [mat(8 * hp + 7) for hp in range(2)]
            RHS = [half(16, hp) for hp in range(2)]
            SAT = [half(17, hp) for hp in range(2)]
            rsl = [slice(0, 64), slice(64, 128)]
            if is_s:
                C.tt("pool", KKM, KW(("c", j), KKv[:, j, 0:T].unsqueeze(1).broadcast_to([128, 8, T])), SEGROW(None, SEGROW.t[:, :, 0:T]), ALU.mult)
                C.tt("pool", RM, PJ(("t", j), PJ.t[:, j, 0:T].unsqueeze(1).broadcast_to([128, 8, T])), SEGROW(None, SEGROW.t[:, :, 0:T]), ALU.mult)
            for hp in range(2):
                rs = rsl[hp]
                rq = PJ(("t", j), PJ.t[rs, j, 0:T])
                kq = PJ(("t", 8 + j), PJ.t[rs, 8 + j, 0:T])
                ps = next_ps8()
                C.mm(ps(None, ps.t[0:T, 0:T]), btc(j, rs), kkc(j, rs))
                C.mm(ps(None, ps.t[0:T, T:2 * T]), btc(j, rs), rq)
                C.tt("dve", Q[hp][0], ps(None, ps.t[0:T, 0:T]), mS, ALU.mult)
                C.tt("dve", BR[hp], ps(None, ps.t[0:T, T:2 * T]), mI, ALU.mult)
                ps = next_ps8()
                C.mm(ps(None, ps.t[0:T, 0:T]), kq, kkc(j, rs))
                C.mm(ps(None, ps.t[0:T, T:2 * T]), kq, rq)
                C.tt("dve", AK[hp], ps(None, ps.t[0:T, 0:T]), mS, ALU.mult)
                C.tt("dve", KR[hp], ps(None, ps.t[0:T, T:2 * T]), mI, ALU.mult)
                ps = next_ps8()
                C.mm(ps(None, ps.t[0:T, 0:T]), kkc(j, rs), btc(j, rs))
                C.tt("dve", QT[hp][0], ps(None, ps.t[0:T, 0:T]), mST, ALU.mult)
                C.stt("dve", Pm[hp], Q[hp][0], -1.0, idT, ALU.mult, ALU.add)
            cur = 0
            for it in range(nsq):
                nxt = 1 - cur
                last_it = (it == nsq - 1)
                for hp in range(2):
                    if not last_it:
                        ps = next_ps8()
                        C.mm(ps(None, ps.t[0:T, 0:T]), QT[hp][cur], Q[hp][cur])
                        C.copy("act", Q[hp][nxt], ps(None, ps.t[0:T, 0:T]))
                    ps = next_ps8()
                    C.mm(ps(None, ps.t[0:T, 0:T]), Q[hp][cur], QT[hp][cur])
                    C.copy("dve", QT[hp][nxt], ps(None, ps.t[0:T, 0:T]))
                for hp in range(2):
                    ps = next_ps8()
                    C.mm(ps(None, ps.t[0:T, 0:T]), QT[hp][nxt], Pm[hp])
                    C.tt("dve", Pm[hp], Pm[hp], ps(None, ps.t[0:T, 0:T]), ALU.add)
                cur = nxt
            for hp in range(2):
                rs = rsl[hp]
                hc = slice((2 * j + hp) * 64, (2 * j + hp) * 64 + 64)
                ps = next_ps8()
                if is_s:
                    for n in range(nseq):
                        C.mm(ps(None, ps.t[0:T, 0:64]), XI("T1", KKM.ap[rs, n, :]), s0t(n, j, rs), start=(n == 0), stop=False)
                else:
                    C.mm(ps(None, ps.t[0:T, 0:64]), kkc(j, rs), SR(None, SR.t[rs, j, :]), start=True, stop=False)
                C.mm(ps(None, ps.t[0:T, 0:64]), AK[hp], Vtm(hc), start=False, stop=True)
                C.act(RHS[hp], ps(None, ps.t[0:T, 0:64]), AF.Identity, scale=-1.0)
                ps = next_ps8()
                C.mm(ps(None, ps.t[0:T, 0:64]), Pm[hp], RHS[hp])
                C.copy("dve", SAT[hp], ps(None, ps.t[0:T, 0:64]))
            psY = next_ps8()
            for hp in range(2):
                rs = rsl[hp]
                hc = slice((2 * j + hp) * 64, (2 * j + hp) * 64 + 64)
                ov = psY(None, psY.t[rs, 0:T])
                if is_s:
                    for n in range(nseq):
                        C.mm(ov, s0t(n, j, rs), XI("T2", RM.ap[rs, n, :]), start=(n == 0), stop=False)
                else:
                    C.mm(ov, SR(None, SR.t[rs, j, :]), PJ(("t", j), PJ.t[rs, j, 0:T]), start=True, stop=False)
                C.mm(ov, SAT[hp], BR[hp], start=False, stop=False)
                C.mm(ov, Vtm(hc), KR[hp], start=False, stop=True)
            y = TMP(None, TMP.t[:, 0:T])
            C.copy("act", y, psY(None, psY.t[:, 0:T]))
            ps = next_psa()
            C.mm(ps(None, ps.t[:, 0:T]), BLK.all(), y)
            C.tt("pool", fs(0), y, y, ALU.mult)
            ps2 = next_psa()
            C.mm(ps2(None, ps2.t[:, 0:T]), BLK.all(), fs(0))
            C.ts("dve", fs(1), ps(None, ps.t[:, 0:T]), 1.0 / 64, None, ALU.mult)
            C.ts("dve", fs(2), ps2(None, ps2.t[:, 0:T]), 1.0 / 64, None, ALU.mult)
            C.tt("dve", fs(3), fs(1), fs(1), ALU.mult)
            C.tt("dve", fs(2), fs(2), fs(3), ALU.subtract)
            C.ts("dve", fs(2), fs(2), 64e-5, None, ALU.add)
            C.act(fs(2), fs(2), AF.Sqrt)
            recip("dve", fs(2), fs(2))
            C.tt("dve", y, y, fs(1), ALU.subtract)
            C.tt("dve", y, y, fs(2), ALU.mult)
            C.act(y, y, AF.Identity, bias=VEC1(None, VEC1.t[:, j, 6:7]), scale=VEC1(None, VEC1.t[:, j, 5:6]))
            C.tt("dve", y, y, ACC(("t", j), ACC.t[:, j, 0:T]), ALU.add)
            C.tt("dve", mixv(8 + j, T), y, pj(25 + j), ALU.mult)
            tmpS = XI(("FS", 0), FSv[:, 0, 0:64])
            if is_s:
                SAM = [CSS[hp](None, CSS[hp].t[0:T, :, :].rearrange("p a b -> p (a b)")[:, 0:512].rearrange("p (n v) -> p n v", n=8)) for hp in range(2)]
                VM = [CSO[hp](None, CSO[hp].t[0:T, :, :].rearrange("p a b -> p (a b)")[:, 0:512].rearrange("p (n v) -> p n v", n=8)) for hp in range(2)]
                segc_b = SEGC(None, SEGC.t[0:T, :].unsqueeze(2).broadcast_to([T, 8, 64]))
                for hp in range(2):
                    hc = slice((2 * j + hp) * 64, (2 * j + hp) * 64 + 64)
                    C.tt("pool", SAM[hp], HB(("m", 17), HBf[0:T, 17 * 128 + 64 * hp:17 * 128 + 64 * hp + 64].unsqueeze(1).broadcast_to([T, 8, 64])), segc_b, ALU.mult)
                    C.tt("pool", VM[hp], XS(None, XS.t[0:T, hc].unsqueeze(1).broadcast_to([T, 8, 64])), segc_b, ALU.mult)
                for n in range(nseq):
                    psS = next_ps8()
                    for hp in range(2):
                        rs = rsl[hp]
                        hc = slice((2 * j + hp) * 64, (2 * j + hp) * 64 + 64)
                        C.mm(psS(None, psS.t[rs, 0:64]), Btm(hc), CSS[hp](None, SAM[hp].ap[:, n, :]), start=True, stop=False)
                        C.mm(psS(None, psS.t[rs, 0:64]), Ktm(hc), CSO[hp](None, VM[hp].ap[:, n, :]), start=False, stop=True)
                    wl = WLB(None, WLB.t[:, j, n:n + 1])
                    C.ts("dve", tmpS, s0t(n, j), wl, None, ALU.mult)
                    C.stt("dve", s0t(n, j), psS(None, psS.t[:, 0:64]), wl, tmpS, ALU.mult, ALU.add)
            else:
                psS = next_ps8()
                for hp in range(2):
                    rs = rsl[hp]
                    hc = slice((2 * j + hp) * 64, (2 * j + hp) * 64 + 64)
                    C.mm(psS(None, psS.t[rs, 0:64]), Btm(hc), SAT[hp], start=True, stop=False)
                    C.mm(psS(None, psS.t[rs, 0:64]), Ktm(hc), Vtm(hc), start=False, stop=True)
                wl = WLB(None, WLB.t[:, j, 0:1])
                srj = SR(None, SR.t[:, j, :])
                C.ts("dve", tmpS, srj, wl, None, ALU.mult)
                C.stt("dve", srj, psS(None, psS.t[:, 0:64]), wl, tmpS, ALU.mult, ALU.add)

        def state_out(src_fn, dst):
            for j in range(8):
                ps = next_psb()
                C.tr(ps(None, ps.t[0:64, 0:128]), src_fn(j), ident.all())
                C.copy(ev_eng(), XS(None, XS.t[0:64, j * 128:(j + 1) * 128]), ps(None, ps.t[0:64, 0:128]))
            C.dma(dst.rearrange("(j hp) v k -> v j hp k", hp=2), XS(None, XS.t[0:64, 0:1024].rearrange("p (j hp k) -> p j hp k", j=8, hp=2)))

        if is_s:
            for n in range(nseq):
                state_out(lambda j, n=n: s0t(n, j), O["s_rs_o"][s0 + n])
        elif tile["last"]:
            state_out(lambda j: SR(None, SR.t[:, j, :]), O["p_rs"])


    def layer1(tile):
        T, nseq, L = tile["T"], tile["nseq"], tile["L"]
        is_s = tile["kind"] == "s"
        kd = tile["kind"]
        s0 = tile["s0"]
        xrhs = lambda kt: XTB(("t", kt), XTB.t[:, kt, 0:T])
        pj = lambda j: PJ(("t", j), PJ.t[:, j, 0:T])
        W = I["od_w_in"]

        def evacM(j, pv):
            if j < 8:
                C.act(pj(j), pv, AF.Identity, scale=1.0 / 16.0)
            elif j < 16:
                C.copy(ev_eng(), pj(j), pv)
            elif j < 24:
                C.act(pj(j), pv, AF.Sigmoid)
            elif j == 24:
                C.copy("dve", PJ(("t", 32), PJ.t[0:8, 32, 0:T]), pv)
            else:
                C.act(pj(j - 1), pv, AF.Silu)

        cols = [(i * 128, 128) for i in range(16)] + [(3072 + i * 128, 128) for i in range(8)] + [(4096, 8)] + \
               [(4104 + i * 128, 128) for i in range(8)]
        stream_mm(W, 16, cols, xrhs, T, evacM)

        def evacV(g, pv):
            C.copy(ev_eng(), VTM(None, VTM.t[0:T, 2 * g:2 * g + 2, 0:256]), pv.buf(None, pv.ap.rearrange("p (h v) -> p h v", h=2)))

        C.memset("pool", VTM(None, VTM.t[:, :, 256:257]), 1.0)
        stream_mm_tok(W, 2048, 1024, T, evacV)

        gx = GX(None, GX.t[0:8, 0:T])
        C.ts("dve", gx, PJ(("t", 32), PJ.t[0:8, 32, 0:T]), GB.all(), None, ALU.add)
        col = lambda a, b: COL(None, COL.t[0:T, a:b])
        ps = next_psb()
        C.tr(ps(None, ps.t[0:T, 0:8]), gx, ident(None, ident.t[0:8, 0:8]))
        C.copy("dve", col(0, 8), ps(None, ps.t[0:T, 0:8]))
        C.act(col(8, 12), col(4, 8), AF.Exp, scale=-1.0)
        C.act(col(8, 12), col(8, 12), AF.Ln, bias=1.0)
        C.ts("dve", col(8, 12), col(8, 12), -1.0, None, ALU.mult)
        ps = next_psb()
        C.mm(ps(None, ps.t[0:T, 0:4]), SEGTRI[kd](None, SEGTRI[kd].t[0:T, 0:T]), col(8, 12))
        C.copy("dve", col(12, 16), ps(None, ps.t[0:T, 0:4]))
        C.tt("dve", col(16, 20), col(0, 4), col(12, 16), ALU.subtract)
        if is_s:
            C.dma(MS.all(), I["s_mm"][s0:s0 + 8, :])
            ps = next_psb()
            C.mm(ps(None, ps.t[0:T, 0:4]), SEGSEL(None, SEGSEL.t[0:8, 0:T]), MS.all())
            C.copy("dve", col(20, 24), ps(None, ps.t[0:T, 0:4]))
            for h in range(4):
                C.copy("dve", MSB.all(), MS(None, MS.t[:, h:h + 1].broadcast_to([8, 128])))
                ps = next_psb()
                C.mm(ps(None, ps.t[:, 0:8]), MSB.all(), ident(None, ident.t[0:8, 0:8]))
                C.copy("dve", MINIT(None, MINIT.t[:, h, :]), ps(None, ps.t[:, 0:8]))
        else:
            if tile["first"]:
                C.memset("dve", MCAR.all(), 0.0)
                C.memset("pool", CS.all(), 0.0)
            C.copy("dve", col(20, 24), MCAR(None, MCAR.t[0:T, :]))
            C.copy("dve", MINIT(None, MINIT.t[:, :, 0:1]), MCAR(None, MCAR.t[:, :].unsqueeze(2)))

        row = lambda k: ROW(None, ROW.t[:, k, 0:T])
        ends = lambda k: ROW(None, ROW.t[:, k, 0:T].rearrange("p (n l) -> p n l", l=L)[:, :, L - 1])
        starts = lambda k: ROW(None, ROW.t[:, k, 0:T].rearrange("p (n l) -> p n l", l=L)[:, :, 0])
        for h in range(4):
            minit_r = MINIT(None, MINIT.t[:, h, 0:nseq])
            ps = next_psb()
            C.mm(ps(None, ps.t[:, 0:T]), ident(None, ident.t[0:8, h:h + 1].broadcast_to([8, 128])), gx)
            C.copy("dve", row(0), ps(None, ps.t[:, 0:T]))
            ps = next_psb()
            C.mm(ps(None, ps.t[:, 0:T]), ident(None, ident.t[0:8, 4 + h:5 + h].broadcast_to([8, 128])), gx)
            C.act(row(1), ps(None, ps.t[:, 0:T]), AF.Exp, scale=-1.0)
            C.act(row(1), row(1), AF.Ln, bias=1.0)
            C.ts("dve", row(1), row(1), -1.0, None, ALU.mult)
            scan(row(2), R01[kd](None, R01[kd].t[:, 0:T]), row(1), 0.0, ALU.mult, ALU.add)
            C.tt("dve", row(3), row(0), row(2), ALU.subtract)
            C.tt("dve", starts(3), starts(3), minit_r, ALU.max)
            scan(row(4), RNEG[kd](None, RNEG[kd].t[:, 0:T]), row(3), -1e30, ALU.add, ALU.max)
            C.copy("dve", ROW(None, ROW.t[:, 5, 0:T].rearrange("p (n l) -> p n l", l=L)),
                   ROW(None, ROW.t[:, 4, 0:T].rearrange("p (n l) -> p n l", l=L)[:, :, L - 1:L].broadcast_to([128, nseq, L])))
            C.tt("dve", MNEW(None, MNEW.t[:, h, 0:nseq]), ends(2), ends(4), ALU.add)
            C.tt("dve", DEC(None, DEC.t[:, h, 0:nseq]), minit_r, ends(4), ALU.subtract)
            C.act(DEC(None, DEC.t[:, h, 0:nseq]), DEC(None, DEC.t[:, h, 0:nseq]), AF.Exp)
            ps = next_psb()
            C.mm(ps(None, ps.t[0:T, 0:1]), row(4), ident(None, ident.t[:, 0:1]))
            C.mm(ps(None, ps.t[0:T, 1:2]), row(5), ident(None, ident.t[:, 0:1]))
            C.copy("dve", col(24, 26), ps(None, ps.t[0:T, 0:2]))
            C.tt("dve", DTB(None, DTB.t[0:T, 0:T]), ROW(None, ROW.t[0:T, 4, 0:T]), MASKBIG[kd](None, MASKBIG[kd].t[0:T, 0:T]), ALU.add)
            C.act(DTB(None, DTB.t[0:T, 0:T]), DTB(None, DTB.t[0:T, 0:T]), AF.Exp, scale=-1.0, bias=col(16 + h, 17 + h))
            ps = next_psa()
            for kt in range(2):
                C.mm(ps(None, ps.t[0:T, 0:T]), pj(8 + 2 * h + kt), pj(2 * h + kt), start=(kt == 0), stop=(kt == 1))
            C.tt("dve", STB(None, STB.t[0:T, 0:T]), ps(None, ps.t[0:T, 0:T]), DTB(None, DTB.t[0:T, 0:T]), ALU.mult)
            ps1 = next_psa()
            C.mm(ps1(None, ps1.t[0:T, 0:257]), STB(None, STB.t[0:T, 0:T]), VTM(None, VTM.t[0:T, h, :]))
            C.copy("act", P1S(None, P1S.t[0:T, :]), ps1(None, ps1.t[0:T, 0:257]))
            ps2 = next_psa()
            if is_s:
                i_ = 0
                for n in range(nseq):
                    cs = CSS[n % 2]
                    C.dma(cs(None, cs.t[:, :, 0:256]), I["s_mc"][s0 + n, h].rearrange("(kt p) v -> p kt v", p=128))
                    C.dma(cs(None, cs.t[:, :, 256:257]), I["s_mn"][s0 + n, h].rearrange("(kt p o) -> p kt o", p=128, o=1), slow=True)
                    for kt in range(2):
                        qm = QM[i_ % 2]
                        i_ += 1
                        C.tt("pool", qm(None, qm.t[:, 0:T]), pj(2 * h + kt), SEGROW(None, SEGROW.t[:, n, 0:T]), ALU.mult)
                        C.mm(ps2(None, ps2.t[0:T, 0:257]), qm(None, qm.t[:, 0:T]), cs(None, cs.t[:, kt, :]),
                             start=(n == 0 and kt == 0), stop=(n == nseq - 1 and kt == 1))
            else:
                for kt in range(2):
                    C.mm(ps2(None, ps2.t[0:T, 0:257]), pj(2 * h + kt), CS(("h", h), CS.t[:, h, kt, :]), start=(kt == 0), stop=(kt == 1))
            sm = lambda a: SM(None, SM.t[0:T, a:a + 1])
            C.tt("dve", sm(0), col(20 + h, 21 + h), col(24, 25), ALU.subtract)
            C.act(sm(0), sm(0), AF.Exp)
            C.stt("dve", NUM(None, NUM.t[0:T, :]), ps2(None, ps2.t[0:T, 0:257]), sm(0), P1S(None, P1S.t[0:T, :]), ALU.mult, ALU.add)
            C.tt("dve", sm(1), col(12 + h, 13 + h), col(24, 25), ALU.add)
            C.act(sm(1), sm(1), AF.Exp, scale=-1.0)
            C.ts("dve", sm(2), NUM(None, NUM.t[0:T, 256:257]), -1.0, None, ALU.mult)
            C.tt("dve", sm(2), sm(2), NUM(None, NUM.t[0:T, 256:257]), ALU.max)
            C.tt("dve", sm(2), sm(2), sm(1), ALU.max)
            recip("dve", sm(2), sm(2))
            C.op("dve", lambda g, T=T: g.reduce_sum(out=SM.t[0:T, 3:4], in_=NUM.t[0:T, 0:256], axis=AX.X),
                 reads=[NUM(None, NUM.t[0:T, 0:256])], writes=[sm(3)])
            C.tt("dve", sm(3), sm(3), sm(2), ALU.mult)
            C.ts("dve", sm(3), sm(3), 1.0 / 256, None, ALU.mult)
            hn = HN(None, HN.t[0:T, :])
            C.ts("dve", hn, NUM(None, NUM.t[0:T, 0:256]), sm(2), sm(3), ALU.mult, ALU.subtract)
            C.tt("dve", P1S(None, P1S.t[0:T, 0:256]), hn, hn, ALU.mult)
            C.op("dve", lambda g, T=T: g.reduce_sum(out=SM.t[0:T, 4:5], in_=P1S.t[0:T, 0:256], axis=AX.X),
                 reads=[P1S(None, P1S.t[0:T, 0:256])], writes=[sm(4)])
            C.ts("dve", sm(4), sm(4), 1.0 / 256, LN_EPS, ALU.mult, ALU.add)
            C.act(sm(4), sm(4), AF.Sqrt)
            recip("dve", sm(4), sm(4))
            C.ts("dve", hn, hn, sm(4), None, ALU.mult)
            for kt in range(2):
                ct = 2 * h + kt
                ps = next_psb()
                C.tr(ps(None, ps.t[:, 0:T]), HN(None, HN.t[0:T, kt * 128:(kt + 1) * 128]), ident(None, ident.t[0:T, 0:T]))
                scr = P1S(None, P1S.t[:, 0:T])
                C.stt("dve", scr, ps(None, ps.t[:, 0:T]), VEC1(None, VEC1.t[:, ct, 0:1]), pj(16 + ct), ALU.mult, ALU.mult)
                C.tt("dve", mixv(ct, T), scr, pj(24 + ct), ALU.mult)
            C.tt("dve", sm(5), col(16 + h, 17 + h), col(25, 26), ALU.subtract)
            C.act(sm(5), sm(5), AF.Exp)
            for kt in range(2):
                ps = next_psb()
                C.tr(ps(None, ps.t[0:T, 0:128]), pj(8 + 2 * h + kt), ident.all())
                C.ts("dve", KW(None, KW.t[0:T, h * 256 + kt * 128:h * 256 + (kt + 1) * 128]), ps(None, ps.t[0:T, 0:128]), sm(5), None, ALU.mult)
            if is_s:
                for n in range(nseq):
                    cs = CSS[n % 2]
                    co = CSO[n % 2]
                    C.dma(cs(None, cs.t[:, :, 0:256]), I["s_mc"][s0 + n, h].rearrange("(kt p) v -> p kt v", p=128))
                    C.dma(cs(None, cs.t[:, :, 256:257]), I["s_mn"][s0 + n, h].rearrange("(kt p o) -> p kt o", p=128, o=1), slow=True)
                    C.ts("pool", KWN(None, KWN.t[0:T, :]), KW(None, KW.t[0:T, h * 256:(h + 1) * 256]), SEGC(None, SEGC.t[0:T, n:n + 1]), None, ALU.mult)
                    for kt in range(2):
                        ps = next_psa()
                        C.mm(ps(None, ps.t[:, 0:257]), KWN(None, KWN.t[0:T, kt * 128:(kt + 1) * 128]), VTM(None, VTM.t[0:T, h, :]))
                        C.stt("dve", co(None, co.t[:, kt, :]), cs(None, cs.t[:, kt, :]), DEC(None, DEC.t[:, h, n:n + 1]), ps(None, ps.t[:, 0:257]), ALU.mult, ALU.add)
                    C.dma(O["s_mc_o"][s0 + n, h].rearrange("(kt p) v -> p kt v", p=128), co(None, co.t[:, :, 0:256]))
                    C.dma(O["s_mn_o"][s0 + n, h].rearrange("(kt p o) -> p kt o", p=128, o=1), co(None, co.t[:, :, 256:257]), slow=True)
                C.dma(O["s_mm_o"][s0:s0 + 8, h:h + 1].rearrange("n o -> o n"), MNEW(None, MNEW.t[0:1, h, 0:8]), slow=True)
            else:
                for kt in range(2):
                    ps = next_psa()
                    C.mm(ps(None, ps.t[:, 0:257]), KW(None, KW.t[0:T, h * 256 + kt * 128:h * 256 + (kt + 1) * 128]), VTM(None, VTM.t[0:T, h, :]))
                    csv = CS(("h", h), CS.t[:, h, kt, :])
                    C.stt("dve", csv, csv, DEC(None, DEC.t[:, h, 0:1]), ps(None, ps.t[:, 0:257]), ALU.mult, ALU.add)
                C.copy("dve", MCAR(None, MCAR.t[:, h:h + 1]), MNEW(None, MNEW.t[:, h, 0:1]))
        if (not is_s) and tile["last"]:
            for h in range(4):
                C.dma(O["p_mc"][h].rearrange("(kt p) v -> p kt v", p=128), CS(("h", h), CS.t[:, h, :, 0:256]))
                C.dma(O["p_mn"][h].rearrange("(kt p o) -> p kt o", p=128, o=1), CS(("h", h), CS.t[:, h, :, 256:257]), slow=True)
            C.dma(O["p_mm"][:, :], MCAR(None, MCAR.t[0:1, :]))

        if do_rwkv:
            (rwkv2 if cfg.get("rwkv2", True) else rwkv)(tile)
        else:
            for ct in range(8, 16):
                C.memset("pool", mixv(ct, T), 0.0)
        out_proj_ln(I["od_w_out"], tile, VECO, 0, 1)

    for ti, tile in enumerate(tile_plan(cfg)):
        wst["id"] = 0
        wst["first"] = (ti == 0)
        load_x(tile)
        if nlayers >= 1:
            layer0(tile)
        if nlayers >= 2:
            layer1(tile)
        store_y(tile)

    C.emit()
    es.close()
    return nc, C


def make_in_maps(inp, cores, consts):
    maps = []
    f = lambda a: np.ascontiguousarray(a, dtype=np.float32)
    for c in cores:
        s = c % 4
        m = {}
        m["xp"] = f(inp["x_prompt"][s])
        m["meta"] = f(inp["meta_tokens"])
        m["xs"] = f(inp["x_sample"][16 * c:16 * c + 16].reshape(128, D))
        m["s_conv"] = f(inp["state_conv"][0, 16 * c:16 * c + 16].reshape(480, 1024))
        m["s_sre"] = f(inp["state_ssm_re"][0, 16 * c:16 * c + 16])
        m["s_sim"] = f(inp["state_ssm_im"][0, 16 * c:16 * c + 16])
        m.update(consts)
        sl = slice(16 * c, 16 * c + 16)
        m["s_mc"] = f(inp["state_mlstm_c"][0, sl])
        m["s_mn"] = f(inp["state_mlstm_n"][0, sl])
        m["s_mm"] = f(inp["state_mlstm_m"][0, sl])
        m["s_rs"] = f(inp["state_rwkv_s"][0, sl])
        m["s_rsh"] = f(inp["state_rwkv_shift"][0, sl])
        m["od_w_in"] = f(inp["od_w_in"][0])
        for nm in ("m_ig_b", "m_fg_b", "m_hn_g", "r_w0", "r_a0", "r_kk", "r_ka", "r_ln_g", "r_ln_b", "r_rk", "r_mu", "od_ln_g", "od_ln_b"):
            m[nm] = f(inp[nm][0].reshape(1, -1))
        for nm in ("r_w2", "r_a2", "od_w_out"):
            m[nm] = f(inp[nm][0])
        m["ev_w_in"] = f(inp["ev_w_in"][0])
        m["a_conv_w"] = f(inp["a_conv_w"][0])
        for nm in ("a_conv_b", "a_ln_g", "a_ln_b", "s5_d", "s5_log_dt", "s5_glu_b", "ev_ln_g", "ev_ln_b"):
            m[nm] = f(inp[nm][0].reshape(1, -1))
        m["a_pw"] = f(inp["a_pw"][0])
        for nm in ("s5_lambda_re", "s5_lambda_im", "s5_b_re", "s5_b_im", "s5_glu_w", "ev_w_out"):
            m[nm] = f(inp[nm][0])
        m["s5_c_re"] = f(inp["s5_c_re"][0].reshape(1024, 64))
        m["s5_c_im"] = f(inp["s5_c_im"][0].reshape(1024, 64))
        maps.append(m)
    return maps


def kernel(**inp):
    cfg = {}
    nc, C = build(cfg)
    consts = make_consts()
    cores = list(range(NCORES))
    maps = make_in_maps(inp, cores, consts)
    res = run_bass_kernel_spmd(nc, maps, core_ids=cores)
    R = res.results
    B = 4
    cat = lambda k, shp: np.concatenate([R[c][k].reshape((16,) + shp) for c in range(NCORES)], 0)[None]
    stk = lambda k, shp: np.stack([R[c][k].reshape(shp) for c in range(B)], 0)[None]
    y_p = np.stack([R[c]["y_p"] for c in range(B)], 0)
    y_s = np.concatenate([R[c]["y_s"].reshape(16, 8, D) for c in range(NCORES)], 0)
    return (y_p, y_s,
            stk("p_conv", (30, 1024)), stk("p_sre", (64, 64)), stk("p_sim", (64, 64)), stk("p_mc", (4, 256, 256)),
            stk("p_mn", (4, 256)), stk("p_mm", (4,)), stk("p_rs", (16, 64, 64)), stk("p_rsh", (3200,)),
            cat("s_conv_o", (30, 1024)), cat("s_sre_o", (64, 64)), cat("s_sim_o", (64, 64)), cat("s_mc_o", (4, 256, 256)),
            cat("s_mn_o", (4, 256)), cat("s_mm_o", (4,)), cat("s_rs_o", (16, 64, 64)), cat("s_rsh_o", (3200,)))
```

```python
import contextlib
import numpy as np
import concourse.bass as bass
import concourse.mybir as mybir
from concourse.bass_utils import run_bass_kernel_spmd

F32 = mybir.dt.float32
BF16 = mybir.dt.bfloat16
AF = mybir.ActivationFunctionType
ALU = mybir.AluOpType
AX = mybir.AxisListType

D = 2048
TT = 128
WG = 512
KQ = 4
NWS = 4
NSLAB = 200
NCORES = 8
ALPHA = 4 ** 0.25
LN_EPS = 1e-5


class Reg:
    __slots__ = ("w", "r")

    def __init__(self):
        self.w = None
        self.r = []


class Buf:
    def __init__(self, ctx, name, t):
        self.ctx, self.name, self.t = ctx, name, t
        self.regs = {"_all": Reg()}
        self.dma_sem = None
        self.dma_cnt = 0

    def __call__(self, key, ap):
        return View(self, key, ap)

    def all(self):
        return View(self, None, self.t[:])

    def _sel(self, key):
        if key is None:
            return list(self.regs.values())
        if key not in self.regs:
            self.regs[key] = Reg()
        return [self.regs[key], self.regs["_all"]]

    def rdeps(self, key):
        return [r.w for r in self._sel(key) if r.w is not None]

    def wdeps(self, key):
        out = []
        for r in self._sel(key):
            if r.w is not None:
                out.append(r.w)
            out.extend(r.r)
        return out

    def note_read(self, key, tok):
        if key is None:
            for r in self.regs.values():
                r.r.append(tok)
        else:
            self._sel(key)[0].r.append(tok)

    def note_write(self, key, tok):
        if key is None:
            self.regs = {"_all": Reg()}
            self.regs["_all"].w = tok
        else:
            r = self._sel(key)[0]
            r.w = tok
            r.r = []


class View:
    __slots__ = ("buf", "key", "ap")

    def __init__(self, buf, key, ap):
        self.buf, self.key, self.ap = buf, key, ap


class Ctx:
    ENG = ("pe", "act", "dve", "pool", "sp")
    EPOCH = 30000

    def __init__(self, nc, es):
        self.nc, self.es = nc, es
        self.prog = {e: [] for e in self.ENG}
        self.cnt = {e: 0 for e in self.ENG}
        self.sem = {e: es.enter_context(nc.semaphore("sem_" + e)) for e in self.ENG}
        self.known = {e: {} for e in self.ENG}
        self.final = []
        self.total = {}
        self.nsem = 5
        self.nbytes = 0

    def sb(self, name, shape, dtype=F32):
        t = self.es.enter_context(self.nc.sbuf_tensor("sb_" + name, list(shape), dtype))
        n = 4
        for s in shape[1:]:
            n *= s
        self.nbytes += n
        return Buf(self, name, t)

    def ps(self, name, shape, dtype=F32):
        t = self.es.enter_context(self.nc.psum_tensor("ps_" + name, list(shape), dtype))
        return Buf(self, name, t)

    def need(self, e, tok):
        sem, val = tok
        k = id(sem)
        if self.known[e].get(k, 0) >= val:
            return
        self.known[e][k] = val
        self.prog[e].append(("wait", sem, val))

    def op(self, e, fn, reads=(), writes=()):
        for v in reads:
            if isinstance(v, View):
                for tok in v.buf.rdeps(v.key):
                    self.need(e, tok)
        for v in writes:
            if isinstance(v, View):
                for tok in v.buf.wdeps(v.key):
                    self.need(e, tok)
        if self.cnt[e] >= self.EPOCH:
            self.total[e] = self.total.get(e, 0) + self.cnt[e]
            self.sem[e] = self.es.enter_context(self.nc.semaphore("sem_%s_%d" % (e, self.total[e])))
            self.cnt[e] = 0
            self.nsem += 1
        self.cnt[e] += 1
        tok = (self.sem[e], self.cnt[e])
        self.prog[e].append(("op", fn, self.sem[e], 1))
        for v in reads:
            if isinstance(v, View):
                v.buf.note_read(v.key, tok)
        for v in writes:
            if isinstance(v, View):
                v.buf.note_write(v.key, tok)
        return tok

    def dma(self, out, in_, q="sp", slow=False):
        sbv = out if isinstance(out, View) else in_
        b = sbv.buf
        if b.dma_sem is None:
            b.dma_sem = self.es.enter_context(self.nc.semaphore("dq_" + b.name))
            self.nsem += 1
        if isinstance(in_, View):
            for tok in in_.buf.rdeps(in_.key):
                self.need(q, tok)
        if isinstance(out, View):
            for tok in out.buf.wdeps(out.key):
                self.need(q, tok)
        b.dma_cnt += 16
        tok = (b.dma_sem, b.dma_cnt)
        oap = out.ap if isinstance(out, View) else out
        iap = in_.ap if isinstance(in_, View) else in_
        if slow:
            fn = lambda eng, oap=oap, iap=iap: eng.dma_start(out=oap, in_=iap, allow_slow_non_contiguous=True)
        else:
            fn = lambda eng, oap=oap, iap=iap: eng.dma_start(out=oap, in_=iap)
        self.prog[q].append(("op", fn, b.dma_sem, 16))
        if isinstance(in_, View):
            in_.buf.note_read(in_.key, tok)
        if isinstance(out, View):
            out.buf.note_write(out.key, tok)
        else:
            self.final.append(tok)
        return tok

    def tt(self, e, out, in0, in1, op):
        return self.op(e, lambda g: g.tensor_tensor(out=out.ap, in0=in0.ap, in1=in1.ap, op=op),
                       reads=[in0, in1], writes=[out])

    def ts(self, e, out, in0, s1, s2, op0, op1=None):
        rd = [in0] + [s for s in (s1, s2) if isinstance(s, View)]
        a1 = s1.ap if isinstance(s1, View) else s1
        a2 = s2.ap if isinstance(s2, View) else s2
        if op1 is None:
            return self.op(e, lambda g: g.tensor_scalar(out=out.ap, in0=in0.ap, scalar1=a1, scalar2=None, op0=op0),
                           reads=rd, writes=[out])
        return self.op(e, lambda g: g.tensor_scalar(out=out.ap, in0=in0.ap, scalar1=a1, scalar2=a2, op0=op0, op1=op1),
                       reads=rd, writes=[out])

    def stt(self, e, out, in0, s, in1, op0, op1):
        rd = [in0, in1] + ([s] if isinstance(s, View) else [])
        a = s.ap if isinstance(s, View) else s
        return self.op(e, lambda g: g.scalar_tensor_tensor(out=out.ap, in0=in0.ap, scalar=a, in1=in1.ap, op0=op0, op1=op1),
                       reads=rd, writes=[out])

    def act(self, out, in_, func, bias=None, scale=None, e="act"):
        rd = [in_] + [s for s in (bias, scale) if isinstance(s, View)]
        kw = {}
        if bias is not None:
            kw["bias"] = bias.ap if isinstance(bias, View) else bias
        if scale is not None:
            kw["scale"] = scale.ap if isinstance(scale, View) else scale
        return self.op(e, lambda g: g.activation(out=out.ap, in_=in_.ap, func=func, **kw), reads=rd, writes=[out])

    def copy(self, e, out, in_):
        if e == "act":
            return self.act(out, in_, AF.Copy)
        return self.op(e, lambda g: g.tensor_copy(out=out.ap, in_=in_.ap), reads=[in_], writes=[out])

    def memset(self, e, out, val):
        return self.op(e, lambda g: g.memset(out.ap, val), writes=[out])

    def mm(self, out, lhsT, rhs, start=True, stop=True):
        return self.op("pe", lambda g: g.matmul(out.ap, lhsT=lhsT.ap, rhs=rhs.ap, start=start, stop=stop),
                       reads=[lhsT, rhs], writes=[out])

    def tr(self, out, in_, ident):
        return self.op("pe", lambda g: g.transpose(out.ap, in_.ap, ident.ap), reads=[in_, ident], writes=[out])

    def emit(self):
        nc = self.nc
        for tok in self.final:
            self.need("sp", tok)
        for e in self.ENG:
            if e != "sp" and self.cnt[e] > 0:
                self.need("sp", (self.sem[e], self.cnt[e]))
        engs = {"pe": "tensor", "act": "scalar", "dve": "vector", "pool": "gpsimd", "sp": "sync"}
        with nc.Block() as block:
            for e, attr in engs.items():
                items = self.prog[e]

                def body(eng, items=items):
                    for it in items:
                        if it[0] == "wait":
                            eng.wait_ge(it[1], it[2])
                        else:
                            it[1](eng).then_inc(it[2], it[3])

                getattr(block, attr)(body)


def make_consts():
    c = {}
    c["ident"] = np.eye(128, dtype=np.float32)
    c["ones"] = np.ones((128, 128), dtype=np.float32)
    r = np.arange(128)
    mrow = (r % 32) // 16
    mcol = np.arange(128) // 64
    bm = (mrow[:, None] == mcol[None, :]).astype(np.float32)
    ev_r = ((r // 32) % 2 == 0).astype(np.float32)
    c["bmask"] = bm * ev_r[:, None]
    c["bmask_o"] = bm * (1 - ev_r)[:, None]
    gl = np.arange(128) // 16
    cm = ((gl[None, :] % 2) == (r[:, None] // 64)).astype(np.float32)
    c["cmask"] = cm * ev_r[None, :]
    c["cmask_o"] = cm * (1 - ev_r)[None, :]
    BIG = 1e30
    i128 = np.arange(128)
    allow_p = (i128[:, None] <= i128[None, :])
    c["maskbig_p"] = np.where(allow_p, 0.0, BIG).astype(np.float32)
    c["segtri_p"] = allow_p.astype(np.float32)
    i64 = np.arange(64)
    allow_s = (i64[:, None] <= i64[None, :]) & ((i64[:, None] // 8) == (i64[None, :] // 8))
    c["maskbig_s"] = np.where(allow_s, 0.0, BIG).astype(np.float32)
    c["segtri_s"] = allow_s.astype(np.float32)
    r01p = np.ones((128, 128), np.float32); r01p[:, 0] = 0
    r01s = np.ones((128, 64), np.float32); r01s[:, ::8] = 0
    c["r01_p"], c["r01_s"] = r01p, r01s
    c["rneg_p"] = ((1 - r01p) * -BIG).astype(np.float32)
    c["rneg_s"] = ((1 - r01s) * -BIG).astype(np.float32)
    segsel = ((i64[None, :] // 8) == np.arange(8)[:, None]).astype(np.float32)
    c["segsel"] = segsel
    c["segc"] = np.ascontiguousarray(segsel.T)
    c["segrow"] = np.ascontiguousarray(np.broadcast_to(segsel.reshape(1, 512), (128, 512))).astype(np.float32)
    c["blk"] = ((i128[:, None] // 64) == (i128[None, :] // 64)).astype(np.float32)
    st_p = (i128[:, None] < i128[None, :])
    st_s = (i64[:, None] < i64[None, :]) & ((i64[:, None] // 8) == (i64[None, :] // 8))
    c["sstri_p"] = st_p.astype(np.float32)
    c["sstriT_p"] = np.ascontiguousarray(st_p.T).astype(np.float32)
    c["sstri_s"] = st_s.astype(np.float32)
    c["sstriT_s"] = np.ascontiguousarray(st_s.T).astype(np.float32)
    return c


CONST_SHAPES = {"ident": [128, 128], "ones": [128, 128], "bmask": [128, 128], "cmask": [128, 128],
                "bmask_o": [128, 128], "cmask_o": [128, 128],
                "maskbig_p": [128, 128], "segtri_p": [128, 128], "maskbig_s": [64, 64], "segtri_s": [64, 64],
                "r01_p": [128, 128], "r01_s": [128, 64], "rneg_p": [128, 128], "rneg_s": [128, 64],
                "segsel": [8, 64], "segc": [64, 8], "segrow": [128, 512], "blk": [128, 128],
                "sstri_p": [128, 128], "sstriT_p": [128, 128], "sstri_s": [64, 64], "sstriT_s": [64, 64]}


def tile_plan(cfg):
    tiles = []
    npt = cfg.get("n_prompt_tiles", 17)
    pos = 0
    for i in range(17):
        T = 16 if i == 16 else TT
        if i < npt:
            tiles.append(dict(kind="p", T=T, pos=pos, nseq=1, L=T, first=(i == 0), last=(i == npt - 1), s0=0))
        pos += T
    if cfg.get("sample", True):
        for h in range(2):
            tiles.append(dict(kind="s", T=64, pos=0, nseq=8, L=8, first=True, last=True, s0=8 * h))
    return tiles


def build(cfg):
    nc = bass.Bass("TRN2", target_bir_lowering=False)
    es = contextlib.ExitStack()
    C = Ctx(nc, es)
    dbg = cfg.get("debug", False)
    nlayers = cfg.get("layers", 2)

    def din(name, shape):
        return nc.dram_tensor(name, list(shape), F32, kind="ExternalInput").ap()

    def dout(name, shape):
        return nc.dram_tensor(name, list(shape), F32, kind="ExternalOutput").ap()

    I = {}
    I["xp"] = din("xp", [2048, D])
    I["meta"] = din("meta", [16, D])
    I["xs"] = din("xs", [128, D])
    I["s_conv"] = din("s_conv", [16 * 30, 1024])
    I["s_sre"] = din("s_sre", [16, 64, 64])
    I["s_sim"] = din("s_sim", [16, 64, 64])
    for nm, shp in CONST_SHAPES.items():
        I[nm] = din(nm, shp)
    I["s_mc"] = din("s_mc", [16, 4, 256, 256])
    I["s_mn"] = din("s_mn", [16, 4, 256])
    I["s_mm"] = din("s_mm", [16, 4])
    I["s_rs"] = din("s_rs", [16, 16, 64, 64])
    I["s_rsh"] = din("s_rsh", [16, 3200])
    I["od_w_in"] = din("od_w_in", [D, 9352])
    I["m_ig_b"] = din("m_ig_b", [1, 4])
    I["m_fg_b"] = din("m_fg_b", [1, 4])
    for nm in ("m_hn_g", "r_w0", "r_a0", "r_kk", "r_ka", "r_ln_g", "r_ln_b", "r_rk"):
        I[nm] = din(nm, [1, 1024])
    I["r_mu"] = din("r_mu", [1, 3200])
    I["r_w2"] = din("r_w2", [64, 1024])
    I["r_a2"] = din("r_a2", [64, 1024])
    I["od_w_out"] = din("od_w_out", [2048, D])
    I["od_ln_g"] = din("od_ln_g", [1, D])
    I["od_ln_b"] = din("od_ln_b", [1, D])
    I["ev_w_in"] = din("ev_w_in", [D, 5120])
    I["a_conv_w"] = din("a_conv_w", [31, 1024])
    for nm in ("a_conv_b", "a_ln_g", "a_ln_b", "s5_d"):
        I[nm] = din(nm, [1, 1024])
    I["a_pw"] = din("a_pw", [1024, 1024])
    I["s5_lambda_re"] = din("s5_lambda_re", [64, 64])
    I["s5_lambda_im"] = din("s5_lambda_im", [64, 64])
    I["s5_log_dt"] = din("s5_log_dt", [1, 64])
    I["s5_b_re"] = din("s5_b_re", [64, 64, 16])
    I["s5_b_im"] = din("s5_b_im", [64, 64, 16])
    I["s5_c_re"] = din("s5_c_re", [1024, 64])
    I["s5_c_im"] = din("s5_c_im", [1024, 64])
    I["s5_glu_w"] = din("s5_glu_w", [1024, 2048])
    I["s5_glu_b"] = din("s5_glu_b", [1, 2048])
    I["ev_w_out"] = din("ev_w_out", [2048, D])
    I["ev_ln_g"] = din("ev_ln_g", [1, D])
    I["ev_ln_b"] = din("ev_ln_b", [1, D])

    O = {}
    O["y_p"] = dout("y_p", [2048, D])
    O["y_s"] = dout("y_s", [128, D])
    O["p_conv"] = dout("p_conv", [30, 1024])
    O["p_sre"] = dout("p_sre", [64, 64])
    O["p_sim"] = dout("p_sim", [64, 64])
    O["s_conv_o"] = dout("s_conv_o", [16 * 30, 1024])
    O["s_sre_o"] = dout("s_sre_o", [16, 64, 64])
    O["s_sim_o"] = dout("s_sim_o", [16, 64, 64])
    O["p_mc"] = dout("p_mc", [4, 256, 256])
    O["p_mn"] = dout("p_mn", [4, 256])
    O["p_mm"] = dout("p_mm", [1, 4])
    O["p_rs"] = dout("p_rs", [16, 64, 64])
    O["p_rsh"] = dout("p_rsh", [1, 3200])
    O["s_mc_o"] = dout("s_mc_o", [16, 4, 256, 256])
    O["s_mn_o"] = dout("s_mn_o", [16, 4, 256])
    O["s_mm_o"] = dout("s_mm_o", [16, 4])
    O["s_rs_o"] = dout("s_rs_o", [16, 16, 64, 64])
    O["s_rsh_o"] = dout("s_rsh_o", [16, 3200])

    ident = C.sb("ident", [128, 128])
    ones = C.sb("ones", [128, 128])
    XT = C.sb("XT", [128, 16, TT])
    XS = C.sb("XS", [128, D])
    PJ = C.sb("PJ", [128, 33, TT])
    MIX = C.sb("MIX", [128, 16 * TT], BF16)
    XTB = C.sb("XTB", [128, 16, TT], BF16)
    GELB = C.sb("GELB", [128, 8, TT], BF16)
    WS = [C.sb("WS%d" % i, [128, KQ, WG], BF16) for i in range(NWS)]
    HB = C.sb("HB", [128, 8, 304])
    HC = C.sb("HC", [128, 8, 30])
    ACC = C.sb("ACC", [128, 8, TT])
    SQ2 = C.sb("SQ2", [128, 2, TT])
    ST = C.sb("ST", [128, 3, TT])
    VEC0 = C.sb("VEC0", [128, 8, 40])
    VECG = C.sb("VECG", [128, 16, 4])
    TMP = C.sb("TMP", [128, 128])
    XR = C.sb("XR", [128, 32, 129])
    XI = C.sb("XI", [128, 32, 129])
    S5C = C.sb("S5C", [128, 2, 32])
    S5A = C.sb("S5A", [128, 8, 32])
    S5T = C.sb("S5T", [128, 10, 32])
    SCS = C.sb("SCS", [128, 2, 32, 8])
    BRE = [C.sb("BRE%d" % i, [128, 8, 128]) for i in range(2)]
    BIM = [C.sb("BIM%d" % i, [128, 8, 128]) for i in range(2)]
    CRE = [C.sb("CRE%d" % i, [128, 8, 128]) for i in range(2)]
    CIM = [C.sb("CIM%d" % i, [128, 8, 128]) for i in range(2)]
    mask_bo = C.sb("mask_bo", [128, 128])
    mask_co = C.sb("mask_co", [128, 128])
    S5ST = C.sb("S5ST", [128, 128])
    mask_b = C.sb("mask_b", [128, 128])
    mask_c = C.sb("mask_c", [128, 128])

    PSA = [C.ps("PSA%d" % i, [128, 512]) for i in range(4)]
    PSB = [C.ps("PSB%d" % i, [128, 512]) for i in range(4)]

    def mixv(ct, T):
        return MIX(("t", ct), MIX.t[:, ct * TT:ct * TT + T])

    C.dma(ident.all(), I["ident"])
    C.dma(ones.all(), I["ones"])
    C.dma(mask_b.all(), I["bmask"])
    C.dma(mask_c.all(), I["cmask"])
    C.dma(mask_bo.all(), I["bmask_o"])
    C.dma(mask_co.all(), I["cmask_o"])

    rr = {"psa": 0, "psb": 0, "ws": 0, "ev": 0, "sq": 0, "tm": 0}

    def next_psa():
        rr["psa"] = (rr["psa"] + 1) % 4
        return PSA[rr["psa"]]

    def next_psb():
        rr["psb"] = (rr["psb"] + 1) % 4
        return PSB[rr["psb"]]

    def ev_eng():
        rr["ev"] += 1
        return "dve" if rr["ev"] % 2 else "act"

    def load_rows_T(dst, rows, ncols, col0=0, ct0=0):
        r0 = 0
        for ap in rows:
            nr = ap.shape[0]
            C.dma(XS(None, XS.t[r0:r0 + nr, 0:ncols]), ap)
            r0 += nr
        nr = r0
        for ct in range(ncols // 128):
            ps = next_psb()
            C.tr(ps(None, ps.t[:, 0:nr]), XS(None, XS.t[0:nr, ct * 128:(ct + 1) * 128]), ident(None, ident.t[0:nr, 0:nr]))
            C.copy("dve", dst(None, dst.t[:, ct0 + ct, col0:col0 + nr]), ps(None, ps.t[:, 0:nr]))

    load_rows_T(VEC0, [I["a_conv_w"], I["a_conv_b"], I["a_ln_g"], I["a_ln_b"], I["s5_d"]], 1024)
    load_rows_T(VECG, [I["s5_glu_b"], I["ev_ln_g"], I["ev_ln_b"]], 2048)

    def gp_ap(ap2d):
        return ap2d.rearrange("(q m) p -> (m p) q", m=2)

    LR, LI, DT, AR, AI, NAI = range(6)
    A = lambda k: S5A(None, S5A.t[:, k, :])
    Tm = lambda k: S5T(None, S5T.t[:, k, :])
    for m in range(2):
        C.dma(S5A(None, S5A.t[64 * m:64 * m + 64, LR, :]), I["s5_lambda_re"].rearrange("(q m) p -> m p q", m=2)[m], slow=True)
        C.dma(S5A(None, S5A.t[64 * m:64 * m + 64, LI, :]), I["s5_lambda_im"].rearrange("(q m) p -> m p q", m=2)[m], slow=True)
    ldt = I["s5_log_dt"]
    for m in range(2):
        src = bass.AP(ldt.tensor, ldt.offset + m, [[0, 64], [2, 32]])
        C.dma(S5A(None, S5A.t[64 * m:64 * m + 64, DT, :]), src, slow=True)
    C.act(A(DT), A(DT), AF.Exp)
    C.tt("dve", Tm(0), A(LR), A(DT), ALU.mult)
    C.act(Tm(1), Tm(0), AF.Exp)
    C.tt("dve", Tm(2), A(LI), A(DT), ALU.mult)
    PI = float(np.pi)
    C.ts("dve", Tm(3), Tm(2), 1.0 / 32, None, ALU.mult)
    C.act(Tm(4), Tm(3), AF.Sin)
    C.ts("dve", Tm(3), Tm(3), PI / 2, None, ALU.add)
    C.act(Tm(5), Tm(3), AF.Sin)
    for _ in range(5):
        C.tt("dve", Tm(3), Tm(4), Tm(5), ALU.mult)
        C.tt("dve", Tm(8), Tm(5), Tm(5), ALU.mult)
        C.tt("dve", Tm(9), Tm(4), Tm(4), ALU.mult)
        C.ts("dve", Tm(4), Tm(3), 2.0, None, ALU.mult)
        C.tt("dve", Tm(5), Tm(8), Tm(9), ALU.subtract)
    C.tt("dve", A(AR), Tm(1), Tm(5), ALU.mult)
    C.tt("dve", A(AI), Tm(1), Tm(4), ALU.mult)
    C.ts("dve", A(NAI), A(AI), -1.0, None, ALU.mult)
    MAG = 6
    C.copy("dve", A(MAG), Tm(1))
    s5tab = nc.dram_tensor("s5tab", [2, 128, 1024], F32).ap()
    tC = lambda a, b: XR(None, XR.t[:, :, a:b])
    tS = lambda a, b: XI(None, XI.t[:, :, a:b])
    C.copy("dve", tC(0, 1), S5T(None, S5T.t[:, 5, :].unsqueeze(2)))
    C.copy("dve", tS(0, 1), S5T(None, S5T.t[:, 4, :].unsqueeze(2)))
    n_ = 1
    while n_ < 32:
        cn = XR(None, XR.t[:, :, n_ - 1:n_].broadcast_to([128, 32, n_]))
        sn = XI(None, XI.t[:, :, n_ - 1:n_].broadcast_to([128, 32, n_]))
        u1 = XR(None, XR.t[:, :, 64:64 + n_])
        u2 = XI(None, XI.t[:, :, 64:64 + n_])
        C.tt("dve", u1, tS(0, n_), sn, ALU.mult)
        C.tt("dve", tC(n_, 2 * n_), tC(0, n_), cn, ALU.mult)
        C.tt("dve", tC(n_, 2 * n_), tC(n_, 2 * n_), u1, ALU.subtract)
        C.tt("dve", u2, tC(0, n_), sn, ALU.mult)
        C.tt("dve", tS(n_, 2 * n_), tS(0, n_), cn, ALU.mult)
        C.tt("dve", tS(n_, 2 * n_), tS(n_, 2 * n_), u2, ALU.add)
        n_ *= 2
    tab_tok = [C.dma(s5tab[0].rearrange("p (q t) -> p q t", t=32), tC(0, 32)),
               C.dma(s5tab[1].rearrange("p (q t) -> p q t", t=32), tS(0, 32))]
    for tk in tab_tok:
        C.need("sp", tk)
    C.tt("dve", Tm(0), A(LR), A(LR), ALU.mult)
    C.tt("dve", Tm(1), A(LI), A(LI), ALU.mult)
    C.tt("dve", Tm(0), Tm(0), Tm(1), ALU.add)
    C.op("dve", lambda g: g.reciprocal(out=S5T.t[:, 0, :], in_=S5T.t[:, 0, :]), reads=[Tm(0)], writes=[Tm(0)])
    C.ts("dve", Tm(1), A(AR), -1.0, None, ALU.add)
    C.tt("dve", Tm(2), Tm(1), A(LR), ALU.mult)
    C.tt("dve", Tm(3), A(AI), A(LI), ALU.mult)
    C.tt("dve", Tm(2), Tm(2), Tm(3), ALU.add)
    C.tt("dve", Tm(6), Tm(2), Tm(0), ALU.mult)
    C.tt("dve", Tm(2), A(AI), A(LR), ALU.mult)
    C.tt("dve", Tm(3), Tm(1), A(LI), ALU.mult)
    C.tt("dve", Tm(2), Tm(2), Tm(3), ALU.subtract)
    C.tt("dve", Tm(7), Tm(2), Tm(0), ALU.mult)
    braw_t = PJ.t[:, 0:8, :].rearrange("p a b -> p (a b)").rearrange("p (k q h) -> p k q h", k=2, q=32)
    bb_t = PJ.t[:, 8:24, :].rearrange("p a b -> p (a b)").rearrange("p (k q h) -> p k q h", k=2, q=32)
    sc_t = PJ.t[:, 24:28, :].rearrange("p a b -> p (a b)").rearrange("p (q h) -> p q h", q=32)
    for k, nm in enumerate(("s5_b_re", "s5_b_im")):
        for m in range(2):
            C.dma(PJ(None, braw_t[64 * m:64 * m + 64, k, :, :]), I[nm].rearrange("(q m) p h -> m p q h", m=2)[m])
    qr_b = S5T(None, S5T.t[:, 6, :].unsqueeze(2).broadcast_to([128, 32, 16]))
    qi_b = S5T(None, S5T.t[:, 7, :].unsqueeze(2).broadcast_to([128, 32, 16]))
    br = PJ(None, braw_t[:, 0, :, :])
    bi = PJ(None, braw_t[:, 1, :, :])
    sc = PJ(None, sc_t)
    for dup in range(2):
        o_r = PJ(None, bb_t[:, 0, :, dup * 16:(dup + 1) * 16])
        o_i = PJ(None, bb_t[:, 1, :, dup * 16:(dup + 1) * 16])
        C.tt("dve", o_r, br, qr_b, ALU.mult)
        C.tt("dve", sc, bi, qi_b, ALU.mult)
        C.tt("dve", o_r, o_r, sc, ALU.subtract)
        C.tt("dve", o_i, bi, qr_b, ALU.mult)
        C.tt("dve", sc, br, qi_b, ALU.mult)
        C.tt("dve", o_i, o_i, sc, ALU.add)
    for k, dst in enumerate((BRE, BIM)):
        for gt in range(8):
            ps = next_psb()
            C.copy("dve", TMP.all(), PJ(None, bb_t[:, k, 4 * gt:4 * gt + 4, :]))
            C.tr(ps(None, ps.t[:, 0:128]), TMP.all(), ident.all())
            C.tt("dve", dst[0](None, dst[0].t[:, gt, :]), ps(None, ps.t[:, 0:128]), mask_b.all(), ALU.mult)
            C.tt("dve", dst[1](None, dst[1].t[:, gt, :]), ps(None, ps.t[:, 0:128]), mask_bo.all(), ALU.mult)
    for k, (nm, dst) in enumerate((("s5_c_re", CRE), ("s5_c_im", CIM))):
        for gt in range(8):
            for dup in range(2):
                C.dma(XS(None, XS.t[:, dup * 64:(dup + 1) * 64]), I[nm][gt * 128:(gt + 1) * 128, :])
            ps = next_psb()
            C.tr(ps(None, ps.t[:, 0:128]), XS(None, XS.t[:, 0:128]), ident.all())
            for par, mk in enumerate((mask_c, mask_co)):
                C.stt("dve", dst[par](None, dst[par].t[:, gt, :]), ps(None, ps.t[:, 0:128]), 1.0 if k == 0 else -1.0,
                      mk.all(), ALU.mult, ALU.mult)

    WSCR = nc.dram_tensor("wscr", [NSLAB, 128, KQ * WG], BF16).ap()
    wst = {"id": 0, "first": True, "tok": {}}

    def load_slab(ws, src, nk, gw):
        sid = wst["id"]
        wst["id"] += 1
        assert sid < NSLAB
        dstv = ws(None, ws.t[:, 0:nk, 0:gw])
        scr = WSCR[sid][:, 0:nk * gw].rearrange("p (k c) -> p k c", k=nk)
        if wst["first"]:
            C.dma(dstv, src, q="pool")
            wst["tok"][sid] = C.dma(scr, dstv, q="sp")
        else:
            C.need("sp", wst["tok"][sid])
            C.dma(dstv, scr, q="sp")

    def stream_mm(W, nkt, coltiles, rhs_fn, T, evac):
        groups = []
        cur = []
        for ct in coltiles:
            if cur and (ct[0] + ct[1] - cur[0][0] > WG):
                groups.append(cur)
                cur = []
            cur.append(ct)
        if cur:
            groups.append(cur)
        Wv = W.rearrange("(kt p) c -> p kt c", p=128)
        j = 0
        for grp in groups:
            g0 = grp[0][0]
            gw = grp[-1][0] + grp[-1][1] - g0
            ps = next_psa()
            pacc = ps(None, ps.t[0:T, 0:gw])
            for kq in range(0, nkt, KQ):
                ws = WS[rr["ws"] % NWS]
                rr["ws"] += 1
                nk = min(KQ, nkt - kq)
                load_slab(ws, Wv[:, kq:kq + nk, g0:g0 + gw], nk, gw)
                for k in range(nk):
                    kt = kq + k
                    C.mm(pacc, rhs_fn(kt), ws(None, ws.t[:, k, 0:gw]), start=(kt == 0), stop=(kt == nkt - 1))
            i_ = rr["tm"] % 2
            rr["tm"] += 1
            C.copy(ev_eng(), XS(("tm", i_), XS.t[0:T, i_ * 512:i_ * 512 + gw]), pacc)
            for (c0, w) in grp:
                pt = next_psb()
                C.tr(pt(None, pt.t[0:w, 0:T]), XS(("tm", i_), XS.t[0:T, i_ * 512 + c0 - g0:i_ * 512 + c0 - g0 + w]),
                     ident(None, ident.t[0:T, 0:T]))
                evac(j, pt(None, pt.t[0:w, 0:T]))
                j += 1

    def layer_norm_cols(src, ntile, T, gcol, bcol, vec, func=AF.Identity, eps=LN_EPS, dst_fn=None, also_fn=None):
        nch = ntile * 128
        ps = next_psb()
        ps2 = next_psb()
        for ct in range(ntile):
            sv = src(("t", ct), src.t[:, ct, 0:T])
            sq = SQ2(("s", rr["sq"] % 2), SQ2.t[:, rr["sq"] % 2, 0:T])
            rr["sq"] += 1
            C.act(sq, sv, AF.Square)
            C.mm(ps(None, ps.t[:, 0:T]), ones.all(), sv, start=(ct == 0), stop=(ct == ntile - 1))
            C.mm(ps2(None, ps2.t[:, 0:T]), ones.all(), sq, start=(ct == 0), stop=(ct == ntile - 1))
        st = lambda k: ST(None, ST.t[:, k, 0:T])
        C.ts("dve", st(0), ps(None, ps.t[:, 0:T]), 1.0 / nch, None, ALU.mult)
        C.ts("dve", st(1), ps2(None, ps2.t[:, 0:T]), 1.0 / nch, None, ALU.mult)
        C.tt("dve", st(2), st(0), st(0), ALU.mult)
        C.tt("dve", st(1), st(1), st(2), ALU.subtract)
        C.ts("dve", st(1), st(1), eps, None, ALU.add)
        C.act(st(1), st(1), AF.Sqrt)
        C.op("dve", lambda g, T=T: g.reciprocal(out=ST.t[:, 1, 0:T], in_=ST.t[:, 1, 0:T]), reads=[st(1)], writes=[st(1)])
        C.tt("dve", st(2), st(0), st(1), ALU.mult)
        C.ts("dve", st(2), st(2), -1.0, None, ALU.mult)
        for ct in range(ntile):
            e = "dve" if ct % 2 == 0 else "pool"
            sv = src(("t", ct), src.t[:, ct, 0:T])
            C.tt(e, sv, sv, st(1), ALU.mult)
            C.tt(e, sv, sv, st(2), ALU.add)
            dv = dst_fn(ct) if dst_fn is not None else sv
            C.act(dv, sv, func, bias=vec(None, vec.t[:, ct, bcol:bcol + 1]), scale=vec(None, vec.t[:, ct, gcol:gcol + 1]))
            if also_fn is not None:
                C.copy("pool" if ct % 2 else "act", also_fn(ct), dv)

    def load_x(tile):
        n = tile["T"]
        if tile["kind"] == "s":
            C.dma(XS(None, XS.t[0:n, :]), I["xs"][tile["s0"] * 8:tile["s0"] * 8 + n, :])
        else:
            p0 = tile["pos"]
            r = 0
            if p0 < 16:
                C.dma(XS(None, XS.t[0:16, :]), I["meta"][:, :])
                r = 16
            x0 = p0 + r - 16
            C.dma(XS(None, XS.t[r:n, :]), I["xp"][x0:x0 + n - r, :])
        for dt_ in range(16):
            ps = next_psb()
            C.tr(ps(None, ps.t[:, 0:n]), XS(None, XS.t[0:n, dt_ * 128:(dt_ + 1) * 128]), ident(None, ident.t[0:n, 0:n]))
            C.copy("act" if dt_ % 2 else "dve", XT(("t", dt_), XT.t[:, dt_, 0:n]), ps(None, ps.t[:, 0:n]))
            C.copy("pool", XTB(("t", dt_), XTB.t[:, dt_, 0:n]), XT(("t", dt_), XT.t[:, dt_, 0:n]))

    def store_y(tile):
        n = tile["T"]
        for dt_ in range(16):
            ps = next_psb()
            C.tr(ps(None, ps.t[0:n, 0:128]), XT(("t", dt_), XT.t[:, dt_, 0:n]), ident.all())
            C.copy("act" if dt_ % 2 else "dve", XS(None, XS.t[0:n, dt_ * 128:(dt_ + 1) * 128]), ps(None, ps.t[0:n, 0:128]))
        if tile["kind"] == "s":
            C.dma(O["y_s"][tile["s0"] * 8:tile["s0"] * 8 + n, :], XS(None, XS.t[0:n, :]))
        else:
            p0 = tile["pos"]
            r = 16 if p0 < 16 else 0
            x0 = p0 + r - 16
            C.dma(O["y_p"][x0:x0 + n - r, :], XS(None, XS.t[r:n, :]))

    def out_proj_ln(W, tile, vec, gcol, bcol):
        T = tile["T"]

        def evac(j, pv):
            xv_ = XT(("t", j), XT.t[:, j, 0:T])
            C.stt("dve", xv_, xv_, ALPHA, pv, ALU.mult, ALU.add)

        stream_mm(W, 16, [(i * 128, 128) for i in range(16)], lambda kt: mixv(kt, T), T, evac)
        layer_norm_cols(XT, 16, T, gcol, bcol, vec, also_fn=lambda ct: XTB(("t", ct), XTB.t[:, ct, 0:T]))

    def layer0(tile):
        T, nseq, L = tile["T"], tile["nseq"], tile["L"]
        is_s = tile["kind"] == "s"
        s0 = tile["s0"]
        xrhs = lambda kt: XTB(("t", kt), XTB.t[:, kt, 0:T])
        pj = lambda j: PJ(("t", j), PJ.t[:, j, 0:T])

        fence(HB, HB.t[0:1, 0, 0:1])
        def evacA(j, pv):
            if j < 8:
                C.copy(ev_eng(), pj(j), pv)
            elif j < 16:
                C.act(pj(j), pv, AF.Sigmoid)
            else:
                C.act(pj(j), pv, AF.Silu)

        stream_mm(I["ev_w_in"], 16, [(i * 128, 128) for i in range(24)], xrhs, T, evacA)

        W_ = 30 + L

        def hb(ct, a, b):
            v = HB.t[:, ct, 0:nseq * W_].rearrange("p (n w) -> p n w", w=W_)[:, :, a:b]
            return HB(("t", ct), v)

        def tokv(buf, ct):
            return buf(("t", ct), buf.t[:, ct, 0:T].rearrange("p (n l) -> p n l", l=L))

        if is_s:
            for q in range(2):
                C.dma(XS(None, XS.t[0:120, 0:1024]), I["s_conv"][s0 * 30 + q * 120:s0 * 30 + (q + 1) * 120, :])
                for ct in range(8):
                    ps = next_psb()
                    C.tr(ps(None, ps.t[:, 0:120]), XS(None, XS.t[0:120, ct * 128:(ct + 1) * 128]), ident(None, ident.t[0:120, 0:120]))
                    dstv = HB.t[:, ct, 0:nseq * W_].rearrange("p (n w) -> p n w", w=W_)[:, 4 * q:4 * q + 4, 0:30]
                    C.copy("dve", HB(("t", ct), dstv), ps(None, ps.t[:, 0:120].rearrange("p (n r) -> p n r", r=30)))
        else:
            for ct in range(8):
                if tile["first"]:
                    C.memset("pool", hb(ct, 0, 30), 0.0)
                else:
                    C.copy("pool", hb(ct, 0, 30), HC(("t", ct), HC.t[:, ct, :].unsqueeze(1)))
        for ct in range(8):
            e = "dve" if ct % 2 == 0 else "pool"
            C.tt(e, hb(ct, 30, 30 + L), tokv(PJ, ct), tokv(PJ, 8 + ct), ALU.mult)
        for ct in range(8):
            e = "dve"
            acc = tokv(ACC, ct)
            wcol = lambda j, ct=ct: VEC0(None, VEC0.t[:, ct, j:j + 1])
            C.ts(e, acc, hb(ct, 0, L), wcol(0), wcol(31), ALU.mult, ALU.add)
            for j in range(1, 31):
                C.stt(e, acc, hb(ct, j, j + L), wcol(j), acc, ALU.mult, ALU.add)
        if is_s:
            for q in range(2):
                for ct in range(8):
                    ps = next_psb()
                    srcv = HB.t[:, ct, 0:nseq * W_].rearrange("p (n w) -> p n w", w=W_)[:, 4 * q:4 * q + 4, L:L + 30]
                    C.copy("pool", TMP(None, TMP.t[:, 0:120].rearrange("p (n r) -> p n r", r=30)), HB(("t", ct), srcv))
                    C.tr(ps(None, ps.t[0:120, 0:128]), TMP(None, TMP.t[:, 0:120]), ident.all())
                    C.copy("dve", XS(None, XS.t[0:120, ct * 128:(ct + 1) * 128]), ps(None, ps.t[0:120, 0:128]))
                C.dma(O["s_conv_o"][s0 * 30 + q * 120:s0 * 30 + (q + 1) * 120, :], XS(None, XS.t[0:120, 0:1024]))
        else:
            for ct in range(8):
                C.copy("pool", TMP(None, TMP.t[:, 0:30]), HB(("t", ct), HB.t[:, ct, L:L + 30]))
                C.copy("pool", HC(("t", ct), HC.t[:, ct, :]), TMP(None, TMP.t[:, 0:30]))
            if tile["last"]:
                for ct in range(8):
                    ps = next_psb()
                    C.tr(ps(None, ps.t[0:30, 0:128]), HC(("t", ct), HC.t[:, ct, :]), ident.all())
                    C.copy("dve", XS(None, XS.t[0:30, ct * 128:(ct + 1) * 128]), ps(None, ps.t[0:30, 0:128]))
                C.dma(O["p_conv"][:, :], XS(None, XS.t[0:30, 0:1024]))
        layer_norm_cols(ACC, 8, T, 32, 33, VEC0, func=AF.Silu, dst_fn=lambda ct: mixv(8 + ct, T))

        def evac_pw(j, pv):
            C.tt("dve", mixv(j, T), pv, pj(16 + j), ALU.mult)

        stream_mm(I["a_pw"], 8, [(i * 128, 128) for i in range(8)], lambda kt: mixv(8 + kt, T), T, evac_pw)

        def evacB(j, pv):
            if j < 8:
                C.copy(ev_eng(), pj(j), pv)
            else:
                C.act(pj(j), pv, AF.Silu)

        stream_mm(I["ev_w_in"], 16, [(3072 + i * 128, 128) for i in range(16)], xrhs, T, evacB)

        Wx = 1 + L

        def xv(buf, a, b, p0=0, p1=32):
            v = buf.t[:, p0:p1, 0:nseq * Wx].rearrange("p q (n w) -> p q n w", w=Wx)[:, :, :, a:b]
            return buf(None, v)

        if is_s:
            for k, (nm, buf) in enumerate((("s_sre", XR), ("s_sim", XI))):
                for q in range(2):
                    for pr in range(16):
                        g0 = 2 * (16 * q + pr)
                        C.dma(S5ST(None, S5ST.t[pr * 8:(pr + 1) * 8, :]),
                              I[nm][s0:s0 + 8, g0:g0 + 2, :].rearrange("n m p -> n (m p)"))
                    ps = next_psb()
                    C.tr(ps(None, ps.t[:, 0:128]), S5ST.all(), ident.all())
                    C.copy("dve", xv(buf, 0, 1, 16 * q, 16 * q + 16),
                           ps(None, ps.t[:, 0:128].rearrange("p (q n o) -> p q n o", n=8, o=1)))
        else:
            for k, buf in enumerate((XR, XI)):
                if tile["first"]:
                    C.memset("pool", xv(buf, 0, 1), 0.0)
                else:
                    C.copy("pool", xv(buf, 0, 1), S5C(None, S5C.t[:, k, :].unsqueeze(2).unsqueeze(3)))
        for q4 in range(8):
            for k, (tab, buf) in enumerate(((BRE, XR), (BIM, XI))):
                ps = next_psa()
                for ip in range(4):
                    hf = ip // 2
                    tb = tab[ip % 2]
                    C.mm(ps(None, ps.t[:, ip * T:(ip + 1) * T]),
                         tb(None, tb.t[64 * hf:64 * hf + 64, q4, :]),
                         PJ(("t", q4), PJ.t[64 * hf:64 * hf + 64, q4, 0:T]))
                C.copy("act", xv(buf, 1, Wx, 4 * q4, 4 * q4 + 4),
                       ps(None, ps.t[:, 0:4 * T].rearrange("p (q n l) -> p q n l", q=4, l=L)))
        arb = S5A(None, S5A.t[:, AR, :].unsqueeze(2).broadcast_to([128, 32, nseq]))
        aib = S5A(None, S5A.t[:, AI, :].unsqueeze(2).broadcast_to([128, 32, nseq]))
        naib = S5A(None, S5A.t[:, NAI, :].unsqueeze(2).broadcast_to([128, 32, nseq]))
        if nseq == 1:
            t1v = S5T(None, S5T.t[:, 8, :].unsqueeze(2))
            t2v = S5T(None, S5T.t[:, 9, :].unsqueeze(2))
        else:
            t1v = SCS(None, SCS.t[:, 0, :, :])
            t2v = SCS(None, SCS.t[:, 1, :, :])

        def col(buf, t):
            v = buf.t[:, :, 0:nseq * Wx].rearrange("p q (n w) -> p q n w", w=Wx)[:, :, :, t]
            return buf(None, v)

        if is_s:
            e = "dve"
            for t in range(L):
                C.tt(e, t1v, col(XR, t), arb, ALU.mult)
                C.tt(e, col(XR, t + 1), col(XR, t + 1), t1v, ALU.add)
                C.tt(e, t1v, col(XI, t), naib, ALU.mult)
                C.tt(e, col(XR, t + 1), col(XR, t + 1), t1v, ALU.add)
                C.tt(e, t2v, col(XI, t), arb, ALU.mult)
                C.tt(e, col(XI, t + 1), col(XI, t + 1), t2v, ALU.add)
                C.tt(e, t2v, col(XR, t), aib, ALU.mult)
                C.tt(e, col(XI, t + 1), col(XI, t + 1), t2v, ALU.add)
        else:
            VTMf_ = VTM.t[:, :, :].rearrange("p a b -> p (a b)")
            PJf_ = PJ.t[:, 24:32, :].rearrange("p a b -> p (a b)")
            ROWf_ = ROW.t[:, :, :].rearrange("p a b -> p (a b)")
            C.dma(VTM(None, VTMf_[:, 0:1024]), s5tab[0])
            C.dma(PJ(None, PJf_), s5tab[1])
            for t0 in range(0, T, 32):
                Tc = min(32, T - t0)
                ec = VTM(None, VTMf_[:, 0:1024].rearrange("p (q t) -> p q t", t=32)[:, :, 0:Tc])
                es = PJ(None, PJf_.rearrange("p (q t) -> p q t", t=32)[:, :, 0:Tc])
                w1 = KW(None, KW.t[:, :].rearrange("p (q t) -> p q t", t=32)[:, :, 0:Tc])
                w2 = ROW(None, ROWf_.rearrange("p (q t) -> p q t", t=32)[:, :, 0:Tc])
                ur = XR(None, XR.t[:, :, 1 + t0:1 + t0 + Tc])
                ui = XI(None, XI.t[:, :, 1 + t0:1 + t0 + Tc])
                C.tt("pool", w1, ur, es, ALU.mult)
                C.tt("dve", w2, ui, es, ALU.mult)
                C.tt("dve", ur, ur, ec, ALU.mult)
                C.tt("pool", ui, ui, ec, ALU.mult)
                C.tt("dve", ur, ur, w2, ALU.add)
                C.tt("pool", ui, ui, w1, ALU.subtract)
                for pr in range(32):
                    rho = S5A(None, S5A.t[:, MAG, pr:pr + 1].broadcast_to([128, Tc]))
                    for buf in (XR, XI):
                        seg = buf(None, buf.t[:, pr, 1 + t0:1 + t0 + Tc])
                        scan(seg, rho, seg, buf(None, buf.t[:, pr, t0:t0 + 1]), ALU.mult, ALU.add)
                C.tt("pool", w1, ur, es, ALU.mult)
                C.tt("dve", w2, ui, es, ALU.mult)
                C.tt("dve", ur, ur, ec, ALU.mult)
                C.tt("pool", ui, ui, ec, ALU.mult)
                C.tt("dve", ur, ur, w2, ALU.subtract)
                C.tt("pool", ui, ui, w1, ALU.add)
        if is_s:
            for k, (nm, buf) in enumerate((("s_sre_o", XR), ("s_sim_o", XI))):
                for q in range(2):
                    C.copy("pool", TMP(None, TMP.t[:, 0:128].rearrange("p (q n o) -> p q n o", n=8, o=1)),
                           xv(buf, L, L + 1, 16 * q, 16 * q + 16))
                    ps = next_psb()
                    C.tr(ps(None, ps.t[:, 0:128]), TMP(None, TMP.t[:, 0:128]), ident.all())
                    C.copy("dve", S5ST.all(), ps(None, ps.t[:, 0:128]))
                    for pr in range(16):
                        g0 = 2 * (16 * q + pr)
                        C.dma(O[nm][s0:s0 + 8, g0:g0 + 2, :].rearrange("n m p -> n (m p)"),
                              S5ST(None, S5ST.t[pr * 8:(pr + 1) * 8, :]))
        else:
            for k, buf in enumerate((XR, XI)):
                C.copy("pool", S5C(None, S5C.t[:, k, :].unsqueeze(2).unsqueeze(3)), xv(buf, L, L + 1))
            if tile["last"]:
                for k, nm in enumerate(("p_sre", "p_sim")):
                    ps = next_psb()
                    C.tr(ps(None, ps.t[0:32, 0:128]), S5C(None, S5C.t[:, k, :]), ident.all())
                    C.copy("dve", S5ST(None, S5ST.t[0:32, :]), ps(None, ps.t[0:32, 0:128]))
                    C.dma(O[nm].rearrange("(q m) p -> q (m p)", m=2), S5ST(None, S5ST.t[0:32, :]))
        for gt in range(8):
            ps = next_psa()
            for ip in range(4):
                pair = 4 * gt + ip
                hf = ip // 2
                ov = ps(None, ps.t[64 * hf:64 * hf + 64, 0:T])
                xr_ = XR(None, XR.t[:, pair, 0:nseq * Wx].rearrange("p (n w) -> p n w", w=Wx)[:, :, 1:Wx])
                xi_ = XI(None, XI.t[:, pair, 0:nseq * Wx].rearrange("p (n w) -> p n w", w=Wx)[:, :, 1:Wx])
                cr, ci = CRE[ip % 2], CIM[ip % 2]
                C.mm(ov, cr(None, cr.t[:, gt, 64 * hf:64 * hf + 64]), xr_, start=(ip % 2 == 0), stop=False)
                C.mm(ov, ci(None, ci.t[:, gt, 64 * hf:64 * hf + 64]), xi_, start=False, stop=(ip % 2 == 1))
            gv = pj(gt)
            C.stt("dve", gv, gv, VEC0(None, VEC0.t[:, gt, 34:35]), ps(None, ps.t[:, 0:T]), ALU.mult, ALU.add)
            C.act(GELB(("t", gt), GELB.t[:, gt, 0:T]), gv, AF.Gelu)

        def evac_glu(j, pv):
            if j < 8:
                C.act(ACC(("t", j), ACC.t[:, j, 0:T]), pv, AF.Identity, bias=VECG(None, VECG.t[:, j, 0:1]))
            else:
                jj = j - 8
                tv = TMP(None, TMP.t[:, 0:T])
                C.act(tv, pv, AF.Sigmoid, bias=VECG(None, VECG.t[:, j, 0:1]))
                C.tt("dve", tv, tv, ACC(("t", jj), ACC.t[:, jj, 0:T]), ALU.mult)
                C.tt("dve", mixv(8 + jj, T), tv, pj(8 + jj), ALU.mult)

        stream_mm(I["s5_glu_w"], 8, [(i * 128, 128) for i in range(16)], lambda kt: GELB(("t", kt), GELB.t[:, kt, 0:T]), T, evac_glu)
        out_proj_ln(I["ev_w_out"], tile, VECG, 1, 2)

    do_rwkv = cfg.get("rwkv", True)
    MASKBIG = {"p": C.sb("mbig_p", [128, 128]), "s": C.sb("mbig_s", [64, 64])}
    SEGTRI = {"p": C.sb("stri_p", [128, 128]), "s": C.sb("stri_s", [64, 64])}
    R01 = {"p": C.sb("r01p", [128, 128]), "s": C.sb("r01s", [128, 64])}
    RNEG = {"p": C.sb("rnegp", [128, 128]), "s": C.sb("rnegs", [128, 64])}
    SEGSEL = C.sb("SEGSEL", [8, 64])
    SEGC = C.sb("SEGC", [64, 8])
    SEGROW = C.sb("SEGROW", [128, 8, 64])
    for k in ("p", "s"):
        C.dma(MASKBIG[k].all(), I["maskbig_" + k])
        C.dma(SEGTRI[k].all(), I["segtri_" + k])
        C.dma(R01[k].all(), I["r01_" + k])
        C.dma(RNEG[k].all(), I["rneg_" + k])
    C.dma(SEGSEL.all(), I["segsel"])
    C.dma(SEGC.all(), I["segc"])
    C.dma(SEGROW.all(), I["segrow"].rearrange("p (n t) -> p n t", n=8))

    VEC1 = C.sb("VEC1", [128, 8, 8])
    VMU = C.sb("VMU", [128, 25, 1])
    VECO = C.sb("VECO", [128, 16, 2])
    GB = C.sb("GB", [8, 1])
    load_rows_T(VEC1, [I[n_] for n_ in ("m_hn_g", "r_w0", "r_a0", "r_kk", "r_ka", "r_ln_g", "r_ln_b", "r_rk")], 1024)
    load_rows_T(VMU, [I["r_mu"][:, 0:2048]], 2048)
    load_rows_T(VMU, [I["r_mu"][:, 2048:3200]], 1152, ct0=16)
    load_rows_T(VECO, [I["od_ln_g"], I["od_ln_b"]], 2048)
    C.dma(GB(None, GB.t[0:4, :]), I["m_ig_b"].rearrange("o h -> h o"), slow=True)
    C.dma(GB(None, GB.t[4:8, :]), I["m_fg_b"].rearrange("o h -> h o"), slow=True)

    VTM = C.sb("VTM", [128, 4, 257])
    KW = C.sb("KW", [128, 1024])
    KWN = C.sb("KWN", [128, 256])
    GX = C.sb("GX", [8, 128])
    ROW = C.sb("ROW", [128, 8, 128])
    COL = C.sb("COL", [128, 64])
    DTB = C.sb("DTB", [128, 128])
    STB = C.sb("STB", [128, 128])
    P1S = C.sb("P1S", [128, 257])
    NUM = C.sb("NUM", [128, 257])
    HN = C.sb("HN", [128, 256])
    SM = C.sb("SM", [128, 16])
    CS = C.sb("CS", [128, 4, 2, 257])
    MCAR = C.sb("MCAR", [128, 4])
    CSS = [C.sb("CSS%d" % i, [128, 2, 257]) for i in range(2)]
    CSO = [C.sb("CSO%d" % i, [128, 2, 257]) for i in range(2)]
    MS = C.sb("MS", [8, 4])
    MSB = C.sb("MSB", [8, 128])
    MINIT = C.sb("MINIT", [128, 4, 8])
    DEC = C.sb("DEC", [128, 4, 8])
    MNEW = C.sb("MNEW", [128, 4, 8])
    QM = [C.sb("QM%d" % i, [128, 64]) for i in range(2)]
    C.memset("pool", VTM(None, VTM.t[:, :, 256:257]), 1.0)

    def stream_mm_tok(W, c0, ncols, T, evac):
        Wv = W.rearrange("(kt p) c -> p kt c", p=128)
        for g in range(ncols // WG):
            ps = next_psa()
            pv = ps(None, ps.t[0:T, 0:WG])
            for kq in range(0, 16, KQ):
                ws = WS[rr["ws"] % NWS]
                rr["ws"] += 1
                load_slab(ws, Wv[:, kq:kq + KQ, c0 + g * WG:c0 + (g + 1) * WG], KQ, WG)
                for k in range(KQ):
                    kt = kq + k
                    C.mm(pv, XTB(("t", kt), XTB.t[:, kt, 0:T]), ws(None, ws.t[:, k, 0:WG]), start=(kt == 0), stop=(kt == 15))
            evac(g, pv)

    def recip(e, out, in_):
        return C.op(e, lambda g: g.reciprocal(out=out.ap, in_=in_.ap), reads=[in_], writes=[out])

    def scan(out, d0, d1, init, op0, op1):
        rd = [d0, d1] + ([init] if isinstance(init, View) else [])
        ia = init.ap if isinstance(init, View) else init
        return C.op("dve", lambda g: g.tensor_tensor_scan(out=out.ap, data0=d0.ap, data1=d1.ap, initial=ia, op0=op0, op1=op1),
                    reads=rd, writes=[out])

    SR = C.sb("SR", [128, 8, 64])
    W2A2 = C.sb("W2A2", [128, 1024])
    BLK = C.sb("BLK", [128, 128])
    OMKA = C.sb("OMKA", [128, 8])
    SHC = C.sb("SHC", [128, 25])
    SUMB = C.sb("SUMB", [128, 8])
    C.dma(W2A2(None, W2A2.t[0:64, :]), I["r_w2"])
    C.dma(W2A2(None, W2A2.t[64:128, :]), I["r_a2"])
    C.dma(BLK.all(), I["blk"])
    C.ts("dve", OMKA.all(), VEC1(None, VEC1.t[:, :, 4]), -1.0, 1.0, ALU.mult, ALU.add)
    XIf = XI.t[:, :, :].rearrange("p a b -> p (a b)")
    T1 = XI("T1", XIf[:, 0:512].rearrange("p (j k) -> p j k", k=64))
    T2 = XI("T2", XIf[:, 512:1024].rearrange("p (j k) -> p j k", k=64))
    FSv = XIf[:, 1024:1536].rearrange("p (i t) -> p i t", t=128)
    SRS = [XI(("SRS", i), XIf[:, 1536 + 512 * i:2048 + 512 * i].rearrange("p (j k) -> p j k", k=64)) for i in range(2)]
    TWv = XIf[:, 2560:2688]
    ALLPS = PSA + PSB

    def next_ps8():
        rr["ps8"] = (rr.get("ps8", 0) + 1) % 8
        return ALLPS[rr["ps8"]]

    def rwkv(tile):
        T, nseq, L = tile["T"], tile["nseq"], tile["L"]
        is_s = tile["kind"] == "s"
        s0 = tile["s0"]
        Wx = 1 + L
        xrhs = lambda kt: XTB(("t", kt), XTB.t[:, kt, 0:T])
        pj = lambda j: PJ(("t", j), PJ.t[:, j, 0:T])
        pj3 = lambda j: PJ(("t", j), PJ.t[:, j, 0:T].rearrange("p (n l) -> p n l", l=L))

        def ppv(j0, j1, a, b):
            v = XR.t[:, j0:j1, 0:nseq * Wx].rearrange("p j (n w) -> p j n w", w=Wx)[:, :, :, a:b]
            return XR(("pp", j0) if j1 == j0 + 1 else None, v)

        if is_s:
            for (c0, ncol, ct0) in ((0, 2048, 0), (2048, 1152, 16)):
                C.dma(XS(None, XS.t[0:8, 0:ncol]), I["s_rsh"][s0:s0 + 8, c0:c0 + ncol])
                for ct in range(ncol // 128):
                    ps = next_psb()
                    C.tr(ps(None, ps.t[:, 0:8]), XS(None, XS.t[0:8, ct * 128:(ct + 1) * 128]), ident(None, ident.t[0:8, 0:8]))
                    C.copy("dve", ppv(ct0 + ct, ct0 + ct + 1, 0, 1), ps(None, ps.t[:, 0:8].rearrange("p (j n o) -> p j n o", j=1, o=1)))
        else:
            if tile["first"]:
                C.memset("pool", ppv(0, 25, 0, 1), 0.0)
            else:
                C.copy("pool", ppv(0, 25, 0, 1), SHC(None, SHC.t[:, :].unsqueeze(2).unsqueeze(3)))

        def evacR(j, pv):
            if j < 25:
                C.copy(ev_eng(), ppv(j, j + 1, 1, Wx), pv.buf(None, pv.ap.rearrange("p (j n l) -> p j n l", j=1, l=L)))
            else:
                C.act(pj(j), pv, AF.Silu)

        stream_mm(I["od_w_in"], 16, [(5128 + i * 128, 128) for i in range(33)], xrhs, T, evacR)

        if is_s:
            for (c0, ncol, ct0) in ((0, 2048, 0), (2048, 1152, 16)):
                for ct in range(ncol // 128):
                    ps = next_psb()
                    C.copy("pool", TMP(None, TMP.t[:, 0:8]), XR(("pp", ct0 + ct), XR.t[:, ct0 + ct, 0:nseq * Wx].rearrange("p (n w) -> p n w", w=Wx)[:, :, L]))
                    C.tr(ps(None, ps.t[0:8, 0:128]), TMP(None, TMP.t[:, 0:8]), ident.all())
                    C.copy("dve", XS(None, XS.t[0:8, ct * 128:(ct + 1) * 128]), ps(None, ps.t[0:8, 0:128]))
                C.dma(O["s_rsh_o"][s0:s0 + 8, c0:c0 + ncol], XS(None, XS.t[0:8, 0:ncol]))
        else:
            C.copy("pool", SHC(None, SHC.t[:, :].unsqueeze(2).unsqueeze(3)), ppv(0, 25, L, L + 1))
            if tile["last"]:
                for (c0, ncol, ct0) in ((0, 2048, 0), (2048, 1152, 16)):
                    for ct in range(ncol // 128):
                        ps = next_psb()
                        C.tr(ps(None, ps.t[0:1, 0:128]), SHC(None, SHC.t[:, ct0 + ct:ct0 + ct + 1]), ident.all())
                        C.copy("dve", XS(None, XS.t[0:1, ct * 128:(ct + 1) * 128]), ps(None, ps.t[0:1, 0:128]))
                    C.dma(O["p_rsh"][:, c0:c0 + ncol], XS(None, XS.t[0:1, 0:ncol]))
        for j in range(25):
            C.tt("pool", pj3(j), XR(("pp", j), XR.t[:, j, 0:nseq * Wx].rearrange("p (n w) -> p n w", w=Wx)[:, :, 0:L]),
                 XR(("pp", j), XR.t[:, j, 0:nseq * Wx].rearrange("p (n w) -> p n w", w=Wx)[:, :, 1:Wx]), ALU.subtract)
            C.stt("dve", pj3(j), pj3(j), VMU(None, VMU.t[:, j, 0:1]),
                  XR(("pp", j), XR.t[:, j, 0:nseq * Wx].rearrange("p (n w) -> p n w", w=Wx)[:, :, 1:Wx]), ALU.mult, ALU.add)

        VTMf = VTM.t[:, :, :].rearrange("p a b -> p (a b)")
        ROWf = ROW.t[:, :, :].rearrange("p a b -> p (a b)")
        KKt = lambda a, b: KW(None, KW.t[0:T, a:b])
        Wt = lambda a, b: VTM(None, VTMf[0:T, a:b])
        KKAt = lambda a, b: ROW(None, ROWf[0:T, a:b])
        KPt = lambda a, b: XS(None, XS.t[0:T, a:b])
        Rt = lambda a, b: XS(None, XS.t[0:T, 1024 + a:1024 + b])
        fs = lambda i: XI(("FS", i), FSv[:, i, 0:T])
        tw = XI("TW", TWv[0:64, 0:T])
        C.act(tw, PJ(("t", 24), PJ.t[0:64, 24, 0:T]), AF.Tanh)

        def to_tok(dst, src):
            ps = next_psb()
            C.tr(ps(None, ps.t[0:T, 0:128]), src, ident.all())
            C.copy(ev_eng(), dst, ps(None, ps.t[0:T, 0:128]))

        NE05 = -float(np.exp(-0.5))
        for ct in range(8):
            r_, k_, v_ = pj(ct), pj(8 + ct), pj(16 + ct)
            cs_ = slice(ct * 128, (ct + 1) * 128)
            ps = next_psa()
            C.mm(ps(None, ps.t[:, 0:T]), W2A2(None, W2A2.t[0:64, cs_]), tw)
            C.act(fs(0), ps(None, ps.t[:, 0:T]), AF.Sigmoid, bias=VEC1(None, VEC1.t[:, ct, 1:2]))
            C.act(fs(0), fs(0), AF.Exp, scale=NE05)
            to_tok(Wt(ct * 128, (ct + 1) * 128), fs(0))
            ps = next_psa()
            C.mm(ps(None, ps.t[:, 0:T]), W2A2(None, W2A2.t[64:128, cs_]), PJ(("t", 24), PJ.t[64:128, 24, 0:T]))
            C.act(fs(1), ps(None, ps.t[:, 0:T]), AF.Sigmoid, bias=VEC1(None, VEC1.t[:, ct, 2:3]))
            C.ts("dve", fs(2), k_, VEC1(None, VEC1.t[:, ct, 3:4]), None, ALU.mult)
            C.tt("pool", fs(3), fs(2), fs(2), ALU.mult)
            ps = next_psa()
            C.mm(ps(None, ps.t[:, 0:T]), BLK.all(), fs(3))
            C.act(fs(3), ps(None, ps.t[:, 0:T]), AF.Sqrt)
            C.ts("dve", fs(3), fs(3), 1e-12, None, ALU.max)
            recip("dve", fs(3), fs(3))
            C.tt("dve", fs(2), fs(2), fs(3), ALU.mult)
            to_tok(KKt(ct * 128, (ct + 1) * 128), fs(2))
            C.tt("dve", fs(3), fs(2), fs(1), ALU.mult)
            to_tok(KKAt(ct * 128, (ct + 1) * 128), fs(3))
            C.ts("dve", fs(1), fs(1), VEC1(None, VEC1.t[:, ct, 4:5]), OMKA(None, OMKA.t[:, ct:ct + 1]), ALU.mult, ALU.add)
            C.tt("dve", fs(1), fs(1), k_, ALU.mult)
            to_tok(KPt(ct * 128, (ct + 1) * 128), fs(1))
            to_tok(Rt(ct * 128, (ct + 1) * 128), r_)
            C.tt("dve", fs(3), r_, fs(1), ALU.mult)
            C.ts("dve", fs(3), fs(3), VEC1(None, VEC1.t[:, ct, 7:8]), None, ALU.mult)
            ps = next_psa()
            C.mm(ps(None, ps.t[:, 0:T]), BLK.all(), fs(3))
            C.tt("dve", ACC(("t", ct), ACC.t[:, ct, 0:T]), ps(None, ps.t[:, 0:T]), v_, ALU.mult)

        Yv = MIX.t[:, 8 * TT:16 * TT].rearrange("p (j t) -> p j t", j=8)
        srcs = (("kk", KW.t[0:T, :]), ("w", VTMf[0:T, 0:1024]), ("kka", ROWf[0:T, 0:1024]), ("kp", XS.t[0:T, 0:1024]), ("r", XS.t[0:T, 1024:2048]))
        bufs = {"kk": KW, "w": VTM, "kka": ROW, "kp": XS, "r": XS}
        for n in range(nseq):
            if is_s:
                sr = SRS[n % 2]
                C.dma(sr, I["s_rs"][s0 + n].rearrange("(j hp) v k -> (hp v) j k", hp=2))
            else:
                sr = SR.all()
                if tile["first"]:
                    C.memset("pool", sr, 0.0)
            for l in range(L):
                t = n * L + l
                oh = ident(None, ident.t[0:T, t:t + 1].broadcast_to([T, 64]))
                bc = {}
                for nm, ap in srcs:
                    ps = next_ps8()
                    xv = ap.rearrange("p (j hp k) -> p hp j k", hp=2, k=64)
                    C.mm(ps(None, ps.t[0:64, 0:512]), oh, bufs[nm](None, xv[:, 0]))
                    C.mm(ps(None, ps.t[64:128, 0:512]), oh, bufs[nm](None, xv[:, 1]))
                    bc[nm] = ps(None, ps.t[:, 0:512].rearrange("p (j k) -> p j k", k=64))
                C.tt("dve", T1, sr, bc["kk"], ALU.mult)
                C.op("dve", lambda g: g.reduce_sum(out=SUMB.t[:, :], in_=T1.ap, axis=AX.X), reads=[T1], writes=[SUMB.all()])
                C.tt("dve", sr, sr, bc["w"], ALU.mult)
                C.tt("dve", T2, bc["kka"], SUMB(None, SUMB.t[:, :].unsqueeze(2).broadcast_to([128, 8, 64])), ALU.mult)
                C.tt("dve", sr, sr, T2, ALU.subtract)
                C.tt("dve", T1, bc["kp"], PJ(None, PJ.t[:, 16:24, t].unsqueeze(2).broadcast_to([128, 8, 64])), ALU.mult)
                C.tt("dve", sr, sr, T1, ALU.add)
                C.tt("dve", T2, sr, bc["r"], ALU.mult)
                yv = MIX(None, Yv[:, :, t])
                C.op("dve", lambda g, yv=yv: g.reduce_sum(out=yv.ap, in_=T2.ap, axis=AX.X), reads=[T2], writes=[yv])
            if is_s:
                C.dma(O["s_rs_o"][s0 + n].rearrange("(j hp) v k -> (hp v) j k", hp=2), sr)
        if (not is_s) and tile["last"]:
            C.dma(O["p_rs"].rearrange("(j hp) v k -> (hp v) j k", hp=2), SR.all())

        for j in range(8):
            y = mixv(8 + j, T)
            ps = next_psa()
            C.mm(ps(None, ps.t[:, 0:T]), BLK.all(), y)
            C.tt("pool", fs(0), y, y, ALU.mult)
            ps2 = next_psa()
            C.mm(ps2(None, ps2.t[:, 0:T]), BLK.all(), fs(0))
            C.ts("dve", fs(1), ps(None, ps.t[:, 0:T]), 1.0 / 64, None, ALU.mult)
            C.ts("dve", fs(2), ps2(None, ps2.t[:, 0:T]), 1.0 / 64, None, ALU.mult)
            C.tt("dve", fs(3), fs(1), fs(1), ALU.mult)
            C.tt("dve", fs(2), fs(2), fs(3), ALU.subtract)
            C.ts("dve", fs(2), fs(2), 64e-5, None, ALU.add)
            C.act(fs(2), fs(2), AF.Sqrt)
            recip("dve", fs(2), fs(2))
            C.tt("dve", y, y, fs(1), ALU.subtract)
            C.tt("dve", y, y, fs(2), ALU.mult)
            C.act(y, y, AF.Identity, bias=VEC1(None, VEC1.t[:, j, 6:7]), scale=VEC1(None, VEC1.t[:, j, 5:6]))
            C.tt("dve", y, y, ACC(("t", j), ACC.t[:, j, 0:T]), ALU.add)
            C.tt("dve", y, y, pj(25 + j), ALU.mult)

    SSTRI = {"p": C.sb("sstri_p", [128, 128]), "s": C.sb("sstri_s", [64, 64])}
    SSTRIT = {"p": C.sb("sstriT_p", [128, 128]), "s": C.sb("sstriT_s", [64, 64])}
    for k_ in ("p", "s"):
        C.dma(SSTRI[k_].all(), I["sstri_" + k_])
        C.dma(SSTRIT[k_].all(), I["sstriT_" + k_])
    WLB = C.sb("WLB", [128, 8, 8])
    NB16 = C.sb("NB16", [128, 10, 128], BF16)
    HBf = HB.t[:, :, :].rearrange("p a b -> p (a b)")
    KKv = KW.t[:, :].rearrange("p (j t) -> p j t", t=128)
    BTv = ROW.t
    XRf = XR.t[:, :, :].rearrange("p a b -> p (a b)")
    S0Tv = XRf[:, 0:4096].rearrange("p (n j v) -> p n j v", n=8, j=8)

    def fence(buf, ap):
        C.op("pool", lambda g: g.memset(ap, 0.0), writes=[buf.all()])

    def rwkv2(tile):
        T, nseq, L = tile["T"], tile["nseq"], tile["L"]
        is_s = tile["kind"] == "s"
        kd = tile["kind"]
        s0 = tile["s0"]
        Wx = 1 + L
        xrhs = lambda kt: XTB(("t", kt), XTB.t[:, kt, 0:T])
        pj = lambda j: PJ(("t", j), PJ.t[:, j, 0:T])
        pj3 = lambda j: PJ(("t", j), PJ.t[:, j, 0:T].rearrange("p (n l) -> p n l", l=L))

        def ppv(j0, j1, a, b):
            v = XR.t[:, j0:j1, 0:nseq * Wx].rearrange("p j (n w) -> p j n w", w=Wx)[:, :, :, a:b]
            return XR(("pp", j0) if j1 == j0 + 1 else None, v)

        if is_s:
            for (c0, ncol, ct0) in ((0, 2048, 0), (2048, 1152, 16)):
                C.dma(XS(None, XS.t[0:8, 0:ncol]), I["s_rsh"][s0:s0 + 8, c0:c0 + ncol])
                for ct in range(ncol // 128):
                    ps = next_psb()
                    C.tr(ps(None, ps.t[:, 0:8]), XS(None, XS.t[0:8, ct * 128:(ct + 1) * 128]), ident(None, ident.t[0:8, 0:8]))
                    C.copy("dve", ppv(ct0 + ct, ct0 + ct + 1, 0, 1), ps(None, ps.t[:, 0:8].rearrange("p (j n o) -> p j n o", j=1, o=1)))
        else:
            if tile["first"]:
                C.memset("pool", ppv(0, 25, 0, 1), 0.0)
            else:
                C.copy("pool", ppv(0, 25, 0, 1), SHC(None, SHC.t[:, :].unsqueeze(2).unsqueeze(3)))

        def evacR(j, pv):
            if j < 25:
                C.copy(ev_eng(), ppv(j, j + 1, 1, Wx), pv.buf(None, pv.ap.rearrange("p (j n l) -> p j n l", j=1, l=L)))
            else:
                C.act(pj(j), pv, AF.Silu)

        stream_mm(I["od_w_in"], 16, [(5128 + i * 128, 128) for i in range(33)], xrhs, T, evacR)

        if is_s:
            for (c0, ncol, ct0) in ((0, 2048, 0), (2048, 1152, 16)):
                for ct in range(ncol // 128):
                    ps = next_psb()
                    C.copy("pool", TMP(None, TMP.t[:, 0:8]), XR(("pp", ct0 + ct), XR.t[:, ct0 + ct, 0:nseq * Wx].rearrange("p (n w) -> p n w", w=Wx)[:, :, L]))
                    C.tr(ps(None, ps.t[0:8, 0:128]), TMP(None, TMP.t[:, 0:8]), ident.all())
                    C.copy("dve", XS(None, XS.t[0:8, ct * 128:(ct + 1) * 128]), ps(None, ps.t[0:8, 0:128]))
                C.dma(O["s_rsh_o"][s0:s0 + 8, c0:c0 + ncol], XS(None, XS.t[0:8, 0:ncol]))
        else:
            C.copy("pool", SHC(None, SHC.t[:, :].unsqueeze(2).unsqueeze(3)), ppv(0, 25, L, L + 1))
            if tile["last"]:
                for (c0, ncol, ct0) in ((0, 2048, 0), (2048, 1152, 16)):
                    for ct in range(ncol // 128):
                        ps = next_psb()
                        C.tr(ps(None, ps.t[0:1, 0:128]), SHC(None, SHC.t[:, ct0 + ct:ct0 + ct + 1]), ident.all())
                        C.copy("dve", XS(None, XS.t[0:1, ct * 128:(ct + 1) * 128]), ps(None, ps.t[0:1, 0:128]))
                    C.dma(O["p_rsh"][:, c0:c0 + ncol], XS(None, XS.t[0:1, 0:ncol]))
        for j in range(25):
            C.tt("pool", pj3(j), XR(("pp", j), XR.t[:, j, 0:nseq * Wx].rearrange("p (n w) -> p n w", w=Wx)[:, :, 0:L]),
                 XR(("pp", j), XR.t[:, j, 0:nseq * Wx].rearrange("p (n w) -> p n w", w=Wx)[:, :, 1:Wx]), ALU.subtract)
            C.stt("dve", pj3(j), pj3(j), VMU(None, VMU.t[:, j, 0:1]),
                  XR(("pp", j), XR.t[:, j, 0:nseq * Wx].rearrange("p (n w) -> p n w", w=Wx)[:, :, 1:Wx]), ALU.mult, ALU.add)

        s0t = lambda n, j, rs=slice(0, 128): XR(("st", n), S0Tv[rs, n, j, :])
        if is_s:
            fence(XR, XR.t[0:1, 0, 0:1])
            for n2 in range(0, 8, 2):
                stg = XS.t[:, :].rearrange("p (n j d k) -> p n j d k", n=2, j=8, d=2)
                for nn in range(2):
                    for d in range(2):
                        C.dma(XS(None, stg[:, nn, :, d, :]), I["s_rs"][s0 + n2 + nn].rearrange("(j hp) v k -> (hp v) j k", hp=2))
                for nn in range(2):
                    n = n2 + nn
                    for j in range(8):
                        ps = next_psb()
                        C.tr(ps(None, ps.t[:, 0:128]), XS(None, stg[:, nn, j, :, :]), ident.all())
                        C.copy("dve", s0t(n, j, slice(0, 64)), ps(None, ps.t[0:64, 0:64]))
                        C.copy("act", s0t(n, j, slice(64, 128)), ps(None, ps.t[64:128, 64:128]))
        else:
            if tile["first"]:
                C.memset("pool", SR.all(), 0.0)

        VTMf = VTM.t[:, :, :].rearrange("p a b -> p (a b)")
        Vtm = lambda hc: XS(None, XS.t[0:T, hc])
        Btm = lambda hc: XS(None, XS.t[0:T, 1024 + hc.start:1024 + hc.stop])
        Ktm = lambda hc: VTM(None, VTMf[0:T, hc])
        fs = lambda i: XI(("FS", i), FSv[:, i, 0:T])
        tw = XI("TW", TWv[0:64, 0:T])
        C.act(tw, PJ(("t", 24), PJ.t[0:64, 24, 0:T]), AF.Tanh)

        def to_tok(dst, src):
            ps = next_psb()
            C.tr(ps(None, ps.t[0:T, 0:128]), src, ident.all())
            C.copy(ev_eng(), dst, ps(None, ps.t[0:T, 0:128]))

        NE05 = -float(np.exp(-0.5))
        kkc = lambda ct, rs=slice(0, 128): KW(("c", ct), KKv[rs, ct, 0:T])
        btc = lambda ct, rs=slice(0, 128): ROW(("c", ct), BTv[rs, ct, 0:T])
        for ct in range(8):
            r_, k_, v_ = pj(ct), pj(8 + ct), pj(16 + ct)
            cs_ = slice(ct * 128, (ct + 1) * 128)
            ps = next_psa()
            C.mm(ps(None, ps.t[:, 0:T]), W2A2(None, W2A2.t[0:64, cs_]), tw)
            C.act(fs(0), ps(None, ps.t[:, 0:T]), AF.Sigmoid, bias=VEC1(None, VEC1.t[:, ct, 1:2]))
            C.ts("dve", fs(0), fs(0), NE05, None, ALU.mult)
            scan(fs(1), R01[kd](None, R01[kd].t[:, 0:T]), fs(0), 0.0, ALU.mult, ALU.add)
            ps = next_psa()
            C.mm(ps(None, ps.t[:, 0:T]), W2A2(None, W2A2.t[64:128, cs_]), PJ(("t", 24), PJ.t[64:128, 24, 0:T]))
            C.act(fs(2), ps(None, ps.t[:, 0:T]), AF.Sigmoid, bias=VEC1(None, VEC1.t[:, ct, 2:3]))
            C.ts("dve", kkc(ct), k_, VEC1(None, VEC1.t[:, ct, 3:4]), None, ALU.mult)
            C.tt("pool", fs(3), kkc(ct), kkc(ct), ALU.mult)
            ps = next_psa()
            C.mm(ps(None, ps.t[:, 0:T]), BLK.all(), fs(3))
            C.act(fs(3), ps(None, ps.t[:, 0:T]), AF.Sqrt)
            C.ts("dve", fs(3), fs(3), 1e-12, None, ALU.max)
            recip("dve", fs(3), fs(3))
            C.tt("dve", kkc(ct), kkc(ct), fs(3), ALU.mult)
            C.tt("dve", btc(ct), kkc(ct), fs(2), ALU.mult)
            C.ts("dve", fs(2), fs(2), VEC1(None, VEC1.t[:, ct, 4:5]), OMKA(None, OMKA.t[:, ct:ct + 1]), ALU.mult, ALU.add)
            C.tt("dve", fs(2), fs(2), k_, ALU.mult)
            C.tt("pool", fs(3), r_, fs(2), ALU.mult)
            C.ts("dve", fs(3), fs(3), VEC1(None, VEC1.t[:, ct, 7:8]), None, ALU.mult)
            ps = next_psa()
            C.mm(ps(None, ps.t[:, 0:T]), BLK.all(), fs(3))
            C.tt("dve", ACC(("t", ct), ACC.t[:, ct, 0:T]), ps(None, ps.t[:, 0:T]), v_, ALU.mult)
            C.act(fs(3), fs(1), AF.Exp)
            C.tt("dve", r_, r_, fs(3), ALU.mult)
            C.copy("pool", WLB(None, WLB.t[:, ct, 0:nseq]),
                   XI(("FS", 3), FSv[:, 3, 0:T].rearrange("p (n l) -> p n l", l=L)[:, :, L - 1]))
            C.tt("dve", fs(3), fs(1), fs(0), ALU.subtract)
            C.act(fs(3), fs(3), AF.Exp)
            C.tt("dve", kkc(ct), kkc(ct), fs(3), ALU.mult)
            C.act(fs(3), fs(1), AF.Exp, scale=-1.0)
            C.tt("dve", btc(ct), btc(ct), fs(3), ALU.mult)
            C.tt("dve", k_, fs(2), fs(3), ALU.mult)
            to_tok(Vtm(cs_), v_)
            to_tok(Btm(cs_), btc(ct))
            to_tok(Ktm(cs_), k_)

        fence(HB, HB.t[0:1, 0, 0:1])
        mat = lambda i: HB(("m", i), HBf[0:T, i * 128:i * 128 + T])
        half = lambda i, a: HB(("m", i), HBf[0:T, i * 128 + 64 * a:i * 128 + 64 * a + 64])
        nsq = max(0, int(np.ceil(np.log2(L))) - 1)
        idT = ident(None, ident.t[0:T, 0:T])
        mS = SSTRI[kd](None, SSTRI[kd].t[0:T, 0:T])
        mST = SSTRIT[kd](None, SSTRIT[kd].t[0:T, 0:T])
        mI = SEGTRI[kd](None, SEGTRI[kd].t[0:T, 0:T])
        if is_s:
            KKM, RM = T1, T2
        for j in range(8):
            nb = lambda i: NB16(("n", i), NB16.t[0:T, i, 0:T])
            Q = [[nb(5 * hp + 0), nb(5 * hp + 1)] for hp in range(2)]
            QT = [[nb(5 * hp + 2), nb(5 * hp + 3)] for hp in range(2)]
            P16 = [nb(5 * hp + 4) for hp in range(2)]
            Pm = [mat(8 * hp + 4) for hp in range(2)]
            BR = [mat(8 * hp + 5) for hp in range(2)]
            AK = [mat(8 * hp + 6) for hp in range(2)]
            KR = [mat(8 * hp + 7) for hp in range(2)]
            RHS = [half(16, hp) for hp in range(2)]
            SAT = [half(17, hp) for hp in range(2)]
            rsl = [slice(0, 64), slice(64, 128)]
            if is_s:
                C.tt("pool", KKM, KW(("c", j), KKv[:, j, 0:T].unsqueeze(1).broadcast_to([128, 8, T])), SEGROW(None, SEGROW.t[:, :, 0:T]), ALU.mult)
                C.tt("pool", RM, PJ(("t", j), PJ.t[:, j, 0:T].unsqueeze(1).broadcast_to([128, 8, T])), SEGROW(None, SEGROW.t[:, :, 0:T]), ALU.mult)
            for hp in range(2):
                rs = rsl[hp]
                rq = PJ(("t", j), PJ.t[rs, j, 0:T])
                kq = PJ(("t", 8 + j), PJ.t[rs, 8 + j, 0:T])
                ps = next_ps8()
                C.mm(ps(None, ps.t[0:T, 0:T]), btc(j, rs), kkc(j, rs))
                C.mm(ps(None, ps.t[0:T, T:2 * T]), btc(j, rs), rq)
                C.tt("dve", Q[hp][0], ps(None, ps.t[0:T, 0:T]), mS, ALU.mult)
                C.tt("dve", BR[hp], ps(None, ps.t[0:T, T:2 * T]), mI, ALU.mult)
                ps = next_ps8()
                C.mm(ps(None, ps.t[0:T, 0:T]), kq, kkc(j, rs))
                C.mm(ps(None, ps.t[0:T, T:2 * T]), kq, rq)
                C.tt("dve", AK[hp], ps(None, ps.t[0:T, 0:T]), mS, ALU.mult)
                C.tt("dve", KR[hp], ps(None, ps.t[0:T, T:2 * T]), mI, ALU.mult)
                ps = next_ps8()
                C.mm(ps(None, ps.t[0:T, 0:T]), kkc(j, rs), btc(j, rs))
                C.tt("dve", QT[hp][0], ps(None, ps.t[0:T, 0:T]), mST, ALU.mult)
                C.stt("dve", Pm[hp], Q[hp][0], -1.0, idT, ALU.mult, ALU.add)
                C.copy("act", P16[hp], Pm[hp])
            cur = 0
            for it in range(nsq):
                nxt = 1 - cur
                last_it = (it == nsq - 1)
                for hp in range(2):
                    if not last_it:
                        ps = next_ps8()
                        C.mm(ps(None, ps.t[0:T, 0:T]), QT[hp][cur], Q[hp][cur])
                        C.copy("act", Q[hp][nxt], ps(None, ps.t[0:T, 0:T]))
                    ps = next_ps8()
                    C.mm(ps(None, ps.t[0:T, 0:T]), Q[hp][cur], QT[hp][cur])
                    C.copy("dve", QT[hp][nxt], ps(None, ps.t[0:T, 0:T]))
                for hp in range(2):
                    ps = next_ps8()
                    C.mm(ps(None, ps.t[0:T, 0:T]), QT[hp][nxt], P16[hp])
                    C.tt("dve", Pm[hp], Pm[hp], ps(None, ps.t[0:T, 0:T]), ALU.add)
                    if not last_it:
                        C.copy("act", P16[hp], Pm[hp])
                cur = nxt
            for hp in range(2):
                rs = rsl[hp]
                hc = slice((2 * j + hp) * 64, (2 * j + hp) * 64 + 64)
                ps = next_ps8()
                if is_s:
                    for n in range(nseq):
                        C.mm(ps(None, ps.t[0:T, 0:64]), XI("T1", KKM.ap[rs, n, :]), s0t(n, j, rs), start=(n == 0), stop=False)
                else:
                    C.mm(ps(None, ps.t[0:T, 0:64]), kkc(j, rs), SR(None, SR.t[rs, j, :]), start=True, stop=False)
                C.mm(ps(None, ps.t[0:T, 0:64]), AK[hp], Vtm(hc), start=False, stop=True)
                C.act(RHS[hp], ps(None, ps.t[0:T, 0:64]), AF.Identity, scale=-1.0)
                ps = next_ps8()
                C.mm(ps(None, ps.t[0:T, 0:64]), Pm[hp], RHS[hp])
                C.copy("dve", SAT[hp], ps(None, ps.t[0:T, 0:64]))
            psY = next_ps8()
            for hp in range(2):
                rs = rsl[hp]
                hc = slice((2 * j + hp) * 64, (2 * j + hp) * 64 + 64)
                ov = psY(None, psY.t[rs, 0:T])
                if is_s:
                    for n in range(nseq):
                        C.mm(ov, s0t(n, j, rs), XI("T2", RM.ap[rs, n, :]), start=(n == 0), stop=False)
                else:
                    C.mm(ov, SR(None, SR.t[rs, j, :]), PJ(("t", j), PJ.t[rs, j, 0:T]), start=True, stop=False)
                C.mm(ov, SAT[hp], BR[hp], start=False, stop=False)
                C.mm(ov, Vtm(hc), KR[hp], start=False, stop=True)
            y = TMP(None, TMP.t[:, 0:T])
            C.copy("act", y, psY(None, psY.t[:, 0:T]))
            ps = next_psa()
            C.mm(ps(None, ps.t[:, 0:T]), BLK.all(), y)
            C.tt("pool", fs(0), y, y, ALU.mult)
            ps2 = next_psa()
            C.mm(ps2(None, ps2.t[:, 0:T]), BLK.all(), fs(0))
            C.ts("dve", fs(1), ps(None, ps.t[:, 0:T]), 1.0 / 64, None, ALU.mult)
            C.ts("dve", fs(2), ps2(None, ps2.t[:, 0:T]), 1.0 / 64, None, ALU.mult)
            C.tt("dve", fs(3), fs(1), fs(1), ALU.mult)
            C.tt("dve", fs(2), fs(2), fs(3), ALU.subtract)
            C.ts("dve", fs(2), fs(2), 64e-5, None, ALU.add)
            C.act(fs(2), fs(2), AF.Sqrt)
            recip("dve", fs(2), fs(2))
            C.tt("dve", y, y, fs(1), ALU.subtract)
            C.tt("dve", y, y, fs(2), ALU.mult)
            C.act(y, y, AF.Identity, bias=VEC1(None, VEC1.t[:, j, 6:7]), scale=VEC1(None, VEC1.t[:, j, 5:6]))
            C.tt("dve", y, y, ACC(("t", j), ACC.t[:, j, 0:T]), ALU.add)
            C.tt("dve", mixv(8 + j, T), y, pj(25 + j), ALU.mult)
            tmpS = XI(("FS", 0), FSv[:, 0, 0:64])
            if is_s:
                SAM = [CSS[hp](None, CSS[hp].t[0:T, :, :].rearrange("p a b -> p (a b)")[:, 0:512].rearrange("p (n v) -> p n v", n=8)) for hp in range(2)]
                VM = [CSO[hp](None, CSO[hp].t[0:T, :, :].rearrange("p a b -> p (a b)")[:, 0:512].rearrange("p (n v) -> p n v", n=8)) for hp in range(2)]
                segc_b = SEGC(None, SEGC.t[0:T, :].unsqueeze(2).broadcast_to([T, 8, 64]))
                for hp in range(2):
                    hc = slice((2 * j + hp) * 64, (2 * j + hp) * 64 + 64)
                    C.tt("pool", SAM[hp], HB(("m", 17), HBf[0:T, 17 * 128 + 64 * hp:17 * 128 + 64 * hp + 64].unsqueeze(1).broadcast_to([T, 8, 64])), segc_b, ALU.mult)
                    C.tt("pool", VM[hp], XS(None, XS.t[0:T, hc].unsqueeze(1).broadcast_to([T, 8, 64])), segc_b, ALU.mult)
                for n in range(nseq):
                    psS = next_ps8()
                    for hp in range(2):
                        rs = rsl[hp]
                        hc = slice((2 * j + hp) * 64, (2 * j + hp) * 64 + 64)
                        C.mm(psS(None, psS.t[rs, 0:64]), Btm(hc), CSS[hp](None, SAM[hp].ap[:, n, :]), start=True, stop=False)
                        C.mm(psS(None, psS.t[rs, 0:64]), Ktm(hc), CSO[hp](None, VM[hp].ap[:, n, :]), start=False, stop=True)
                    wl = WLB(None, WLB.t[:, j, n:n + 1])
                    C.ts("dve", tmpS, s0t(n, j), wl, None, ALU.mult)
                    C.stt("dve", s0t(n, j), psS(None, psS.t[:, 0:64]), wl, tmpS, ALU.mult, ALU.add)
            else:
                psS = next_ps8()
                for hp in range(2):
                    rs = rsl[hp]
                    hc = slice((2 * j + hp) * 64, (2 * j + hp) * 64 + 64)
                    C.mm(psS(None, psS.t[rs, 0:64]), Btm(hc), SAT[hp], start=True, stop=False)
                    C.mm(psS(None, psS.t[rs, 0:64]), Ktm(hc), Vtm(hc), start=False, stop=True)
                wl = WLB(None, WLB.t[:, j, 0:1])
                srj = SR(None, SR.t[:, j, :])
                C.ts("dve", tmpS, srj, wl, None, ALU.mult)
                C.stt("dve", srj, psS(None, psS.t[:, 0:64]), wl, tmpS, ALU.mult, ALU.add)

        def state_out(src_fn, dst):
            for j in range(8):
                ps = next_psb()
                C.tr(ps(None, ps.t[0:64, 0:128]), src_fn(j), ident.all())
                C.copy(ev_eng(), XS(None, XS.t[0:64, j * 128:(j + 1) * 128]), ps(None, ps.t[0:64, 0:128]))
            C.dma(dst.rearrange("(j hp) v k -> v j hp k", hp=2), XS(None, XS.t[0:64, 0:1024].rearrange("p (j hp k) -> p j hp k", j=8, hp=2)))

        if is_s:
            for n in range(nseq):
                state_out(lambda j, n=n: s0t(n, j), O["s_rs_o"][s0 + n])
        elif tile["last"]:
            state_out(lambda j: SR(None, SR.t[:, j, :]), O["p_rs"])


    def layer1(tile):
        T, nseq, L = tile["T"], tile["nseq"], tile["L"]
        is_s = tile["kind"] == "s"
        kd = tile["kind"]
        s0 = tile["s0"]
        xrhs = lambda kt: XTB(("t", kt), XTB.t[:, kt, 0:T])
        pj = lambda j: PJ(("t", j), PJ.t[:, j, 0:T])
        W = I["od_w_in"]

        def evacM(j, pv):
            if j < 8:
                C.act(pj(j), pv, AF.Identity, scale=1.0 / 16.0)
            elif j < 16:
                C.copy(ev_eng(), pj(j), pv)
            elif j < 24:
                C.act(pj(j), pv, AF.Sigmoid)
            elif j == 24:
                C.copy("dve", PJ(("t", 32), PJ.t[0:8, 32, 0:T]), pv)
            else:
                C.act(pj(j - 1), pv, AF.Silu)

        cols = [(i * 128, 128) for i in range(16)] + [(3072 + i * 128, 128) for i in range(8)] + [(4096, 8)] + \
               [(4104 + i * 128, 128) for i in range(8)]
        stream_mm(W, 16, cols, xrhs, T, evacM)

        def evacV(g, pv):
            C.copy(ev_eng(), VTM(None, VTM.t[0:T, 2 * g:2 * g + 2, 0:256]), pv.buf(None, pv.ap.rearrange("p (h v) -> p h v", h=2)))

        C.memset("pool", VTM(None, VTM.t[:, :, 256:257]), 1.0)
        stream_mm_tok(W, 2048, 1024, T, evacV)

        gx = GX(None, GX.t[0:8, 0:T])
        C.ts("dve", gx, PJ(("t", 32), PJ.t[0:8, 32, 0:T]), GB.all(), None, ALU.add)
        col = lambda a, b: COL(None, COL.t[0:T, a:b])
        ps = next_psb()
        C.tr(ps(None, ps.t[0:T, 0:8]), gx, ident(None, ident.t[0:8, 0:8]))
        C.copy("dve", col(0, 8), ps(None, ps.t[0:T, 0:8]))
        C.act(col(8, 12), col(4, 8), AF.Exp, scale=-1.0)
        C.act(col(8, 12), col(8, 12), AF.Ln, bias=1.0)
        C.ts("dve", col(8, 12), col(8, 12), -1.0, None, ALU.mult)
        ps = next_psb()
        C.mm(ps(None, ps.t[0:T, 0:4]), SEGTRI[kd](None, SEGTRI[kd].t[0:T, 0:T]), col(8, 12))
        C.copy("dve", col(12, 16), ps(None, ps.t[0:T, 0:4]))
        C.tt("dve", col(16, 20), col(0, 4), col(12, 16), ALU.subtract)
        if is_s:
            C.dma(MS.all(), I["s_mm"][s0:s0 + 8, :])
            ps = next_psb()
            C.mm(ps(None, ps.t[0:T, 0:4]), SEGSEL(None, SEGSEL.t[0:8, 0:T]), MS.all())
            C.copy("dve", col(20, 24), ps(None, ps.t[0:T, 0:4]))
            for h in range(4):
                C.copy("dve", MSB.all(), MS(None, MS.t[:, h:h + 1].broadcast_to([8, 128])))
                ps = next_psb()
                C.mm(ps(None, ps.t[:, 0:8]), MSB.all(), ident(None, ident.t[0:8, 0:8]))
                C.copy("dve", MINIT(None, MINIT.t[:, h, :]), ps(None, ps.t[:, 0:8]))
        else:
            if tile["first"]:
                C.memset("dve", MCAR.all(), 0.0)
                C.memset("pool", CS.all(), 0.0)
            C.copy("dve", col(20, 24), MCAR(None, MCAR.t[0:T, :]))
            C.copy("dve", MINIT(None, MINIT.t[:, :, 0:1]), MCAR(None, MCAR.t[:, :].unsqueeze(2)))

        row = lambda k: ROW(None, ROW.t[:, k, 0:T])
        ends = lambda k: ROW(None, ROW.t[:, k, 0:T].rearrange("p (n l) -> p n l", l=L)[:, :, L - 1])
        starts = lambda k: ROW(None, ROW.t[:, k, 0:T].rearrange("p (n l) -> p n l", l=L)[:, :, 0])
        for h in range(4):
            minit_r = MINIT(None, MINIT.t[:, h, 0:nseq])
            ps = next_psb()
            C.mm(ps(None, ps.t[:, 0:T]), ident(None, ident.t[0:8, h:h + 1].broadcast_to([8, 128])), gx)
            C.copy("dve", row(0), ps(None, ps.t[:, 0:T]))
            ps = next_psb()
            C.mm(ps(None, ps.t[:, 0:T]), ident(None, ident.t[0:8, 4 + h:5 + h].broadcast_to([8, 128])), gx)
            C.act(row(1), ps(None, ps.t[:, 0:T]), AF.Exp, scale=-1.0)
            C.act(row(1), row(1), AF.Ln, bias=1.0)
            C.ts("dve", row(1), row(1), -1.0, None, ALU.mult)
            scan(row(2), R01[kd](None, R01[kd].t[:, 0:T]), row(1), 0.0, ALU.mult, ALU.add)
            C.tt("dve", row(3), row(0), row(2), ALU.subtract)
            C.tt("dve", starts(3), starts(3), minit_r, ALU.max)
            scan(row(4), RNEG[kd](None, RNEG[kd].t[:, 0:T]), row(3), -1e30, ALU.add, ALU.max)
            C.copy("dve", ROW(None, ROW.t[:, 5, 0:T].rearrange("p (n l) -> p n l", l=L)),
                   ROW(None, ROW.t[:, 4, 0:T].rearrange("p (n l) -> p n l", l=L)[:, :, L - 1:L].broadcast_to([128, nseq, L])))
            C.tt("dve", MNEW(None, MNEW.t[:, h, 0:nseq]), ends(2), ends(4), ALU.add)
            C.tt("dve", DEC(None, DEC.t[:, h, 0:nseq]), minit_r, ends(4), ALU.subtract)
            C.act(DEC(None, DEC.t[:, h, 0:nseq]), DEC(None, DEC.t[:, h, 0:nseq]), AF.Exp)
            ps = next_psb()
            C.mm(ps(None, ps.t[0:T, 0:1]), row(4), ident(None, ident.t[:, 0:1]))
            C.mm(ps(None, ps.t[0:T, 1:2]), row(5), ident(None, ident.t[:, 0:1]))
            C.copy("dve", col(24, 26), ps(None, ps.t[0:T, 0:2]))
            C.tt("dve", DTB(None, DTB.t[0:T, 0:T]), ROW(None, ROW.t[0:T, 4, 0:T]), MASKBIG[kd](None, MASKBIG[kd].t[0:T, 0:T]), ALU.add)
            C.act(DTB(None, DTB.t[0:T, 0:T]), DTB(None, DTB.t[0:T, 0:T]), AF.Exp, scale=-1.0, bias=col(16 + h, 17 + h))
            ps = next_psa()
            for kt in range(2):
                C.mm(ps(None, ps.t[0:T, 0:T]), pj(8 + 2 * h + kt), pj(2 * h + kt), start=(kt == 0), stop=(kt == 1))
            C.tt("dve", STB(None, STB.t[0:T, 0:T]), ps(None, ps.t[0:T, 0:T]), DTB(None, DTB.t[0:T, 0:T]), ALU.mult)
            ps1 = next_psa()
            C.mm(ps1(None, ps1.t[0:T, 0:257]), STB(None, STB.t[0:T, 0:T]), VTM(None, VTM.t[0:T, h, :]))
            C.copy("act", P1S(None, P1S.t[0:T, :]), ps1(None, ps1.t[0:T, 0:257]))
            ps2 = next_psa()
            if is_s:
                i_ = 0
                for n in range(nseq):
                    cs = CSS[n % 2]
                    C.dma(cs(None, cs.t[:, :, 0:256]), I["s_mc"][s0 + n, h].rearrange("(kt p) v -> p kt v", p=128))
                    C.dma(cs(None, cs.t[:, :, 256:257]), I["s_mn"][s0 + n, h].rearrange("(kt p o) -> p kt o", p=128, o=1), slow=True)
                    for kt in range(2):
                        qm = QM[i_ % 2]
                        i_ += 1
                        C.tt("pool", qm(None, qm.t[:, 0:T]), pj(2 * h + kt), SEGROW(None, SEGROW.t[:, n, 0:T]), ALU.mult)
                        C.mm(ps2(None, ps2.t[0:T, 0:257]), qm(None, qm.t[:, 0:T]), cs(None, cs.t[:, kt, :]),
                             start=(n == 0 and kt == 0), stop=(n == nseq - 1 and kt == 1))
            else:
                for kt in range(2):
                    C.mm(ps2(None, ps2.t[0:T, 0:257]), pj(2 * h + kt), CS(("h", h), CS.t[:, h, kt, :]), start=(kt == 0), stop=(kt == 1))
            sm = lambda a: SM(None, SM.t[0:T, a:a + 1])
            C.tt("dve", sm(0), col(20 + h, 21 + h), col(24, 25), ALU.subtract)
            C.act(sm(0), sm(0), AF.Exp)
            C.stt("dve", NUM(None, NUM.t[0:T, :]), ps2(None, ps2.t[0:T, 0:257]), sm(0), P1S(None, P1S.t[0:T, :]), ALU.mult, ALU.add)
            C.tt("dve", sm(1), col(12 + h, 13 + h), col(24, 25), ALU.add)
            C.act(sm(1), sm(1), AF.Exp, scale=-1.0)
            C.ts("dve", sm(2), NUM(None, NUM.t[0:T, 256:257]), -1.0, None, ALU.mult)
            C.tt("dve", sm(2), sm(2), NUM(None, NUM.t[0:T, 256:257]), ALU.max)
            C.tt("dve", sm(2), sm(2), sm(1), ALU.max)
            recip("dve", sm(2), sm(2))
            C.op("dve", lambda g, T=T: g.reduce_sum(out=SM.t[0:T, 3:4], in_=NUM.t[0:T, 0:256], axis=AX.X),
                 reads=[NUM(None, NUM.t[0:T, 0:256])], writes=[sm(3)])
            C.tt("dve", sm(3), sm(3), sm(2), ALU.mult)
            C.ts("dve", sm(3), sm(3), 1.0 / 256, None, ALU.mult)
            hn = HN(None, HN.t[0:T, :])
            C.ts("dve", hn, NUM(None, NUM.t[0:T, 0:256]), sm(2), sm(3), ALU.mult, ALU.subtract)
            C.tt("dve", P1S(None, P1S.t[0:T, 0:256]), hn, hn, ALU.mult)
            C.op("dve", lambda g, T=T: g.reduce_sum(out=SM.t[0:T, 4:5], in_=P1S.t[0:T, 0:256], axis=AX.X),
                 reads=[P1S(None, P1S.t[0:T, 0:256])], writes=[sm(4)])
            C.ts("dve", sm(4), sm(4), 1.0 / 256, LN_EPS, ALU.mult, ALU.add)
            C.act(sm(4), sm(4), AF.Sqrt)
            recip("dve", sm(4), sm(4))
            C.ts("dve", hn, hn, sm(4), None, ALU.mult)
            for kt in range(2):
                ct = 2 * h + kt
                ps = next_psb()
                C.tr(ps(None, ps.t[:, 0:T]), HN(None, HN.t[0:T, kt * 128:(kt + 1) * 128]), ident(None, ident.t[0:T, 0:T]))
                scr = P1S(None, P1S.t[:, 0:T])
                C.stt("dve", scr, ps(None, ps.t[:, 0:T]), VEC1(None, VEC1.t[:, ct, 0:1]), pj(16 + ct), ALU.mult, ALU.mult)
                C.tt("dve", mixv(ct, T), scr, pj(24 + ct), ALU.mult)
            C.tt("dve", sm(5), col(16 + h, 17 + h), col(25, 26), ALU.subtract)
            C.act(sm(5), sm(5), AF.Exp)
            for kt in range(2):
                ps = next_psb()
                C.tr(ps(None, ps.t[0:T, 0:128]), pj(8 + 2 * h + kt), ident.all())
                C.ts("dve", KW(None, KW.t[0:T, h * 256 + kt * 128:h * 256 + (kt + 1) * 128]), ps(None, ps.t[0:T, 0:128]), sm(5), None, ALU.mult)
            if is_s:
                for n in range(nseq):
                    cs = CSS[n % 2]
                    co = CSO[n % 2]
                    C.dma(cs(None, cs.t[:, :, 0:256]), I["s_mc"][s0 + n, h].rearrange("(kt p) v -> p kt v", p=128))
                    C.dma(cs(None, cs.t[:, :, 256:257]), I["s_mn"][s0 + n, h].rearrange("(kt p o) -> p kt o", p=128, o=1), slow=True)
                    C.ts("pool", KWN(None, KWN.t[0:T, :]), KW(None, KW.t[0:T, h * 256:(h + 1) * 256]), SEGC(None, SEGC.t[0:T, n:n + 1]), None, ALU.mult)
                    for kt in range(2):
                        ps = next_psa()
                        C.mm(ps(None, ps.t[:, 0:257]), KWN(None, KWN.t[0:T, kt * 128:(kt + 1) * 128]), VTM(None, VTM.t[0:T, h, :]))
                        C.stt("dve", co(None, co.t[:, kt, :]), cs(None, cs.t[:, kt, :]), DEC(None, DEC.t[:, h, n:n + 1]), ps(None, ps.t[:, 0:257]), ALU.mult, ALU.add)
                    C.dma(O["s_mc_o"][s0 + n, h].rearrange("(kt p) v -> p kt v", p=128), co(None, co.t[:, :, 0:256]))
                    C.dma(O["s_mn_o"][s0 + n, h].rearrange("(kt p o) -> p kt o", p=128, o=1), co(None, co.t[:, :, 256:257]), slow=True)
                C.dma(O["s_mm_o"][s0:s0 + 8, h:h + 1].rearrange("n o -> o n"), MNEW(None, MNEW.t[0:1, h, 0:8]), slow=True)
            else:
                for kt in range(2):
                    ps = next_psa()
                    C.mm(ps(None, ps.t[:, 0:257]), KW(None, KW.t[0:T, h * 256 + kt * 128:h * 256 + (kt + 1) * 128]), VTM(None, VTM.t[0:T, h, :]))
                    csv = CS(("h", h), CS.t[:, h, kt, :])
                    C.stt("dve", csv, csv, DEC(None, DEC.t[:, h, 0:1]), ps(None, ps.t[:, 0:257]), ALU.mult, ALU.add)
                C.copy("dve", MCAR(None, MCAR.t[:, h:h + 1]), MNEW(None, MNEW.t[:, h, 0:1]))
        if (not is_s) and tile["last"]:
            for h in range(4):
                C.dma(O["p_mc"][h].rearrange("(kt p) v -> p kt v", p=128), CS(("h", h), CS.t[:, h, :, 0:256]))
                C.dma(O["p_mn"][h].rearrange("(kt p o) -> p kt o", p=128, o=1), CS(("h", h), CS.t[:, h, :, 256:257]), slow=True)
            C.dma(O["p_mm"][:, :], MCAR(None, MCAR.t[0:1, :]))

        if do_rwkv:
            (rwkv2 if cfg.get("rwkv2", True) else rwkv)(tile)
        else:
            for ct in range(8, 16):
                C.memset("pool", mixv(ct, T), 0.0)
        out_proj_ln(I["od_w_out"], tile, VECO, 0, 1)

    for ti, tile in enumerate(tile_plan(cfg)):
        wst["id"] = 0
        wst["first"] = (ti == 0)
        load_x(tile)
        if nlayers >= 1:
            layer0(tile)
        if nlayers >= 2:
            layer1(tile)
        store_y(tile)

    C.emit()
    es.close()
    return nc, C


def make_in_maps(inp, cores, consts):
    maps = []
    f = lambda a: np.ascontiguousarray(a, dtype=np.float32)
    for c in cores:
        s = c % 4
        m = {}
        m["xp"] = f(inp["x_prompt"][s])
        m["meta"] = f(inp["meta_tokens"])
        m["xs"] = f(inp["x_sample"][16 * c:16 * c + 16].reshape(128, D))
        m["s_conv"] = f(inp["state_conv"][0, 16 * c:16 * c + 16].reshape(480, 1024))
        m["s_sre"] = f(inp["state_ssm_re"][0, 16 * c:16 * c + 16])
        m["s_sim"] = f(inp["state_ssm_im"][0, 16 * c:16 * c + 16])
        m.update(consts)
        sl = slice(16 * c, 16 * c + 16)
        m["s_mc"] = f(inp["state_mlstm_c"][0, sl])
        m["s_mn"] = f(inp["state_mlstm_n"][0, sl])
        m["s_mm"] = f(inp["state_mlstm_m"][0, sl])
        m["s_rs"] = f(inp["state_rwkv_s"][0, sl])
        m["s_rsh"] = f(inp["state_rwkv_shift"][0, sl])
        m["od_w_in"] = f(inp["od_w_in"][0])
        for nm in ("m_ig_b", "m_fg_b", "m_hn_g", "r_w0", "r_a0", "r_kk", "r_ka", "r_ln_g", "r_ln_b", "r_rk", "r_mu", "od_ln_g", "od_ln_b"):
            m[nm] = f(inp[nm][0].reshape(1, -1))
        for nm in ("r_w2", "r_a2", "od_w_out"):
            m[nm] = f(inp[nm][0])
        m["ev_w_in"] = f(inp["ev_w_in"][0])
        m["a_conv_w"] = f(inp["a_conv_w"][0])
        for nm in ("a_conv_b", "a_ln_g", "a_ln_b", "s5_d", "s5_log_dt", "s5_glu_b", "ev_ln_g", "ev_ln_b"):
            m[nm] = f(inp[nm][0].reshape(1, -1))
        m["a_pw"] = f(inp["a_pw"][0])
        for nm in ("s5_lambda_re", "s5_lambda_im", "s5_b_re", "s5_b_im", "s5_glu_w", "ev_w_out"):
            m[nm] = f(inp[nm][0])
        m["s5_c_re"] = f(inp["s5_c_re"][0].reshape(1024, 64))
        m["s5_c_im"] = f(inp["s5_c_im"][0].reshape(1024, 64))
        maps.append(m)
    return maps


def kernel(**inp):
    cfg = {}
    nc, C = build(cfg)
    consts = make_consts()
    cores = list(range(NCORES))
    maps = make_in_maps(inp, cores, consts)
    res = run_bass_kernel_spmd(nc, maps, core_ids=cores)
    R = res.results
    B = 4
    cat = lambda k, shp: np.concatenate([R[c][k].reshape((16,) + shp) for c in range(NCORES)], 0)[None]
    stk = lambda k, shp: np.stack([R[c][k].reshape(shp) for c in range(B)], 0)[None]
    y_p = np.stack([R[c]["y_p"] for c in range(B)], 0)
    y_s = np.concatenate([R[c]["y_s"].reshape(16, 8, D) for c in range(NCORES)], 0)
    return (y_p, y_s,
            stk("p_conv", (30, 1024)), stk("p_sre", (64, 64)), stk("p_sim", (64, 64)), stk("p_mc", (4, 256, 256)),
            stk("p_mn", (4, 256)), stk("p_mm", (4,)), stk("p_rs", (16, 64, 64)), stk("p_rsh", (3200,)),
            cat("s_conv_o", (30, 1024)), cat("s_sre_o", (64, 64)), cat("s_sim_o", (64, 64)), cat("s_mc_o", (4, 256, 256)),
            cat("s_mn_o", (4, 256)), cat("s_mm_o", (4,)), cat("s_rs_o", (16, 64, 64)), cat("s_rsh_o", (3200,)))
```

```python
import contextlib
import numpy as np
import concourse.bass as bass
import concourse.mybir as mybir
from concourse.bass_utils import run_bass_kernel_spmd

F32 = mybir.dt.float32
BF16 = mybir.dt.bfloat16
AF = mybir.ActivationFunctionType
ALU = mybir.AluOpType
AX = mybir.AxisListType

D = 2048
TT = 128
WG = 512
KQ = 4
NWS = 4
NSLAB = 200
NCORES = 8
ALPHA = 4 ** 0.25
LN_EPS = 1e-5


class Reg:
    __slots__ = ("w", "r")

    def __init__(self):
        self.w = None
        self.r = []


class Buf:
    def __init__(self, ctx, name, t):
        self.ctx, self.name, self.t = ctx, name, t
        self.regs = {"_all": Reg()}
        self.dma_sem = None
        self.dma_cnt = 0

    def __call__(self, key, ap):
        return View(self, key, ap)

    def all(self):
        return View(self, None, self.t[:])

    def _sel(self, key):
        if key is None:
            return list(self.regs.values())
        if key not in self.regs:
            self.regs[key] = Reg()
        return [self.regs[key], self.regs["_all"]]

    def rdeps(self, key):
        return [r.w for r in self._sel(key) if r.w is not None]

    def wdeps(self, key):
        out = []
        for r in self._sel(key):
            if r.w is not None:
                out.append(r.w)
            out.extend(r.r)
        return out

    def note_read(self, key, tok):
        if key is None:
            for r in self.regs.values():
                r.r.append(tok)
        else:
            self._sel(key)[0].r.append(tok)

    def note_write(self, key, tok):
        if key is None:
            self.regs = {"_all": Reg()}
            self.regs["_all"].w = tok
        else:
            r = self._sel(key)[0]
            r.w = tok
            r.r = []


class View:
    __slots__ = ("buf", "key", "ap")

    def __init__(self, buf, key, ap):
        self.buf, self.key, self.ap = buf, key, ap


class Ctx:
    ENG = ("pe", "act", "dve", "pool", "sp")
    EPOCH = 30000

    def __init__(self, nc, es):
        self.nc, self.es = nc, es
        self.prog = {e: [] for e in self.ENG}
        self.cnt = {e: 0 for e in self.ENG}
        self.sem = {e: es.enter_context(nc.semaphore("sem_" + e)) for e in self.ENG}
        self.known = {e: {} for e in self.ENG}
        self.final = []
        self.total = {}
        self.nsem = 5
        self.nbytes = 0

    def sb(self, name, shape, dtype=F32):
        t = self.es.enter_context(self.nc.sbuf_tensor("sb_" + name, list(shape), dtype))
        n = 4
        for s in shape[1:]:
            n *= s
        self.nbytes += n
        return Buf(self, name, t)

    def ps(self, name, shape, dtype=F32):
        t = self.es.enter_context(self.nc.psum_tensor("ps_" + name, list(shape), dtype))
        return Buf(self, name, t)

    def need(self, e, tok):
        sem, val = tok
        k = id(sem)
        if self.known[e].get(k, 0) >= val:
            return
        self.known[e][k] = val
        self.prog[e].append(("wait", sem, val))

    def op(self, e, fn, reads=(), writes=()):
        for v in reads:
            if isinstance(v, View):
                for tok in v.buf.rdeps(v.key):
                    self.need(e, tok)
        for v in writes:
            if isinstance(v, View):
                for tok in v.buf.wdeps(v.key):
                    self.need(e, tok)
        if self.cnt[e] >= self.EPOCH:
            self.total[e] = self.total.get(e, 0) + self.cnt[e]
            self.sem[e] = self.es.enter_context(self.nc.semaphore("sem_%s_%d" % (e, self.total[e])))
            self.cnt[e] = 0
            self.nsem += 1
        self.cnt[e] += 1
        tok = (self.sem[e], self.cnt[e])
        self.prog[e].append(("op", fn, self.sem[e], 1))
        for v in reads:
            if isinstance(v, View):
                v.buf.note_read(v.key, tok)
        for v in writes:
            if isinstance(v, View):
                v.buf.note_write(v.key, tok)
        return tok

    def dma(self, out, in_, q="sp", slow=False):
        sbv = out if isinstance(out, View) else in_
        b = sbv.buf
        if b.dma_sem is None:
            b.dma_sem = self.es.enter_context(self.nc.semaphore("dq_" + b.name))
            self.nsem += 1
        if isinstance(in_, View):
            for tok in in_.buf.rdeps(in_.key):
                self.need(q, tok)
        if isinstance(out, View):
            for tok in out.buf.wdeps(out.key):
                self.need(q, tok)
        b.dma_cnt += 16
        tok = (b.dma_sem, b.dma_cnt)
        oap = out.ap if isinstance(out, View) else out
        iap = in_.ap if isinstance(in_, View) else in_
        if slow:
            fn = lambda eng, oap=oap, iap=iap: eng.dma_start(out=oap, in_=iap, allow_slow_non_contiguous=True)
        else:
            fn = lambda eng, oap=oap, iap=iap: eng.dma_start(out=oap, in_=iap)
        self.prog[q].append(("op", fn, b.dma_sem, 16))
        if isinstance(in_, View):
            in_.buf.note_read(in_.key, tok)
        if isinstance(out, View):
            out.buf.note_write(out.key, tok)
        else:
            self.final.append(tok)
        return tok

    def tt(self, e, out, in0, in1, op):
        return self.op(e, lambda g: g.tensor_tensor(out=out.ap, in0=in0.ap, in1=in1.ap, op=op),
                       reads=[in0, in1], writes=[out])

    def ts(self, e, out, in0, s1, s2, op0, op1=None):
        rd = [in0] + [s for s in (s1, s2) if isinstance(s, View)]
        a1 = s1.ap if isinstance(s1, View) else s1
        a2 = s2.ap if isinstance(s2, View) else s2
        if op1 is None:
            return self.op(e, lambda g: g.tensor_scalar(out=out.ap, in0=in0.ap, scalar1=a1, scalar2=None, op0=op0),
                           reads=rd, writes=[out])
        return self.op(e, lambda g: g.tensor_scalar(out=out.ap, in0=in0.ap, scalar1=a1, scalar2=a2, op0=op0, op1=op1),
                       reads=rd, writes=[out])

    def stt(self, e, out, in0, s, in1, op0, op1):
        rd = [in0, in1] + ([s] if isinstance(s, View) else [])
        a = s.ap if isinstance(s, View) else s
        return self.op(e, lambda g: g.scalar_tensor_tensor(out=out.ap, in0=in0.ap, scalar=a, in1=in1.ap, op0=op0, op1=op1),
                       reads=rd, writes=[out])

    def act(self, out, in_, func, bias=None, scale=None, e="act"):
        rd = [in_] + [s for s in (bias, scale) if isinstance(s, View)]
        kw = {}
        if bias is not None:
            kw["bias"] = bias.ap if isinstance(bias, View) else bias
        if scale is not None:
            kw["scale"] = scale.ap if isinstance(scale, View) else scale
        return self.op(e, lambda g: g.activation(out=out.ap, in_=in_.ap, func=func, **kw), reads=rd, writes=[out])

    def copy(self, e, out, in_):
        if e == "act":
            return self.act(out, in_, AF.Copy)
        return self.op(e, lambda g: g.tensor_copy(out=out.ap, in_=in_.ap), reads=[in_], writes=[out])

    def memset(self, e, out, val):
        return self.op(e, lambda g: g.memset(out.ap, val), writes=[out])

    def mm(self, out, lhsT, rhs, start=True, stop=True):
        return self.op("pe", lambda g: g.matmul(out.ap, lhsT=lhsT.ap, rhs=rhs.ap, start=start, stop=stop),
                       reads=[lhsT, rhs], writes=[out])

    def tr(self, out, in_, ident):
        return self.op("pe", lambda g: g.transpose(out.ap, in_.ap, ident.ap), reads=[in_, ident], writes=[out])

    def emit(self):
        nc = self.nc
        for tok in self.final:
            self.need("sp", tok)
        for e in self.ENG:
            if e != "sp" and self.cnt[e] > 0:
                self.need("sp", (self.sem[e], self.cnt[e]))
        engs = {"pe": "tensor", "act": "scalar", "dve": "vector", "pool": "gpsimd", "sp": "sync"}
        with nc.Block() as block:
            for e, attr in engs.items():
                items = self.prog[e]

                def body(eng, items=items):
                    for it in items:
                        if it[0] == "wait":
                            eng.wait_ge(it[1], it[2])
                        else:
                            it[1](eng).then_inc(it[2], it[3])

                getattr(block, attr)(body)


def make_consts():
    c = {}
    c["ident"] = np.eye(128, dtype=np.float32)
    c["ones"] = np.ones((128, 128), dtype=np.float32)
    r = np.arange(128)
    mrow = (r % 32) // 16
    mcol = np.arange(128) // 64
    bm = (mrow[:, None] == mcol[None, :]).astype(np.float32)
    ev_r = ((r // 32) % 2 == 0).astype(np.float32)
    c["bmask"] = bm * ev_r[:, None]
    c["bmask_o"] = bm * (1 - ev_r)[:, None]
    gl = np.arange(128) // 16
    cm = ((gl[None, :] % 2) == (r[:, None] // 64)).astype(np.float32)
    c["cmask"] = cm * ev_r[None, :]
    c["cmask_o"] = cm * (1 - ev_r)[None, :]
    BIG = 1e30
    i128 = np.arange(128)
    allow_p = (i128[:, None] <= i128[None, :])
    c["maskbig_p"] = np.where(allow_p, 0.0, BIG).astype(np.float32)
    c["segtri_p"] = allow_p.astype(np.float32)
    i64 = np.arange(64)
    allow_s = (i64[:, None] <= i64[None, :]) & ((i64[:, None] // 8) == (i64[None, :] // 8))
    c["maskbig_s"] = np.where(allow_s, 0.0, BIG).astype(np.float32)
    c["segtri_s"] = allow_s.astype(np.float32)
    r01p = np.ones((128, 128), np.float32); r01p[:, 0] = 0
    r01s = np.ones((128, 64), np.float32); r01s[:, ::8] = 0
    c["r01_p"], c["r01_s"] = r01p, r01s
    c["rneg_p"] = ((1 - r01p) * -BIG).astype(np.float32)
    c["rneg_s"] = ((1 - r01s) * -BIG).astype(np.float32)
    segsel = ((i64[None, :] // 8) == np.arange(8)[:, None]).astype(np.float32)
    c["segsel"] = segsel
    c["segc"] = np.ascontiguousarray(segsel.T)
    c["segrow"] = np.ascontiguousarray(np.broadcast_to(segsel.reshape(1, 512), (128, 512))).astype(np.float32)
    c["blk"] = ((i128[:, None] // 64) == (i128[None, :] // 64)).astype(np.float32)
    st_p = (i128[:, None] < i128[None, :])
    st_s = (i64[:, None] < i64[None, :]) & ((i64[:, None] // 8) == (i64[None, :] // 8))
    c["sstri_p"] = st_p.astype(np.float32)
    c["sstriT_p"] = np.ascontiguousarray(st_p.T).astype(np.float32)
    c["sstri_s"] = st_s.astype(np.float32)
    c["sstriT_s"] = np.ascontiguousarray(st_s.T).astype(np.float32)
    return c


CONST_SHAPES = {"ident": [128, 128], "ones": [128, 128], "bmask": [128, 128], "cmask": [128, 128],
                "bmask_o": [128, 128], "cmask_o": [128, 128],
                "maskbig_p": [128, 128], "segtri_p": [128, 128], "maskbig_s": [64, 64], "segtri_s": [64, 64],
                "r01_p": [128, 128], "r01_s": [128, 64], "rneg_p": [128, 128], "rneg_s": [128, 64],
                "segsel": [8, 64], "segc": [64, 8], "segrow": [128, 512], "blk": [128, 128],
                "sstri_p": [128, 128], "sstriT_p": [128, 128], "sstri_s": [64, 64], "sstriT_s": [64, 64]}


def tile_plan(cfg):
    tiles = []
    npt = cfg.get("n_prompt_tiles", 17)
    pos = 0
    for i in range(17):
        T = 16 if i == 16 else TT
        if i < npt:
            tiles.append(dict(kind="p", T=T, pos=pos, nseq=1, L=T, first=(i == 0), last=(i == npt - 1), s0=0))
        pos += T
    if cfg.get("sample", True):
        for h in range(2):
            tiles.append(dict(kind="s", T=64, pos=0, nseq=8, L=8, first=True, last=True, s0=8 * h))
    return tiles


def build(cfg):
    nc = bass.Bass("TRN2", target_bir_lowering=False)
    es = contextlib.ExitStack()
    C = Ctx(nc, es)
    dbg = cfg.get("debug", False)
    nlayers = cfg.get("layers", 2)

    def din(name, shape):
        return nc.dram_tensor(name, list(shape), F32, kind="ExternalInput").ap()

    def dout(name, shape):
        return nc.dram_tensor(name, list(shape), F32, kind="ExternalOutput").ap()

    I = {}
    I["xp"] = din("xp", [2048, D])
    I["meta"] = din("meta", [16, D])
    I["xs"] = din("xs", [128, D])
    I["s_conv"] = din("s_conv", [16 * 30, 1024])
    I["s_sre"] = din("s_sre", [16, 64, 64])
    I["s_sim"] = din("s_sim", [16, 64, 64])
    for nm, shp in CONST_SHAPES.items():
        I[nm] = din(nm, shp)
    I["s_mc"] = din("s_mc", [16, 4, 256, 256])
    I["s_mn"] = din("s_mn", [16, 4, 256])
    I["s_mm"] = din("s_mm", [16, 4])
    I["s_rs"] = din("s_rs", [16, 16, 64, 64])
    I["s_rsh"] = din("s_rsh", [16, 3200])
    I["od_w_in"] = din("od_w_in", [D, 9352])
    I["m_ig_b"] = din("m_ig_b", [1, 4])
    I["m_fg_b"] = din("m_fg_b", [1, 4])
    for nm in ("m_hn_g", "r_w0", "r_a0", "r_kk", "r_ka", "r_ln_g", "r_ln_b", "r_rk"):
        I[nm] = din(nm, [1, 1024])
    I["r_mu"] = din("r_mu", [1, 3200])
    I["r_w2"] = din("r_w2", [64, 1024])
    I["r_a2"] = din("r_a2", [64, 1024])
    I["od_w_out"] = din("od_w_out", [2048, D])
    I["od_ln_g"] = din("od_ln_g", [1, D])
    I["od_ln_b"] = din("od_ln_b", [1, D])
    I["ev_w_in"] = din("ev_w_in", [D, 5120])
    I["a_conv_w"] = din("a_conv_w", [31, 1024])
    for nm in ("a_conv_b", "a_ln_g", "a_ln_b", "s5_d"):
        I[nm] = din(nm, [1, 1024])
    I["a_pw"] = din("a_pw", [1024, 1024])
    I["s5_lambda_re"] = din("s5_lambda_re", [64, 64])
    I["s5_lambda_im"] = din("s5_lambda_im", [64, 64])
    I["s5_log_dt"] = din("s5_log_dt", [1, 64])
    I["s5_b_re"] = din("s5_b_re", [64, 64, 16])
    I["s5_b_im"] = din("s5_b_im", [64, 64, 16])
    I["s5_c_re"] = din("s5_c_re", [1024, 64])
    I["s5_c_im"] = din("s5_c_im", [1024, 64])
    I["s5_glu_w"] = din("s5_glu_w", [1024, 2048])
    I["s5_glu_b"] = din("s5_glu_b", [1, 2048])
    I["ev_w_out"] = din("ev_w_out", [2048, D])
    I["ev_ln_g"] = din("ev_ln_g", [1, D])
    I["ev_ln_b"] = din("ev_ln_b", [1, D])

    O = {}
    O["y_p"] = dout("y_p", [2048, D])
    O["y_s"] = dout("y_s", [128, D])
    O["p_conv"] = dout("p_conv", [30, 1024])
    O["p_sre"] = dout("p_sre", [64, 64])
    O["p_sim"] = dout("p_sim", [64, 64])
    O["s_conv_o"] = dout("s_conv_o", [16 * 30, 1024])
    O["s_sre_o"] = dout("s_sre_o", [16, 64, 64])
    O["s_sim_o"] = dout("s_sim_o", [16, 64, 64])
    O["p_mc"] = dout("p_mc", [4, 256, 256])
    O["p_mn"] = dout("p_mn", [4, 256])
    O["p_mm"] = dout("p_mm", [1, 4])
    O["p_rs"] = dout("p_rs", [16, 64, 64])
    O["p_rsh"] = dout("p_rsh", [1, 3200])
    O["s_mc_o"] = dout("s_mc_o", [16, 4, 256, 256])
    O["s_mn_o"] = dout("s_mn_o", [16, 4, 256])
    O["s_mm_o"] = dout("s_mm_o", [16, 4])
    O["s_rs_o"] = dout("s_rs_o", [16, 16, 64, 64])
    O["s_rsh_o"] = dout("s_rsh_o", [16, 3200])

    ident = C.sb("ident", [128, 128])
    ones = C.sb("ones", [128, 128])
    XT = C.sb("XT", [128, 16, TT])
    XS = C.sb("XS", [128, D])
    PJ = C.sb("PJ", [128, 33, TT])
    MIX = C.sb("MIX", [128, 16 * TT], BF16)
    XTB = C.sb("XTB", [128, 16, TT], BF16)
    GELB = C.sb("GELB", [128, 8, TT], BF16)
    WS = [C.sb("WS%d" % i, [128, KQ, WG], BF16) for i in range(NWS)]
    HB = C.sb("HB", [128, 8, 304])
    HC = C.sb("HC", [128, 8, 30])
    ACC = C.sb("ACC", [128, 8, TT])
    SQ2 = C.sb("SQ2", [128, 2, TT])
    ST = C.sb("ST", [128, 3, TT])
    VEC0 = C.sb("VEC0", [128, 8, 40])
    VECG = C.sb("VECG", [128, 16, 4])
    TMP = C.sb("TMP", [128, 128])
    XR = C.sb("XR", [128, 32, 129])
    XI = C.sb("XI", [128, 32, 129])
    S5C = C.sb("S5C", [128, 2, 32])
    S5A = C.sb("S5A", [128, 8, 32])
    S5T = C.sb("S5T", [128, 10, 32])
    SCS = C.sb("SCS", [128, 2, 32, 8])
    BRE = [C.sb("BRE%d" % i, [128, 8, 128]) for i in range(2)]
    BIM = [C.sb("BIM%d" % i, [128, 8, 128]) for i in range(2)]
    CRE = [C.sb("CRE%d" % i, [128, 8, 128]) for i in range(2)]
    CIM = [C.sb("CIM%d" % i, [128, 8, 128]) for i in range(2)]
    mask_bo = C.sb("mask_bo", [128, 128])
    mask_co = C.sb("mask_co", [128, 128])
    S5ST = C.sb("S5ST", [128, 128])
    mask_b = C.sb("mask_b", [128, 128])
    mask_c = C.sb("mask_c", [128, 128])

    PSA = [C.ps("PSA%d" % i, [128, 512]) for i in range(4)]
    PSB = [C.ps("PSB%d" % i, [128, 512]) for i in range(4)]

    def mixv(ct, T):
        return MIX(("t", ct), MIX.t[:, ct * TT:ct * TT + T])

    C.dma(ident.all(), I["ident"])
    C.dma(ones.all(), I["ones"])
    C.dma(mask_b.all(), I["bmask"])
    C.dma(mask_c.all(), I["cmask"])
    C.dma(mask_bo.all(), I["bmask_o"])
    C.dma(mask_co.all(), I["cmask_o"])

    rr = {"psa": 0, "psb": 0, "ws": 0, "ev": 0, "sq": 0, "tm": 0}

    def next_psa():
        rr["psa"] = (rr["psa"] + 1) % 4
        return PSA[rr["psa"]]

    def next_psb():
        rr["psb"] = (rr["psb"] + 1) % 4
        return PSB[rr["psb"]]

    def ev_eng():
        rr["ev"] += 1
        return "dve" if rr["ev"] % 2 else "act"

    def load_rows_T(dst, rows, ncols, col0=0, ct0=0):
        r0 = 0
        for ap in rows:
            nr = ap.shape[0]
            C.dma(XS(None, XS.t[r0:r0 + nr, 0:ncols]), ap)
            r0 += nr
        nr = r0
        for ct in range(ncols // 128):
            ps = next_psb()
            C.tr(ps(None, ps.t[:, 0:nr]), XS(None, XS.t[0:nr, ct * 128:(ct + 1) * 128]), ident(None, ident.t[0:nr, 0:nr]))
            C.copy("dve", dst(None, dst.t[:, ct0 + ct, col0:col0 + nr]), ps(None, ps.t[:, 0:nr]))

    load_rows_T(VEC0, [I["a_conv_w"], I["a_conv_b"], I["a_ln_g"], I["a_ln_b"], I["s5_d"]], 1024)
    load_rows_T(VECG, [I["s5_glu_b"], I["ev_ln_g"], I["ev_ln_b"]], 2048)

    def gp_ap(ap2d):
        return ap2d.rearrange("(q m) p -> (m p) q", m=2)

    LR, LI, DT, AR, AI, NAI = range(6)
    A = lambda k: S5A(None, S5A.t[:, k, :])
    Tm = lambda k: S5T(None, S5T.t[:, k, :])
    for m in range(2):
        C.dma(S5A(None, S5A.t[64 * m:64 * m + 64, LR, :]), I["s5_lambda_re"].rearrange("(q m) p -> m p q", m=2)[m], slow=True)
        C.dma(S5A(None, S5A.t[64 * m:64 * m + 64, LI, :]), I["s5_lambda_im"].rearrange("(q m) p -> m p q", m=2)[m], slow=True)
    ldt = I["s5_log_dt"]
    for m in range(2):
        src = bass.AP(ldt.tensor, ldt.offset + m, [[0, 64], [2, 32]])
        C.dma(S5A(None, S5A.t[64 * m:64 * m + 64, DT, :]), src, slow=True)
    C.act(A(DT), A(DT), AF.Exp)
    C.tt("dve", Tm(0), A(LR), A(DT), ALU.mult)
    C.act(Tm(1), Tm(0), AF.Exp)
    C.tt("dve", Tm(2), A(LI), A(DT), ALU.mult)
    PI = float(np.pi)
    C.ts("dve", Tm(3), Tm(2), 1.0 / 32, None, ALU.mult)
    C.act(Tm(4), Tm(3), AF.Sin)
    C.ts("dve", Tm(3), Tm(3), PI / 2, None, ALU.add)
    C.act(Tm(5), Tm(3), AF.Sin)
    for _ in range(5):
        C.tt("dve", Tm(3), Tm(4), Tm(5), ALU.mult)
        C.tt("dve", Tm(8), Tm(5), Tm(5), ALU.mult)
        C.tt("dve", Tm(9), Tm(4), Tm(4), ALU.mult)
        C.ts("dve", Tm(4), Tm(3), 2.0, None, ALU.mult)
        C.tt("dve", Tm(5), Tm(8), Tm(9), ALU.subtract)
    C.tt("dve", A(AR), Tm(1), Tm(5), ALU.mult)
    C.tt("dve", A(AI), Tm(1), Tm(4), ALU.mult)
    C.ts("dve", A(NAI), A(AI), -1.0, None, ALU.mult)
    MAG = 6
    C.copy("dve", A(MAG), Tm(1))
    s5tab = nc.dram_tensor("s5tab", [2, 128, 1024], F32).ap()
    tC = lambda a, b: XR(None, XR.t[:, :, a:b])
    tS = lambda a, b: XI(None, XI.t[:, :, a:b])
    C.copy("dve", tC(0, 1), S5T(None, S5T.t[:, 5, :].unsqueeze(2)))
    C.copy("dve", tS(0, 1), S5T(None, S5T.t[:, 4, :].unsqueeze(2)))
    n_ = 1
    while n_ < 32:
        cn = XR(None, XR.t[:, :, n_ - 1:n_].broadcast_to([128, 32, n_]))
        sn = XI(None, XI.t[:, :, n_ - 1:n_].broadcast_to([128, 32, n_]))
        u1 = XR(None, XR.t[:, :, 64:64 + n_])
        u2 = XI(None, XI.t[:, :, 64:64 + n_])
        C.tt("dve", u1, tS(0, n_), sn, ALU.mult)
        C.tt("dve", tC(n_, 2 * n_), tC(0, n_), cn, ALU.mult)
        C.tt("dve", tC(n_, 2 * n_), tC(n_, 2 * n_), u1, ALU.subtract)
        C.tt("dve", u2, tC(0, n_), sn, ALU.mult)
        C.tt("dve", tS(n_, 2 * n_), tS(0, n_), cn, ALU.mult)
        C.tt("dve", tS(n_, 2 * n_), tS(n_, 2 * n_), u2, ALU.add)
        n_ *= 2
    tab_tok = [C.dma(s5tab[0].rearrange("p (q t) -> p q t", t=32), tC(0, 32)),
               C.dma(s5tab[1].rearrange("p (q t) -> p q t", t=32), tS(0, 32))]
    for tk in tab_tok:
        C.need("sp", tk)
    C.tt("dve", Tm(0), A(LR), A(LR), ALU.mult)
    C.tt("dve", Tm(1), A(LI), A(LI), ALU.mult)
    C.tt("dve", Tm(0), Tm(0), Tm(1), ALU.add)
    C.op("dve", lambda g: g.reciprocal(out=S5T.t[:, 0, :], in_=S5T.t[:, 0, :]), reads=[Tm(0)], writes=[Tm(0)])
    C.ts("dve", Tm(1), A(AR), -1.0, None, ALU.add)
    C.tt("dve", Tm(2), Tm(1), A(LR), ALU.mult)
    C.tt("dve", Tm(3), A(AI), A(LI), ALU.mult)
    C.tt("dve", Tm(2), Tm(2), Tm(3), ALU.add)
    C.tt("dve", Tm(6), Tm(2), Tm(0), ALU.mult)
    C.tt("dve", Tm(2), A(AI), A(LR), ALU.mult)
    C.tt("dve", Tm(3), Tm(1), A(LI), ALU.mult)
    C.tt("dve", Tm(2), Tm(2), Tm(3), ALU.subtract)
    C.tt("dve", Tm(7), Tm(2), Tm(0), ALU.mult)
    braw_t = PJ.t[:, 0:8, :].rearrange("p a b -> p (a b)").rearrange("p (k q h) -> p k q h", k=2, q=32)
    bb_t = PJ.t[:, 8:24, :].rearrange("p a b -> p (a b)").rearrange("p (k q h) -> p k q h", k=2, q=32)
    sc_t = PJ.t[:, 24:28, :].rearrange("p a b -> p (a b)").rearrange("p (q h) -> p q h", q=32)
    for k, nm in enumerate(("s5_b_re", "s5_b_im")):
        for m in range(2):
            C.dma(PJ(None, braw_t[64 * m:64 * m + 64, k, :, :]), I[nm].rearrange("(q m) p h -> m p q h", m=2)[m])
    qr_b = S5T(None, S5T.t[:, 6, :].unsqueeze(2).broadcast_to([128, 32, 16]))
    qi_b = S5T(None, S5T.t[:, 7, :].unsqueeze(2).broadcast_to([128, 32, 16]))
    br = PJ(None, braw_t[:, 0, :, :])
    bi = PJ(None, braw_t[:, 1, :, :])
    sc = PJ(None, sc_t)
    for dup in range(2):
        o_r = PJ(None, bb_t[:, 0, :, dup * 16:(dup + 1) * 16])
        o_i = PJ(None, bb_t[:, 1, :, dup * 16:(dup + 1) * 16])
        C.tt("dve", o_r, br, qr_b, ALU.mult)
        C.tt("dve", sc, bi, qi_b, ALU.mult)
        C.tt("dve", o_r, o_r, sc, ALU.subtract)
        C.tt("dve", o_i, bi, qr_b, ALU.mult)
        C.tt("dve", sc, br, qi_b, ALU.mult)
        C.tt("dve", o_i, o_i, sc, ALU.add)
    for k, dst in enumerate((BRE, BIM)):
        for gt in range(8):
            ps = next_psb()
            C.copy("dve", TMP.all(), PJ(None, bb_t[:, k, 4 * gt:4 * gt + 4, :]))
            C.tr(ps(None, ps.t[:, 0:128]), TMP.all(), ident.all())
            C.tt("dve", dst[0](None, dst[0].t[:, gt, :]), ps(None, ps.t[:, 0:128]), mask_b.all(), ALU.mult)
            C.tt("dve", dst[1](None, dst[1].t[:, gt, :]), ps(None, ps.t[:, 0:128]), mask_bo.all(), ALU.mult)
    for k, (nm, dst) in enumerate((("s5_c_re", CRE), ("s5_c_im", CIM))):
        for gt in range(8):
            for dup in range(2):
                C.dma(XS(None, XS.t[:, dup * 64:(dup + 1) * 64]), I[nm][gt * 128:(gt + 1) * 128, :])
            ps = next_psb()
            C.tr(ps(None, ps.t[:, 0:128]), XS(None, XS.t[:, 0:128]), ident.all())
            for par, mk in enumerate((mask_c, mask_co)):
                C.stt("dve", dst[par](None, dst[par].t[:, gt, :]), ps(None, ps.t[:, 0:128]), 1.0 if k == 0 else -1.0,
                      mk.all(), ALU.mult, ALU.mult)

    WSCR = nc.dram_tensor("wscr", [NSLAB, 128, KQ * WG], BF16).ap()
    wst = {"id": 0, "first": True, "tok": {}}

    def load_slab(ws, src, nk, gw):
        sid = wst["id"]
        wst["id"] += 1
        assert sid < NSLAB
        dstv = ws(None, ws.t[:, 0:nk, 0:gw])
        scr = WSCR[sid][:, 0:nk * gw].rearrange("p (k c) -> p k c", k=nk)
        if wst["first"]:
            C.dma(dstv, src, q="pool")
            wst["tok"][sid] = C.dma(scr, dstv, q="sp")
        else:
            C.need("sp", wst["tok"][sid])
            C.dma(dstv, scr, q="sp")

    def stream_mm(W, nkt, coltiles, rhs_fn, T, evac):
        for _ in stream_mm_g(W, nkt, coltiles, rhs_fn, T, evac):
            pass

    def stream_mm_g(W, nkt, coltiles, rhs_fn, T, evac):
        groups = []
        cur = []
        for ct in coltiles:
            if cur and (ct[0] + ct[1] - cur[0][0] > WG):
                groups.append(cur)
                cur = []
            cur.append(ct)
        if cur:
            groups.append(cur)
        Wv = W.rearrange("(kt p) c -> p kt c", p=128)
        j = 0
        for grp in groups:
            g0 = grp[0][0]
            gw = grp[-1][0] + grp[-1][1] - g0
            ps = next_psa()
            pacc = ps(None, ps.t[0:T, 0:gw])
            for kq in range(0, nkt, KQ):
                ws = WS[rr["ws"] % NWS]
                rr["ws"] += 1
                nk = min(KQ, nkt - kq)
                load_slab(ws, Wv[:, kq:kq + nk, g0:g0 + gw], nk, gw)
                for k in range(nk):
                    kt = kq + k
                    C.mm(pacc, rhs_fn(kt), ws(None, ws.t[:, k, 0:gw]), start=(kt == 0), stop=(kt == nkt - 1))
            i_ = rr["tm"] % 2
            rr["tm"] += 1
            C.copy(ev_eng(), XS(("tm", i_), XS.t[0:T, i_ * 512:i_ * 512 + gw]), pacc)
            for (c0, w) in grp:
                pt = next_psb()
                C.tr(pt(None, pt.t[0:w, 0:T]), XS(("tm", i_), XS.t[0:T, i_ * 512 + c0 - g0:i_ * 512 + c0 - g0 + w]),
                     ident(None, ident.t[0:T, 0:T]))
                evac(j, pt(None, pt.t[0:w, 0:T]))
                j += 1
            yield "g"

    def layer_norm_cols(src, ntile, T, gcol, bcol, vec, func=AF.Identity, eps=LN_EPS, dst_fn=None, also_fn=None):
        nch = ntile * 128
        ps = next_psb()
        ps2 = next_psb()
        for ct in range(ntile):
            sv = src(("t", ct), src.t[:, ct, 0:T])
            sq = SQ2(("s", rr["sq"] % 2), SQ2.t[:, rr["sq"] % 2, 0:T])
            rr["sq"] += 1
            C.act(sq, sv, AF.Square)
            C.mm(ps(None, ps.t[:, 0:T]), ones.all(), sv, start=(ct == 0), stop=(ct == ntile - 1))
            C.mm(ps2(None, ps2.t[:, 0:T]), ones.all(), sq, start=(ct == 0), stop=(ct == ntile - 1))
        st = lambda k: ST(None, ST.t[:, k, 0:T])
        C.ts("dve", st(0), ps(None, ps.t[:, 0:T]), 1.0 / nch, None, ALU.mult)
        C.ts("dve", st(1), ps2(None, ps2.t[:, 0:T]), 1.0 / nch, None, ALU.mult)
        C.tt("dve", st(2), st(0), st(0), ALU.mult)
        C.tt("dve", st(1), st(1), st(2), ALU.subtract)
        C.ts("dve", st(1), st(1), eps, None, ALU.add)
        C.act(st(1), st(1), AF.Sqrt)
        C.op("dve", lambda g, T=T: g.reciprocal(out=ST.t[:, 1, 0:T], in_=ST.t[:, 1, 0:T]), reads=[st(1)], writes=[st(1)])
        C.tt("dve", st(2), st(0), st(1), ALU.mult)
        C.ts("dve", st(2), st(2), -1.0, None, ALU.mult)
        for ct in range(ntile):
            e = "dve" if ct % 2 == 0 else "pool"
            sv = src(("t", ct), src.t[:, ct, 0:T])
            C.tt(e, sv, sv, st(1), ALU.mult)
            C.tt(e, sv, sv, st(2), ALU.add)
            dv = dst_fn(ct) if dst_fn is not None else sv
            C.act(dv, sv, func, bias=vec(None, vec.t[:, ct, bcol:bcol + 1]), scale=vec(None, vec.t[:, ct, gcol:gcol + 1]))
            if also_fn is not None:
                C.copy("pool" if ct % 2 else "act", also_fn(ct), dv)

    def load_x(tile):
        n = tile["T"]
        if tile["kind"] == "s":
            C.dma(XS(None, XS.t[0:n, :]), I["xs"][tile["s0"] * 8:tile["s0"] * 8 + n, :])
        else:
            p0 = tile["pos"]
            r = 0
            if p0 < 16:
                C.dma(XS(None, XS.t[0:16, :]), I["meta"][:, :])
                r = 16
            x0 = p0 + r - 16
            C.dma(XS(None, XS.t[r:n, :]), I["xp"][x0:x0 + n - r, :])
        for dt_ in range(16):
            ps = next_psb()
            C.tr(ps(None, ps.t[:, 0:n]), XS(None, XS.t[0:n, dt_ * 128:(dt_ + 1) * 128]), ident(None, ident.t[0:n, 0:n]))
            C.copy("act" if dt_ % 2 else "dve", XT(("t", dt_), XT.t[:, dt_, 0:n]), ps(None, ps.t[:, 0:n]))
            C.copy("pool", XTB(("t", dt_), XTB.t[:, dt_, 0:n]), XT(("t", dt_), XT.t[:, dt_, 0:n]))

    def store_y(tile):
        n = tile["T"]
        for dt_ in range(16):
            ps = next_psb()
            C.tr(ps(None, ps.t[0:n, 0:128]), XT(("t", dt_), XT.t[:, dt_, 0:n]), ident.all())
            C.copy("act" if dt_ % 2 else "dve", XS(None, XS.t[0:n, dt_ * 128:(dt_ + 1) * 128]), ps(None, ps.t[0:n, 0:128]))
        if tile["kind"] == "s":
            C.dma(O["y_s"][tile["s0"] * 8:tile["s0"] * 8 + n, :], XS(None, XS.t[0:n, :]))
        else:
            p0 = tile["pos"]
            r = 16 if p0 < 16 else 0
            x0 = p0 + r - 16
            C.dma(O["y_p"][x0:x0 + n - r, :], XS(None, XS.t[r:n, :]))

    def out_proj_ln(W, tile, vec, gcol, bcol):
        T = tile["T"]

        def evac(j, pv):
            xv_ = XT(("t", j), XT.t[:, j, 0:T])
            C.stt("dve", xv_, xv_, ALPHA, pv, ALU.mult, ALU.add)

        stream_mm(W, 16, [(i * 128, 128) for i in range(16)], lambda kt: mixv(kt, T), T, evac)
        layer_norm_cols(XT, 16, T, gcol, bcol, vec, also_fn=lambda ct: XTB(("t", ct), XTB.t[:, ct, 0:T]))

    def layer0(tile):
        T, nseq, L = tile["T"], tile["nseq"], tile["L"]
        is_s = tile["kind"] == "s"
        s0 = tile["s0"]
        xrhs = lambda kt: XTB(("t", kt), XTB.t[:, kt, 0:T])
        pj = lambda j: PJ(("t", j), PJ.t[:, j, 0:T])

        fence(HB, HB.t[0:1, 0, 0:1])
        def evacA(j, pv):
            if j < 8:
                C.copy(ev_eng(), pj(j), pv)
            elif j < 16:
                C.act(pj(j), pv, AF.Sigmoid)
            else:
                C.act(pj(j), pv, AF.Silu)

        stream_mm(I["ev_w_in"], 16, [(i * 128, 128) for i in range(24)], xrhs, T, evacA)

        W_ = 30 + L

        def hb(ct, a, b):
            v = HB.t[:, ct, 0:nseq * W_].rearrange("p (n w) -> p n w", w=W_)[:, :, a:b]
            return HB(("t", ct), v)

        def tokv(buf, ct):
            return buf(("t", ct), buf.t[:, ct, 0:T].rearrange("p (n l) -> p n l", l=L))

        if is_s:
            for q in range(2):
                C.dma(XS(None, XS.t[0:120, 0:1024]), I["s_conv"][s0 * 30 + q * 120:s0 * 30 + (q + 1) * 120, :])
                for ct in range(8):
                    ps = next_psb()
                    C.tr(ps(None, ps.t[:, 0:120]), XS(None, XS.t[0:120, ct * 128:(ct + 1) * 128]), ident(None, ident.t[0:120, 0:120]))
                    dstv = HB.t[:, ct, 0:nseq * W_].rearrange("p (n w) -> p n w", w=W_)[:, 4 * q:4 * q + 4, 0:30]
                    C.copy("dve", HB(("t", ct), dstv), ps(None, ps.t[:, 0:120].rearrange("p (n r) -> p n r", r=30)))
        else:
            for ct in range(8):
                if tile["first"]:
                    C.memset("pool", hb(ct, 0, 30), 0.0)
                else:
                    C.copy("pool", hb(ct, 0, 30), HC(("t", ct), HC.t[:, ct, :].unsqueeze(1)))
        for ct in range(8):
            e = "dve" if ct % 2 == 0 else "pool"
            C.tt(e, hb(ct, 30, 30 + L), tokv(PJ, ct), tokv(PJ, 8 + ct), ALU.mult)
        for ct in range(8):
            e = "dve"
            acc = tokv(ACC, ct)
            wcol = lambda j, ct=ct: VEC0(None, VEC0.t[:, ct, j:j + 1])
            C.ts(e, acc, hb(ct, 0, L), wcol(0), wcol(31), ALU.mult, ALU.add)
            for j in range(1, 31):
                C.stt(e, acc, hb(ct, j, j + L), wcol(j), acc, ALU.mult, ALU.add)
        if is_s:
            for q in range(2):
                for ct in range(8):
                    ps = next_psb()
                    srcv = HB.t[:, ct, 0:nseq * W_].rearrange("p (n w) -> p n w", w=W_)[:, 4 * q:4 * q + 4, L:L + 30]
                    C.copy("pool", TMP(None, TMP.t[:, 0:120].rearrange("p (n r) -> p n r", r=30)), HB(("t", ct), srcv))
                    C.tr(ps(None, ps.t[0:120, 0:128]), TMP(None, TMP.t[:, 0:120]), ident.all())
                    C.copy("dve", XS(None, XS.t[0:120, ct * 128:(ct + 1) * 128]), ps(None, ps.t[0:120, 0:128]))
                C.dma(O["s_conv_o"][s0 * 30 + q * 120:s0 * 30 + (q + 1) * 120, :], XS(None, XS.t[0:120, 0:1024]))
        else:
            for ct in range(8):
                C.copy("pool", TMP(None, TMP.t[:, 0:30]), HB(("t", ct), HB.t[:, ct, L:L + 30]))
                C.copy("pool", HC(("t", ct), HC.t[:, ct, :]), TMP(None, TMP.t[:, 0:30]))
            if tile["last"]:
                for ct in range(8):
                    ps = next_psb()
                    C.tr(ps(None, ps.t[0:30, 0:128]), HC(("t", ct), HC.t[:, ct, :]), ident.all())
                    C.copy("dve", XS(None, XS.t[0:30, ct * 128:(ct + 1) * 128]), ps(None, ps.t[0:30, 0:128]))
                C.dma(O["p_conv"][:, :], XS(None, XS.t[0:30, 0:1024]))
        layer_norm_cols(ACC, 8, T, 32, 33, VEC0, func=AF.Silu, dst_fn=lambda ct: mixv(8 + ct, T))

        def evac_pw(j, pv):
            C.tt("dve", mixv(j, T), pv, pj(16 + j), ALU.mult)

        stream_mm(I["a_pw"], 8, [(i * 128, 128) for i in range(8)], lambda kt: mixv(8 + kt, T), T, evac_pw)

        def evacB(j, pv):
            if j < 8:
                C.copy(ev_eng(), pj(j), pv)
            else:
                C.act(pj(j), pv, AF.Silu)

        stream_mm(I["ev_w_in"], 16, [(3072 + i * 128, 128) for i in range(16)], xrhs, T, evacB)

        Wx = 1 + L

        def xv(buf, a, b, p0=0, p1=32):
            v = buf.t[:, p0:p1, 0:nseq * Wx].rearrange("p q (n w) -> p q n w", w=Wx)[:, :, :, a:b]
            return buf(None, v)

        if is_s:
            for k, (nm, buf) in enumerate((("s_sre", XR), ("s_sim", XI))):
                for q in range(2):
                    for pr in range(16):
                        g0 = 2 * (16 * q + pr)
                        C.dma(S5ST(None, S5ST.t[pr * 8:(pr + 1) * 8, :]),
                              I[nm][s0:s0 + 8, g0:g0 + 2, :].rearrange("n m p -> n (m p)"))
                    ps = next_psb()
                    C.tr(ps(None, ps.t[:, 0:128]), S5ST.all(), ident.all())
                    C.copy("dve", xv(buf, 0, 1, 16 * q, 16 * q + 16),
                           ps(None, ps.t[:, 0:128].rearrange("p (q n o) -> p q n o", n=8, o=1)))
        else:
            for k, buf in enumerate((XR, XI)):
                if tile["first"]:
                    C.memset("pool", xv(buf, 0, 1), 0.0)
                else:
                    C.copy("pool", xv(buf, 0, 1), S5C(None, S5C.t[:, k, :].unsqueeze(2).unsqueeze(3)))
        for q4 in range(8):
            for k, (tab, buf) in enumerate(((BRE, XR), (BIM, XI))):
                ps = next_psa()
                for ip in range(4):
                    hf = ip // 2
                    tb = tab[ip % 2]
                    C.mm(ps(None, ps.t[:, ip * T:(ip + 1) * T]),
                         tb(None, tb.t[64 * hf:64 * hf + 64, q4, :]),
                         PJ(("t", q4), PJ.t[64 * hf:64 * hf + 64, q4, 0:T]))
                C.copy("act", xv(buf, 1, Wx, 4 * q4, 4 * q4 + 4),
                       ps(None, ps.t[:, 0:4 * T].rearrange("p (q n l) -> p q n l", q=4, l=L)))
        arb = S5A(None, S5A.t[:, AR, :].unsqueeze(2).broadcast_to([128, 32, nseq]))
        aib = S5A(None, S5A.t[:, AI, :].unsqueeze(2).broadcast_to([128, 32, nseq]))
        naib = S5A(None, S5A.t[:, NAI, :].unsqueeze(2).broadcast_to([128, 32, nseq]))
        if nseq == 1:
            t1v = S5T(None, S5T.t[:, 8, :].unsqueeze(2))
            t2v = S5T(None, S5T.t[:, 9, :].unsqueeze(2))
        else:
            t1v = SCS(None, SCS.t[:, 0, :, :])
            t2v = SCS(None, SCS.t[:, 1, :, :])

        def col(buf, t):
            v = buf.t[:, :, 0:nseq * Wx].rearrange("p q (n w) -> p q n w", w=Wx)[:, :, :, t]
            return buf(None, v)

        if is_s:
            e = "dve"
            for t in range(L):
                C.tt(e, t1v, col(XR, t), arb, ALU.mult)
                C.tt(e, col(XR, t + 1), col(XR, t + 1), t1v, ALU.add)
                C.tt(e, t1v, col(XI, t), naib, ALU.mult)
                C.tt(e, col(XR, t + 1), col(XR, t + 1), t1v, ALU.add)
                C.tt(e, t2v, col(XI, t), arb, ALU.mult)
                C.tt(e, col(XI, t + 1), col(XI, t + 1), t2v, ALU.add)
                C.tt(e, t2v, col(XR, t), aib, ALU.mult)
                C.tt(e, col(XI, t + 1), col(XI, t + 1), t2v, ALU.add)
        else:
            VTMf_ = VTM.t[:, :, :].rearrange("p a b -> p (a b)")
            PJf_ = PJ.t[:, 24:32, :].rearrange("p a b -> p (a b)")
            ROWf_ = ROW.t[:, :, :].rearrange("p a b -> p (a b)")
            C.dma(VTM(None, VTMf_[:, 0:1024]), s5tab[0])
            C.dma(PJ(None, PJf_), s5tab[1])
            for t0 in range(0, T, 32):
                Tc = min(32, T - t0)
                ec = VTM(None, VTMf_[:, 0:1024].rearrange("p (q t) -> p q t", t=32)[:, :, 0:Tc])
                es = PJ(None, PJf_.rearrange("p (q t) -> p q t", t=32)[:, :, 0:Tc])
                w1 = KW(None, KW.t[:, :].rearrange("p (q t) -> p q t", t=32)[:, :, 0:Tc])
                w2 = ROW(None, ROWf_.rearrange("p (q t) -> p q t", t=32)[:, :, 0:Tc])
                ur = XR(None, XR.t[:, :, 1 + t0:1 + t0 + Tc])
                ui = XI(None, XI.t[:, :, 1 + t0:1 + t0 + Tc])
                C.tt("pool", w1, ur, es, ALU.mult)
                C.tt("dve", w2, ui, es, ALU.mult)
                C.tt("dve", ur, ur, ec, ALU.mult)
                C.tt("pool", ui, ui, ec, ALU.mult)
                C.tt("dve", ur, ur, w2, ALU.add)
                C.tt("pool", ui, ui, w1, ALU.subtract)
                for pr in range(32):
                    rho = S5A(None, S5A.t[:, MAG, pr:pr + 1].broadcast_to([128, Tc]))
                    for buf in (XR, XI):
                        seg = buf(None, buf.t[:, pr, 1 + t0:1 + t0 + Tc])
                        scan(seg, rho, seg, buf(None, buf.t[:, pr, t0:t0 + 1]), ALU.mult, ALU.add)
                C.tt("pool", w1, ur, es, ALU.mult)
                C.tt("dve", w2, ui, es, ALU.mult)
                C.tt("dve", ur, ur, ec, ALU.mult)
                C.tt("pool", ui, ui, ec, ALU.mult)
                C.tt("dve", ur, ur, w2, ALU.subtract)
                C.tt("pool", ui, ui, w1, ALU.add)
        if is_s:
            for k, (nm, buf) in enumerate((("s_sre_o", XR), ("s_sim_o", XI))):
                for q in range(2):
                    C.copy("pool", TMP(None, TMP.t[:, 0:128].rearrange("p (q n o) -> p q n o", n=8, o=1)),
                           xv(buf, L, L + 1, 16 * q, 16 * q + 16))
                    ps = next_psb()
                    C.tr(ps(None, ps.t[:, 0:128]), TMP(None, TMP.t[:, 0:128]), ident.all())
                    C.copy("dve", S5ST.all(), ps(None, ps.t[:, 0:128]))
                    for pr in range(16):
                        g0 = 2 * (16 * q + pr)
                        C.dma(O[nm][s0:s0 + 8, g0:g0 + 2, :].rearrange("n m p -> n (m p)"),
                              S5ST(None, S5ST.t[pr * 8:(pr + 1) * 8, :]))
        else:
            for k, buf in enumerate((XR, XI)):
                C.copy("pool", S5C(None, S5C.t[:, k, :].unsqueeze(2).unsqueeze(3)), xv(buf, L, L + 1))
            if tile["last"]:
                for k, nm in enumerate(("p_sre", "p_sim")):
                    ps = next_psb()
                    C.tr(ps(None, ps.t[0:32, 0:128]), S5C(None, S5C.t[:, k, :]), ident.all())
                    C.copy("dve", S5ST(None, S5ST.t[0:32, :]), ps(None, ps.t[0:32, 0:128]))
                    C.dma(O[nm].rearrange("(q m) p -> q (m p)", m=2), S5ST(None, S5ST.t[0:32, :]))
        for gt in range(8):
            ps = next_psa()
            for ip in range(4):
                pair = 4 * gt + ip
                hf = ip // 2
                ov = ps(None, ps.t[64 * hf:64 * hf + 64, 0:T])
                xr_ = XR(None, XR.t[:, pair, 0:nseq * Wx].rearrange("p (n w) -> p n w", w=Wx)[:, :, 1:Wx])
                xi_ = XI(None, XI.t[:, pair, 0:nseq * Wx].rearrange("p (n w) -> p n w", w=Wx)[:, :, 1:Wx])
                cr, ci = CRE[ip % 2], CIM[ip % 2]
                C.mm(ov, cr(None, cr.t[:, gt, 64 * hf:64 * hf + 64]), xr_, start=(ip % 2 == 0), stop=False)
                C.mm(ov, ci(None, ci.t[:, gt, 64 * hf:64 * hf + 64]), xi_, start=False, stop=(ip % 2 == 1))
            gv = pj(gt)
            C.stt("dve", gv, gv, VEC0(None, VEC0.t[:, gt, 34:35]), ps(None, ps.t[:, 0:T]), ALU.mult, ALU.add)
            C.act(GELB(("t", gt), GELB.t[:, gt, 0:T]), gv, AF.Gelu)

        def evac_glu(j, pv):
            if j < 8:
                C.act(ACC(("t", j), ACC.t[:, j, 0:T]), pv, AF.Identity, bias=VECG(None, VECG.t[:, j, 0:1]))
            else:
                jj = j - 8
                tv = TMP(None, TMP.t[:, 0:T])
                C.act(tv, pv, AF.Sigmoid, bias=VECG(None, VECG.t[:, j, 0:1]))
                C.tt("dve", tv, tv, ACC(("t", jj), ACC.t[:, jj, 0:T]), ALU.mult)
                C.tt("dve", mixv(8 + jj, T), tv, pj(8 + jj), ALU.mult)

        stream_mm(I["s5_glu_w"], 8, [(i * 128, 128) for i in range(16)], lambda kt: GELB(("t", kt), GELB.t[:, kt, 0:T]), T, evac_glu)
        out_proj_ln(I["ev_w_out"], tile, VECG, 1, 2)

    do_rwkv = cfg.get("rwkv", True)
    MASKBIG = {"p": C.sb("mbig_p", [128, 128]), "s": C.sb("mbig_s", [64, 64])}
    SEGTRI = {"p": C.sb("stri_p", [128, 128]), "s": C.sb("stri_s", [64, 64])}
    R01 = {"p": C.sb("r01p", [128, 128]), "s": C.sb("r01s", [128, 64])}
    RNEG = {"p": C.sb("rnegp", [128, 128]), "s": C.sb("rnegs", [128, 64])}
    SEGSEL = C.sb("SEGSEL", [8, 64])
    SEGC = C.sb("SEGC", [64, 8])
    SEGROW = C.sb("SEGROW", [128, 8, 64])
    for k in ("p", "s"):
        C.dma(MASKBIG[k].all(), I["maskbig_" + k])
        C.dma(SEGTRI[k].all(), I["segtri_" + k])
        C.dma(R01[k].all(), I["r01_" + k])
        C.dma(RNEG[k].all(), I["rneg_" + k])
    C.dma(SEGSEL.all(), I["segsel"])
    C.dma(SEGC.all(), I["segc"])
    C.dma(SEGROW.all(), I["segrow"].rearrange("p (n t) -> p n t", n=8))

    VEC1 = C.sb("VEC1", [128, 8, 8])
    VMU = C.sb("VMU", [128, 25, 1])
    VECO = C.sb("VECO", [128, 16, 2])
    GB = C.sb("GB", [8, 1])
    load_rows_T(VEC1, [I[n_] for n_ in ("m_hn_g", "r_w0", "r_a0", "r_kk", "r_ka", "r_ln_g", "r_ln_b", "r_rk")], 1024)
    load_rows_T(VMU, [I["r_mu"][:, 0:2048]], 2048)
    load_rows_T(VMU, [I["r_mu"][:, 2048:3200]], 1152, ct0=16)
    load_rows_T(VECO, [I["od_ln_g"], I["od_ln_b"]], 2048)
    C.dma(GB(None, GB.t[0:4, :]), I["m_ig_b"].rearrange("o h -> h o"), slow=True)
    C.dma(GB(None, GB.t[4:8, :]), I["m_fg_b"].rearrange("o h -> h o"), slow=True)

    VTM = C.sb("VTM", [128, 4, 257])
    KW = C.sb("KW", [128, 1024])
    KWN = C.sb("KWN", [128, 256])
    GX = C.sb("GX", [8, 128])
    ROW = C.sb("ROW", [128, 8, 128])
    COL = C.sb("COL", [128, 64])
    DTB = C.sb("DTB", [128, 128])
    STB = C.sb("STB", [128, 128])
    P1S = C.sb("P1S", [128, 257])
    NUM = C.sb("NUM", [128, 257])
    HN = C.sb("HN", [128, 256])
    SM = C.sb("SM", [128, 16])
    CS = C.sb("CS", [128, 4, 2, 257])
    MCAR = C.sb("MCAR", [128, 4])
    CSS = [C.sb("CSS%d" % i, [128, 2, 257]) for i in range(2)]
    CSO = [C.sb("CSO%d" % i, [128, 2, 257]) for i in range(2)]
    MS = C.sb("MS", [8, 4])
    MSB = C.sb("MSB", [8, 128])
    MINIT = C.sb("MINIT", [128, 4, 8])
    DEC = C.sb("DEC", [128, 4, 8])
    MNEW = C.sb("MNEW", [128, 4, 8])
    QM = [C.sb("QM%d" % i, [128, 64]) for i in range(2)]
    C.memset("pool", VTM(None, VTM.t[:, :, 256:257]), 1.0)

    def stream_mm_tok(W, c0, ncols, T, evac):
        Wv = W.rearrange("(kt p) c -> p kt c", p=128)
        for g in range(ncols // WG):
            ps = next_psa()
            pv = ps(None, ps.t[0:T, 0:WG])
            for kq in range(0, 16, KQ):
                ws = WS[rr["ws"] % NWS]
                rr["ws"] += 1
                load_slab(ws, Wv[:, kq:kq + KQ, c0 + g * WG:c0 + (g + 1) * WG], KQ, WG)
                for k in range(KQ):
                    kt = kq + k
                    C.mm(pv, XTB(("t", kt), XTB.t[:, kt, 0:T]), ws(None, ws.t[:, k, 0:WG]), start=(kt == 0), stop=(kt == 15))
            evac(g, pv)

    def recip(e, out, in_):
        return C.op(e, lambda g: g.reciprocal(out=out.ap, in_=in_.ap), reads=[in_], writes=[out])

    def scan(out, d0, d1, init, op0, op1):
        rd = [d0, d1] + ([init] if isinstance(init, View) else [])
        ia = init.ap if isinstance(init, View) else init
        return C.op("dve", lambda g: g.tensor_tensor_scan(out=out.ap, data0=d0.ap, data1=d1.ap, initial=ia, op0=op0, op1=op1),
                    reads=rd, writes=[out])

    SR = C.sb("SR", [128, 8, 64])
    W2A2 = C.sb("W2A2", [128, 1024])
    BLK = C.sb("BLK", [128, 128])
    OMKA = C.sb("OMKA", [128, 8])
    SHC = C.sb("SHC", [128, 25])
    SUMB = C.sb("SUMB", [128, 8])
    C.dma(W2A2(None, W2A2.t[0:64, :]), I["r_w2"])
    C.dma(W2A2(None, W2A2.t[64:128, :]), I["r_a2"])
    C.dma(BLK.all(), I["blk"])
    C.ts("dve", OMKA.all(), VEC1(None, VEC1.t[:, :, 4]), -1.0, 1.0, ALU.mult, ALU.add)
    XIf = XI.t[:, :, :].rearrange("p a b -> p (a b)")
    T1 = XI("T1", XIf[:, 0:512].rearrange("p (j k) -> p j k", k=64))
    T2 = XI("T2", XIf[:, 512:1024].rearrange("p (j k) -> p j k", k=64))
    FSv = XIf[:, 1024:1536].rearrange("p (i t) -> p i t", t=128)
    SRS = [XI(("SRS", i), XIf[:, 1536 + 512 * i:2048 + 512 * i].rearrange("p (j k) -> p j k", k=64)) for i in range(2)]
    TWv = XIf[:, 2560:2688]
    ALLPS = PSA + PSB

    def next_ps8():
        rr["ps8"] = (rr.get("ps8", 0) + 1) % 8
        return ALLPS[rr["ps8"]]

    def rwkv(tile):
        T, nseq, L = tile["T"], tile["nseq"], tile["L"]
        is_s = tile["kind"] == "s"
        s0 = tile["s0"]
        Wx = 1 + L
        xrhs = lambda kt: XTB(("t", kt), XTB.t[:, kt, 0:T])
        pj = lambda j: PJ(("t", j), PJ.t[:, j, 0:T])
        pj3 = lambda j: PJ(("t", j), PJ.t[:, j, 0:T].rearrange("p (n l) -> p n l", l=L))

        def ppv(j0, j1, a, b):
            v = XR.t[:, j0:j1, 0:nseq * Wx].rearrange("p j (n w) -> p j n w", w=Wx)[:, :, :, a:b]
            return XR(("pp", j0) if j1 == j0 + 1 else None, v)

        if is_s:
            for (c0, ncol, ct0) in ((0, 2048, 0), (2048, 1152, 16)):
                C.dma(XS(None, XS.t[0:8, 0:ncol]), I["s_rsh"][s0:s0 + 8, c0:c0 + ncol])
                for ct in range(ncol // 128):
                    ps = next_psb()
                    C.tr(ps(None, ps.t[:, 0:8]), XS(None, XS.t[0:8, ct * 128:(ct + 1) * 128]), ident(None, ident.t[0:8, 0:8]))
                    C.copy("dve", ppv(ct0 + ct, ct0 + ct + 1, 0, 1), ps(None, ps.t[:, 0:8].rearrange("p (j n o) -> p j n o", j=1, o=1)))
        else:
            if tile["first"]:
                C.memset("pool", ppv(0, 25, 0, 1), 0.0)
            else:
                C.copy("pool", ppv(0, 25, 0, 1), SHC(None, SHC.t[:, :].unsqueeze(2).unsqueeze(3)))

        def evacR(j, pv):
            if j < 25:
                C.copy(ev_eng(), ppv(j, j + 1, 1, Wx), pv.buf(None, pv.ap.rearrange("p (j n l) -> p j n l", j=1, l=L)))
            else:
                C.act(pj(j), pv, AF.Silu)

        stream_mm(I["od_w_in"], 16, [(5128 + i * 128, 128) for i in range(33)], xrhs, T, evacR)

        if is_s:
            for (c0, ncol, ct0) in ((0, 2048, 0), (2048, 1152, 16)):
                for ct in range(ncol // 128):
                    ps = next_psb()
                    C.copy("pool", TMP(None, TMP.t[:, 0:8]), XR(("pp", ct0 + ct), XR.t[:, ct0 + ct, 0:nseq * Wx].rearrange("p (n w) -> p n w", w=Wx)[:, :, L]))
                    C.tr(ps(None, ps.t[0:8, 0:128]), TMP(None, TMP.t[:, 0:8]), ident.all())
                    C.copy("dve", XS(None, XS.t[0:8, ct * 128:(ct + 1) * 128]), ps(None, ps.t[0:8, 0:128]))
                C.dma(O["s_rsh_o"][s0:s0 + 8, c0:c0 + ncol], XS(None, XS.t[0:8, 0:ncol]))
        else:
            C.copy("pool", SHC(None, SHC.t[:, :].unsqueeze(2).unsqueeze(3)), ppv(0, 25, L, L + 1))
            if tile["last"]:
                for (c0, ncol, ct0) in ((0, 2048, 0), (2048, 1152, 16)):
                    for ct in range(ncol // 128):
                        ps = next_psb()
                        C.tr(ps(None, ps.t[0:1, 0:128]), SHC(None, SHC.t[:, ct0 + ct:ct0 + ct + 1]), ident.all())
                        C.copy("dve", XS(None, XS.t[0:1, ct * 128:(ct + 1) * 128]), ps(None, ps.t[0:1, 0:128]))
                    C.dma(O["p_rsh"][:, c0:c0 + ncol], XS(None, XS.t[0:1, 0:ncol]))
        for j in range(25):
            C.tt("pool", pj3(j), XR(("pp", j), XR.t[:, j, 0:nseq * Wx].rearrange("p (n w) -> p n w", w=Wx)[:, :, 0:L]),
                 XR(("pp", j), XR.t[:, j, 0:nseq * Wx].rearrange("p (n w) -> p n w", w=Wx)[:, :, 1:Wx]), ALU.subtract)
            C.stt("dve", pj3(j), pj3(j), VMU(None, VMU.t[:, j, 0:1]),
                  XR(("pp", j), XR.t[:, j, 0:nseq * Wx].rearrange("p (n w) -> p n w", w=Wx)[:, :, 1:Wx]), ALU.mult, ALU.add)

        VTMf = VTM.t[:, :, :].rearrange("p a b -> p (a b)")
        ROWf = ROW.t[:, :, :].rearrange("p a b -> p (a b)")
        KKt = lambda a, b: KW(None, KW.t[0:T, a:b])
        Wt = lambda a, b: VTM(None, VTMf[0:T, a:b])
        KKAt = lambda a, b: ROW(None, ROWf[0:T, a:b])
        KPt = lambda a, b: XS(None, XS.t[0:T, a:b])
        Rt = lambda a, b: XS(None, XS.t[0:T, 1024 + a:1024 + b])
        fs = lambda i: XI(("FS", i), FSv[:, i, 0:T])
        tw = XI("TW", TWv[0:64, 0:T])
        C.act(tw, PJ(("t", 24), PJ.t[0:64, 24, 0:T]), AF.Tanh)

        def to_tok(dst, src):
            ps = next_psb()
            C.tr(ps(None, ps.t[0:T, 0:128]), src, ident.all())
            C.copy(ev_eng(), dst, ps(None, ps.t[0:T, 0:128]))

        NE05 = -float(np.exp(-0.5))
        for ct in range(8):
            r_, k_, v_ = pj(ct), pj(8 + ct), pj(16 + ct)
            cs_ = slice(ct * 128, (ct + 1) * 128)
            ps = next_psa()
            C.mm(ps(None, ps.t[:, 0:T]), W2A2(None, W2A2.t[0:64, cs_]), tw)
            C.act(fs(0), ps(None, ps.t[:, 0:T]), AF.Sigmoid, bias=VEC1(None, VEC1.t[:, ct, 1:2]))
            C.act(fs(0), fs(0), AF.Exp, scale=NE05)
            to_tok(Wt(ct * 128, (ct + 1) * 128), fs(0))
            ps = next_psa()
            C.mm(ps(None, ps.t[:, 0:T]), W2A2(None, W2A2.t[64:128, cs_]), PJ(("t", 24), PJ.t[64:128, 24, 0:T]))
            C.act(fs(1), ps(None, ps.t[:, 0:T]), AF.Sigmoid, bias=VEC1(None, VEC1.t[:, ct, 2:3]))
            C.ts("dve", fs(2), k_, VEC1(None, VEC1.t[:, ct, 3:4]), None, ALU.mult)
            C.tt("pool", fs(3), fs(2), fs(2), ALU.mult)
            ps = next_psa()
            C.mm(ps(None, ps.t[:, 0:T]), BLK.all(), fs(3))
            C.act(fs(3), ps(None, ps.t[:, 0:T]), AF.Sqrt)
            C.ts("dve", fs(3), fs(3), 1e-12, None, ALU.max)
            recip("dve", fs(3), fs(3))
            C.tt("dve", fs(2), fs(2), fs(3), ALU.mult)
            to_tok(KKt(ct * 128, (ct + 1) * 128), fs(2))
            C.tt("dve", fs(3), fs(2), fs(1), ALU.mult)
            to_tok(KKAt(ct * 128, (ct + 1) * 128), fs(3))
            C.ts("dve", fs(1), fs(1), VEC1(None, VEC1.t[:, ct, 4:5]), OMKA(None, OMKA.t[:, ct:ct + 1]), ALU.mult, ALU.add)
            C.tt("dve", fs(1), fs(1), k_, ALU.mult)
            to_tok(KPt(ct * 128, (ct + 1) * 128), fs(1))
            to_tok(Rt(ct * 128, (ct + 1) * 128), r_)
            C.tt("dve", fs(3), r_, fs(1), ALU.mult)
            C.ts("dve", fs(3), fs(3), VEC1(None, VEC1.t[:, ct, 7:8]), None, ALU.mult)
            ps = next_psa()
            C.mm(ps(None, ps.t[:, 0:T]), BLK.all(), fs(3))
            C.tt("dve", ACC(("t", ct), ACC.t[:, ct, 0:T]), ps(None, ps.t[:, 0:T]), v_, ALU.mult)

        Yv = MIX.t[:, 8 * TT:16 * TT].rearrange("p (j t) -> p j t", j=8)
        srcs = (("kk", KW.t[0:T, :]), ("w", VTMf[0:T, 0:1024]), ("kka", ROWf[0:T, 0:1024]), ("kp", XS.t[0:T, 0:1024]), ("r", XS.t[0:T, 1024:2048]))
        bufs = {"kk": KW, "w": VTM, "kka": ROW, "kp": XS, "r": XS}
        for n in range(nseq):
            if is_s:
                sr = SRS[n % 2]
                C.dma(sr, I["s_rs"][s0 + n].rearrange("(j hp) v k -> (hp v) j k", hp=2))
            else:
                sr = SR.all()
                if tile["first"]:
                    C.memset("pool", sr, 0.0)
            for l in range(L):
                t = n * L + l
                oh = ident(None, ident.t[0:T, t:t + 1].broadcast_to([T, 64]))
                bc = {}
                for nm, ap in srcs:
                    ps = next_ps8()
                    xv = ap.rearrange("p (j hp k) -> p hp j k", hp=2, k=64)
                    C.mm(ps(None, ps.t[0:64, 0:512]), oh, bufs[nm](None, xv[:, 0]))
                    C.mm(ps(None, ps.t[64:128, 0:512]), oh, bufs[nm](None, xv[:, 1]))
                    bc[nm] = ps(None, ps.t[:, 0:512].rearrange("p (j k) -> p j k", k=64))
                C.tt("dve", T1, sr, bc["kk"], ALU.mult)
                C.op("dve", lambda g: g.reduce_sum(out=SUMB.t[:, :], in_=T1.ap, axis=AX.X), reads=[T1], writes=[SUMB.all()])
                C.tt("dve", sr, sr, bc["w"], ALU.mult)
                C.tt("dve", T2, bc["kka"], SUMB(None, SUMB.t[:, :].unsqueeze(2).broadcast_to([128, 8, 64])), ALU.mult)
                C.tt("dve", sr, sr, T2, ALU.subtract)
                C.tt("dve", T1, bc["kp"], PJ(None, PJ.t[:, 16:24, t].unsqueeze(2).broadcast_to([128, 8, 64])), ALU.mult)
                C.tt("dve", sr, sr, T1, ALU.add)
                C.tt("dve", T2, sr, bc["r"], ALU.mult)
                yv = MIX(None, Yv[:, :, t])
                C.op("dve", lambda g, yv=yv: g.reduce_sum(out=yv.ap, in_=T2.ap, axis=AX.X), reads=[T2], writes=[yv])
            if is_s:
                C.dma(O["s_rs_o"][s0 + n].rearrange("(j hp) v k -> (hp v) j k", hp=2), sr)
        if (not is_s) and tile["last"]:
            C.dma(O["p_rs"].rearrange("(j hp) v k -> (hp v) j k", hp=2), SR.all())

        for j in range(8):
            y = mixv(8 + j, T)
            ps = next_psa()
            C.mm(ps(None, ps.t[:, 0:T]), BLK.all(), y)
            C.tt("pool", fs(0), y, y, ALU.mult)
            ps2 = next_psa()
            C.mm(ps2(None, ps2.t[:, 0:T]), BLK.all(), fs(0))
            C.ts("dve", fs(1), ps(None, ps.t[:, 0:T]), 1.0 / 64, None, ALU.mult)
            C.ts("dve", fs(2), ps2(None, ps2.t[:, 0:T]), 1.0 / 64, None, ALU.mult)
            C.tt("dve", fs(3), fs(1), fs(1), ALU.mult)
            C.tt("dve", fs(2), fs(2), fs(3), ALU.subtract)
            C.ts("dve", fs(2), fs(2), 64e-5, None, ALU.add)
            C.act(fs(2), fs(2), AF.Sqrt)
            recip("dve", fs(2), fs(2))
            C.tt("dve", y, y, fs(1), ALU.subtract)
            C.tt("dve", y, y, fs(2), ALU.mult)
            C.act(y, y, AF.Identity, bias=VEC1(None, VEC1.t[:, j, 6:7]), scale=VEC1(None, VEC1.t[:, j, 5:6]))
            C.tt("dve", y, y, ACC(("t", j), ACC.t[:, j, 0:T]), ALU.add)
            C.tt("dve", y, y, pj(25 + j), ALU.mult)

    SSTRI = {"p": C.sb("sstri_p", [128, 128]), "s": C.sb("sstri_s", [64, 64])}
    SSTRIT = {"p": C.sb("sstriT_p", [128, 128]), "s": C.sb("sstriT_s", [64, 64])}
    for k_ in ("p", "s"):
        C.dma(SSTRI[k_].all(), I["sstri_" + k_])
        C.dma(SSTRIT[k_].all(), I["sstriT_" + k_])
    WLB = C.sb("WLB", [128, 8, 8])
    NB16 = C.sb("NB16", [128, 10, 128], BF16)
    HBf = HB.t[:, :, :].rearrange("p a b -> p (a b)")
    KKv = KW.t[:, :].rearrange("p (j t) -> p j t", t=128)
    BTv = ROW.t
    XRf = XR.t[:, :, :].rearrange("p a b -> p (a b)")
    S0Tv = XRf[:, 0:4096].rearrange("p (n j v) -> p n j v", n=8, j=8)

    def fence(buf, ap):
        C.op("pool", lambda g: g.memset(ap, 0.0), writes=[buf.all()])

    def rwkv2(tile):
        T, nseq, L = tile["T"], tile["nseq"], tile["L"]
        is_s = tile["kind"] == "s"
        kd = tile["kind"]
        s0 = tile["s0"]
        Wx = 1 + L
        xrhs = lambda kt: XTB(("t", kt), XTB.t[:, kt, 0:T])
        pj = lambda j: PJ(("t", j), PJ.t[:, j, 0:T])
        pj3 = lambda j: PJ(("t", j), PJ.t[:, j, 0:T].rearrange("p (n l) -> p n l", l=L))

        def ppv(j0, j1, a, b):
            v = XR.t[:, j0:j1, 0:nseq * Wx].rearrange("p j (n w) -> p j n w", w=Wx)[:, :, :, a:b]
            return XR(("pp", j0) if j1 == j0 + 1 else None, v)

        if is_s:
            for (c0, ncol, ct0) in ((0, 2048, 0), (2048, 1152, 16)):
                C.dma(XS(None, XS.t[0:8, 0:ncol]), I["s_rsh"][s0:s0 + 8, c0:c0 + ncol])
                for ct in range(ncol // 128):
                    ps = next_psb()
                    C.tr(ps(None, ps.t[:, 0:8]), XS(None, XS.t[0:8, ct * 128:(ct + 1) * 128]), ident(None, ident.t[0:8, 0:8]))
                    C.copy("dve", ppv(ct0 + ct, ct0 + ct + 1, 0, 1), ps(None, ps.t[:, 0:8].rearrange("p (j n o) -> p j n o", j=1, o=1)))
        else:
            if tile["first"]:
                C.memset("pool", ppv(0, 25, 0, 1), 0.0)
            else:
                C.copy("pool", ppv(0, 25, 0, 1), SHC(None, SHC.t[:, :].unsqueeze(2).unsqueeze(3)))

        def evacR(j, pv):
            if j < 25:
                C.copy(ev_eng(), ppv(j, j + 1, 1, Wx), pv.buf(None, pv.ap.rearrange("p (j n l) -> p j n l", j=1, l=L)))
            else:
                C.act(pj(j), pv, AF.Silu)

        for _ in stream_mm_g(I["od_w_in"], 16, [(5128 + i * 128, 128) for i in range(25)], xrhs, T, evacR):
            yield "g"
        yield "PD_DONE"
        stream_mm(I["od_w_in"], 16, [(5128 + (25 + i) * 128, 128) for i in range(8)], xrhs, T, lambda j, pv: evacR(25 + j, pv))

        if is_s:
            for (c0, ncol, ct0) in ((0, 2048, 0), (2048, 1152, 16)):
                for ct in range(ncol // 128):
                    ps = next_psb()
                    C.copy("pool", TMP(None, TMP.t[:, 0:8]), XR(("pp", ct0 + ct), XR.t[:, ct0 + ct, 0:nseq * Wx].rearrange("p (n w) -> p n w", w=Wx)[:, :, L]))
                    C.tr(ps(None, ps.t[0:8, 0:128]), TMP(None, TMP.t[:, 0:8]), ident.all())
                    C.copy("dve", XS(None, XS.t[0:8, ct * 128:(ct + 1) * 128]), ps(None, ps.t[0:8, 0:128]))
                C.dma(O["s_rsh_o"][s0:s0 + 8, c0:c0 + ncol], XS(None, XS.t[0:8, 0:ncol]))
        else:
            C.copy("pool", SHC(None, SHC.t[:, :].unsqueeze(2).unsqueeze(3)), ppv(0, 25, L, L + 1))
            if tile["last"]:
                for (c0, ncol, ct0) in ((0, 2048, 0), (2048, 1152, 16)):
                    for ct in range(ncol // 128):
                        ps = next_psb()
                        C.tr(ps(None, ps.t[0:1, 0:128]), SHC(None, SHC.t[:, ct0 + ct:ct0 + ct + 1]), ident.all())
                        C.copy("dve", XS(None, XS.t[0:1, ct * 128:(ct + 1) * 128]), ps(None, ps.t[0:1, 0:128]))
                    C.dma(O["p_rsh"][:, c0:c0 + ncol], XS(None, XS.t[0:1, 0:ncol]))
        for j in range(25):
            C.tt("pool", pj3(j), XR(("pp", j), XR.t[:, j, 0:nseq * Wx].rearrange("p (n w) -> p n w", w=Wx)[:, :, 0:L]),
                 XR(("pp", j), XR.t[:, j, 0:nseq * Wx].rearrange("p (n w) -> p n w", w=Wx)[:, :, 1:Wx]), ALU.subtract)
            C.stt("dve", pj3(j), pj3(j), VMU(None, VMU.t[:, j, 0:1]),
                  XR(("pp", j), XR.t[:, j, 0:nseq * Wx].rearrange("p (n w) -> p n w", w=Wx)[:, :, 1:Wx]), ALU.mult, ALU.add)

        s0t = lambda n, j, rs=slice(0, 128): XR(("st", n), S0Tv[rs, n, j, :])
        if is_s:
            fence(XR, XR.t[0:1, 0, 0:1])
            for n2 in range(0, 8, 2):
                stg = XS.t[:, :].rearrange("p (n j d k) -> p n j d k", n=2, j=8, d=2)
                for nn in range(2):
                    for d in range(2):
                        C.dma(XS(None, stg[:, nn, :, d, :]), I["s_rs"][s0 + n2 + nn].rearrange("(j hp) v k -> (hp v) j k", hp=2))
                for nn in range(2):
                    n = n2 + nn
                    for j in range(8):
                        ps = next_psb()
                        C.tr(ps(None, ps.t[:, 0:128]), XS(None, stg[:, nn, j, :, :]), ident.all())
                        C.copy("dve", s0t(n, j, slice(0, 64)), ps(None, ps.t[0:64, 0:64]))
                        C.copy("act", s0t(n, j, slice(64, 128)), ps(None, ps.t[64:128, 64:128]))
        else:
            if tile["first"]:
                C.memset("pool", SR.all(), 0.0)

        VTMf = VTM.t[:, :, :].rearrange("p a b -> p (a b)")
        Vtm = lambda hc: XS(None, XS.t[0:T, hc])
        Btm = lambda hc: XS(None, XS.t[0:T, 1024 + hc.start:1024 + hc.stop])
        Ktm = lambda hc: VTM(None, VTMf[0:T, hc])
        fs = lambda i: XI(("FS", i), FSv[:, i, 0:T])
        tw = XI("TW", TWv[0:64, 0:T])
        C.act(tw, PJ(("t", 24), PJ.t[0:64, 24, 0:T]), AF.Tanh)

        def to_tok(dst, src):
            ps = next_psb()
            C.tr(ps(None, ps.t[0:T, 0:128]), src, ident.all())
            C.copy(ev_eng(), dst, ps(None, ps.t[0:T, 0:128]))

        NE05 = -float(np.exp(-0.5))
        kkc = lambda ct, rs=slice(0, 128): KW(("c", ct), KKv[rs, ct, 0:T])
        btc = lambda ct, rs=slice(0, 128): ROW(("c", ct), BTv[rs, ct, 0:T])
        for ct in range(8):
            r_, k_, v_ = pj(ct), pj(8 + ct), pj(16 + ct)
            cs_ = slice(ct * 128, (ct + 1) * 128)
            ps = next_psa()
            C.mm(ps(None, ps.t[:, 0:T]), W2A2(None, W2A2.t[0:64, cs_]), tw)
            C.act(fs(0), ps(None, ps.t[:, 0:T]), AF.Sigmoid, bias=VEC1(None, VEC1.t[:, ct, 1:2]))
            C.ts("dve", fs(0), fs(0), NE05, None, ALU.mult)
            scan(fs(1), R01[kd](None, R01[kd].t[:, 0:T]), fs(0), 0.0, ALU.mult, ALU.add)
            ps = next_psa()
            C.mm(ps(None, ps.t[:, 0:T]), W2A2(None, W2A2.t[64:128, cs_]), PJ(("t", 24), PJ.t[64:128, 24, 0:T]))
            C.act(fs(2), ps(None, ps.t[:, 0:T]), AF.Sigmoid, bias=VEC1(None, VEC1.t[:, ct, 2:3]))
            C.ts("dve", kkc(ct), k_, VEC1(None, VEC1.t[:, ct, 3:4]), None, ALU.mult)
            C.tt("pool", fs(3), kkc(ct), kkc(ct), ALU.mult)
            ps = next_psa()
            C.mm(ps(None, ps.t[:, 0:T]), BLK.all(), fs(3))
            C.act(fs(3), ps(None, ps.t[:, 0:T]), AF.Sqrt)
            C.ts("dve", fs(3), fs(3), 1e-12, None, ALU.max)
            recip("dve", fs(3), fs(3))
            C.tt("dve", kkc(ct), kkc(ct), fs(3), ALU.mult)
            C.tt("dve", btc(ct), kkc(ct), fs(2), ALU.mult)
            C.ts("dve", fs(2), fs(2), VEC1(None, VEC1.t[:, ct, 4:5]), OMKA(None, OMKA.t[:, ct:ct + 1]), ALU.mult, ALU.add)
            C.tt("dve", fs(2), fs(2), k_, ALU.mult)
            C.tt("pool", fs(3), r_, fs(2), ALU.mult)
            C.ts("dve", fs(3), fs(3), VEC1(None, VEC1.t[:, ct, 7:8]), None, ALU.mult)
            ps = next_psa()
            C.mm(ps(None, ps.t[:, 0:T]), BLK.all(), fs(3))
            C.tt("dve", ACC(("t", ct), ACC.t[:, ct, 0:T]), ps(None, ps.t[:, 0:T]), v_, ALU.mult)
            C.act(fs(3), fs(1), AF.Exp)
            C.tt("dve", r_, r_, fs(3), ALU.mult)
            C.copy("pool", WLB(None, WLB.t[:, ct, 0:nseq]),
                   XI(("FS", 3), FSv[:, 3, 0:T].rearrange("p (n l) -> p n l", l=L)[:, :, L - 1]))
            C.tt("dve", fs(3), fs(1), fs(0), ALU.subtract)
            C.act(fs(3), fs(3), AF.Exp)
            C.tt("dve", kkc(ct), kkc(ct), fs(3), ALU.mult)
            C.act(fs(3), fs(1), AF.Exp, scale=-1.0)
            C.tt("dve", btc(ct), btc(ct), fs(3), ALU.mult)
            C.tt("dve", k_, fs(2), fs(3), ALU.mult)
            to_tok(Vtm(cs_), v_)
            to_tok(Btm(cs_), btc(ct))
            to_tok(Ktm(cs_), k_)

        fence(HB, HB.t[0:1, 0, 0:1])
        mat = lambda i: HB(("m", i), HBf[0:T, i * 128:i * 128 + T])
        half = lambda i, a: HB(("m", i), HBf[0:T, i * 128 + 64 * a:i * 128 + 64 * a + 64])
        nsq = max(0, int(np.ceil(np.log2(L))) - 1)
        idT = ident(None, ident.t[0:T, 0:T])
        mS = SSTRI[kd](None, SSTRI[kd].t[0:T, 0:T])
        mST = SSTRIT[kd](None, SSTRIT[kd].t[0:T, 0:T])
        mI = SEGTRI[kd](None, SEGTRI[kd].t[0:T, 0:T])
        if is_s:
            KKM, RM = T1, T2
        for j in range(8):
            nb = lambda i: NB16(("n", i), NB16.t[0:T, i, 0:T])
            Q = [[nb(5 * hp + 0), nb(5 * hp + 1)] for hp in range(2)]
            QT = [[nb(5 * hp + 2), nb(5 * hp + 3)] for hp in range(2)]
            P16 = [nb(5 * hp + 4) for hp in range(2)]
            Pm = [mat(8 * hp + 4) for hp in range(2)]
            BR = [mat(8 * hp + 5) for hp in range(2)]
            AK = [mat(8 * hp + 6) for hp in range(2)]
            KR = [mat(8 * hp + 7) for hp in range(2)]
            RHS = [half(16, hp) for hp in range(2)]
            SAT = [half(17, hp) for hp in range(2)]
            rsl = [slice(0, 64), slice(64, 128)]
            if is_s:
                C.tt("pool", KKM, KW(("c", j), KKv[:, j, 0:T].unsqueeze(1).broadcast_to([128, 8, T])), SEGROW(None, SEGROW.t[:, :, 0:T]), ALU.mult)
                C.tt("pool", RM, PJ(("t", j), PJ.t[:, j, 0:T].unsqueeze(1).broadcast_to([128, 8, T])), SEGROW(None, SEGROW.t[:, :, 0:T]), ALU.mult)
            for hp in range(2):
                rs = rsl[hp]
                rq = PJ(("t", j), PJ.t[rs, j, 0:T])
                kq = PJ(("t", 8 + j), PJ.t[rs, 8 + j, 0:T])
                ps = next_ps8()
                C.mm(ps(None, ps.t[0:T, 0:T]), btc(j, rs), kkc(j, rs))
                C.mm(ps(None, ps.t[0:T, T:2 * T]), btc(j, rs), rq)
                C.tt("dve", Q[hp][0], ps(None, ps.t[0:T, 0:T]), mS, ALU.mult)
                C.tt("dve", BR[hp], ps(None, ps.t[0:T, T:2 * T]), mI, ALU.mult)
                ps = next_ps8()
                C.mm(ps(None, ps.t[0:T, 0:T]), kq, kkc(j, rs))
                C.mm(ps(None, ps.t[0:T, T:2 * T]), kq, rq)
                C.tt("dve", AK[hp], ps(None, ps.t[0:T, 0:T]), mS, ALU.mult)
                C.tt("dve", KR[hp], ps(None, ps.t[0:T, T:2 * T]), mI, ALU.mult)
                ps = next_ps8()
                C.mm(ps(None, ps.t[0:T, 0:T]), kkc(j, rs), btc(j, rs))
                C.tt("dve", QT[hp][0], ps(None, ps.t[0:T, 0:T]), mST, ALU.mult)
                C.stt("dve", Pm[hp], Q[hp][0], -1.0, idT, ALU.mult, ALU.add)
                C.copy("act", P16[hp], Pm[hp])
            cur = 0
            for it in range(nsq):
                nxt = 1 - cur
                last_it = (it == nsq - 1)
                for hp in range(2):
                    if not last_it:
                        ps = next_ps8()
                        C.mm(ps(None, ps.t[0:T, 0:T]), QT[hp][cur], Q[hp][cur])
                        C.copy("act", Q[hp][nxt], ps(None, ps.t[0:T, 0:T]))
                    ps = next_ps8()
                    C.mm(ps(None, ps.t[0:T, 0:T]), Q[hp][cur], QT[hp][cur])
                    C.copy("dve", QT[hp][nxt], ps(None, ps.t[0:T, 0:T]))
                for hp in range(2):
                    ps = next_ps8()
                    C.mm(ps(None, ps.t[0:T, 0:T]), QT[hp][nxt], P16[hp])
                    C.tt("dve", Pm[hp], Pm[hp], ps(None, ps.t[0:T, 0:T]), ALU.add)
                    if not last_it:
                        C.copy("act", P16[hp], Pm[hp])
                cur = nxt
            for hp in range(2):
                rs = rsl[hp]
                hc = slice((2 * j + hp) * 64, (2 * j + hp) * 64 + 64)
                ps = next_ps8()
                if is_s:
                    for n in range(nseq):
                        C.mm(ps(None, ps.t[0:T, 0:64]), XI("T1", KKM.ap[rs, n, :]), s0t(n, j, rs), start=(n == 0), stop=False)
                else:
                    C.mm(ps(None, ps.t[0:T, 0:64]), kkc(j, rs), SR(None, SR.t[rs, j, :]), start=True, stop=False)
                C.mm(ps(None, ps.t[0:T, 0:64]), AK[hp], Vtm(hc), start=False, stop=True)
                C.act(RHS[hp], ps(None, ps.t[0:T, 0:64]), AF.Identity, scale=-1.0)
                ps = next_ps8()
                C.mm(ps(None, ps.t[0:T, 0:64]), Pm[hp], RHS[hp])
                C.copy("dve", SAT[hp], ps(None, ps.t[0:T, 0:64]))
            psY = next_ps8()
            for hp in range(2):
                rs = rsl[hp]
                hc = slice((2 * j + hp) * 64, (2 * j + hp) * 64 + 64)
                ov = psY(None, psY.t[rs, 0:T])
                if is_s:
                    for n in range(nseq):
                        C.mm(ov, s0t(n, j, rs), XI("T2", RM.ap[rs, n, :]), start=(n == 0), stop=False)
                else:
                    C.mm(ov, SR(None, SR.t[rs, j, :]), PJ(("t", j), PJ.t[rs, j, 0:T]), start=True, stop=False)
                C.mm(ov, SAT[hp], BR[hp], start=False, stop=False)
                C.mm(ov, Vtm(hc), KR[hp], start=False, stop=True)
            y = TMP(None, TMP.t[:, 0:T])
            C.copy("act", y, psY(None, psY.t[:, 0:T]))
            ps = next_psa()
            C.mm(ps(None, ps.t[:, 0:T]), BLK.all(), y)
            C.tt("pool", fs(0), y, y, ALU.mult)
            ps2 = next_psa()
            C.mm(ps2(None, ps2.t[:, 0:T]), BLK.all(), fs(0))
            C.ts("dve", fs(1), ps(None, ps.t[:, 0:T]), 1.0 / 64, None, ALU.mult)
            C.ts("dve", fs(2), ps2(None, ps2.t[:, 0:T]), 1.0 / 64, None, ALU.mult)
            C.tt("dve", fs(3), fs(1), fs(1), ALU.mult)
            C.tt("dve", fs(2), fs(2), fs(3), ALU.subtract)
            C.ts("dve", fs(2), fs(2), 64e-5, None, ALU.add)
            C.act(fs(2), fs(2), AF.Sqrt)
            recip("dve", fs(2), fs(2))
            C.tt("dve", y, y, fs(1), ALU.subtract)
            C.tt("dve", y, y, fs(2), ALU.mult)
            C.act(y, y, AF.Identity, bias=VEC1(None, VEC1.t[:, j, 6:7]), scale=VEC1(None, VEC1.t[:, j, 5:6]))
            C.tt("dve", y, y, ACC(("t", j), ACC.t[:, j, 0:T]), ALU.add)
            C.tt("dve", mixv(8 + j, T), y, pj(25 + j), ALU.mult)
            tmpS = XI(("FS", 0), FSv[:, 0, 0:64])
            if is_s:
                SAM = [CSS[hp](None, CSS[hp].t[0:T, :, :].rearrange("p a b -> p (a b)")[:, 0:512].rearrange("p (n v) -> p n v", n=8)) for hp in range(2)]
                VM = [CSO[hp](None, CSO[hp].t[0:T, :, :].rearrange("p a b -> p (a b)")[:, 0:512].rearrange("p (n v) -> p n v", n=8)) for hp in range(2)]
                segc_b = SEGC(None, SEGC.t[0:T, :].unsqueeze(2).broadcast_to([T, 8, 64]))
                for hp in range(2):
                    hc = slice((2 * j + hp) * 64, (2 * j + hp) * 64 + 64)
                    C.tt("pool", SAM[hp], HB(("m", 17), HBf[0:T, 17 * 128 + 64 * hp:17 * 128 + 64 * hp + 64].unsqueeze(1).broadcast_to([T, 8, 64])), segc_b, ALU.mult)
                    C.tt("pool", VM[hp], XS(None, XS.t[0:T, hc].unsqueeze(1).broadcast_to([T, 8, 64])), segc_b, ALU.mult)
                for n in range(nseq):
                    psS = next_ps8()
                    for hp in range(2):
                        rs = rsl[hp]
                        hc = slice((2 * j + hp) * 64, (2 * j + hp) * 64 + 64)
                        C.mm(psS(None, psS.t[rs, 0:64]), Btm(hc), CSS[hp](None, SAM[hp].ap[:, n, :]), start=True, stop=False)
                        C.mm(psS(None, psS.t[rs, 0:64]), Ktm(hc), CSO[hp](None, VM[hp].ap[:, n, :]), start=False, stop=True)
                    wl = WLB(None, WLB.t[:, j, n:n + 1])
                    C.ts("dve", tmpS, s0t(n, j), wl, None, ALU.mult)
                    C.stt("dve", s0t(n, j), psS(None, psS.t[:, 0:64]), wl, tmpS, ALU.mult, ALU.add)
            else:
                psS = next_ps8()
                for hp in range(2):
                    rs = rsl[hp]
                    hc = slice((2 * j + hp) * 64, (2 * j + hp) * 64 + 64)
                    C.mm(psS(None, psS.t[rs, 0:64]), Btm(hc), SAT[hp], start=True, stop=False)
                    C.mm(psS(None, psS.t[rs, 0:64]), Ktm(hc), Vtm(hc), start=False, stop=True)
                wl = WLB(None, WLB.t[:, j, 0:1])
                srj = SR(None, SR.t[:, j, :])
                C.ts("dve", tmpS, srj, wl, None, ALU.mult)
                C.stt("dve", srj, psS(None, psS.t[:, 0:64]), wl, tmpS, ALU.mult, ALU.add)

        def state_out(src_fn, dst):
            for j in range(8):
                ps = next_psb()
                C.tr(ps(None, ps.t[0:64, 0:128]), src_fn(j), ident.all())
                C.copy(ev_eng(), XS(None, XS.t[0:64, j * 128:(j + 1) * 128]), ps(None, ps.t[0:64, 0:128]))
            C.dma(dst.rearrange("(j hp) v k -> v j hp k", hp=2), XS(None, XS.t[0:64, 0:1024].rearrange("p (j hp k) -> p j hp k", j=8, hp=2)))

        if is_s:
            for n in range(nseq):
                state_out(lambda j, n=n: s0t(n, j), O["s_rs_o"][s0 + n])
        elif tile["last"]:
            state_out(lambda j: SR(None, SR.t[:, j, :]), O["p_rs"])


    def layer1(tile):
        T, nseq, L = tile["T"], tile["nseq"], tile["L"]
        is_s = tile["kind"] == "s"
        kd = tile["kind"]
        s0 = tile["s0"]
        xrhs = lambda kt: XTB(("t", kt), XTB.t[:, kt, 0:T])
        pj = lambda j: PJ(("t", j), PJ.t[:, j, 0:T])
        W = I["od_w_in"]

        def evacM(j, pv):
            if j < 8:
                C.act(pj(j), pv, AF.Identity, scale=1.0 / 16.0)
            elif j < 16:
                C.copy(ev_eng(), pj(j), pv)
            elif j < 24:
                C.act(pj(j), pv, AF.Sigmoid)
            elif j == 24:
                C.copy("dve", PJ(("t", 32), PJ.t[0:8, 32, 0:T]), pv)
            else:
                C.act(pj(j - 1), pv, AF.Silu)

        cols = [(i * 128, 128) for i in range(16)] + [(3072 + i * 128, 128) for i in range(8)] + [(4096, 8)] + \
               [(4104 + i * 128, 128) for i in range(8)]
        stream_mm(W, 16, cols, xrhs, T, evacM)

        def evacV(g, pv):
            C.copy(ev_eng(), VTM(None, VTM.t[0:T, 2 * g:2 * g + 2, 0:256]), pv.buf(None, pv.ap.rearrange("p (h v) -> p h v", h=2)))

        C.memset("pool", VTM(None, VTM.t[:, :, 256:257]), 1.0)
        stream_mm_tok(W, 2048, 1024, T, evacV)

        gx = GX(None, GX.t[0:8, 0:T])
        C.ts("dve", gx, PJ(("t", 32), PJ.t[0:8, 32, 0:T]), GB.all(), None, ALU.add)
        col = lambda a, b: COL(None, COL.t[0:T, a:b])
        ps = next_psb()
        C.tr(ps(None, ps.t[0:T, 0:8]), gx, ident(None, ident.t[0:8, 0:8]))
        C.copy("dve", col(0, 8), ps(None, ps.t[0:T, 0:8]))
        C.act(col(8, 12), col(4, 8), AF.Exp, scale=-1.0)
        C.act(col(8, 12), col(8, 12), AF.Ln, bias=1.0)
        C.ts("dve", col(8, 12), col(8, 12), -1.0, None, ALU.mult)
        ps = next_psb()
        C.mm(ps(None, ps.t[0:T, 0:4]), SEGTRI[kd](None, SEGTRI[kd].t[0:T, 0:T]), col(8, 12))
        C.copy("dve", col(12, 16), ps(None, ps.t[0:T, 0:4]))
        C.tt("dve", col(16, 20), col(0, 4), col(12, 16), ALU.subtract)
        if is_s:
            C.dma(MS.all(), I["s_mm"][s0:s0 + 8, :])
            ps = next_psb()
            C.mm(ps(None, ps.t[0:T, 0:4]), SEGSEL(None, SEGSEL.t[0:8, 0:T]), MS.all())
            C.copy("dve", col(20, 24), ps(None, ps.t[0:T, 0:4]))
            for h in range(4):
                C.copy("dve", MSB.all(), MS(None, MS.t[:, h:h + 1].broadcast_to([8, 128])))
                ps = next_psb()
                C.mm(ps(None, ps.t[:, 0:8]), MSB.all(), ident(None, ident.t[0:8, 0:8]))
                C.copy("dve", MINIT(None, MINIT.t[:, h, :]), ps(None, ps.t[:, 0:8]))
        else:
            if tile["first"]:
                C.memset("dve", MCAR.all(), 0.0)
                C.memset("pool", CS.all(), 0.0)
            C.copy("dve", col(20, 24), MCAR(None, MCAR.t[0:T, :]))
            C.copy("dve", MINIT(None, MINIT.t[:, :, 0:1]), MCAR(None, MCAR.t[:, :].unsqueeze(2)))

        row = lambda k: ROW(None, ROW.t[:, k, 0:T])
        ends = lambda k: ROW(None, ROW.t[:, k, 0:T].rearrange("p (n l) -> p n l", l=L)[:, :, L - 1])
        starts = lambda k: ROW(None, ROW.t[:, k, 0:T].rearrange("p (n l) -> p n l", l=L)[:, :, 0])
        rw = rwkv2(tile) if do_rwkv else iter(())
        rw_pd = {"done": not do_rwkv}

        def rw_advance(n):
            for _ in range(n):
                if rw_pd["done"]:
                    return
                if next(rw, "PD_DONE") == "PD_DONE":
                    rw_pd["done"] = True

        rw_advance(1)
        for h in range(4):
            minit_r = MINIT(None, MINIT.t[:, h, 0:nseq])
            ps = next_psb()
            C.mm(ps(None, ps.t[:, 0:T]), ident(None, ident.t[0:8, h:h + 1].broadcast_to([8, 128])), gx)
            C.copy("dve", row(0), ps(None, ps.t[:, 0:T]))
            ps = next_psb()
            C.mm(ps(None, ps.t[:, 0:T]), ident(None, ident.t[0:8, 4 + h:5 + h].broadcast_to([8, 128])), gx)
            C.act(row(1), ps(None, ps.t[:, 0:T]), AF.Exp, scale=-1.0)
            C.act(row(1), row(1), AF.Ln, bias=1.0)
            C.ts("dve", row(1), row(1), -1.0, None, ALU.mult)
            scan(row(2), R01[kd](None, R01[kd].t[:, 0:T]), row(1), 0.0, ALU.mult, ALU.add)
            C.tt("dve", row(3), row(0), row(2), ALU.subtract)
            C.tt("dve", starts(3), starts(3), minit_r, ALU.max)
            scan(row(4), RNEG[kd](None, RNEG[kd].t[:, 0:T]), row(3), -1e30, ALU.add, ALU.max)
            C.copy("dve", ROW(None, ROW.t[:, 5, 0:T].rearrange("p (n l) -> p n l", l=L)),
                   ROW(None, ROW.t[:, 4, 0:T].rearrange("p (n l) -> p n l", l=L)[:, :, L - 1:L].broadcast_to([128, nseq, L])))
            C.tt("dve", MNEW(None, MNEW.t[:, h, 0:nseq]), ends(2), ends(4), ALU.add)
            C.tt("dve", DEC(None, DEC.t[:, h, 0:nseq]), minit_r, ends(4), ALU.subtract)
            C.act(DEC(None, DEC.t[:, h, 0:nseq]), DEC(None, DEC.t[:, h, 0:nseq]), AF.Exp)
            ps = next_psb()
            C.mm(ps(None, ps.t[0:T, 0:1]), row(4), ident(None, ident.t[:, 0:1]))
            C.mm(ps(None, ps.t[0:T, 1:2]), row(5), ident(None, ident.t[:, 0:1]))
            C.copy("dve", col(24, 26), ps(None, ps.t[0:T, 0:2]))
            C.tt("dve", DTB(None, DTB.t[0:T, 0:T]), ROW(None, ROW.t[0:T, 4, 0:T]), MASKBIG[kd](None, MASKBIG[kd].t[0:T, 0:T]), ALU.add)
            C.act(DTB(None, DTB.t[0:T, 0:T]), DTB(None, DTB.t[0:T, 0:T]), AF.Exp, scale=-1.0, bias=col(16 + h, 17 + h))
            ps = next_psa()
            for kt in range(2):
                C.mm(ps(None, ps.t[0:T, 0:T]), pj(8 + 2 * h + kt), pj(2 * h + kt), start=(kt == 0), stop=(kt == 1))
            C.tt("dve", STB(None, STB.t[0:T, 0:T]), ps(None, ps.t[0:T, 0:T]), DTB(None, DTB.t[0:T, 0:T]), ALU.mult)
            ps1 = next_psa()
            C.mm(ps1(None, ps1.t[0:T, 0:257]), STB(None, STB.t[0:T, 0:T]), VTM(None, VTM.t[0:T, h, :]))
            C.copy("act", P1S(None, P1S.t[0:T, :]), ps1(None, ps1.t[0:T, 0:257]))
            ps2 = next_psa()
            if is_s:
                i_ = 0
                for n in range(nseq):
                    cs = CSS[n % 2]
                    C.dma(cs(None, cs.t[:, :, 0:256]), I["s_mc"][s0 + n, h].rearrange("(kt p) v -> p kt v", p=128))
                    C.dma(cs(None, cs.t[:, :, 256:257]), I["s_mn"][s0 + n, h].rearrange("(kt p o) -> p kt o", p=128, o=1), slow=True)
                    for kt in range(2):
                        qm = QM[i_ % 2]
                        i_ += 1
                        C.tt("pool", qm(None, qm.t[:, 0:T]), pj(2 * h + kt), SEGROW(None, SEGROW.t[:, n, 0:T]), ALU.mult)
                        C.mm(ps2(None, ps2.t[0:T, 0:257]), qm(None, qm.t[:, 0:T]), cs(None, cs.t[:, kt, :]),
                             start=(n == 0 and kt == 0), stop=(n == nseq - 1 and kt == 1))
            else:
                for kt in range(2):
                    C.mm(ps2(None, ps2.t[0:T, 0:257]), pj(2 * h + kt), CS(("h", h), CS.t[:, h, kt, :]), start=(kt == 0), stop=(kt == 1))
            sm = lambda a: SM(None, SM.t[0:T, a:a + 1])
            C.tt("dve", sm(0), col(20 + h, 21 + h), col(24, 25), ALU.subtract)
            C.act(sm(0), sm(0), AF.Exp)
            C.stt("dve", NUM(None, NUM.t[0:T, :]), ps2(None, ps2.t[0:T, 0:257]), sm(0), P1S(None, P1S.t[0:T, :]), ALU.mult, ALU.add)
            C.tt("dve", sm(1), col(12 + h, 13 + h), col(24, 25), ALU.add)
            C.act(sm(1), sm(1), AF.Exp, scale=-1.0)
            C.ts("dve", sm(2), NUM(None, NUM.t[0:T, 256:257]), -1.0, None, ALU.mult)
            C.tt("dve", sm(2), sm(2), NUM(None, NUM.t[0:T, 256:257]), ALU.max)
            C.tt("dve", sm(2), sm(2), sm(1), ALU.max)
            recip("dve", sm(2), sm(2))
            C.op("dve", lambda g, T=T: g.reduce_sum(out=SM.t[0:T, 3:4], in_=NUM.t[0:T, 0:256], axis=AX.X),
                 reads=[NUM(None, NUM.t[0:T, 0:256])], writes=[sm(3)])
            C.tt("dve", sm(3), sm(3), sm(2), ALU.mult)
            C.ts("dve", sm(3), sm(3), 1.0 / 256, None, ALU.mult)
            hn = HN(None, HN.t[0:T, :])
            C.ts("dve", hn, NUM(None, NUM.t[0:T, 0:256]), sm(2), sm(3), ALU.mult, ALU.subtract)
            C.tt("dve", P1S(None, P1S.t[0:T, 0:256]), hn, hn, ALU.mult)
            C.op("dve", lambda g, T=T: g.reduce_sum(out=SM.t[0:T, 4:5], in_=P1S.t[0:T, 0:256], axis=AX.X),
                 reads=[P1S(None, P1S.t[0:T, 0:256])], writes=[sm(4)])
            C.ts("dve", sm(4), sm(4), 1.0 / 256, LN_EPS, ALU.mult, ALU.add)
            C.act(sm(4), sm(4), AF.Sqrt)
            recip("dve", sm(4), sm(4))
            C.ts("dve", hn, hn, sm(4), None, ALU.mult)
            for kt in range(2):
                ct = 2 * h + kt
                ps = next_psb()
                C.tr(ps(None, ps.t[:, 0:T]), HN(None, HN.t[0:T, kt * 128:(kt + 1) * 128]), ident(None, ident.t[0:T, 0:T]))
                scr = P1S(None, P1S.t[:, 0:T])
                C.stt("dve", scr, ps(None, ps.t[:, 0:T]), VEC1(None, VEC1.t[:, ct, 0:1]), pj(16 + ct), ALU.mult, ALU.mult)
                C.tt("dve", mixv(ct, T), scr, pj(24 + ct), ALU.mult)
            C.tt("dve", sm(5), col(16 + h, 17 + h), col(25, 26), ALU.subtract)
            C.act(sm(5), sm(5), AF.Exp)
            for kt in range(2):
                ps = next_psb()
                C.tr(ps(None, ps.t[0:T, 0:128]), pj(8 + 2 * h + kt), ident.all())
                C.ts("dve", KW(None, KW.t[0:T, h * 256 + kt * 128:h * 256 + (kt + 1) * 128]), ps(None, ps.t[0:T, 0:128]), sm(5), None, ALU.mult)
            if is_s:
                for n in range(nseq):
                    cs = CSS[n % 2]
                    co = CSO[n % 2]
                    C.dma(cs(None, cs.t[:, :, 0:256]), I["s_mc"][s0 + n, h].rearrange("(kt p) v -> p kt v", p=128))
                    C.dma(cs(None, cs.t[:, :, 256:257]), I["s_mn"][s0 + n, h].rearrange("(kt p o) -> p kt o", p=128, o=1), slow=True)
                    C.ts("pool", KWN(None, KWN.t[0:T, :]), KW(None, KW.t[0:T, h * 256:(h + 1) * 256]), SEGC(None, SEGC.t[0:T, n:n + 1]), None, ALU.mult)
                    for kt in range(2):
                        ps = next_psa()
                        C.mm(ps(None, ps.t[:, 0:257]), KWN(None, KWN.t[0:T, kt * 128:(kt + 1) * 128]), VTM(None, VTM.t[0:T, h, :]))
                        C.stt("dve", co(None, co.t[:, kt, :]), cs(None, cs.t[:, kt, :]), DEC(None, DEC.t[:, h, n:n + 1]), ps(None, ps.t[:, 0:257]), ALU.mult, ALU.add)
                    C.dma(O["s_mc_o"][s0 + n, h].rearrange("(kt p) v -> p kt v", p=128), co(None, co.t[:, :, 0:256]))
                    C.dma(O["s_mn_o"][s0 + n, h].rearrange("(kt p o) -> p kt o", p=128, o=1), co(None, co.t[:, :, 256:257]), slow=True)
                C.dma(O["s_mm_o"][s0:s0 + 8, h:h + 1].rearrange("n o -> o n"), MNEW(None, MNEW.t[0:1, h, 0:8]), slow=True)
            else:
                for kt in range(2):
                    ps = next_psa()
                    C.mm(ps(None, ps.t[:, 0:257]), KW(None, KW.t[0:T, h * 256 + kt * 128:h * 256 + (kt + 1) * 128]), VTM(None, VTM.t[0:T, h, :]))
                    csv = CS(("h", h), CS.t[:, h, kt, :])
                    C.stt("dve", csv, csv, DEC(None, DEC.t[:, h, 0:1]), ps(None, ps.t[:, 0:257]), ALU.mult, ALU.add)
                C.copy("dve", MCAR(None, MCAR.t[:, h:h + 1]), MNEW(None, MNEW.t[:, h, 0:1]))
            rw_advance(2)
        if (not is_s) and tile["last"]:
            for h in range(4):
                C.dma(O["p_mc"][h].rearrange("(kt p) v -> p kt v", p=128), CS(("h", h), CS.t[:, h, :, 0:256]))
                C.dma(O["p_mn"][h].rearrange("(kt p o) -> p kt o", p=128, o=1), CS(("h", h), CS.t[:, h, :, 256:257]), slow=True)
            C.dma(O["p_mm"][:, :], MCAR(None, MCAR.t[0:1, :]))

        if do_rwkv:
            rw_advance(100)
            for _ in rw:
                pass
        else:
            for ct in range(8, 16):
                C.memset("pool", mixv(ct, T), 0.0)
        out_proj_ln(I["od_w_out"], tile, VECO, 0, 1)

    for ti, tile in enumerate(tile_plan(cfg)):
        wst["id"] = 0
        wst["first"] = (ti == 0)
        load_x(tile)
        if nlayers >= 1:
            layer0(tile)
        if nlayers >= 2:
            layer1(tile)
        store_y(tile)

    C.emit()
    es.close()
    return nc, C


def make_in_maps(inp, cores, consts):
    maps = []
    f = lambda a: np.ascontiguousarray(a, dtype=np.float32)
    for c in cores:
        s = c % 4
        m = {}
        m["xp"] = f(inp["x_prompt"][s])
        m["meta"] = f(inp["meta_tokens"])
        m["xs"] = f(inp["x_sample"][16 * c:16 * c + 16].reshape(128, D))
        m["s_conv"] = f(inp["state_conv"][0, 16 * c:16 * c + 16].reshape(480, 1024))
        m["s_sre"] = f(inp["state_ssm_re"][0, 16 * c:16 * c + 16])
        m["s_sim"] = f(inp["state_ssm_im"][0, 16 * c:16 * c + 16])
        m.update(consts)
        sl = slice(16 * c, 16 * c + 16)
        m["s_mc"] = f(inp["state_mlstm_c"][0, sl])
        m["s_mn"] = f(inp["state_mlstm_n"][0, sl])
        m["s_mm"] = f(inp["state_mlstm_m"][0, sl])
        m["s_rs"] = f(inp["state_rwkv_s"][0, sl])
        m["s_rsh"] = f(inp["state_rwkv_shift"][0, sl])
        m["od_w_in"] = f(inp["od_w_in"][0])
        for nm in ("m_ig_b", "m_fg_b", "m_hn_g", "r_w0", "r_a0", "r_kk", "r_ka", "r_ln_g", "r_ln_b", "r_rk", "r_mu", "od_ln_g", "od_ln_b"):
            m[nm] = f(inp[nm][0].reshape(1, -1))
        for nm in ("r_w2", "r_a2", "od_w_out"):
            m[nm] = f(inp[nm][0])
        m["ev_w_in"] = f(inp["ev_w_in"][0])
        m["a_conv_w"] = f(inp["a_conv_w"][0])
        for nm in ("a_conv_b", "a_ln_g", "a_ln_b", "s5_d", "s5_log_dt", "s5_glu_b", "ev_ln_g", "ev_ln_b"):
            m[nm] = f(inp[nm][0].reshape(1, -1))
        m["a_pw"] = f(inp["a_pw"][0])
        for nm in ("s5_lambda_re", "s5_lambda_im", "s5_b_re", "s5_b_im", "s5_glu_w", "ev_w_out"):
            m[nm] = f(inp[nm][0])
        m["s5_c_re"] = f(inp["s5_c_re"][0].reshape(1024, 64))
        m["s5_c_im"] = f(inp["s5_c_im"][0].reshape(1024, 64))
        maps.append(m)
    return maps


def kernel(**inp):
    cfg = {}
    nc, C = build(cfg)
    consts = make_consts()
    cores = list(range(NCORES))
    maps = make_in_maps(inp, cores, consts)
    res = run_bass_kernel_spmd(nc, maps, core_ids=cores)
    R = res.results
    B = 4
    cat = lambda k, shp: np.concatenate([R[c][k].reshape((16,) + shp) for c in range(NCORES)], 0)[None]
    stk = lambda k, shp: np.stack([R[c][k].reshape(shp) for c in range(B)], 0)[None]
    y_p = np.stack([R[c]["y_p"] for c in range(B)], 0)
    y_s = np.concatenate([R[c]["y_s"].reshape(16, 8, D) for c in range(NCORES)], 0)
    return (y_p, y_s,
            stk("p_conv", (30, 1024)), stk("p_sre", (64, 64)), stk("p_sim", (64, 64)), stk("p_mc", (4, 256, 256)),
            stk("p_mn", (4, 256)), stk("p_mm", (4,)), stk("p_rs", (16, 64, 64)), stk("p_rsh", (3200,)),
            cat("s_conv_o", (30, 1024)), cat("s_sre_o", (64, 64)), cat("s_sim_o", (64, 64)), cat("s_mc_o", (4, 256, 256)),
            cat("s_mn_o", (4, 256)), cat("s_mm_o", (4,)), cat("s_rs_o", (16, 64, 64)), cat("s_rsh_o", (3200,)))
```

```python
import contextlib
import numpy as np
import concourse.bass as bass
import concourse.mybir as mybir
from concourse.bass_utils import run_bass_kernel_spmd

F32 = mybir.dt.float32
BF16 = mybir.dt.bfloat16
AF = mybir.ActivationFunctionType
ALU = mybir.AluOpType
AX = mybir.AxisListType

D = 2048
TT = 128
WG = 512
KQ = 4
NWS = 4
NSLAB = 200
NCORES = 8
ALPHA = 4 ** 0.25
LN_EPS = 1e-5


class Reg:
    __slots__ = ("w", "r")

    def __init__(self):
        self.w = None
        self.r = []


class Buf:
    def __init__(self, ctx, name, t):
        self.ctx, self.name, self.t = ctx, name, t
        self.regs = {"_all": Reg()}
        self.dma_sem = None
        self.dma_cnt = 0

    def __call__(self, key, ap):
        return View(self, key, ap)

    def all(self):
        return View(self, None, self.t[:])

    def _sel(self, key):
        if key is None:
            return list(self.regs.values())
        if key not in self.regs:
            self.regs[key] = Reg()
        return [self.regs[key], self.regs["_all"]]

    def rdeps(self, key):
        return [r.w for r in self._sel(key) if r.w is not None]

    def wdeps(self, key):
        out = []
        for r in self._sel(key):
            if r.w is not None:
                out.append(r.w)
            out.extend(r.r)
        return out

    def note_read(self, key, tok):
        if key is None:
            for r in self.regs.values():
                r.r.append(tok)
        else:
            self._sel(key)[0].r.append(tok)

    def note_write(self, key, tok):
        if key is None:
            self.regs = {"_all": Reg()}
            self.regs["_all"].w = tok
        else:
            r = self._sel(key)[0]
            r.w = tok
            r.r = []


class View:
    __slots__ = ("buf", "key", "ap")

    def __init__(self, buf, key, ap):
        self.buf, self.key, self.ap = buf, key, ap


class Ctx:
    ENG = ("pe", "act", "dve", "pool", "sp")
    EPOCH = 30000

    def __init__(self, nc, es):
        self.nc, self.es = nc, es
        self.prog = {e: [] for e in self.ENG}
        self.cnt = {e: 0 for e in self.ENG}
        self.sem = {e: es.enter_context(nc.semaphore("sem_" + e)) for e in self.ENG}
        self.known = {e: {} for e in self.ENG}
        self.final = []
        self.total = {}
        self.nsem = 5
        self.nbytes = 0

    def sb(self, name, shape, dtype=F32):
        t = self.es.enter_context(self.nc.sbuf_tensor("sb_" + name, list(shape), dtype))
        n = 4
        for s in shape[1:]:
            n *= s
        self.nbytes += n
        return Buf(self, name, t)

    def ps(self, name, shape, dtype=F32):
        t = self.es.enter_context(self.nc.psum_tensor("ps_" + name, list(shape), dtype))
        return Buf(self, name, t)

    def need(self, e, tok):
        sem, val = tok
        k = id(sem)
        if self.known[e].get(k, 0) >= val:
            return
        self.known[e][k] = val
        self.prog[e].append(("wait", sem, val))

    def op(self, e, fn, reads=(), writes=()):
        for v in reads:
            if isinstance(v, View):
                for tok in v.buf.rdeps(v.key):
                    self.need(e, tok)
        for v in writes:
            if isinstance(v, View):
                for tok in v.buf.wdeps(v.key):
                    self.need(e, tok)
        if self.cnt[e] >= self.EPOCH:
            self.total[e] = self.total.get(e, 0) + self.cnt[e]
            self.sem[e] = self.es.enter_context(self.nc.semaphore("sem_%s_%d" % (e, self.total[e])))
            self.cnt[e] = 0
            self.nsem += 1
        self.cnt[e] += 1
        tok = (self.sem[e], self.cnt[e])
        self.prog[e].append(("op", fn, self.sem[e], 1))
        for v in reads:
            if isinstance(v, View):
                v.buf.note_read(v.key, tok)
        for v in writes:
            if isinstance(v, View):
                v.buf.note_write(v.key, tok)
        return tok

    def dma(self, out, in_, q="sp", slow=False):
        sbv = out if isinstance(out, View) else in_
        b = sbv.buf
        if b.dma_sem is None:
            b.dma_sem = self.es.enter_context(self.nc.semaphore("dq_" + b.name))
            self.nsem += 1
        if isinstance(in_, View):
            for tok in in_.buf.rdeps(in_.key):
                self.need(q, tok)
        if isinstance(out, View):
            for tok in out.buf.wdeps(out.key):
                self.need(q, tok)
        b.dma_cnt += 16
        tok = (b.dma_sem, b.dma_cnt)
        oap = out.ap if isinstance(out, View) else out
        iap = in_.ap if isinstance(in_, View) else in_
        if slow:
            fn = lambda eng, oap=oap, iap=iap: eng.dma_start(out=oap, in_=iap, allow_slow_non_contiguous=True)
        else:
            fn = lambda eng, oap=oap, iap=iap: eng.dma_start(out=oap, in_=iap)
        self.prog[q].append(("op", fn, b.dma_sem, 16))
        if isinstance(in_, View):
            in_.buf.note_read(in_.key, tok)
        if isinstance(out, View):
            out.buf.note_write(out.key, tok)
        else:
            self.final.append(tok)
        return tok

    def tt(self, e, out, in0, in1, op):
        return self.op(e, lambda g: g.tensor_tensor(out=out.ap, in0=in0.ap, in1=in1.ap, op=op),
                       reads=[in0, in1], writes=[out])

    def ts(self, e, out, in0, s1, s2, op0, op1=None):
        rd = [in0] + [s for s in (s1, s2) if isinstance(s, View)]
        a1 = s1.ap if isinstance(s1, View) else s1
        a2 = s2.ap if isinstance(s2, View) else s2
        if op1 is None:
            return self.op(e, lambda g: g.tensor_scalar(out=out.ap, in0=in0.ap, scalar1=a1, scalar2=None, op0=op0),
                           reads=rd, writes=[out])
        return self.op(e, lambda g: g.tensor_scalar(out=out.ap, in0=in0.ap, scalar1=a1, scalar2=a2, op0=op0, op1=op1),
                       reads=rd, writes=[out])

    def stt(self, e, out, in0, s, in1, op0, op1):
        rd = [in0, in1] + ([s] if isinstance(s, View) else [])
        a = s.ap if isinstance(s, View) else s
        return self.op(e, lambda g: g.scalar_tensor_tensor(out=out.ap, in0=in0.ap, scalar=a, in1=in1.ap, op0=op0, op1=op1),
                       reads=rd, writes=[out])

    def act(self, out, in_, func, bias=None, scale=None, e="act"):
        rd = [in_] + [s for s in (bias, scale) if isinstance(s, View)]
        kw = {}
        if bias is not None:
            kw["bias"] = bias.ap if isinstance(bias, View) else bias
        if scale is not None:
            kw["scale"] = scale.ap if isinstance(scale, View) else scale
        return self.op(e, lambda g: g.activation(out=out.ap, in_=in_.ap, func=func, **kw), reads=rd, writes=[out])

    def copy(self, e, out, in_):
        if e == "act":
            return self.act(out, in_, AF.Copy)
        return self.op(e, lambda g: g.tensor_copy(out=out.ap, in_=in_.ap), reads=[in_], writes=[out])

    def memset(self, e, out, val):
        return self.op(e, lambda g: g.memset(out.ap, val), writes=[out])

    def mm(self, out, lhsT, rhs, start=True, stop=True):
        return self.op("pe", lambda g: g.matmul(out.ap, lhsT=lhsT.ap, rhs=rhs.ap, start=start, stop=stop),
                       reads=[lhsT, rhs], writes=[out])

    def tr(self, out, in_, ident):
        return self.op("pe", lambda g: g.transpose(out.ap, in_.ap, ident.ap), reads=[in_, ident], writes=[out])

    def emit(self):
        nc = self.nc
        for tok in self.final:
            self.need("sp", tok)
        for e in self.ENG:
            if e != "sp" and self.cnt[e] > 0:
                self.need("sp", (self.sem[e], self.cnt[e]))
        engs = {"pe": "tensor", "act": "scalar", "dve": "vector", "pool": "gpsimd", "sp": "sync"}
        with nc.Block() as block:
            for e, attr in engs.items():
                items = self.prog[e]

                def body(eng, items=items):
                    for it in items:
                        if it[0] == "wait":
                            eng.wait_ge(it[1], it[2])
                        else:
                            it[1](eng).then_inc(it[2], it[3])

                getattr(block, attr)(body)


def make_consts():
    c = {}
    c["ident"] = np.eye(128, dtype=np.float32)
    c["ones"] = np.ones((128, 128), dtype=np.float32)
    r = np.arange(128)
    mrow = (r % 32) // 16
    mcol = np.arange(128) // 64
    bm = (mrow[:, None] == mcol[None, :]).astype(np.float32)
    ev_r = ((r // 32) % 2 == 0).astype(np.float32)
    c["bmask"] = bm * ev_r[:, None]
    c["bmask_o"] = bm * (1 - ev_r)[:, None]
    gl = np.arange(128) // 16
    cm = ((gl[None, :] % 2) == (r[:, None] // 64)).astype(np.float32)
    c["cmask"] = cm * ev_r[None, :]
    c["cmask_o"] = cm * (1 - ev_r)[None, :]
    BIG = 1e30
    i128 = np.arange(128)
    allow_p = (i128[:, None] <= i128[None, :])
    c["maskbig_p"] = np.where(allow_p, 0.0, BIG).astype(np.float32)
    c["segtri_p"] = allow_p.astype(np.float32)
    i64 = np.arange(64)
    allow_s = (i64[:, None] <= i64[None, :]) & ((i64[:, None] // 8) == (i64[None, :] // 8))
    c["maskbig_s"] = np.where(allow_s, 0.0, BIG).astype(np.float32)
    c["segtri_s"] = allow_s.astype(np.float32)
    r01p = np.ones((128, 128), np.float32); r01p[:, 0] = 0
    r01s = np.ones((128, 64), np.float32); r01s[:, ::8] = 0
    c["r01_p"], c["r01_s"] = r01p, r01s
    c["rneg_p"] = ((1 - r01p) * -BIG).astype(np.float32)
    c["rneg_s"] = ((1 - r01s) * -BIG).astype(np.float32)
    segsel = ((i64[None, :] // 8) == np.arange(8)[:, None]).astype(np.float32)
    c["segsel"] = segsel
    c["segc"] = np.ascontiguousarray(segsel.T)
    c["segrow"] = np.ascontiguousarray(np.broadcast_to(segsel.reshape(1, 512), (128, 512))).astype(np.float32)
    c["blk"] = ((i128[:, None] // 64) == (i128[None, :] // 64)).astype(np.float32)
    st_p = (i128[:, None] < i128[None, :])
    st_s = (i64[:, None] < i64[None, :]) & ((i64[:, None] // 8) == (i64[None, :] // 8))
    c["sstri_p"] = st_p.astype(np.float32)
    c["sstriT_p"] = np.ascontiguousarray(st_p.T).astype(np.float32)
    c["sstri_s"] = st_s.astype(np.float32)
    c["sstriT_s"] = np.ascontiguousarray(st_s.T).astype(np.float32)
    return c


CONST_SHAPES = {"ident": [128, 128], "ones": [128, 128], "bmask": [128, 128], "cmask": [128, 128],
                "bmask_o": [128, 128], "cmask_o": [128, 128],
                "maskbig_p": [128, 128], "segtri_p": [128, 128], "maskbig_s": [64, 64], "segtri_s": [64, 64],
                "r01_p": [128, 128], "r01_s": [128, 64], "rneg_p": [128, 128], "rneg_s": [128, 64],
                "segsel": [8, 64], "segc": [64, 8], "segrow": [128, 512], "blk": [128, 128],
                "sstri_p": [128, 128], "sstriT_p": [128, 128], "sstri_s": [64, 64], "sstriT_s": [64, 64]}


def tile_plan(cfg):
    tiles = []
    npt = cfg.get("n_prompt_tiles", 17)
    pos = 0
    for i in range(17):
        T = 16 if i == 16 else TT
        if i < npt:
            tiles.append(dict(kind="p", T=T, pos=pos, nseq=1, L=T, first=(i == 0), last=(i == npt - 1), s0=0))
        pos += T
    if cfg.get("sample", True):
        for h in range(2):
            tiles.append(dict(kind="s", T=64, pos=0, nseq=8, L=8, first=True, last=True, s0=8 * h))
    return tiles


def build(cfg):
    nc = bass.Bass("TRN2", target_bir_lowering=False)
    es = contextlib.ExitStack()
    C = Ctx(nc, es)
    dbg = cfg.get("debug", False)
    nlayers = cfg.get("layers", 2)

    def din(name, shape):
        return nc.dram_tensor(name, list(shape), F32, kind="ExternalInput").ap()

    def dout(name, shape):
        return nc.dram_tensor(name, list(shape), F32, kind="ExternalOutput").ap()

    I = {}
    I["xp"] = din("xp", [2048, D])
    I["meta"] = din("meta", [16, D])
    I["xs"] = din("xs", [128, D])
    I["s_conv"] = din("s_conv", [16 * 30, 1024])
    I["s_sre"] = din("s_sre", [16, 64, 64])
    I["s_sim"] = din("s_sim", [16, 64, 64])
    for nm, shp in CONST_SHAPES.items():
        I[nm] = din(nm, shp)
    I["s_mc"] = din("s_mc", [16, 4, 256, 256])
    I["s_mn"] = din("s_mn", [16, 4, 256])
    I["s_mm"] = din("s_mm", [16, 4])
    I["s_rs"] = din("s_rs", [16, 16, 64, 64])
    I["s_rsh"] = din("s_rsh", [16, 3200])
    I["od_w_in"] = din("od_w_in", [D, 9352])
    I["m_ig_b"] = din("m_ig_b", [1, 4])
    I["m_fg_b"] = din("m_fg_b", [1, 4])
    for nm in ("m_hn_g", "r_w0", "r_a0", "r_kk", "r_ka", "r_ln_g", "r_ln_b", "r_rk"):
        I[nm] = din(nm, [1, 1024])
    I["r_mu"] = din("r_mu", [1, 3200])
    I["r_w2"] = din("r_w2", [64, 1024])
    I["r_a2"] = din("r_a2", [64, 1024])
    I["od_w_out"] = din("od_w_out", [2048, D])
    I["od_ln_g"] = din("od_ln_g", [1, D])
    I["od_ln_b"] = din("od_ln_b", [1, D])
    I["ev_w_in"] = din("ev_w_in", [D, 5120])
    I["a_conv_w"] = din("a_conv_w", [31, 1024])
    for nm in ("a_conv_b", "a_ln_g", "a_ln_b", "s5_d"):
        I[nm] = din(nm, [1, 1024])
    I["a_pw"] = din("a_pw", [1024, 1024])
    I["s5_lambda_re"] = din("s5_lambda_re", [64, 64])
    I["s5_lambda_im"] = din("s5_lambda_im", [64, 64])
    I["s5_log_dt"] = din("s5_log_dt", [1, 64])
    I["s5_b_re"] = din("s5_b_re", [64, 64, 16])
    I["s5_b_im"] = din("s5_b_im", [64, 64, 16])
    I["s5_c_re"] = din("s5_c_re", [1024, 64])
    I["s5_c_im"] = din("s5_c_im", [1024, 64])
    I["s5_glu_w"] = din("s5_glu_w", [1024, 2048])
    I["s5_glu_b"] = din("s5_glu_b", [1, 2048])
    I["ev_w_out"] = din("ev_w_out", [2048, D])
    I["ev_ln_g"] = din("ev_ln_g", [1, D])
    I["ev_ln_b"] = din("ev_ln_b", [1, D])

    O = {}
    O["y_p"] = dout("y_p", [2048, D])
    O["y_s"] = dout("y_s", [128, D])
    O["p_conv"] = dout("p_conv", [30, 1024])
    O["p_sre"] = dout("p_sre", [64, 64])
    O["p_sim"] = dout("p_sim", [64, 64])
    O["s_conv_o"] = dout("s_conv_o", [16 * 30, 1024])
    O["s_sre_o"] = dout("s_sre_o", [16, 64, 64])
    O["s_sim_o"] = dout("s_sim_o", [16, 64, 64])
    O["p_mc"] = dout("p_mc", [4, 256, 256])
    O["p_mn"] = dout("p_mn", [4, 256])
    O["p_mm"] = dout("p_mm", [1, 4])
    O["p_rs"] = dout("p_rs", [16, 64, 64])
    O["p_rsh"] = dout("p_rsh", [1, 3200])
    O["s_mc_o"] = dout("s_mc_o", [16, 4, 256, 256])
    O["s_mn_o"] = dout("s_mn_o", [16, 4, 256])
    O["s_mm_o"] = dout("s_mm_o", [16, 4])
    O["s_rs_o"] = dout("s_rs_o", [16, 16, 64, 64])
    O["s_rsh_o"] = dout("s_rsh_o", [16, 3200])

    ident = C.sb("ident", [128, 128])
    ones = C.sb("ones", [128, 128])
    XT = C.sb("XT", [128, 16, TT])
    XS = C.sb("XS", [128, D])
    PJ = C.sb("PJ", [128, 33, TT])
    MIX = C.sb("MIX", [128, 16 * TT], BF16)
    XTB = C.sb("XTB", [128, 16, TT], BF16)
    GELB = C.sb("GELB", [128, 8, TT], BF16)
    WS = [C.sb("WS%d" % i, [128, KQ, WG], BF16) for i in range(NWS)]
    HB = C.sb("HB", [128, 8, 304])
    HC = C.sb("HC", [128, 8, 30])
    ACC = C.sb("ACC", [128, 8, TT])
    SQ2 = C.sb("SQ2", [128, 2, TT])
    ST = C.sb("ST", [128, 3, TT])
    VEC0 = C.sb("VEC0", [128, 8, 40])
    VECG = C.sb("VECG", [128, 16, 4])
    TMP = C.sb("TMP", [128, 128])
    XR = C.sb("XR", [128, 32, 129])
    XI = C.sb("XI", [128, 32, 129])
    S5C = C.sb("S5C", [128, 2, 32])
    S5A = C.sb("S5A", [128, 8, 32])
    S5T = C.sb("S5T", [128, 10, 32])
    SCS = C.sb("SCS", [128, 2, 32, 8])
    BRE = [C.sb("BRE%d" % i, [128, 8, 128]) for i in range(2)]
    BIM = [C.sb("BIM%d" % i, [128, 8, 128]) for i in range(2)]
    CRE = [C.sb("CRE%d" % i, [128, 8, 128]) for i in range(2)]
    CIM = [C.sb("CIM%d" % i, [128, 8, 128]) for i in range(2)]
    mask_bo = C.sb("mask_bo", [128, 128])
    mask_co = C.sb("mask_co", [128, 128])
    S5ST = C.sb("S5ST", [128, 128])
    mask_b = C.sb("mask_b", [128, 128])
    mask_c = C.sb("mask_c", [128, 128])

    PSA = [C.ps("PSA%d" % i, [128, 512]) for i in range(4)]
    PSB = [C.ps("PSB%d" % i, [128, 512]) for i in range(4)]

    def mixv(ct, T):
        return MIX(("t", ct), MIX.t[:, ct * TT:ct * TT + T])

    C.dma(ident.all(), I["ident"])
    C.dma(ones.all(), I["ones"])
    C.dma(mask_b.all(), I["bmask"])
    C.dma(mask_c.all(), I["cmask"])
    C.dma(mask_bo.all(), I["bmask_o"])
    C.dma(mask_co.all(), I["cmask_o"])

    rr = {"psa": 0, "psb": 0, "ws": 0, "ev": 0, "sq": 0, "tm": 0}

    def next_psa():
        rr["psa"] = (rr["psa"] + 1) % 4
        return PSA[rr["psa"]]

    def next_psb():
        rr["psb"] = (rr["psb"] + 1) % 4
        return PSB[rr["psb"]]

    def ev_eng():
        rr["ev"] += 1
        return "dve" if rr["ev"] % 2 else "act"

    def load_rows_T(dst, rows, ncols, col0=0, ct0=0):
        r0 = 0
        for ap in rows:
            nr = ap.shape[0]
            C.dma(XS(None, XS.t[r0:r0 + nr, 0:ncols]), ap)
            r0 += nr
        nr = r0
        for ct in range(ncols // 128):
            ps = next_psb()
            C.tr(ps(None, ps.t[:, 0:nr]), XS(None, XS.t[0:nr, ct * 128:(ct + 1) * 128]), ident(None, ident.t[0:nr, 0:nr]))
            C.copy("dve", dst(None, dst.t[:, ct0 + ct, col0:col0 + nr]), ps(None, ps.t[:, 0:nr]))

    load_rows_T(VEC0, [I["a_conv_w"], I["a_conv_b"], I["a_ln_g"], I["a_ln_b"], I["s5_d"]], 1024)
    load_rows_T(VECG, [I["s5_glu_b"], I["ev_ln_g"], I["ev_ln_b"]], 2048)

    def gp_ap(ap2d):
        return ap2d.rearrange("(q m) p -> (m p) q", m=2)

    LR, LI, DT, AR, AI, NAI = range(6)
    A = lambda k: S5A(None, S5A.t[:, k, :])
    Tm = lambda k: S5T(None, S5T.t[:, k, :])
    for m in range(2):
        C.dma(S5A(None, S5A.t[64 * m:64 * m + 64, LR, :]), I["s5_lambda_re"].rearrange("(q m) p -> m p q", m=2)[m], slow=True)
        C.dma(S5A(None, S5A.t[64 * m:64 * m + 64, LI, :]), I["s5_lambda_im"].rearrange("(q m) p -> m p q", m=2)[m], slow=True)
    ldt = I["s5_log_dt"]
    for m in range(2):
        src = bass.AP(ldt.tensor, ldt.offset + m, [[0, 64], [2, 32]])
        C.dma(S5A(None, S5A.t[64 * m:64 * m + 64, DT, :]), src, slow=True)
    C.act(A(DT), A(DT), AF.Exp)
    C.tt("dve", Tm(0), A(LR), A(DT), ALU.mult)
    C.act(Tm(1), Tm(0), AF.Exp)
    C.tt("dve", Tm(2), A(LI), A(DT), ALU.mult)
    PI = float(np.pi)
    C.ts("dve", Tm(3), Tm(2), 1.0 / 32, None, ALU.mult)
    C.act(Tm(4), Tm(3), AF.Sin)
    C.ts("dve", Tm(3), Tm(3), PI / 2, None, ALU.add)
    C.act(Tm(5), Tm(3), AF.Sin)
    for _ in range(5):
        C.tt("dve", Tm(3), Tm(4), Tm(5), ALU.mult)
        C.tt("dve", Tm(8), Tm(5), Tm(5), ALU.mult)
        C.tt("dve", Tm(9), Tm(4), Tm(4), ALU.mult)
        C.ts("dve", Tm(4), Tm(3), 2.0, None, ALU.mult)
        C.tt("dve", Tm(5), Tm(8), Tm(9), ALU.subtract)
    C.tt("dve", A(AR), Tm(1), Tm(5), ALU.mult)
    C.tt("dve", A(AI), Tm(1), Tm(4), ALU.mult)
    C.ts("dve", A(NAI), A(AI), -1.0, None, ALU.mult)
    MAG = 6
    C.copy("dve", A(MAG), Tm(1))
    s5tab = nc.dram_tensor("s5tab", [2, 128, 1024], F32).ap()
    tC = lambda a, b: XR(None, XR.t[:, :, a:b])
    tS = lambda a, b: XI(None, XI.t[:, :, a:b])
    C.copy("dve", tC(0, 1), S5T(None, S5T.t[:, 5, :].unsqueeze(2)))
    C.copy("dve", tS(0, 1), S5T(None, S5T.t[:, 4, :].unsqueeze(2)))
    n_ = 1
    while n_ < 32:
        cn = XR(None, XR.t[:, :, n_ - 1:n_].broadcast_to([128, 32, n_]))
        sn = XI(None, XI.t[:, :, n_ - 1:n_].broadcast_to([128, 32, n_]))
        u1 = XR(None, XR.t[:, :, 64:64 + n_])
        u2 = XI(None, XI.t[:, :, 64:64 + n_])
        C.tt("dve", u1, tS(0, n_), sn, ALU.mult)
        C.tt("dve", tC(n_, 2 * n_), tC(0, n_), cn, ALU.mult)
        C.tt("dve", tC(n_, 2 * n_), tC(n_, 2 * n_), u1, ALU.subtract)
        C.tt("dve", u2, tC(0, n_), sn, ALU.mult)
        C.tt("dve", tS(n_, 2 * n_), tS(0, n_), cn, ALU.mult)
        C.tt("dve", tS(n_, 2 * n_), tS(n_, 2 * n_), u2, ALU.add)
        n_ *= 2
    tab_tok = [C.dma(s5tab[0].rearrange("p (q t) -> p q t", t=32), tC(0, 32)),
               C.dma(s5tab[1].rearrange("p (q t) -> p q t", t=32), tS(0, 32))]
    for tk in tab_tok:
        C.need("sp", tk)
    C.tt("dve", Tm(0), A(LR), A(LR), ALU.mult)
    C.tt("dve", Tm(1), A(LI), A(LI), ALU.mult)
    C.tt("dve", Tm(0), Tm(0), Tm(1), ALU.add)
    C.op("dve", lambda g: g.reciprocal(out=S5T.t[:, 0, :], in_=S5T.t[:, 0, :]), reads=[Tm(0)], writes=[Tm(0)])
    C.ts("dve", Tm(1), A(AR), -1.0, None, ALU.add)
    C.tt("dve", Tm(2), Tm(1), A(LR), ALU.mult)
    C.tt("dve", Tm(3), A(AI), A(LI), ALU.mult)
    C.tt("dve", Tm(2), Tm(2), Tm(3), ALU.add)
    C.tt("dve", Tm(6), Tm(2), Tm(0), ALU.mult)
    C.tt("dve", Tm(2), A(AI), A(LR), ALU.mult)
    C.tt("dve", Tm(3), Tm(1), A(LI), ALU.mult)
    C.tt("dve", Tm(2), Tm(2), Tm(3), ALU.subtract)
    C.tt("dve", Tm(7), Tm(2), Tm(0), ALU.mult)
    braw_t = PJ.t[:, 0:8, :].rearrange("p a b -> p (a b)").rearrange("p (k q h) -> p k q h", k=2, q=32)
    bb_t = PJ.t[:, 8:24, :].rearrange("p a b -> p (a b)").rearrange("p (k q h) -> p k q h", k=2, q=32)
    sc_t = PJ.t[:, 24:28, :].rearrange("p a b -> p (a b)").rearrange("p (q h) -> p q h", q=32)
    for k, nm in enumerate(("s5_b_re", "s5_b_im")):
        for m in range(2):
            C.dma(PJ(None, braw_t[64 * m:64 * m + 64, k, :, :]), I[nm].rearrange("(q m) p h -> m p q h", m=2)[m])
    qr_b = S5T(None, S5T.t[:, 6, :].unsqueeze(2).broadcast_to([128, 32, 16]))
    qi_b = S5T(None, S5T.t[:, 7, :].unsqueeze(2).broadcast_to([128, 32, 16]))
    br = PJ(None, braw_t[:, 0, :, :])
    bi = PJ(None, braw_t[:, 1, :, :])
    sc = PJ(None, sc_t)
    for dup in range(2):
        o_r = PJ(None, bb_t[:, 0, :, dup * 16:(dup + 1) * 16])
        o_i = PJ(None, bb_t[:, 1, :, dup * 16:(dup + 1) * 16])
        C.tt("dve", o_r, br, qr_b, ALU.mult)
        C.tt("dve", sc, bi, qi_b, ALU.mult)
        C.tt("dve", o_r, o_r, sc, ALU.subtract)
        C.tt("dve", o_i, bi, qr_b, ALU.mult)
        C.tt("dve", sc, br, qi_b, ALU.mult)
        C.tt("dve", o_i, o_i, sc, ALU.add)
    for k, dst in enumerate((BRE, BIM)):
        for gt in range(8):
            ps = next_psb()
            C.copy("dve", TMP.all(), PJ(None, bb_t[:, k, 4 * gt:4 * gt + 4, :]))
            C.tr(ps(None, ps.t[:, 0:128]), TMP.all(), ident.all())
            C.tt("dve", dst[0](None, dst[0].t[:, gt, :]), ps(None, ps.t[:, 0:128]), mask_b.all(), ALU.mult)
            C.tt("dve", dst[1](None, dst[1].t[:, gt, :]), ps(None, ps.t[:, 0:128]), mask_bo.all(), ALU.mult)
    for k, (nm, dst) in enumerate((("s5_c_re", CRE), ("s5_c_im", CIM))):
        for gt in range(8):
            for dup in range(2):
                C.dma(XS(None, XS.t[:, dup * 64:(dup + 1) * 64]), I[nm][gt * 128:(gt + 1) * 128, :])
            ps = next_psb()
            C.tr(ps(None, ps.t[:, 0:128]), XS(None, XS.t[:, 0:128]), ident.all())
            for par, mk in enumerate((mask_c, mask_co)):
                C.stt("dve", dst[par](None, dst[par].t[:, gt, :]), ps(None, ps.t[:, 0:128]), 1.0 if k == 0 else -1.0,
                      mk.all(), ALU.mult, ALU.mult)

    WSCR = nc.dram_tensor("wscr", [NSLAB, 128, KQ * WG], BF16).ap()
    wst = {"id": 0, "first": True, "tok": {}}

    def load_slab(ws, src, nk, gw):
        sid = wst["id"]
        wst["id"] += 1
        assert sid < NSLAB
        dstv = ws(None, ws.t[:, 0:nk, 0:gw])
        scr = WSCR[sid][:, 0:nk * gw].rearrange("p (k c) -> p k c", k=nk)
        if wst["first"]:
            C.dma(dstv, src, q="pool")
            wst["tok"][sid] = C.dma(scr, dstv, q="sp")
        else:
            C.need("sp", wst["tok"][sid])
            C.dma(dstv, scr, q="sp")

    def stream_mm(W, nkt, coltiles, rhs_fn, T, evac, ev=None):
        for _ in stream_mm_g(W, nkt, coltiles, rhs_fn, T, evac, ev):
            pass

    def stream_mm_g(W, nkt, coltiles, rhs_fn, T, evac, ev=None):
        groups = []
        cur = []
        for ct in coltiles:
            if cur and (ct[0] + ct[1] - cur[0][0] > WG):
                groups.append(cur)
                cur = []
            cur.append(ct)
        if cur:
            groups.append(cur)
        Wv = W.rearrange("(kt p) c -> p kt c", p=128)
        j = 0
        for grp in groups:
            g0 = grp[0][0]
            gw = grp[-1][0] + grp[-1][1] - g0
            ps = next_psa()
            pacc = ps(None, ps.t[0:T, 0:gw])
            for kq in range(0, nkt, KQ):
                ws = WS[rr["ws"] % NWS]
                rr["ws"] += 1
                nk = min(KQ, nkt - kq)
                load_slab(ws, Wv[:, kq:kq + nk, g0:g0 + gw], nk, gw)
                for k in range(nk):
                    kt = kq + k
                    C.mm(pacc, rhs_fn(kt), ws(None, ws.t[:, k, 0:gw]), start=(kt == 0), stop=(kt == nkt - 1))
            i_ = rr["tm"] % 2
            rr["tm"] += 1
            C.copy(ev or ev_eng(), XS(("tm", i_), XS.t[0:T, i_ * 512:i_ * 512 + gw]), pacc)
            for (c0, w) in grp:
                pt = next_psb()
                C.tr(pt(None, pt.t[0:w, 0:T]), XS(("tm", i_), XS.t[0:T, i_ * 512 + c0 - g0:i_ * 512 + c0 - g0 + w]),
                     ident(None, ident.t[0:T, 0:T]))
                evac(j, pt(None, pt.t[0:w, 0:T]))
                j += 1
            yield "g"

    def layer_norm_cols(src, ntile, T, gcol, bcol, vec, func=AF.Identity, eps=LN_EPS, dst_fn=None, also_fn=None):
        nch = ntile * 128
        ps = next_psb()
        ps2 = next_psb()
        for ct in range(ntile):
            sv = src(("t", ct), src.t[:, ct, 0:T])
            sq = SQ2(("s", rr["sq"] % 2), SQ2.t[:, rr["sq"] % 2, 0:T])
            rr["sq"] += 1
            C.act(sq, sv, AF.Square)
            C.mm(ps(None, ps.t[:, 0:T]), ones.all(), sv, start=(ct == 0), stop=(ct == ntile - 1))
            C.mm(ps2(None, ps2.t[:, 0:T]), ones.all(), sq, start=(ct == 0), stop=(ct == ntile - 1))
        st = lambda k: ST(None, ST.t[:, k, 0:T])
        C.ts("dve", st(0), ps(None, ps.t[:, 0:T]), 1.0 / nch, None, ALU.mult)
        C.ts("dve", st(1), ps2(None, ps2.t[:, 0:T]), 1.0 / nch, None, ALU.mult)
        C.tt("dve", st(2), st(0), st(0), ALU.mult)
        C.tt("dve", st(1), st(1), st(2), ALU.subtract)
        C.ts("dve", st(1), st(1), eps, None, ALU.add)
        C.act(st(1), st(1), AF.Sqrt)
        C.op("dve", lambda g, T=T: g.reciprocal(out=ST.t[:, 1, 0:T], in_=ST.t[:, 1, 0:T]), reads=[st(1)], writes=[st(1)])
        C.tt("dve", st(2), st(0), st(1), ALU.mult)
        C.ts("dve", st(2), st(2), -1.0, None, ALU.mult)
        for ct in range(ntile):
            e = "dve" if ct % 2 == 0 else "pool"
            sv = src(("t", ct), src.t[:, ct, 0:T])
            C.tt(e, sv, sv, st(1), ALU.mult)
            C.tt(e, sv, sv, st(2), ALU.add)
            dv = dst_fn(ct) if dst_fn is not None else sv
            C.act(dv, sv, func, bias=vec(None, vec.t[:, ct, bcol:bcol + 1]), scale=vec(None, vec.t[:, ct, gcol:gcol + 1]))
            if also_fn is not None:
                C.copy("pool" if ct % 2 else "act", also_fn(ct), dv)

    def load_x(tile):
        n = tile["T"]
        if tile["kind"] == "s":
            C.dma(XS(None, XS.t[0:n, :]), I["xs"][tile["s0"] * 8:tile["s0"] * 8 + n, :])
        else:
            p0 = tile["pos"]
            r = 0
            if p0 < 16:
                C.dma(XS(None, XS.t[0:16, :]), I["meta"][:, :])
                r = 16
            x0 = p0 + r - 16
            C.dma(XS(None, XS.t[r:n, :]), I["xp"][x0:x0 + n - r, :])
        for dt_ in range(16):
            ps = next_psb()
            C.tr(ps(None, ps.t[:, 0:n]), XS(None, XS.t[0:n, dt_ * 128:(dt_ + 1) * 128]), ident(None, ident.t[0:n, 0:n]))
            C.copy("act" if dt_ % 2 else "dve", XT(("t", dt_), XT.t[:, dt_, 0:n]), ps(None, ps.t[:, 0:n]))
            C.copy("pool", XTB(("t", dt_), XTB.t[:, dt_, 0:n]), XT(("t", dt_), XT.t[:, dt_, 0:n]))

    def store_y(tile):
        n = tile["T"]
        for dt_ in range(16):
            ps = next_psb()
            C.tr(ps(None, ps.t[0:n, 0:128]), XT(("t", dt_), XT.t[:, dt_, 0:n]), ident.all())
            C.copy("act" if dt_ % 2 else "dve", XS(None, XS.t[0:n, dt_ * 128:(dt_ + 1) * 128]), ps(None, ps.t[0:n, 0:128]))
        if tile["kind"] == "s":
            C.dma(O["y_s"][tile["s0"] * 8:tile["s0"] * 8 + n, :], XS(None, XS.t[0:n, :]))
        else:
            p0 = tile["pos"]
            r = 16 if p0 < 16 else 0
            x0 = p0 + r - 16
            C.dma(O["y_p"][x0:x0 + n - r, :], XS(None, XS.t[r:n, :]))

    def out_proj_ln(W, tile, vec, gcol, bcol):
        T = tile["T"]

        def evac(j, pv):
            xv_ = XT(("t", j), XT.t[:, j, 0:T])
            C.stt("dve", xv_, xv_, ALPHA, pv, ALU.mult, ALU.add)

        stream_mm(W, 16, [(i * 128, 128) for i in range(16)], lambda kt: mixv(kt, T), T, evac)
        layer_norm_cols(XT, 16, T, gcol, bcol, vec, also_fn=lambda ct: XTB(("t", ct), XTB.t[:, ct, 0:T]))

    def layer0(tile):
        T, nseq, L = tile["T"], tile["nseq"], tile["L"]
        is_s = tile["kind"] == "s"
        s0 = tile["s0"]
        xrhs = lambda kt: XTB(("t", kt), XTB.t[:, kt, 0:T])
        pj = lambda j: PJ(("t", j), PJ.t[:, j, 0:T])

        fence(HB, HB.t[0:1, 0, 0:1])
        def evacA(j, pv):
            if j < 8:
                C.copy(ev_eng(), pj(j), pv)
            elif j < 16:
                C.act(pj(j), pv, AF.Sigmoid)
            else:
                C.act(pj(j), pv, AF.Silu)

        stream_mm(I["ev_w_in"], 16, [(i * 128, 128) for i in range(24)], xrhs, T, evacA)

        W_ = 30 + L

        def hb(ct, a, b):
            v = HB.t[:, ct, 0:nseq * W_].rearrange("p (n w) -> p n w", w=W_)[:, :, a:b]
            return HB(("t", ct), v)

        def tokv(buf, ct):
            return buf(("t", ct), buf.t[:, ct, 0:T].rearrange("p (n l) -> p n l", l=L))

        if is_s:
            for q in range(2):
                C.dma(XS(None, XS.t[0:120, 0:1024]), I["s_conv"][s0 * 30 + q * 120:s0 * 30 + (q + 1) * 120, :])
                for ct in range(8):
                    ps = next_psb()
                    C.tr(ps(None, ps.t[:, 0:120]), XS(None, XS.t[0:120, ct * 128:(ct + 1) * 128]), ident(None, ident.t[0:120, 0:120]))
                    dstv = HB.t[:, ct, 0:nseq * W_].rearrange("p (n w) -> p n w", w=W_)[:, 4 * q:4 * q + 4, 0:30]
                    C.copy("dve", HB(("t", ct), dstv), ps(None, ps.t[:, 0:120].rearrange("p (n r) -> p n r", r=30)))
        else:
            for ct in range(8):
                if tile["first"]:
                    C.memset("pool", hb(ct, 0, 30), 0.0)
                else:
                    C.copy("pool", hb(ct, 0, 30), HC(("t", ct), HC.t[:, ct, :].unsqueeze(1)))
        for ct in range(8):
            e = "dve" if ct % 2 == 0 else "pool"
            C.tt(e, hb(ct, 30, 30 + L), tokv(PJ, ct), tokv(PJ, 8 + ct), ALU.mult)
        def evacB(j, pv):
            if j < 8:
                C.copy("act", pj(j), pv)
            else:
                C.act(pj(j), pv, AF.Silu)

        stream_mm(I["ev_w_in"], 16, [(3072 + i * 128, 128) for i in range(16)], xrhs, T, evacB, ev="act")
        for ct in range(8):
            e = "dve"
            acc = tokv(ACC, ct)
            wcol = lambda j, ct=ct: VEC0(None, VEC0.t[:, ct, j:j + 1])
            C.ts(e, acc, hb(ct, 0, L), wcol(0), wcol(31), ALU.mult, ALU.add)
            for j in range(1, 31):
                C.stt(e, acc, hb(ct, j, j + L), wcol(j), acc, ALU.mult, ALU.add)
        if is_s:
            for q in range(2):
                for ct in range(8):
                    ps = next_psb()
                    srcv = HB.t[:, ct, 0:nseq * W_].rearrange("p (n w) -> p n w", w=W_)[:, 4 * q:4 * q + 4, L:L + 30]
                    C.copy("pool", TMP(None, TMP.t[:, 0:120].rearrange("p (n r) -> p n r", r=30)), HB(("t", ct), srcv))
                    C.tr(ps(None, ps.t[0:120, 0:128]), TMP(None, TMP.t[:, 0:120]), ident.all())
                    C.copy("dve", XS(None, XS.t[0:120, ct * 128:(ct + 1) * 128]), ps(None, ps.t[0:120, 0:128]))
                C.dma(O["s_conv_o"][s0 * 30 + q * 120:s0 * 30 + (q + 1) * 120, :], XS(None, XS.t[0:120, 0:1024]))
        else:
            for ct in range(8):
                C.copy("pool", TMP(None, TMP.t[:, 0:30]), HB(("t", ct), HB.t[:, ct, L:L + 30]))
                C.copy("pool", HC(("t", ct), HC.t[:, ct, :]), TMP(None, TMP.t[:, 0:30]))
            if tile["last"]:
                for ct in range(8):
                    ps = next_psb()
                    C.tr(ps(None, ps.t[0:30, 0:128]), HC(("t", ct), HC.t[:, ct, :]), ident.all())
                    C.copy("dve", XS(None, XS.t[0:30, ct * 128:(ct + 1) * 128]), ps(None, ps.t[0:30, 0:128]))
                C.dma(O["p_conv"][:, :], XS(None, XS.t[0:30, 0:1024]))
        layer_norm_cols(ACC, 8, T, 32, 33, VEC0, func=AF.Silu, dst_fn=lambda ct: mixv(8 + ct, T))

        def evac_pw(j, pv):
            C.tt("dve", mixv(j, T), pv, pj(16 + j), ALU.mult)

        stream_mm(I["a_pw"], 8, [(i * 128, 128) for i in range(8)], lambda kt: mixv(8 + kt, T), T, evac_pw)


        Wx = 1 + L

        def xv(buf, a, b, p0=0, p1=32):
            v = buf.t[:, p0:p1, 0:nseq * Wx].rearrange("p q (n w) -> p q n w", w=Wx)[:, :, :, a:b]
            return buf(None, v)

        if is_s:
            for k, (nm, buf) in enumerate((("s_sre", XR), ("s_sim", XI))):
                for q in range(2):
                    for pr in range(16):
                        g0 = 2 * (16 * q + pr)
                        C.dma(S5ST(None, S5ST.t[pr * 8:(pr + 1) * 8, :]),
                              I[nm][s0:s0 + 8, g0:g0 + 2, :].rearrange("n m p -> n (m p)"))
                    ps = next_psb()
                    C.tr(ps(None, ps.t[:, 0:128]), S5ST.all(), ident.all())
                    C.copy("dve", xv(buf, 0, 1, 16 * q, 16 * q + 16),
                           ps(None, ps.t[:, 0:128].rearrange("p (q n o) -> p q n o", n=8, o=1)))
        else:
            for k, buf in enumerate((XR, XI)):
                if tile["first"]:
                    C.memset("pool", xv(buf, 0, 1), 0.0)
                else:
                    C.copy("pool", xv(buf, 0, 1), S5C(None, S5C.t[:, k, :].unsqueeze(2).unsqueeze(3)))
        for q4 in range(8):
            for k, (tab, buf) in enumerate(((BRE, XR), (BIM, XI))):
                ps = next_psa()
                for ip in range(4):
                    hf = ip // 2
                    tb = tab[ip % 2]
                    C.mm(ps(None, ps.t[:, ip * T:(ip + 1) * T]),
                         tb(None, tb.t[64 * hf:64 * hf + 64, q4, :]),
                         PJ(("t", q4), PJ.t[64 * hf:64 * hf + 64, q4, 0:T]))
                C.copy("act", xv(buf, 1, Wx, 4 * q4, 4 * q4 + 4),
                       ps(None, ps.t[:, 0:4 * T].rearrange("p (q n l) -> p q n l", q=4, l=L)))
        arb = S5A(None, S5A.t[:, AR, :].unsqueeze(2).broadcast_to([128, 32, nseq]))
        aib = S5A(None, S5A.t[:, AI, :].unsqueeze(2).broadcast_to([128, 32, nseq]))
        naib = S5A(None, S5A.t[:, NAI, :].unsqueeze(2).broadcast_to([128, 32, nseq]))
        if nseq == 1:
            t1v = S5T(None, S5T.t[:, 8, :].unsqueeze(2))
            t2v = S5T(None, S5T.t[:, 9, :].unsqueeze(2))
        else:
            t1v = SCS(None, SCS.t[:, 0, :, :])
            t2v = SCS(None, SCS.t[:, 1, :, :])

        def col(buf, t):
            v = buf.t[:, :, 0:nseq * Wx].rearrange("p q (n w) -> p q n w", w=Wx)[:, :, :, t]
            return buf(None, v)

        if is_s:
            e = "dve"
            for t in range(L):
                C.tt(e, t1v, col(XR, t), arb, ALU.mult)
                C.tt(e, col(XR, t + 1), col(XR, t + 1), t1v, ALU.add)
                C.tt(e, t1v, col(XI, t), naib, ALU.mult)
                C.tt(e, col(XR, t + 1), col(XR, t + 1), t1v, ALU.add)
                C.tt(e, t2v, col(XI, t), arb, ALU.mult)
                C.tt(e, col(XI, t + 1), col(XI, t + 1), t2v, ALU.add)
                C.tt(e, t2v, col(XR, t), aib, ALU.mult)
                C.tt(e, col(XI, t + 1), col(XI, t + 1), t2v, ALU.add)
        else:
            VTMf_ = VTM.t[:, :, :].rearrange("p a b -> p (a b)")
            PJf_ = PJ.t[:, 24:32, :].rearrange("p a b -> p (a b)")
            ROWf_ = ROW.t[:, :, :].rearrange("p a b -> p (a b)")
            C.dma(VTM(None, VTMf_[:, 0:1024]), s5tab[0])
            C.dma(PJ(None, PJf_), s5tab[1])
            for t0 in range(0, T, 32):
                Tc = min(32, T - t0)
                ec = VTM(None, VTMf_[:, 0:1024].rearrange("p (q t) -> p q t", t=32)[:, :, 0:Tc])
                es = PJ(None, PJf_.rearrange("p (q t) -> p q t", t=32)[:, :, 0:Tc])
                w1 = KW(None, KW.t[:, :].rearrange("p (q t) -> p q t", t=32)[:, :, 0:Tc])
                w2 = ROW(None, ROWf_.rearrange("p (q t) -> p q t", t=32)[:, :, 0:Tc])
                ur = XR(None, XR.t[:, :, 1 + t0:1 + t0 + Tc])
                ui = XI(None, XI.t[:, :, 1 + t0:1 + t0 + Tc])
                C.tt("pool", w1, ur, es, ALU.mult)
                C.tt("dve", w2, ui, es, ALU.mult)
                C.tt("dve", ur, ur, ec, ALU.mult)
                C.tt("pool", ui, ui, ec, ALU.mult)
                C.tt("dve", ur, ur, w2, ALU.add)
                C.tt("pool", ui, ui, w1, ALU.subtract)
                for pr in range(32):
                    rho = S5A(None, S5A.t[:, MAG, pr:pr + 1].broadcast_to([128, Tc]))
                    for buf in (XR, XI):
                        seg = buf(None, buf.t[:, pr, 1 + t0:1 + t0 + Tc])
                        scan(seg, rho, seg, buf(None, buf.t[:, pr, t0:t0 + 1]), ALU.mult, ALU.add)
                C.tt("pool", w1, ur, es, ALU.mult)
                C.tt("dve", w2, ui, es, ALU.mult)
                C.tt("dve", ur, ur, ec, ALU.mult)
                C.tt("pool", ui, ui, ec, ALU.mult)
                C.tt("dve", ur, ur, w2, ALU.subtract)
                C.tt("pool", ui, ui, w1, ALU.add)
        if is_s:
            for k, (nm, buf) in enumerate((("s_sre_o", XR), ("s_sim_o", XI))):
                for q in range(2):
                    C.copy("pool", TMP(None, TMP.t[:, 0:128].rearrange("p (q n o) -> p q n o", n=8, o=1)),
                           xv(buf, L, L + 1, 16 * q, 16 * q + 16))
                    ps = next_psb()
                    C.tr(ps(None, ps.t[:, 0:128]), TMP(None, TMP.t[:, 0:128]), ident.all())
                    C.copy("dve", S5ST.all(), ps(None, ps.t[:, 0:128]))
                    for pr in range(16):
                        g0 = 2 * (16 * q + pr)
                        C.dma(O[nm][s0:s0 + 8, g0:g0 + 2, :].rearrange("n m p -> n (m p)"),
                              S5ST(None, S5ST.t[pr * 8:(pr + 1) * 8, :]))
        else:
            for k, buf in enumerate((XR, XI)):
                C.copy("pool", S5C(None, S5C.t[:, k, :].unsqueeze(2).unsqueeze(3)), xv(buf, L, L + 1))
            if tile["last"]:
                for k, nm in enumerate(("p_sre", "p_sim")):
                    ps = next_psb()
                    C.tr(ps(None, ps.t[0:32, 0:128]), S5C(None, S5C.t[:, k, :]), ident.all())
                    C.copy("dve", S5ST(None, S5ST.t[0:32, :]), ps(None, ps.t[0:32, 0:128]))
                    C.dma(O[nm].rearrange("(q m) p -> q (m p)", m=2), S5ST(None, S5ST.t[0:32, :]))
        for gt in range(8):
            ps = next_psa()
            for ip in range(4):
                pair = 4 * gt + ip
                hf = ip // 2
                ov = ps(None, ps.t[64 * hf:64 * hf + 64, 0:T])
                xr_ = XR(None, XR.t[:, pair, 0:nseq * Wx].rearrange("p (n w) -> p n w", w=Wx)[:, :, 1:Wx])
                xi_ = XI(None, XI.t[:, pair, 0:nseq * Wx].rearrange("p (n w) -> p n w", w=Wx)[:, :, 1:Wx])
                cr, ci = CRE[ip % 2], CIM[ip % 2]
                C.mm(ov, cr(None, cr.t[:, gt, 64 * hf:64 * hf + 64]), xr_, start=(ip % 2 == 0), stop=False)
                C.mm(ov, ci(None, ci.t[:, gt, 64 * hf:64 * hf + 64]), xi_, start=False, stop=(ip % 2 == 1))
            gv = pj(gt)
            C.stt("dve", gv, gv, VEC0(None, VEC0.t[:, gt, 34:35]), ps(None, ps.t[:, 0:T]), ALU.mult, ALU.add)
            C.act(GELB(("t", gt), GELB.t[:, gt, 0:T]), gv, AF.Gelu)

        def evac_glu(j, pv):
            if j < 8:
                C.act(ACC(("t", j), ACC.t[:, j, 0:T]), pv, AF.Identity, bias=VECG(None, VECG.t[:, j, 0:1]))
            else:
                jj = j - 8
                tv = TMP(None, TMP.t[:, 0:T])
                C.act(tv, pv, AF.Sigmoid, bias=VECG(None, VECG.t[:, j, 0:1]))
                C.tt("dve", tv, tv, ACC(("t", jj), ACC.t[:, jj, 0:T]), ALU.mult)
                C.tt("dve", mixv(8 + jj, T), tv, pj(8 + jj), ALU.mult)

        stream_mm(I["s5_glu_w"], 8, [(i * 128, 128) for i in range(16)], lambda kt: GELB(("t", kt), GELB.t[:, kt, 0:T]), T, evac_glu)
        out_proj_ln(I["ev_w_out"], tile, VECG, 1, 2)

    do_rwkv = cfg.get("rwkv", True)
    MASKBIG = {"p": C.sb("mbig_p", [128, 128]), "s": C.sb("mbig_s", [64, 64])}
    SEGTRI = {"p": C.sb("stri_p", [128, 128]), "s": C.sb("stri_s", [64, 64])}
    R01 = {"p": C.sb("r01p", [128, 128]), "s": C.sb("r01s", [128, 64])}
    RNEG = {"p": C.sb("rnegp", [128, 128]), "s": C.sb("rnegs", [128, 64])}
    SEGSEL = C.sb("SEGSEL", [8, 64])
    SEGC = C.sb("SEGC", [64, 8])
    SEGROW = C.sb("SEGROW", [128, 8, 64])
    for k in ("p", "s"):
        C.dma(MASKBIG[k].all(), I["maskbig_" + k])
        C.dma(SEGTRI[k].all(), I["segtri_" + k])
        C.dma(R01[k].all(), I["r01_" + k])
        C.dma(RNEG[k].all(), I["rneg_" + k])
    C.dma(SEGSEL.all(), I["segsel"])
    C.dma(SEGC.all(), I["segc"])
    C.dma(SEGROW.all(), I["segrow"].rearrange("p (n t) -> p n t", n=8))

    VEC1 = C.sb("VEC1", [128, 8, 8])
    VMU = C.sb("VMU", [128, 25, 1])
    VECO = C.sb("VECO", [128, 16, 2])
    GB = C.sb("GB", [8, 1])
    load_rows_T(VEC1, [I[n_] for n_ in ("m_hn_g", "r_w0", "r_a0", "r_kk", "r_ka", "r_ln_g", "r_ln_b", "r_rk")], 1024)
    load_rows_T(VMU, [I["r_mu"][:, 0:2048]], 2048)
    load_rows_T(VMU, [I["r_mu"][:, 2048:3200]], 1152, ct0=16)
    load_rows_T(VECO, [I["od_ln_g"], I["od_ln_b"]], 2048)
    C.dma(GB(None, GB.t[0:4, :]), I["m_ig_b"].rearrange("o h -> h o"), slow=True)
    C.dma(GB(None, GB.t[4:8, :]), I["m_fg_b"].rearrange("o h -> h o"), slow=True)

    VTM = C.sb("VTM", [128, 4, 257])
    KW = C.sb("KW", [128, 1024])
    KWN = C.sb("KWN", [128, 256])
    GX = C.sb("GX", [8, 128])
    ROW = C.sb("ROW", [128, 8, 128])
    COL = C.sb("COL", [128, 64])
    DTB = C.sb("DTB", [128, 128])
    STB = C.sb("STB", [128, 128])
    P1S = C.sb("P1S", [128, 257])
    NUM = C.sb("NUM", [128, 257])
    HN = C.sb("HN", [128, 256])
    SM = C.sb("SM", [128, 16])
    CS = C.sb("CS", [128, 4, 2, 257])
    MCAR = C.sb("MCAR", [128, 4])
    CSS = [C.sb("CSS%d" % i, [128, 2, 257]) for i in range(2)]
    CSO = [C.sb("CSO%d" % i, [128, 2, 257]) for i in range(2)]
    MS = C.sb("MS", [8, 4])
    MSB = C.sb("MSB", [8, 128])
    MINIT = C.sb("MINIT", [128, 4, 8])
    DEC = C.sb("DEC", [128, 4, 8])
    MNEW = C.sb("MNEW", [128, 4, 8])
    QM = [C.sb("QM%d" % i, [128, 64]) for i in range(2)]
    C.memset("pool", VTM(None, VTM.t[:, :, 256:257]), 1.0)

    def stream_mm_tok(W, c0, ncols, T, evac):
        Wv = W.rearrange("(kt p) c -> p kt c", p=128)
        for g in range(ncols // WG):
            ps = next_psa()
            pv = ps(None, ps.t[0:T, 0:WG])
            for kq in range(0, 16, KQ):
                ws = WS[rr["ws"] % NWS]
                rr["ws"] += 1
                load_slab(ws, Wv[:, kq:kq + KQ, c0 + g * WG:c0 + (g + 1) * WG], KQ, WG)
                for k in range(KQ):
                    kt = kq + k
                    C.mm(pv, XTB(("t", kt), XTB.t[:, kt, 0:T]), ws(None, ws.t[:, k, 0:WG]), start=(kt == 0), stop=(kt == 15))
            evac(g, pv)

    def recip(e, out, in_):
        return C.op(e, lambda g: g.reciprocal(out=out.ap, in_=in_.ap), reads=[in_], writes=[out])

    def scan(out, d0, d1, init, op0, op1):
        rd = [d0, d1] + ([init] if isinstance(init, View) else [])
        ia = init.ap if isinstance(init, View) else init
        return C.op("dve", lambda g: g.tensor_tensor_scan(out=out.ap, data0=d0.ap, data1=d1.ap, initial=ia, op0=op0, op1=op1),
                    reads=rd, writes=[out])

    SR = C.sb("SR", [128, 8, 64])
    W2A2 = C.sb("W2A2", [128, 1024])
    BLK = C.sb("BLK", [128, 128])
    OMKA = C.sb("OMKA", [128, 8])
    SHC = C.sb("SHC", [128, 25])
    SUMB = C.sb("SUMB", [128, 8])
    C.dma(W2A2(None, W2A2.t[0:64, :]), I["r_w2"])
    C.dma(W2A2(None, W2A2.t[64:128, :]), I["r_a2"])
    C.dma(BLK.all(), I["blk"])
    C.ts("dve", OMKA.all(), VEC1(None, VEC1.t[:, :, 4]), -1.0, 1.0, ALU.mult, ALU.add)
    XIf = XI.t[:, :, :].rearrange("p a b -> p (a b)")
    T1 = XI("T1", XIf[:, 0:512].rearrange("p (j k) -> p j k", k=64))
    T2 = XI("T2", XIf[:, 512:1024].rearrange("p (j k) -> p j k", k=64))
    FSv = XIf[:, 1024:1536].rearrange("p (i t) -> p i t", t=128)
    SRS = [XI(("SRS", i), XIf[:, 1536 + 512 * i:2048 + 512 * i].rearrange("p (j k) -> p j k", k=64)) for i in range(2)]
    TWv = XIf[:, 2560:2688]
    ALLPS = PSA + PSB

    def next_ps8():
        rr["ps8"] = (rr.get("ps8", 0) + 1) % 8
        return ALLPS[rr["ps8"]]

    def rwkv(tile):
        T, nseq, L = tile["T"], tile["nseq"], tile["L"]
        is_s = tile["kind"] == "s"
        s0 = tile["s0"]
        Wx = 1 + L
        xrhs = lambda kt: XTB(("t", kt), XTB.t[:, kt, 0:T])
        pj = lambda j: PJ(("t", j), PJ.t[:, j, 0:T])
        pj3 = lambda j: PJ(("t", j), PJ.t[:, j, 0:T].rearrange("p (n l) -> p n l", l=L))

        def ppv(j0, j1, a, b):
            v = XR.t[:, j0:j1, 0:nseq * Wx].rearrange("p j (n w) -> p j n w", w=Wx)[:, :, :, a:b]
            return XR(("pp", j0) if j1 == j0 + 1 else None, v)

        if is_s:
            for (c0, ncol, ct0) in ((0, 2048, 0), (2048, 1152, 16)):
                C.dma(XS(None, XS.t[0:8, 0:ncol]), I["s_rsh"][s0:s0 + 8, c0:c0 + ncol])
                for ct in range(ncol // 128):
                    ps = next_psb()
                    C.tr(ps(None, ps.t[:, 0:8]), XS(None, XS.t[0:8, ct * 128:(ct + 1) * 128]), ident(None, ident.t[0:8, 0:8]))
                    C.copy("dve", ppv(ct0 + ct, ct0 + ct + 1, 0, 1), ps(None, ps.t[:, 0:8].rearrange("p (j n o) -> p j n o", j=1, o=1)))
        else:
            if tile["first"]:
                C.memset("pool", ppv(0, 25, 0, 1), 0.0)
            else:
                C.copy("pool", ppv(0, 25, 0, 1), SHC(None, SHC.t[:, :].unsqueeze(2).unsqueeze(3)))

        def evacR(j, pv):
            if j < 25:
                C.copy(ev_eng(), ppv(j, j + 1, 1, Wx), pv.buf(None, pv.ap.rearrange("p (j n l) -> p j n l", j=1, l=L)))
            else:
                C.act(pj(j), pv, AF.Silu)

        stream_mm(I["od_w_in"], 16, [(5128 + i * 128, 128) for i in range(33)], xrhs, T, evacR)

        if is_s:
            for (c0, ncol, ct0) in ((0, 2048, 0), (2048, 1152, 16)):
                for ct in range(ncol // 128):
                    ps = next_psb()
                    C.copy("pool", TMP(None, TMP.t[:, 0:8]), XR(("pp", ct0 + ct), XR.t[:, ct0 + ct, 0:nseq * Wx].rearrange("p (n w) -> p n w", w=Wx)[:, :, L]))
                    C.tr(ps(None, ps.t[0:8, 0:128]), TMP(None, TMP.t[:, 0:8]), ident.all())
                    C.copy("dve", XS(None, XS.t[0:8, ct * 128:(ct + 1) * 128]), ps(None, ps.t[0:8, 0:128]))
                C.dma(O["s_rsh_o"][s0:s0 + 8, c0:c0 + ncol], XS(None, XS.t[0:8, 0:ncol]))
        else:
            C.copy("pool", SHC(None, SHC.t[:, :].unsqueeze(2).unsqueeze(3)), ppv(0, 25, L, L + 1))
            if tile["last"]:
                for (c0, ncol, ct0) in ((0, 2048, 0), (2048, 1152, 16)):
                    for ct in range(ncol // 128):
                        ps = next_psb()
                        C.tr(ps(None, ps.t[0:1, 0:128]), SHC(None, SHC.t[:, ct0 + ct:ct0 + ct + 1]), ident.all())
                        C.copy("dve", XS(None, XS.t[0:1, ct * 128:(ct + 1) * 128]), ps(None, ps.t[0:1, 0:128]))
                    C.dma(O["p_rsh"][:, c0:c0 + ncol], XS(None, XS.t[0:1, 0:ncol]))
        for j in range(25):
            C.tt("pool", pj3(j), XR(("pp", j), XR.t[:, j, 0:nseq * Wx].rearrange("p (n w) -> p n w", w=Wx)[:, :, 0:L]),
                 XR(("pp", j), XR.t[:, j, 0:nseq * Wx].rearrange("p (n w) -> p n w", w=Wx)[:, :, 1:Wx]), ALU.subtract)
            C.stt("dve", pj3(j), pj3(j), VMU(None, VMU.t[:, j, 0:1]),
                  XR(("pp", j), XR.t[:, j, 0:nseq * Wx].rearrange("p (n w) -> p n w", w=Wx)[:, :, 1:Wx]), ALU.mult, ALU.add)

        VTMf = VTM.t[:, :, :].rearrange("p a b -> p (a b)")
        ROWf = ROW.t[:, :, :].rearrange("p a b -> p (a b)")
        KKt = lambda a, b: KW(None, KW.t[0:T, a:b])
        Wt = lambda a, b: VTM(None, VTMf[0:T, a:b])
        KKAt = lambda a, b: ROW(None, ROWf[0:T, a:b])
        KPt = lambda a, b: XS(None, XS.t[0:T, a:b])
        Rt = lambda a, b: XS(None, XS.t[0:T, 1024 + a:1024 + b])
        fs = lambda i: XI(("FS", i), FSv[:, i, 0:T])
        tw = XI("TW", TWv[0:64, 0:T])
        C.act(tw, PJ(("t", 24), PJ.t[0:64, 24, 0:T]), AF.Tanh)

        def to_tok(dst, src):
            ps = next_psb()
            C.tr(ps(None, ps.t[0:T, 0:128]), src, ident.all())
            C.copy(ev_eng(), dst, ps(None, ps.t[0:T, 0:128]))

        NE05 = -float(np.exp(-0.5))
        for ct in range(8):
            r_, k_, v_ = pj(ct), pj(8 + ct), pj(16 + ct)
            cs_ = slice(ct * 128, (ct + 1) * 128)
            ps = next_psa()
            C.mm(ps(None, ps.t[:, 0:T]), W2A2(None, W2A2.t[0:64, cs_]), tw)
            C.act(fs(0), ps(None, ps.t[:, 0:T]), AF.Sigmoid, bias=VEC1(None, VEC1.t[:, ct, 1:2]))
            C.act(fs(0), fs(0), AF.Exp, scale=NE05)
            to_tok(Wt(ct * 128, (ct + 1) * 128), fs(0))
            ps = next_psa()
            C.mm(ps(None, ps.t[:, 0:T]), W2A2(None, W2A2.t[64:128, cs_]), PJ(("t", 24), PJ.t[64:128, 24, 0:T]))
            C.act(fs(1), ps(None, ps.t[:, 0:T]), AF.Sigmoid, bias=VEC1(None, VEC1.t[:, ct, 2:3]))
            C.ts("dve", fs(2), k_, VEC1(None, VEC1.t[:, ct, 3:4]), None, ALU.mult)
            C.tt("pool", fs(3), fs(2), fs(2), ALU.mult)
            ps = next_psa()
            C.mm(ps(None, ps.t[:, 0:T]), BLK.all(), fs(3))
            C.act(fs(3), ps(None, ps.t[:, 0:T]), AF.Sqrt)
            C.ts("dve", fs(3), fs(3), 1e-12, None, ALU.max)
            recip("dve", fs(3), fs(3))
            C.tt("dve", fs(2), fs(2), fs(3), ALU.mult)
            to_tok(KKt(ct * 128, (ct + 1) * 128), fs(2))
            C.tt("dve", fs(3), fs(2), fs(1), ALU.mult)
            to_tok(KKAt(ct * 128, (ct + 1) * 128), fs(3))
            C.ts("dve", fs(1), fs(1), VEC1(None, VEC1.t[:, ct, 4:5]), OMKA(None, OMKA.t[:, ct:ct + 1]), ALU.mult, ALU.add)
            C.tt("dve", fs(1), fs(1), k_, ALU.mult)
            to_tok(KPt(ct * 128, (ct + 1) * 128), fs(1))
            to_tok(Rt(ct * 128, (ct + 1) * 128), r_)
            C.tt("dve", fs(3), r_, fs(1), ALU.mult)
            C.ts("dve", fs(3), fs(3), VEC1(None, VEC1.t[:, ct, 7:8]), None, ALU.mult)
            ps = next_psa()
            C.mm(ps(None, ps.t[:, 0:T]), BLK.all(), fs(3))
            C.tt("dve", ACC(("t", ct), ACC.t[:, ct, 0:T]), ps(None, ps.t[:, 0:T]), v_, ALU.mult)

        Yv = MIX.t[:, 8 * TT:16 * TT].rearrange("p (j t) -> p j t", j=8)
        srcs = (("kk", KW.t[0:T, :]), ("w", VTMf[0:T, 0:1024]), ("kka", ROWf[0:T, 0:1024]), ("kp", XS.t[0:T, 0:1024]), ("r", XS.t[0:T, 1024:2048]))
        bufs = {"kk": KW, "w": VTM, "kka": ROW, "kp": XS, "r": XS}
        for n in range(nseq):
            if is_s:
                sr = SRS[n % 2]
                C.dma(sr, I["s_rs"][s0 + n].rearrange("(j hp) v k -> (hp v) j k", hp=2))
            else:
                sr = SR.all()
                if tile["first"]:
                    C.memset("pool", sr, 0.0)
            for l in range(L):
                t = n * L + l
                oh = ident(None, ident.t[0:T, t:t + 1].broadcast_to([T, 64]))
                bc = {}
                for nm, ap in srcs:
                    ps = next_ps8()
                    xv = ap.rearrange("p (j hp k) -> p hp j k", hp=2, k=64)
                    C.mm(ps(None, ps.t[0:64, 0:512]), oh, bufs[nm](None, xv[:, 0]))
                    C.mm(ps(None, ps.t[64:128, 0:512]), oh, bufs[nm](None, xv[:, 1]))
                    bc[nm] = ps(None, ps.t[:, 0:512].rearrange("p (j k) -> p j k", k=64))
                C.tt("dve", T1, sr, bc["kk"], ALU.mult)
                C.op("dve", lambda g: g.reduce_sum(out=SUMB.t[:, :], in_=T1.ap, axis=AX.X), reads=[T1], writes=[SUMB.all()])
                C.tt("dve", sr, sr, bc["w"], ALU.mult)
                C.tt("dve", T2, bc["kka"], SUMB(None, SUMB.t[:, :].unsqueeze(2).broadcast_to([128, 8, 64])), ALU.mult)
                C.tt("dve", sr, sr, T2, ALU.subtract)
                C.tt("dve", T1, bc["kp"], PJ(None, PJ.t[:, 16:24, t].unsqueeze(2).broadcast_to([128, 8, 64])), ALU.mult)
                C.tt("dve", sr, sr, T1, ALU.add)
                C.tt("dve", T2, sr, bc["r"], ALU.mult)
                yv = MIX(None, Yv[:, :, t])
                C.op("dve", lambda g, yv=yv: g.reduce_sum(out=yv.ap, in_=T2.ap, axis=AX.X), reads=[T2], writes=[yv])
            if is_s:
                C.dma(O["s_rs_o"][s0 + n].rearrange("(j hp) v k -> (hp v) j k", hp=2), sr)
        if (not is_s) and tile["last"]:
            C.dma(O["p_rs"].rearrange("(j hp) v k -> (hp v) j k", hp=2), SR.all())

        for j in range(8):
            y = mixv(8 + j, T)
            ps = next_psa()
            C.mm(ps(None, ps.t[:, 0:T]), BLK.all(), y)
            C.tt("pool", fs(0), y, y, ALU.mult)
            ps2 = next_psa()
            C.mm(ps2(None, ps2.t[:, 0:T]), BLK.all(), fs(0))
            C.ts("dve", fs(1), ps(None, ps.t[:, 0:T]), 1.0 / 64, None, ALU.mult)
            C.ts("dve", fs(2), ps2(None, ps2.t[:, 0:T]), 1.0 / 64, None, ALU.mult)
            C.tt("dve", fs(3), fs(1), fs(1), ALU.mult)
            C.tt("dve", fs(2), fs(2), fs(3), ALU.subtract)
            C.ts("dve", fs(2), fs(2), 64e-5, None, ALU.add)
            C.act(fs(2), fs(2), AF.Sqrt)
            recip("dve", fs(2), fs(2))
            C.tt("dve", y, y, fs(1), ALU.subtract)
            C.tt("dve", y, y, fs(2), ALU.mult)
            C.act(y, y, AF.Identity, bias=VEC1(None, VEC1.t[:, j, 6:7]), scale=VEC1(None, VEC1.t[:, j, 5:6]))
            C.tt("dve", y, y, ACC(("t", j), ACC.t[:, j, 0:T]), ALU.add)
            C.tt("dve", y, y, pj(25 + j), ALU.mult)

    SSTRI = {"p": C.sb("sstri_p", [128, 128]), "s": C.sb("sstri_s", [64, 64])}
    SSTRIT = {"p": C.sb("sstriT_p", [128, 128]), "s": C.sb("sstriT_s", [64, 64])}
    for k_ in ("p", "s"):
        C.dma(SSTRI[k_].all(), I["sstri_" + k_])
        C.dma(SSTRIT[k_].all(), I["sstriT_" + k_])
    WLB = C.sb("WLB", [128, 8, 8])
    NB16 = C.sb("NB16", [128, 10, 128], BF16)
    HBf = HB.t[:, :, :].rearrange("p a b -> p (a b)")
    KKv = KW.t[:, :].rearrange("p (j t) -> p j t", t=128)
    BTv = ROW.t
    XRf = XR.t[:, :, :].rearrange("p a b -> p (a b)")
    S0Tv = XRf[:, 0:4096].rearrange("p (n j v) -> p n j v", n=8, j=8)

    def fence(buf, ap):
        C.op("pool", lambda g: g.memset(ap, 0.0), writes=[buf.all()])

    def rwkv2(tile):
        T, nseq, L = tile["T"], tile["nseq"], tile["L"]
        is_s = tile["kind"] == "s"
        kd = tile["kind"]
        s0 = tile["s0"]
        Wx = 1 + L
        xrhs = lambda kt: XTB(("t", kt), XTB.t[:, kt, 0:T])
        pj = lambda j: PJ(("t", j), PJ.t[:, j, 0:T])
        pj3 = lambda j: PJ(("t", j), PJ.t[:, j, 0:T].rearrange("p (n l) -> p n l", l=L))

        def ppv(j0, j1, a, b):
            v = XR.t[:, j0:j1, 0:nseq * Wx].rearrange("p j (n w) -> p j n w", w=Wx)[:, :, :, a:b]
            return XR(("pp", j0) if j1 == j0 + 1 else None, v)

        if is_s:
            for (c0, ncol, ct0) in ((0, 2048, 0), (2048, 1152, 16)):
                C.dma(XS(None, XS.t[0:8, 0:ncol]), I["s_rsh"][s0:s0 + 8, c0:c0 + ncol])
                for ct in range(ncol // 128):
                    ps = next_psb()
                    C.tr(ps(None, ps.t[:, 0:8]), XS(None, XS.t[0:8, ct * 128:(ct + 1) * 128]), ident(None, ident.t[0:8, 0:8]))
                    C.copy("dve", ppv(ct0 + ct, ct0 + ct + 1, 0, 1), ps(None, ps.t[:, 0:8].rearrange("p (j n o) -> p j n o", j=1, o=1)))
        else:
            if tile["first"]:
                C.memset("pool", ppv(0, 25, 0, 1), 0.0)
            else:
                C.copy("pool", ppv(0, 25, 0, 1), SHC(None, SHC.t[:, :].unsqueeze(2).unsqueeze(3)))

        def evacR(j, pv):
            if j < 25:
                C.copy(ev_eng(), ppv(j, j + 1, 1, Wx), pv.buf(None, pv.ap.rearrange("p (j n l) -> p j n l", j=1, l=L)))
            else:
                C.act(pj(j), pv, AF.Silu)

        for _ in stream_mm_g(I["od_w_in"], 16, [(5128 + i * 128, 128) for i in range(25)], xrhs, T, evacR):
            yield "g"
        yield "PD_DONE"
        stream_mm(I["od_w_in"], 16, [(5128 + (25 + i) * 128, 128) for i in range(8)], xrhs, T, lambda j, pv: evacR(25 + j, pv))

        if is_s:
            for (c0, ncol, ct0) in ((0, 2048, 0), (2048, 1152, 16)):
                for ct in range(ncol // 128):
                    ps = next_psb()
                    C.copy("pool", TMP(None, TMP.t[:, 0:8]), XR(("pp", ct0 + ct), XR.t[:, ct0 + ct, 0:nseq * Wx].rearrange("p (n w) -> p n w", w=Wx)[:, :, L]))
                    C.tr(ps(None, ps.t[0:8, 0:128]), TMP(None, TMP.t[:, 0:8]), ident.all())
                    C.copy("dve", XS(None, XS.t[0:8, ct * 128:(ct + 1) * 128]), ps(None, ps.t[0:8, 0:128]))
                C.dma(O["s_rsh_o"][s0:s0 + 8, c0:c0 + ncol], XS(None, XS.t[0:8, 0:ncol]))
        else:
            C.copy("pool", SHC(None, SHC.t[:, :].unsqueeze(2).unsqueeze(3)), ppv(0, 25, L, L + 1))
            if tile["last"]:
                for (c0, ncol, ct0) in ((0, 2048, 0), (2048, 1152, 16)):
                    for ct in range(ncol // 128):
                        ps = next_psb()
                        C.tr(ps(None, ps.t[0:1, 0:128]), SHC(None, SHC.t[:, ct0 + ct:ct0 + ct + 1]), ident.all())
                        C.copy("dve", XS(None, XS.t[0:1, ct * 128:(ct + 1) * 128]), ps(None, ps.t[0:1, 0:128]))
                    C.dma(O["p_rsh"][:, c0:c0 + ncol], XS(None, XS.t[0:1, 0:ncol]))
        for j in range(25):
            C.tt("pool", pj3(j), XR(("pp", j), XR.t[:, j, 0:nseq * Wx].rearrange("p (n w) -> p n w", w=Wx)[:, :, 0:L]),
                 XR(("pp", j), XR.t[:, j, 0:nseq * Wx].rearrange("p (n w) -> p n w", w=Wx)[:, :, 1:Wx]), ALU.subtract)
            C.stt("dve", pj3(j), pj3(j), VMU(None, VMU.t[:, j, 0:1]),
                  XR(("pp", j), XR.t[:, j, 0:nseq * Wx].rearrange("p (n w) -> p n w", w=Wx)[:, :, 1:Wx]), ALU.mult, ALU.add)

        s0t = lambda n, j, rs=slice(0, 128): XR(("st", n), S0Tv[rs, n, j, :])
        if is_s:
            fence(XR, XR.t[0:1, 0, 0:1])
            for n2 in range(0, 8, 2):
                stg = XS.t[:, :].rearrange("p (n j d k) -> p n j d k", n=2, j=8, d=2)
                for nn in range(2):
                    for d in range(2):
                        C.dma(XS(None, stg[:, nn, :, d, :]), I["s_rs"][s0 + n2 + nn].rearrange("(j hp) v k -> (hp v) j k", hp=2))
                for nn in range(2):
                    n = n2 + nn
                    for j in range(8):
                        ps = next_psb()
                        C.tr(ps(None, ps.t[:, 0:128]), XS(None, stg[:, nn, j, :, :]), ident.all())
                        C.copy("dve", s0t(n, j, slice(0, 64)), ps(None, ps.t[0:64, 0:64]))
                        C.copy("act", s0t(n, j, slice(64, 128)), ps(None, ps.t[64:128, 64:128]))
        else:
            if tile["first"]:
                C.memset("pool", SR.all(), 0.0)

        VTMf = VTM.t[:, :, :].rearrange("p a b -> p (a b)")
        Vtm = lambda hc: XS(None, XS.t[0:T, hc])
        Btm = lambda hc: XS(None, XS.t[0:T, 1024 + hc.start:1024 + hc.stop])
        Ktm = lambda hc: VTM(None, VTMf[0:T, hc])
        fs = lambda i: XI(("FS", i), FSv[:, i, 0:T])
        tw = XI("TW", TWv[0:64, 0:T])
        C.act(tw, PJ(("t", 24), PJ.t[0:64, 24, 0:T]), AF.Tanh)

        def to_tok(dst, src):
            ps = next_psb()
            C.tr(ps(None, ps.t[0:T, 0:128]), src, ident.all())
            C.copy(ev_eng(), dst, ps(None, ps.t[0:T, 0:128]))

        NE05 = -float(np.exp(-0.5))
        kkc = lambda ct, rs=slice(0, 128): KW(("c", ct), KKv[rs, ct, 0:T])
        btc = lambda ct, rs=slice(0, 128): ROW(("c", ct), BTv[rs, ct, 0:T])
        for ct in range(8):
            r_, k_, v_ = pj(ct), pj(8 + ct), pj(16 + ct)
            cs_ = slice(ct * 128, (ct + 1) * 128)
            ps = next_psa()
            C.mm(ps(None, ps.t[:, 0:T]), W2A2(None, W2A2.t[0:64, cs_]), tw)
            C.act(fs(0), ps(None, ps.t[:, 0:T]), AF.Sigmoid, bias=VEC1(None, VEC1.t[:, ct, 1:2]))
            C.ts("dve", fs(0), fs(0), NE05, None, ALU.mult)
            scan(fs(1), R01[kd](None, R01[kd].t[:, 0:T]), fs(0), 0.0, ALU.mult, ALU.add)
            ps = next_psa()
            C.mm(ps(None, ps.t[:, 0:T]), W2A2(None, W2A2.t[64:128, cs_]), PJ(("t", 24), PJ.t[64:128, 24, 0:T]))
            C.act(fs(2), ps(None, ps.t[:, 0:T]), AF.Sigmoid, bias=VEC1(None, VEC1.t[:, ct, 2:3]))
            C.ts("dve", kkc(ct), k_, VEC1(None, VEC1.t[:, ct, 3:4]), None, ALU.mult)
            C.tt("pool", fs(3), kkc(ct), kkc(ct), ALU.mult)
            ps = next_psa()
            C.mm(ps(None, ps.t[:, 0:T]), BLK.all(), fs(3))
            C.act(fs(3), ps(None, ps.t[:, 0:T]), AF.Sqrt)
            C.ts("dve", fs(3), fs(3), 1e-12, None, ALU.max)
            recip("dve", fs(3), fs(3))
            C.tt("dve", kkc(ct), kkc(ct), fs(3), ALU.mult)
            C.tt("dve", btc(ct), kkc(ct), fs(2), ALU.mult)
            C.ts("dve", fs(2), fs(2), VEC1(None, VEC1.t[:, ct, 4:5]), OMKA(None, OMKA.t[:, ct:ct + 1]), ALU.mult, ALU.add)
            C.tt("dve", fs(2), fs(2), k_, ALU.mult)
            C.tt("pool", fs(3), r_, fs(2), ALU.mult)
            C.ts("dve", fs(3), fs(3), VEC1(None, VEC1.t[:, ct, 7:8]), None, ALU.mult)
            ps = next_psa()
            C.mm(ps(None, ps.t[:, 0:T]), BLK.all(), fs(3))
            C.tt("dve", ACC(("t", ct), ACC.t[:, ct, 0:T]), ps(None, ps.t[:, 0:T]), v_, ALU.mult)
            C.act(fs(3), fs(1), AF.Exp)
            C.tt("dve", r_, r_, fs(3), ALU.mult)
            C.copy("pool", WLB(None, WLB.t[:, ct, 0:nseq]),
                   XI(("FS", 3), FSv[:, 3, 0:T].rearrange("p (n l) -> p n l", l=L)[:, :, L - 1]))
            C.tt("dve", fs(3), fs(1), fs(0), ALU.subtract)
            C.act(fs(3), fs(3), AF.Exp)
            C.tt("dve", kkc(ct), kkc(ct), fs(3), ALU.mult)
            C.act(fs(3), fs(1), AF.Exp, scale=-1.0)
            C.tt("dve", btc(ct), btc(ct), fs(3), ALU.mult)
            C.tt("dve", k_, fs(2), fs(3), ALU.mult)
            to_tok(Vtm(cs_), v_)
            to_tok(Btm(cs_), btc(ct))
            to_tok(Ktm(cs_), k_)

        fence(HB, HB.t[0:1, 0, 0:1])
        mat = lambda i: HB(("m", i), HBf[0:T, i * 128:i * 128 + T])
        half = lambda i, a: HB(("m", i), HBf[0:T, i * 128 + 64 * a:i * 128 + 64 * a + 64])
        nsq = max(0, int(np.ceil(np.log2(L))) - 1)
        idT = ident(None, ident.t[0:T, 0:T])
        mS = SSTRI[kd](None, SSTRI[kd].t[0:T, 0:T])
        mST = SSTRIT[kd](None, SSTRIT[kd].t[0:T, 0:T])
        mI = SEGTRI[kd](None, SEGTRI[kd].t[0:T, 0:T])
        if is_s:
            KKM, RM = T1, T2
        for j in range(8):
            nb = lambda i: NB16(("n", i), NB16.t[0:T, i, 0:T])
            Q = [[nb(5 * hp + 0), nb(5 * hp + 1)] for hp in range(2)]
            QT = [[nb(5 * hp + 2), nb(5 * hp + 3)] for hp in range(2)]
            P16 = [nb(5 * hp + 4) for hp in range(2)]
            Pm = [mat(8 * hp + 4) for hp in range(2)]
            BR = [mat(8 * hp + 5) for hp in range(2)]
            AK = [mat(8 * hp + 6) for hp in range(2)]
            KR = [mat(8 * hp + 7) for hp in range(2)]
            RHS = [half(16, hp) for hp in range(2)]
            SAT = [half(17, hp) for hp in range(2)]
            rsl = [slice(0, 64), slice(64, 128)]
            if is_s:
                C.tt("pool", KKM, KW(("c", j), KKv[:, j, 0:T].unsqueeze(1).broadcast_to([128, 8, T])), SEGROW(None, SEGROW.t[:, :, 0:T]), ALU.mult)
                C.tt("pool", RM, PJ(("t", j), PJ.t[:, j, 0:T].unsqueeze(1).broadcast_to([128, 8, T])), SEGROW(None, SEGROW.t[:, :, 0:T]), ALU.mult)
            for hp in range(2):
                rs = rsl[hp]
                rq = PJ(("t", j), PJ.t[rs, j, 0:T])
                kq = PJ(("t", 8 + j), PJ.t[rs, 8 + j, 0:T])
                ps = next_ps8()
                C.mm(ps(None, ps.t[0:T, 0:T]), btc(j, rs), kkc(j, rs))
                C.mm(ps(None, ps.t[0:T, T:2 * T]), btc(j, rs), rq)
                C.tt("dve", Q[hp][0], ps(None, ps.t[0:T, 0:T]), mS, ALU.mult)
                C.tt("dve", BR[hp], ps(None, ps.t[0:T, T:2 * T]), mI, ALU.mult)
                ps = next_ps8()
                C.mm(ps(None, ps.t[0:T, 0:T]), kq, kkc(j, rs))
                C.mm(ps(None, ps.t[0:T, T:2 * T]), kq, rq)
                C.tt("dve", AK[hp], ps(None, ps.t[0:T, 0:T]), mS, ALU.mult)
                C.tt("dve", KR[hp], ps(None, ps.t[0:T, T:2 * T]), mI, ALU.mult)
                ps = next_ps8()
                C.mm(ps(None, ps.t[0:T, 0:T]), kkc(j, rs), btc(j, rs))
                C.tt("dve", QT[hp][0], ps(None, ps.t[0:T, 0:T]), mST, ALU.mult)
                C.stt("dve", Pm[hp], Q[hp][0], -1.0, idT, ALU.mult, ALU.add)
                C.copy("act", P16[hp], Pm[hp])
            cur = 0
            for it in range(nsq):
                nxt = 1 - cur
                last_it = (it == nsq - 1)
                for hp in range(2):
                    if not last_it:
                        ps = next_ps8()
                        C.mm(ps(None, ps.t[0:T, 0:T]), QT[hp][cur], Q[hp][cur])
                        C.copy("act", Q[hp][nxt], ps(None, ps.t[0:T, 0:T]))
                    ps = next_ps8()
                    C.mm(ps(None, ps.t[0:T, 0:T]), Q[hp][cur], QT[hp][cur])
                    C.copy("dve", QT[hp][nxt], ps(None, ps.t[0:T, 0:T]))
                for hp in range(2):
                    ps = next_ps8()
                    C.mm(ps(None, ps.t[0:T, 0:T]), QT[hp][nxt], P16[hp])
                    C.tt("dve", Pm[hp], Pm[hp], ps(None, ps.t[0:T, 0:T]), ALU.add)
                    if not last_it:
                        C.copy("act", P16[hp], Pm[hp])
                cur = nxt
            for hp in range(2):
                rs = rsl[hp]
                hc = slice((2 * j + hp) * 64, (2 * j + hp) * 64 + 64)
                ps = next_ps8()
                if is_s:
                    for n in range(nseq):
                        C.mm(ps(None, ps.t[0:T, 0:64]), XI("T1", KKM.ap[rs, n, :]), s0t(n, j, rs), start=(n == 0), stop=False)
                else:
                    C.mm(ps(None, ps.t[0:T, 0:64]), kkc(j, rs), SR(None, SR.t[rs, j, :]), start=True, stop=False)
                C.mm(ps(None, ps.t[0:T, 0:64]), AK[hp], Vtm(hc), start=False, stop=True)
                C.act(RHS[hp], ps(None, ps.t[0:T, 0:64]), AF.Identity, scale=-1.0)
                ps = next_ps8()
                C.mm(ps(None, ps.t[0:T, 0:64]), Pm[hp], RHS[hp])
                C.copy("dve", SAT[hp], ps(None, ps.t[0:T, 0:64]))
            psY = next_ps8()
            for hp in range(2):
                rs = rsl[hp]
                hc = slice((2 * j + hp) * 64, (2 * j + hp) * 64 + 64)
                ov = psY(None, psY.t[rs, 0:T])
                if is_s:
                    for n in range(nseq):
                        C.mm(ov, s0t(n, j, rs), XI("T2", RM.ap[rs, n, :]), start=(n == 0), stop=False)
                else:
                    C.mm(ov, SR(None, SR.t[rs, j, :]), PJ(("t", j), PJ.t[rs, j, 0:T]), start=True, stop=False)
                C.mm(ov, SAT[hp], BR[hp], start=False, stop=False)
                C.mm(ov, Vtm(hc), KR[hp], start=False, stop=True)
            y = TMP(None, TMP.t[:, 0:T])
            C.copy("act", y, psY(None, psY.t[:, 0:T]))
            ps = next_psa()
            C.mm(ps(None, ps.t[:, 0:T]), BLK.all(), y)
            C.tt("pool", fs(0), y, y, ALU.mult)
            ps2 = next_psa()
            C.mm(ps2(None, ps2.t[:, 0:T]), BLK.all(), fs(0))
            C.ts("dve", fs(1), ps(None, ps.t[:, 0:T]), 1.0 / 64, None, ALU.mult)
            C.ts("dve", fs(2), ps2(None, ps2.t[:, 0:T]), 1.0 / 64, None, ALU.mult)
            C.tt("dve", fs(3), fs(1), fs(1), ALU.mult)
            C.tt("dve", fs(2), fs(2), fs(3), ALU.subtract)
            C.ts("dve", fs(2), fs(2), 64e-5, None, ALU.add)
            C.act(fs(2), fs(2), AF.Sqrt)
            recip("dve", fs(2), fs(2))
            C.tt("dve", y, y, fs(1), ALU.subtract)
            C.tt("dve", y, y, fs(2), ALU.mult)
            C.act(y, y, AF.Identity, bias=VEC1(None, VEC1.t[:, j, 6:7]), scale=VEC1(None, VEC1.t[:, j, 5:6]))
            C.tt("dve", y, y, ACC(("t", j), ACC.t[:, j, 0:T]), ALU.add)
            C.tt("dve", mixv(8 + j, T), y, pj(25 + j), ALU.mult)
            tmpS = XI(("FS", 0), FSv[:, 0, 0:64])
            if is_s:
                SAM = [CSS[hp](None, CSS[hp].t[0:T, :, :].rearrange("p a b -> p (a b)")[:, 0:512].rearrange("p (n v) -> p n v", n=8)) for hp in range(2)]
                VM = [CSO[hp](None, CSO[hp].t[0:T, :, :].rearrange("p a b -> p (a b)")[:, 0:512].rearrange("p (n v) -> p n v", n=8)) for hp in range(2)]
                segc_b = SEGC(None, SEGC.t[0:T, :].unsqueeze(2).broadcast_to([T, 8, 64]))
                for hp in range(2):
                    hc = slice((2 * j + hp) * 64, (2 * j + hp) * 64 + 64)
                    C.tt("pool", SAM[hp], HB(("m", 17), HBf[0:T, 17 * 128 + 64 * hp:17 * 128 + 64 * hp + 64].unsqueeze(1).broadcast_to([T, 8, 64])), segc_b, ALU.mult)
                    C.tt("pool", VM[hp], XS(None, XS.t[0:T, hc].unsqueeze(1).broadcast_to([T, 8, 64])), segc_b, ALU.mult)
                for n in range(nseq):
                    psS = next_ps8()
                    for hp in range(2):
                        rs = rsl[hp]
                        hc = slice((2 * j + hp) * 64, (2 * j + hp) * 64 + 64)
                        C.mm(psS(None, psS.t[rs, 0:64]), Btm(hc), CSS[hp](None, SAM[hp].ap[:, n, :]), start=True, stop=False)
                        C.mm(psS(None, psS.t[rs, 0:64]), Ktm(hc), CSO[hp](None, VM[hp].ap[:, n, :]), start=False, stop=True)
                    wl = WLB(None, WLB.t[:, j, n:n + 1])
                    C.ts("dve", tmpS, s0t(n, j), wl, None, ALU.mult)
                    C.stt("dve", s0t(n, j), psS(None, psS.t[:, 0:64]), wl, tmpS, ALU.mult, ALU.add)
            else:
                psS = next_ps8()
                for hp in range(2):
                    rs = rsl[hp]
                    hc = slice((2 * j + hp) * 64, (2 * j + hp) * 64 + 64)
                    C.mm(psS(None, psS.t[rs, 0:64]), Btm(hc), SAT[hp], start=True, stop=False)
                    C.mm(psS(None, psS.t[rs, 0:64]), Ktm(hc), Vtm(hc), start=False, stop=True)
                wl = WLB(None, WLB.t[:, j, 0:1])
                srj = SR(None, SR.t[:, j, :])
                C.ts("dve", tmpS, srj, wl, None, ALU.mult)
                C.stt("dve", srj, psS(None, psS.t[:, 0:64]), wl, tmpS, ALU.mult, ALU.add)

        def state_out(src_fn, dst):
            for j in range(8):
                ps = next_psb()
                C.tr(ps(None, ps.t[0:64, 0:128]), src_fn(j), ident.all())
                C.copy(ev_eng(), XS(None, XS.t[0:64, j * 128:(j + 1) * 128]), ps(None, ps.t[0:64, 0:128]))
            C.dma(dst.rearrange("(j hp) v k -> v j hp k", hp=2), XS(None, XS.t[0:64, 0:1024].rearrange("p (j hp k) -> p j hp k", j=8, hp=2)))

        if is_s:
            for n in range(nseq):
                state_out(lambda j, n=n: s0t(n, j), O["s_rs_o"][s0 + n])
        elif tile["last"]:
            state_out(lambda j: SR(None, SR.t[:, j, :]), O["p_rs"])


    def layer1(tile):
        T, nseq, L = tile["T"], tile["nseq"], tile["L"]
        is_s = tile["kind"] == "s"
        kd = tile["kind"]
        s0 = tile["s0"]
        xrhs = lambda kt: XTB(("t", kt), XTB.t[:, kt, 0:T])
        pj = lambda j: PJ(("t", j), PJ.t[:, j, 0:T])
        W = I["od_w_in"]

        def evacM(j, pv):
            if j < 8:
                C.act(pj(j), pv, AF.Identity, scale=1.0 / 16.0)
            elif j < 16:
                C.copy(ev_eng(), pj(j), pv)
            elif j < 24:
                C.act(pj(j), pv, AF.Sigmoid)
            elif j == 24:
                C.copy("dve", PJ(("t", 32), PJ.t[0:8, 32, 0:T]), pv)
            else:
                C.act(pj(j - 1), pv, AF.Silu)

        cols = [(i * 128, 128) for i in range(16)] + [(3072 + i * 128, 128) for i in range(8)] + [(4096, 8)] + \
               [(4104 + i * 128, 128) for i in range(8)]
        stream_mm(W, 16, cols, xrhs, T, evacM)

        def evacV(g, pv):
            C.copy(ev_eng(), VTM(None, VTM.t[0:T, 2 * g:2 * g + 2, 0:256]), pv.buf(None, pv.ap.rearrange("p (h v) -> p h v", h=2)))

        C.memset("pool", VTM(None, VTM.t[:, :, 256:257]), 1.0)
        stream_mm_tok(W, 2048, 1024, T, evacV)

        gx = GX(None, GX.t[0:8, 0:T])
        C.ts("dve", gx, PJ(("t", 32), PJ.t[0:8, 32, 0:T]), GB.all(), None, ALU.add)
        col = lambda a, b: COL(None, COL.t[0:T, a:b])
        ps = next_psb()
        C.tr(ps(None, ps.t[0:T, 0:8]), gx, ident(None, ident.t[0:8, 0:8]))
        C.copy("dve", col(0, 8), ps(None, ps.t[0:T, 0:8]))
        C.act(col(8, 12), col(4, 8), AF.Exp, scale=-1.0)
        C.act(col(8, 12), col(8, 12), AF.Ln, bias=1.0)
        C.ts("dve", col(8, 12), col(8, 12), -1.0, None, ALU.mult)
        ps = next_psb()
        C.mm(ps(None, ps.t[0:T, 0:4]), SEGTRI[kd](None, SEGTRI[kd].t[0:T, 0:T]), col(8, 12))
        C.copy("dve", col(12, 16), ps(None, ps.t[0:T, 0:4]))
        C.tt("dve", col(16, 20), col(0, 4), col(12, 16), ALU.subtract)
        if is_s:
            C.dma(MS.all(), I["s_mm"][s0:s0 + 8, :])
            ps = next_psb()
            C.mm(ps(None, ps.t[0:T, 0:4]), SEGSEL(None, SEGSEL.t[0:8, 0:T]), MS.all())
            C.copy("dve", col(20, 24), ps(None, ps.t[0:T, 0:4]))
            for h in range(4):
                C.copy("dve", MSB.all(), MS(None, MS.t[:, h:h + 1].broadcast_to([8, 128])))
                ps = next_psb()
                C.mm(ps(None, ps.t[:, 0:8]), MSB.all(), ident(None, ident.t[0:8, 0:8]))
                C.copy("dve", MINIT(None, MINIT.t[:, h, :]), ps(None, ps.t[:, 0:8]))
        else:
            if tile["first"]:
                C.memset("dve", MCAR.all(), 0.0)
                C.memset("pool", CS.all(), 0.0)
            C.copy("dve", col(20, 24), MCAR(None, MCAR.t[0:T, :]))
            C.copy("dve", MINIT(None, MINIT.t[:, :, 0:1]), MCAR(None, MCAR.t[:, :].unsqueeze(2)))

        row = lambda k: ROW(None, ROW.t[:, k, 0:T])
        ends = lambda k: ROW(None, ROW.t[:, k, 0:T].rearrange("p (n l) -> p n l", l=L)[:, :, L - 1])
        starts = lambda k: ROW(None, ROW.t[:, k, 0:T].rearrange("p (n l) -> p n l", l=L)[:, :, 0])
        rw = rwkv2(tile) if do_rwkv else iter(())
        rw_pd = {"done": not do_rwkv}

        def rw_advance(n):
            for _ in range(n):
                if rw_pd["done"]:
                    return
                if next(rw, "PD_DONE") == "PD_DONE":
                    rw_pd["done"] = True

        rw_advance(1)
        for h in range(4):
            minit_r = MINIT(None, MINIT.t[:, h, 0:nseq])
            ps = next_psb()
            C.mm(ps(None, ps.t[:, 0:T]), ident(None, ident.t[0:8, h:h + 1].broadcast_to([8, 128])), gx)
            C.copy("dve", row(0), ps(None, ps.t[:, 0:T]))
            ps = next_psb()
            C.mm(ps(None, ps.t[:, 0:T]), ident(None, ident.t[0:8, 4 + h:5 + h].broadcast_to([8, 128])), gx)
            C.act(row(1), ps(None, ps.t[:, 0:T]), AF.Exp, scale=-1.0)
            C.act(row(1), row(1), AF.Ln, bias=1.0)
            C.ts("dve", row(1), row(1), -1.0, None, ALU.mult)
            scan(row(2), R01[kd](None, R01[kd].t[:, 0:T]), row(1), 0.0, ALU.mult, ALU.add)
            C.tt("dve", row(3), row(0), row(2), ALU.subtract)
            C.tt("dve", starts(3), starts(3), minit_r, ALU.max)
            scan(row(4), RNEG[kd](None, RNEG[kd].t[:, 0:T]), row(3), -1e30, ALU.add, ALU.max)
            C.copy("dve", ROW(None, ROW.t[:, 5, 0:T].rearrange("p (n l) -> p n l", l=L)),
                   ROW(None, ROW.t[:, 4, 0:T].rearrange("p (n l) -> p n l", l=L)[:, :, L - 1:L].broadcast_to([128, nseq, L])))
            C.tt("dve", MNEW(None, MNEW.t[:, h, 0:nseq]), ends(2), ends(4), ALU.add)
            C.tt("dve", DEC(None, DEC.t[:, h, 0:nseq]), minit_r, ends(4), ALU.subtract)
            C.act(DEC(None, DEC.t[:, h, 0:nseq]), DEC(None, DEC.t[:, h, 0:nseq]), AF.Exp)
            ps = next_psb()
            C.mm(ps(None, ps.t[0:T, 0:1]), row(4), ident(None, ident.t[:, 0:1]))
            C.mm(ps(None, ps.t[0:T, 1:2]), row(5), ident(None, ident.t[:, 0:1]))
            C.copy("dve", col(24, 26), ps(None, ps.t[0:T, 0:2]))
            C.tt("dve", DTB(None, DTB.t[0:T, 0:T]), ROW(None, ROW.t[0:T, 4, 0:T]), MASKBIG[kd](None, MASKBIG[kd].t[0:T, 0:T]), ALU.add)
            C.act(DTB(None, DTB.t[0:T, 0:T]), DTB(None, DTB.t[0:T, 0:T]), AF.Exp, scale=-1.0, bias=col(16 + h, 17 + h))
            ps = next_psa()
            for kt in range(2):
                C.mm(ps(None, ps.t[0:T, 0:T]), pj(8 + 2 * h + kt), pj(2 * h + kt), start=(kt == 0), stop=(kt == 1))
            C.tt("dve", STB(None, STB.t[0:T, 0:T]), ps(None, ps.t[0:T, 0:T]), DTB(None, DTB.t[0:T, 0:T]), ALU.mult)
            ps1 = next_psa()
            C.mm(ps1(None, ps1.t[0:T, 0:257]), STB(None, STB.t[0:T, 0:T]), VTM(None, VTM.t[0:T, h, :]))
            C.copy("act", P1S(None, P1S.t[0:T, :]), ps1(None, ps1.t[0:T, 0:257]))
            ps2 = next_psa()
            if is_s:
                i_ = 0
                for n in range(nseq):
                    cs = CSS[n % 2]
                    C.dma(cs(None, cs.t[:, :, 0:256]), I["s_mc"][s0 + n, h].rearrange("(kt p) v -> p kt v", p=128))
                    C.dma(cs(None, cs.t[:, :, 256:257]), I["s_mn"][s0 + n, h].rearrange("(kt p o) -> p kt o", p=128, o=1), slow=True)
                    for kt in range(2):
                        qm = QM[i_ % 2]
                        i_ += 1
                        C.tt("pool", qm(None, qm.t[:, 0:T]), pj(2 * h + kt), SEGROW(None, SEGROW.t[:, n, 0:T]), ALU.mult)
                        C.mm(ps2(None, ps2.t[0:T, 0:257]), qm(None, qm.t[:, 0:T]), cs(None, cs.t[:, kt, :]),
                             start=(n == 0 and kt == 0), stop=(n == nseq - 1 and kt == 1))
            else:
                for kt in range(2):
                    C.mm(ps2(None, ps2.t[0:T, 0:257]), pj(2 * h + kt), CS(("h", h), CS.t[:, h, kt, :]), start=(kt == 0), stop=(kt == 1))
            sm = lambda a: SM(None, SM.t[0:T, a:a + 1])
            C.tt("dve", sm(0), col(20 + h, 21 + h), col(24, 25), ALU.subtract)
            C.act(sm(0), sm(0), AF.Exp)
            C.stt("dve", NUM(None, NUM.t[0:T, :]), ps2(None, ps2.t[0:T, 0:257]), sm(0), P1S(None, P1S.t[0:T, :]), ALU.mult, ALU.add)
            C.tt("dve", sm(1), col(12 + h, 13 + h), col(24, 25), ALU.add)
            C.act(sm(1), sm(1), AF.Exp, scale=-1.0)
            C.ts("dve", sm(2), NUM(None, NUM.t[0:T, 256:257]), -1.0, None, ALU.mult)
            C.tt("dve", sm(2), sm(2), NUM(None, NUM.t[0:T, 256:257]), ALU.max)
            C.tt("dve", sm(2), sm(2), sm(1), ALU.max)
            recip("dve", sm(2), sm(2))
            C.op("dve", lambda g, T=T: g.reduce_sum(out=SM.t[0:T, 3:4], in_=NUM.t[0:T, 0:256], axis=AX.X),
                 reads=[NUM(None, NUM.t[0:T, 0:256])], writes=[sm(3)])
            C.tt("dve", sm(3), sm(3), sm(2), ALU.mult)
            C.ts("dve", sm(3), sm(3), 1.0 / 256, None, ALU.mult)
            hn = HN(None, HN.t[0:T, :])
            C.ts("dve", hn, NUM(None, NUM.t[0:T, 0:256]), sm(2), sm(3), ALU.mult, ALU.subtract)
            C.tt("dve", P1S(None, P1S.t[0:T, 0:256]), hn, hn, ALU.mult)
            C.op("dve", lambda g, T=T: g.reduce_sum(out=SM.t[0:T, 4:5], in_=P1S.t[0:T, 0:256], axis=AX.X),
                 reads=[P1S(None, P1S.t[0:T, 0:256])], writes=[sm(4)])
            C.ts("dve", sm(4), sm(4), 1.0 / 256, LN_EPS, ALU.mult, ALU.add)
            C.act(sm(4), sm(4), AF.Sqrt)
            recip("dve", sm(4), sm(4))
            C.ts("dve", hn, hn, sm(4), None, ALU.mult)
            for kt in range(2):
                ct = 2 * h + kt
                ps = next_psb()
                C.tr(ps(None, ps.t[:, 0:T]), HN(None, HN.t[0:T, kt * 128:(kt + 1) * 128]), ident(None, ident.t[0:T, 0:T]))
                scr = P1S(None, P1S.t[:, 0:T])
                C.stt("dve", scr, ps(None, ps.t[:, 0:T]), VEC1(None, VEC1.t[:, ct, 0:1]), pj(16 + ct), ALU.mult, ALU.mult)
                C.tt("dve", mixv(ct, T), scr, pj(24 + ct), ALU.mult)
            C.tt("dve", sm(5), col(16 + h, 17 + h), col(25, 26), ALU.subtract)
            C.act(sm(5), sm(5), AF.Exp)
            for kt in range(2):
                ps = next_psb()
                C.tr(ps(None, ps.t[0:T, 0:128]), pj(8 + 2 * h + kt), ident.all())
                C.ts("dve", KW(None, KW.t[0:T, h * 256 + kt * 128:h * 256 + (kt + 1) * 128]), ps(None, ps.t[0:T, 0:128]), sm(5), None, ALU.mult)
            if is_s:
                for n in range(nseq):
                    cs = CSS[n % 2]
                    co = CSO[n % 2]
                    C.dma(cs(None, cs.t[:, :, 0:256]), I["s_mc"][s0 + n, h].rearrange("(kt p) v -> p kt v", p=128))
                    C.dma(cs(None, cs.t[:, :, 256:257]), I["s_mn"][s0 + n, h].rearrange("(kt p o) -> p kt o", p=128, o=1), slow=True)
                    C.ts("pool", KWN(None, KWN.t[0:T, :]), KW(None, KW.t[0:T, h * 256:(h + 1) * 256]), SEGC(None, SEGC.t[0:T, n:n + 1]), None, ALU.mult)
                    for kt in range(2):
                        ps = next_psa()
                        C.mm(ps(None, ps.t[:, 0:257]), KWN(None, KWN.t[0:T, kt * 128:(kt + 1) * 128]), VTM(None, VTM.t[0:T, h, :]))
                        C.stt("dve", co(None, co.t[:, kt, :]), cs(None, cs.t[:, kt, :]), DEC(None, DEC.t[:, h, n:n + 1]), ps(None, ps.t[:, 0:257]), ALU.mult, ALU.add)
                    C.dma(O["s_mc_o"][s0 + n, h].rearrange("(kt p) v -> p kt v", p=128), co(None, co.t[:, :, 0:256]))
                    C.dma(O["s_mn_o"][s0 + n, h].rearrange("(kt p o) -> p kt o", p=128, o=1), co(None, co.t[:, :, 256:257]), slow=True)
                C.dma(O["s_mm_o"][s0:s0 + 8, h:h + 1].rearrange("n o -> o n"), MNEW(None, MNEW.t[0:1, h, 0:8]), slow=True)
            else:
                for kt in range(2):
                    ps = next_psa()
                    C.mm(ps(None, ps.t[:, 0:257]), KW(None, KW.t[0:T, h * 256 + kt * 128:h * 256 + (kt + 1) * 128]), VTM(None, VTM.t[0:T, h, :]))
                    csv = CS(("h", h), CS.t[:, h, kt, :])
                    C.stt("dve", csv, csv, DEC(None, DEC.t[:, h, 0:1]), ps(None, ps.t[:, 0:257]), ALU.mult, ALU.add)
                C.copy("dve", MCAR(None, MCAR.t[:, h:h + 1]), MNEW(None, MNEW.t[:, h, 0:1]))
            rw_advance(2)
        if (not is_s) and tile["last"]:
            for h in range(4):
                C.dma(O["p_mc"][h].rearrange("(kt p) v -> p kt v", p=128), CS(("h", h), CS.t[:, h, :, 0:256]))
                C.dma(O["p_mn"][h].rearrange("(kt p o) -> p kt o", p=128, o=1), CS(("h", h), CS.t[:, h, :, 256:257]), slow=True)
            C.dma(O["p_mm"][:, :], MCAR(None, MCAR.t[0:1, :]))

        if do_rwkv:
            rw_advance(100)
            for _ in rw:
                pass
        else:
            for ct in range(8, 16):
                C.memset("pool", mixv(ct, T), 0.0)
        out_proj_ln(I["od_w_out"], tile, VECO, 0, 1)

    for ti, tile in enumerate(tile_plan(cfg)):
        wst["id"] = 0
        wst["first"] = (ti == 0)
        load_x(tile)
        if nlayers >= 1:
            layer0(tile)
        if nlayers >= 2:
            layer1(tile)
        store_y(tile)

    C.emit()
    es.close()
    return nc, C


def make_in_maps(inp, cores, consts):
    maps = []
    f = lambda a: np.ascontiguousarray(a, dtype=np.float32)
    for c in cores:
        s = c % 4
        m = {}
        m["xp"] = f(inp["x_prompt"][s])
        m["meta"] = f(inp["meta_tokens"])
        m["xs"] = f(inp["x_sample"][16 * c:16 * c + 16].reshape(128, D))
        m["s_conv"] = f(inp["state_conv"][0, 16 * c:16 * c + 16].reshape(480, 1024))
        m["s_sre"] = f(inp["state_ssm_re"][0, 16 * c:16 * c + 16])
        m["s_sim"] = f(inp["state_ssm_im"][0, 16 * c:16 * c + 16])
        m.update(consts)
        sl = slice(16 * c, 16 * c + 16)
        m["s_mc"] = f(inp["state_mlstm_c"][0, sl])
        m["s_mn"] = f(inp["state_mlstm_n"][0, sl])
        m["s_mm"] = f(inp["state_mlstm_m"][0, sl])
        m["s_rs"] = f(inp["state_rwkv_s"][0, sl])
        m["s_rsh"] = f(inp["state_rwkv_shift"][0, sl])
        m["od_w_in"] = f(inp["od_w_in"][0])
        for nm in ("m_ig_b", "m_fg_b", "m_hn_g", "r_w0", "r_a0", "r_kk", "r_ka", "r_ln_g", "r_ln_b", "r_rk", "r_mu", "od_ln_g", "od_ln_b"):
            m[nm] = f(inp[nm][0].reshape(1, -1))
        for nm in ("r_w2", "r_a2", "od_w_out"):
            m[nm] = f(inp[nm][0])
        m["ev_w_in"] = f(inp["ev_w_in"][0])
        m["a_conv_w"] = f(inp["a_conv_w"][0])
        for nm in ("a_conv_b", "a_ln_g", "a_ln_b", "s5_d", "s5_log_dt", "s5_glu_b", "ev_ln_g", "ev_ln_b"):
            m[nm] = f(inp[nm][0].reshape(1, -1))
        m["a_pw"] = f(inp["a_pw"][0])
        for nm in ("s5_lambda_re", "s5_lambda_im", "s5_b_re", "s5_b_im", "s5_glu_w", "ev_w_out"):
            m[nm] = f(inp[nm][0])
        m["s5_c_re"] = f(inp["s5_c_re"][0].reshape(1024, 64))
        m["s5_c_im"] = f(inp["s5_c_im"][0].reshape(1024, 64))
        maps.append(m)
    return maps


def kernel(**inp):
    cfg = {}
    nc, C = build(cfg)
    consts = make_consts()
    cores = list(range(NCORES))
    maps = make_in_maps(inp, cores, consts)
    res = run_bass_kernel_spmd(nc, maps, core_ids=cores)
    R = res.results
    B = 4
    cat = lambda k, shp: np.concatenate([R[c][k].reshape((16,) + shp) for c in range(NCORES)], 0)[None]
    stk = lambda k, shp: np.stack([R[c][k].reshape(shp) for c in range(B)], 0)[None]
    y_p = np.stack([R[c]["y_p"] for c in range(B)], 0)
    y_s = np.concatenate([R[c]["y_s"].reshape(16, 8, D) for c in range(NCORES)], 0)
    return (y_p, y_s,
            stk("p_conv", (30, 1024)), stk("p_sre", (64, 64)), stk("p_sim", (64, 64)), stk("p_mc", (4, 256, 256)),
            stk("p_mn", (4, 256)), stk("p_mm", (4,)), stk("p_rs", (16, 64, 64)), stk("p_rsh", (3200,)),
            cat("s_conv_o", (30, 1024)), cat("s_sre_o", (64, 64)), cat("s_sim_o", (64, 64)), cat("s_mc_o", (4, 256, 256)),
            cat("s_mn_o", (4, 256)), cat("s_mm_o", (4,)), cat("s_rs_o", (16, 64, 64)), cat("s_rsh_o", (3200,)))
```

```python
import contextlib
import numpy as np
import concourse.bass as bass
import concourse.mybir as mybir
from concourse.bass_utils import run_bass_kernel_spmd

F32 = mybir.dt.float32
BF16 = mybir.dt.bfloat16
AF = mybir.ActivationFunctionType
ALU = mybir.AluOpType
AX = mybir.AxisListType

D = 2048
TT = 128
WG = 512
KQ = 4
NWS = 4
NSLAB = 200
NCORES = 8
ALPHA = 4 ** 0.25
LN_EPS = 1e-5


class Reg:
    __slots__ = ("w", "r")

    def __init__(self):
        self.w = None
        self.r = []


class Buf:
    def __init__(self, ctx, name, t):
        self.ctx, self.name, self.t = ctx, name, t
        self.regs = {"_all": Reg()}
        self.dma_sem = None
        self.dma_cnt = 0

    def __call__(self, key, ap):
        return View(self, key, ap)

    def all(self):
        return View(self, None, self.t[:])

    def _sel(self, key):
        if key is None:
            return list(self.regs.values())
        if key not in self.regs:
            self.regs[key] = Reg()
        return [self.regs[key], self.regs["_all"]]

    def rdeps(self, key):
        return [r.w for r in self._sel(key) if r.w is not None]

    def wdeps(self, key):
        out = []
        for r in self._sel(key):
            if r.w is not None:
                out.append(r.w)
            out.extend(r.r)
        return out

    def note_read(self, key, tok):
        if key is None:
            for r in self.regs.values():
                r.r.append(tok)
        else:
            self._sel(key)[0].r.append(tok)

    def note_write(self, key, tok):
        if key is None:
            self.regs = {"_all": Reg()}
            self.regs["_all"].w = tok
        else:
            r = self._sel(key)[0]
            r.w = tok
            r.r = []


class View:
    __slots__ = ("buf", "key", "ap")

    def __init__(self, buf, key, ap):
        self.buf, self.key, self.ap = buf, key, ap


class Ctx:
    ENG = ("pe", "act", "dve", "pool", "sp")
    EPOCH = 30000

    def __init__(self, nc, es):
        self.nc, self.es = nc, es
        self.prog = {e: [] for e in self.ENG}
        self.cnt = {e: 0 for e in self.ENG}
        self.sem = {e: es.enter_context(nc.semaphore("sem_" + e)) for e in self.ENG}
        self.known = {e: {} for e in self.ENG}
        self.final = []
        self.total = {}
        self.nsem = 5
        self.nbytes = 0

    def sb(self, name, shape, dtype=F32):
        t = self.es.enter_context(self.nc.sbuf_tensor("sb_" + name, list(shape), dtype))
        n = 4
        for s in shape[1:]:
            n *= s
        self.nbytes += n
        return Buf(self, name, t)

    def ps(self, name, shape, dtype=F32):
        t = self.es.enter_context(self.nc.psum_tensor("ps_" + name, list(shape), dtype))
        return Buf(self, name, t)

    def need(self, e, tok):
        sem, val = tok
        k = id(sem)
        if self.known[e].get(k, 0) >= val:
            return
        self.known[e][k] = val
        self.prog[e].append(("wait", sem, val))

    def op(self, e, fn, reads=(), writes=()):
        for v in reads:
            if isinstance(v, View):
                for tok in v.buf.rdeps(v.key):
                    self.need(e, tok)
        for v in writes:
            if isinstance(v, View):
                for tok in v.buf.wdeps(v.key):
                    self.need(e, tok)
        if self.cnt[e] >= self.EPOCH:
            self.total[e] = self.total.get(e, 0) + self.cnt[e]
            self.sem[e] = self.es.enter_context(self.nc.semaphore("sem_%s_%d" % (e, self.total[e])))
            self.cnt[e] = 0
            self.nsem += 1
        self.cnt[e] += 1
        tok = (self.sem[e], self.cnt[e])
        self.prog[e].append(("op", fn, self.sem[e], 1))
        for v in reads:
            if isinstance(v, View):
                v.buf.note_read(v.key, tok)
        for v in writes:
            if isinstance(v, View):
                v.buf.note_write(v.key, tok)
        return tok

    def dma(self, out, in_, q="sp", slow=False):
        sbv = out if isinstance(out, View) else in_
        b = sbv.buf
        if b.dma_sem is None:
            b.dma_sem = self.es.enter_context(self.nc.semaphore("dq_" + b.name))
            self.nsem += 1
        if isinstance(in_, View):
            for tok in in_.buf.rdeps(in_.key):
                self.need(q, tok)
        if isinstance(out, View):
            for tok in out.buf.wdeps(out.key):
                self.need(q, tok)
        b.dma_cnt += 16
        tok = (b.dma_sem, b.dma_cnt)
        oap = out.ap if isinstance(out, View) else out
        iap = in_.ap if isinstance(in_, View) else in_
        if slow:
            fn = lambda eng, oap=oap, iap=iap: eng.dma_start(out=oap, in_=iap, allow_slow_non_contiguous=True)
        else:
            fn = lambda eng, oap=oap, iap=iap: eng.dma_start(out=oap, in_=iap)
        self.prog[q].append(("op", fn, b.dma_sem, 16))
        if isinstance(in_, View):
            in_.buf.note_read(in_.key, tok)
        if isinstance(out, View):
            out.buf.note_write(out.key, tok)
        else:
            self.final.append(tok)
        return tok

    def tt(self, e, out, in0, in1, op):
        return self.op(e, lambda g: g.tensor_tensor(out=out.ap, in0=in0.ap, in1=in1.ap, op=op),
                       reads=[in0, in1], writes=[out])

    def ts(self, e, out, in0, s1, s2, op0, op1=None):
        rd = [in0] + [s for s in (s1, s2) if isinstance(s, View)]
        a1 = s1.ap if isinstance(s1, View) else s1
        a2 = s2.ap if isinstance(s2, View) else s2
        if op1 is None:
            return self.op(e, lambda g: g.tensor_scalar(out=out.ap, in0=in0.ap, scalar1=a1, scalar2=None, op0=op0),
                           reads=rd, writes=[out])
        return self.op(e, lambda g: g.tensor_scalar(out=out.ap, in0=in0.ap, scalar1=a1, scalar2=a2, op0=op0, op1=op1),
                       reads=rd, writes=[out])

    def stt(self, e, out, in0, s, in1, op0, op1):
        rd = [in0, in1] + ([s] if isinstance(s, View) else [])
        a = s.ap if isinstance(s, View) else s
        return self.op(e, lambda g: g.scalar_tensor_tensor(out=out.ap, in0=in0.ap, scalar=a, in1=in1.ap, op0=op0, op1=op1),
                       reads=rd, writes=[out])

    def act(self, out, in_, func, bias=None, scale=None, e="act"):
        rd = [in_] + [s for s in (bias, scale) if isinstance(s, View)]
        kw = {}
        if bias is not None:
            kw["bias"] = bias.ap if isinstance(bias, View) else bias
        if scale is not None:
            kw["scale"] = scale.ap if isinstance(scale, View) else scale
        return self.op(e, lambda g: g.activation(out=out.ap, in_=in_.ap, func=func, **kw), reads=rd, writes=[out])

    def copy(self, e, out, in_):
        if e == "act":
            return self.act(out, in_, AF.Copy)
        return self.op(e, lambda g: g.tensor_copy(out=out.ap, in_=in_.ap), reads=[in_], writes=[out])

    def memset(self, e, out, val):
        return self.op(e, lambda g: g.memset(out.ap, val), writes=[out])

    def mm(self, out, lhsT, rhs, start=True, stop=True):
        return self.op("pe", lambda g: g.matmul(out.ap, lhsT=lhsT.ap, rhs=rhs.ap, start=start, stop=stop),
                       reads=[lhsT, rhs], writes=[out])

    def tr(self, out, in_, ident):
        return self.op("pe", lambda g: g.transpose(out.ap, in_.ap, ident.ap), reads=[in_, ident], writes=[out])

    def emit(self):
        nc = self.nc
        for tok in self.final:
            self.need("sp", tok)
        for e in self.ENG:
            if e != "sp" and self.cnt[e] > 0:
                self.need("sp", (self.sem[e], self.cnt[e]))
        engs = {"pe": "tensor", "act": "scalar", "dve": "vector", "pool": "gpsimd", "sp": "sync"}
        with nc.Block() as block:
            for e, attr in engs.items():
                items = self.prog[e]

                def body(eng, items=items):
                    for it in items:
                        if it[0] == "wait":
                            eng.wait_ge(it[1], it[2])
                        else:
                            it[1](eng).then_inc(it[2], it[3])

                getattr(block, attr)(body)


def make_consts():
    c = {}
    c["ident"] = np.eye(128, dtype=np.float32)
    c["ones"] = np.ones((128, 128), dtype=np.float32)
    r = np.arange(128)
    mrow = (r % 32) // 16
    mcol = np.arange(128) // 64
    bm = (mrow[:, None] == mcol[None, :]).astype(np.float32)
    ev_r = ((r // 32) % 2 == 0).astype(np.float32)
    c["bmask"] = bm * ev_r[:, None]
    c["bmask_o"] = bm * (1 - ev_r)[:, None]
    gl = np.arange(128) // 16
    cm = ((gl[None, :] % 2) == (r[:, None] // 64)).astype(np.float32)
    c["cmask"] = cm * ev_r[None, :]
    c["cmask_o"] = cm * (1 - ev_r)[None, :]
    BIG = 1e30
    i128 = np.arange(128)
    allow_p = (i128[:, None] <= i128[None, :])
    c["maskbig_p"] = np.where(allow_p, 0.0, BIG).astype(np.float32)
    c["segtri_p"] = allow_p.astype(np.float32)
    i64 = np.arange(64)
    allow_s = (i64[:, None] <= i64[None, :]) & ((i64[:, None] // 8) == (i64[None, :] // 8))
    c["maskbig_s"] = np.where(allow_s, 0.0, BIG).astype(np.float32)
    c["segtri_s"] = allow_s.astype(np.float32)
    r01p = np.ones((128, 128), np.float32); r01p[:, 0] = 0
    r01s = np.ones((128, 64), np.float32); r01s[:, ::8] = 0
    c["r01_p"], c["r01_s"] = r01p, r01s
    c["rneg_p"] = ((1 - r01p) * -BIG).astype(np.float32)
    c["rneg_s"] = ((1 - r01s) * -BIG).astype(np.float32)
    segsel = ((i64[None, :] // 8) == np.arange(8)[:, None]).astype(np.float32)
    c["segsel"] = segsel
    c["segc"] = np.ascontiguousarray(segsel.T)
    c["segrow"] = np.ascontiguousarray(np.broadcast_to(segsel.reshape(1, 512), (128, 512))).astype(np.float32)
    c["blk"] = ((i128[:, None] // 64) == (i128[None, :] // 64)).astype(np.float32)
    st_p = (i128[:, None] < i128[None, :])
    st_s = (i64[:, None] < i64[None, :]) & ((i64[:, None] // 8) == (i64[None, :] // 8))
    c["sstri_p"] = st_p.astype(np.float32)
    c["sstriT_p"] = np.ascontiguousarray(st_p.T).astype(np.float32)
    c["sstri_s"] = st_s.astype(np.float32)
    c["sstriT_s"] = np.ascontiguousarray(st_s.T).astype(np.float32)
    return c


CONST_SHAPES = {"ident": [128, 128], "ones": [128, 128], "bmask": [128, 128], "cmask": [128, 128],
                "bmask_o": [128, 128], "cmask_o": [128, 128],
                "maskbig_p": [128, 128], "segtri_p": [128, 128], "maskbig_s": [64, 64], "segtri_s": [64, 64],
                "r01_p": [128, 128], "r01_s": [128, 64], "rneg_p": [128, 128], "rneg_s": [128, 64],
                "segsel": [8, 64], "segc": [64, 8], "segrow": [128, 512], "blk": [128, 128],
                "sstri_p": [128, 128], "sstriT_p": [128, 128], "sstri_s": [64, 64], "sstriT_s": [64, 64]}


def tile_plan(cfg):
    tiles = []
    npt = cfg.get("n_prompt_tiles", 17)
    pos = 0
    for i in range(17):
        T = 16 if i == 16 else TT
        if i < npt:
            tiles.append(dict(kind="p", T=T, pos=pos, nseq=1, L=T, first=(i == 0), last=(i == npt - 1), s0=0))
        pos += T
    if cfg.get("sample", True):
        for h in range(2):
            tiles.append(dict(kind="s", T=64, pos=0, nseq=8, L=8, first=True, last=True, s0=8 * h))
    return tiles


def build(cfg):
    nc = bass.Bass("TRN2", target_bir_lowering=False)
    es = contextlib.ExitStack()
    C = Ctx(nc, es)
    dbg = cfg.get("debug", False)
    nlayers = cfg.get("layers", 2)

    def din(name, shape):
        return nc.dram_tensor(name, list(shape), F32, kind="ExternalInput").ap()

    def dout(name, shape):
        return nc.dram_tensor(name, list(shape), F32, kind="ExternalOutput").ap()

    I = {}
    I["xp"] = din("xp", [2048, D])
    I["meta"] = din("meta", [16, D])
    I["xs"] = din("xs", [128, D])
    I["s_conv"] = din("s_conv", [16 * 30, 1024])
    I["s_sre"] = din("s_sre", [16, 64, 64])
    I["s_sim"] = din("s_sim", [16, 64, 64])
    for nm, shp in CONST_SHAPES.items():
        I[nm] = din(nm, shp)
    I["s_mc"] = din("s_mc", [16, 4, 256, 256])
    I["s_mn"] = din("s_mn", [16, 4, 256])
    I["s_mm"] = din("s_mm", [16, 4])
    I["s_rs"] = din("s_rs", [16, 16, 64, 64])
    I["s_rsh"] = din("s_rsh", [16, 3200])
    I["od_w_in"] = din("od_w_in", [D, 9352])
    I["m_ig_b"] = din("m_ig_b", [1, 4])
    I["m_fg_b"] = din("m_fg_b", [1, 4])
    for nm in ("m_hn_g", "r_w0", "r_a0", "r_kk", "r_ka", "r_ln_g", "r_ln_b", "r_rk"):
        I[nm] = din(nm, [1, 1024])
    I["r_mu"] = din("r_mu", [1, 3200])
    I["r_w2"] = din("r_w2", [64, 1024])
    I["r_a2"] = din("r_a2", [64, 1024])
    I["od_w_out"] = din("od_w_out", [2048, D])
    I["od_ln_g"] = din("od_ln_g", [1, D])
    I["od_ln_b"] = din("od_ln_b", [1, D])
    I["ev_w_in"] = din("ev_w_in", [D, 5120])
    I["a_conv_w"] = din("a_conv_w", [31, 1024])
    for nm in ("a_conv_b", "a_ln_g", "a_ln_b", "s5_d"):
        I[nm] = din(nm, [1, 1024])
    I["a_pw"] = din("a_pw", [1024, 1024])
    I["s5_lambda_re"] = din("s5_lambda_re", [64, 64])
    I["s5_lambda_im"] = din("s5_lambda_im", [64, 64])
    I["s5_log_dt"] = din("s5_log_dt", [1, 64])
    I["s5_b_re"] = din("s5_b_re", [64, 64, 16])
    I["s5_b_im"] = din("s5_b_im", [64, 64, 16])
    I["s5_c_re"] = din("s5_c_re", [1024, 64])
    I["s5_c_im"] = din("s5_c_im", [1024, 64])
    I["s5_glu_w"] = din("s5_glu_w", [1024, 2048])
    I["s5_glu_b"] = din("s5_glu_b", [1, 2048])
    I["ev_w_out"] = din("ev_w_out", [2048, D])
    I["ev_ln_g"] = din("ev_ln_g", [1, D])
    I["ev_ln_b"] = din("ev_ln_b", [1, D])

    O = {}
    O["y_p"] = dout("y_p", [2048, D])
    O["y_s"] = dout("y_s", [128, D])
    O["p_conv"] = dout("p_conv", [30, 1024])
    O["p_sre"] = dout("p_sre", [64, 64])
    O["p_sim"] = dout("p_sim", [64, 64])
    O["s_conv_o"] = dout("s_conv_o", [16 * 30, 1024])
    O["s_sre_o"] = dout("s_sre_o", [16, 64, 64])
    O["s_sim_o"] = dout("s_sim_o", [16, 64, 64])
    O["p_mc"] = dout("p_mc", [4, 256, 256])
    O["p_mn"] = dout("p_mn", [4, 256])
    O["p_mm"] = dout("p_mm", [1, 4])
    O["p_rs"] = dout("p_rs", [16, 64, 64])
    O["p_rsh"] = dout("p_rsh", [1, 3200])
    O["s_mc_o"] = dout("s_mc_o", [16, 4, 256, 256])
    O["s_mn_o"] = dout("s_mn_o", [16, 4, 256])
    O["s_mm_o"] = dout("s_mm_o", [16, 4])
    O["s_rs_o"] = dout("s_rs_o", [16, 16, 64, 64])
    O["s_rsh_o"] = dout("s_rsh_o", [16, 3200])

    ident = C.sb("ident", [128, 128])
    ones = C.sb("ones", [128, 128])
    XT = C.sb("XT", [128, 16, TT])
    XS = C.sb("XS", [128, D])
    PJ = C.sb("PJ", [128, 33, TT])
    MIX = C.sb("MIX", [128, 16 * TT], BF16)
    XTB = C.sb("XTB", [128, 16, TT], BF16)
    GELB = C.sb("GELB", [128, 8, TT], BF16)
    WS = [C.sb("WS%d" % i, [128, KQ, WG], BF16) for i in range(NWS)]
    HB = C.sb("HB", [128, 8, 304])
    HC = C.sb("HC", [128, 8, 30])
    ACC = C.sb("ACC", [128, 8, TT])
    SQ2 = C.sb("SQ2", [128, 2, TT])
    ST = C.sb("ST", [128, 3, TT])
    VEC0 = C.sb("VEC0", [128, 8, 40])
    VECG = C.sb("VECG", [128, 16, 4])
    TMP = C.sb("TMP", [128, 128])
    XR = C.sb("XR", [128, 32, 129])
    XI = C.sb("XI", [128, 32, 129])
    S5C = C.sb("S5C", [128, 2, 32])
    S5A = C.sb("S5A", [128, 8, 32])
    S5T = C.sb("S5T", [128, 10, 32])
    SCS = C.sb("SCS", [128, 2, 32, 8])
    BRE = [C.sb("BRE%d" % i, [128, 8, 128]) for i in range(2)]
    BIM = [C.sb("BIM%d" % i, [128, 8, 128]) for i in range(2)]
    CRE = [C.sb("CRE%d" % i, [128, 8, 128]) for i in range(2)]
    CIM = [C.sb("CIM%d" % i, [128, 8, 128]) for i in range(2)]
    mask_bo = C.sb("mask_bo", [128, 128])
    mask_co = C.sb("mask_co", [128, 128])
    S5ST = C.sb("S5ST", [128, 128])
    mask_b = C.sb("mask_b", [128, 128])
    mask_c = C.sb("mask_c", [128, 128])

    PSA = [C.ps("PSA%d" % i, [128, 512]) for i in range(4)]
    PSB = [C.ps("PSB%d" % i, [128, 512]) for i in range(4)]

    def mixv(ct, T):
        return MIX(("t", ct), MIX.t[:, ct * TT:ct * TT + T])

    C.dma(ident.all(), I["ident"])
    C.dma(ones.all(), I["ones"])
    C.dma(mask_b.all(), I["bmask"])
    C.dma(mask_c.all(), I["cmask"])
    C.dma(mask_bo.all(), I["bmask_o"])
    C.dma(mask_co.all(), I["cmask_o"])

    rr = {"psa": 0, "psb": 0, "ws": 0, "ev": 0, "sq": 0, "tm": 0}

    def next_psa():
        rr["psa"] = (rr["psa"] + 1) % 4
        return PSA[rr["psa"]]

    def next_psb():
        rr["psb"] = (rr["psb"] + 1) % 4
        return PSB[rr["psb"]]

    def ev_eng():
        rr["ev"] += 1
        return "dve" if rr["ev"] % 2 else "act"

    def load_rows_T(dst, rows, ncols, col0=0, ct0=0):
        r0 = 0
        for ap in rows:
            nr = ap.shape[0]
            C.dma(XS(None, XS.t[r0:r0 + nr, 0:ncols]), ap)
            r0 += nr
        nr = r0
        for ct in range(ncols // 128):
            ps = next_psb()
            C.tr(ps(None, ps.t[:, 0:nr]), XS(None, XS.t[0:nr, ct * 128:(ct + 1) * 128]), ident(None, ident.t[0:nr, 0:nr]))
            C.copy("dve", dst(None, dst.t[:, ct0 + ct, col0:col0 + nr]), ps(None, ps.t[:, 0:nr]))

    load_rows_T(VEC0, [I["a_conv_w"], I["a_conv_b"], I["a_ln_g"], I["a_ln_b"], I["s5_d"]], 1024)
    load_rows_T(VECG, [I["s5_glu_b"], I["ev_ln_g"], I["ev_ln_b"]], 2048)

    def gp_ap(ap2d):
        return ap2d.rearrange("(q m) p -> (m p) q", m=2)

    LR, LI, DT, AR, AI, NAI = range(6)
    A = lambda k: S5A(None, S5A.t[:, k, :])
    Tm = lambda k: S5T(None, S5T.t[:, k, :])
    for m in range(2):
        C.dma(S5A(None, S5A.t[64 * m:64 * m + 64, LR, :]), I["s5_lambda_re"].rearrange("(q m) p -> m p q", m=2)[m], slow=True)
        C.dma(S5A(None, S5A.t[64 * m:64 * m + 64, LI, :]), I["s5_lambda_im"].rearrange("(q m) p -> m p q", m=2)[m], slow=True)
    ldt = I["s5_log_dt"]
    for m in range(2):
        src = bass.AP(ldt.tensor, ldt.offset + m, [[0, 64], [2, 32]])
        C.dma(S5A(None, S5A.t[64 * m:64 * m + 64, DT, :]), src, slow=True)
    C.act(A(DT), A(DT), AF.Exp)
    C.tt("dve", Tm(0), A(LR), A(DT), ALU.mult)
    C.act(Tm(1), Tm(0), AF.Exp)
    C.tt("dve", Tm(2), A(LI), A(DT), ALU.mult)
    PI = float(np.pi)
    C.ts("dve", Tm(3), Tm(2), 1.0 / 32, None, ALU.mult)
    C.act(Tm(4), Tm(3), AF.Sin)
    C.ts("dve", Tm(3), Tm(3), PI / 2, None, ALU.add)
    C.act(Tm(5), Tm(3), AF.Sin)
    for _ in range(5):
        C.tt("dve", Tm(3), Tm(4), Tm(5), ALU.mult)
        C.tt("dve", Tm(8), Tm(5), Tm(5), ALU.mult)
        C.tt("dve", Tm(9), Tm(4), Tm(4), ALU.mult)
        C.ts("dve", Tm(4), Tm(3), 2.0, None, ALU.mult)
        C.tt("dve", Tm(5), Tm(8), Tm(9), ALU.subtract)
    C.tt("dve", A(AR), Tm(1), Tm(5), ALU.mult)
    C.tt("dve", A(AI), Tm(1), Tm(4), ALU.mult)
    C.ts("dve", A(NAI), A(AI), -1.0, None, ALU.mult)
    MAG = 6
    C.copy("dve", A(MAG), Tm(1))
    s5tab = nc.dram_tensor("s5tab", [2, 128, 1024], F32).ap()
    tC = lambda a, b: XR(None, XR.t[:, :, a:b])
    tS = lambda a, b: XI(None, XI.t[:, :, a:b])
    C.copy("dve", tC(0, 1), S5T(None, S5T.t[:, 5, :].unsqueeze(2)))
    C.copy("dve", tS(0, 1), S5T(None, S5T.t[:, 4, :].unsqueeze(2)))
    n_ = 1
    while n_ < 32:
        cn = XR(None, XR.t[:, :, n_ - 1:n_].broadcast_to([128, 32, n_]))
        sn = XI(None, XI.t[:, :, n_ - 1:n_].broadcast_to([128, 32, n_]))
        u1 = XR(None, XR.t[:, :, 64:64 + n_])
        u2 = XI(None, XI.t[:, :, 64:64 + n_])
        C.tt("dve", u1, tS(0, n_), sn, ALU.mult)
        C.tt("dve", tC(n_, 2 * n_), tC(0, n_), cn, ALU.mult)
        C.tt("dve", tC(n_, 2 * n_), tC(n_, 2 * n_), u1, ALU.subtract)
        C.tt("dve", u2, tC(0, n_), sn, ALU.mult)
        C.tt("dve", tS(n_, 2 * n_), tS(0, n_), cn, ALU.mult)
        C.tt("dve", tS(n_, 2 * n_), tS(n_, 2 * n_), u2, ALU.add)
        n_ *= 2
    tab_tok = [C.dma(s5tab[0].rearrange("p (q t) -> p q t", t=32), tC(0, 32)),
               C.dma(s5tab[1].rearrange("p (q t) -> p q t", t=32), tS(0, 32))]
    for tk in tab_tok:
        C.need("sp", tk)
    C.tt("dve", Tm(0), A(LR), A(LR), ALU.mult)
    C.tt("dve", Tm(1), A(LI), A(LI), ALU.mult)
    C.tt("dve", Tm(0), Tm(0), Tm(1), ALU.add)
    C.op("dve", lambda g: g.reciprocal(out=S5T.t[:, 0, :], in_=S5T.t[:, 0, :]), reads=[Tm(0)], writes=[Tm(0)])
    C.ts("dve", Tm(1), A(AR), -1.0, None, ALU.add)
    C.tt("dve", Tm(2), Tm(1), A(LR), ALU.mult)
    C.tt("dve", Tm(3), A(AI), A(LI), ALU.mult)
    C.tt("dve", Tm(2), Tm(2), Tm(3), ALU.add)
    C.tt("dve", Tm(6), Tm(2), Tm(0), ALU.mult)
    C.tt("dve", Tm(2), A(AI), A(LR), ALU.mult)
    C.tt("dve", Tm(3), Tm(1), A(LI), ALU.mult)
    C.tt("dve", Tm(2), Tm(2), Tm(3), ALU.subtract)
    C.tt("dve", Tm(7), Tm(2), Tm(0), ALU.mult)
    braw_t = PJ.t[:, 0:8, :].rearrange("p a b -> p (a b)").rearrange("p (k q h) -> p k q h", k=2, q=32)
    bb_t = PJ.t[:, 8:24, :].rearrange("p a b -> p (a b)").rearrange("p (k q h) -> p k q h", k=2, q=32)
    sc_t = PJ.t[:, 24:28, :].rearrange("p a b -> p (a b)").rearrange("p (q h) -> p q h", q=32)
    for k, nm in enumerate(("s5_b_re", "s5_b_im")):
        for m in range(2):
            C.dma(PJ(None, braw_t[64 * m:64 * m + 64, k, :, :]), I[nm].rearrange("(q m) p h -> m p q h", m=2)[m])
    qr_b = S5T(None, S5T.t[:, 6, :].unsqueeze(2).broadcast_to([128, 32, 16]))
    qi_b = S5T(None, S5T.t[:, 7, :].unsqueeze(2).broadcast_to([128, 32, 16]))
    br = PJ(None, braw_t[:, 0, :, :])
    bi = PJ(None, braw_t[:, 1, :, :])
    sc = PJ(None, sc_t)
    for dup in range(2):
        o_r = PJ(None, bb_t[:, 0, :, dup * 16:(dup + 1) * 16])
        o_i = PJ(None, bb_t[:, 1, :, dup * 16:(dup + 1) * 16])
        C.tt("dve", o_r, br, qr_b, ALU.mult)
        C.tt("dve", sc, bi, qi_b, ALU.mult)
        C.tt("dve", o_r, o_r, sc, ALU.subtract)
        C.tt("dve", o_i, bi, qr_b, ALU.mult)
        C.tt("dve", sc, br, qi_b, ALU.mult)
        C.tt("dve", o_i, o_i, sc, ALU.add)
    for k, dst in enumerate((BRE, BIM)):
        for gt in range(8):
            ps = next_psb()
            C.copy("dve", TMP.all(), PJ(None, bb_t[:, k, 4 * gt:4 * gt + 4, :]))
            C.tr(ps(None, ps.t[:, 0:128]), TMP.all(), ident.all())
            C.tt("dve", dst[0](None, dst[0].t[:, gt, :]), ps(None, ps.t[:, 0:128]), mask_b.all(), ALU.mult)
            C.tt("dve", dst[1](None, dst[1].t[:, gt, :]), ps(None, ps.t[:, 0:128]), mask_bo.all(), ALU.mult)
    for k, (nm, dst) in enumerate((("s5_c_re", CRE), ("s5_c_im", CIM))):
        for gt in range(8):
            for dup in range(2):
                C.dma(XS(None, XS.t[:, dup * 64:(dup + 1) * 64]), I[nm][gt * 128:(gt + 1) * 128, :])
            ps = next_psb()
            C.tr(ps(None, ps.t[:, 0:128]), XS(None, XS.t[:, 0:128]), ident.all())
            for par, mk in enumerate((mask_c, mask_co)):
                C.stt("dve", dst[par](None, dst[par].t[:, gt, :]), ps(None, ps.t[:, 0:128]), 1.0 if k == 0 else -1.0,
                      mk.all(), ALU.mult, ALU.mult)

    WSCR = nc.dram_tensor("wscr", [NSLAB, 128, KQ * WG], BF16).ap()
    wst = {"id": 0, "first": True, "tok": {}}

    def load_slab(ws, src, nk, gw):
        sid = wst["id"]
        wst["id"] += 1
        assert sid < NSLAB
        dstv = ws(None, ws.t[:, 0:nk, 0:gw])
        scr = WSCR[sid][:, 0:nk * gw].rearrange("p (k c) -> p k c", k=nk)
        if wst["first"]:
            C.dma(dstv, src, q="pool")
            wst["tok"][sid] = C.dma(scr, dstv, q="sp")
        else:
            C.need("sp", wst["tok"][sid])
            C.dma(dstv, scr, q="sp")

    def stream_mm(W, nkt, coltiles, rhs_fn, T, evac, ev=None):
        for _ in stream_mm_g(W, nkt, coltiles, rhs_fn, T, evac, ev):
            pass

    def stream_mm_g(W, nkt, coltiles, rhs_fn, T, evac, ev=None):
        groups = []
        cur = []
        for ct in coltiles:
            if cur and (ct[0] + ct[1] - cur[0][0] > WG):
                groups.append(cur)
                cur = []
            cur.append(ct)
        if cur:
            groups.append(cur)
        Wv = W.rearrange("(kt p) c -> p kt c", p=128)
        j = 0
        for grp in groups:
            g0 = grp[0][0]
            gw = grp[-1][0] + grp[-1][1] - g0
            ps = next_psa()
            pacc = ps(None, ps.t[0:T, 0:gw])
            for kq in range(0, nkt, KQ):
                ws = WS[rr["ws"] % NWS]
                rr["ws"] += 1
                nk = min(KQ, nkt - kq)
                load_slab(ws, Wv[:, kq:kq + nk, g0:g0 + gw], nk, gw)
                for k in range(nk):
                    kt = kq + k
                    C.mm(pacc, rhs_fn(kt), ws(None, ws.t[:, k, 0:gw]), start=(kt == 0), stop=(kt == nkt - 1))
            i_ = rr["tm"] % 2
            rr["tm"] += 1
            C.copy(ev or ev_eng(), XS(("tm", i_), XS.t[0:T, i_ * 512:i_ * 512 + gw]), pacc)
            for (c0, w) in grp:
                pt = next_psb()
                C.tr(pt(None, pt.t[0:w, 0:T]), XS(("tm", i_), XS.t[0:T, i_ * 512 + c0 - g0:i_ * 512 + c0 - g0 + w]),
                     ident(None, ident.t[0:T, 0:T]))
                evac(j, pt(None, pt.t[0:w, 0:T]))
                j += 1
            yield "g"

    def layer_norm_cols(src, ntile, T, gcol, bcol, vec, func=AF.Identity, eps=LN_EPS, dst_fn=None, also_fn=None):
        nch = ntile * 128
        ps = next_psb()
        ps2 = next_psb()
        for ct in range(ntile):
            sv = src(("t", ct), src.t[:, ct, 0:T])
            sq = SQ2(("s", rr["sq"] % 2), SQ2.t[:, rr["sq"] % 2, 0:T])
            rr["sq"] += 1
            C.act(sq, sv, AF.Square)
            C.mm(ps(None, ps.t[:, 0:T]), ones.all(), sv, start=(ct == 0), stop=(ct == ntile - 1))
            C.mm(ps2(None, ps2.t[:, 0:T]), ones.all(), sq, start=(ct == 0), stop=(ct == ntile - 1))
        st = lambda k: ST(None, ST.t[:, k, 0:T])
        C.ts("dve", st(0), ps(None, ps.t[:, 0:T]), 1.0 / nch, None, ALU.mult)
        C.ts("dve", st(1), ps2(None, ps2.t[:, 0:T]), 1.0 / nch, None, ALU.mult)
        C.tt("dve", st(2), st(0), st(0), ALU.mult)
        C.tt("dve", st(1), st(1), st(2), ALU.subtract)
        C.ts("dve", st(1), st(1), eps, None, ALU.add)
        C.act(st(1), st(1), AF.Sqrt)
        C.op("dve", lambda g, T=T: g.reciprocal(out=ST.t[:, 1, 0:T], in_=ST.t[:, 1, 0:T]), reads=[st(1)], writes=[st(1)])
        C.tt("dve", st(2), st(0), st(1), ALU.mult)
        C.ts("dve", st(2), st(2), -1.0, None, ALU.mult)
        for ct in range(ntile):
            e = "dve" if ct % 2 == 0 else "pool"
            sv = src(("t", ct), src.t[:, ct, 0:T])
            C.tt(e, sv, sv, st(1), ALU.mult)
            C.tt(e, sv, sv, st(2), ALU.add)
            dv = dst_fn(ct) if dst_fn is not None else sv
            C.act(dv, sv, func, bias=vec(None, vec.t[:, ct, bcol:bcol + 1]), scale=vec(None, vec.t[:, ct, gcol:gcol + 1]))
            if also_fn is not None:
                C.copy("pool" if ct % 2 else "act", also_fn(ct), dv)

    def load_x(tile):
        n = tile["T"]
        if tile["kind"] == "s":
            C.dma(XS(None, XS.t[0:n, :]), I["xs"][tile["s0"] * 8:tile["s0"] * 8 + n, :])
        else:
            p0 = tile["pos"]
            r = 0
            if p0 < 16:
                C.dma(XS(None, XS.t[0:16, :]), I["meta"][:, :])
                r = 16
            x0 = p0 + r - 16
            C.dma(XS(None, XS.t[r:n, :]), I["xp"][x0:x0 + n - r, :])
        for dt_ in range(16):
            ps = next_psb()
            C.tr(ps(None, ps.t[:, 0:n]), XS(None, XS.t[0:n, dt_ * 128:(dt_ + 1) * 128]), ident(None, ident.t[0:n, 0:n]))
            C.copy("act" if dt_ % 2 else "dve", XT(("t", dt_), XT.t[:, dt_, 0:n]), ps(None, ps.t[:, 0:n]))
            C.copy("pool", XTB(("t", dt_), XTB.t[:, dt_, 0:n]), XT(("t", dt_), XT.t[:, dt_, 0:n]))

    def store_y(tile):
        n = tile["T"]
        for dt_ in range(16):
            ps = next_psb()
            C.tr(ps(None, ps.t[0:n, 0:128]), XT(("t", dt_), XT.t[:, dt_, 0:n]), ident.all())
            C.copy("act" if dt_ % 2 else "dve", XS(None, XS.t[0:n, dt_ * 128:(dt_ + 1) * 128]), ps(None, ps.t[0:n, 0:128]))
        if tile["kind"] == "s":
            C.dma(O["y_s"][tile["s0"] * 8:tile["s0"] * 8 + n, :], XS(None, XS.t[0:n, :]))
        else:
            p0 = tile["pos"]
            r = 16 if p0 < 16 else 0
            x0 = p0 + r - 16
            C.dma(O["y_p"][x0:x0 + n - r, :], XS(None, XS.t[r:n, :]))

    def out_proj_ln(W, tile, vec, gcol, bcol):
        T = tile["T"]

        def evac(j, pv):
            xv_ = XT(("t", j), XT.t[:, j, 0:T])
            C.stt("dve", xv_, xv_, ALPHA, pv, ALU.mult, ALU.add)

        stream_mm(W, 16, [(i * 128, 128) for i in range(16)], lambda kt: mixv(kt, T), T, evac)
        layer_norm_cols(XT, 16, T, gcol, bcol, vec, also_fn=lambda ct: XTB(("t", ct), XTB.t[:, ct, 0:T]))

    def layer0(tile):
        T, nseq, L = tile["T"], tile["nseq"], tile["L"]
        is_s = tile["kind"] == "s"
        s0 = tile["s0"]
        xrhs = lambda kt: XTB(("t", kt), XTB.t[:, kt, 0:T])
        pj = lambda j: PJ(("t", j), PJ.t[:, j, 0:T])

        fence(HB, HB.t[0:1, 0, 0:1])
        def evacA(j, pv):
            if j < 8:
                C.copy(ev_eng(), pj(j), pv)
            elif j < 16:
                C.act(pj(j), pv, AF.Sigmoid)
            else:
                C.act(pj(j), pv, AF.Silu)

        stream_mm(I["ev_w_in"], 16, [(i * 128, 128) for i in range(24)], xrhs, T, evacA)

        W_ = 30 + L

        def hb(ct, a, b):
            v = HB.t[:, ct, 0:nseq * W_].rearrange("p (n w) -> p n w", w=W_)[:, :, a:b]
            return HB(("t", ct), v)

        def tokv(buf, ct):
            return buf(("t", ct), buf.t[:, ct, 0:T].rearrange("p (n l) -> p n l", l=L))

        if is_s:
            for q in range(2):
                C.dma(XS(None, XS.t[0:120, 0:1024]), I["s_conv"][s0 * 30 + q * 120:s0 * 30 + (q + 1) * 120, :])
                for ct in range(8):
                    ps = next_psb()
                    C.tr(ps(None, ps.t[:, 0:120]), XS(None, XS.t[0:120, ct * 128:(ct + 1) * 128]), ident(None, ident.t[0:120, 0:120]))
                    dstv = HB.t[:, ct, 0:nseq * W_].rearrange("p (n w) -> p n w", w=W_)[:, 4 * q:4 * q + 4, 0:30]
                    C.copy("dve", HB(("t", ct), dstv), ps(None, ps.t[:, 0:120].rearrange("p (n r) -> p n r", r=30)))
        else:
            for ct in range(8):
                if tile["first"]:
                    C.memset("pool", hb(ct, 0, 30), 0.0)
                else:
                    C.copy("pool", hb(ct, 0, 30), HC(("t", ct), HC.t[:, ct, :].unsqueeze(1)))
        for ct in range(8):
            e = "dve" if ct % 2 == 0 else "pool"
            C.tt(e, hb(ct, 30, 30 + L), tokv(PJ, ct), tokv(PJ, 8 + ct), ALU.mult)
        def evacB(j, pv):
            if j < 8:
                C.copy("act", pj(j), pv)
            else:
                C.act(pj(j), pv, AF.Silu)

        stream_mm(I["ev_w_in"], 16, [(3072 + i * 128, 128) for i in range(16)], xrhs, T, evacB, ev="act")
        Wx = 1 + L

        def xv(buf, a, b, p0=0, p1=32):
            v = buf.t[:, p0:p1, 0:nseq * Wx].rearrange("p q (n w) -> p q n w", w=Wx)[:, :, :, a:b]
            return buf(None, v)

        if is_s:
            for k, (nm, buf) in enumerate((("s_sre", XR), ("s_sim", XI))):
                for q in range(2):
                    for pr in range(16):
                        g0 = 2 * (16 * q + pr)
                        C.dma(S5ST(None, S5ST.t[pr * 8:(pr + 1) * 8, :]),
                              I[nm][s0:s0 + 8, g0:g0 + 2, :].rearrange("n m p -> n (m p)"))
                    ps = next_psb()
                    C.tr(ps(None, ps.t[:, 0:128]), S5ST.all(), ident.all())
                    C.copy("dve", xv(buf, 0, 1, 16 * q, 16 * q + 16),
                           ps(None, ps.t[:, 0:128].rearrange("p (q n o) -> p q n o", n=8, o=1)))
        else:
            for k, buf in enumerate((XR, XI)):
                if tile["first"]:
                    C.memset("pool", xv(buf, 0, 1), 0.0)
                else:
                    C.copy("pool", xv(buf, 0, 1), S5C(None, S5C.t[:, k, :].unsqueeze(2).unsqueeze(3)))
        for q4 in range(8):
            for k, (tab, buf) in enumerate(((BRE, XR), (BIM, XI))):
                ps = next_psa()
                for ip in range(4):
                    hf = ip // 2
                    tb = tab[ip % 2]
                    C.mm(ps(None, ps.t[:, ip * T:(ip + 1) * T]),
                         tb(None, tb.t[64 * hf:64 * hf + 64, q4, :]),
                         PJ(("t", q4), PJ.t[64 * hf:64 * hf + 64, q4, 0:T]))
                C.copy("act", xv(buf, 1, Wx, 4 * q4, 4 * q4 + 4),
                       ps(None, ps.t[:, 0:4 * T].rearrange("p (q n l) -> p q n l", q=4, l=L)))
        for ct in range(8):
            e = "dve"
            acc = tokv(ACC, ct)
            wcol = lambda j, ct=ct: VEC0(None, VEC0.t[:, ct, j:j + 1])
            C.ts(e, acc, hb(ct, 0, L), wcol(0), wcol(31), ALU.mult, ALU.add)
            for j in range(1, 31):
                C.stt(e, acc, hb(ct, j, j + L), wcol(j), acc, ALU.mult, ALU.add)
        if is_s:
            for q in range(2):
                for ct in range(8):
                    ps = next_psb()
                    srcv = HB.t[:, ct, 0:nseq * W_].rearrange("p (n w) -> p n w", w=W_)[:, 4 * q:4 * q + 4, L:L + 30]
                    C.copy("pool", TMP(None, TMP.t[:, 0:120].rearrange("p (n r) -> p n r", r=30)), HB(("t", ct), srcv))
                    C.tr(ps(None, ps.t[0:120, 0:128]), TMP(None, TMP.t[:, 0:120]), ident.all())
                    C.copy("dve", XS(None, XS.t[0:120, ct * 128:(ct + 1) * 128]), ps(None, ps.t[0:120, 0:128]))
                C.dma(O["s_conv_o"][s0 * 30 + q * 120:s0 * 30 + (q + 1) * 120, :], XS(None, XS.t[0:120, 0:1024]))
        else:
            for ct in range(8):
                C.copy("pool", TMP(None, TMP.t[:, 0:30]), HB(("t", ct), HB.t[:, ct, L:L + 30]))
                C.copy("pool", HC(("t", ct), HC.t[:, ct, :]), TMP(None, TMP.t[:, 0:30]))
            if tile["last"]:
                for ct in range(8):
                    ps = next_psb()
                    C.tr(ps(None, ps.t[0:30, 0:128]), HC(("t", ct), HC.t[:, ct, :]), ident.all())
                    C.copy("dve", XS(None, XS.t[0:30, ct * 128:(ct + 1) * 128]), ps(None, ps.t[0:30, 0:128]))
                C.dma(O["p_conv"][:, :], XS(None, XS.t[0:30, 0:1024]))
        layer_norm_cols(ACC, 8, T, 32, 33, VEC0, func=AF.Silu, dst_fn=lambda ct: mixv(8 + ct, T))

        def evac_pw(j, pv):
            C.tt("dve", mixv(j, T), pv, pj(16 + j), ALU.mult)

        stream_mm(I["a_pw"], 8, [(i * 128, 128) for i in range(8)], lambda kt: mixv(8 + kt, T), T, evac_pw)


        arb = S5A(None, S5A.t[:, AR, :].unsqueeze(2).broadcast_to([128, 32, nseq]))
        aib = S5A(None, S5A.t[:, AI, :].unsqueeze(2).broadcast_to([128, 32, nseq]))
        naib = S5A(None, S5A.t[:, NAI, :].unsqueeze(2).broadcast_to([128, 32, nseq]))
        if nseq == 1:
            t1v = S5T(None, S5T.t[:, 8, :].unsqueeze(2))
            t2v = S5T(None, S5T.t[:, 9, :].unsqueeze(2))
        else:
            t1v = SCS(None, SCS.t[:, 0, :, :])
            t2v = SCS(None, SCS.t[:, 1, :, :])

        def col(buf, t):
            v = buf.t[:, :, 0:nseq * Wx].rearrange("p q (n w) -> p q n w", w=Wx)[:, :, :, t]
            return buf(None, v)

        if is_s:
            e = "dve"
            for t in range(L):
                C.tt(e, t1v, col(XR, t), arb, ALU.mult)
                C.tt(e, col(XR, t + 1), col(XR, t + 1), t1v, ALU.add)
                C.tt(e, t1v, col(XI, t), naib, ALU.mult)
                C.tt(e, col(XR, t + 1), col(XR, t + 1), t1v, ALU.add)
                C.tt(e, t2v, col(XI, t), arb, ALU.mult)
                C.tt(e, col(XI, t + 1), col(XI, t + 1), t2v, ALU.add)
                C.tt(e, t2v, col(XR, t), aib, ALU.mult)
                C.tt(e, col(XI, t + 1), col(XI, t + 1), t2v, ALU.add)
        else:
            VTMf_ = VTM.t[:, :, :].rearrange("p a b -> p (a b)")
            PJf_ = PJ.t[:, 24:32, :].rearrange("p a b -> p (a b)")
            ROWf_ = ROW.t[:, :, :].rearrange("p a b -> p (a b)")
            C.dma(VTM(None, VTMf_[:, 0:1024]), s5tab[0])
            C.dma(PJ(None, PJf_), s5tab[1])
            for t0 in range(0, T, 32):
                Tc = min(32, T - t0)
                ec = VTM(None, VTMf_[:, 0:1024].rearrange("p (q t) -> p q t", t=32)[:, :, 0:Tc])
                es = PJ(None, PJf_.rearrange("p (q t) -> p q t", t=32)[:, :, 0:Tc])
                w1 = KW(None, KW.t[:, :].rearrange("p (q t) -> p q t", t=32)[:, :, 0:Tc])
                w2 = ROW(None, ROWf_.rearrange("p (q t) -> p q t", t=32)[:, :, 0:Tc])
                ur = XR(None, XR.t[:, :, 1 + t0:1 + t0 + Tc])
                ui = XI(None, XI.t[:, :, 1 + t0:1 + t0 + Tc])
                C.tt("pool", w1, ur, es, ALU.mult)
                C.tt("dve", w2, ui, es, ALU.mult)
                C.tt("dve", ur, ur, ec, ALU.mult)
                C.tt("pool", ui, ui, ec, ALU.mult)
                C.tt("dve", ur, ur, w2, ALU.add)
                C.tt("pool", ui, ui, w1, ALU.subtract)
                for pr in range(32):
                    rho = S5A(None, S5A.t[:, MAG, pr:pr + 1].broadcast_to([128, Tc]))
                    for buf in (XR, XI):
                        seg = buf(None, buf.t[:, pr, 1 + t0:1 + t0 + Tc])
                        scan(seg, rho, seg, buf(None, buf.t[:, pr, t0:t0 + 1]), ALU.mult, ALU.add)
                C.tt("pool", w1, ur, es, ALU.mult)
                C.tt("dve", w2, ui, es, ALU.mult)
                C.tt("dve", ur, ur, ec, ALU.mult)
                C.tt("pool", ui, ui, ec, ALU.mult)
                C.tt("dve", ur, ur, w2, ALU.subtract)
                C.tt("pool", ui, ui, w1, ALU.add)
        if is_s:
            for k, (nm, buf) in enumerate((("s_sre_o", XR), ("s_sim_o", XI))):
                for q in range(2):
                    C.copy("pool", TMP(None, TMP.t[:, 0:128].rearrange("p (q n o) -> p q n o", n=8, o=1)),
                           xv(buf, L, L + 1, 16 * q, 16 * q + 16))
                    ps = next_psb()
                    C.tr(ps(None, ps.t[:, 0:128]), TMP(None, TMP.t[:, 0:128]), ident.all())
                    C.copy("dve", S5ST.all(), ps(None, ps.t[:, 0:128]))
                    for pr in range(16):
                        g0 = 2 * (16 * q + pr)
                        C.dma(O[nm][s0:s0 + 8, g0:g0 + 2, :].rearrange("n m p -> n (m p)"),
                              S5ST(None, S5ST.t[pr * 8:(pr + 1) * 8, :]))
        else:
            for k, buf in enumerate((XR, XI)):
                C.copy("pool", S5C(None, S5C.t[:, k, :].unsqueeze(2).unsqueeze(3)), xv(buf, L, L + 1))
            if tile["last"]:
                for k, nm in enumerate(("p_sre", "p_sim")):
                    ps = next_psb()
                    C.tr(ps(None, ps.t[0:32, 0:128]), S5C(None, S5C.t[:, k, :]), ident.all())
                    C.copy("dve", S5ST(None, S5ST.t[0:32, :]), ps(None, ps.t[0:32, 0:128]))
                    C.dma(O[nm].rearrange("(q m) p -> q (m p)", m=2), S5ST(None, S5ST.t[0:32, :]))
        for gt in range(8):
            ps = next_psa()
            for ip in range(4):
                pair = 4 * gt + ip
                hf = ip // 2
                ov = ps(None, ps.t[64 * hf:64 * hf + 64, 0:T])
                xr_ = XR(None, XR.t[:, pair, 0:nseq * Wx].rearrange("p (n w) -> p n w", w=Wx)[:, :, 1:Wx])
                xi_ = XI(None, XI.t[:, pair, 0:nseq * Wx].rearrange("p (n w) -> p n w", w=Wx)[:, :, 1:Wx])
                cr, ci = CRE[ip % 2], CIM[ip % 2]
                C.mm(ov, cr(None, cr.t[:, gt, 64 * hf:64 * hf + 64]), xr_, start=(ip % 2 == 0), stop=False)
                C.mm(ov, ci(None, ci.t[:, gt, 64 * hf:64 * hf + 64]), xi_, start=False, stop=(ip % 2 == 1))
            gv = pj(gt)
            C.stt("dve", gv, gv, VEC0(None, VEC0.t[:, gt, 34:35]), ps(None, ps.t[:, 0:T]), ALU.mult, ALU.add)
            C.act(GELB(("t", gt), GELB.t[:, gt, 0:T]), gv, AF.Gelu)

        def evac_glu(j, pv):
            if j < 8:
                C.act(ACC(("t", j), ACC.t[:, j, 0:T]), pv, AF.Identity, bias=VECG(None, VECG.t[:, j, 0:1]))
            else:
                jj = j - 8
                tv = TMP(None, TMP.t[:, 0:T])
                C.act(tv, pv, AF.Sigmoid, bias=VECG(None, VECG.t[:, j, 0:1]))
                C.tt("dve", tv, tv, ACC(("t", jj), ACC.t[:, jj, 0:T]), ALU.mult)
                C.tt("dve", mixv(8 + jj, T), tv, pj(8 + jj), ALU.mult)

        stream_mm(I["s5_glu_w"], 8, [(i * 128, 128) for i in range(16)], lambda kt: GELB(("t", kt), GELB.t[:, kt, 0:T]), T, evac_glu)
        out_proj_ln(I["ev_w_out"], tile, VECG, 1, 2)

    do_rwkv = cfg.get("rwkv", True)
    MASKBIG = {"p": C.sb("mbig_p", [128, 128]), "s": C.sb("mbig_s", [64, 64])}
    SEGTRI = {"p": C.sb("stri_p", [128, 128]), "s": C.sb("stri_s", [64, 64])}
    R01 = {"p": C.sb("r01p", [128, 128]), "s": C.sb("r01s", [128, 64])}
    RNEG = {"p": C.sb("rnegp", [128, 128]), "s": C.sb("rnegs", [128, 64])}
    SEGSEL = C.sb("SEGSEL", [8, 64])
    SEGC = C.sb("SEGC", [64, 8])
    SEGROW = C.sb("SEGROW", [128, 8, 64])
    for k in ("p", "s"):
        C.dma(MASKBIG[k].all(), I["maskbig_" + k])
        C.dma(SEGTRI[k].all(), I["segtri_" + k])
        C.dma(R01[k].all(), I["r01_" + k])
        C.dma(RNEG[k].all(), I["rneg_" + k])
    C.dma(SEGSEL.all(), I["segsel"])
    C.dma(SEGC.all(), I["segc"])
    C.dma(SEGROW.all(), I["segrow"].rearrange("p (n t) -> p n t", n=8))

    VEC1 = C.sb("VEC1", [128, 8, 8])
    VMU = C.sb("VMU", [128, 25, 1])
    VECO = C.sb("VECO", [128, 16, 2])
    GB = C.sb("GB", [8, 1])
    load_rows_T(VEC1, [I[n_] for n_ in ("m_hn_g", "r_w0", "r_a0", "r_kk", "r_ka", "r_ln_g", "r_ln_b", "r_rk")], 1024)
    load_rows_T(VMU, [I["r_mu"][:, 0:2048]], 2048)
    load_rows_T(VMU, [I["r_mu"][:, 2048:3200]], 1152, ct0=16)
    load_rows_T(VECO, [I["od_ln_g"], I["od_ln_b"]], 2048)
    C.dma(GB(None, GB.t[0:4, :]), I["m_ig_b"].rearrange("o h -> h o"), slow=True)
    C.dma(GB(None, GB.t[4:8, :]), I["m_fg_b"].rearrange("o h -> h o"), slow=True)

    VTM = C.sb("VTM", [128, 4, 257])
    KW = C.sb("KW", [128, 1024])
    KWN = C.sb("KWN", [128, 256])
    GX = C.sb("GX", [8, 128])
    ROW = C.sb("ROW", [128, 8, 128])
    COL = C.sb("COL", [128, 64])
    DTB = C.sb("DTB", [128, 128])
    STB = C.sb("STB", [128, 128])
    P1S = C.sb("P1S", [128, 257])
    NUM = C.sb("NUM", [128, 257])
    HN = C.sb("HN", [128, 256])
    SM = C.sb("SM", [128, 16])
    CS = C.sb("CS", [128, 4, 2, 257])
    MCAR = C.sb("MCAR", [128, 4])
    CSS = [C.sb("CSS%d" % i, [128, 2, 257]) for i in range(2)]
    CSO = [C.sb("CSO%d" % i, [128, 2, 257]) for i in range(2)]
    MS = C.sb("MS", [8, 4])
    MSB = C.sb("MSB", [8, 128])
    MINIT = C.sb("MINIT", [128, 4, 8])
    DEC = C.sb("DEC", [128, 4, 8])
    MNEW = C.sb("MNEW", [128, 4, 8])
    QM = [C.sb("QM%d" % i, [128, 64]) for i in range(2)]
    C.memset("pool", VTM(None, VTM.t[:, :, 256:257]), 1.0)

    def stream_mm_tok(W, c0, ncols, T, evac):
        Wv = W.rearrange("(kt p) c -> p kt c", p=128)
        for g in range(ncols // WG):
            ps = next_psa()
            pv = ps(None, ps.t[0:T, 0:WG])
            for kq in range(0, 16, KQ):
                ws = WS[rr["ws"] % NWS]
                rr["ws"] += 1
                load_slab(ws, Wv[:, kq:kq + KQ, c0 + g * WG:c0 + (g + 1) * WG], KQ, WG)
                for k in range(KQ):
                    kt = kq + k
                    C.mm(pv, XTB(("t", kt), XTB.t[:, kt, 0:T]), ws(None, ws.t[:, k, 0:WG]), start=(kt == 0), stop=(kt == 15))
            evac(g, pv)

    def recip(e, out, in_):
        return C.op(e, lambda g: g.reciprocal(out=out.ap, in_=in_.ap), reads=[in_], writes=[out])

    def scan(out, d0, d1, init, op0, op1):
        rd = [d0, d1] + ([init] if isinstance(init, View) else [])
        ia = init.ap if isinstance(init, View) else init
        return C.op("dve", lambda g: g.tensor_tensor_scan(out=out.ap, data0=d0.ap, data1=d1.ap, initial=ia, op0=op0, op1=op1),
                    reads=rd, writes=[out])

    SR = C.sb("SR", [128, 8, 64])
    W2A2 = C.sb("W2A2", [128, 1024])
    BLK = C.sb("BLK", [128, 128])
    OMKA = C.sb("OMKA", [128, 8])
    SHC = C.sb("SHC", [128, 25])
    SUMB = C.sb("SUMB", [128, 8])
    C.dma(W2A2(None, W2A2.t[0:64, :]), I["r_w2"])
    C.dma(W2A2(None, W2A2.t[64:128, :]), I["r_a2"])
    C.dma(BLK.all(), I["blk"])
    C.ts("dve", OMKA.all(), VEC1(None, VEC1.t[:, :, 4]), -1.0, 1.0, ALU.mult, ALU.add)
    XIf = XI.t[:, :, :].rearrange("p a b -> p (a b)")
    T1 = XI("T1", XIf[:, 0:512].rearrange("p (j k) -> p j k", k=64))
    T2 = XI("T2", XIf[:, 512:1024].rearrange("p (j k) -> p j k", k=64))
    FSv = XIf[:, 1024:1536].rearrange("p (i t) -> p i t", t=128)
    SRS = [XI(("SRS", i), XIf[:, 1536 + 512 * i:2048 + 512 * i].rearrange("p (j k) -> p j k", k=64)) for i in range(2)]
    TWv = XIf[:, 2560:2688]
    ALLPS = PSA + PSB

    def next_ps8():
        rr["ps8"] = (rr.get("ps8", 0) + 1) % 8
        return ALLPS[rr["ps8"]]

    def rwkv(tile):
        T, nseq, L = tile["T"], tile["nseq"], tile["L"]
        is_s = tile["kind"] == "s"
        s0 = tile["s0"]
        Wx = 1 + L
        xrhs = lambda kt: XTB(("t", kt), XTB.t[:, kt, 0:T])
        pj = lambda j: PJ(("t", j), PJ.t[:, j, 0:T])
        pj3 = lambda j: PJ(("t", j), PJ.t[:, j, 0:T].rearrange("p (n l) -> p n l", l=L))

        def ppv(j0, j1, a, b):
            v = XR.t[:, j0:j1, 0:nseq * Wx].rearrange("p j (n w) -> p j n w", w=Wx)[:, :, :, a:b]
            return XR(("pp", j0) if j1 == j0 + 1 else None, v)

        if is_s:
            for (c0, ncol, ct0) in ((0, 2048, 0), (2048, 1152, 16)):
                C.dma(XS(None, XS.t[0:8, 0:ncol]), I["s_rsh"][s0:s0 + 8, c0:c0 + ncol])
                for ct in range(ncol // 128):
                    ps = next_psb()
                    C.tr(ps(None, ps.t[:, 0:8]), XS(None, XS.t[0:8, ct * 128:(ct + 1) * 128]), ident(None, ident.t[0:8, 0:8]))
                    C.copy("dve", ppv(ct0 + ct, ct0 + ct + 1, 0, 1), ps(None, ps.t[:, 0:8].rearrange("p (j n o) -> p j n o", j=1, o=1)))
        else:
            if tile["first"]:
                C.memset("pool", ppv(0, 25, 0, 1), 0.0)
            else:
                C.copy("pool", ppv(0, 25, 0, 1), SHC(None, SHC.t[:, :].unsqueeze(2).unsqueeze(3)))

        def evacR(j, pv):
            if j < 25:
                C.copy(ev_eng(), ppv(j, j + 1, 1, Wx), pv.buf(None, pv.ap.rearrange("p (j n l) -> p j n l", j=1, l=L)))
            else:
                C.act(pj(j), pv, AF.Silu)

        stream_mm(I["od_w_in"], 16, [(5128 + i * 128, 128) for i in range(33)], xrhs, T, evacR)

        if is_s:
            for (c0, ncol, ct0) in ((0, 2048, 0), (2048, 1152, 16)):
                for ct in range(ncol // 128):
                    ps = next_psb()
                    C.copy("pool", TMP(None, TMP.t[:, 0:8]), XR(("pp", ct0 + ct), XR.t[:, ct0 + ct, 0:nseq * Wx].rearrange("p (n w) -> p n w", w=Wx)[:, :, L]))
                    C.tr(ps(None, ps.t[0:8, 0:128]), TMP(None, TMP.t[:, 0:8]), ident.all())
                    C.copy("dve", XS(None, XS.t[0:8, ct * 128:(ct + 1) * 128]), ps(None, ps.t[0:8, 0:128]))
                C.dma(O["s_rsh_o"][s0:s0 + 8, c0:c0 + ncol], XS(None, XS.t[0:8, 0:ncol]))
        else:
            C.copy("pool", SHC(None, SHC.t[:, :].unsqueeze(2).unsqueeze(3)), ppv(0, 25, L, L + 1))
            if tile["last"]:
                for (c0, ncol, ct0) in ((0, 2048, 0), (2048, 1152, 16)):
                    for ct in range(ncol // 128):
                        ps = next_psb()
                        C.tr(ps(None, ps.t[0:1, 0:128]), SHC(None, SHC.t[:, ct0 + ct:ct0 + ct + 1]), ident.all())
                        C.copy("dve", XS(None, XS.t[0:1, ct * 128:(ct + 1) * 128]), ps(None, ps.t[0:1, 0:128]))
                    C.dma(O["p_rsh"][:, c0:c0 + ncol], XS(None, XS.t[0:1, 0:ncol]))
        for j in range(25):
            C.tt("pool", pj3(j), XR(("pp", j), XR.t[:, j, 0:nseq * Wx].rearrange("p (n w) -> p n w", w=Wx)[:, :, 0:L]),
                 XR(("pp", j), XR.t[:, j, 0:nseq * Wx].rearrange("p (n w) -> p n w", w=Wx)[:, :, 1:Wx]), ALU.subtract)
            C.stt("dve", pj3(j), pj3(j), VMU(None, VMU.t[:, j, 0:1]),
                  XR(("pp", j), XR.t[:, j, 0:nseq * Wx].rearrange("p (n w) -> p n w", w=Wx)[:, :, 1:Wx]), ALU.mult, ALU.add)

        VTMf = VTM.t[:, :, :].rearrange("p a b -> p (a b)")
        ROWf = ROW.t[:, :, :].rearrange("p a b -> p (a b)")
        KKt = lambda a, b: KW(None, KW.t[0:T, a:b])
        Wt = lambda a, b: VTM(None, VTMf[0:T, a:b])
        KKAt = lambda a, b: ROW(None, ROWf[0:T, a:b])
        KPt = lambda a, b: XS(None, XS.t[0:T, a:b])
        Rt = lambda a, b: XS(None, XS.t[0:T, 1024 + a:1024 + b])
        fs = lambda i: XI(("FS", i), FSv[:, i, 0:T])
        tw = XI("TW", TWv[0:64, 0:T])
        C.act(tw, PJ(("t", 24), PJ.t[0:64, 24, 0:T]), AF.Tanh)

        def to_tok(dst, src):
            ps = next_psb()
            C.tr(ps(None, ps.t[0:T, 0:128]), src, ident.all())
            C.copy(ev_eng(), dst, ps(None, ps.t[0:T, 0:128]))

        NE05 = -float(np.exp(-0.5))
        for ct in range(8):
            r_, k_, v_ = pj(ct), pj(8 + ct), pj(16 + ct)
            cs_ = slice(ct * 128, (ct + 1) * 128)
            ps = next_psa()
            C.mm(ps(None, ps.t[:, 0:T]), W2A2(None, W2A2.t[0:64, cs_]), tw)
            C.act(fs(0), ps(None, ps.t[:, 0:T]), AF.Sigmoid, bias=VEC1(None, VEC1.t[:, ct, 1:2]))
            C.act(fs(0), fs(0), AF.Exp, scale=NE05)
            to_tok(Wt(ct * 128, (ct + 1) * 128), fs(0))
            ps = next_psa()
            C.mm(ps(None, ps.t[:, 0:T]), W2A2(None, W2A2.t[64:128, cs_]), PJ(("t", 24), PJ.t[64:128, 24, 0:T]))
            C.act(fs(1), ps(None, ps.t[:, 0:T]), AF.Sigmoid, bias=VEC1(None, VEC1.t[:, ct, 2:3]))
            C.ts("dve", fs(2), k_, VEC1(None, VEC1.t[:, ct, 3:4]), None, ALU.mult)
            C.tt("pool", fs(3), fs(2), fs(2), ALU.mult)
            ps = next_psa()
            C.mm(ps(None, ps.t[:, 0:T]), BLK.all(), fs(3))
            C.act(fs(3), ps(None, ps.t[:, 0:T]), AF.Sqrt)
            C.ts("dve", fs(3), fs(3), 1e-12, None, ALU.max)
            recip("dve", fs(3), fs(3))
            C.tt("dve", fs(2), fs(2), fs(3), ALU.mult)
            to_tok(KKt(ct * 128, (ct + 1) * 128), fs(2))
            C.tt("dve", fs(3), fs(2), fs(1), ALU.mult)
            to_tok(KKAt(ct * 128, (ct + 1) * 128), fs(3))
            C.ts("dve", fs(1), fs(1), VEC1(None, VEC1.t[:, ct, 4:5]), OMKA(None, OMKA.t[:, ct:ct + 1]), ALU.mult, ALU.add)
            C.tt("dve", fs(1), fs(1), k_, ALU.mult)
            to_tok(KPt(ct * 128, (ct + 1) * 128), fs(1))
            to_tok(Rt(ct * 128, (ct + 1) * 128), r_)
            C.tt("dve", fs(3), r_, fs(1), ALU.mult)
            C.ts("dve", fs(3), fs(3), VEC1(None, VEC1.t[:, ct, 7:8]), None, ALU.mult)
            ps = next_psa()
            C.mm(ps(None, ps.t[:, 0:T]), BLK.all(), fs(3))
            C.tt("dve", ACC(("t", ct), ACC.t[:, ct, 0:T]), ps(None, ps.t[:, 0:T]), v_, ALU.mult)

        Yv = MIX.t[:, 8 * TT:16 * TT].rearrange("p (j t) -> p j t", j=8)
        srcs = (("kk", KW.t[0:T, :]), ("w", VTMf[0:T, 0:1024]), ("kka", ROWf[0:T, 0:1024]), ("kp", XS.t[0:T, 0:1024]), ("r", XS.t[0:T, 1024:2048]))
        bufs = {"kk": KW, "w": VTM, "kka": ROW, "kp": XS, "r": XS}
        for n in range(nseq):
            if is_s:
                sr = SRS[n % 2]
                C.dma(sr, I["s_rs"][s0 + n].rearrange("(j hp) v k -> (hp v) j k", hp=2))
            else:
                sr = SR.all()
                if tile["first"]:
                    C.memset("pool", sr, 0.0)
            for l in range(L):
                t = n * L + l
                oh = ident(None, ident.t[0:T, t:t + 1].broadcast_to([T, 64]))
                bc = {}
                for nm, ap in srcs:
                    ps = next_ps8()
                    xv = ap.rearrange("p (j hp k) -> p hp j k", hp=2, k=64)
                    C.mm(ps(None, ps.t[0:64, 0:512]), oh, bufs[nm](None, xv[:, 0]))
                    C.mm(ps(None, ps.t[64:128, 0:512]), oh, bufs[nm](None, xv[:, 1]))
                    bc[nm] = ps(None, ps.t[:, 0:512].rearrange("p (j k) -> p j k", k=64))
                C.tt("dve", T1, sr, bc["kk"], ALU.mult)
                C.op("dve", lambda g: g.reduce_sum(out=SUMB.t[:, :], in_=T1.ap, axis=AX.X), reads=[T1], writes=[SUMB.all()])
                C.tt("dve", sr, sr, bc["w"], ALU.mult)
                C.tt("dve", T2, bc["kka"], SUMB(None, SUMB.t[:, :].unsqueeze(2).broadcast_to([128, 8, 64])), ALU.mult)
                C.tt("dve", sr, sr, T2, ALU.subtract)
                C.tt("dve", T1, bc["kp"], PJ(None, PJ.t[:, 16:24, t].unsqueeze(2).broadcast_to([128, 8, 64])), ALU.mult)
                C.tt("dve", sr, sr, T1, ALU.add)
                C.tt("dve", T2, sr, bc["r"], ALU.mult)
                yv = MIX(None, Yv[:, :, t])
                C.op("dve", lambda g, yv=yv: g.reduce_sum(out=yv.ap, in_=T2.ap, axis=AX.X), reads=[T2], writes=[yv])
            if is_s:
                C.dma(O["s_rs_o"][s0 + n].rearrange("(j hp) v k -> (hp v) j k", hp=2), sr)
        if (not is_s) and tile["last"]:
            C.dma(O["p_rs"].rearrange("(j hp) v k -> (hp v) j k", hp=2), SR.all())

        for j in range(8):
            y = mixv(8 + j, T)
            ps = next_psa()
            C.mm(ps(None, ps.t[:, 0:T]), BLK.all(), y)
            C.tt("pool", fs(0), y, y, ALU.mult)
            ps2 = next_psa()
            C.mm(ps2(None, ps2.t[:, 0:T]), BLK.all(), fs(0))
            C.ts("dve", fs(1), ps(None, ps.t[:, 0:T]), 1.0 / 64, None, ALU.mult)
            C.ts("dve", fs(2), ps2(None, ps2.t[:, 0:T]), 1.0 / 64, None, ALU.mult)
            C.tt("dve", fs(3), fs(1), fs(1), ALU.mult)
            C.tt("dve", fs(2), fs(2), fs(3), ALU.subtract)
            C.ts("dve", fs(2), fs(2), 64e-5, None, ALU.add)
            C.act(fs(2), fs(2), AF.Sqrt)
            recip("dve", fs(2), fs(2))
            C.tt("dve", y, y, fs(1), ALU.subtract)
            C.tt("dve", y, y, fs(2), ALU.mult)
            C.act(y, y, AF.Identity, bias=VEC1(None, VEC1.t[:, j, 6:7]), scale=VEC1(None, VEC1.t[:, j, 5:6]))
            C.tt("dve", y, y, ACC(("t", j), ACC.t[:, j, 0:T]), ALU.add)
            C.tt("dve", y, y, pj(25 + j), ALU.mult)

    SSTRI = {"p": C.sb("sstri_p", [128, 128]), "s": C.sb("sstri_s", [64, 64])}
    SSTRIT = {"p": C.sb("sstriT_p", [128, 128]), "s": C.sb("sstriT_s", [64, 64])}
    for k_ in ("p", "s"):
        C.dma(SSTRI[k_].all(), I["sstri_" + k_])
        C.dma(SSTRIT[k_].all(), I["sstriT_" + k_])
    WLB = C.sb("WLB", [128, 8, 8])
    NB16 = C.sb("NB16", [128, 10, 128], BF16)
    HBf = HB.t[:, :, :].rearrange("p a b -> p (a b)")
    KKv = KW.t[:, :].rearrange("p (j t) -> p j t", t=128)
    BTv = ROW.t
    XRf = XR.t[:, :, :].rearrange("p a b -> p (a b)")
    S0Tv = XRf[:, 0:4096].rearrange("p (n j v) -> p n j v", n=8, j=8)

    def fence(buf, ap):
        C.op("pool", lambda g: g.memset(ap, 0.0), writes=[buf.all()])

    def rwkv2(tile):
        T, nseq, L = tile["T"], tile["nseq"], tile["L"]
        is_s = tile["kind"] == "s"
        kd = tile["kind"]
        s0 = tile["s0"]
        Wx = 1 + L
        xrhs = lambda kt: XTB(("t", kt), XTB.t[:, kt, 0:T])
        pj = lambda j: PJ(("t", j), PJ.t[:, j, 0:T])
        pj3 = lambda j: PJ(("t", j), PJ.t[:, j, 0:T].rearrange("p (n l) -> p n l", l=L))

        def ppv(j0, j1, a, b):
            v = XR.t[:, j0:j1, 0:nseq * Wx].rearrange("p j (n w) -> p j n w", w=Wx)[:, :, :, a:b]
            return XR(("pp", j0) if j1 == j0 + 1 else None, v)

        if is_s:
            for (c0, ncol, ct0) in ((0, 2048, 0), (2048, 1152, 16)):
                C.dma(XS(None, XS.t[0:8, 0:ncol]), I["s_rsh"][s0:s0 + 8, c0:c0 + ncol])
                for ct in range(ncol // 128):
                    ps = next_psb()
                    C.tr(ps(None, ps.t[:, 0:8]), XS(None, XS.t[0:8, ct * 128:(ct + 1) * 128]), ident(None, ident.t[0:8, 0:8]))
                    C.copy("dve", ppv(ct0 + ct, ct0 + ct + 1, 0, 1), ps(None, ps.t[:, 0:8].rearrange("p (j n o) -> p j n o", j=1, o=1)))
        else:
            if tile["first"]:
                C.memset("pool", ppv(0, 25, 0, 1), 0.0)
            else:
                C.copy("pool", ppv(0, 25, 0, 1), SHC(None, SHC.t[:, :].unsqueeze(2).unsqueeze(3)))

        def evacR(j, pv):
            if j < 25:
                C.copy(ev_eng(), ppv(j, j + 1, 1, Wx), pv.buf(None, pv.ap.rearrange("p (j n l) -> p j n l", j=1, l=L)))
            else:
                C.act(pj(j), pv, AF.Silu)

        for _ in stream_mm_g(I["od_w_in"], 16, [(5128 + i * 128, 128) for i in range(25)], xrhs, T, evacR):
            yield "g"
        yield "PD_DONE"
        stream_mm(I["od_w_in"], 16, [(5128 + (25 + i) * 128, 128) for i in range(8)], xrhs, T, lambda j, pv: evacR(25 + j, pv))

        if is_s:
            for (c0, ncol, ct0) in ((0, 2048, 0), (2048, 1152, 16)):
                for ct in range(ncol // 128):
                    ps = next_psb()
                    C.copy("pool", TMP(None, TMP.t[:, 0:8]), XR(("pp", ct0 + ct), XR.t[:, ct0 + ct, 0:nseq * Wx].rearrange("p (n w) -> p n w", w=Wx)[:, :, L]))
                    C.tr(ps(None, ps.t[0:8, 0:128]), TMP(None, TMP.t[:, 0:8]), ident.all())
                    C.copy("dve", XS(None, XS.t[0:8, ct * 128:(ct + 1) * 128]), ps(None, ps.t[0:8, 0:128]))
                C.dma(O["s_rsh_o"][s0:s0 + 8, c0:c0 + ncol], XS(None, XS.t[0:8, 0:ncol]))
        else:
            C.copy("pool", SHC(None, SHC.t[:, :].unsqueeze(2).unsqueeze(3)), ppv(0, 25, L, L + 1))
            if tile["last"]:
                for (c0, ncol, ct0) in ((0, 2048, 0), (2048, 1152, 16)):
                    for ct in range(ncol // 128):
                        ps = next_psb()
                        C.tr(ps(None, ps.t[0:1, 0:128]), SHC(None, SHC.t[:, ct0 + ct:ct0 + ct + 1]), ident.all())
                        C.copy("dve", XS(None, XS.t[0:1, ct * 128:(ct + 1) * 128]), ps(None, ps.t[0:1, 0:128]))
                    C.dma(O["p_rsh"][:, c0:c0 + ncol], XS(None, XS.t[0:1, 0:ncol]))
        for j in range(25):
            C.tt("pool", pj3(j), XR(("pp", j), XR.t[:, j, 0:nseq * Wx].rearrange("p (n w) -> p n w", w=Wx)[:, :, 0:L]),
                 XR(("pp", j), XR.t[:, j, 0:nseq * Wx].rearrange("p (n w) -> p n w", w=Wx)[:, :, 1:Wx]), ALU.subtract)
            C.stt("dve", pj3(j), pj3(j), VMU(None, VMU.t[:, j, 0:1]),
                  XR(("pp", j), XR.t[:, j, 0:nseq * Wx].rearrange("p (n w) -> p n w", w=Wx)[:, :, 1:Wx]), ALU.mult, ALU.add)

        s0t = lambda n, j, rs=slice(0, 128): XR(("st", n), S0Tv[rs, n, j, :])
        if is_s:
            fence(XR, XR.t[0:1, 0, 0:1])
            for n2 in range(0, 8, 2):
                stg = XS.t[:, :].rearrange("p (n j d k) -> p n j d k", n=2, j=8, d=2)
                for nn in range(2):
                    for d in range(2):
                        C.dma(XS(None, stg[:, nn, :, d, :]), I["s_rs"][s0 + n2 + nn].rearrange("(j hp) v k -> (hp v) j k", hp=2))
                for nn in range(2):
                    n = n2 + nn
                    for j in range(8):
                        ps = next_psb()
                        C.tr(ps(None, ps.t[:, 0:128]), XS(None, stg[:, nn, j, :, :]), ident.all())
                        C.copy("dve", s0t(n, j, slice(0, 64)), ps(None, ps.t[0:64, 0:64]))
                        C.copy("act", s0t(n, j, slice(64, 128)), ps(None, ps.t[64:128, 64:128]))
        else:
            if tile["first"]:
                C.memset("pool", SR.all(), 0.0)

        VTMf = VTM.t[:, :, :].rearrange("p a b -> p (a b)")
        Vtm = lambda hc: XS(None, XS.t[0:T, hc])
        Btm = lambda hc: XS(None, XS.t[0:T, 1024 + hc.start:1024 + hc.stop])
        Ktm = lambda hc: VTM(None, VTMf[0:T, hc])
        fs = lambda i: XI(("FS", i), FSv[:, i, 0:T])
        tw = XI("TW", TWv[0:64, 0:T])
        C.act(tw, PJ(("t", 24), PJ.t[0:64, 24, 0:T]), AF.Tanh)

        def to_tok(dst, src):
            ps = next_psb()
            C.tr(ps(None, ps.t[0:T, 0:128]), src, ident.all())
            C.copy(ev_eng(), dst, ps(None, ps.t[0:T, 0:128]))

        NE05 = -float(np.exp(-0.5))
        kkc = lambda ct, rs=slice(0, 128): KW(("c", ct), KKv[rs, ct, 0:T])
        btc = lambda ct, rs=slice(0, 128): ROW(("c", ct), BTv[rs, ct, 0:T])
        for ct in range(8):
            r_, k_, v_ = pj(ct), pj(8 + ct), pj(16 + ct)
            cs_ = slice(ct * 128, (ct + 1) * 128)
            ps = next_psa()
            C.mm(ps(None, ps.t[:, 0:T]), W2A2(None, W2A2.t[0:64, cs_]), tw)
            C.act(fs(0), ps(None, ps.t[:, 0:T]), AF.Sigmoid, bias=VEC1(None, VEC1.t[:, ct, 1:2]))
            C.ts("dve", fs(0), fs(0), NE05, None, ALU.mult)
            scan(fs(1), R01[kd](None, R01[kd].t[:, 0:T]), fs(0), 0.0, ALU.mult, ALU.add)
            ps = next_psa()
            C.mm(ps(None, ps.t[:, 0:T]), W2A2(None, W2A2.t[64:128, cs_]), PJ(("t", 24), PJ.t[64:128, 24, 0:T]))
            C.act(fs(2), ps(None, ps.t[:, 0:T]), AF.Sigmoid, bias=VEC1(None, VEC1.t[:, ct, 2:3]))
            C.ts("dve", kkc(ct), k_, VEC1(None, VEC1.t[:, ct, 3:4]), None, ALU.mult)
            C.tt("pool", fs(3), kkc(ct), kkc(ct), ALU.mult)
            ps = next_psa()
            C.mm(ps(None, ps.t[:, 0:T]), BLK.all(), fs(3))
            C.act(fs(3), ps(None, ps.t[:, 0:T]), AF.Sqrt)
            C.ts("dve", fs(3), fs(3), 1e-12, None, ALU.max)
            recip("dve", fs(3), fs(3))
            C.tt("dve", kkc(ct), kkc(ct), fs(3), ALU.mult)
            C.tt("dve", btc(ct), kkc(ct), fs(2), ALU.mult)
            C.ts("dve", fs(2), fs(2), VEC1(None, VEC1.t[:, ct, 4:5]), OMKA(None, OMKA.t[:, ct:ct + 1]), ALU.mult, ALU.add)
            C.tt("dve", fs(2), fs(2), k_, ALU.mult)
            C.tt("pool", fs(3), r_, fs(2), ALU.mult)
            C.ts("dve", fs(3), fs(3), VEC1(None, VEC1.t[:, ct, 7:8]), None, ALU.mult)
            ps = next_psa()
            C.mm(ps(None, ps.t[:, 0:T]), BLK.all(), fs(3))
            C.tt("dve", ACC(("t", ct), ACC.t[:, ct, 0:T]), ps(None, ps.t[:, 0:T]), v_, ALU.mult)
            C.act(fs(3), fs(1), AF.Exp)
            C.tt("dve", r_, r_, fs(3), ALU.mult)
            C.copy("pool", WLB(None, WLB.t[:, ct, 0:nseq]),
                   XI(("FS", 3), FSv[:, 3, 0:T].rearrange("p (n l) -> p n l", l=L)[:, :, L - 1]))
            C.tt("dve", fs(3), fs(1), fs(0), ALU.subtract)
            C.act(fs(3), fs(3), AF.Exp)
            C.tt("dve", kkc(ct), kkc(ct), fs(3), ALU.mult)
            C.act(fs(3), fs(1), AF.Exp, scale=-1.0)
            C.tt("dve", btc(ct), btc(ct), fs(3), ALU.mult)
            C.tt("dve", k_, fs(2), fs(3), ALU.mult)
            to_tok(Vtm(cs_), v_)
            to_tok(Btm(cs_), btc(ct))
            to_tok(Ktm(cs_), k_)

        fence(HB, HB.t[0:1, 0, 0:1])
        mat = lambda i: HB(("m", i), HBf[0:T, i * 128:i * 128 + T])
        half = lambda i, a: HB(("m", i), HBf[0:T, i * 128 + 64 * a:i * 128 + 64 * a + 64])
        nsq = max(0, int(np.ceil(np.log2(L))) - 1)
        idT = ident(None, ident.t[0:T, 0:T])
        mS = SSTRI[kd](None, SSTRI[kd].t[0:T, 0:T])
        mST = SSTRIT[kd](None, SSTRIT[kd].t[0:T, 0:T])
        mI = SEGTRI[kd](None, SEGTRI[kd].t[0:T, 0:T])
        if is_s:
            KKM, RM = T1, T2
        for j in range(8):
            nb = lambda i: NB16(("n", i), NB16.t[0:T, i, 0:T])
            Q = [[nb(5 * hp + 0), nb(5 * hp + 1)] for hp in range(2)]
            QT = [[nb(5 * hp + 2), nb(5 * hp + 3)] for hp in range(2)]
            P16 = [nb(5 * hp + 4) for hp in range(2)]
            Pm = [mat(8 * hp + 4) for hp in range(2)]
            BR = [mat(8 * hp + 5) for hp in range(2)]
            AK = [mat(8 * hp + 6) for hp in range(2)]
            KR = [mat(8 * hp + 7) for hp in range(2)]
            RHS = [half(16, hp) for hp in range(2)]
            SAT = [half(17, hp) for hp in range(2)]
            rsl = [slice(0, 64), slice(64, 128)]
            if is_s:
                C.tt("pool", KKM, KW(("c", j), KKv[:, j, 0:T].unsqueeze(1).broadcast_to([128, 8, T])), SEGROW(None, SEGROW.t[:, :, 0:T]), ALU.mult)
                C.tt("pool", RM, PJ(("t", j), PJ.t[:, j, 0:T].unsqueeze(1).broadcast_to([128, 8, T])), SEGROW(None, SEGROW.t[:, :, 0:T]), ALU.mult)
            for hp in range(2):
                rs = rsl[hp]
                rq = PJ(("t", j), PJ.t[rs, j, 0:T])
                kq = PJ(("t", 8 + j), PJ.t[rs, 8 + j, 0:T])
                ps = next_ps8()
                C.mm(ps(None, ps.t[0:T, 0:T]), btc(j, rs), kkc(j, rs))
                C.mm(ps(None, ps.t[0:T, T:2 * T]), btc(j, rs), rq)
                C.tt("dve", Q[hp][0], ps(None, ps.t[0:T, 0:T]), mS, ALU.mult)
                C.tt("dve", BR[hp], ps(None, ps.t[0:T, T:2 * T]), mI, ALU.mult)
                ps = next_ps8()
                C.mm(ps(None, ps.t[0:T, 0:T]), kq, kkc(j, rs))
                C.mm(ps(None, ps.t[0:T, T:2 * T]), kq, rq)
                C.tt("dve", AK[hp], ps(None, ps.t[0:T, 0:T]), mS, ALU.mult)
                C.tt("dve", KR[hp], ps(None, ps.t[0:T, T:2 * T]), mI, ALU.mult)
                ps = next_ps8()
                C.mm(ps(None, ps.t[0:T, 0:T]), kkc(j, rs), btc(j, rs))
                C.tt("dve", QT[hp][0], ps(None, ps.t[0:T, 0:T]), mST, ALU.mult)
                C.stt("dve", Pm[hp], Q[hp][0], -1.0, idT, ALU.mult, ALU.add)
                C.copy("act", P16[hp], Pm[hp])
            cur = 0
            for it in range(nsq):
                nxt = 1 - cur
                last_it = (it == nsq - 1)
                for hp in range(2):
                    if not last_it:
                        ps = next_ps8()
                        C.mm(ps(None, ps.t[0:T, 0:T]), QT[hp][cur], Q[hp][cur])
                        C.copy("act", Q[hp][nxt], ps(None, ps.t[0:T, 0:T]))
                    ps = next_ps8()
                    C.mm(ps(None, ps.t[0:T, 0:T]), Q[hp][cur], QT[hp][cur])
                    C.copy("dve", QT[hp][nxt], ps(None, ps.t[0:T, 0:T]))
                for hp in range(2):
                    ps = next_ps8()
                    C.mm(ps(None, ps.t[0:T, 0:T]), QT[hp][nxt], P16[hp])
                    C.tt("dve", Pm[hp], Pm[hp], ps(None, ps.t[0:T, 0:T]), ALU.add)
                    if not last_it:
                        C.copy("act", P16[hp], Pm[hp])
                cur = nxt
            for hp in range(2):
                rs = rsl[hp]
                hc = slice((2 * j + hp) * 64, (2 * j + hp) * 64 + 64)
                ps = next_ps8()
                if is_s:
                    for n in range(nseq):
                        C.mm(ps(None, ps.t[0:T, 0:64]), XI("T1", KKM.ap[rs, n, :]), s0t(n, j, rs), start=(n == 0), stop=False)
                else:
                    C.mm(ps(None, ps.t[0:T, 0:64]), kkc(j, rs), SR(None, SR.t[rs, j, :]), start=True, stop=False)
                C.mm(ps(None, ps.t[0:T, 0:64]), AK[hp], Vtm(hc), start=False, stop=True)
                C.act(RHS[hp], ps(None, ps.t[0:T, 0:64]), AF.Identity, scale=-1.0)
                ps = next_ps8()
                C.mm(ps(None, ps.t[0:T, 0:64]), Pm[hp], RHS[hp])
                C.copy("dve", SAT[hp], ps(None, ps.t[0:T, 0:64]))
            psY = next_ps8()
            for hp in range(2):
                rs = rsl[hp]
                hc = slice((2 * j + hp) * 64, (2 * j + hp) * 64 + 64)
                ov = psY(None, psY.t[rs, 0:T])
                if is_s:
                    for n in range(nseq):
                        C.mm(ov, s0t(n, j, rs), XI("T2", RM.ap[rs, n, :]), start=(n == 0), stop=False)
                else:
                    C.mm(ov, SR(None, SR.t[rs, j, :]), PJ(("t", j), PJ.t[rs, j, 0:T]), start=True, stop=False)
                C.mm(ov, SAT[hp], BR[hp], start=False, stop=False)
                C.mm(ov, Vtm(hc), KR[hp], start=False, stop=True)
            y = TMP(None, TMP.t[:, 0:T])
            C.copy("act", y, psY(None, psY.t[:, 0:T]))
            ps = next_psa()
            C.mm(ps(None, ps.t[:, 0:T]), BLK.all(), y)
            C.tt("pool", fs(0), y, y, ALU.mult)
            ps2 = next_psa()
            C.mm(ps2(None, ps2.t[:, 0:T]), BLK.all(), fs(0))
            C.ts("dve", fs(1), ps(None, ps.t[:, 0:T]), 1.0 / 64, None, ALU.mult)
            C.ts("dve", fs(2), ps2(None, ps2.t[:, 0:T]), 1.0 / 64, None, ALU.mult)
            C.tt("dve", fs(3), fs(1), fs(1), ALU.mult)
            C.tt("dve", fs(2), fs(2), fs(3), ALU.subtract)
            C.ts("dve", fs(2), fs(2), 64e-5, None, ALU.add)
            C.act(fs(2), fs(2), AF.Sqrt)
            recip("dve", fs(2), fs(2))
            C.tt("dve", y, y, fs(1), ALU.subtract)
            C.tt("dve", y, y, fs(2), ALU.mult)
            C.act(y, y, AF.Identity, bias=VEC1(None, VEC1.t[:, j, 6:7]), scale=VEC1(None, VEC1.t[:, j, 5:6]))
            C.tt("dve", y, y, ACC(("t", j), ACC.t[:, j, 0:T]), ALU.add)
            C.tt("dve", mixv(8 + j, T), y, pj(25 + j), ALU.mult)
            tmpS = XI(("FS", 0), FSv[:, 0, 0:64])
            if is_s:
                SAM = [CSS[hp](None, CSS[hp].t[0:T, :, :].rearrange("p a b -> p (a b)")[:, 0:512].rearrange("p (n v) -> p n v", n=8)) for hp in range(2)]
                VM = [CSO[hp](None, CSO[hp].t[0:T, :, :].rearrange("p a b -> p (a b)")[:, 0:512].rearrange("p (n v) -> p n v", n=8)) for hp in range(2)]
                segc_b = SEGC(None, SEGC.t[0:T, :].unsqueeze(2).broadcast_to([T, 8, 64]))
                for hp in range(2):
                    hc = slice((2 * j + hp) * 64, (2 * j + hp) * 64 + 64)
                    C.tt("pool", SAM[hp], HB(("m", 17), HBf[0:T, 17 * 128 + 64 * hp:17 * 128 + 64 * hp + 64].unsqueeze(1).broadcast_to([T, 8, 64])), segc_b, ALU.mult)
                    C.tt("pool", VM[hp], XS(None, XS.t[0:T, hc].unsqueeze(1).broadcast_to([T, 8, 64])), segc_b, ALU.mult)
                for n in range(nseq):
                    psS = next_ps8()
                    for hp in range(2):
                        rs = rsl[hp]
                        hc = slice((2 * j + hp) * 64, (2 * j + hp) * 64 + 64)
                        C.mm(psS(None, psS.t[rs, 0:64]), Btm(hc), CSS[hp](None, SAM[hp].ap[:, n, :]), start=True, stop=False)
                        C.mm(psS(None, psS.t[rs, 0:64]), Ktm(hc), CSO[hp](None, VM[hp].ap[:, n, :]), start=False, stop=True)
                    wl = WLB(None, WLB.t[:, j, n:n + 1])
                    C.ts("dve", tmpS, s0t(n, j), wl, None, ALU.mult)
                    C.stt("dve", s0t(n, j), psS(None, psS.t[:, 0:64]), wl, tmpS, ALU.mult, ALU.add)
            else:
                psS = next_ps8()
                for hp in range(2):
                    rs = rsl[hp]
                    hc = slice((2 * j + hp) * 64, (2 * j + hp) * 64 + 64)
                    C.mm(psS(None, psS.t[rs, 0:64]), Btm(hc), SAT[hp], start=True, stop=False)
                    C.mm(psS(None, psS.t[rs, 0:64]), Ktm(hc), Vtm(hc), start=False, stop=True)
                wl = WLB(None, WLB.t[:, j, 0:1])
                srj = SR(None, SR.t[:, j, :])
                C.ts("dve", tmpS, srj, wl, None, ALU.mult)
                C.stt("dve", srj, psS(None, psS.t[:, 0:64]), wl, tmpS, ALU.mult, ALU.add)

        def state_out(src_fn, dst):
            for j in range(8):
                ps = next_psb()
                C.tr(ps(None, ps.t[0:64, 0:128]), src_fn(j), ident.all())
                C.copy(ev_eng(), XS(None, XS.t[0:64, j * 128:(j + 1) * 128]), ps(None, ps.t[0:64, 0:128]))
            C.dma(dst.rearrange("(j hp) v k -> v j hp k", hp=2), XS(None, XS.t[0:64, 0:1024].rearrange("p (j hp k) -> p j hp k", j=8, hp=2)))

        if is_s:
            for n in range(nseq):
                state_out(lambda j, n=n: s0t(n, j), O["s_rs_o"][s0 + n])
        elif tile["last"]:
            state_out(lambda j: SR(None, SR.t[:, j, :]), O["p_rs"])


    def layer1(tile):
        T, nseq, L = tile["T"], tile["nseq"], tile["L"]
        is_s = tile["kind"] == "s"
        kd = tile["kind"]
        s0 = tile["s0"]
        xrhs = lambda kt: XTB(("t", kt), XTB.t[:, kt, 0:T])
        pj = lambda j: PJ(("t", j), PJ.t[:, j, 0:T])
        W = I["od_w_in"]

        def evacM(j, pv):
            if j < 8:
                C.act(pj(j), pv, AF.Identity, scale=1.0 / 16.0)
            elif j < 16:
                C.copy(ev_eng(), pj(j), pv)
            elif j < 24:
                C.act(pj(j), pv, AF.Sigmoid)
            elif j == 24:
                C.copy("dve", PJ(("t", 32), PJ.t[0:8, 32, 0:T]), pv)
            else:
                C.act(pj(j - 1), pv, AF.Silu)

        cols = [(i * 128, 128) for i in range(16)] + [(3072 + i * 128, 128) for i in range(8)] + [(4096, 8)] + \
               [(4104 + i * 128, 128) for i in range(8)]
        stream_mm(W, 16, cols, xrhs, T, evacM)

        def evacV(g, pv):
            C.copy(ev_eng(), VTM(None, VTM.t[0:T, 2 * g:2 * g + 2, 0:256]), pv.buf(None, pv.ap.rearrange("p (h v) -> p h v", h=2)))

        C.memset("pool", VTM(None, VTM.t[:, :, 256:257]), 1.0)
        stream_mm_tok(W, 2048, 1024, T, evacV)

        gx = GX(None, GX.t[0:8, 0:T])
        C.ts("dve", gx, PJ(("t", 32), PJ.t[0:8, 32, 0:T]), GB.all(), None, ALU.add)
        col = lambda a, b: COL(None, COL.t[0:T, a:b])
        ps = next_psb()
        C.tr(ps(None, ps.t[0:T, 0:8]), gx, ident(None, ident.t[0:8, 0:8]))
        C.copy("dve", col(0, 8), ps(None, ps.t[0:T, 0:8]))
        C.act(col(8, 12), col(4, 8), AF.Exp, scale=-1.0)
        C.act(col(8, 12), col(8, 12), AF.Ln, bias=1.0)
        C.ts("dve", col(8, 12), col(8, 12), -1.0, None, ALU.mult)
        ps = next_psb()
        C.mm(ps(None, ps.t[0:T, 0:4]), SEGTRI[kd](None, SEGTRI[kd].t[0:T, 0:T]), col(8, 12))
        C.copy("dve", col(12, 16), ps(None, ps.t[0:T, 0:4]))
        C.tt("dve", col(16, 20), col(0, 4), col(12, 16), ALU.subtract)
        if is_s:
            C.dma(MS.all(), I["s_mm"][s0:s0 + 8, :])
            ps = next_psb()
            C.mm(ps(None, ps.t[0:T, 0:4]), SEGSEL(None, SEGSEL.t[0:8, 0:T]), MS.all())
            C.copy("dve", col(20, 24), ps(None, ps.t[0:T, 0:4]))
            for h in range(4):
                C.copy("dve", MSB.all(), MS(None, MS.t[:, h:h + 1].broadcast_to([8, 128])))
                ps = next_psb()
                C.mm(ps(None, ps.t[:, 0:8]), MSB.all(), ident(None, ident.t[0:8, 0:8]))
                C.copy("dve", MINIT(None, MINIT.t[:, h, :]), ps(None, ps.t[:, 0:8]))
        else:
            if tile["first"]:
                C.memset("dve", MCAR.all(), 0.0)
                C.memset("pool", CS.all(), 0.0)
            C.copy("dve", col(20, 24), MCAR(None, MCAR.t[0:T, :]))
            C.copy("dve", MINIT(None, MINIT.t[:, :, 0:1]), MCAR(None, MCAR.t[:, :].unsqueeze(2)))

        row = lambda k: ROW(None, ROW.t[:, k, 0:T])
        ends = lambda k: ROW(None, ROW.t[:, k, 0:T].rearrange("p (n l) -> p n l", l=L)[:, :, L - 1])
        starts = lambda k: ROW(None, ROW.t[:, k, 0:T].rearrange("p (n l) -> p n l", l=L)[:, :, 0])
        rw = rwkv2(tile) if do_rwkv else iter(())
        rw_pd = {"done": not do_rwkv}

        def rw_advance(n):
            for _ in range(n):
                if rw_pd["done"]:
                    return
                if next(rw, "PD_DONE") == "PD_DONE":
                    rw_pd["done"] = True

        rw_advance(1)
        for h in range(4):
            minit_r = MINIT(None, MINIT.t[:, h, 0:nseq])
            ps = next_psb()
            C.mm(ps(None, ps.t[:, 0:T]), ident(None, ident.t[0:8, h:h + 1].broadcast_to([8, 128])), gx)
            C.copy("dve", row(0), ps(None, ps.t[:, 0:T]))
            ps = next_psb()
            C.mm(ps(None, ps.t[:, 0:T]), ident(None, ident.t[0:8, 4 + h:5 + h].broadcast_to([8, 128])), gx)
            C.act(row(1), ps(None, ps.t[:, 0:T]), AF.Exp, scale=-1.0)
            C.act(row(1), row(1), AF.Ln, bias=1.0)
            C.ts("dve", row(1), row(1), -1.0, None, ALU.mult)
            scan(row(2), R01[kd](None, R01[kd].t[:, 0:T]), row(1), 0.0, ALU.mult, ALU.add)
            C.tt("dve", row(3), row(0), row(2), ALU.subtract)
            C.tt("dve", starts(3), starts(3), minit_r, ALU.max)
            scan(row(4), RNEG[kd](None, RNEG[kd].t[:, 0:T]), row(3), -1e30, ALU.add, ALU.max)
            C.copy("dve", ROW(None, ROW.t[:, 5, 0:T].rearrange("p (n l) -> p n l", l=L)),
                   ROW(None, ROW.t[:, 4, 0:T].rearrange("p (n l) -> p n l", l=L)[:, :, L - 1:L].broadcast_to([128, nseq, L])))
            C.tt("dve", MNEW(None, MNEW.t[:, h, 0:nseq]), ends(2), ends(4), ALU.add)
            C.tt("dve", DEC(None, DEC.t[:, h, 0:nseq]), minit_r, ends(4), ALU.subtract)
            C.act(DEC(None, DEC.t[:, h, 0:nseq]), DEC(None, DEC.t[:, h, 0:nseq]), AF.Exp)
            ps = next_psb()
            C.mm(ps(None, ps.t[0:T, 0:1]), row(4), ident(None, ident.t[:, 0:1]))
            C.mm(ps(None, ps.t[0:T, 1:2]), row(5), ident(None, ident.t[:, 0:1]))
            C.copy("dve", col(24, 26), ps(None, ps.t[0:T, 0:2]))
            C.tt("dve", DTB(None, DTB.t[0:T, 0:T]), ROW(None, ROW.t[0:T, 4, 0:T]), MASKBIG[kd](None, MASKBIG[kd].t[0:T, 0:T]), ALU.add)
            C.act(DTB(None, DTB.t[0:T, 0:T]), DTB(None, DTB.t[0:T, 0:T]), AF.Exp, scale=-1.0, bias=col(16 + h, 17 + h))
            ps = next_psa()
            for kt in range(2):
                C.mm(ps(None, ps.t[0:T, 0:T]), pj(8 + 2 * h + kt), pj(2 * h + kt), start=(kt == 0), stop=(kt == 1))
            C.tt("dve", STB(None, STB.t[0:T, 0:T]), ps(None, ps.t[0:T, 0:T]), DTB(None, DTB.t[0:T, 0:T]), ALU.mult)
            ps1 = next_psa()
            C.mm(ps1(None, ps1.t[0:T, 0:257]), STB(None, STB.t[0:T, 0:T]), VTM(None, VTM.t[0:T, h, :]))
            C.copy("act", P1S(None, P1S.t[0:T, :]), ps1(None, ps1.t[0:T, 0:257]))
            ps2 = next_psa()
            if is_s:
                i_ = 0
                for n in range(nseq):
                    cs = CSS[n % 2]
                    C.dma(cs(None, cs.t[:, :, 0:256]), I["s_mc"][s0 + n, h].rearrange("(kt p) v -> p kt v", p=128))
                    C.dma(cs(None, cs.t[:, :, 256:257]), I["s_mn"][s0 + n, h].rearrange("(kt p o) -> p kt o", p=128, o=1), slow=True)
                    for kt in range(2):
                        qm = QM[i_ % 2]
                        i_ += 1
                        C.tt("pool", qm(None, qm.t[:, 0:T]), pj(2 * h + kt), SEGROW(None, SEGROW.t[:, n, 0:T]), ALU.mult)
                        C.mm(ps2(None, ps2.t[0:T, 0:257]), qm(None, qm.t[:, 0:T]), cs(None, cs.t[:, kt, :]),
                             start=(n == 0 and kt == 0), stop=(n == nseq - 1 and kt == 1))
            else:
                for kt in range(2):
                    C.mm(ps2(None, ps2.t[0:T, 0:257]), pj(2 * h + kt), CS(("h", h), CS.t[:, h, kt, :]), start=(kt == 0), stop=(kt == 1))
            sm = lambda a: SM(None, SM.t[0:T, a:a + 1])
            C.tt("dve", sm(0), col(20 + h, 21 + h), col(24, 25), ALU.subtract)
            C.act(sm(0), sm(0), AF.Exp)
            C.stt("dve", NUM(None, NUM.t[0:T, :]), ps2(None, ps2.t[0:T, 0:257]), sm(0), P1S(None, P1S.t[0:T, :]), ALU.mult, ALU.add)
            C.tt("dve", sm(1), col(12 + h, 13 + h), col(24, 25), ALU.add)
            C.act(sm(1), sm(1), AF.Exp, scale=-1.0)
            C.ts("dve", sm(2), NUM(None, NUM.t[0:T, 256:257]), -1.0, None, ALU.mult)
            C.tt("dve", sm(2), sm(2), NUM(None, NUM.t[0:T, 256:257]), ALU.max)
            C.tt("dve", sm(2), sm(2), sm(1), ALU.max)
            recip("dve", sm(2), sm(2))
            C.op("dve", lambda g, T=T: g.reduce_sum(out=SM.t[0:T, 3:4], in_=NUM.t[0:T, 0:256], axis=AX.X),
                 reads=[NUM(None, NUM.t[0:T, 0:256])], writes=[sm(3)])
            C.tt("dve", sm(3), sm(3), sm(2), ALU.mult)
            C.ts("dve", sm(3), sm(3), 1.0 / 256, None, ALU.mult)
            hn = HN(None, HN.t[0:T, :])
            C.ts("dve", hn, NUM(None, NUM.t[0:T, 0:256]), sm(2), sm(3), ALU.mult, ALU.subtract)
            C.tt("dve", P1S(None, P1S.t[0:T, 0:256]), hn, hn, ALU.mult)
            C.op("dve", lambda g, T=T: g.reduce_sum(out=SM.t[0:T, 4:5], in_=P1S.t[0:T, 0:256], axis=AX.X),
                 reads=[P1S(None, P1S.t[0:T, 0:256])], writes=[sm(4)])
            C.ts("dve", sm(4), sm(4), 1.0 / 256, LN_EPS, ALU.mult, ALU.add)
            C.act(sm(4), sm(4), AF.Sqrt)
            recip("dve", sm(4), sm(4))
            C.ts("dve", hn, hn, sm(4), None, ALU.mult)
            for kt in range(2):
                ct = 2 * h + kt
                ps = next_psb()
                C.tr(ps(None, ps.t[:, 0:T]), HN(None, HN.t[0:T, kt * 128:(kt + 1) * 128]), ident(None, ident.t[0:T, 0:T]))
                scr = P1S(None, P1S.t[:, 0:T])
                C.stt("dve", scr, ps(None, ps.t[:, 0:T]), VEC1(None, VEC1.t[:, ct, 0:1]), pj(16 + ct), ALU.mult, ALU.mult)
                C.tt("dve", mixv(ct, T), scr, pj(24 + ct), ALU.mult)
            C.tt("dve", sm(5), col(16 + h, 17 + h), col(25, 26), ALU.subtract)
            C.act(sm(5), sm(5), AF.Exp)
            for kt in range(2):
                ps = next_psb()
                C.tr(ps(None, ps.t[0:T, 0:128]), pj(8 + 2 * h + kt), ident.all())
                C.ts("dve", KW(None, KW.t[0:T, h * 256 + kt * 128:h * 256 + (kt + 1) * 128]), ps(None, ps.t[0:T, 0:128]), sm(5), None, ALU.mult)
            if is_s:
                for n in range(nseq):
                    cs = CSS[n % 2]
                    co = CSO[n % 2]
                    C.dma(cs(None, cs.t[:, :, 0:256]), I["s_mc"][s0 + n, h].rearrange("(kt p) v -> p kt v", p=128))
                    C.dma(cs(None, cs.t[:, :, 256:257]), I["s_mn"][s0 + n, h].rearrange("(kt p o) -> p kt o", p=128, o=1), slow=True)
                    C.ts("pool", KWN(None, KWN.t[0:T, :]), KW(None, KW.t[0:T, h * 256:(h + 1) * 256]), SEGC(None, SEGC.t[0:T, n:n + 1]), None, ALU.mult)
                    for kt in range(2):
                        ps = next_psa()
                        C.mm(ps(None, ps.t[:, 0:257]), KWN(None, KWN.t[0:T, kt * 128:(kt + 1) * 128]), VTM(None, VTM.t[0:T, h, :]))
                        C.stt("dve", co(None, co.t[:, kt, :]), cs(None, cs.t[:, kt, :]), DEC(None, DEC.t[:, h, n:n + 1]), ps(None, ps.t[:, 0:257]), ALU.mult, ALU.add)
                    C.dma(O["s_mc_o"][s0 + n, h].rearrange("(kt p) v -> p kt v", p=128), co(None, co.t[:, :, 0:256]))
                    C.dma(O["s_mn_o"][s0 + n, h].rearrange("(kt p o) -> p kt o", p=128, o=1), co(None, co.t[:, :, 256:257]), slow=True)
                C.dma(O["s_mm_o"][s0:s0 + 8, h:h + 1].rearrange("n o -> o n"), MNEW(None, MNEW.t[0:1, h, 0:8]), slow=True)
            else:
                for kt in range(2):
                    ps = next_psa()
                    C.mm(ps(None, ps.t[:, 0:257]), KW(None, KW.t[0:T, h * 256 + kt * 128:h * 256 + (kt + 1) * 128]), VTM(None, VTM.t[0:T, h, :]))
                    csv = CS(("h", h), CS.t[:, h, kt, :])
                    C.stt("dve", csv, csv, DEC(None, DEC.t[:, h, 0:1]), ps(None, ps.t[:, 0:257]), ALU.mult, ALU.add)
                C.copy("dve", MCAR(None, MCAR.t[:, h:h + 1]), MNEW(None, MNEW.t[:, h, 0:1]))
            rw_advance(2)
        if (not is_s) and tile["last"]:
            for h in range(4):
                C.dma(O["p_mc"][h].rearrange("(kt p) v -> p kt v", p=128), CS(("h", h), CS.t[:, h, :, 0:256]))
                C.dma(O["p_mn"][h].rearrange("(kt p o) -> p kt o", p=128, o=1), CS(("h", h), CS.t[:, h, :, 256:257]), slow=True)
            C.dma(O["p_mm"][:, :], MCAR(None, MCAR.t[0:1, :]))

        if do_rwkv:
            rw_advance(100)
            for _ in rw:
                pass
        else:
            for ct in range(8, 16):
                C.memset("pool", mixv(ct, T), 0.0)
        out_proj_ln(I["od_w_out"], tile, VECO, 0, 1)

    for ti, tile in enumerate(tile_plan(cfg)):
        wst["id"] = 0
        wst["first"] = (ti == 0)
        load_x(tile)
        if nlayers >= 1:
            layer0(tile)
        if nlayers >= 2:
            layer1(tile)
        store_y(tile)

    C.emit()
    es.close()
    return nc, C


def make_in_maps(inp, cores, consts):
    maps = []
    f = lambda a: np.ascontiguousarray(a, dtype=np.float32)
    for c in cores:
        s = c % 4
        m = {}
        m["xp"] = f(inp["x_prompt"][s])
        m["meta"] = f(inp["meta_tokens"])
        m["xs"] = f(inp["x_sample"][16 * c:16 * c + 16].reshape(128, D))
        m["s_conv"] = f(inp["state_conv"][0, 16 * c:16 * c + 16].reshape(480, 1024))
        m["s_sre"] = f(inp["state_ssm_re"][0, 16 * c:16 * c + 16])
        m["s_sim"] = f(inp["state_ssm_im"][0, 16 * c:16 * c + 16])
        m.update(consts)
        sl = slice(16 * c, 16 * c + 16)
        m["s_mc"] = f(inp["state_mlstm_c"][0, sl])
        m["s_mn"] = f(inp["state_mlstm_n"][0, sl])
        m["s_mm"] = f(inp["state_mlstm_m"][0, sl])
        m["s_rs"] = f(inp["state_rwkv_s"][0, sl])
        m["s_rsh"] = f(inp["state_rwkv_shift"][0, sl])
        m["od_w_in"] = f(inp["od_w_in"][0])
        for nm in ("m_ig_b", "m_fg_b", "m_hn_g", "r_w0", "r_a0", "r_kk", "r_ka", "r_ln_g", "r_ln_b", "r_rk", "r_mu", "od_ln_g", "od_ln_b"):
            m[nm] = f(inp[nm][0].reshape(1, -1))
        for nm in ("r_w2", "r_a2", "od_w_out"):
            m[nm] = f(inp[nm][0])
        m["ev_w_in"] = f(inp["ev_w_in"][0])
        m["a_conv_w"] = f(inp["a_conv_w"][0])
        for nm in ("a_conv_b", "a_ln_g", "a_ln_b", "s5_d", "s5_log_dt", "s5_glu_b", "ev_ln_g", "ev_ln_b"):
            m[nm] = f(inp[nm][0].reshape(1, -1))
        m["a_pw"] = f(inp["a_pw"][0])
        for nm in ("s5_lambda_re", "s5_lambda_im", "s5_b_re", "s5_b_im", "s5_glu_w", "ev_w_out"):
            m[nm] = f(inp[nm][0])
        m["s5_c_re"] = f(inp["s5_c_re"][0].reshape(1024, 64))
        m["s5_c_im"] = f(inp["s5_c_im"][0].reshape(1024, 64))
        maps.append(m)
    return maps


def kernel(**inp):
    cfg = {}
    nc, C = build(cfg)
    consts = make_consts()
    cores = list(range(NCORES))
    maps = make_in_maps(inp, cores, consts)
    res = run_bass_kernel_spmd(nc, maps, core_ids=cores)
    R = res.results
    B = 4
    cat = lambda k, shp: np.concatenate([R[c][k].reshape((16,) + shp) for c in range(NCORES)], 0)[None]
    stk = lambda k, shp: np.stack([R[c][k].reshape(shp) for c in range(B)], 0)[None]
    y_p = np.stack([R[c]["y_p"] for c in range(B)], 0)
    y_s = np.concatenate([R[c]["y_s"].reshape(16, 8, D) for c in range(NCORES)], 0)
    return (y_p, y_s,
            stk("p_conv", (30, 1024)), stk("p_sre", (64, 64)), stk("p_sim", (64, 64)), stk("p_mc", (4, 256, 256)),
            stk("p_mn", (4, 256)), stk("p_mm", (4,)), stk("p_rs", (16, 64, 64)), stk("p_rsh", (3200,)),
            cat("s_conv_o", (30, 1024)), cat("s_sre_o", (64, 64)), cat("s_sim_o", (64, 64)), cat("s_mc_o", (4, 256, 256)),
            cat("s_mn_o", (4, 256)), cat("s_mm_o", (4,)), cat("s_rs_o", (16, 64, 64)), cat("s_rsh_o", (3200,)))
```
